# Optimizing a Trainium2 kernel written in Bass

```python
import math
import jax, jax.numpy as jnp
from jax import lax
import numpy as np

D_MODEL = 1024
BATCH = 4
SEQ = 8192
DEPTH = 2

CHUNK = 64
D_PLE = 256
BRANCH_W = 512
D_MIX = 3 * BRANCH_W
RWKV_HEADS = 8
RWKV_HEAD_DIM = BRANCH_W // RWKV_HEADS
DECAY_LORA = 64
ICLR_LORA = 64
RWKV_GN_EPS = 64e-5
S5_GROUP = 16
S5_GROUPS = BRANCH_W // S5_GROUP
S5_STATE = 64
LRU_BLOCKS = 8
LRU_BLOCK_DIM = BRANCH_W // LRU_BLOCKS
CONV_WIDTH = 4
LRU_C = 8.0
NORM_EPS = 1e-6
RWKV_SHIFT_W = 3 * BRANCH_W + DECAY_LORA + ICLR_LORA
SPLITS = [RWKV_SHIFT_W,
          RWKV_SHIFT_W + BRANCH_W,
          RWKV_SHIFT_W + 2 * BRANCH_W,
          RWKV_SHIFT_W + 3 * BRANCH_W,
          RWKV_SHIFT_W + 4 * BRANCH_W]
D_IN = RWKV_SHIFT_W + 5 * BRANCH_W

kernel_name = "hymba_rwkv7_s5_rglru_ple"


def rmsnorm(x, g):
    xf = x.astype(jnp.float32)
    return xf * lax.rsqrt(jnp.mean(xf * xf, axis=-1, keepdims=True) + NORM_EPS) * g.astype(jnp.float32)


def linear_binop(e1, e2):
    a1, b1 = e1
    a2, b2 = e2
    return a1 * a2, a2 * b1 + b2


def rwkv7_group(z, gate, mu, w0, w2, a0, a2, k_k, k_a, r_k, ln_w, ln_b):
    bsz, seq, _ = z.shape
    z = z.astype(jnp.float32)
    z_prev = jnp.pad(z, ((0, 0), (1, 0), (0, 0)))[:, :-1]
    z = z + mu * (z_prev - z)
    r, k, v, wd, ad = jnp.split(z, [BRANCH_W, 2 * BRANCH_W, 3 * BRANCH_W, 3 * BRANCH_W + DECAY_LORA], axis=-1)
    w = -jax.nn.softplus(-(w0 + jnp.tanh(wd) @ w2)) - 0.5
    log_decay = -jnp.exp(w)
    a = jax.nn.sigmoid(a0 + ad @ a2)
    heads = lambda t: t.reshape(bsz, seq, RWKV_HEADS, RWKV_HEAD_DIM)
    kk = heads(k * k_k)
    kk = kk / jnp.maximum(jnp.sqrt(jnp.sum(kk * kk, axis=-1, keepdims=True)), 1e-12)
    k = k * (1.0 + (a - 1.0) * k_a)
    r_h, k_h, v_h, a_h, dec_h = heads(r), heads(k), heads(v), heads(a), heads(jnp.exp(log_decay))
    b_h = kk * a_h

    def step(S, inp):
        r_t, k_t, v_t, kk_t, b_t, d_t = inp
        S = (S * d_t[:, :, None, :]
             - jnp.einsum('bhvk,bhk->bhv', S, kk_t)[..., None] * b_t[:, :, None, :]
             + v_t[..., None] * k_t[:, :, None, :])
        return S, jnp.einsum('bhvk,bhk->bhv', S, r_t)

    tm = lambda t: jnp.moveaxis(t, 1, 0)
    S0 = jnp.zeros((bsz, RWKV_HEADS, RWKV_HEAD_DIM, RWKV_HEAD_DIM), jnp.float32)
    _, y = lax.scan(step, S0, (tm(r_h), tm(k_h), tm(v_h), tm(kk), tm(b_h), tm(dec_h)))
    y = jnp.moveaxis(y, 0, 1)
    mean = jnp.mean(y, axis=-1, keepdims=True)
    var = jnp.mean(jnp.square(y - mean), axis=-1, keepdims=True)
    y = ((y - mean) * lax.rsqrt(var + RWKV_GN_EPS)).reshape(bsz, seq, BRANCH_W) * ln_w + ln_b
    bonus = jnp.sum(r_h * k_h * r_k, axis=-1, keepdims=True) * v_h
    y = y + bonus.reshape(bsz, seq, BRANCH_W)
    return y * jax.nn.silu(gate.astype(jnp.float32))


def s5_group(u, gate, a_re, a_im, log_dt, b_re, b_im, c_re, c_im, d, glu_w, glu_b):
    bsz, seq, _ = u.shape
    f32 = jnp.float32
    u = u.astype(f32)
    lam = lax.complex(a_re.astype(f32), a_im.astype(f32))
    dt = jnp.exp(log_dt.astype(f32))[:, None]
    lam_bar = jnp.exp(lam * dt)
    b_bar = ((lam_bar - 1.0) / lam)[:, :, None] * lax.complex(b_re.astype(f32), b_im.astype(f32))
    c = lax.complex(c_re.astype(f32), c_im.astype(f32))
    steps = jnp.arange(1, CHUNK + 1, dtype=f32)[:, None, None]
    pows = jnp.exp(lam * dt * steps)
    lam_b = jnp.broadcast_to(lam_bar, (bsz, CHUNK, S5_GROUPS, S5_STATE))
    ug = u.reshape(bsz, seq // CHUNK, CHUNK, S5_GROUPS, S5_GROUP)
    ug = jnp.moveaxis(ug, 1, 0)

    def chunk_step(state, u_c):
        bu = jnp.einsum('bcgh,gph->bcgp', u_c, b_bar)
        _, xs = lax.associative_scan(linear_binop, (lam_b, bu), axis=1)
        xs = xs + pows[None] * state[:, None]
        y = jnp.real(jnp.einsum('bcgp,ghp->bcgh', xs, c))
        return xs[:, -1], y

    state0 = jnp.zeros((bsz, S5_GROUPS, S5_STATE), jnp.complex64)
    _, y = lax.scan(chunk_step, state0, ug)
    y = jnp.moveaxis(y, 0, 1).reshape(bsz, seq, BRANCH_W) + d * u
    zg = jax.nn.gelu(y)
    zg = zg * jax.nn.sigmoid(zg @ glu_w + glu_b)
    return zg * jax.nn.silu(gate.astype(f32))


def rglru_group(xb, gate, conv_w, conv_b, wa, ba, wx, bx, lam):
    bsz, seq, _ = xb.shape
    xb = xb.astype(jnp.float32)
    xp = jnp.pad(xb, ((0, 0), (CONV_WIDTH - 1, 0), (0, 0)))
    xc = conv_b + sum(xp[:, j:j + seq] * conv_w[j] for j in range(CONV_WIDTH))
    xh = xc.reshape(bsz, seq, LRU_BLOCKS, LRU_BLOCK_DIM)
    r = jax.nn.sigmoid(jnp.einsum('blhi,hij->blhj', xh, wa).reshape(bsz, seq, BRANCH_W) + ba)
    i = jax.nn.sigmoid(jnp.einsum('blhi,hij->blhj', xh, wx).reshape(bsz, seq, BRANCH_W) + bx)
    log_a = -LRU_C * r * jax.nn.softplus(-lam)
    a = jnp.exp(log_a)
    mult = jnp.sqrt(-jnp.expm1(2.0 * log_a))
    _, h = lax.associative_scan(linear_binop, (a, mult * (i * xc)), axis=1)
    return h * jax.nn.silu(gate.astype(jnp.float32))


def hybrid_layer(h, p_i, norm_g, w_in,
                 rwkv_mu, rwkv_w0, rwkv_w2, rwkv_a0, rwkv_a2, rwkv_k_k, rwkv_k_a, rwkv_r_k, rwkv_ln_w, rwkv_ln_b,
                 s5_a_re, s5_a_im, s5_log_dt, s5_b_re, s5_b_im, s5_c_re, s5_c_im, s5_d, s5_glu_w, s5_glu_b,
                 lru_conv_w, lru_conv_b, lru_wa, lru_ba, lru_wx, lru_bx, lru_lambda,
                 w_out, ple_w, ple_norm_g, ple_gate_w):
    xn = rmsnorm(h, norm_g)
    zin = xn @ w_in.astype(jnp.float32)
    z_rw, g_rw, u_s5, g_s5, x_lru, g_lru = jnp.split(zin, SPLITS, axis=-1)
    y_rw = rwkv7_group(z_rw, g_rw, rwkv_mu, rwkv_w0, rwkv_w2, rwkv_a0, rwkv_a2,
                       rwkv_k_k, rwkv_k_a, rwkv_r_k, rwkv_ln_w, rwkv_ln_b)
    y_s5 = s5_group(u_s5, g_s5, s5_a_re, s5_a_im, s5_log_dt, s5_b_re, s5_b_im,
                    s5_c_re, s5_c_im, s5_d, s5_glu_w, s5_glu_b)
    y_lru = rglru_group(x_lru, g_lru, lru_conv_w, lru_conv_b, lru_wa, lru_ba, lru_wx, lru_bx, lru_lambda)
    y = jnp.concatenate([y_rw, y_s5, y_lru], axis=-1) @ w_out.astype(jnp.float32)
    h = h + y
    e = rmsnorm(p_i @ ple_w, ple_norm_g)
    return h + e * jax.nn.sigmoid(h @ ple_gate_w.astype(jnp.float32))


def setup_inputs(seed: int = 0) -> dict:
    key = jax.random.key(seed)
    ks = iter(jax.random.split(key, 48))
    f32 = jnp.float32
    L, W = DEPTH, BRANCH_W
    G, P, Hg = S5_GROUPS, S5_STATE, S5_GROUP
    NB, BD = LRU_BLOCKS, LRU_BLOCK_DIM

    def nrm(shape, scale):
        return scale * jax.random.normal(next(ks), shape, f32)

    def uni(shape, lo, hi):
        return jax.random.uniform(next(ks), shape, f32, lo, hi)

    x = nrm((BATCH, SEQ, D_MODEL), 1.0)
    p = nrm((DEPTH, BATCH, SEQ, D_PLE), 1.0)
    norm_g = 1.0 + nrm((L, D_MODEL), 0.02)
    w_in = nrm((L, D_MODEL, D_IN), D_MODEL ** -0.5)
    rwkv_mu = uni((L, RWKV_SHIFT_W), 0.0, 1.0)
    rwkv_w0 = uni((L, W), -4.0, 1.0)
    rwkv_w2 = nrm((L, DECAY_LORA, W), 0.1 * DECAY_LORA ** -0.5)
    rwkv_a0 = nrm((L, W), 0.1)
    rwkv_a2 = nrm((L, ICLR_LORA, W), 0.1 * ICLR_LORA ** -0.5)
    rwkv_k_k = 0.85 + nrm((L, W), 0.02)
    rwkv_k_a = 1.0 + nrm((L, W), 0.02)
    rwkv_r_k = nrm((L, RWKV_HEADS, RWKV_HEAD_DIM), 0.1)
    rwkv_ln_w = 1.0 + nrm((L, W), 0.02)
    rwkv_ln_b = nrm((L, W), 0.02)
    s5_a_re = -0.5 + nrm((L, G, P), 0.01)
    s5_a_im = math.pi * jnp.arange(P, dtype=f32) + nrm((L, G, P), 0.01)
    s5_log_dt = uni((L, G), math.log(1e-3), math.log(1e-1))
    s5_b_re = nrm((L, G, P, Hg), (2 * Hg) ** -0.5)
    s5_b_im = nrm((L, G, P, Hg), (2 * Hg) ** -0.5)
    s5_c_re = nrm((L, G, Hg, P), P ** -0.5)
    s5_c_im = nrm((L, G, Hg, P), P ** -0.5)
    s5_d = nrm((L, W), 1.0)
    s5_glu_w = nrm((L, W, W), W ** -0.5)
    s5_glu_b = nrm((L, W), 0.02)
    lru_conv_w = nrm((L, CONV_WIDTH, W), CONV_WIDTH ** -0.5)
    lru_conv_b = nrm((L, W), 0.02)
    lru_wa = nrm((L, NB, BD, BD), BD ** -0.5)
    lru_ba = nrm((L, W), 0.02)
    lru_wx = nrm((L, NB, BD, BD), BD ** -0.5)
    lru_bx = nrm((L, W), 0.02)
    a_c = uni((L, W), 0.9, 0.999) ** (1.0 / LRU_C)
    lru_lambda = jnp.log(a_c) - jnp.log1p(-a_c)
    w_out = nrm((L, D_MIX, D_MODEL), D_MIX ** -0.5)
    ple_w = nrm((L, D_PLE, D_MODEL), D_PLE ** -0.5)
    ple_norm_g = 1.0 + nrm((L, D_MODEL), 0.02)
    ple_gate_w = nrm((L, D_MODEL, D_MODEL), D_MODEL ** -0.5)
    final_norm_g = 1.0 + nrm((D_MODEL,), 0.02)
    return {"x": x, "p": p, "norm_g": norm_g, "w_in": w_in,
            "rwkv_mu": rwkv_mu, "rwkv_w0": rwkv_w0, "rwkv_w2": rwkv_w2, "rwkv_a0": rwkv_a0,
            "rwkv_a2": rwkv_a2, "rwkv_k_k": rwkv_k_k, "rwkv_k_a": rwkv_k_a, "rwkv_r_k": rwkv_r_k,
            "rwkv_ln_w": rwkv_ln_w, "rwkv_ln_b": rwkv_ln_b,
            "s5_a_re": s5_a_re, "s5_a_im": s5_a_im, "s5_log_dt": s5_log_dt,
            "s5_b_re": s5_b_re, "s5_b_im": s5_b_im, "s5_c_re": s5_c_re, "s5_c_im": s5_c_im,
            "s5_d": s5_d, "s5_glu_w": s5_glu_w, "s5_glu_b": s5_glu_b,
            "lru_conv_w": lru_conv_w, "lru_conv_b": lru_conv_b, "lru_wa": lru_wa, "lru_ba": lru_ba,
            "lru_wx": lru_wx, "lru_bx": lru_bx, "lru_lambda": lru_lambda,
            "w_out": w_out, "ple_w": ple_w, "ple_norm_g": ple_norm_g, "ple_gate_w": ple_gate_w,
            "final_norm_g": final_norm_g}


def reference(x, p, norm_g, w_in,
              rwkv_mu, rwkv_w0, rwkv_w2, rwkv_a0, rwkv_a2, rwkv_k_k, rwkv_k_a, rwkv_r_k, rwkv_ln_w, rwkv_ln_b,
              s5_a_re, s5_a_im, s5_log_dt, s5_b_re, s5_b_im, s5_c_re, s5_c_im, s5_d, s5_glu_w, s5_glu_b,
              lru_conv_w, lru_conv_b, lru_wa, lru_ba, lru_wx, lru_bx, lru_lambda,
              w_out, ple_w, ple_norm_g, ple_gate_w, final_norm_g):
    h = x.astype(jnp.float32)
    for i in range(DEPTH):
        h = hybrid_layer(h, p[i].astype(jnp.float32), norm_g[i], w_in[i],
                         rwkv_mu[i], rwkv_w0[i], rwkv_w2[i], rwkv_a0[i], rwkv_a2[i], rwkv_k_k[i],
                         rwkv_k_a[i], rwkv_r_k[i], rwkv_ln_w[i], rwkv_ln_b[i],
                         s5_a_re[i], s5_a_im[i], s5_log_dt[i], s5_b_re[i], s5_b_im[i], s5_c_re[i],
                         s5_c_im[i], s5_d[i], s5_glu_w[i], s5_glu_b[i],
                         lru_conv_w[i], lru_conv_b[i], lru_wa[i], lru_ba[i], lru_wx[i], lru_bx[i],
                         lru_lambda[i], w_out[i], ple_w[i], ple_norm_g[i], ple_gate_w[i])
    return rmsnorm(h, final_norm_g).astype(x.dtype)
```

```python
import contextlib
import math
import numpy as np
import concourse.bass as bass
import concourse.mybir as mybir
from concourse.bass_utils import run_bass_kernel_spmd

F32 = mybir.dt.float32
BF16 = mybir.dt.bfloat16
ALU = mybir.AluOpType
AF = mybir.ActivationFunctionType

D = 1024
DIN = 4224
DMIX = 1536
DPLE = 256
TT = 256
LCH = 64
NCH = TT // LCH
TS5 = 128
GN_EPS = 64e-5
NORM_EPS = 1e-6


class Tok:
    __slots__ = ("sem", "val", "eng", "dma")

    def __init__(self, sem, val, eng, dma):
        self.sem, self.val, self.eng, self.dma = sem, val, eng, dma


class Buf:
    def __init__(self, t, name):
        self.t = t
        self.name = name
        self.w = None
        self.r = []

    def __getitem__(self, idx):
        return self.t[idx]


class Prog:
    ENGS = ("pe", "act", "dve", "pool", "sp")

    def __init__(self, nc, n_dma_sems=32):
        self.nc = nc
        self.es = contextlib.ExitStack()
        self.ops = {e: [] for e in self.ENGS}
        self.cnt = {e: 0 for e in self.ENGS}
        self.sem = {e: self.es.enter_context(nc.semaphore("s_" + e)) for e in self.ENGS}
        self.dsem = [self.es.enter_context(nc.semaphore("d%d" % i)) for i in range(n_dma_sems)]
        self.duse = [0] * n_dma_sems
        self.dnext = 0
        self.seen = {e: {} for e in self.ENGS}
        self.nbuf = 0
        self.ninst = 0
        self.stack = [self.es]

    def push(self):
        st = contextlib.ExitStack()
        self.stack.append(st)

    def pop(self):
        self.barrier()
        self.stack.pop().close()

    def barrier(self):
        toks = []
        for f in self.ENGS:
            if self.cnt[f] > 0:
                toks.append(Tok(self.sem[f], self.cnt[f], f, False))
        for i, s in enumerate(self.dsem):
            if self.duse[i] > 0:
                toks.append(Tok(s, 16 * self.duse[i], "dma", True))
        for e in self.ENGS:
            wl = []
            for t in toks:
                if t.eng == e and not t.dma:
                    continue
                k = id(t.sem)
                if self.seen[e].get(k, 0) >= t.val:
                    continue
                self.seen[e][k] = t.val
                wl.append((t.sem, t.val))

            def run(en, wl=wl):
                for (s, v) in wl:
                    en.wait_ge(s, v)
            self.ops[e].append(run)

    def sb(self, shape, dt=F32, name=None):
        self.nbuf += 1
        name = name or ("b%d" % self.nbuf)
        t = self.stack[-1].enter_context(self.nc.sbuf_tensor("sb_" + name, list(shape), dt))
        return Buf(t, name)

    def ps(self, name, dt=F32, cols=512):
        t = self.es.enter_context(self.nc.psum_tensor(name, [128, cols], dt))
        return Buf(t, name)

    def wrap(self, t, name):
        return Buf(t, name)

    def _need(self, eng, tok, waits, is_dma_issue):
        if tok is None:
            return
        if tok.eng == eng and not tok.dma and not is_dma_issue and eng == "pe":
            return
        k = id(tok.sem)
        if self.seen[eng].get(k, 0) >= tok.val:
            return
        cur = waits.get(k)
        if cur is None or cur[1] < tok.val:
            waits[k] = (tok.sem, tok.val)

    def emit(self, eng, fn, reads=(), writes=(), dma=False):
        waits = {}
        for b in reads:
            self._need(eng, b.w, waits, dma)
        for b in writes:
            self._need(eng, b.w, waits, dma)
            for t in b.r:
                self._need(eng, t, waits, dma)
        if dma:
            i = self.dnext
            self.dnext = (self.dnext + 1) % len(self.dsem)
            s = self.dsem[i]
            if self.duse[i] > 0:
                self._need(eng, Tok(s, 16 * self.duse[i], "dma", True), waits, True)
            self.duse[i] += 1
            tok = Tok(s, 16 * self.duse[i], "dma", True)
            inc = 16
        else:
            self.cnt[eng] += 1
            tok = Tok(self.sem[eng], self.cnt[eng], eng, False)
            inc = 1
        wl = list(waits.values())
        for (s, v) in wl:
            self.seen[eng][id(s)] = v
        tsem = tok.sem
        self.ninst += 1 + len(wl)

        def run(e, wl=wl, fn=fn, tsem=tsem, inc=inc):
            for (s, v) in wl:
                e.wait_ge(s, v)
            fn(e).then_inc(tsem, inc)

        self.ops[eng].append(run)
        for b in reads:
            b.r.append(tok)
            if len(b.r) > 48:
                b.r = b.r[-48:]
        for b in writes:
            b.w = tok
            b.r = []
        return tok

    def finish(self, out_toks):
        wl = [(t.sem, t.val) for t in out_toks]

        def run(e, wl=wl):
            for (s, v) in wl:
                e.wait_ge(s, v)

        self.ops["sp"].append(run)
        nc = self.nc
        ops = self.ops
        with nc.Block() as block:
            @block.tensor
            def _(e):
                for f in ops["pe"]:
                    f(e)

            @block.scalar
            def _(e):
                for f in ops["act"]:
                    f(e)

            @block.vector
            def _(e):
                for f in ops["dve"]:
                    f(e)

            @block.gpsimd
            def _(e):
                for f in ops["pool"]:
                    f(e)

            @block.sync
            def _(e):
                for f in ops["sp"]:
                    f(e)
        self.es.close()

    def dma(self, out, in_, reads=(), writes=(), eng="sp", **kw):
        return self.emit(eng, lambda e: e.dma_start(out=out, in_=in_, **kw), reads, writes, dma=True)

    def mm(self, out, lhsT, rhs, start, stop, reads, writes):
        return self.emit("pe", lambda e: e.matmul(out, lhsT, rhs, start=start, stop=stop), reads, writes)

    def act(self, out, in_, func, reads, writes, bias=None, scale=None):
        kw = {}
        if bias is not None:
            kw["bias"] = bias
        if scale is not None:
            kw["scale"] = scale
        return self.emit("act", lambda e: e.activation(out=out, in_=in_, func=func, **kw), reads, writes)

    def tt(self, eng, out, in0, in1, op, reads, writes):
        return self.emit(eng, lambda e: e.tensor_tensor(out=out, in0=in0, in1=in1, op=op), reads, writes)

    def ts(self, eng, out, in0, s1, s2, op0, op1, reads, writes):
        if op1 is None:
            return self.emit(eng, lambda e: e.tensor_scalar(out, in0, s1, None, op0), reads, writes)
        return self.emit(eng, lambda e: e.tensor_scalar(out, in0, s1, s2, op0, op1), reads, writes)

    def stt(self, out, in0, scalar, in1, op0, op1, reads, writes):
        return self.emit("dve", lambda e: e.scalar_tensor_tensor(out, in0, scalar, in1, op0, op1), reads, writes)

    def copy(self, eng, out, in_, reads, writes):
        if eng == "act":
            return self.emit("act", lambda e: e.activation(out=out, in_=in_, func=AF.Copy), reads, writes)
        return self.emit(eng, lambda e: e.tensor_copy(out, in_), reads, writes)

    def memset(self, eng, ap, val, writes):
        return self.emit(eng, lambda e: e.memset(ap, val), (), writes)

    def scan(self, out, d0, d1, init, reads, writes):
        return self.emit("dve", lambda e: e.tensor_tensor_scan(out, d0, d1, init, ALU.mult, ALU.add), reads, writes)

    def recip(self, out, in_, reads, writes):
        return self.emit("dve", lambda e: e.reciprocal(out, in_), reads, writes)


VEC_FIELDS = [("ng", 8), ("mu", 13), ("w0", 4), ("a0", 4), ("kk", 4), ("ka", 4), ("rk", 4), ("lnw", 4),
              ("lnb", 4), ("s5d", 4), ("glub", 4), ("cw", 16), ("cb", 4), ("ba", 4), ("bx", 4), ("lam", 4),
              ("png", 8)]
VEC_PER_LAYER = sum(n for _, n in VEC_FIELDS)
VEC_OFF = {}
_o = 0
for _n, _c in VEC_FIELDS:
    VEC_OFF[_n] = _o
    _o += _c
NVEC = 2 * VEC_PER_LAYER + 8

CST_IDENT = 0
CST_ONESBD = 128
CST_MUSN = 256
CST_MLSN = 384
CST_MUSP = 512
CST_MCI = 640
CST_SCAN = 704
NCST = 704 + TT


def vcol(l, name, i=0):
    return l * VEC_PER_LAYER + VEC_OFF[name] + i


def _pp(v, n):
    return np.ascontiguousarray(np.asarray(v, np.float32).reshape(n, 128).T)


def make_consts():
    c = np.zeros((128, NCST), np.float32)
    i = np.arange(128)[:, None]
    j = np.arange(128)[None, :]
    c[:, CST_IDENT:CST_IDENT + 128] = (i == j)
    c[:, CST_ONESBD:CST_ONESBD + 128] = ((i // 64) == (j // 64))
    c[:, CST_MUSN:CST_MUSN + 128] = -1.0 * (j > i)
    c[:, CST_MLSN:CST_MLSN + 128] = -1.0 * (i > j)
    c[:, CST_MUSP:CST_MUSP + 128] = 1.0 * (j > i)
    t = np.arange(64)[None, :]
    c[:, CST_MCI:CST_MCI + 64] = 1.0 * (t >= (i % 64))
    tt = np.arange(TT)[None, :]
    c[:, CST_SCAN:CST_SCAN + TT] = 1.0 * ((tt % LCH) != 0)
    return c


def build_nc(TC, dbg=None):
    assert TC % TT == 0
    NT = TC // TT
    dbg = dbg or set()
    nc = bass.Bass("TRN2", target_bir_lowering=False)

    def din(name, shape, dt=F32):
        return nc.dram_tensor(name, list(shape), dt, kind="ExternalInput").ap()

    xT = din("xT", [D, TC])
    pT = din("pT", [2, DPLE, TC])
    w_in = din("w_in", [2, D, DIN])
    w_out = din("w_out", [2, DMIX, D])
    ple_w = din("ple_w", [2, DPLE, D])
    ple_gw = din("ple_gw", [2, D, D])
    glu_w = din("glu_w", [2, 512, 512])
    vec_d = din("vec", [128, NVEC])
    cst_d = din("cst", [128, NCST])
    w2a2_d = din("w2a2", [2, 128, 512])
    lruw_d = din("lruw", [2, 128, 8 * 128])
    s5s_d = din("s5s", [2, 128, 3 * 16])
    s5b_d = din("s5b", [2, 128, 2 * 16 * 16])
    s5c_d = din("s5c", [2, 128, 2 * 16 * 64])
    oT = nc.dram_tensor("oT", [D, TC], F32, kind="ExternalOutput").ap()
    dbg_out = {}

    def dram_int(name, shape, dt):
        return nc.dram_tensor(name, list(shape), dt, kind="Internal").ap()

    w_in_b = dram_int("w_in_b", [2, D, DIN], BF16)
    w_out_b = dram_int("w_out_b", [2, DMIX, D], BF16)
    ple_w_b = dram_int("ple_w_b", [2, DPLE, D], BF16)
    ple_gw_b = dram_int("ple_gw_b", [2, D, D], BF16)
    glu_w_b = dram_int("glu_w_b", [2, 512, 512], BF16)
    s5tab_d = dram_int("s5tab", [2, 128, 2 * 16 * TS5], F32)

    P = Prog(nc)
    wsb = P.wrap(None, "wscratch")
    tabsb = P.wrap(None, "s5tabscr")

    def dbg_dump(name, buf, ap, shape, dt=F32):
        if name not in dbg:
            return
        o = nc.dram_tensor("dbg_" + name, list(shape), dt, kind="ExternalOutput").ap()
        dbg_out[name] = P.dma(o, ap, reads=[buf])

    vec = P.sb([128, NVEC], F32, "vec")
    cst = P.sb([128, NCST], F32, "cst")
    P.dma(vec[:], vec_d, writes=[vec])
    P.dma(cst[:], cst_d, writes=[cst])
    cstb = P.sb([128, NCST], BF16, "cstb")
    P.copy("dve", cstb[:], cst[:], [cst], [cstb])
    ident_f = cst[:, CST_IDENT:CST_IDENT + 128]
    ident_b = cstb[:, CST_IDENT:CST_IDENT + 128]
    onesbd_f = cst[:, CST_ONESBD:CST_ONESBD + 128]
    ones_f = P.sb([128, 128], F32, "ones_f")
    P.memset("pool", ones_f[:], 1.0, [ones_f])
    one_t = P.sb([128, 1], F32, "one_t")
    P.memset("pool", one_t[:], 1.0, [one_t])

    def V(l, name, i=0, n=1):
        c = vcol(l, name, i)
        return vec[:, c:c + n]

    for l in range(2):
        for (src, dst, rows) in ((w_in, w_in_b, D), (w_out, w_out_b, DMIX), (ple_w, ple_w_b, DPLE),
                                 (ple_gw, ple_gw_b, D), (glu_w, glu_w_b, 512)):
            for r0 in range(0, rows, 128):
                P.dma(dst[l, r0:r0 + 128, :], src[l, r0:r0 + 128, :], writes=[wsb], eng="pool",
                      max_dma_last_dim=4096)

    w2a2 = []
    lruw = []
    for l in range(2):
        w2a2.append(P.sb([128, 512], BF16, "w2a2_%d" % l))
        lruw.append(P.sb([128, 1024], BF16, "lruw_%d" % l))
    lru_c = P.sb([128, 2, 8], F32, "lru_c")
    s5B = [P.sb([128, 2 * 4 * 2 * 128], BF16, "s5B%d" % l) for l in range(2)]
    s5C = [P.sb([128, 2048], BF16, "s5C%d" % l) for l in range(2)]
    s5keep = [P.sb([128, 3, 16], F32, "s5keep%d" % l) for l in range(2)]
    s5rotb = [P.sb([128, 2, 16], F32, "s5rot%d" % l) for l in range(2)]
    ps_misc = P.ps("ps7")
    P.push()
    stage = P.sb([128, 1024], F32, "stage")
    for l in range(2):
        P.dma(stage[:, 0:512], w2a2_d[l], writes=[stage])
        P.copy("act", w2a2[l][:], stage[:, 0:512], [stage], [w2a2[l]])
        P.dma(stage[:], lruw_d[l], writes=[stage])
        P.copy("act", lruw[l][:], stage[:], [stage], [lruw[l]])

    for l in range(2):
        tmp = P.sb([128, 4], F32, "lrutmp%d" % l)
        P.act(tmp[:], V(l, "lam", 0, 4), AF.Exp, [vec], [tmp], scale=-1.0)
        P.act(tmp[:], tmp[:], AF.Ln, [tmp, one_t], [tmp], bias=one_t[:, 0:1])
        P.ts("dve", lru_c[:, l, 0:4], tmp[:], -8.0, None, ALU.mult, None, [tmp], [lru_c])
        P.ts("dve", lru_c[:, l, 4:8], tmp[:], -16.0, None, ALU.mult, None, [tmp], [lru_c])

    s5rot = []
    for l in range(2):
        s5s = P.sb([128, 48], F32, "s5s%d" % l)
        P.dma(s5s[:], s5s_d[l], writes=[s5s])
        a_re = s5s[:, 0:16]
        a_im = s5s[:, 16:32]
        ldt = s5s[:, 32:48]
        w = P.sb([128, 16, 16], F32, "s5w%d" % l)
        R = [w]

        def row(i):
            return w[:, i, :]
        dt_, rho, th, cc, ss, t1, t2, lr, li, den, qre, qim, nr = [row(i) for i in range(13)]
        P.act(dt_, ldt, AF.Exp, [s5s], R)
        P.tt("dve", rho, a_re, dt_, ALU.mult, [s5s] + R, R)
        P.act(rho, rho, AF.Exp, R, R)
        P.tt("dve", th, a_im, dt_, ALU.mult, [s5s] + R, R)
        hp = P.sb([128, 1], F32, "halfpi%d" % l)
        P.memset("dve", hp[:], math.pi / 2, [hp])
        P.act(cc, th, AF.Sin, R + [hp], R, bias=hp[:, 0:1], scale=1.0 / 16)
        P.act(ss, th, AF.Sin, R, R, scale=1.0 / 16)

        def csq(c_, s_):
            P.tt("dve", t1, c_, c_, ALU.mult, R, R)
            P.tt("dve", t2, s_, s_, ALU.mult, R, R)
            P.stt(s_, c_, 2.0, s_, ALU.mult, ALU.mult, R, R)
            P.tt("dve", c_, t1, t2, ALU.subtract, R, R)
        for _ in range(4):
            csq(cc, ss)
        P.tt("dve", lr, rho, cc, ALU.mult, R, R)
        P.tt("dve", li, rho, ss, ALU.mult, R, R)
        P.tt("dve", t1, a_re, a_re, ALU.mult, [s5s] + R, R)
        P.tt("dve", t2, a_im, a_im, ALU.mult, [s5s] + R, R)
        P.tt("dve", den, t1, t2, ALU.add, R, R)
        P.recip(den, den, R, R)
        P.ts("dve", nr, lr, -1.0, None, ALU.add, None, R, R)
        P.tt("dve", t1, nr, a_re, ALU.mult, [s5s] + R, R)
        P.tt("dve", t2, li, a_im, ALU.mult, [s5s] + R, R)
        P.tt("dve", t1, t1, t2, ALU.add, R, R)
        P.tt("dve", qre, t1, den, ALU.mult, R, R)
        P.tt("dve", t1, li, a_re, ALU.mult, [s5s] + R, R)
        P.tt("dve", t2, nr, a_im, ALU.mult, [s5s] + R, R)
        P.tt("dve", t1, t1, t2, ALU.subtract, R, R)
        P.tt("dve", qim, t1, den, ALU.mult, R, R)
        keep = s5keep[l]
        P.copy("dve", keep[:, 0, :], rho, R, [keep])
        P.copy("dve", keep[:, 1, :], cc, R, [keep])
        P.copy("dve", keep[:, 2, :], ss, R, [keep])

        sbf = P.sb([128, 512], F32, "s5b_in%d" % l)
        P.dma(sbf[:], s5b_d[l], writes=[sbf])
        bre = sbf[:, 0:256].rearrange("p (j h) -> p j h", h=16)
        bim = sbf[:, 256:512].rearrange("p (j h) -> p j h", h=16)
        Bt = s5B[l]
        Btv = Bt[:, :].rearrange("p (r b q m) -> p r b q m", r=2, b=4, q=2)
        bpad = P.sb([128, 2, 128], BF16, "s5bpad%d" % l)
        tb = P.sb([128, 2, 16], F32, "s5tb%d" % l)
        for j in range(16):
            P.ts("dve", tb[:, 0, :], bim[:, j, :], qim[:, j:j + 1], None, ALU.mult, None, [sbf] + R, [tb])
            P.stt(tb[:, 0, :], bre[:, j, :], qre[:, j:j + 1], tb[:, 0, :], ALU.mult, ALU.subtract, [sbf, tb] + R, [tb])
            P.ts("dve", tb[:, 1, :], bre[:, j, :], qim[:, j:j + 1], None, ALU.mult, None, [sbf] + R, [tb])
            P.stt(tb[:, 1, :], bim[:, j, :], qre[:, j:j + 1], tb[:, 1, :], ALU.mult, ALU.add, [sbf, tb] + R, [tb])
            P.memset("pool", bpad[:], 0.0, [bpad])
            for gh in range(2):
                col0 = 32 * (j % 4) + gh * 16
                for ri in range(2):
                    P.copy("pool", bpad[gh * 64:(gh + 1) * 64, ri, col0:col0 + 16],
                           tb[gh * 64:(gh + 1) * 64, ri, :], [tb], [bpad])
            for ri in range(2):
                P.mm(ps_misc[:, ri * 128:(ri + 1) * 128], bpad[:, ri, :], ident_b, True, True, [bpad, cstb], [ps_misc])
            hf = (j % 4) // 2
            for ri in range(2):
                P.copy("act", Btv[64 * hf:64 * hf + 64, ri, j // 4, j % 2, :],
                       ps_misc[64 * hf:64 * hf + 64, ri * 128:(ri + 1) * 128], [ps_misc], [Bt])

        scf = P.sb([128, 2048], F32, "s5c_in%d" % l)
        P.dma(scf[:], s5c_d[l], writes=[scf])
        Ct = s5C[l]
        P.copy("act", Ct[:, 0:1024], scf[:, 0:1024], [scf], [Ct])
        P.ts("dve", Ct[:, 1024:2048], scf[:, 1024:2048], -1.0, None, ALU.mult, None, [scf], [Ct])

        tab = P.sb([128, 2, 16, TS5], F32, "s5tabb%d" % l)
        P.memset("pool", tab[:, 0, :, 0:1], 1.0, [tab])
        P.memset("pool", tab[:, 1, :, 0:1], 0.0, [tab])
        ec = P.sb([128, 2, 16], F32, "s5ec%d" % l)
        P.copy("dve", ec[:, 0, :], cc, R, [ec])
        P.copy("dve", ec[:, 1, :], ss, R, [ec])
        m = 1
        while m < TS5:
            for j in range(16):
                cj = ec[:, 0, j:j + 1]
                sj = ec[:, 1, j:j + 1]
                src_c = tab[:, 0, j, 0:m]
                src_s = tab[:, 1, j, 0:m]
                dst_c = tab[:, 0, j, m:2 * m]
                dst_s = tab[:, 1, j, m:2 * m]
                P.ts("dve", dst_c, src_s, sj, None, ALU.mult, None, [tab, ec], [tab])
                P.stt(dst_c, src_c, cj, dst_c, ALU.mult, ALU.subtract, [tab, ec], [tab])
                P.ts("dve", dst_s, src_c, sj, None, ALU.mult, None, [tab, ec], [tab])
                P.stt(dst_s, src_s, cj, dst_s, ALU.mult, ALU.add, [tab, ec], [tab])
            e_c = ec[:, 0, :]
            e_s = ec[:, 1, :]
            P.tt("dve", t1, e_c, e_c, ALU.mult, [ec] + R, R)
            P.tt("dve", t2, e_s, e_s, ALU.mult, [ec] + R, R)
            P.stt(e_s, e_c, 2.0, e_s, ALU.mult, ALU.mult, [ec], [ec])
            P.tt("dve", e_c, t1, t2, ALU.subtract, R, [ec])
            m *= 2
        rot = s5rotb[l]
        P.copy("dve", rot[:], ec[:], [ec], [rot])
        s5rot.append((rot, keep))
        P.dma(s5tab_d[l], tab[:, :, :, :].rearrange("p a j t -> p (a j t)"), reads=[tab], writes=[tabsb])
        dbg_dump("s5w%d" % l, w, w[:, :, :].rearrange("p a b -> p (a b)"), [128, 256])
        dbg_dump("s5keep%d" % l, keep, keep[:, :, :].rearrange("p a b -> p (a b)"), [128, 48])
        dbg_dump("s5tab%d" % l, tab, tab[:, :, :, :].rearrange("p a j t -> p (a j t)"), [128, 2 * 16 * TS5])
        dbg_dump("s5B%d" % l, Bt, Bt[:, :], [128, 2048], BF16)
    P.pop()

    hT = P.sb([128, 8, TT], F32, "hT")
    xn = P.sb([128, 8, TT], BF16, "xn")
    zst = [P.sb([128, 1 + TT], F32, "zst%d" % i) for i in range(2)]
    zs = P.sb([128, 13, TT], F32, "zs")
    zg = P.sb([128, 4, 3 + TT], F32, "zg")
    ycat = P.sb([128, 12, TT], BF16, "ycat")
    pin = P.sb([128, 2, TT], F32, "pin")
    pbf = P.sb([128, 2, TT], BF16, "pbf")
    ring = [P.sb([128, 4096], BF16, "ring%d" % i) for i in range(3)]
    ringi = [0]
    s5tab = P.sb([128, 2, 16, TS5], F32, "s5tab")
    banks = [P.ps("ps%d" % i) for i in range(7)] + [ps_misc]
    rot_i = {"proj": 0, "rw": 0}

    def bank(group):
        ids = (0, 1) if group == "proj" else (2, 3)
        i = rot_i[group]
        rot_i[group] = (i + 1) % len(ids)
        return banks[ids[i]]

    def next_ring():
        r = ring[ringi[0]]
        ringi[0] = (ringi[0] + 1) % 3
        return r

    cz = [P.sb([128, 13], F32, "cz%d" % l) for l in range(2)]
    cl = [P.sb([128, 4, 3], F32, "cl%d" % l) for l in range(2)]
    ch = [P.sb([128, 4], F32, "ch%d" % l) for l in range(2)]
    s5z = [P.sb([128, 2, 16], F32, "s5z%d" % l) for l in range(2)]
    s5zl = P.sb([128, 2, 16], F32, "s5zl")
    Tst = [[P.sb([128, 128], BF16, "T%d_%d" % (l, pb)) for pb in range(4)] for l in range(2)]
    for l in range(2):
        P.memset("pool", cz[l][:], 0.0, [cz[l]])
        P.memset("pool", cl[l][:], 0.0, [cl[l]])
        P.memset("pool", ch[l][:], 0.0, [ch[l]])
        P.memset("pool", s5z[l][:], 0.0, [s5z[l]])
        for pb in range(4):
            P.memset("pool", Tst[l][pb][:], 0.0, [Tst[l][pb]])

    NF = 18
    fs = [P.sb([128, TT], F32, "rf%d" % i) for i in range(NF)]
    pad_names = ["RTp", "KTp", "CTp", "BTp", "VTp", "KGp", "BGp"]
    pads = {n: P.sb([128, NCH * 128], BF16, n) for n in pad_names}
    for n in pad_names:
        P.memset("pool", pads[n][:], 0.0, [pads[n]])
    RTc = P.sb([128, TT], BF16, "RTc")
    tanh_wd = P.sb([128, TT], BF16, "tanhwd")
    Blev = [P.sb([128, NCH * 128], BF16, "Blev%d" % i) for i in range(2)]
    BTlev = [P.sb([128, NCH * 128], BF16, "BTlev%d" % i) for i in range(2)]
    AkkT = P.sb([128, NCH * 128], BF16, "AkkT")
    X32 = P.sb([128, NCH * 256], F32, "X32")
    Xbf = P.sb([128, NCH * 256], BF16, "Xbf")
    NPI = 2
    pp = []
    for i in range(NPI):
        d = {}
        for n in ("Vbd", "KGbd", "BGbd", "PT", "nU0", "Wbd"):
            d[n] = P.sb([128, NCH * 128], BF16, "%s_%d" % (n, i))
        for n in ("Rhat", "ArkT", "ArbT"):
            d[n] = P.sb([128, TT], BF16, "%s_%d" % (n, i))
        d["bonus"] = P.sb([128, TT], F32, "bonus_%d" % i)
        d["GL"] = P.sb([128, NCH], F32, "GL_%d" % i)
        d["rt32"] = P.sb([128, TT], F32, "rt32_%d" % i)
        pp.append(d)
    mix = P.sb([128, 4, TT], F32, "mix")

    s5f = [P.sb([128, TS5], F32, "s5f%d" % i) for i in range(9)]
    s5x = P.sb([128, 2, TS5], BF16, "s5x")
    ubf = P.sb([128, 4, TT], BF16, "ubf")
    zgb = P.sb([128, 4, TT], BF16, "zgb")
    lb = P.sb([128, TT], BF16, "lb")
    rstd = P.sb([128, TT], F32, "rstd")
    sq = P.sb([128, 8, TT], F32, "sq")

    def mask_tile(col, n):
        return cstb[:, col:col + n]

    out_toks = []
    eps_t = {}
    for e_ in (NORM_EPS, GN_EPS):
        t = P.sb([128, 1], F32, "eps%d" % len(eps_t))
        P.memset("pool", t[:], e_, [t])
        eps_t[e_] = t
    neg_half = -math.exp(-0.5)

    def rms_rstd(src_buf, src_ap_fn, nblk, eps):
        for k in range(nblk):
            P.act(sq[:, k, :], src_ap_fn(k), AF.Square, [src_buf], [sq])
        b = bank("proj")
        for k in range(nblk):
            P.mm(b[:, 0:TT], ones_f[:], sq[:, k, :], k == 0, k == nblk - 1, [ones_f, sq], [b])
        P.act(rstd[:], b[:, 0:TT], AF.Sqrt, [b, eps_t[eps]], [rstd], bias=eps_t[eps][:, 0:1], scale=1.0 / (nblk * 128))
        P.recip(rstd[:], rstd[:], [rstd], [rstd])

    for it in range(NT):
        t0 = it * TT
        first = (it == 0)
        P.dma(hT[:], xT[:, t0:t0 + TT].rearrange("(k p) t -> p k t", p=128), writes=[hT])
        for l in range(2):
            dbg_on = first and l == 0
            P.dma(pin[:], pT[l, :, t0:t0 + TT].rearrange("(k p) t -> p k t", p=128), writes=[pin])
            P.copy("act", pbf[:], pin[:], [pin], [pbf])
            P.dma(s5tab[:, :, :, :].rearrange("p a j t -> p (a j t)"), s5tab_d[l], reads=[tabsb], writes=[s5tab])
            rms_rstd(hT, lambda k: hT[:, k, :], 8, NORM_EPS)
            for k in range(8):
                P.stt(xn[:, k, :], hT[:, k, :], V(l, "ng", k), rstd[:], ALU.mult, ALU.mult, [hT, vec, rstd], [xn])
            if dbg_on:
                dbg_dump("xn", xn, xn[:, :, :].rearrange("p a t -> p (a t)"), [128, 8 * TT], BF16)

            wchunk = {}

            def in_block(cb, l=l, wchunk=wchunk):
                ci = cb // 4
                if ci not in wchunk:
                    r = next_ring()
                    ncol = 512 if ci < 8 else 128
                    P.dma(r[:, 0:8 * ncol].rearrange("p (k n) -> p k n", k=8),
                          w_in_b[l].rearrange("(k p) n -> p k n", p=128)[:, :, ci * 512:ci * 512 + ncol],
                          reads=[wsb], writes=[r])
                    wchunk[ci] = (r, ncol)
                r, ncol = wchunk[ci]
                rv = r[:, 0:8 * ncol].rearrange("p (k n) -> p k n", k=8)
                c0 = (cb % 4) * 128
                b = bank("proj")
                for k in range(8):
                    P.mm(b[:, 0:TT], rv[:, k, c0:c0 + 128], xn[:, k, :], k == 0, k == 7, [r, xn], [b])
                return b

            for cb in range(13):
                st = zst[cb % 2]
                b = in_block(cb)
                P.copy("act", st[:, 0:1], cz[l][:, cb:cb + 1], [cz[l]], [st])
                P.copy("act", st[:, 1:1 + TT], b[:, 0:TT], [b], [st])
                P.copy("act", cz[l][:, cb:cb + 1], st[:, TT:TT + 1], [st], [cz[l]])
                d_ = fs[0]
                P.tt("pool", d_[:], st[:, 0:TT], st[:, 1:1 + TT], ALU.subtract, [st], [d_])
                P.stt(zs[:, cb, :], d_[:], V(l, "mu", cb), st[:, 1:1 + TT], ALU.mult, ALU.add, [d_, vec, st], [zs])
            if dbg_on:
                dbg_dump("zs", zs, zs[:, :, :].rearrange("p a t -> p (a t)"), [128, 13 * TT])

            P.act(tanh_wd[0:64, :], zs[0:64, 12, :], AF.Tanh, [zs], [tanh_wd])
            P.copy("act", tanh_wd[64:128, :], zs[64:128, 12, :], [zs], [tanh_wd])

            def prep(pb, inst, l=l, dbg_on=dbg_on):
                d = pp[inst]
                r_ = zs[:, pb, :]
                k_ = zs[:, 4 + pb, :]
                v_ = zs[:, 8 + pb, :]
                (sg, ld, a_, kk_, kk2, sqk, kap, t1, kp, b_, lg, eg, ieg, eg1, dl, egl, rk) = fs[:17]
                rt32 = d["rt32"]
                cols = slice(pb * 128, (pb + 1) * 128)
                bw = bank("rw")
                P.mm(bw[:, 0:TT], w2a2[l][0:64, cols], tanh_wd[0:64, :], True, True, [w2a2[l], tanh_wd], [bw])
                P.act(sg[:], bw[:, 0:TT], AF.Sigmoid, [bw, vec], [sg], bias=V(l, "w0", pb))
                P.ts("pool", ld[:], sg[:], neg_half, None, ALU.mult, None, [sg], [ld])
                ba_ = bank("rw")
                P.mm(ba_[:, 0:TT], w2a2[l][64:128, cols], tanh_wd[64:128, :], True, True, [w2a2[l], tanh_wd], [ba_])
                P.act(a_[:], ba_[:, 0:TT], AF.Sigmoid, [ba_, vec], [a_], bias=V(l, "a0", pb))
                P.ts("dve", kk_[:], k_, V(l, "kk", pb), None, ALU.mult, None, [zs, vec], [kk_])
                P.tt("pool", kk2[:], kk_[:], kk_[:], ALU.mult, [kk_], [kk2])
                bs = bank("rw")
                P.mm(bs[:, 0:TT], onesbd_f, kk2[:], True, True, [cst, kk2], [bs])
                P.act(sqk[:], bs[:, 0:TT], AF.Sqrt, [bs], [sqk])
                P.ts("dve", sqk[:], sqk[:], 1e-12, None, ALU.max, None, [sqk], [sqk])
                P.recip(sqk[:], sqk[:], [sqk], [sqk])
                P.tt("pool", kap[:], kk_[:], sqk[:], ALU.mult, [kk_, sqk], [kap])
                P.ts("dve", t1[:], a_[:], -1.0, V(l, "ka", pb), ALU.add, ALU.mult, [a_, vec], [t1])
                P.stt(kp[:], t1[:], 1.0, k_, ALU.add, ALU.mult, [t1, zs], [kp])
                P.tt("pool", b_[:], kap[:], a_[:], ALU.mult, [kap, a_], [b_])
                P.scan(lg[:], cst[:, CST_SCAN:CST_SCAN + TT], ld[:], 0.0, [cst, ld], [lg])
                P.act(eg[:], lg[:], AF.Exp, [lg], [eg])
                P.act(ieg[:], lg[:], AF.Exp, [lg], [ieg], scale=-1.0)
                P.tt("pool", eg1[:], lg[:], ld[:], ALU.subtract, [lg, ld], [eg1])
                P.act(eg1[:], eg1[:], AF.Exp, [eg1], [eg1])
                for c in range(NCH):
                    cs = slice(c * LCH, (c + 1) * LCH)
                    P.ts("dve", dl[:, cs], lg[:, cs], -1.0, lg[:, c * LCH + LCH - 1:c * LCH + LCH], ALU.mult, ALU.add,
                         [lg], [dl])
                P.act(egl[:], dl[:], AF.Exp, [dl], [egl])
                P.copy("act", d["GL"][:, :], eg[:, :].rearrange("p (c t) -> p c t", t=LCH)[:, :, LCH - 1], [eg], [d["GL"]])
                P.tt("dve", rt32[:], r_, eg[:], ALU.mult, [zs, eg], [rt32])
                P.copy("act", RTc[:], rt32[:], [rt32], [RTc])

                def padw(name, eng, in0, in1, rd):
                    t = pads[name]
                    tv = t[:, :].rearrange("p (c h t) -> p c h t", c=NCH, h=2)
                    for hh in range(2):
                        ps_ = slice(hh * 64, (hh + 1) * 64)
                        o = tv[ps_, :, hh, :]
                        i0 = in0[ps_, :].rearrange("p (c t) -> p c t", t=LCH)
                        if in1 is None:
                            P.copy(eng, o, i0, rd, [t])
                        else:
                            i1 = in1[ps_, :].rearrange("p (c t) -> p c t", t=LCH)
                            P.tt(eng, o, i0, i1, ALU.mult, rd, [t])
                padw("RTp", "act", rt32, None, [rt32])
                padw("KTp", "dve", kp, ieg, [kp, ieg])
                padw("CTp", "pool", kap, eg1, [kap, eg1])
                padw("BTp", "dve", b_, ieg, [b_, ieg])
                padw("VTp", "act", zs[:, 8 + pb, :], None, [zs])
                padw("KGp", "pool", kp, egl, [kp, egl])
                padw("BGp", "dve", b_, egl, [b_, egl])
                P.stt(rk[:], r_, V(l, "rk", pb), kp[:], ALU.mult, ALU.mult, [zs, vec, kp], [rk])
                bb = bank("rw")
                P.mm(bb[:, 0:TT], onesbd_f, rk[:], True, True, [cst, rk], [bb])
                P.tt("dve", d["bonus"][:], bb[:, 0:TT], v_, ALU.mult, [bb, zs], [d["bonus"]])
                if dbg_on and pb == 0:
                    dbg_dump("lg", lg, lg[:], [128, TT])
                    dbg_dump("kap", kap, kap[:], [128, TT])
                    dbg_dump("kp", kp, kp[:], [128, TT])
                    dbg_dump("a", a_, a_[:], [128, TT])

                def chunkmm(dst_bank, lname, rname, rbuf=None, rcols=128):
                    lt = pads[lname]
                    for c in range(NCH):
                        if rbuf is None:
                            rb_ = pads[rname]
                            rap = rb_[:, c * 128:(c + 1) * 128]
                        else:
                            rb_ = rbuf
                            rap = rbuf[:, c * rcols:(c + 1) * rcols]
                        P.mm(dst_bank[:, c * rcols:(c + 1) * rcols], lt[:, c * 128:(c + 1) * 128], rap, True, True,
                             [lt, rb_], [dst_bank])

                def masked(dst, src_bank, mcol, w):
                    for c in range(NCH):
                        cs = slice(c * w, (c + 1) * w)
                        P.tt("dve", dst[:, cs], src_bank[:, cs], mask_tile(mcol, w), ALU.mult, [src_bank, cstb], [dst])
                b1 = bank("rw")
                chunkmm(b1, "BTp", "CTp")
                masked(BTlev[0], b1, CST_MUSN, 128)
                b2 = bank("rw")
                chunkmm(b2, "CTp", "BTp")
                masked(Blev[0], b2, CST_MLSN, 128)
                b3 = bank("rw")
                chunkmm(b3, "KTp", "CTp")
                masked(AkkT, b3, CST_MUSP, 128)
                b4 = bank("rw")
                chunkmm(b4, "KTp", None, RTc, LCH)
                masked(d["ArkT"], b4, CST_MCI, LCH)
                b5 = bank("rw")
                chunkmm(b5, "BTp", None, RTc, LCH)
                masked(d["ArbT"], b5, CST_MCI, LCH)

            def tokmajor(src_name, dst_buf, dst_ap_fn, eng):
                bt_ = bank("rw")
                lt = pads[src_name]
                for c in range(NCH):
                    P.mm(bt_[:, c * 128:(c + 1) * 128], lt[:, c * 128:(c + 1) * 128], ident_b, True, True,
                         [lt, cstb], [bt_])
                for c in range(NCH):
                    P.copy(eng, dst_ap_fn(c), bt_[:, c * 128:(c + 1) * 128], [bt_], [dst_buf])

            def solve(pb, inst, l=l):
                d = pp[inst]
                X32v = X32[:, :].rearrange("p (c n) -> p c n", n=256)
                Xbfv = Xbf[:, :].rearrange("p (c n) -> p c n", n=256)
                tokmajor("VTp", d["Vbd"], lambda c: d["Vbd"][:, c * 128:(c + 1) * 128], "act")
                tokmajor("KGp", d["KGbd"], lambda c: d["KGbd"][:, c * 128:(c + 1) * 128], "act")
                tokmajor("BGp", d["BGbd"], lambda c: d["BGbd"][:, c * 128:(c + 1) * 128], "act")
                tokmajor("CTp", X32, lambda c: X32v[:, c, 0:128], "act")
                bt_ = bank("rw")
                for c in range(NCH):
                    cs = slice(c * 128, (c + 1) * 128)
                    P.mm(bt_[:, cs], AkkT[:, cs], d["Vbd"][:, cs], True, True, [AkkT, d["Vbd"]], [bt_])
                for c in range(NCH):
                    P.copy("act", X32v[:, c, 128:256], bt_[:, c * 128:(c + 1) * 128], [bt_], [X32])
                P.copy("pool", Xbf[:], X32[:], [X32], [Xbf])
                cur = 0
                for lev in range(6):
                    for half in range(2):
                        bx_ = banks[4 + half]
                        for cc_ in range(2):
                            c = half * 2 + cc_
                            P.mm(bx_[:, cc_ * 256:(cc_ + 1) * 256], BTlev[cur][:, c * 128:(c + 1) * 128], Xbfv[:, c, :],
                                 True, True, [BTlev[cur], Xbf], [bx_])
                        xs_ = X32[:, half * 512:(half + 1) * 512]
                        P.tt("dve", xs_, xs_, bx_[:, 0:512], ALU.add, [X32, bx_], [X32])
                    P.copy("pool", Xbf[:], X32[:], [X32], [Xbf])
                    if lev < 5:
                        nxt = 1 - cur
                        bq = bank("rw")
                        for c in range(NCH):
                            cs = slice(c * 128, (c + 1) * 128)
                            P.mm(bq[:, cs], Blev[cur][:, cs], BTlev[cur][:, cs], True, True, [Blev[cur], BTlev[cur]], [bq])
                        if lev < 4:
                            bq2 = bank("rw")
                            for c in range(NCH):
                                cs = slice(c * 128, (c + 1) * 128)
                                P.mm(bq2[:, cs], BTlev[cur][:, cs], Blev[cur][:, cs], True, True,
                                     [Blev[cur], BTlev[cur]], [bq2])
                            P.copy("act", Blev[nxt][:], bq2[:, 0:512], [bq2], [Blev[nxt]])
                        P.copy("act", BTlev[nxt][:], bq[:, 0:512], [bq], [BTlev[nxt]])
                        cur = nxt
                nU0v = d["nU0"][:, :].rearrange("p (c n) -> p c n", n=128)
                Wbdv = d["Wbd"][:, :].rearrange("p (c n) -> p c n", n=128)
                P.ts("pool", nU0v, X32v[:, :, 128:256], -1.0, None, ALU.mult, None, [X32], [d["nU0"]])
                P.copy("pool", Wbdv, X32v[:, :, 0:128], [X32], [d["Wbd"]])
                br = bank("rw")
                for c in range(NCH):
                    P.mm(br[:, c * LCH:(c + 1) * LCH], d["Wbd"][:, c * 128:(c + 1) * 128],
                         d["ArbT"][:, c * LCH:(c + 1) * LCH], True, True, [d["Wbd"], d["ArbT"]], [br])
                P.tt("dve", d["Rhat"][:], d["rt32"][:], br[:, 0:TT], ALU.subtract, [d["rt32"], br], [d["Rhat"]])
                bp = bank("rw")
                for c in range(NCH):
                    cs = slice(c * 128, (c + 1) * 128)
                    P.mm(bp[:, cs], d["Wbd"][:, cs], d["BGbd"][:, cs], True, True, [d["Wbd"], d["BGbd"]], [bp])
                for c in range(NCH):
                    cs = slice(c * 128, (c + 1) * 128)
                    P.stt(d["PT"][:, cs], ident_f, d["GL"][:, c:c + 1], bp[:, cs], ALU.mult, ALU.subtract,
                          [cst, d["GL"], bp], [d["PT"]])

            def seq(pb, inst, c, l=l):
                d = pp[inst]
                T = Tst[l][pb]
                yb = banks[6]
                tb_ = banks[4 + inst]
                ycols = slice(inst * TT + c * LCH, inst * TT + (c + 1) * LCH)
                cs = slice(c * 128, (c + 1) * 128)
                cl_ = slice(c * LCH, (c + 1) * LCH)
                P.mm(yb[:, ycols], T[:], d["Rhat"][:, cl_], True, False, [T, d["Rhat"]], [yb])
                P.mm(yb[:, ycols], d["Vbd"][:, cs], d["ArkT"][:, cl_], False, False, [d["Vbd"], d["ArkT"]], [yb])
                P.mm(yb[:, ycols], d["nU0"][:, cs], d["ArbT"][:, cl_], False, True, [d["nU0"], d["ArbT"]], [yb])
                P.mm(tb_[:, 0:128], d["PT"][:, cs], T[:], True, False, [d["PT"], T], [tb_])
                P.mm(tb_[:, 0:128], d["KGbd"][:, cs], d["Vbd"][:, cs], False, False, [d["KGbd"], d["Vbd"]], [tb_])
                P.mm(tb_[:, 0:128], d["BGbd"][:, cs], d["nU0"][:, cs], False, True, [d["BGbd"], d["nU0"]], [tb_])
                P.copy("act", T[:], tb_[:, 0:128], [tb_], [T])

            def fin(pb, inst, l=l, dbg_on=dbg_on):
                d = pp[inst]
                yb = banks[6]
                y32, yc, ysq, rs = fs[0], fs[1], fs[2], fs[3]
                P.copy("act", y32[:], yb[:, inst * TT:(inst + 1) * TT], [yb], [y32])
                if dbg_on:
                    dbg_dump("y_rw%d" % pb, y32, y32[:], [128, TT])
                bm = bank("rw")
                P.mm(bm[:, 0:TT], onesbd_f, y32[:], True, True, [cst, y32], [bm])
                P.stt(yc[:], bm[:, 0:TT], -1.0 / 64, y32[:], ALU.mult, ALU.add, [bm, y32], [yc])
                P.act(ysq[:], yc[:], AF.Square, [yc], [ysq])
                bv = bank("rw")
                P.mm(bv[:, 0:TT], onesbd_f, ysq[:], True, True, [cst, ysq], [bv])
                P.act(rs[:], bv[:, 0:TT], AF.Sqrt, [bv, eps_t[GN_EPS]], [rs], bias=eps_t[GN_EPS][:, 0:1], scale=1.0 / 64)
                P.recip(rs[:], rs[:], [rs], [rs])
                P.tt("dve", yc[:], yc[:], rs[:], ALU.mult, [yc, rs], [yc])
                P.ts("dve", yc[:], yc[:], V(l, "lnw", pb), V(l, "lnb", pb), ALU.mult, ALU.add, [yc, vec], [yc])
                P.tt("pool", mix[:, pb, :], yc[:], d["bonus"][:], ALU.add, [yc, d["bonus"]], [mix])

            for half in range(2):
                pbs = (2 * half, 2 * half + 1)
                for inst, pb in enumerate(pbs):
                    prep(pb, inst)
                    solve(pb, inst)
                for c in range(NCH):
                    for inst, pb in enumerate(pbs):
                        seq(pb, inst, c)
                for inst, pb in enumerate(pbs):
                    fin(pb, inst)

            def gate_group(cb0, ybase, l=l):
                for blk in range(4):
                    b = in_block(cb0 + blk)
                    sl = fs[4 + (blk % 2)]
                    P.act(sl[:], b[:, 0:TT], AF.Silu, [b], [sl])
                    P.tt("dve", ycat[:, ybase + blk, :], mix[:, blk, :], sl[:], ALU.mult, [mix, sl], [ycat])

            gate_group(13, 0)
            if dbg_on:
                dbg_dump("ycat_rw", ycat, ycat[:, 0:4, :].rearrange("p a t -> p (a t)"), [128, 4 * TT], BF16)

            for blk in range(4):
                b = in_block(17 + blk)
                P.copy("act", zg[:, blk, 3:3 + TT], b[:, 0:TT], [b], [zg])
            P.copy("pool", ubf[:], zg[:, :, 3:3 + TT], [zg], [ubf])
            Bv = s5B[l][:, :].rearrange("p (r b q m) -> p r b q m", r=2, b=4, q=2)
            Cv = s5C[l][:, :].rearrange("p (r j m) -> p r j m", r=2, j=16)
            rotk, keep = s5rot[l]
            NS = TT // TS5
            yb5 = banks[7]
            for blk in range(4):
                for s in range(NS):
                    tsl = slice(s * TS5, (s + 1) * TS5)
                    for jj in range(4):
                        j = blk * 4 + jj
                        hf, jh = jj // 2, jj % 2
                        hs = slice(64 * hf, 64 * hf + 64)
                        bu = bank("rw")
                        P.mm(bu[:, 0:TS5], Bv[hs, 0, blk, jh, :], ubf[hs, blk, tsl], True, True,
                             [s5B[l], ubf], [bu])
                        P.mm(bu[:, TS5:2 * TS5], Bv[hs, 1, blk, jh, :], ubf[hs, blk, tsl], True, True,
                             [s5B[l], ubf], [bu])
                        cosT = s5tab[:, 0, j, :]
                        sinT = s5tab[:, 1, j, :]
                        (t1, t2, bzr, bzi, zr, zi, rho_t, t3, t4) = s5f
                        bre = bu[:, 0:TS5]
                        bim = bu[:, TS5:2 * TS5]
                        P.tt("dve", t1[:], bre, cosT, ALU.mult, [bu, s5tab], [t1])
                        P.tt("dve", t2[:], bim, sinT, ALU.mult, [bu, s5tab], [t2])
                        P.tt("pool", bzr[:], t1[:], t2[:], ALU.add, [t1, t2], [bzr])
                        P.tt("dve", t3[:], bim, cosT, ALU.mult, [bu, s5tab], [t3])
                        P.tt("dve", t4[:], bre, sinT, ALU.mult, [bu, s5tab], [t4])
                        P.tt("pool", bzi[:], t3[:], t4[:], ALU.subtract, [t3, t4], [bzi])
                        P.ts("pool", rho_t[:], ones_f[:, 0:TS5], keep[:, 0, j:j + 1], None, ALU.mult, None,
                             [ones_f, keep], [rho_t])
                        P.scan(zr[:], rho_t[:], bzr[:], s5z[l][:, 0, j:j + 1], [rho_t, bzr, s5z[l]], [zr])
                        P.scan(zi[:], rho_t[:], bzi[:], s5z[l][:, 1, j:j + 1], [rho_t, bzi, s5z[l]], [zi])
                        rc = rotk[:, 0, j:j + 1]
                        rs_ = rotk[:, 1, j:j + 1]
                        zlr = zr[:, TS5 - 1:TS5]
                        zli = zi[:, TS5 - 1:TS5]
                        P.ts("dve", s5zl[:, 0, j:j + 1], zli, rs_, None, ALU.mult, None, [zi, rotk], [s5zl])
                        P.ts("dve", s5zl[:, 1, j:j + 1], zlr, rs_, None, ALU.mult, None, [zr, rotk], [s5zl])
                        P.stt(s5z[l][:, 0, j:j + 1], zlr, rc, s5zl[:, 0, j:j + 1], ALU.mult, ALU.subtract,
                              [zr, rotk, s5zl], [s5z[l]])
                        P.stt(s5z[l][:, 1, j:j + 1], zli, rc, s5zl[:, 1, j:j + 1], ALU.mult, ALU.add,
                              [zi, rotk, s5zl], [s5z[l]])
                        P.tt("pool", t1[:], zr[:], cosT, ALU.mult, [zr, s5tab], [t1])
                        P.tt("pool", t2[:], zi[:], sinT, ALU.mult, [zi, s5tab], [t2])
                        P.tt("pool", s5x[:, 0, :], t1[:], t2[:], ALU.subtract, [t1, t2], [s5x])
                        P.tt("dve", t3[:], zr[:], sinT, ALU.mult, [zr, s5tab], [t3])
                        P.tt("pool", t4[:], zi[:], cosT, ALU.mult, [zi, s5tab], [t4])
                        P.tt("pool", s5x[:, 1, :], t3[:], t4[:], ALU.add, [t3, t4], [s5x])
                        P.mm(yb5[hs, tsl], Cv[:, 0, j, :], s5x[:, 0, :], jh == 0, False, [s5C[l], s5x], [yb5])
                        P.mm(yb5[hs, tsl], Cv[:, 1, j, :], s5x[:, 1, :], False, jh == 1, [s5C[l], s5x], [yb5])
                ys, x2, q_, sg_ = fs[0], fs[1], fs[2], fs[3]
                P.stt(ys[:], zg[:, blk, 3:3 + TT], V(l, "s5d", blk), yb5[:, 0:TT], ALU.mult, ALU.add, [zg, vec, yb5], [ys])
                if dbg_on:
                    dbg_dump("s5y%d" % blk, ys, ys[:], [128, TT])
                P.act(x2[:], ys[:], AF.Square, [ys], [x2])
                P.ts("dve", x2[:], x2[:], 0.044715, 1.0, ALU.mult, ALU.add, [x2], [x2])
                P.tt("pool", q_[:], x2[:], ys[:], ALU.mult, [x2, ys], [q_])
                P.act(sg_[:], q_[:], AF.Sigmoid, [q_], [sg_], scale=2.0 * math.sqrt(2.0 / math.pi))
                P.tt("dve", mix[:, blk, :], ys[:], sg_[:], ALU.mult, [ys, sg_], [mix])
            P.copy("pool", zgb[:], mix[:], [mix], [zgb])
            rg = next_ring()
            P.dma(rg[:, 0:2048].rearrange("p (k n) -> p k n", k=4), glu_w_b[l].rearrange("(k p) n -> p k n", p=128),
                  reads=[wsb], writes=[rg])
            rgv = rg[:, 0:2048].rearrange("p (k n) -> p k n", k=4)
            for ob in range(4):
                b = bank("proj")
                for k in range(4):
                    P.mm(b[:, 0:TT], rgv[:, k, ob * 128:(ob + 1) * 128], zgb[:, k, :], k == 0, k == 3, [rg, zgb], [b])
                sg_ = fs[3]
                P.act(sg_[:], b[:, 0:TT], AF.Sigmoid, [b, vec], [sg_], bias=V(l, "glub", ob))
                P.tt("dve", mix[:, ob, :], mix[:, ob, :], sg_[:], ALU.mult, [mix, sg_], [mix])
            gate_group(21, 4)

            P.copy("act", zg[:, :, 0:3], cl[l][:, :, :], [cl[l]], [zg])
            for blk in range(4):
                b = in_block(25 + blk)
                P.copy("act", zg[:, blk, 3:3 + TT], b[:, 0:TT], [b], [zg])
            P.copy("act", cl[l][:, :, :], zg[:, :, TT:TT + 3], [zg], [cl[l]])
            for blk in range(4):
                xc, r_, i_, a_, a2, gx = fs[0], fs[1], fs[2], fs[3], fs[4], fs[5]
                P.ts("dve", xc[:], zg[:, blk, 0:TT], V(l, "cw", 0 * 4 + blk), V(l, "cb", blk), ALU.mult, ALU.add,
                     [zg, vec], [xc])
                for j in range(1, 4):
                    P.stt(xc[:], zg[:, blk, j:j + TT], V(l, "cw", j * 4 + blk), xc[:], ALU.mult, ALU.add,
                          [zg, vec, xc], [xc])
                P.copy("act", lb[:], xc[:], [xc], [lb])
                br_ = bank("rw")
                P.mm(br_[:, 0:TT], lruw[l][:, blk * 128:(blk + 1) * 128], lb[:], True, True, [lruw[l], lb], [br_])
                P.act(r_[:], br_[:, 0:TT], AF.Sigmoid, [br_, vec], [r_], bias=V(l, "ba", blk))
                bi_ = bank("rw")
                P.mm(bi_[:, 0:TT], lruw[l][:, (4 + blk) * 128:(5 + blk) * 128], lb[:], True, True, [lruw[l], lb], [bi_])
                P.act(i_[:], bi_[:, 0:TT], AF.Sigmoid, [bi_, vec], [i_], bias=V(l, "bx", blk))
                P.act(a_[:], r_[:], AF.Exp, [r_, lru_c], [a_], scale=lru_c[:, l, blk:blk + 1])
                P.act(a2[:], r_[:], AF.Exp, [r_, lru_c], [a2], scale=lru_c[:, l, 4 + blk:5 + blk])
                P.act(a2[:], a2[:], AF.Sqrt, [a2, one_t], [a2], bias=one_t[:, 0:1], scale=-1.0)
                P.tt("pool", gx[:], i_[:], xc[:], ALU.mult, [i_, xc], [gx])
                P.tt("pool", gx[:], gx[:], a2[:], ALU.mult, [gx, a2], [gx])
                if dbg_on and blk == 0:
                    dbg_dump("lru_xc", xc, xc[:], [128, TT])
                    dbg_dump("lru_r", r_, r_[:], [128, TT])
                    dbg_dump("lru_a", a_, a_[:], [128, TT])
                    dbg_dump("lru_m", a2, a2[:], [128, TT])
                    dbg_dump("lru_gx", gx, gx[:], [128, TT])
                    dbg_dump("lru_c", lru_c, lru_c[:, :, :].rearrange("p a b -> p (a b)"), [128, 16])
                P.scan(mix[:, blk, :], a_[:], gx[:], ch[l][:, blk:blk + 1], [a_, gx, ch[l]], [mix])
                P.copy("act", ch[l][:, blk:blk + 1], mix[:, blk, TT - 1:TT], [mix], [ch[l]])
                if dbg_on:
                    dbg_dump("lru%d" % blk, mix, mix[:, blk, :], [128, TT])
            gate_group(29, 8)

            for oc in range(4):
                r = next_ring()
                P.dma(r[:, 0:12 * 256].rearrange("p (k n) -> p k n", k=12),
                      w_out_b[l].rearrange("(k p) n -> p k n", p=128)[:, :, oc * 256:(oc + 1) * 256],
                      reads=[wsb], writes=[r])
                rv = r[:, 0:12 * 256].rearrange("p (k n) -> p k n", k=12)
                for ob2 in range(2):
                    ob = oc * 2 + ob2
                    b = bank("proj")
                    for k in range(12):
                        P.mm(b[:, 0:TT], rv[:, k, ob2 * 128:(ob2 + 1) * 128], ycat[:, k, :], k == 0, k == 11,
                             [r, ycat], [b])
                    P.tt("dve", hT[:, ob, :], hT[:, ob, :], b[:, 0:TT], ALU.add, [hT, b], [hT])
            r = next_ring()
            P.dma(r[:, 0:2048].rearrange("p (k n) -> p k n", k=2), ple_w_b[l].rearrange("(k p) n -> p k n", p=128),
                  reads=[wsb], writes=[r])
            rv = r[:, 0:2048].rearrange("p (k n) -> p k n", k=2)
            epre = zs
            for ob in range(8):
                b = bank("proj")
                for k in range(2):
                    P.mm(b[:, 0:TT], rv[:, k, ob * 128:(ob + 1) * 128], pbf[:, k, :], k == 0, k == 1, [r, pbf], [b])
                P.copy("act", epre[:, ob, :], b[:, 0:TT], [b], [epre])
            rms_rstd(epre, lambda k: epre[:, k, :], 8, NORM_EPS)
            P.copy("pool", xn[:], hT[:], [hT], [xn])
            for gc in range(2):
                r = next_ring()
                P.dma(r[:, 0:4096].rearrange("p (k n) -> p k n", k=8),
                      ple_gw_b[l].rearrange("(k p) n -> p k n", p=128)[:, :, gc * 512:(gc + 1) * 512],
                      reads=[wsb], writes=[r])
                rv = r[:, 0:4096].rearrange("p (k n) -> p k n", k=8)
                for ob2 in range(4):
                    ob = gc * 4 + ob2
                    b = bank("proj")
                    for k in range(8):
                        P.mm(b[:, 0:TT], rv[:, k, ob2 * 128:(ob2 + 1) * 128], xn[:, k, :], k == 0, k == 7, [r, xn], [b])
                    sg_, e_ = fs[6], fs[7]
                    P.act(sg_[:], b[:, 0:TT], AF.Sigmoid, [b], [sg_])
                    P.stt(e_[:], epre[:, ob, :], V(l, "png", ob), rstd[:], ALU.mult, ALU.mult, [epre, vec, rstd], [e_])
                    P.tt("pool", e_[:], e_[:], sg_[:], ALU.mult, [e_, sg_], [e_])
                    P.tt("dve", hT[:, ob, :], hT[:, ob, :], e_[:], ALU.add, [hT, e_], [hT])
            if dbg_on:
                dbg_dump("h1", hT, hT[:, :, :].rearrange("p a t -> p (a t)"), [128, 8 * TT])
        rms_rstd(hT, lambda k: hT[:, k, :], 8, NORM_EPS)
        ob_ = zs
        fo = 2 * VEC_PER_LAYER
        for k in range(8):
            P.stt(ob_[:, k, :], hT[:, k, :], vec[:, fo + k:fo + k + 1], rstd[:], ALU.mult, ALU.mult, [hT, vec, rstd], [ob_])
        out_toks.append(P.dma(oT[:, t0:t0 + TT].rearrange("(k p) t -> p k t", p=128), ob_[:, 0:8, :], reads=[ob_]))
    out_toks.extend(dbg_out.values())
    ninst = P.ninst
    P.finish(out_toks)
    return nc, ninst


def pack_shared(inp):
    f = lambda a: np.asarray(a, np.float32)
    vec = np.zeros((128, NVEC), np.float32)
    for l in range(2):
        def put(name, arr, n):
            c = vcol(l, name)
            vec[:, c:c + n] = _pp(arr, n)
        put("ng", f(inp["norm_g"])[l], 8)
        put("mu", f(inp["rwkv_mu"])[l], 13)
        put("w0", f(inp["rwkv_w0"])[l], 4)
        put("a0", f(inp["rwkv_a0"])[l], 4)
        put("kk", f(inp["rwkv_k_k"])[l], 4)
        put("ka", f(inp["rwkv_k_a"])[l], 4)
        put("rk", f(inp["rwkv_r_k"])[l].reshape(512), 4)
        put("lnw", f(inp["rwkv_ln_w"])[l], 4)
        put("lnb", f(inp["rwkv_ln_b"])[l], 4)
        put("s5d", f(inp["s5_d"])[l], 4)
        put("glub", f(inp["s5_glu_b"])[l], 4)
        cw = f(inp["lru_conv_w"])[l]
        c = vcol(l, "cw")
        for j in range(4):
            vec[:, c + 4 * j:c + 4 * j + 4] = _pp(cw[j], 4)
        put("cb", f(inp["lru_conv_b"])[l], 4)
        put("ba", f(inp["lru_ba"])[l], 4)
        put("bx", f(inp["lru_bx"])[l], 4)
        put("lam", f(inp["lru_lambda"])[l], 4)
        put("png", f(inp["ple_norm_g"])[l], 8)
    vec[:, 2 * VEC_PER_LAYER:2 * VEC_PER_LAYER + 8] = _pp(f(inp["final_norm_g"]), 8)

    w2a2 = np.zeros((2, 128, 512), np.float32)
    w2a2[:, 0:64] = f(inp["rwkv_w2"])
    w2a2[:, 64:128] = f(inp["rwkv_a2"])
    lruw = np.zeros((2, 128, 8, 128), np.float32)
    for l in range(2):
        for m, key in enumerate(("lru_wa", "lru_wx")):
            w = f(inp[key])[l]
            for q in range(4):
                for b2 in range(2):
                    lruw[l, b2 * 64:(b2 + 1) * 64, m * 4 + q, b2 * 64:(b2 + 1) * 64] = w[2 * q + b2]
    lruw = lruw.reshape(2, 128, 1024)
    def modes(a):
        a = f(a).reshape(2, 16, 2, 64)
        return np.ascontiguousarray(a.transpose(0, 2, 3, 1).reshape(2, 128, 16))
    s5s = np.zeros((2, 128, 3, 16), np.float32)
    s5s[:, :, 0] = modes(inp["s5_a_re"])
    s5s[:, :, 1] = modes(inp["s5_a_im"])
    ldt = np.broadcast_to(f(inp["s5_log_dt"])[:, :, None], (2, 32, 64))
    s5s[:, :, 2] = modes(ldt)
    s5s = s5s.reshape(2, 128, 48)
    def bmodes(a):
        a = f(a).reshape(2, 16, 2, 64, 16)
        return a.transpose(0, 2, 3, 1, 4).reshape(2, 128, 16, 16)
    s5b = np.stack([bmodes(inp["s5_b_re"]), bmodes(inp["s5_b_im"])], axis=2).reshape(2, 128, 512)
    s5c = np.zeros((2, 128, 2, 16, 64), np.float32)
    for ri, key in enumerate(("s5_c_re", "s5_c_im")):
        c = f(inp[key]).reshape(2, 16, 2, 16, 64)
        for gh in range(2):
            for jh in range(2):
                c0 = 32 * jh + 16 * gh
                s5c[:, gh * 64:(gh + 1) * 64, ri, jh::2, c0:c0 + 16] = c[:, jh::2, gh].transpose(0, 3, 1, 2)
    s5c = s5c.reshape(2, 128, 2048)
    return {
        "w_in": np.ascontiguousarray(f(inp["w_in"])), "w_out": np.ascontiguousarray(f(inp["w_out"])),
        "ple_w": np.ascontiguousarray(f(inp["ple_w"])), "ple_gw": np.ascontiguousarray(f(inp["ple_gate_w"])),
        "glu_w": np.ascontiguousarray(f(inp["s5_glu_w"])), "vec": vec, "cst": make_consts(),
        "w2a2": w2a2, "lruw": np.ascontiguousarray(lruw), "s5s": np.ascontiguousarray(s5s),
        "s5b": np.ascontiguousarray(s5b), "s5c": np.ascontiguousarray(s5c),
    }


_NC_CACHE = {}


def run_cores(inp, TC, batches, dbg=None):
    key = (TC, tuple(sorted(dbg)) if dbg else None)
    if key not in _NC_CACHE:
        _NC_CACHE[key] = build_nc(TC, dbg)
    nc, ninst = _NC_CACHE[key]
    shared = pack_shared(inp)
    x = np.asarray(inp["x"], np.float32)
    p = np.asarray(inp["p"], np.float32)
    in_maps = []
    for b in batches:
        m = dict(shared)
        m["xT"] = np.ascontiguousarray(x[b, :TC].T)
        m["pT"] = np.ascontiguousarray(p[:, b, :TC].transpose(0, 2, 1))
        in_maps.append(m)
    res = run_bass_kernel_spmd(nc, in_maps, core_ids=list(range(len(batches))))
    return res


def kernel(**inputs):
    x = np.asarray(inputs["x"])
    B, S, _ = x.shape
    batches = [c % B for c in range(8)]
    res = run_cores(inputs, S, batches)
    out = np.empty((B, S, D), np.float32)
    for b in range(B):
        out[b] = res.results[b]["oT"].T
    return out.astype(x.dtype)
```

```python
import contextlib
import math
import numpy as np
import concourse.bass as bass
import concourse.mybir as mybir
from concourse.bass_utils import run_bass_kernel_spmd

F32 = mybir.dt.float32
BF16 = mybir.dt.bfloat16
ALU = mybir.AluOpType
AF = mybir.ActivationFunctionType

D = 1024
DIN = 4224
DMIX = 1536
DPLE = 256
TT = 256
LCH = 64
NCH = TT // LCH
TS5 = 128
import os
SKIP = set(os.environ.get("KSKIP", "").split(","))
GN_EPS = 64e-5
NORM_EPS = 1e-6


class Tok:
    __slots__ = ("sem", "val", "eng", "dma")

    def __init__(self, sem, val, eng, dma):
        self.sem, self.val, self.eng, self.dma = sem, val, eng, dma


class Buf:
    def __init__(self, t, name):
        self.t = t
        self.name = name
        self.w = None
        self.r = []

    def __getitem__(self, idx):
        return self.t[idx]


class Prog:
    ENGS = ("pe", "act", "dve", "pool", "sp")

    def __init__(self, nc, n_dma_sems=32):
        self.nc = nc
        self.es = contextlib.ExitStack()
        self.ops = {e: [] for e in self.ENGS}
        self.cnt = {e: 0 for e in self.ENGS}
        self.sem = {e: self.es.enter_context(nc.semaphore("s_" + e)) for e in self.ENGS}
        self.dsem = [self.es.enter_context(nc.semaphore("d%d" % i)) for i in range(n_dma_sems)]
        self.duse = [0] * n_dma_sems
        self.dnext = 0
        self.seen = {e: {} for e in self.ENGS}
        self.nbuf = 0
        self.ninst = 0
        self.stack = [self.es]

    def push(self):
        st = contextlib.ExitStack()
        self.stack.append(st)

    def pop(self):
        self.barrier()
        self.stack.pop().close()

    def barrier(self):
        toks = []
        for f in self.ENGS:
            if self.cnt[f] > 0:
                toks.append(Tok(self.sem[f], self.cnt[f], f, False))
        for i, s in enumerate(self.dsem):
            if self.duse[i] > 0:
                toks.append(Tok(s, 16 * self.duse[i], "dma", True))
        for e in self.ENGS:
            wl = []
            for t in toks:
                if t.eng == e and not t.dma:
                    continue
                k = id(t.sem)
                if self.seen[e].get(k, 0) >= t.val:
                    continue
                self.seen[e][k] = t.val
                wl.append((t.sem, t.val))

            def run(en, wl=wl):
                for (s, v) in wl:
                    en.wait_ge(s, v)
            self.ops[e].append(run)

    def sb(self, shape, dt=F32, name=None):
        self.nbuf += 1
        name = name or ("b%d" % self.nbuf)
        t = self.stack[-1].enter_context(self.nc.sbuf_tensor("sb_" + name, list(shape), dt))
        return Buf(t, name)

    def ps(self, name, dt=F32, cols=512):
        t = self.es.enter_context(self.nc.psum_tensor(name, [128, cols], dt))
        return Buf(t, name)

    def wrap(self, t, name):
        return Buf(t, name)

    def views(self, buf, n):
        return [Buf(buf.t, "%s.v%d" % (buf.name, i)) for i in range(n)]

    def _need(self, eng, tok, waits, is_dma_issue):
        if tok is None:
            return
        if tok.eng == eng and not tok.dma and not is_dma_issue and eng == "pe":
            return
        k = id(tok.sem)
        if self.seen[eng].get(k, 0) >= tok.val:
            return
        cur = waits.get(k)
        if cur is None or cur[1] < tok.val:
            waits[k] = (tok.sem, tok.val)

    def emit(self, eng, fn, reads=(), writes=(), dma=False):
        waits = {}
        for b in reads:
            self._need(eng, b.w, waits, dma)
        for b in writes:
            self._need(eng, b.w, waits, dma)
            for t in b.r:
                self._need(eng, t, waits, dma)
        if dma:
            i = self.dnext
            self.dnext = (self.dnext + 1) % len(self.dsem)
            s = self.dsem[i]
            if self.duse[i] > 0:
                self._need(eng, Tok(s, 16 * self.duse[i], "dma", True), waits, True)
            self.duse[i] += 1
            tok = Tok(s, 16 * self.duse[i], "dma", True)
            inc = 16
        else:
            self.cnt[eng] += 1
            tok = Tok(self.sem[eng], self.cnt[eng], eng, False)
            inc = 1
        wl = list(waits.values())
        for (s, v) in wl:
            self.seen[eng][id(s)] = v
        tsem = tok.sem
        self.ninst += 1 + len(wl)

        def run(e, wl=wl, fn=fn, tsem=tsem, inc=inc):
            for (s, v) in wl:
                e.wait_ge(s, v)
            fn(e).then_inc(tsem, inc)

        self.ops[eng].append(run)
        for b in reads:
            b.r.append(tok)
            if len(b.r) > 48:
                b.r = b.r[-48:]
        for b in writes:
            b.w = tok
            b.r = []
        return tok

    def finish(self, out_toks):
        wl = [(t.sem, t.val) for t in out_toks]

        def run(e, wl=wl):
            for (s, v) in wl:
                e.wait_ge(s, v)

        self.ops["sp"].append(run)
        nc = self.nc
        ops = self.ops
        with nc.Block() as block:
            @block.tensor
            def _(e):
                for f in ops["pe"]:
                    f(e)

            @block.scalar
            def _(e):
                for f in ops["act"]:
                    f(e)

            @block.vector
            def _(e):
                for f in ops["dve"]:
                    f(e)

            @block.gpsimd
            def _(e):
                for f in ops["pool"]:
                    f(e)

            @block.sync
            def _(e):
                for f in ops["sp"]:
                    f(e)
        self.es.close()

    def dma(self, out, in_, reads=(), writes=(), eng="sp", **kw):
        return self.emit(eng, lambda e: e.dma_start(out=out, in_=in_, **kw), reads, writes, dma=True)

    def mm(self, out, lhsT, rhs, start, stop, reads, writes):
        return self.emit("pe", lambda e: e.matmul(out, lhsT, rhs, start=start, stop=stop), reads, writes)

    def act(self, out, in_, func, reads, writes, bias=None, scale=None):
        kw = {}
        if bias is not None:
            kw["bias"] = bias
        if scale is not None:
            kw["scale"] = scale
        return self.emit("act", lambda e: e.activation(out=out, in_=in_, func=func, **kw), reads, writes)

    def tt(self, eng, out, in0, in1, op, reads, writes):
        return self.emit(eng, lambda e: e.tensor_tensor(out=out, in0=in0, in1=in1, op=op), reads, writes)

    def ts(self, eng, out, in0, s1, s2, op0, op1, reads, writes):
        if op1 is None:
            return self.emit(eng, lambda e: e.tensor_scalar(out, in0, s1, None, op0), reads, writes)
        return self.emit(eng, lambda e: e.tensor_scalar(out, in0, s1, s2, op0, op1), reads, writes)

    def stt(self, out, in0, scalar, in1, op0, op1, reads, writes):
        return self.emit("dve", lambda e: e.scalar_tensor_tensor(out, in0, scalar, in1, op0, op1), reads, writes)

    def copy(self, eng, out, in_, reads, writes):
        if eng == "act":
            return self.emit("act", lambda e: e.activation(out=out, in_=in_, func=AF.Copy), reads, writes)
        return self.emit(eng, lambda e: e.tensor_copy(out, in_), reads, writes)

    def memset(self, eng, ap, val, writes):
        return self.emit(eng, lambda e: e.memset(ap, val), (), writes)

    def scan(self, out, d0, d1, init, reads, writes):
        return self.emit("dve", lambda e: e.tensor_tensor_scan(out, d0, d1, init, ALU.mult, ALU.add), reads, writes)

    def recip(self, out, in_, reads, writes):
        return self.emit("dve", lambda e: e.reciprocal(out, in_), reads, writes)


VEC_FIELDS = [("ng", 8), ("mu", 13), ("w0", 4), ("a0", 4), ("kk", 4), ("ka", 4), ("rk", 4), ("lnw", 4),
              ("lnb", 4), ("s5d", 4), ("glub", 4), ("cw", 16), ("cb", 4), ("ba", 4), ("bx", 4), ("lam", 4),
              ("png", 8)]
VEC_PER_LAYER = sum(n for _, n in VEC_FIELDS)
VEC_OFF = {}
_o = 0
for _n, _c in VEC_FIELDS:
    VEC_OFF[_n] = _o
    _o += _c
NVEC = 2 * VEC_PER_LAYER + 8

CST_IDENT = 0
CST_ONESBD = 128
CST_SCAN = 256
NCST = 256 + TT
MSK_USN = 0
MSK_LSN = NCH * 128
MSK_USP = 2 * NCH * 128
MSK_CI = 3 * NCH * 128
NMSK = 3 * NCH * 128 + NCH * LCH


def vcol(l, name, i=0):
    return l * VEC_PER_LAYER + VEC_OFF[name] + i


def _pp(v, n):
    return np.ascontiguousarray(np.asarray(v, np.float32).reshape(n, 128).T)


def make_consts():
    c = np.zeros((128, NCST), np.float32)
    i = np.arange(128)[:, None]
    j = np.arange(128)[None, :]
    c[:, CST_IDENT:CST_IDENT + 128] = (i == j)
    c[:, CST_ONESBD:CST_ONESBD + 128] = ((i // 64) == (j // 64))
    tt = np.arange(TT)[None, :]
    c[:, CST_SCAN:CST_SCAN + TT] = 1.0 * ((tt % LCH) != 0)
    m = np.zeros((128, NMSK), np.float32)
    t = np.arange(64)[None, :]
    for ch in range(NCH):
        m[:, MSK_USN + ch * 128:MSK_USN + (ch + 1) * 128] = -1.0 * (j > i)
        m[:, MSK_LSN + ch * 128:MSK_LSN + (ch + 1) * 128] = -1.0 * (i > j)
        m[:, MSK_USP + ch * 128:MSK_USP + (ch + 1) * 128] = 1.0 * (j > i)
        m[:, MSK_CI + ch * 64:MSK_CI + (ch + 1) * 64] = 1.0 * (t >= (i % 64))
    return c, m


def build_nc(TC, dbg=None):
    assert TC % TT == 0
    NT = TC // TT
    dbg = dbg or set()
    nc = bass.Bass("TRN2", target_bir_lowering=False)

    def din(name, shape, dt=F32):
        return nc.dram_tensor(name, list(shape), dt, kind="ExternalInput").ap()

    xT = din("xT", [D, TC])
    pT = din("pT", [2, DPLE, TC])
    w_in = din("w_in", [2, D, DIN])
    w_out = din("w_out", [2, DMIX, D])
    ple_w = din("ple_w", [2, DPLE, D])
    ple_gw = din("ple_gw", [2, D, D])
    glu_w = din("glu_w", [2, 512, 512])
    vec_d = din("vec", [128, NVEC])
    cst_d = din("cst", [128, NCST])
    msk_d = din("msk", [128, NMSK])
    w2a2_d = din("w2a2", [2, 128, 512])
    lruw_d = din("lruw", [2, 128, 8 * 128])
    s5s_d = din("s5s", [2, 128, 3 * 16])
    s5b_d = din("s5b", [2, 128, 2 * 16 * 16])
    s5c_d = din("s5c", [2, 128, 2 * 16 * 64])
    oT = nc.dram_tensor("oT", [D, TC], F32, kind="ExternalOutput").ap()
    dbg_out = {}

    def dram_int(name, shape, dt):
        return nc.dram_tensor(name, list(shape), dt, kind="Internal").ap()

    w_in_b = dram_int("w_in_b", [2, D, DIN], BF16)
    w_out_b = dram_int("w_out_b", [2, DMIX, D], BF16)
    ple_w_b = dram_int("ple_w_b", [2, DPLE, D], BF16)
    ple_gw_b = dram_int("ple_gw_b", [2, D, D], BF16)
    glu_w_b = dram_int("glu_w_b", [2, 512, 512], BF16)
    s5tab_d = dram_int("s5tab", [2, 128, 2 * 16 * TS5], F32)

    P = Prog(nc)
    wsb = P.wrap(None, "wscratch")
    tabsb = P.wrap(None, "s5tabscr")

    def dbg_dump(name, buf, ap, shape, dt=F32):
        if name not in dbg:
            return
        o = nc.dram_tensor("dbg_" + name, list(shape), dt, kind="ExternalOutput").ap()
        dbg_out[name] = P.dma(o, ap, reads=[buf])

    vec = P.sb([128, NVEC], F32, "vec")
    cst = P.sb([128, NCST], F32, "cst")
    P.dma(vec[:], vec_d, writes=[vec])
    P.dma(cst[:], cst_d, writes=[cst])
    cstb = P.sb([128, 128], BF16, "cstb")
    P.copy("dve", cstb[:], cst[:, CST_IDENT:CST_IDENT + 128], [cst], [cstb])
    mskb = P.sb([128, NMSK], BF16, "mskb")
    ident_f = cst[:, CST_IDENT:CST_IDENT + 128]
    ident_b = cstb[:, 0:128]
    onesbd_f = cst[:, CST_ONESBD:CST_ONESBD + 128]
    ones_f = P.sb([128, 128], F32, "ones_f")
    P.memset("pool", ones_f[:], 1.0, [ones_f])
    one_t = P.sb([128, 1], F32, "one_t")
    P.memset("pool", one_t[:], 1.0, [one_t])

    def V(l, name, i=0, n=1):
        c = vcol(l, name, i)
        return vec[:, c:c + n]

    for l in range(2):
        for (src, dst, rows) in ((w_in, w_in_b, D), (w_out, w_out_b, DMIX), (ple_w, ple_w_b, DPLE),
                                 (ple_gw, ple_gw_b, D), (glu_w, glu_w_b, 512)):
            for r0 in range(0, rows, 128):
                P.dma(dst[l, r0:r0 + 128, :], src[l, r0:r0 + 128, :], writes=[wsb], eng="pool",
                      max_dma_last_dim=4096)

    w2a2 = []
    lruw = []
    for l in range(2):
        w2a2.append(P.sb([128, 512], BF16, "w2a2_%d" % l))
        lruw.append(P.sb([128, 1024], BF16, "lruw_%d" % l))
    lru_c = P.sb([128, 2, 8], F32, "lru_c")
    s5B = [P.sb([128, 2 * 4 * 2 * 128], BF16, "s5B%d" % l) for l in range(2)]
    s5C = [P.sb([128, 2048], BF16, "s5C%d" % l) for l in range(2)]
    s5keep = [P.sb([128, 3, 16], F32, "s5keep%d" % l) for l in range(2)]
    s5rotb = [P.sb([128, 2, 16], F32, "s5rot%d" % l) for l in range(2)]
    ps_misc = P.ps("ps7")
    P.push()
    stage = P.sb([128, 1024], F32, "stage")
    mstage = P.sb([128, NMSK], F32, "mstage")
    P.dma(mstage[:], msk_d, writes=[mstage])
    P.copy("act", mskb[:], mstage[:], [mstage], [mskb])
    for l in range(2):
        P.dma(stage[:, 0:512], w2a2_d[l], writes=[stage])
        P.copy("act", w2a2[l][:], stage[:, 0:512], [stage], [w2a2[l]])
        P.dma(stage[:], lruw_d[l], writes=[stage])
        P.copy("act", lruw[l][:], stage[:], [stage], [lruw[l]])

    for l in range(2):
        tmp = P.sb([128, 4], F32, "lrutmp%d" % l)
        P.act(tmp[:], V(l, "lam", 0, 4), AF.Exp, [vec], [tmp], scale=-1.0)
        P.act(tmp[:], tmp[:], AF.Ln, [tmp, one_t], [tmp], bias=one_t[:, 0:1])
        P.ts("dve", lru_c[:, l, 0:4], tmp[:], -8.0, None, ALU.mult, None, [tmp], [lru_c])
        P.ts("dve", lru_c[:, l, 4:8], tmp[:], -16.0, None, ALU.mult, None, [tmp], [lru_c])

    s5rot = []
    for l in range(2):
        s5s = P.sb([128, 48], F32, "s5s%d" % l)
        P.dma(s5s[:], s5s_d[l], writes=[s5s])
        a_re = s5s[:, 0:16]
        a_im = s5s[:, 16:32]
        ldt = s5s[:, 32:48]
        w = P.sb([128, 16, 16], F32, "s5w%d" % l)
        R = [w]

        def row(i):
            return w[:, i, :]
        dt_, rho, th, cc, ss, t1, t2, lr, li, den, qre, qim, nr = [row(i) for i in range(13)]
        P.act(dt_, ldt, AF.Exp, [s5s], R)
        P.tt("dve", rho, a_re, dt_, ALU.mult, [s5s] + R, R)
        P.act(rho, rho, AF.Exp, R, R)
        P.tt("dve", th, a_im, dt_, ALU.mult, [s5s] + R, R)
        hp = P.sb([128, 1], F32, "halfpi%d" % l)
        P.memset("dve", hp[:], math.pi / 2, [hp])
        P.act(cc, th, AF.Sin, R + [hp], R, bias=hp[:, 0:1], scale=1.0 / 16)
        P.act(ss, th, AF.Sin, R, R, scale=1.0 / 16)

        def csq(c_, s_):
            P.tt("dve", t1, c_, c_, ALU.mult, R, R)
            P.tt("dve", t2, s_, s_, ALU.mult, R, R)
            P.stt(s_, c_, 2.0, s_, ALU.mult, ALU.mult, R, R)
            P.tt("dve", c_, t1, t2, ALU.subtract, R, R)
        for _ in range(4):
            csq(cc, ss)
        P.tt("dve", lr, rho, cc, ALU.mult, R, R)
        P.tt("dve", li, rho, ss, ALU.mult, R, R)
        P.tt("dve", t1, a_re, a_re, ALU.mult, [s5s] + R, R)
        P.tt("dve", t2, a_im, a_im, ALU.mult, [s5s] + R, R)
        P.tt("dve", den, t1, t2, ALU.add, R, R)
        P.recip(den, den, R, R)
        P.ts("dve", nr, lr, -1.0, None, ALU.add, None, R, R)
        P.tt("dve", t1, nr, a_re, ALU.mult, [s5s] + R, R)
        P.tt("dve", t2, li, a_im, ALU.mult, [s5s] + R, R)
        P.tt("dve", t1, t1, t2, ALU.add, R, R)
        P.tt("dve", qre, t1, den, ALU.mult, R, R)
        P.tt("dve", t1, li, a_re, ALU.mult, [s5s] + R, R)
        P.tt("dve", t2, nr, a_im, ALU.mult, [s5s] + R, R)
        P.tt("dve", t1, t1, t2, ALU.subtract, R, R)
        P.tt("dve", qim, t1, den, ALU.mult, R, R)
        keep = s5keep[l]
        P.copy("dve", keep[:, 0, :], rho, R, [keep])
        P.copy("dve", keep[:, 1, :], cc, R, [keep])
        P.copy("dve", keep[:, 2, :], ss, R, [keep])

        sbf = P.sb([128, 512], F32, "s5b_in%d" % l)
        P.dma(sbf[:], s5b_d[l], writes=[sbf])
        bre = sbf[:, 0:256].rearrange("p (j h) -> p j h", h=16)
        bim = sbf[:, 256:512].rearrange("p (j h) -> p j h", h=16)
        Bt = s5B[l]
        Btv = Bt[:, :].rearrange("p (r b q m) -> p r b q m", r=2, b=4, q=2)
        bpad = P.sb([128, 2, 128], BF16, "s5bpad%d" % l)
        tb = P.sb([128, 2, 16], F32, "s5tb%d" % l)
        for j in range(16):
            P.ts("dve", tb[:, 0, :], bim[:, j, :], qim[:, j:j + 1], None, ALU.mult, None, [sbf] + R, [tb])
            P.stt(tb[:, 0, :], bre[:, j, :], qre[:, j:j + 1], tb[:, 0, :], ALU.mult, ALU.subtract, [sbf, tb] + R, [tb])
            P.ts("dve", tb[:, 1, :], bre[:, j, :], qim[:, j:j + 1], None, ALU.mult, None, [sbf] + R, [tb])
            P.stt(tb[:, 1, :], bim[:, j, :], qre[:, j:j + 1], tb[:, 1, :], ALU.mult, ALU.add, [sbf, tb] + R, [tb])
            P.memset("pool", bpad[:], 0.0, [bpad])
            for gh in range(2):
                col0 = 32 * (j % 4) + gh * 16
                for ri in range(2):
                    P.copy("pool", bpad[gh * 64:(gh + 1) * 64, ri, col0:col0 + 16],
                           tb[gh * 64:(gh + 1) * 64, ri, :], [tb], [bpad])
            for ri in range(2):
                P.mm(ps_misc[:, ri * 128:(ri + 1) * 128], bpad[:, ri, :], ident_b, True, True, [bpad, cstb], [ps_misc])
            hf = (j % 4) // 2
            for ri in range(2):
                P.copy("act", Btv[64 * hf:64 * hf + 64, ri, j // 4, j % 2, :],
                       ps_misc[64 * hf:64 * hf + 64, ri * 128:(ri + 1) * 128], [ps_misc], [Bt])

        scf = P.sb([128, 2048], F32, "s5c_in%d" % l)
        P.dma(scf[:], s5c_d[l], writes=[scf])
        Ct = s5C[l]
        P.copy("act", Ct[:, 0:1024], scf[:, 0:1024], [scf], [Ct])
        P.ts("dve", Ct[:, 1024:2048], scf[:, 1024:2048], -1.0, None, ALU.mult, None, [scf], [Ct])

        tab = P.sb([128, 2, 16, TS5], F32, "s5tabb%d" % l)
        P.memset("pool", tab[:, 0, :, 0:1], 1.0, [tab])
        P.memset("pool", tab[:, 1, :, 0:1], 0.0, [tab])
        ec = P.sb([128, 2, 16], F32, "s5ec%d" % l)
        P.copy("dve", ec[:, 0, :], cc, R, [ec])
        P.copy("dve", ec[:, 1, :], ss, R, [ec])
        m = 1
        while m < TS5:
            for j in range(16):
                cj = ec[:, 0, j:j + 1]
                sj = ec[:, 1, j:j + 1]
                src_c = tab[:, 0, j, 0:m]
                src_s = tab[:, 1, j, 0:m]
                dst_c = tab[:, 0, j, m:2 * m]
                dst_s = tab[:, 1, j, m:2 * m]
                P.ts("dve", dst_c, src_s, sj, None, ALU.mult, None, [tab, ec], [tab])
                P.stt(dst_c, src_c, cj, dst_c, ALU.mult, ALU.subtract, [tab, ec], [tab])
                P.ts("dve", dst_s, src_c, sj, None, ALU.mult, None, [tab, ec], [tab])
                P.stt(dst_s, src_s, cj, dst_s, ALU.mult, ALU.add, [tab, ec], [tab])
            e_c = ec[:, 0, :]
            e_s = ec[:, 1, :]
            P.tt("dve", t1, e_c, e_c, ALU.mult, [ec] + R, R)
            P.tt("dve", t2, e_s, e_s, ALU.mult, [ec] + R, R)
            P.stt(e_s, e_c, 2.0, e_s, ALU.mult, ALU.mult, [ec], [ec])
            P.tt("dve", e_c, t1, t2, ALU.subtract, R, [ec])
            m *= 2
        rot = s5rotb[l]
        P.copy("dve", rot[:], ec[:], [ec], [rot])
        s5rot.append((rot, keep))
        P.dma(s5tab_d[l], tab[:, :, :, :].rearrange("p a j t -> p (a j t)"), reads=[tab], writes=[tabsb])
        dbg_dump("s5w%d" % l, w, w[:, :, :].rearrange("p a b -> p (a b)"), [128, 256])
        dbg_dump("s5keep%d" % l, keep, keep[:, :, :].rearrange("p a b -> p (a b)"), [128, 48])
        dbg_dump("s5tab%d" % l, tab, tab[:, :, :, :].rearrange("p a j t -> p (a j t)"), [128, 2 * 16 * TS5])
        dbg_dump("s5B%d" % l, Bt, Bt[:, :], [128, 2048], BF16)
    P.pop()

    hT = P.sb([128, 8, TT], F32, "hT")
    xn = P.sb([128, 8, TT], BF16, "xn")
    zst = [P.sb([128, 1 + TT], F32, "zst%d" % i) for i in range(2)]
    zs = P.sb([128, 13, TT], F32, "zs")
    zsv = P.views(zs, 13)
    zu = P.sb([128, 4, TT], F32, "zu")
    zx = P.sb([128, 4, 3 + TT], F32, "zx")
    sgate = P.sb([128, 12, TT], BF16, "sgate")
    sgv = P.views(sgate, 12)
    ycat = P.sb([128, 12, TT], BF16, "ycat")
    ycv = P.views(ycat, 12)
    pbf = P.sb([128, 2, TT], BF16, "pbf")
    ring = [P.sb([128, 4096], BF16, "ring%d" % i) for i in range(3)]
    ringi = [0]
    s5tab = P.sb([128, 2, 16, TS5], F32, "s5tab")
    banks = [P.ps("ps%d" % i) for i in range(7)] + [ps_misc]
    rot_i = {"proj": 0, "rw": 0}

    def bank(group):
        ids = (0, 1) if group == "proj" else (2, 3)
        i = rot_i[group]
        rot_i[group] = (i + 1) % len(ids)
        return banks[ids[i]]

    def next_ring():
        r = ring[ringi[0]]
        ringi[0] = (ringi[0] + 1) % 3
        return r

    cz = [P.sb([128, 13], F32, "cz%d" % l) for l in range(2)]
    cl = [P.sb([128, 4, 3], F32, "cl%d" % l) for l in range(2)]
    ch = [P.sb([128, 4], F32, "ch%d" % l) for l in range(2)]
    s5z = [P.sb([128, 2, 16], F32, "s5z%d" % l) for l in range(2)]
    s5zv = [P.views(s5z[l], 16) for l in range(2)]
    Tst = [[P.sb([128, 128], BF16, "T%d_%d" % (l, pb)) for pb in range(4)] for l in range(2)]
    for l in range(2):
        P.memset("pool", cz[l][:], 0.0, [cz[l]])
        P.memset("pool", cl[l][:], 0.0, [cl[l]])
        P.memset("pool", ch[l][:], 0.0, [ch[l]])
        P.memset("pool", s5z[l][:], 0.0, s5zv[l])
        for pb in range(4):
            P.memset("pool", Tst[l][pb][:], 0.0, [Tst[l][pb]])

    NF = 17
    fs = [P.sb([128, TT], F32, "rf%d" % i) for i in range(NF)]
    pad_names = ["RTp", "KTp", "CTp", "BTp", "VTp", "KGp", "BGp"]
    pads = {n: P.sb([128, NCH * 128], BF16, n) for n in pad_names}
    for n in pad_names:
        P.memset("pool", pads[n][:], 0.0, [pads[n]])
    RTc = P.sb([128, TT], BF16, "RTc")
    tanh_wd = P.sb([128, TT], BF16, "tanhwd")
    Blev_i = [[P.sb([128, NCH * 128], BF16, "Blev%d_%d" % (i, k)) for i in range(2)] for k in range(2)]
    BTlev_i = [[P.sb([128, NCH * 128], BF16, "BTlev%d_%d" % (i, k)) for i in range(2)] for k in range(2)]
    AkkT_i = [P.sb([128, NCH * 128], BF16, "AkkT_%d" % k) for k in range(2)]
    Xbf_i = [P.sb([128, NCH * 256], BF16, "Xbf_%d" % k) for k in range(2)]
    NPI = 2
    pp = []
    for i in range(NPI):
        d = {}
        for n in ("Vbd", "KGbd", "BGbd", "PT", "nU0", "Wbd"):
            d[n] = P.sb([128, NCH * 128], BF16, "%s_%d" % (n, i))
        for n in ("Rhat", "ArkT", "ArbT"):
            d[n] = P.sb([128, TT], BF16, "%s_%d" % (n, i))
        d["bonus"] = P.sb([128, TT], F32, "bonus_%d" % i)
        d["GL"] = P.sb([128, NCH], F32, "GL_%d" % i)
        d["rt32"] = P.sb([128, TT], F32, "rt32_%d" % i)
        pp.append(d)
    mix = P.sb([128, 4, TT], F32, "mix")
    mixv = P.views(mix, 4)

    NS5SET = 2
    s5f = [[P.sb([128, TS5], F32, "s5f%d_%d" % (k, i)) for i in range(8)] for k in range(NS5SET)]
    s5x = [P.sb([128, 2, TS5], BF16, "s5x%d" % k) for k in range(NS5SET)]
    s5zl = [P.sb([128, 2], F32, "s5zl%d" % k) for k in range(NS5SET)]
    spf = [P.sb([128, TT], F32, "spf%d" % i) for i in range(3)]
    lf = [P.sb([128, TT], F32, "lf%d" % i) for i in range(4)]
    ubf = P.sb([128, 4, TT], BF16, "ubf")
    lb = P.sb([128, TT], BF16, "lb")
    rstd = P.sb([128, TT], F32, "rstd")
    sq = P.sb([128, 4, TT], F32, "sq")

    out_toks = []
    eps_t = {}
    for e_ in (NORM_EPS, GN_EPS):
        t = P.sb([128, 1], F32, "eps%d" % len(eps_t))
        P.memset("pool", t[:], e_, [t])
        eps_t[e_] = t
    neg_half = -math.exp(-0.5)

    def rms_rstd(src_bufs, src_ap_fn, nblk, eps):
        b = bank("proj")
        for k in range(nblk):
            P.act(sq[:, k % 4, :], src_ap_fn(k), AF.Square, src_bufs, [sq])
            P.mm(b[:, 0:TT], ones_f[:], sq[:, k % 4, :], k == 0, k == nblk - 1, [ones_f, sq], [b])
        P.act(rstd[:], b[:, 0:TT], AF.Sqrt, [b, eps_t[eps]], [rstd], bias=eps_t[eps][:, 0:1], scale=1.0 / (nblk * 128))
        P.recip(rstd[:], rstd[:], [rstd], [rstd])

    def drive(items):
        active = list(items)
        while active:
            for item in list(active):
                g, w = item
                for _ in range(w):
                    try:
                        next(g)
                    except StopIteration:
                        active.remove(item)
                        break

    s5ctr = [0]

    for it in range(NT):
        t0 = it * TT
        first = (it == 0)
        P.dma(hT[:], xT[:, t0:t0 + TT].rearrange("(k p) t -> p k t", p=128), writes=[hT])
        for l in range(2):
            dbg_on = first and l == 0
            P.dma(pbf[:], pT[l, :, t0:t0 + TT].rearrange("(k p) t -> p k t", p=128), writes=[pbf], eng="pool")
            P.dma(s5tab[:, :, :, :].rearrange("p a j t -> p (a j t)"), s5tab_d[l], reads=[tabsb], writes=[s5tab])
            rms_rstd([hT], lambda k: hT[:, k, :], 8, NORM_EPS)
            for k in range(8):
                P.stt(xn[:, k, :], hT[:, k, :], V(l, "ng", k), rstd[:], ALU.mult, ALU.mult, [hT, vec, rstd], [xn])
            if dbg_on:
                dbg_dump("xn", xn, xn[:, :, :].rearrange("p a t -> p (a t)"), [128, 8 * TT], BF16)

            wchunk = {}

            def in_block(cb, l=l, wchunk=wchunk):
                ci = cb // 4
                if ci not in wchunk:
                    r = next_ring()
                    ncol = 512 if ci < 8 else 128
                    P.dma(r[:, 0:8 * ncol].rearrange("p (k n) -> p k n", k=8),
                          w_in_b[l].rearrange("(k p) n -> p k n", p=128)[:, :, ci * 512:ci * 512 + ncol],
                          reads=[wsb], writes=[r])
                    wchunk[ci] = (r, ncol)
                r, ncol = wchunk[ci]
                rv = r[:, 0:8 * ncol].rearrange("p (k n) -> p k n", k=8)
                c0 = (cb % 4) * 128
                b = bank("proj")
                for k in range(8):
                    P.mm(b[:, 0:TT], rv[:, k, c0:c0 + 128], xn[:, k, :], k == 0, k == 7, [r, xn], [b])
                return b

            for cb in range(13):
                st = zst[cb % 2]
                b = in_block(cb)
                P.copy("act", st[:, 0:1], cz[l][:, cb:cb + 1], [cz[l]], [st])
                P.copy("act", st[:, 1:1 + TT], b[:, 0:TT], [b], [st])
                P.copy("act", cz[l][:, cb:cb + 1], st[:, TT:TT + 1], [st], [cz[l]])
                d_ = lf[cb % 2]
                P.tt("pool", d_[:], st[:, 0:TT], st[:, 1:1 + TT], ALU.subtract, [st], [d_])
                P.stt(zs[:, cb, :], d_[:], V(l, "mu", cb), st[:, 1:1 + TT], ALU.mult, ALU.add, [d_, vec, st], [zsv[cb]])
            if dbg_on:
                dbg_dump("zs", zs, zs[:, :, :].rearrange("p a t -> p (a t)"), [128, 13 * TT])
            P.act(tanh_wd[0:64, :], zs[0:64, 12, :], AF.Tanh, [zsv[12]], [tanh_wd])
            P.copy("act", tanh_wd[64:128, :], zs[64:128, 12, :], [zsv[12]], [tanh_wd])
            for blk in range(4):
                b = in_block(13 + blk)
                P.act(sgate[:, blk, :], b[:, 0:TT], AF.Silu, [b], [sgv[blk]])
            for blk in range(4):
                b = in_block(17 + blk)
                P.copy("act", zu[:, blk, :], b[:, 0:TT], [b], [zu])
            P.copy("pool", ubf[:], zu[:], [zu], [ubf])
            for blk in range(4):
                b = in_block(21 + blk)
                P.act(sgate[:, 4 + blk, :], b[:, 0:TT], AF.Silu, [b], [sgv[4 + blk]])
            P.copy("act", zx[:, :, 0:3], cl[l][:, :, :], [cl[l]], [zx])
            for blk in range(4):
                b = in_block(25 + blk)
                P.copy("act", zx[:, blk, 3:3 + TT], b[:, 0:TT], [b], [zx])
            P.copy("act", cl[l][:, :, :], zx[:, :, TT:TT + 3], [zx], [cl[l]])
            for blk in range(4):
                b = in_block(29 + blk)
                P.act(sgate[:, 8 + blk, :], b[:, 0:TT], AF.Silu, [b], [sgv[8 + blk]])

            def prep(pb, inst, l=l, dbg_on=dbg_on):
                d = pp[inst]
                Blev, BTlev, AkkT = Blev_i[inst], BTlev_i[inst], AkkT_i[inst]
                r_ = zs[:, pb, :]
                k_ = zs[:, 4 + pb, :]
                v_ = zs[:, 8 + pb, :]
                zr_, zk_, zv_ = zsv[pb], zsv[4 + pb], zsv[8 + pb]
                (sg, ld, a_, kk_, kk2, sqk, kap, t1, kp, b_, lg, eg, ieg, eg1, dl, egl, rk) = fs[:17]
                rt32 = d["rt32"]
                cols = slice(pb * 128, (pb + 1) * 128)
                bw = bank("rw")
                P.mm(bw[:, 0:TT], w2a2[l][0:64, cols], tanh_wd[0:64, :], True, True, [w2a2[l], tanh_wd], [bw])
                P.act(sg[:], bw[:, 0:TT], AF.Sigmoid, [bw, vec], [sg], bias=V(l, "w0", pb))
                P.ts("dve", ld[:], sg[:], neg_half, None, ALU.mult, None, [sg], [ld])
                ba_ = bank("rw")
                P.mm(ba_[:, 0:TT], w2a2[l][64:128, cols], tanh_wd[64:128, :], True, True, [w2a2[l], tanh_wd], [ba_])
                P.act(a_[:], ba_[:, 0:TT], AF.Sigmoid, [ba_, vec], [a_], bias=V(l, "a0", pb))
                yield
                P.scan(lg[:], cst[:, CST_SCAN:CST_SCAN + TT], ld[:], 0.0, [cst, ld], [lg])
                P.ts("dve", kk_[:], k_, V(l, "kk", pb), None, ALU.mult, None, [zk_, vec], [kk_])
                P.tt("pool", kk2[:], kk_[:], kk_[:], ALU.mult, [kk_], [kk2])
                bs = bank("rw")
                P.mm(bs[:, 0:TT], onesbd_f, kk2[:], True, True, [cst, kk2], [bs])
                P.act(sqk[:], bs[:, 0:TT], AF.Sqrt, [bs], [sqk])
                yield
                P.act(eg[:], lg[:], AF.Exp, [lg], [eg])
                P.act(ieg[:], lg[:], AF.Exp, [lg], [ieg], scale=-1.0)
                P.tt("pool", eg1[:], lg[:], ld[:], ALU.subtract, [lg, ld], [eg1])
                P.act(eg1[:], eg1[:], AF.Exp, [eg1], [eg1])
                P.ts("dve", sqk[:], sqk[:], 1e-12, None, ALU.max, None, [sqk], [sqk])
                P.recip(sqk[:], sqk[:], [sqk], [sqk])
                P.tt("pool", kap[:], kk_[:], sqk[:], ALU.mult, [kk_, sqk], [kap])
                yield
                P.ts("dve", t1[:], a_[:], -1.0, V(l, "ka", pb), ALU.add, ALU.mult, [a_, vec], [t1])
                P.stt(kp[:], t1[:], 1.0, k_, ALU.add, ALU.mult, [t1, zk_], [kp])
                P.tt("pool", b_[:], kap[:], a_[:], ALU.mult, [kap, a_], [b_])
                lg3 = lg[:, :].rearrange("p (c t) -> p c t", t=LCH)
                P.tt("dve", dl[:, :].rearrange("p (c t) -> p c t", t=LCH), lg3[:, :, LCH - 1:LCH].to_broadcast([128, NCH, LCH]),
                     lg3, ALU.subtract, [lg], [dl])
                P.act(egl[:], dl[:], AF.Exp, [dl], [egl])
                P.copy("act", d["GL"][:, :], eg[:, :].rearrange("p (c t) -> p c t", t=LCH)[:, :, LCH - 1], [eg], [d["GL"]])
                yield
                P.tt("dve", rt32[:], r_, eg[:], ALU.mult, [zr_, eg], [rt32])
                P.copy("act", RTc[:], rt32[:], [rt32], [RTc])

                def padw(name, eng, in0, in1, rd):
                    t = pads[name]
                    tv = t[:, :].rearrange("p (c h t) -> p c h t", c=NCH, h=2)
                    for hh in range(2):
                        ps_ = slice(hh * 64, (hh + 1) * 64)
                        o = tv[ps_, :, hh, :]
                        i0 = in0[ps_, :].rearrange("p (c t) -> p c t", t=LCH)
                        if in1 is None:
                            P.copy(eng, o, i0, rd, [t])
                        else:
                            i1 = in1[ps_, :].rearrange("p (c t) -> p c t", t=LCH)
                            P.tt(eng, o, i0, i1, ALU.mult, rd, [t])
                padw("RTp", "act", rt32, None, [rt32])
                padw("KTp", "dve", kp, ieg, [kp, ieg])
                padw("CTp", "pool", kap, eg1, [kap, eg1])
                yield
                padw("BTp", "dve", b_, ieg, [b_, ieg])
                padw("VTp", "act", zs[:, 8 + pb, :], None, [zv_])
                padw("KGp", "pool", kp, egl, [kp, egl])
                padw("BGp", "dve", b_, egl, [b_, egl])
                yield
                P.stt(rk[:], r_, V(l, "rk", pb), kp[:], ALU.mult, ALU.mult, [zr_, vec, kp], [rk])
                bb = bank("rw")
                P.mm(bb[:, 0:TT], onesbd_f, rk[:], True, True, [cst, rk], [bb])
                P.tt("dve", d["bonus"][:], bb[:, 0:TT], v_, ALU.mult, [bb, zv_], [d["bonus"]])
                if dbg_on and pb == 0:
                    dbg_dump("lg", lg, lg[:], [128, TT])
                    dbg_dump("kap", kap, kap[:], [128, TT])
                    dbg_dump("kp", kp, kp[:], [128, TT])
                    dbg_dump("a", a_, a_[:], [128, TT])
                yield

                def chunkmm(dst_bank, lname, rname, rbuf=None, rcols=128):
                    lt = pads[lname]
                    for c in range(NCH):
                        if rbuf is None:
                            rb_ = pads[rname]
                            rap = rb_[:, c * 128:(c + 1) * 128]
                        else:
                            rb_ = rbuf
                            rap = rbuf[:, c * rcols:(c + 1) * rcols]
                        P.mm(dst_bank[:, c * rcols:(c + 1) * rcols], lt[:, c * 128:(c + 1) * 128], rap, True, True,
                             [lt, rb_], [dst_bank])

                def masked(dst, src_bank, mcol, w):
                    n = NCH * w
                    P.tt("dve", dst[:, 0:n], src_bank[:, 0:n], mskb[:, mcol:mcol + n], ALU.mult, [src_bank, mskb], [dst])
                b1 = bank("rw")
                chunkmm(b1, "BTp", "CTp")
                masked(BTlev[0], b1, MSK_USN, 128)
                b2 = bank("rw")
                chunkmm(b2, "CTp", "BTp")
                masked(Blev[0], b2, MSK_LSN, 128)
                yield
                b3 = bank("rw")
                chunkmm(b3, "KTp", "CTp")
                masked(AkkT, b3, MSK_USP, 128)
                b4 = bank("rw")
                chunkmm(b4, "KTp", None, RTc, LCH)
                masked(d["ArkT"], b4, MSK_CI, LCH)
                b5 = bank("rw")
                chunkmm(b5, "BTp", None, RTc, LCH)
                masked(d["ArbT"], b5, MSK_CI, LCH)
                yield

            def tokmajor(src_name, dst_buf, dst_ap, eng):
                bt_ = bank("rw")
                lt = pads[src_name]
                for c in range(NCH):
                    P.mm(bt_[:, c * 128:(c + 1) * 128], lt[:, c * 128:(c + 1) * 128], ident_b, True, True,
                         [lt, cstb], [bt_])
                P.copy(eng, dst_ap, bt_[:, 0:NCH * 128] if len(dst_ap.shape) == 2 else
                       bt_[:, 0:NCH * 128].rearrange("p (c n) -> p c n", n=128), [bt_], [dst_buf])

            def solve(pb, inst, l=l):
                d = pp[inst]
                Blev, BTlev, AkkT, Xbf = Blev_i[inst], BTlev_i[inst], AkkT_i[inst], Xbf_i[inst]
                Xbfv = Xbf[:, :].rearrange("p (c n) -> p c n", n=256)
                tokmajor("VTp", d["Vbd"], d["Vbd"][:, :], "act")
                tokmajor("KGp", d["KGbd"], d["KGbd"][:, :], "act")
                yield
                tokmajor("BGp", d["BGbd"], d["BGbd"][:, :], "act")
                tokmajor("CTp", Xbf, Xbfv[:, :, 0:128], "act")
                bt_ = bank("rw")
                for c in range(NCH):
                    cs = slice(c * 128, (c + 1) * 128)
                    P.mm(bt_[:, cs], AkkT[:, cs], d["Vbd"][:, cs], True, True, [AkkT, d["Vbd"]], [bt_])
                P.copy("act", Xbfv[:, :, 128:256], bt_[:, 0:NCH * 128].rearrange("p (c n) -> p c n", n=128), [bt_], [Xbf])
                yield "tok_done"
                cur = 0
                NLEV = int(os.environ.get('KNLEV', '6'))
                for lev in range(NLEV):
                    for half in range(2):
                        bx_ = banks[4 + half]
                        for cc_ in range(2):
                            c = half * 2 + cc_
                            P.mm(bx_[:, cc_ * 256:(cc_ + 1) * 256], BTlev[cur][:, c * 128:(c + 1) * 128], Xbfv[:, c, :],
                                 True, True, [BTlev[cur], Xbf], [bx_])
                    if lev < NLEV - 1:
                        nxt = 1 - cur
                        bq = bank("rw")
                        for c in range(NCH):
                            cs = slice(c * 128, (c + 1) * 128)
                            P.mm(bq[:, cs], Blev[cur][:, cs], BTlev[cur][:, cs], True, True, [Blev[cur], BTlev[cur]], [bq])
                        if lev < NLEV - 2:
                            bq2 = bank("rw")
                            for c in range(NCH):
                                cs = slice(c * 128, (c + 1) * 128)
                                P.mm(bq2[:, cs], BTlev[cur][:, cs], Blev[cur][:, cs], True, True,
                                     [Blev[cur], BTlev[cur]], [bq2])
                    for half in range(2):
                        bx_ = banks[4 + half]
                        xs_ = Xbf[:, half * 512:(half + 1) * 512]
                        P.tt("dve", xs_, xs_, bx_[:, 0:512], ALU.add, [Xbf, bx_], [Xbf])
                    if lev < NLEV - 1:
                        if lev < NLEV - 2:
                            P.copy("act", Blev[nxt][:], bq2[:, 0:512], [bq2], [Blev[nxt]])
                        P.copy("act", BTlev[nxt][:], bq[:, 0:512], [bq], [BTlev[nxt]])
                        cur = nxt
                    yield
                nU0v = d["nU0"][:, :].rearrange("p (c n) -> p c n", n=128)
                Wbdv = d["Wbd"][:, :].rearrange("p (c n) -> p c n", n=128)
                P.ts("dve", nU0v, Xbfv[:, :, 128:256], -1.0, None, ALU.mult, None, [Xbf], [d["nU0"]])
                P.copy("pool", Wbdv, Xbfv[:, :, 0:128], [Xbf], [d["Wbd"]])
                br = bank("rw")
                for c in range(NCH):
                    P.mm(br[:, c * LCH:(c + 1) * LCH], d["Wbd"][:, c * 128:(c + 1) * 128],
                         d["ArbT"][:, c * LCH:(c + 1) * LCH], True, True, [d["Wbd"], d["ArbT"]], [br])
                P.tt("dve", d["Rhat"][:], d["rt32"][:], br[:, 0:TT], ALU.subtract, [d["rt32"], br], [d["Rhat"]])
                bp = bank("rw")
                for c in range(NCH):
                    cs = slice(c * 128, (c + 1) * 128)
                    P.mm(bp[:, cs], d["Wbd"][:, cs], d["BGbd"][:, cs], True, True, [d["Wbd"], d["BGbd"]], [bp])
                for c in range(NCH):
                    cs = slice(c * 128, (c + 1) * 128)
                    P.stt(d["PT"][:, cs], ident_f, d["GL"][:, c:c + 1], bp[:, cs], ALU.mult, ALU.subtract,
                          [cst, d["GL"], bp], [d["PT"]])
                yield

            def seq(pb, inst, c, l=l):
                d = pp[inst]
                T = Tst[l][pb]
                yb = banks[6]
                tb_ = banks[4 + inst]
                ycols = slice(inst * TT + c * LCH, inst * TT + (c + 1) * LCH)
                cs = slice(c * 128, (c + 1) * 128)
                cl_ = slice(c * LCH, (c + 1) * LCH)
                P.mm(yb[:, ycols], T[:], d["Rhat"][:, cl_], True, False, [T, d["Rhat"]], [yb])
                P.mm(yb[:, ycols], d["Vbd"][:, cs], d["ArkT"][:, cl_], False, False, [d["Vbd"], d["ArkT"]], [yb])
                P.mm(yb[:, ycols], d["nU0"][:, cs], d["ArbT"][:, cl_], False, True, [d["nU0"], d["ArbT"]], [yb])
                P.mm(tb_[:, 0:128], d["PT"][:, cs], T[:], True, False, [d["PT"], T], [tb_])
                P.mm(tb_[:, 0:128], d["KGbd"][:, cs], d["Vbd"][:, cs], False, False, [d["KGbd"], d["Vbd"]], [tb_])
                P.mm(tb_[:, 0:128], d["BGbd"][:, cs], d["nU0"][:, cs], False, True, [d["BGbd"], d["nU0"]], [tb_])
                P.copy("act", T[:], tb_[:, 0:128], [tb_], [T])

            def fin(pb, inst, l=l, dbg_on=dbg_on):
                d = pp[inst]
                yb = banks[6]
                y32, yc, ysq, rs = fs[0], fs[1], fs[2], fs[3]
                P.copy("act", y32[:], yb[:, inst * TT:(inst + 1) * TT], [yb], [y32])
                if dbg_on:
                    dbg_dump("y_rw%d" % pb, y32, y32[:], [128, TT])
                bm = bank("rw")
                P.mm(bm[:, 0:TT], onesbd_f, y32[:], True, True, [cst, y32], [bm])
                P.stt(yc[:], bm[:, 0:TT], -1.0 / 64, y32[:], ALU.mult, ALU.add, [bm, y32], [yc])
                P.act(ysq[:], yc[:], AF.Square, [yc], [ysq])
                bv = bank("rw")
                P.mm(bv[:, 0:TT], onesbd_f, ysq[:], True, True, [cst, ysq], [bv])
                P.act(rs[:], bv[:, 0:TT], AF.Sqrt, [bv, eps_t[GN_EPS]], [rs], bias=eps_t[GN_EPS][:, 0:1], scale=1.0 / 64)
                P.recip(rs[:], rs[:], [rs], [rs])
                P.tt("dve", yc[:], yc[:], rs[:], ALU.mult, [yc, rs], [yc])
                P.ts("dve", yc[:], yc[:], V(l, "lnw", pb), V(l, "lnb", pb), ALU.mult, ALU.add, [yc, vec], [yc])
                P.tt("pool", yc[:], yc[:], d["bonus"][:], ALU.add, [yc, d["bonus"]], [yc])
                P.tt("dve", ycat[:, pb, :], yc[:], sgate[:, pb, :], ALU.mult, [yc, sgv[pb]], [ycv[pb]])

            def rwkv_gen():
                for half in range(2):
                    pbs = (2 * half, 2 * half + 1)
                    def chain(pb, inst):
                        yield from prep(pb, inst)
                        yield from solve(pb, inst)
                    gA, gB = chain(pbs[0], 0), chain(pbs[1], 1)
                    for v in gA:
                        yield
                        if v == "tok_done":
                            break
                    doneA = doneB = False
                    while not (doneA and doneB):
                        if not doneA:
                            try:
                                next(gA)
                                yield
                            except StopIteration:
                                doneA = True
                        if not doneB:
                            try:
                                next(gB)
                                yield
                            except StopIteration:
                                doneB = True
                    for c in range(NCH):
                        for inst, pb in enumerate(pbs):
                            seq(pb, inst, c)
                            yield
                    for inst, pb in enumerate(pbs):
                        fin(pb, inst)
                        yield

            def s5_gen(l=l, dbg_on=dbg_on):
                Bv = s5B[l][:, :].rearrange("p (r b q m) -> p r b q m", r=2, b=4, q=2)
                Cv = s5C[l][:, :].rearrange("p (r j m) -> p r j m", r=2, j=16)
                rotk, keep = s5rot[l]
                NS = TT // TS5
                yb5 = banks[7]
                for blk in range(4):
                    for s in range(NS):
                        tsl = slice(s * TS5, (s + 1) * TS5)
                        for jj in range(4):
                            j = blk * 4 + jj
                            hf, jh = jj // 2, jj % 2
                            hs = slice(64 * hf, 64 * hf + 64)
                            k = s5ctr[0] % NS5SET
                            s5ctr[0] += 1
                            (t1, t2, bzr, bzi, zr, zi, t3, t4) = s5f[k]
                            sx = s5x[k]
                            zl = s5zl[k]
                            zv = s5zv[l][j]
                            bu = banks[0]
                            c0 = (k % 2) * 2 * TS5
                            bre = bu[:, c0:c0 + TS5]
                            bim = bu[:, c0 + TS5:c0 + 2 * TS5]
                            P.mm(bre, Bv[hs, 0, blk, jh, :], ubf[hs, blk, tsl], True, True, [s5B[l], ubf], [bu])
                            P.mm(bim, Bv[hs, 1, blk, jh, :], ubf[hs, blk, tsl], True, True, [s5B[l], ubf], [bu])
                            cosT = s5tab[:, 0, j, :]
                            sinT = s5tab[:, 1, j, :]
                            P.tt("dve", t1[:], bre, cosT, ALU.mult, [bu, s5tab], [t1])
                            P.tt("dve", t2[:], bim, sinT, ALU.mult, [bu, s5tab], [t2])
                            P.tt("dve", t3[:], bim, cosT, ALU.mult, [bu, s5tab], [t3])
                            P.tt("dve", t4[:], bre, sinT, ALU.mult, [bu, s5tab], [t4])
                            P.tt("pool", bzr[:], t1[:], t2[:], ALU.add, [t1, t2], [bzr])
                            P.tt("pool", bzi[:], t3[:], t4[:], ALU.subtract, [t3, t4], [bzi])
                            yield
                            rho_b = keep[:, 0, j:j + 1].to_broadcast([128, TS5])
                            P.scan(zr[:], rho_b, bzr[:], s5z[l][:, 0, j:j + 1], [keep, bzr, zv], [zr])
                            P.scan(zi[:], rho_b, bzi[:], s5z[l][:, 1, j:j + 1], [keep, bzi, zv], [zi])
                            rc = rotk[:, 0, j:j + 1]
                            rs_ = rotk[:, 1, j:j + 1]
                            zlr = zr[:, TS5 - 1:TS5]
                            zli = zi[:, TS5 - 1:TS5]
                            P.ts("dve", zl[:, 0:1], zli, rs_, None, ALU.mult, None, [zi, rotk], [zl])
                            P.ts("dve", zl[:, 1:2], zlr, rs_, None, ALU.mult, None, [zr, rotk], [zl])
                            P.stt(s5z[l][:, 0, j:j + 1], zlr, rc, zl[:, 0:1], ALU.mult, ALU.subtract, [zr, rotk, zl], [zv])
                            P.stt(s5z[l][:, 1, j:j + 1], zli, rc, zl[:, 1:2], ALU.mult, ALU.add, [zi, rotk, zl], [zv])
                            yield
                            P.tt("dve", t1[:], zr[:], cosT, ALU.mult, [zr, s5tab], [t1])
                            P.tt("dve", t2[:], zi[:], sinT, ALU.mult, [zi, s5tab], [t2])
                            P.tt("pool", sx[:, 0, :], t1[:], t2[:], ALU.subtract, [t1, t2], [sx])
                            P.tt("dve", t3[:], zr[:], sinT, ALU.mult, [zr, s5tab], [t3])
                            P.tt("pool", t4[:], zi[:], cosT, ALU.mult, [zi, s5tab], [t4])
                            P.tt("pool", sx[:, 1, :], t3[:], t4[:], ALU.add, [t3, t4], [sx])
                            P.mm(yb5[hs, tsl], Cv[:, 0, j, :], sx[:, 0, :], jh == 0, False, [s5C[l], sx], [yb5])
                            P.mm(yb5[hs, tsl], Cv[:, 1, j, :], sx[:, 1, :], False, jh == 1, [s5C[l], sx], [yb5])
                            yield
                    ys, x2, q_ = spf
                    P.stt(ys[:], zu[:, blk, :], V(l, "s5d", blk), yb5[:, 0:TT], ALU.mult, ALU.add, [zu, vec, yb5], [ys])
                    if dbg_on:
                        dbg_dump("s5y%d" % blk, ys, ys[:], [128, TT])
                    P.act(x2[:], ys[:], AF.Square, [ys], [x2])
                    P.ts("dve", x2[:], x2[:], 0.044715, 1.0, ALU.mult, ALU.add, [x2], [x2])
                    P.tt("pool", q_[:], x2[:], ys[:], ALU.mult, [x2, ys], [q_])
                    P.act(x2[:], q_[:], AF.Sigmoid, [q_], [x2], scale=2.0 * math.sqrt(2.0 / math.pi))
                    P.tt("dve", mix[:, blk, :], ys[:], x2[:], ALU.mult, [ys, x2], [mixv[blk]])
                    yield
                zgb = ubf
                P.copy("pool", zgb[:], mix[:], mixv, [zgb])
                rg = next_ring()
                P.dma(rg[:, 0:2048].rearrange("p (k n) -> p k n", k=4), glu_w_b[l].rearrange("(k p) n -> p k n", p=128),
                      reads=[wsb], writes=[rg])
                rgv = rg[:, 0:2048].rearrange("p (k n) -> p k n", k=4)
                for ob in range(4):
                    b = banks[0]
                    for k in range(4):
                        P.mm(b[:, 0:TT], rgv[:, k, ob * 128:(ob + 1) * 128], zgb[:, k, :], k == 0, k == 3, [rg, zgb], [b])
                    sg_ = spf[ob % 2]
                    P.act(sg_[:], b[:, 0:TT], AF.Sigmoid, [b, vec], [sg_], bias=V(l, "glub", ob))
                    P.tt("pool", sg_[:], sg_[:], sgate[:, 4 + ob, :], ALU.mult, [sg_, sgv[4 + ob]], [sg_])
                    P.tt("dve", ycat[:, 4 + ob, :], mix[:, ob, :], sg_[:], ALU.mult, [mixv[ob], sg_], [ycv[4 + ob]])
                    yield

            def lru_gen(l=l, dbg_on=dbg_on):
                for blk in range(4):
                    A, B, C, Dd = lf
                    bl = banks[1]
                    P.ts("dve", A[:], zx[:, blk, 0:TT], V(l, "cw", 0 * 4 + blk), V(l, "cb", blk), ALU.mult, ALU.add,
                         [zx, vec], [A])
                    for j in range(1, 4):
                        P.stt(A[:], zx[:, blk, j:j + TT], V(l, "cw", j * 4 + blk), A[:], ALU.mult, ALU.add,
                              [zx, vec, A], [A])
                    P.copy("act", lb[:], A[:], [A], [lb])
                    yield
                    P.mm(bl[:, 0:TT], lruw[l][:, blk * 128:(blk + 1) * 128], lb[:], True, True, [lruw[l], lb], [bl])
                    P.mm(bl[:, TT:2 * TT], lruw[l][:, (4 + blk) * 128:(5 + blk) * 128], lb[:], True, True,
                         [lruw[l], lb], [bl])
                    P.act(B[:], bl[:, 0:TT], AF.Sigmoid, [bl, vec], [B], bias=V(l, "ba", blk))
                    P.act(C[:], bl[:, TT:2 * TT], AF.Sigmoid, [bl, vec], [C], bias=V(l, "bx", blk))
                    yield
                    P.act(Dd[:], B[:], AF.Exp, [B, lru_c], [Dd], scale=lru_c[:, l, blk:blk + 1])
                    P.act(B[:], B[:], AF.Exp, [B, lru_c], [B], scale=lru_c[:, l, 4 + blk:5 + blk])
                    P.act(B[:], B[:], AF.Sqrt, [B, one_t], [B], bias=one_t[:, 0:1], scale=-1.0)
                    P.tt("pool", C[:], C[:], A[:], ALU.mult, [C, A], [C])
                    P.tt("pool", C[:], C[:], B[:], ALU.mult, [C, B], [C])
                    yield
                    P.scan(A[:], Dd[:], C[:], ch[l][:, blk:blk + 1], [Dd, C, ch[l]], [A])
                    P.copy("act", ch[l][:, blk:blk + 1], A[:, TT - 1:TT], [A], [ch[l]])
                    if dbg_on:
                        dbg_dump("lru%d" % blk, A, A[:], [128, TT])
                    P.tt("pool", ycat[:, 8 + blk, :], A[:], sgate[:, 8 + blk, :], ALU.mult, [A, sgv[8 + blk]], [ycv[8 + blk]])
                    yield

            gens = []
            if "rwkv" not in SKIP:
                gens.append((rwkv_gen(), 3))
            if "s5" not in SKIP:
                gens.append((s5_gen(), 2))
            if "lru" not in SKIP:
                gens.append((lru_gen(), 1))
            if "serial" in SKIP:
                for g, w in gens:
                    drive([(g, 1)])
            else:
                drive(gens)
            if dbg_on:
                dbg_dump("ycat_rw", ycat, ycat[:, 0:4, :].rearrange("p a t -> p (a t)"), [128, 4 * TT], BF16)

            for oc in range(4):
                r = next_ring()
                P.dma(r[:, 0:12 * 256].rearrange("p (k n) -> p k n", k=12),
                      w_out_b[l].rearrange("(k p) n -> p k n", p=128)[:, :, oc * 256:(oc + 1) * 256],
                      reads=[wsb], writes=[r])
                rv = r[:, 0:12 * 256].rearrange("p (k n) -> p k n", k=12)
                for ob2 in range(2):
                    ob = oc * 2 + ob2
                    b = bank("proj")
                    for k in range(12):
                        P.mm(b[:, 0:TT], rv[:, k, ob2 * 128:(ob2 + 1) * 128], ycat[:, k, :], k == 0, k == 11,
                             [r] + ycv, [b])
                    P.tt("dve", hT[:, ob, :], hT[:, ob, :], b[:, 0:TT], ALU.add, [hT, b], [hT])
            r = next_ring()
            P.dma(r[:, 0:2048].rearrange("p (k n) -> p k n", k=2), ple_w_b[l].rearrange("(k p) n -> p k n", p=128),
                  reads=[wsb], writes=[r])
            rv = r[:, 0:2048].rearrange("p (k n) -> p k n", k=2)
            epre = zs
            for ob in range(8):
                b = bank("proj")
                for k in range(2):
                    P.mm(b[:, 0:TT], rv[:, k, ob * 128:(ob + 1) * 128], pbf[:, k, :], k == 0, k == 1, [r, pbf], [b])
                P.copy("act", epre[:, ob, :], b[:, 0:TT], [b], [zsv[ob]])
            rms_rstd(zsv[0:8], lambda k: epre[:, k, :], 8, NORM_EPS)
            P.copy("pool", xn[:], hT[:], [hT], [xn])
            for gc in range(2):
                r = next_ring()
                P.dma(r[:, 0:4096].rearrange("p (k n) -> p k n", k=8),
                      ple_gw_b[l].rearrange("(k p) n -> p k n", p=128)[:, :, gc * 512:(gc + 1) * 512],
                      reads=[wsb], writes=[r])
                rv = r[:, 0:4096].rearrange("p (k n) -> p k n", k=8)
                for ob2 in range(4):
                    ob = gc * 4 + ob2
                    b = bank("proj")
                    for k in range(8):
                        P.mm(b[:, 0:TT], rv[:, k, ob2 * 128:(ob2 + 1) * 128], xn[:, k, :], k == 0, k == 7, [r, xn], [b])
                    sg_, e_ = fs[6 + 2 * (ob % 2)], fs[7 + 2 * (ob % 2)]
                    P.act(sg_[:], b[:, 0:TT], AF.Sigmoid, [b], [sg_])
                    P.stt(e_[:], epre[:, ob, :], V(l, "png", ob), rstd[:], ALU.mult, ALU.mult, [zsv[ob], vec, rstd], [e_])
                    P.tt("pool", e_[:], e_[:], sg_[:], ALU.mult, [e_, sg_], [e_])
                    P.tt("dve", hT[:, ob, :], hT[:, ob, :], e_[:], ALU.add, [hT, e_], [hT])
            if dbg_on:
                dbg_dump("h1", hT, hT[:, :, :].rearrange("p a t -> p (a t)"), [128, 8 * TT])
        rms_rstd([hT], lambda k: hT[:, k, :], 8, NORM_EPS)
        fo = 2 * VEC_PER_LAYER
        for k in range(8):
            P.stt(zs[:, k, :], hT[:, k, :], vec[:, fo + k:fo + k + 1], rstd[:], ALU.mult, ALU.mult, [hT, vec, rstd], [zsv[k]])
        out_toks.append(P.dma(oT[:, t0:t0 + TT].rearrange("(k p) t -> p k t", p=128), zs[:, 0:8, :], reads=zsv[0:8]))
    out_toks.extend(dbg_out.values())
    ninst = P.ninst
    P.finish(out_toks)
    return nc, ninst


def pack_shared(inp):
    f = lambda a: np.asarray(a, np.float32)
    vec = np.zeros((128, NVEC), np.float32)
    for l in range(2):
        def put(name, arr, n):
            c = vcol(l, name)
            vec[:, c:c + n] = _pp(arr, n)
        put("ng", f(inp["norm_g"])[l], 8)
        put("mu", f(inp["rwkv_mu"])[l], 13)
        put("w0", f(inp["rwkv_w0"])[l], 4)
        put("a0", f(inp["rwkv_a0"])[l], 4)
        put("kk", f(inp["rwkv_k_k"])[l], 4)
        put("ka", f(inp["rwkv_k_a"])[l], 4)
        put("rk", f(inp["rwkv_r_k"])[l].reshape(512), 4)
        put("lnw", f(inp["rwkv_ln_w"])[l], 4)
        put("lnb", f(inp["rwkv_ln_b"])[l], 4)
        put("s5d", f(inp["s5_d"])[l], 4)
        put("glub", f(inp["s5_glu_b"])[l], 4)
        cw = f(inp["lru_conv_w"])[l]
        c = vcol(l, "cw")
        for j in range(4):
            vec[:, c + 4 * j:c + 4 * j + 4] = _pp(cw[j], 4)
        put("cb", f(inp["lru_conv_b"])[l], 4)
        put("ba", f(inp["lru_ba"])[l], 4)
        put("bx", f(inp["lru_bx"])[l], 4)
        put("lam", f(inp["lru_lambda"])[l], 4)
        put("png", f(inp["ple_norm_g"])[l], 8)
    vec[:, 2 * VEC_PER_LAYER:2 * VEC_PER_LAYER + 8] = _pp(f(inp["final_norm_g"]), 8)

    w2a2 = np.zeros((2, 128, 512), np.float32)
    w2a2[:, 0:64] = f(inp["rwkv_w2"])
    w2a2[:, 64:128] = f(inp["rwkv_a2"])
    lruw = np.zeros((2, 128, 8, 128), np.float32)
    for l in range(2):
        for m, key in enumerate(("lru_wa", "lru_wx")):
            w = f(inp[key])[l]
            for q in range(4):
                for b2 in range(2):
                    lruw[l, b2 * 64:(b2 + 1) * 64, m * 4 + q, b2 * 64:(b2 + 1) * 64] = w[2 * q + b2]
    lruw = lruw.reshape(2, 128, 1024)
    def modes(a):
        a = f(a).reshape(2, 16, 2, 64)
        return np.ascontiguousarray(a.transpose(0, 2, 3, 1).reshape(2, 128, 16))
    s5s = np.zeros((2, 128, 3, 16), np.float32)
    s5s[:, :, 0] = modes(inp["s5_a_re"])
    s5s[:, :, 1] = modes(inp["s5_a_im"])
    ldt = np.broadcast_to(f(inp["s5_log_dt"])[:, :, None], (2, 32, 64))
    s5s[:, :, 2] = modes(ldt)
    s5s = s5s.reshape(2, 128, 48)
    def bmodes(a):
        a = f(a).reshape(2, 16, 2, 64, 16)
        return a.transpose(0, 2, 3, 1, 4).reshape(2, 128, 16, 16)
    s5b = np.stack([bmodes(inp["s5_b_re"]), bmodes(inp["s5_b_im"])], axis=2).reshape(2, 128, 512)
    s5c = np.zeros((2, 128, 2, 16, 64), np.float32)
    for ri, key in enumerate(("s5_c_re", "s5_c_im")):
        c = f(inp[key]).reshape(2, 16, 2, 16, 64)
        for gh in range(2):
            for jh in range(2):
                c0 = 32 * jh + 16 * gh
                s5c[:, gh * 64:(gh + 1) * 64, ri, jh::2, c0:c0 + 16] = c[:, jh::2, gh].transpose(0, 3, 1, 2)
    s5c = s5c.reshape(2, 128, 2048)
    return {
        "w_in": np.ascontiguousarray(f(inp["w_in"])), "w_out": np.ascontiguousarray(f(inp["w_out"])),
        "ple_w": np.ascontiguousarray(f(inp["ple_w"])), "ple_gw": np.ascontiguousarray(f(inp["ple_gate_w"])),
        "glu_w": np.ascontiguousarray(f(inp["s5_glu_w"])), "vec": vec, "cst": make_consts()[0], "msk": make_consts()[1],
        "w2a2": w2a2, "lruw": np.ascontiguousarray(lruw), "s5s": np.ascontiguousarray(s5s),
        "s5b": np.ascontiguousarray(s5b), "s5c": np.ascontiguousarray(s5c),
    }


_NC_CACHE = {}


def run_cores(inp, TC, batches, dbg=None):
    key = (TC, tuple(sorted(dbg)) if dbg else None)
    if key not in _NC_CACHE:
        _NC_CACHE[key] = build_nc(TC, dbg)
    nc, ninst = _NC_CACHE[key]
    shared = pack_shared(inp)
    x = np.asarray(inp["x"], np.float32)
    p = np.asarray(inp["p"], np.float32)
    in_maps = []
    for b in batches:
        m = dict(shared)
        m["xT"] = np.ascontiguousarray(x[b, :TC].T)
        m["pT"] = np.ascontiguousarray(p[:, b, :TC].transpose(0, 2, 1))
        in_maps.append(m)
    res = run_bass_kernel_spmd(nc, in_maps, core_ids=list(range(len(batches))))
    return res


def kernel(**inputs):
    x = np.asarray(inputs["x"])
    B, S, _ = x.shape
    batches = [c % B for c in range(8)]
    res = run_cores(inputs, S, batches)
    out = np.empty((B, S, D), np.float32)
    for b in range(B):
        out[b] = res.results[b]["oT"].T
    return out.astype(x.dtype)
```

```python
import contextlib
import math
import numpy as np
import concourse.bass as bass
import concourse.mybir as mybir
from concourse.bass_utils import run_bass_kernel_spmd

F32 = mybir.dt.float32
BF16 = mybir.dt.bfloat16
ALU = mybir.AluOpType
AF = mybir.ActivationFunctionType

D = 1024
DIN = 4224
DMIX = 1536
DPLE = 256
TT = 256
LCH = 64
NCH = TT // LCH
TS5 = 128
import os
SKIP = set(os.environ.get("KSKIP", "").split(","))
S5E = os.environ.get("KS5E", "pool")
GW = tuple(int(v) for v in os.environ.get("KGW", "1,1,1").split(","))
GN_EPS = 64e-5
NORM_EPS = 1e-6


class Tok:
    __slots__ = ("sem", "val", "eng", "dma")

    def __init__(self, sem, val, eng, dma):
        self.sem, self.val, self.eng, self.dma = sem, val, eng, dma


class Buf:
    def __init__(self, t, name):
        self.t = t
        self.name = name
        self.w = None
        self.r = []

    def __getitem__(self, idx):
        return self.t[idx]


class Prog:
    ENGS = ("pe", "act", "dve", "pool", "sp")

    def __init__(self, nc, n_dma_sems=32):
        self.nc = nc
        self.es = contextlib.ExitStack()
        self.ops = {e: [] for e in self.ENGS}
        self.cnt = {e: 0 for e in self.ENGS}
        self.sem = {e: self.es.enter_context(nc.semaphore("s_" + e)) for e in self.ENGS}
        self.dsem = [self.es.enter_context(nc.semaphore("d%d" % i)) for i in range(n_dma_sems)]
        self.duse = [0] * n_dma_sems
        self.dnext = 0
        self.seen = {e: {} for e in self.ENGS}
        self.nbuf = 0
        self.ninst = 0
        self.stack = [self.es]

    def push(self):
        st = contextlib.ExitStack()
        self.stack.append(st)

    def pop(self):
        self.barrier()
        self.stack.pop().close()

    def barrier(self):
        toks = []
        for f in self.ENGS:
            if self.cnt[f] > 0:
                toks.append(Tok(self.sem[f], self.cnt[f], f, False))
        for i, s in enumerate(self.dsem):
            if self.duse[i] > 0:
                toks.append(Tok(s, 16 * self.duse[i], "dma", True))
        for e in self.ENGS:
            wl = []
            for t in toks:
                if t.eng == e and not t.dma:
                    continue
                k = id(t.sem)
                if self.seen[e].get(k, 0) >= t.val:
                    continue
                self.seen[e][k] = t.val
                wl.append((t.sem, t.val))

            def run(en, wl=wl):
                for (s, v) in wl:
                    en.wait_ge(s, v)
            self.ops[e].append(run)

    def sb(self, shape, dt=F32, name=None):
        self.nbuf += 1
        name = name or ("b%d" % self.nbuf)
        t = self.stack[-1].enter_context(self.nc.sbuf_tensor("sb_" + name, list(shape), dt))
        return Buf(t, name)

    def ps(self, name, dt=F32, cols=512):
        t = self.es.enter_context(self.nc.psum_tensor(name, [128, cols], dt))
        return Buf(t, name)

    def wrap(self, t, name):
        return Buf(t, name)

    def views(self, buf, n):
        return [Buf(buf.t, "%s.v%d" % (buf.name, i)) for i in range(n)]

    def _need(self, eng, tok, waits, is_dma_issue):
        if tok is None:
            return
        if tok.eng == eng and not tok.dma and not is_dma_issue and eng == "pe":
            return
        k = id(tok.sem)
        if self.seen[eng].get(k, 0) >= tok.val:
            return
        cur = waits.get(k)
        if cur is None or cur[1] < tok.val:
            waits[k] = (tok.sem, tok.val)

    def emit(self, eng, fn, reads=(), writes=(), dma=False):
        waits = {}
        for b in reads:
            self._need(eng, b.w, waits, dma)
        for b in writes:
            self._need(eng, b.w, waits, dma)
            for t in b.r:
                self._need(eng, t, waits, dma)
        if dma:
            i = self.dnext
            self.dnext = (self.dnext + 1) % len(self.dsem)
            s = self.dsem[i]
            if self.duse[i] > 0:
                self._need(eng, Tok(s, 16 * self.duse[i], "dma", True), waits, True)
            self.duse[i] += 1
            tok = Tok(s, 16 * self.duse[i], "dma", True)
            inc = 16
        else:
            self.cnt[eng] += 1
            tok = Tok(self.sem[eng], self.cnt[eng], eng, False)
            inc = 1
        wl = list(waits.values())
        for (s, v) in wl:
            self.seen[eng][id(s)] = v
        tsem = tok.sem
        self.ninst += 1 + len(wl)

        def run(e, wl=wl, fn=fn, tsem=tsem, inc=inc):
            for (s, v) in wl:
                e.wait_ge(s, v)
            fn(e).then_inc(tsem, inc)

        self.ops[eng].append(run)
        for b in reads:
            b.r.append(tok)
            if len(b.r) > 48:
                b.r = b.r[-48:]
        for b in writes:
            b.w = tok
            b.r = []
        return tok

    def finish(self, out_toks):
        wl = [(t.sem, t.val) for t in out_toks]

        def run(e, wl=wl):
            for (s, v) in wl:
                e.wait_ge(s, v)

        self.ops["sp"].append(run)
        nc = self.nc
        ops = self.ops
        with nc.Block() as block:
            @block.tensor
            def _(e):
                for f in ops["pe"]:
                    f(e)

            @block.scalar
            def _(e):
                for f in ops["act"]:
                    f(e)

            @block.vector
            def _(e):
                for f in ops["dve"]:
                    f(e)

            @block.gpsimd
            def _(e):
                for f in ops["pool"]:
                    f(e)

            @block.sync
            def _(e):
                for f in ops["sp"]:
                    f(e)
        self.es.close()

    def dma(self, out, in_, reads=(), writes=(), eng="sp", **kw):
        return self.emit(eng, lambda e: e.dma_start(out=out, in_=in_, **kw), reads, writes, dma=True)

    def mm(self, out, lhsT, rhs, start, stop, reads, writes):
        return self.emit("pe", lambda e: e.matmul(out, lhsT, rhs, start=start, stop=stop), reads, writes)

    def act(self, out, in_, func, reads, writes, bias=None, scale=None):
        kw = {}
        if bias is not None:
            kw["bias"] = bias
        if scale is not None:
            kw["scale"] = scale
        return self.emit("act", lambda e: e.activation(out=out, in_=in_, func=func, **kw), reads, writes)

    def tt(self, eng, out, in0, in1, op, reads, writes):
        return self.emit(eng, lambda e: e.tensor_tensor(out=out, in0=in0, in1=in1, op=op), reads, writes)

    def ts(self, eng, out, in0, s1, s2, op0, op1, reads, writes):
        if op1 is None:
            return self.emit(eng, lambda e: e.tensor_scalar(out, in0, s1, None, op0), reads, writes)
        return self.emit(eng, lambda e: e.tensor_scalar(out, in0, s1, s2, op0, op1), reads, writes)

    def stt(self, out, in0, scalar, in1, op0, op1, reads, writes):
        return self.emit("dve", lambda e: e.scalar_tensor_tensor(out, in0, scalar, in1, op0, op1), reads, writes)

    def copy(self, eng, out, in_, reads, writes):
        if eng == "act":
            return self.emit("act", lambda e: e.activation(out=out, in_=in_, func=AF.Copy), reads, writes)
        return self.emit(eng, lambda e: e.tensor_copy(out, in_), reads, writes)

    def memset(self, eng, ap, val, writes):
        return self.emit(eng, lambda e: e.memset(ap, val), (), writes)

    def scan(self, out, d0, d1, init, reads, writes):
        return self.emit("dve", lambda e: e.tensor_tensor_scan(out, d0, d1, init, ALU.mult, ALU.add), reads, writes)

    def recip(self, out, in_, reads, writes):
        return self.emit("dve", lambda e: e.reciprocal(out, in_), reads, writes)


VEC_FIELDS = [("ng", 8), ("mu", 13), ("w0", 4), ("a0", 4), ("kk", 4), ("ka", 4), ("rk", 4), ("lnw", 4),
              ("lnb", 4), ("s5d", 4), ("glub", 4), ("cw", 16), ("cb", 4), ("ba", 4), ("bx", 4), ("lam", 4),
              ("png", 8)]
VEC_PER_LAYER = sum(n for _, n in VEC_FIELDS)
VEC_OFF = {}
_o = 0
for _n, _c in VEC_FIELDS:
    VEC_OFF[_n] = _o
    _o += _c
NVEC = 2 * VEC_PER_LAYER + 8

CST_IDENT = 0
CST_ONESBD = 128
CST_SCAN = 256
NCST = 256 + TT
MSK_USN = 0
MSK_LSN = NCH * 128
MSK_USP = 2 * NCH * 128
MSK_CI = 3 * NCH * 128
NMSK = 3 * NCH * 128 + NCH * LCH


def vcol(l, name, i=0):
    return l * VEC_PER_LAYER + VEC_OFF[name] + i


def _pp(v, n):
    return np.ascontiguousarray(np.asarray(v, np.float32).reshape(n, 128).T)


def make_consts():
    c = np.zeros((128, NCST), np.float32)
    i = np.arange(128)[:, None]
    j = np.arange(128)[None, :]
    c[:, CST_IDENT:CST_IDENT + 128] = (i == j)
    c[:, CST_ONESBD:CST_ONESBD + 128] = ((i // 64) == (j // 64))
    tt = np.arange(TT)[None, :]
    c[:, CST_SCAN:CST_SCAN + TT] = 1.0 * ((tt % LCH) != 0)
    m = np.zeros((128, NMSK), np.float32)
    t = np.arange(64)[None, :]
    for ch in range(NCH):
        m[:, MSK_USN + ch * 128:MSK_USN + (ch + 1) * 128] = -1.0 * (j > i)
        m[:, MSK_LSN + ch * 128:MSK_LSN + (ch + 1) * 128] = -1.0 * (i > j)
        m[:, MSK_USP + ch * 128:MSK_USP + (ch + 1) * 128] = 1.0 * (j > i)
        m[:, MSK_CI + ch * 64:MSK_CI + (ch + 1) * 64] = 1.0 * (t >= (i % 64))
    return c, m


def build_nc(TC, dbg=None):
    assert TC % TT == 0
    NT = TC // TT
    dbg = dbg or set()
    nc = bass.Bass("TRN2", target_bir_lowering=False)

    def din(name, shape, dt=F32):
        return nc.dram_tensor(name, list(shape), dt, kind="ExternalInput").ap()

    xT = din("xT", [D, TC])
    pT = din("pT", [2, DPLE, TC])
    w_in = din("w_in", [2, D, DIN])
    w_out = din("w_out", [2, DMIX, D])
    ple_w = din("ple_w", [2, DPLE, D])
    ple_gw = din("ple_gw", [2, D, D])
    glu_w = din("glu_w", [2, 512, 512])
    vec_d = din("vec", [128, NVEC])
    cst_d = din("cst", [128, NCST])
    msk_d = din("msk", [128, NMSK])
    w2a2_d = din("w2a2", [2, 128, 512])
    lruw_d = din("lruw", [2, 128, 8 * 128])
    s5s_d = din("s5s", [2, 128, 3 * 16])
    s5b_d = din("s5b", [2, 128, 2 * 16 * 16])
    s5c_d = din("s5c", [2, 128, 2 * 16 * 64])
    oT = nc.dram_tensor("oT", [D, TC], F32, kind="ExternalOutput").ap()
    dbg_out = {}

    def dram_int(name, shape, dt):
        return nc.dram_tensor(name, list(shape), dt, kind="Internal").ap()

    w_in_b = dram_int("w_in_b", [2, D, DIN], BF16)
    w_out_b = dram_int("w_out_b", [2, DMIX, D], BF16)
    ple_w_b = dram_int("ple_w_b", [2, DPLE, D], BF16)
    ple_gw_b = dram_int("ple_gw_b", [2, D, D], BF16)
    glu_w_b = dram_int("glu_w_b", [2, 512, 512], BF16)
    s5tab_d = dram_int("s5tab", [2, 128, 2 * 16 * TS5], F32)

    P = Prog(nc)
    wsb = P.wrap(None, "wscratch")
    tabsb = P.wrap(None, "s5tabscr")

    def dbg_dump(name, buf, ap, shape, dt=F32):
        if name not in dbg:
            return
        o = nc.dram_tensor("dbg_" + name, list(shape), dt, kind="ExternalOutput").ap()
        dbg_out[name] = P.dma(o, ap, reads=[buf])

    vec = P.sb([128, NVEC], F32, "vec")
    cst = P.sb([128, NCST], F32, "cst")
    P.dma(vec[:], vec_d, writes=[vec])
    P.dma(cst[:], cst_d, writes=[cst])
    cstb = P.sb([128, 128], BF16, "cstb")
    P.copy("dve", cstb[:], cst[:, CST_IDENT:CST_IDENT + 128], [cst], [cstb])
    mskb = P.sb([128, NMSK], BF16, "mskb")
    ident_f = cst[:, CST_IDENT:CST_IDENT + 128]
    ident_b = cstb[:, 0:128]
    onesbd_f = cst[:, CST_ONESBD:CST_ONESBD + 128]
    ones_f = P.sb([128, 128], F32, "ones_f")
    P.memset("pool", ones_f[:], 1.0, [ones_f])
    one_t = P.sb([128, 1], F32, "one_t")
    P.memset("pool", one_t[:], 1.0, [one_t])

    def V(l, name, i=0, n=1):
        c = vcol(l, name, i)
        return vec[:, c:c + n]

    for l in range(2):
        for (src, dst, rows) in ((w_in, w_in_b, D), (w_out, w_out_b, DMIX), (ple_w, ple_w_b, DPLE),
                                 (ple_gw, ple_gw_b, D), (glu_w, glu_w_b, 512)):
            for r0 in range(0, rows, 128):
                P.dma(dst[l, r0:r0 + 128, :], src[l, r0:r0 + 128, :], writes=[wsb], eng="pool",
                      max_dma_last_dim=4096)

    w2a2 = []
    lruw = []
    for l in range(2):
        w2a2.append(P.sb([128, 512], BF16, "w2a2_%d" % l))
        lruw.append(P.sb([128, 1024], BF16, "lruw_%d" % l))
    lru_c = P.sb([128, 2, 8], F32, "lru_c")
    s5B = [P.sb([128, 2 * 4 * 2 * 128], BF16, "s5B%d" % l) for l in range(2)]
    s5C = [P.sb([128, 2048], BF16, "s5C%d" % l) for l in range(2)]
    s5keep = [P.sb([128, 3, 16], F32, "s5keep%d" % l) for l in range(2)]
    s5rotb = [P.sb([128, 2, 16], F32, "s5rot%d" % l) for l in range(2)]
    ps_misc = P.ps("ps7")
    P.push()
    stage = P.sb([128, 1024], F32, "stage")
    mstage = P.sb([128, NMSK], F32, "mstage")
    P.dma(mstage[:], msk_d, writes=[mstage])
    P.copy("act", mskb[:], mstage[:], [mstage], [mskb])
    for l in range(2):
        P.dma(stage[:, 0:512], w2a2_d[l], writes=[stage])
        P.copy("act", w2a2[l][:], stage[:, 0:512], [stage], [w2a2[l]])
        P.dma(stage[:], lruw_d[l], writes=[stage])
        P.copy("act", lruw[l][:], stage[:], [stage], [lruw[l]])

    for l in range(2):
        tmp = P.sb([128, 4], F32, "lrutmp%d" % l)
        P.act(tmp[:], V(l, "lam", 0, 4), AF.Exp, [vec], [tmp], scale=-1.0)
        P.act(tmp[:], tmp[:], AF.Ln, [tmp, one_t], [tmp], bias=one_t[:, 0:1])
        P.ts("dve", lru_c[:, l, 0:4], tmp[:], -8.0, None, ALU.mult, None, [tmp], [lru_c])
        P.ts("dve", lru_c[:, l, 4:8], tmp[:], -16.0, None, ALU.mult, None, [tmp], [lru_c])

    s5rot = []
    for l in range(2):
        s5s = P.sb([128, 48], F32, "s5s%d" % l)
        P.dma(s5s[:], s5s_d[l], writes=[s5s])
        a_re = s5s[:, 0:16]
        a_im = s5s[:, 16:32]
        ldt = s5s[:, 32:48]
        w = P.sb([128, 16, 16], F32, "s5w%d" % l)
        R = [w]

        def row(i):
            return w[:, i, :]
        dt_, rho, th, cc, ss, t1, t2, lr, li, den, qre, qim, nr = [row(i) for i in range(13)]
        P.act(dt_, ldt, AF.Exp, [s5s], R)
        P.tt("dve", rho, a_re, dt_, ALU.mult, [s5s] + R, R)
        P.act(rho, rho, AF.Exp, R, R)
        P.tt("dve", th, a_im, dt_, ALU.mult, [s5s] + R, R)
        hp = P.sb([128, 1], F32, "halfpi%d" % l)
        P.memset("dve", hp[:], math.pi / 2, [hp])
        P.act(cc, th, AF.Sin, R + [hp], R, bias=hp[:, 0:1], scale=1.0 / 16)
        P.act(ss, th, AF.Sin, R, R, scale=1.0 / 16)

        def csq(c_, s_):
            P.tt("dve", t1, c_, c_, ALU.mult, R, R)
            P.tt("dve", t2, s_, s_, ALU.mult, R, R)
            P.stt(s_, c_, 2.0, s_, ALU.mult, ALU.mult, R, R)
            P.tt("dve", c_, t1, t2, ALU.subtract, R, R)
        for _ in range(4):
            csq(cc, ss)
        P.tt("dve", lr, rho, cc, ALU.mult, R, R)
        P.tt("dve", li, rho, ss, ALU.mult, R, R)
        P.tt("dve", t1, a_re, a_re, ALU.mult, [s5s] + R, R)
        P.tt("dve", t2, a_im, a_im, ALU.mult, [s5s] + R, R)
        P.tt("dve", den, t1, t2, ALU.add, R, R)
        P.recip(den, den, R, R)
        P.ts("dve", nr, lr, -1.0, None, ALU.add, None, R, R)
        P.tt("dve", t1, nr, a_re, ALU.mult, [s5s] + R, R)
        P.tt("dve", t2, li, a_im, ALU.mult, [s5s] + R, R)
        P.tt("dve", t1, t1, t2, ALU.add, R, R)
        P.tt("dve", qre, t1, den, ALU.mult, R, R)
        P.tt("dve", t1, li, a_re, ALU.mult, [s5s] + R, R)
        P.tt("dve", t2, nr, a_im, ALU.mult, [s5s] + R, R)
        P.tt("dve", t1, t1, t2, ALU.subtract, R, R)
        P.tt("dve", qim, t1, den, ALU.mult, R, R)
        keep = s5keep[l]
        P.copy("dve", keep[:, 0, :], rho, R, [keep])
        P.copy("dve", keep[:, 1, :], cc, R, [keep])
        P.copy("dve", keep[:, 2, :], ss, R, [keep])

        sbf = P.sb([128, 512], F32, "s5b_in%d" % l)
        P.dma(sbf[:], s5b_d[l], writes=[sbf])
        bre = sbf[:, 0:256].rearrange("p (j h) -> p j h", h=16)
        bim = sbf[:, 256:512].rearrange("p (j h) -> p j h", h=16)
        Bt = s5B[l]
        Btv = Bt[:, :].rearrange("p (r b q m) -> p r b q m", r=2, b=4, q=2)
        bpad = P.sb([128, 2, 128], BF16, "s5bpad%d" % l)
        tb = P.sb([128, 2, 16], F32, "s5tb%d" % l)
        for j in range(16):
            P.ts("dve", tb[:, 0, :], bim[:, j, :], qim[:, j:j + 1], None, ALU.mult, None, [sbf] + R, [tb])
            P.stt(tb[:, 0, :], bre[:, j, :], qre[:, j:j + 1], tb[:, 0, :], ALU.mult, ALU.subtract, [sbf, tb] + R, [tb])
            P.ts("dve", tb[:, 1, :], bre[:, j, :], qim[:, j:j + 1], None, ALU.mult, None, [sbf] + R, [tb])
            P.stt(tb[:, 1, :], bim[:, j, :], qre[:, j:j + 1], tb[:, 1, :], ALU.mult, ALU.add, [sbf, tb] + R, [tb])
            P.memset("pool", bpad[:], 0.0, [bpad])
            for gh in range(2):
                col0 = 32 * (j % 4) + gh * 16
                for ri in range(2):
                    P.copy("pool", bpad[gh * 64:(gh + 1) * 64, ri, col0:col0 + 16],
                           tb[gh * 64:(gh + 1) * 64, ri, :], [tb], [bpad])
            for ri in range(2):
                P.mm(ps_misc[:, ri * 128:(ri + 1) * 128], bpad[:, ri, :], ident_b, True, True, [bpad, cstb], [ps_misc])
            hf = (j % 4) // 2
            for ri in range(2):
                P.copy("act", Btv[64 * hf:64 * hf + 64, ri, j // 4, j % 2, :],
                       ps_misc[64 * hf:64 * hf + 64, ri * 128:(ri + 1) * 128], [ps_misc], [Bt])

        scf = P.sb([128, 2048], F32, "s5c_in%d" % l)
        P.dma(scf[:], s5c_d[l], writes=[scf])
        Ct = s5C[l]
        P.copy("act", Ct[:, 0:1024], scf[:, 0:1024], [scf], [Ct])
        P.ts("dve", Ct[:, 1024:2048], scf[:, 1024:2048], -1.0, None, ALU.mult, None, [scf], [Ct])

        tab = P.sb([128, 2, 16, TS5], F32, "s5tabb%d" % l)
        P.memset("pool", tab[:, 0, :, 0:1], 1.0, [tab])
        P.memset("pool", tab[:, 1, :, 0:1], 0.0, [tab])
        ec = P.sb([128, 2, 16], F32, "s5ec%d" % l)
        P.copy("dve", ec[:, 0, :], cc, R, [ec])
        P.copy("dve", ec[:, 1, :], ss, R, [ec])
        m = 1
        while m < TS5:
            for j in range(16):
                cj = ec[:, 0, j:j + 1]
                sj = ec[:, 1, j:j + 1]
                src_c = tab[:, 0, j, 0:m]
                src_s = tab[:, 1, j, 0:m]
                dst_c = tab[:, 0, j, m:2 * m]
                dst_s = tab[:, 1, j, m:2 * m]
                P.ts("dve", dst_c, src_s, sj, None, ALU.mult, None, [tab, ec], [tab])
                P.stt(dst_c, src_c, cj, dst_c, ALU.mult, ALU.subtract, [tab, ec], [tab])
                P.ts("dve", dst_s, src_c, sj, None, ALU.mult, None, [tab, ec], [tab])
                P.stt(dst_s, src_s, cj, dst_s, ALU.mult, ALU.add, [tab, ec], [tab])
            e_c = ec[:, 0, :]
            e_s = ec[:, 1, :]
            P.tt("dve", t1, e_c, e_c, ALU.mult, [ec] + R, R)
            P.tt("dve", t2, e_s, e_s, ALU.mult, [ec] + R, R)
            P.stt(e_s, e_c, 2.0, e_s, ALU.mult, ALU.mult, [ec], [ec])
            P.tt("dve", e_c, t1, t2, ALU.subtract, R, [ec])
            m *= 2
        rot = s5rotb[l]
        P.copy("dve", rot[:], ec[:], [ec], [rot])
        s5rot.append((rot, keep))
        P.dma(s5tab_d[l], tab[:, :, :, :].rearrange("p a j t -> p (a j t)"), reads=[tab], writes=[tabsb])
        dbg_dump("s5w%d" % l, w, w[:, :, :].rearrange("p a b -> p (a b)"), [128, 256])
        dbg_dump("s5keep%d" % l, keep, keep[:, :, :].rearrange("p a b -> p (a b)"), [128, 48])
        dbg_dump("s5tab%d" % l, tab, tab[:, :, :, :].rearrange("p a j t -> p (a j t)"), [128, 2 * 16 * TS5])
        dbg_dump("s5B%d" % l, Bt, Bt[:, :], [128, 2048], BF16)
    P.pop()

    hT = P.sb([128, 8, TT], F32, "hT")
    xn = P.sb([128, 8, TT], BF16, "xn")
    zst = [P.sb([128, 1 + TT], F32, "zst%d" % i) for i in range(2)]
    zs = P.sb([128, 13, TT], F32, "zs")
    zsv = P.views(zs, 13)
    zu = P.sb([128, 4, TT], F32, "zu")
    zx = P.sb([128, 4, 3 + TT], F32, "zx")
    sgate = P.sb([128, 12, TT], BF16, "sgate")
    sgv = P.views(sgate, 12)
    ycat = P.sb([128, 12, TT], BF16, "ycat")
    ycv = P.views(ycat, 12)
    pbf = P.sb([128, 2, TT], BF16, "pbf")
    ring = [P.sb([128, 4096], BF16, "ring%d" % i) for i in range(3)]
    ringi = [0]
    s5tab = P.sb([128, 2, 16, TS5], F32, "s5tab")
    banks = [P.ps("ps%d" % i) for i in range(7)] + [ps_misc]
    rot_i = {"proj": 0, "rw": 0}

    def bank(group):
        ids = (0, 1) if group == "proj" else (2, 3)
        i = rot_i[group]
        rot_i[group] = (i + 1) % len(ids)
        return banks[ids[i]]

    def next_ring():
        r = ring[ringi[0]]
        ringi[0] = (ringi[0] + 1) % 3
        return r

    cz = [P.sb([128, 13], F32, "cz%d" % l) for l in range(2)]
    cl = [P.sb([128, 4, 3], F32, "cl%d" % l) for l in range(2)]
    ch = [P.sb([128, 4], F32, "ch%d" % l) for l in range(2)]
    s5z = [P.sb([128, 2, 16], F32, "s5z%d" % l) for l in range(2)]
    s5zv = [P.views(s5z[l], 16) for l in range(2)]
    Tst = [[P.sb([128, 128], BF16, "T%d_%d" % (l, pb)) for pb in range(4)] for l in range(2)]
    for l in range(2):
        P.memset("pool", cz[l][:], 0.0, [cz[l]])
        P.memset("pool", cl[l][:], 0.0, [cl[l]])
        P.memset("pool", ch[l][:], 0.0, [ch[l]])
        P.memset("pool", s5z[l][:], 0.0, s5zv[l])
        for pb in range(4):
            P.memset("pool", Tst[l][pb][:], 0.0, [Tst[l][pb]])

    NF = 17
    fs = [P.sb([128, TT], F32, "rf%d" % i) for i in range(NF)]
    pad_names = ["RTp", "KTp", "CTp", "BTp", "VTp", "KGp", "BGp"]
    pads = {n: P.sb([128, NCH * 128], BF16, n) for n in pad_names}
    for n in pad_names:
        P.memset("pool", pads[n][:], 0.0, [pads[n]])
    RTc = P.sb([128, TT], BF16, "RTc")
    tanh_wd = P.sb([128, TT], BF16, "tanhwd")
    Blev_i = [[P.sb([128, NCH * 128], BF16, "Blev%d_%d" % (i, k)) for i in range(2)] for k in range(2)]
    BTlev_i = [[P.sb([128, NCH * 128], BF16, "BTlev%d_%d" % (i, k)) for i in range(2)] for k in range(2)]
    AkkT_i = [P.sb([128, NCH * 128], BF16, "AkkT_%d" % k) for k in range(2)]
    Xbf_i = [P.sb([128, NCH * 256], BF16, "Xbf_%d" % k) for k in range(2)]
    NPI = 2
    pp = []
    for i in range(NPI):
        d = {}
        for n in ("Vbd", "KGbd", "BGbd", "PT", "nU0", "Wbd"):
            d[n] = P.sb([128, NCH * 128], BF16, "%s_%d" % (n, i))
        for n in ("Rhat", "ArkT", "ArbT"):
            d[n] = P.sb([128, TT], BF16, "%s_%d" % (n, i))
        d["bonus"] = P.sb([128, TT], F32, "bonus_%d" % i)
        d["GL"] = P.sb([128, NCH], F32, "GL_%d" % i)
        d["rt32"] = P.sb([128, TT], F32, "rt32_%d" % i)
        pp.append(d)
    mix = P.sb([128, 4, TT], F32, "mix")
    mixv = P.views(mix, 4)

    NS5SET = 2
    s5f = [[P.sb([128, TS5], F32, "s5f%d_%d" % (k, i)) for i in range(8)] for k in range(NS5SET)]
    s5x = [P.sb([128, 2, TS5], BF16, "s5x%d" % k) for k in range(NS5SET)]
    s5zl = [P.sb([128, 2], F32, "s5zl%d" % k) for k in range(NS5SET)]
    spf = [P.sb([128, TT], F32, "spf%d" % i) for i in range(3)]
    lf = [P.sb([128, TT], F32, "lf%d" % i) for i in range(4)]
    ubf = P.sb([128, 4, TT], BF16, "ubf")
    lb = P.sb([128, TT], BF16, "lb")
    rstd = P.sb([128, TT], F32, "rstd")
    sq = P.sb([128, 4, TT], F32, "sq")

    out_toks = []
    eps_t = {}
    for e_ in (NORM_EPS, GN_EPS):
        t = P.sb([128, 1], F32, "eps%d" % len(eps_t))
        P.memset("pool", t[:], e_, [t])
        eps_t[e_] = t
    neg_half = -math.exp(-0.5)

    def rms_rstd(src_bufs, src_ap_fn, nblk, eps):
        b = bank("proj")
        for k in range(nblk):
            P.act(sq[:, k % 4, :], src_ap_fn(k), AF.Square, src_bufs, [sq])
            P.mm(b[:, 0:TT], ones_f[:], sq[:, k % 4, :], k == 0, k == nblk - 1, [ones_f, sq], [b])
        P.act(rstd[:], b[:, 0:TT], AF.Ln, [b, eps_t[eps]], [rstd], bias=eps_t[eps][:, 0:1], scale=1.0 / (nblk * 128))
        P.act(rstd[:], rstd[:], AF.Exp, [rstd], [rstd], scale=-0.5)

    def drive(items):
        active = list(items)
        while active:
            for item in list(active):
                g, w = item
                for _ in range(w):
                    try:
                        next(g)
                    except StopIteration:
                        active.remove(item)
                        break

    s5ctr = [0]

    for it in range(NT):
        t0 = it * TT
        first = (it == 0)
        P.dma(hT[:], xT[:, t0:t0 + TT].rearrange("(k p) t -> p k t", p=128), writes=[hT])
        for l in range(2):
            dbg_on = first and l == 0
            P.dma(pbf[:], pT[l, :, t0:t0 + TT].rearrange("(k p) t -> p k t", p=128), writes=[pbf], eng="pool")
            P.dma(s5tab[:, :, :, :].rearrange("p a j t -> p (a j t)"), s5tab_d[l], reads=[tabsb], writes=[s5tab])
            rms_rstd([hT], lambda k: hT[:, k, :], 8, NORM_EPS)
            for k in range(8):
                P.stt(xn[:, k, :], hT[:, k, :], V(l, "ng", k), rstd[:], ALU.mult, ALU.mult, [hT, vec, rstd], [xn])
            if dbg_on:
                dbg_dump("xn", xn, xn[:, :, :].rearrange("p a t -> p (a t)"), [128, 8 * TT], BF16)

            wchunk = {}

            def in_block(cb, l=l, wchunk=wchunk):
                ci = cb // 4
                if ci not in wchunk:
                    r = next_ring()
                    ncol = 512 if ci < 8 else 128
                    P.dma(r[:, 0:8 * ncol].rearrange("p (k n) -> p k n", k=8),
                          w_in_b[l].rearrange("(k p) n -> p k n", p=128)[:, :, ci * 512:ci * 512 + ncol],
                          reads=[wsb], writes=[r])
                    wchunk[ci] = (r, ncol)
                r, ncol = wchunk[ci]
                rv = r[:, 0:8 * ncol].rearrange("p (k n) -> p k n", k=8)
                c0 = (cb % 4) * 128
                b = bank("proj")
                for k in range(8):
                    P.mm(b[:, 0:TT], rv[:, k, c0:c0 + 128], xn[:, k, :], k == 0, k == 7, [r, xn], [b])
                return b

            for cb in range(13):
                st = zst[cb % 2]
                b = in_block(cb)
                P.copy("act", st[:, 0:1], cz[l][:, cb:cb + 1], [cz[l]], [st])
                P.copy("act", st[:, 1:1 + TT], b[:, 0:TT], [b], [st])
                P.copy("act", cz[l][:, cb:cb + 1], st[:, TT:TT + 1], [st], [cz[l]])
                d_ = lf[cb % 2]
                P.tt("pool", d_[:], st[:, 0:TT], st[:, 1:1 + TT], ALU.subtract, [st], [d_])
                P.stt(zs[:, cb, :], d_[:], V(l, "mu", cb), st[:, 1:1 + TT], ALU.mult, ALU.add, [d_, vec, st], [zsv[cb]])
            if dbg_on:
                dbg_dump("zs", zs, zs[:, :, :].rearrange("p a t -> p (a t)"), [128, 13 * TT])
            P.act(tanh_wd[0:64, :], zs[0:64, 12, :], AF.Tanh, [zsv[12]], [tanh_wd])
            P.copy("act", tanh_wd[64:128, :], zs[64:128, 12, :], [zsv[12]], [tanh_wd])
            for blk in range(4):
                b = in_block(13 + blk)
                P.act(sgate[:, blk, :], b[:, 0:TT], AF.Silu, [b], [sgv[blk]])
            for blk in range(4):
                b = in_block(17 + blk)
                P.copy("act", zu[:, blk, :], b[:, 0:TT], [b], [zu])
            P.copy("pool", ubf[:], zu[:], [zu], [ubf])
            for blk in range(4):
                b = in_block(21 + blk)
                P.act(sgate[:, 4 + blk, :], b[:, 0:TT], AF.Silu, [b], [sgv[4 + blk]])
            P.copy("act", zx[:, :, 0:3], cl[l][:, :, :], [cl[l]], [zx])
            for blk in range(4):
                b = in_block(25 + blk)
                P.copy("act", zx[:, blk, 3:3 + TT], b[:, 0:TT], [b], [zx])
            P.copy("act", cl[l][:, :, :], zx[:, :, TT:TT + 3], [zx], [cl[l]])
            for blk in range(4):
                b = in_block(29 + blk)
                P.act(sgate[:, 8 + blk, :], b[:, 0:TT], AF.Silu, [b], [sgv[8 + blk]])

            def prep(pb, inst, l=l, dbg_on=dbg_on):
                d = pp[inst]
                Blev, BTlev, AkkT = Blev_i[inst], BTlev_i[inst], AkkT_i[inst]
                r_ = zs[:, pb, :]
                k_ = zs[:, 4 + pb, :]
                v_ = zs[:, 8 + pb, :]
                zr_, zk_, zv_ = zsv[pb], zsv[4 + pb], zsv[8 + pb]
                (sg, ld, a_, kk_, kk2, sqk, kap, t1, kp, b_, lg, eg, ieg, eg1, dl, egl, rk) = fs[:17]
                rt32 = d["rt32"]
                cols = slice(pb * 128, (pb + 1) * 128)
                bw = bank("rw")
                P.mm(bw[:, 0:TT], w2a2[l][0:64, cols], tanh_wd[0:64, :], True, True, [w2a2[l], tanh_wd], [bw])
                P.act(sg[:], bw[:, 0:TT], AF.Sigmoid, [bw, vec], [sg], bias=V(l, "w0", pb))
                P.ts("dve", ld[:], sg[:], neg_half, None, ALU.mult, None, [sg], [ld])
                ba_ = bank("rw")
                P.mm(ba_[:, 0:TT], w2a2[l][64:128, cols], tanh_wd[64:128, :], True, True, [w2a2[l], tanh_wd], [ba_])
                P.act(a_[:], ba_[:, 0:TT], AF.Sigmoid, [ba_, vec], [a_], bias=V(l, "a0", pb))
                yield
                P.scan(lg[:], cst[:, CST_SCAN:CST_SCAN + TT], ld[:], 0.0, [cst, ld], [lg])
                P.ts("dve", kk_[:], k_, V(l, "kk", pb), None, ALU.mult, None, [zk_, vec], [kk_])
                P.tt("pool", kk2[:], kk_[:], kk_[:], ALU.mult, [kk_], [kk2])
                bs = bank("rw")
                P.mm(bs[:, 0:TT], onesbd_f, kk2[:], True, True, [cst, kk2], [bs])
                P.act(sqk[:], bs[:, 0:TT], AF.Sqrt, [bs], [sqk])
                yield
                P.act(eg[:], lg[:], AF.Exp, [lg], [eg])
                P.act(ieg[:], lg[:], AF.Exp, [lg], [ieg], scale=-1.0)
                P.tt("pool", eg1[:], lg[:], ld[:], ALU.subtract, [lg, ld], [eg1])
                P.act(eg1[:], eg1[:], AF.Exp, [eg1], [eg1])
                P.ts("dve", sqk[:], sqk[:], 1e-12, None, ALU.max, None, [sqk], [sqk])
                P.recip(sqk[:], sqk[:], [sqk], [sqk])
                P.tt("pool", kap[:], kk_[:], sqk[:], ALU.mult, [kk_, sqk], [kap])
                yield
                P.ts("dve", t1[:], a_[:], -1.0, V(l, "ka", pb), ALU.add, ALU.mult, [a_, vec], [t1])
                P.stt(kp[:], t1[:], 1.0, k_, ALU.add, ALU.mult, [t1, zk_], [kp])
                P.tt("pool", b_[:], kap[:], a_[:], ALU.mult, [kap, a_], [b_])
                lg3 = lg[:, :].rearrange("p (c t) -> p c t", t=LCH)
                P.tt("dve", dl[:, :].rearrange("p (c t) -> p c t", t=LCH), lg3[:, :, LCH - 1:LCH].to_broadcast([128, NCH, LCH]),
                     lg3, ALU.subtract, [lg], [dl])
                P.act(egl[:], dl[:], AF.Exp, [dl], [egl])
                P.copy("act", d["GL"][:, :], eg[:, :].rearrange("p (c t) -> p c t", t=LCH)[:, :, LCH - 1], [eg], [d["GL"]])
                yield
                P.tt("dve", rt32[:], r_, eg[:], ALU.mult, [zr_, eg], [rt32])
                P.copy("act", RTc[:], rt32[:], [rt32], [RTc])

                def padw(name, eng, in0, in1, rd):
                    t = pads[name]
                    tv = t[:, :].rearrange("p (c h t) -> p c h t", c=NCH, h=2)
                    for hh in range(2):
                        ps_ = slice(hh * 64, (hh + 1) * 64)
                        o = tv[ps_, :, hh, :]
                        i0 = in0[ps_, :].rearrange("p (c t) -> p c t", t=LCH)
                        if in1 is None:
                            P.copy(eng, o, i0, rd, [t])
                        else:
                            i1 = in1[ps_, :].rearrange("p (c t) -> p c t", t=LCH)
                            P.tt(eng, o, i0, i1, ALU.mult, rd, [t])
                padw("RTp", "act", rt32, None, [rt32])
                padw("KTp", "dve", kp, ieg, [kp, ieg])
                padw("CTp", "pool", kap, eg1, [kap, eg1])
                yield
                padw("BTp", "dve", b_, ieg, [b_, ieg])
                padw("VTp", "act", zs[:, 8 + pb, :], None, [zv_])
                padw("KGp", "pool", kp, egl, [kp, egl])
                padw("BGp", "dve", b_, egl, [b_, egl])
                yield
                P.stt(rk[:], r_, V(l, "rk", pb), kp[:], ALU.mult, ALU.mult, [zr_, vec, kp], [rk])
                bb = bank("rw")
                P.mm(bb[:, 0:TT], onesbd_f, rk[:], True, True, [cst, rk], [bb])
                P.tt("dve", d["bonus"][:], bb[:, 0:TT], v_, ALU.mult, [bb, zv_], [d["bonus"]])
                if dbg_on and pb == 0:
                    dbg_dump("lg", lg, lg[:], [128, TT])
                    dbg_dump("kap", kap, kap[:], [128, TT])
                    dbg_dump("kp", kp, kp[:], [128, TT])
                    dbg_dump("a", a_, a_[:], [128, TT])
                yield

                def chunkmm(dst_bank, lname, rname, rbuf=None, rcols=128):
                    lt = pads[lname]
                    for c in range(NCH):
                        if rbuf is None:
                            rb_ = pads[rname]
                            rap = rb_[:, c * 128:(c + 1) * 128]
                        else:
                            rb_ = rbuf
                            rap = rbuf[:, c * rcols:(c + 1) * rcols]
                        P.mm(dst_bank[:, c * rcols:(c + 1) * rcols], lt[:, c * 128:(c + 1) * 128], rap, True, True,
                             [lt, rb_], [dst_bank])

                def masked(dst, src_bank, mcol, w):
                    n = NCH * w
                    P.tt("dve", dst[:, 0:n], src_bank[:, 0:n], mskb[:, mcol:mcol + n], ALU.mult, [src_bank, mskb], [dst])
                b1 = bank("rw")
                chunkmm(b1, "BTp", "CTp")
                masked(BTlev[0], b1, MSK_USN, 128)
                b2 = bank("rw")
                chunkmm(b2, "CTp", "BTp")
                masked(Blev[0], b2, MSK_LSN, 128)
                yield
                b3 = bank("rw")
                chunkmm(b3, "KTp", "CTp")
                masked(AkkT, b3, MSK_USP, 128)
                b4 = bank("rw")
                chunkmm(b4, "KTp", None, RTc, LCH)
                masked(d["ArkT"], b4, MSK_CI, LCH)
                b5 = bank("rw")
                chunkmm(b5, "BTp", None, RTc, LCH)
                masked(d["ArbT"], b5, MSK_CI, LCH)
                yield

            def tokmajor(src_name, dst_buf, dst_ap, eng):
                bt_ = bank("rw")
                lt = pads[src_name]
                for c in range(NCH):
                    P.mm(bt_[:, c * 128:(c + 1) * 128], lt[:, c * 128:(c + 1) * 128], ident_b, True, True,
                         [lt, cstb], [bt_])
                P.copy(eng, dst_ap, bt_[:, 0:NCH * 128] if len(dst_ap.shape) == 2 else
                       bt_[:, 0:NCH * 128].rearrange("p (c n) -> p c n", n=128), [bt_], [dst_buf])

            def solve(pb, inst, l=l):
                d = pp[inst]
                Blev, BTlev, AkkT, Xbf = Blev_i[inst], BTlev_i[inst], AkkT_i[inst], Xbf_i[inst]
                Xbfv = Xbf[:, :].rearrange("p (c n) -> p c n", n=256)
                tokmajor("VTp", d["Vbd"], d["Vbd"][:, :], "act")
                tokmajor("KGp", d["KGbd"], d["KGbd"][:, :], "act")
                yield
                tokmajor("BGp", d["BGbd"], d["BGbd"][:, :], "act")
                tokmajor("CTp", Xbf, Xbfv[:, :, 0:128], "act")
                bt_ = bank("rw")
                for c in range(NCH):
                    cs = slice(c * 128, (c + 1) * 128)
                    P.mm(bt_[:, cs], AkkT[:, cs], d["Vbd"][:, cs], True, True, [AkkT, d["Vbd"]], [bt_])
                P.copy("act", Xbfv[:, :, 128:256], bt_[:, 0:NCH * 128].rearrange("p (c n) -> p c n", n=128), [bt_], [Xbf])
                yield "tok_done"
                cur = 0
                NLEV = 6
                for lev in range(NLEV):
                    if lev < NLEV - 1:
                        nxt = 1 - cur
                        bq = bank("rw")
                        for c in range(NCH):
                            cs = slice(c * 128, (c + 1) * 128)
                            P.mm(bq[:, cs], Blev[cur][:, cs], BTlev[cur][:, cs], True, True, [Blev[cur], BTlev[cur]], [bq])
                        if lev < NLEV - 2:
                            bq2 = bank("rw")
                            for c in range(NCH):
                                cs = slice(c * 128, (c + 1) * 128)
                                P.mm(bq2[:, cs], BTlev[cur][:, cs], Blev[cur][:, cs], True, True,
                                     [Blev[cur], BTlev[cur]], [bq2])
                    for half in range(2):
                        bx_ = banks[4 + half]
                        for cc_ in range(2):
                            c = half * 2 + cc_
                            P.mm(bx_[:, cc_ * 256:(cc_ + 1) * 256], BTlev[cur][:, c * 128:(c + 1) * 128], Xbfv[:, c, :],
                                 True, False, [BTlev[cur], Xbf], [bx_])
                            P.mm(bx_[:, cc_ * 256:(cc_ + 1) * 256], ident_b, Xbfv[:, c, :],
                                 False, True, [cstb, Xbf], [bx_])
                    if lev < NLEV - 1:
                        P.copy("act", BTlev[nxt][:], bq[:, 0:512], [bq], [BTlev[nxt]])
                        if lev < NLEV - 2:
                            P.copy("dve", Blev[nxt][:], bq2[:, 0:512], [bq2], [Blev[nxt]])
                    P.copy("act", Xbf[:, 0:512], banks[4][:, 0:512], [banks[4]], [Xbf])
                    P.copy("dve", Xbf[:, 512:1024], banks[5][:, 0:512], [banks[5]], [Xbf])
                    if lev < NLEV - 1:
                        cur = nxt
                    yield
                nU0v = d["nU0"][:, :].rearrange("p (c n) -> p c n", n=128)
                Wbdv = d["Wbd"][:, :].rearrange("p (c n) -> p c n", n=128)
                P.ts("dve", nU0v, Xbfv[:, :, 128:256], -1.0, None, ALU.mult, None, [Xbf], [d["nU0"]])
                P.copy("pool", Wbdv, Xbfv[:, :, 0:128], [Xbf], [d["Wbd"]])
                br = bank("rw")
                for c in range(NCH):
                    P.mm(br[:, c * LCH:(c + 1) * LCH], d["Wbd"][:, c * 128:(c + 1) * 128],
                         d["ArbT"][:, c * LCH:(c + 1) * LCH], True, True, [d["Wbd"], d["ArbT"]], [br])
                P.tt("dve", d["Rhat"][:], d["rt32"][:], br[:, 0:TT], ALU.subtract, [d["rt32"], br], [d["Rhat"]])
                bp = bank("rw")
                for c in range(NCH):
                    cs = slice(c * 128, (c + 1) * 128)
                    P.mm(bp[:, cs], d["Wbd"][:, cs], d["BGbd"][:, cs], True, True, [d["Wbd"], d["BGbd"]], [bp])
                for c in range(NCH):
                    cs = slice(c * 128, (c + 1) * 128)
                    P.stt(d["PT"][:, cs], ident_f, d["GL"][:, c:c + 1], bp[:, cs], ALU.mult, ALU.subtract,
                          [cst, d["GL"], bp], [d["PT"]])
                yield

            def seq(pb, inst, c, l=l):
                d = pp[inst]
                T = Tst[l][pb]
                yb = banks[6]
                tb_ = banks[4 + inst]
                ycols = slice(inst * TT + c * LCH, inst * TT + (c + 1) * LCH)
                cs = slice(c * 128, (c + 1) * 128)
                cl_ = slice(c * LCH, (c + 1) * LCH)
                P.mm(yb[:, ycols], T[:], d["Rhat"][:, cl_], True, False, [T, d["Rhat"]], [yb])
                P.mm(yb[:, ycols], d["Vbd"][:, cs], d["ArkT"][:, cl_], False, False, [d["Vbd"], d["ArkT"]], [yb])
                P.mm(yb[:, ycols], d["nU0"][:, cs], d["ArbT"][:, cl_], False, True, [d["nU0"], d["ArbT"]], [yb])
                P.mm(tb_[:, 0:128], d["PT"][:, cs], T[:], True, False, [d["PT"], T], [tb_])
                P.mm(tb_[:, 0:128], d["KGbd"][:, cs], d["Vbd"][:, cs], False, False, [d["KGbd"], d["Vbd"]], [tb_])
                P.mm(tb_[:, 0:128], d["BGbd"][:, cs], d["nU0"][:, cs], False, True, [d["BGbd"], d["nU0"]], [tb_])
                P.copy("act", T[:], tb_[:, 0:128], [tb_], [T])

            def fin(pb, inst, l=l, dbg_on=dbg_on):
                d = pp[inst]
                yb = banks[6]
                y32, yc, ysq, rs = fs[0], fs[1], fs[2], fs[3]
                P.copy("act", y32[:], yb[:, inst * TT:(inst + 1) * TT], [yb], [y32])
                if dbg_on:
                    dbg_dump("y_rw%d" % pb, y32, y32[:], [128, TT])
                bm = bank("rw")
                P.mm(bm[:, 0:TT], onesbd_f, y32[:], True, True, [cst, y32], [bm])
                P.stt(yc[:], bm[:, 0:TT], -1.0 / 64, y32[:], ALU.mult, ALU.add, [bm, y32], [yc])
                P.act(ysq[:], yc[:], AF.Square, [yc], [ysq])
                bv = bank("rw")
                P.mm(bv[:, 0:TT], onesbd_f, ysq[:], True, True, [cst, ysq], [bv])
                P.act(rs[:], bv[:, 0:TT], AF.Ln, [bv, eps_t[GN_EPS]], [rs], bias=eps_t[GN_EPS][:, 0:1], scale=1.0 / 64)
                P.act(rs[:], rs[:], AF.Exp, [rs], [rs], scale=-0.5)
                P.tt("dve", yc[:], yc[:], rs[:], ALU.mult, [yc, rs], [yc])
                P.ts("dve", yc[:], yc[:], V(l, "lnw", pb), V(l, "lnb", pb), ALU.mult, ALU.add, [yc, vec], [yc])
                P.tt("pool", yc[:], yc[:], d["bonus"][:], ALU.add, [yc, d["bonus"]], [yc])
                P.tt("dve", ycat[:, pb, :], yc[:], sgate[:, pb, :], ALU.mult, [yc, sgv[pb]], [ycv[pb]])

            def rwkv_gen():
                for half in range(2):
                    pbs = (2 * half, 2 * half + 1)
                    def chain(pb, inst):
                        yield from prep(pb, inst)
                        yield from solve(pb, inst)
                    gA, gB = chain(pbs[0], 0), chain(pbs[1], 1)
                    for v in gA:
                        yield
                        if v == "tok_done":
                            break
                    doneA = doneB = False
                    while not (doneA and doneB):
                        if not doneA:
                            try:
                                next(gA)
                                yield
                            except StopIteration:
                                doneA = True
                        if not doneB:
                            try:
                                next(gB)
                                yield
                            except StopIteration:
                                doneB = True
                    for c in range(NCH):
                        for inst, pb in enumerate(pbs):
                            seq(pb, inst, c)
                            yield
                    for inst, pb in enumerate(pbs):
                        fin(pb, inst)
                        yield

            def s5_gen(l=l, dbg_on=dbg_on):
                Bv = s5B[l][:, :].rearrange("p (r b q m) -> p r b q m", r=2, b=4, q=2)
                Cv = s5C[l][:, :].rearrange("p (r j m) -> p r j m", r=2, j=16)
                rotk, keep = s5rot[l]
                NS = TT // TS5
                yb5 = banks[7]

                def stage1(blk, s, jj, k):
                    j = blk * 4 + jj
                    hf, jh = jj // 2, jj % 2
                    hs = slice(64 * hf, 64 * hf + 64)
                    tsl = slice(s * TS5, (s + 1) * TS5)
                    (t1, t2, bzr, bzi, zr, zi, t3, t4) = s5f[k]
                    bu = banks[0]
                    c0 = (k % 2) * 2 * TS5
                    bre = bu[:, c0:c0 + TS5]
                    bim = bu[:, c0 + TS5:c0 + 2 * TS5]
                    P.mm(bre, Bv[hs, 0, blk, jh, :], ubf[hs, blk, tsl], True, True, [s5B[l], ubf], [bu])
                    P.mm(bim, Bv[hs, 1, blk, jh, :], ubf[hs, blk, tsl], True, True, [s5B[l], ubf], [bu])
                    cosT = s5tab[:, 0, j, :]
                    sinT = s5tab[:, 1, j, :]
                    P.tt("dve", t1[:], bre, cosT, ALU.mult, [bu, s5tab], [t1])
                    P.tt("dve", t2[:], bim, sinT, ALU.mult, [bu, s5tab], [t2])
                    P.tt("dve", t3[:], bim, cosT, ALU.mult, [bu, s5tab], [t3])
                    P.tt("dve", t4[:], bre, sinT, ALU.mult, [bu, s5tab], [t4])
                    P.tt(S5E, bzr[:], t1[:], t2[:], ALU.add, [t1, t2], [bzr])
                    P.tt(S5E, bzi[:], t3[:], t4[:], ALU.subtract, [t3, t4], [bzi])

                def stage2(blk, s, jj, k):
                    j = blk * 4 + jj
                    hf, jh = jj // 2, jj % 2
                    hs = slice(64 * hf, 64 * hf + 64)
                    tsl = slice(s * TS5, (s + 1) * TS5)
                    (t1, t2, bzr, bzi, zr, zi, t3, t4) = s5f[k]
                    sx = s5x[k]
                    zl = s5zl[k]
                    zv = s5zv[l][j]
                    cosT = s5tab[:, 0, j, :]
                    sinT = s5tab[:, 1, j, :]
                    rho_b = keep[:, 0, j:j + 1].to_broadcast([128, TS5])
                    P.scan(zr[:], rho_b, bzr[:], s5z[l][:, 0, j:j + 1], [keep, bzr, zv], [zr])
                    P.scan(zi[:], rho_b, bzi[:], s5z[l][:, 1, j:j + 1], [keep, bzi, zv], [zi])
                    P.tt("dve", t1[:], zr[:], cosT, ALU.mult, [zr, s5tab], [t1])
                    P.tt("dve", t2[:], zi[:], sinT, ALU.mult, [zi, s5tab], [t2])
                    P.tt(S5E, sx[:, 0, :], t1[:], t2[:], ALU.subtract, [t1, t2], [sx])
                    P.tt("dve", t3[:], zr[:], sinT, ALU.mult, [zr, s5tab], [t3])
                    P.tt(S5E, t4[:], zi[:], cosT, ALU.mult, [zi, s5tab], [t4])
                    P.tt(S5E, sx[:, 1, :], t3[:], t4[:], ALU.add, [t3, t4], [sx])
                    P.mm(yb5[hs, tsl], Cv[:, 0, j, :], sx[:, 0, :], jh == 0, False, [s5C[l], sx], [yb5])
                    P.mm(yb5[hs, tsl], Cv[:, 1, j, :], sx[:, 1, :], False, jh == 1, [s5C[l], sx], [yb5])
                    rc = rotk[:, 0, j:j + 1]
                    rs_ = rotk[:, 1, j:j + 1]
                    zlr = zr[:, TS5 - 1:TS5]
                    zli = zi[:, TS5 - 1:TS5]
                    P.ts("dve", zl[:, 0:1], zli, rs_, None, ALU.mult, None, [zi, rotk], [zl])
                    P.ts("dve", zl[:, 1:2], zlr, rs_, None, ALU.mult, None, [zr, rotk], [zl])
                    P.stt(s5z[l][:, 0, j:j + 1], zlr, rc, zl[:, 0:1], ALU.mult, ALU.subtract, [zr, rotk, zl], [zv])
                    P.stt(s5z[l][:, 1, j:j + 1], zli, rc, zl[:, 1:2], ALU.mult, ALU.add, [zi, rotk, zl], [zv])

                for blk in range(4):
                    units = [(s, jj) for s in range(NS) for jj in range(4)]
                    ks = []
                    for (s, jj) in units:
                        ks.append(s5ctr[0] % NS5SET)
                        s5ctr[0] += 1
                    stage1(blk, units[0][0], units[0][1], ks[0])
                    yield
                    for u in range(len(units)):
                        if u + 1 < len(units):
                            stage1(blk, units[u + 1][0], units[u + 1][1], ks[u + 1])
                            yield
                        stage2(blk, units[u][0], units[u][1], ks[u])
                        yield
                    ys, x2, q_ = spf
                    P.stt(ys[:], zu[:, blk, :], V(l, "s5d", blk), yb5[:, 0:TT], ALU.mult, ALU.add, [zu, vec, yb5], [ys])
                    if dbg_on:
                        dbg_dump("s5y%d" % blk, ys, ys[:], [128, TT])
                    P.act(x2[:], ys[:], AF.Square, [ys], [x2])
                    P.ts("dve", x2[:], x2[:], 0.044715, 1.0, ALU.mult, ALU.add, [x2], [x2])
                    P.tt("pool", q_[:], x2[:], ys[:], ALU.mult, [x2, ys], [q_])
                    P.act(x2[:], q_[:], AF.Sigmoid, [q_], [x2], scale=2.0 * math.sqrt(2.0 / math.pi))
                    P.tt("dve", mix[:, blk, :], ys[:], x2[:], ALU.mult, [ys, x2], [mixv[blk]])
                    yield
                zgb = ubf
                P.copy("pool", zgb[:], mix[:], mixv, [zgb])
                rg = next_ring()
                P.dma(rg[:, 0:2048].rearrange("p (k n) -> p k n", k=4), glu_w_b[l].rearrange("(k p) n -> p k n", p=128),
                      reads=[wsb], writes=[rg])
                rgv = rg[:, 0:2048].rearrange("p (k n) -> p k n", k=4)
                for ob in range(4):
                    b = banks[0]
                    for k in range(4):
                        P.mm(b[:, 0:TT], rgv[:, k, ob * 128:(ob + 1) * 128], zgb[:, k, :], k == 0, k == 3, [rg, zgb], [b])
                    sg_ = spf[ob % 2]
                    P.act(sg_[:], b[:, 0:TT], AF.Sigmoid, [b, vec], [sg_], bias=V(l, "glub", ob))
                    P.tt("pool", sg_[:], sg_[:], sgate[:, 4 + ob, :], ALU.mult, [sg_, sgv[4 + ob]], [sg_])
                    P.tt("dve", ycat[:, 4 + ob, :], mix[:, ob, :], sg_[:], ALU.mult, [mixv[ob], sg_], [ycv[4 + ob]])
                    yield

            def lru_gen(l=l, dbg_on=dbg_on):
                for blk in range(4):
                    A, B, C, Dd = lf
                    bl = banks[1]
                    P.ts("dve", A[:], zx[:, blk, 0:TT], V(l, "cw", 0 * 4 + blk), V(l, "cb", blk), ALU.mult, ALU.add,
                         [zx, vec], [A])
                    for j in range(1, 4):
                        P.stt(A[:], zx[:, blk, j:j + TT], V(l, "cw", j * 4 + blk), A[:], ALU.mult, ALU.add,
                              [zx, vec, A], [A])
                    P.copy("act", lb[:], A[:], [A], [lb])
                    yield
                    P.mm(bl[:, 0:TT], lruw[l][:, blk * 128:(blk + 1) * 128], lb[:], True, True, [lruw[l], lb], [bl])
                    P.mm(bl[:, TT:2 * TT], lruw[l][:, (4 + blk) * 128:(5 + blk) * 128], lb[:], True, True,
                         [lruw[l], lb], [bl])
                    P.act(B[:], bl[:, 0:TT], AF.Sigmoid, [bl, vec], [B], bias=V(l, "ba", blk))
                    P.act(C[:], bl[:, TT:2 * TT], AF.Sigmoid, [bl, vec], [C], bias=V(l, "bx", blk))
                    yield
                    P.act(Dd[:], B[:], AF.Exp, [B, lru_c], [Dd], scale=lru_c[:, l, blk:blk + 1])
                    P.act(B[:], B[:], AF.Exp, [B, lru_c], [B], scale=lru_c[:, l, 4 + blk:5 + blk])
                    P.act(B[:], B[:], AF.Ln, [B, one_t], [B], bias=one_t[:, 0:1], scale=-1.0)
                    P.act(B[:], B[:], AF.Exp, [B], [B], scale=0.5)
                    P.tt("pool", C[:], C[:], A[:], ALU.mult, [C, A], [C])
                    P.tt("pool", C[:], C[:], B[:], ALU.mult, [C, B], [C])
                    yield
                    P.scan(A[:], Dd[:], C[:], ch[l][:, blk:blk + 1], [Dd, C, ch[l]], [A])
                    P.copy("act", ch[l][:, blk:blk + 1], A[:, TT - 1:TT], [A], [ch[l]])
                    if dbg_on:
                        dbg_dump("lru%d" % blk, A, A[:], [128, TT])
                    P.tt("pool", ycat[:, 8 + blk, :], A[:], sgate[:, 8 + blk, :], ALU.mult, [A, sgv[8 + blk]], [ycv[8 + blk]])
                    yield

            gens = []
            if "rwkv" not in SKIP:
                gens.append((rwkv_gen(), GW[0]))
            if "s5" not in SKIP:
                gens.append((s5_gen(), GW[1]))
            if "lru" not in SKIP:
                gens.append((lru_gen(), GW[2]))
            if "serial" in SKIP:
                for g, w in gens:
                    drive([(g, 1)])
            else:
                drive(gens)
            if dbg_on:
                dbg_dump("ycat_rw", ycat, ycat[:, 0:4, :].rearrange("p a t -> p (a t)"), [128, 4 * TT], BF16)

            for oc in range(4):
                r = next_ring()
                P.dma(r[:, 0:12 * 256].rearrange("p (k n) -> p k n", k=12),
                      w_out_b[l].rearrange("(k p) n -> p k n", p=128)[:, :, oc * 256:(oc + 1) * 256],
                      reads=[wsb], writes=[r])
                rv = r[:, 0:12 * 256].rearrange("p (k n) -> p k n", k=12)
                for ob2 in range(2):
                    ob = oc * 2 + ob2
                    b = bank("proj")
                    for k in range(12):
                        P.mm(b[:, 0:TT], rv[:, k, ob2 * 128:(ob2 + 1) * 128], ycat[:, k, :], k == 0, k == 11,
                             [r] + ycv, [b])
                    P.tt("dve", hT[:, ob, :], hT[:, ob, :], b[:, 0:TT], ALU.add, [hT, b], [hT])
            r = next_ring()
            P.dma(r[:, 0:2048].rearrange("p (k n) -> p k n", k=2), ple_w_b[l].rearrange("(k p) n -> p k n", p=128),
                  reads=[wsb], writes=[r])
            rv = r[:, 0:2048].rearrange("p (k n) -> p k n", k=2)
            epre = zs
            for ob in range(8):
                b = bank("proj")
                for k in range(2):
                    P.mm(b[:, 0:TT], rv[:, k, ob * 128:(ob + 1) * 128], pbf[:, k, :], k == 0, k == 1, [r, pbf], [b])
                P.copy("act", epre[:, ob, :], b[:, 0:TT], [b], [zsv[ob]])
            rms_rstd(zsv[0:8], lambda k: epre[:, k, :], 8, NORM_EPS)
            P.copy("pool", xn[:], hT[:], [hT], [xn])
            for gc in range(2):
                r = next_ring()
                P.dma(r[:, 0:4096].rearrange("p (k n) -> p k n", k=8),
                      ple_gw_b[l].rearrange("(k p) n -> p k n", p=128)[:, :, gc * 512:(gc + 1) * 512],
                      reads=[wsb], writes=[r])
                rv = r[:, 0:4096].rearrange("p (k n) -> p k n", k=8)
                for ob2 in range(4):
                    ob = gc * 4 + ob2
                    b = bank("proj")
                    for k in range(8):
                        P.mm(b[:, 0:TT], rv[:, k, ob2 * 128:(ob2 + 1) * 128], xn[:, k, :], k == 0, k == 7, [r, xn], [b])
                    sg_, e_ = fs[6 + 2 * (ob % 2)], fs[7 + 2 * (ob % 2)]
                    P.act(sg_[:], b[:, 0:TT], AF.Sigmoid, [b], [sg_])
                    P.stt(e_[:], epre[:, ob, :], V(l, "png", ob), rstd[:], ALU.mult, ALU.mult, [zsv[ob], vec, rstd], [e_])
                    P.tt("pool", e_[:], e_[:], sg_[:], ALU.mult, [e_, sg_], [e_])
                    P.tt("dve", hT[:, ob, :], hT[:, ob, :], e_[:], ALU.add, [hT, e_], [hT])
            if dbg_on:
                dbg_dump("h1", hT, hT[:, :, :].rearrange("p a t -> p (a t)"), [128, 8 * TT])
        rms_rstd([hT], lambda k: hT[:, k, :], 8, NORM_EPS)
        fo = 2 * VEC_PER_LAYER
        for k in range(8):
            P.stt(zs[:, k, :], hT[:, k, :], vec[:, fo + k:fo + k + 1], rstd[:], ALU.mult, ALU.mult, [hT, vec, rstd], [zsv[k]])
        out_toks.append(P.dma(oT[:, t0:t0 + TT].rearrange("(k p) t -> p k t", p=128), zs[:, 0:8, :], reads=zsv[0:8]))
    out_toks.extend(dbg_out.values())
    ninst = P.ninst
    P.finish(out_toks)
    return nc, ninst


def pack_shared(inp):
    f = lambda a: np.asarray(a, np.float32)
    vec = np.zeros((128, NVEC), np.float32)
    for l in range(2):
        def put(name, arr, n):
            c = vcol(l, name)
            vec[:, c:c + n] = _pp(arr, n)
        put("ng", f(inp["norm_g"])[l], 8)
        put("mu", f(inp["rwkv_mu"])[l], 13)
        put("w0", f(inp["rwkv_w0"])[l], 4)
        put("a0", f(inp["rwkv_a0"])[l], 4)
        put("kk", f(inp["rwkv_k_k"])[l], 4)
        put("ka", f(inp["rwkv_k_a"])[l], 4)
        put("rk", f(inp["rwkv_r_k"])[l].reshape(512), 4)
        put("lnw", f(inp["rwkv_ln_w"])[l], 4)
        put("lnb", f(inp["rwkv_ln_b"])[l], 4)
        put("s5d", f(inp["s5_d"])[l], 4)
        put("glub", f(inp["s5_glu_b"])[l], 4)
        cw = f(inp["lru_conv_w"])[l]
        c = vcol(l, "cw")
        for j in range(4):
            vec[:, c + 4 * j:c + 4 * j + 4] = _pp(cw[j], 4)
        put("cb", f(inp["lru_conv_b"])[l], 4)
        put("ba", f(inp["lru_ba"])[l], 4)
        put("bx", f(inp["lru_bx"])[l], 4)
        put("lam", f(inp["lru_lambda"])[l], 4)
        put("png", f(inp["ple_norm_g"])[l], 8)
    vec[:, 2 * VEC_PER_LAYER:2 * VEC_PER_LAYER + 8] = _pp(f(inp["final_norm_g"]), 8)

    w2a2 = np.zeros((2, 128, 512), np.float32)
    w2a2[:, 0:64] = f(inp["rwkv_w2"])
    w2a2[:, 64:128] = f(inp["rwkv_a2"])
    lruw = np.zeros((2, 128, 8, 128), np.float32)
    for l in range(2):
        for m, key in enumerate(("lru_wa", "lru_wx")):
            w = f(inp[key])[l]
            for q in range(4):
                for b2 in range(2):
                    lruw[l, b2 * 64:(b2 + 1) * 64, m * 4 + q, b2 * 64:(b2 + 1) * 64] = w[2 * q + b2]
    lruw = lruw.reshape(2, 128, 1024)
    def modes(a):
        a = f(a).reshape(2, 16, 2, 64)
        return np.ascontiguousarray(a.transpose(0, 2, 3, 1).reshape(2, 128, 16))
    s5s = np.zeros((2, 128, 3, 16), np.float32)
    s5s[:, :, 0] = modes(inp["s5_a_re"])
    s5s[:, :, 1] = modes(inp["s5_a_im"])
    ldt = np.broadcast_to(f(inp["s5_log_dt"])[:, :, None], (2, 32, 64))
    s5s[:, :, 2] = modes(ldt)
    s5s = s5s.reshape(2, 128, 48)
    def bmodes(a):
        a = f(a).reshape(2, 16, 2, 64, 16)
        return a.transpose(0, 2, 3, 1, 4).reshape(2, 128, 16, 16)
    s5b = np.stack([bmodes(inp["s5_b_re"]), bmodes(inp["s5_b_im"])], axis=2).reshape(2, 128, 512)
    s5c = np.zeros((2, 128, 2, 16, 64), np.float32)
    for ri, key in enumerate(("s5_c_re", "s5_c_im")):
        c = f(inp[key]).reshape(2, 16, 2, 16, 64)
        for gh in range(2):
            for jh in range(2):
                c0 = 32 * jh + 16 * gh
                s5c[:, gh * 64:(gh + 1) * 64, ri, jh::2, c0:c0 + 16] = c[:, jh::2, gh].transpose(0, 3, 1, 2)
    s5c = s5c.reshape(2, 128, 2048)
    return {
        "w_in": np.ascontiguousarray(f(inp["w_in"])), "w_out": np.ascontiguousarray(f(inp["w_out"])),
        "ple_w": np.ascontiguousarray(f(inp["ple_w"])), "ple_gw": np.ascontiguousarray(f(inp["ple_gate_w"])),
        "glu_w": np.ascontiguousarray(f(inp["s5_glu_w"])), "vec": vec, "cst": make_consts()[0], "msk": make_consts()[1],
        "w2a2": w2a2, "lruw": np.ascontiguousarray(lruw), "s5s": np.ascontiguousarray(s5s),
        "s5b": np.ascontiguousarray(s5b), "s5c": np.ascontiguousarray(s5c),
    }


_NC_CACHE = {}


def run_cores(inp, TC, batches, dbg=None):
    key = (TC, tuple(sorted(dbg)) if dbg else None)
    if key not in _NC_CACHE:
        _NC_CACHE[key] = build_nc(TC, dbg)
    nc, ninst = _NC_CACHE[key]
    shared = pack_shared(inp)
    x = np.asarray(inp["x"], np.float32)
    p = np.asarray(inp["p"], np.float32)
    in_maps = []
    for b in batches:
        m = dict(shared)
        m["xT"] = np.ascontiguousarray(x[b, :TC].T)
        m["pT"] = np.ascontiguousarray(p[:, b, :TC].transpose(0, 2, 1))
        in_maps.append(m)
    res = run_bass_kernel_spmd(nc, in_maps, core_ids=list(range(len(batches))))
    return res


def kernel(**inputs):
    x = np.asarray(inputs["x"])
    B, S, _ = x.shape
    batches = [c % B for c in range(8)]
    res = run_cores(inputs, S, batches)
    out = np.empty((B, S, D), np.float32)
    for b in range(B):
        out[b] = res.results[b]["oT"].T
    return out.astype(x.dtype)
```

```python
import contextlib
import math
import numpy as np
import concourse.bass as bass
import concourse.mybir as mybir
from concourse.bass_utils import run_bass_kernel_spmd

F32 = mybir.dt.float32
BF16 = mybir.dt.bfloat16
ALU = mybir.AluOpType
AF = mybir.ActivationFunctionType

D = 1024
DIN = 4224
DMIX = 1536
DPLE = 256
TT = 256
LCH = 64
NCH = TT // LCH
TS5 = 128
import os
SKIP = set(os.environ.get("KSKIP", "").split(","))
S5E = os.environ.get("KS5E", "pool")
GW = tuple(int(v) for v in os.environ.get("KGW", "1,1,1").split(","))
GN_EPS = 64e-5
NORM_EPS = 1e-6


class Tok:
    __slots__ = ("sem", "val", "eng", "dma")

    def __init__(self, sem, val, eng, dma):
        self.sem, self.val, self.eng, self.dma = sem, val, eng, dma


class Buf:
    def __init__(self, t, name):
        self.t = t
        self.name = name
        self.w = None
        self.r = []

    def __getitem__(self, idx):
        return self.t[idx]


class Prog:
    ENGS = ("pe", "act", "dve", "pool", "sp")

    def __init__(self, nc, n_dma_sems=32):
        self.nc = nc
        self.es = contextlib.ExitStack()
        self.ops = {e: [] for e in self.ENGS}
        self.cnt = {e: 0 for e in self.ENGS}
        self.sem = {e: self.es.enter_context(nc.semaphore("s_" + e)) for e in self.ENGS}
        self.dsem = [self.es.enter_context(nc.semaphore("d%d" % i)) for i in range(n_dma_sems)]
        self.duse = [0] * n_dma_sems
        self.dnext = 0
        self.seen = {e: {} for e in self.ENGS}
        self.nbuf = 0
        self.ninst = 0
        self.stack = [self.es]

    def push(self):
        st = contextlib.ExitStack()
        self.stack.append(st)

    def pop(self):
        self.barrier()
        self.stack.pop().close()

    def barrier(self):
        toks = []
        for f in self.ENGS:
            if self.cnt[f] > 0:
                toks.append(Tok(self.sem[f], self.cnt[f], f, False))
        for i, s in enumerate(self.dsem):
            if self.duse[i] > 0:
                toks.append(Tok(s, 16 * self.duse[i], "dma", True))
        for e in self.ENGS:
            wl = []
            for t in toks:
                if t.eng == e and not t.dma:
                    continue
                k = id(t.sem)
                if self.seen[e].get(k, 0) >= t.val:
                    continue
                self.seen[e][k] = t.val
                wl.append((t.sem, t.val))

            def run(en, wl=wl):
                for (s, v) in wl:
                    en.wait_ge(s, v)
            self.ops[e].append(run)

    def sb(self, shape, dt=F32, name=None):
        self.nbuf += 1
        name = name or ("b%d" % self.nbuf)
        t = self.stack[-1].enter_context(self.nc.sbuf_tensor("sb_" + name, list(shape), dt))
        return Buf(t, name)

    def ps(self, name, dt=F32, cols=512):
        t = self.es.enter_context(self.nc.psum_tensor(name, [128, cols], dt))
        return Buf(t, name)

    def wrap(self, t, name):
        return Buf(t, name)

    def views(self, buf, n):
        return [Buf(buf.t, "%s.v%d" % (buf.name, i)) for i in range(n)]

    def _need(self, eng, tok, waits, is_dma_issue):
        if tok is None:
            return
        if tok.eng == eng and not tok.dma and not is_dma_issue and eng == "pe":
            return
        k = id(tok.sem)
        if self.seen[eng].get(k, 0) >= tok.val:
            return
        cur = waits.get(k)
        if cur is None or cur[1] < tok.val:
            waits[k] = (tok.sem, tok.val)

    def emit(self, eng, fn, reads=(), writes=(), dma=False):
        waits = {}
        for b in reads:
            self._need(eng, b.w, waits, dma)
        for b in writes:
            self._need(eng, b.w, waits, dma)
            for t in b.r:
                self._need(eng, t, waits, dma)
        if dma:
            i = self.dnext
            self.dnext = (self.dnext + 1) % len(self.dsem)
            s = self.dsem[i]
            if self.duse[i] > 0:
                self._need(eng, Tok(s, 16 * self.duse[i], "dma", True), waits, True)
            self.duse[i] += 1
            tok = Tok(s, 16 * self.duse[i], "dma", True)
            inc = 16
        else:
            self.cnt[eng] += 1
            tok = Tok(self.sem[eng], self.cnt[eng], eng, False)
            inc = 1
        wl = list(waits.values())
        for (s, v) in wl:
            self.seen[eng][id(s)] = v
        tsem = tok.sem
        self.ninst += 1 + len(wl)

        def run(e, wl=wl, fn=fn, tsem=tsem, inc=inc):
            for (s, v) in wl:
                e.wait_ge(s, v)
            fn(e).then_inc(tsem, inc)

        self.ops[eng].append(run)
        for b in reads:
            b.r.append(tok)
            if len(b.r) > 48:
                b.r = b.r[-48:]
        for b in writes:
            b.w = tok
            b.r = []
        return tok

    def finish(self, out_toks):
        wl = [(t.sem, t.val) for t in out_toks]

        def run(e, wl=wl):
            for (s, v) in wl:
                e.wait_ge(s, v)

        self.ops["sp"].append(run)
        nc = self.nc
        ops = self.ops
        with nc.Block() as block:
            @block.tensor
            def _(e):
                for f in ops["pe"]:
                    f(e)

            @block.scalar
            def _(e):
                for f in ops["act"]:
                    f(e)

            @block.vector
            def _(e):
                for f in ops["dve"]:
                    f(e)

            @block.gpsimd
            def _(e):
                for f in ops["pool"]:
                    f(e)

            @block.sync
            def _(e):
                for f in ops["sp"]:
                    f(e)
        self.es.close()

    def dma(self, out, in_, reads=(), writes=(), eng="sp", **kw):
        return self.emit(eng, lambda e: e.dma_start(out=out, in_=in_, **kw), reads, writes, dma=True)

    def mm(self, out, lhsT, rhs, start, stop, reads, writes):
        return self.emit("pe", lambda e: e.matmul(out, lhsT, rhs, start=start, stop=stop), reads, writes)

    def act(self, out, in_, func, reads, writes, bias=None, scale=None):
        kw = {}
        if bias is not None:
            kw["bias"] = bias
        if scale is not None:
            kw["scale"] = scale
        return self.emit("act", lambda e: e.activation(out=out, in_=in_, func=func, **kw), reads, writes)

    def tt(self, eng, out, in0, in1, op, reads, writes):
        return self.emit(eng, lambda e: e.tensor_tensor(out=out, in0=in0, in1=in1, op=op), reads, writes)

    def ts(self, eng, out, in0, s1, s2, op0, op1, reads, writes):
        if op1 is None:
            return self.emit(eng, lambda e: e.tensor_scalar(out, in0, s1, None, op0), reads, writes)
        return self.emit(eng, lambda e: e.tensor_scalar(out, in0, s1, s2, op0, op1), reads, writes)

    def stt(self, out, in0, scalar, in1, op0, op1, reads, writes):
        return self.emit("dve", lambda e: e.scalar_tensor_tensor(out, in0, scalar, in1, op0, op1), reads, writes)

    def copy(self, eng, out, in_, reads, writes):
        if eng == "act":
            return self.emit("act", lambda e: e.activation(out=out, in_=in_, func=AF.Copy), reads, writes)
        return self.emit(eng, lambda e: e.tensor_copy(out, in_), reads, writes)

    def memset(self, eng, ap, val, writes):
        return self.emit(eng, lambda e: e.memset(ap, val), (), writes)

    def scan(self, out, d0, d1, init, reads, writes):
        return self.emit("dve", lambda e: e.tensor_tensor_scan(out, d0, d1, init, ALU.mult, ALU.add), reads, writes)

    def recip(self, out, in_, reads, writes):
        return self.emit("dve", lambda e: e.reciprocal(out, in_), reads, writes)


VEC_FIELDS = [("ng", 8), ("mu", 13), ("w0", 4), ("a0", 4), ("kk", 4), ("ka", 4), ("rk", 4), ("lnw", 4),
              ("lnb", 4), ("s5d", 4), ("glub", 4), ("cw", 16), ("cb", 4), ("ba", 4), ("bx", 4), ("lam", 4),
              ("png", 8)]
VEC_PER_LAYER = sum(n for _, n in VEC_FIELDS)
VEC_OFF = {}
_o = 0
for _n, _c in VEC_FIELDS:
    VEC_OFF[_n] = _o
    _o += _c
NVEC = 2 * VEC_PER_LAYER + 8

CST_IDENT = 0
CST_ONESBD = 128
CST_SCAN = 256
NCST = 256 + TT
MSK_USN = 0
MSK_LSN = NCH * 128
MSK_USP = 2 * NCH * 128
MSK_CI = 3 * NCH * 128
NMSK = 3 * NCH * 128 + NCH * LCH


def vcol(l, name, i=0):
    return l * VEC_PER_LAYER + VEC_OFF[name] + i


def _pp(v, n):
    return np.ascontiguousarray(np.asarray(v, np.float32).reshape(n, 128).T)


def make_consts():
    c = np.zeros((128, NCST), np.float32)
    i = np.arange(128)[:, None]
    j = np.arange(128)[None, :]
    c[:, CST_IDENT:CST_IDENT + 128] = (i == j)
    c[:, CST_ONESBD:CST_ONESBD + 128] = ((i // 64) == (j // 64))
    tt = np.arange(TT)[None, :]
    c[:, CST_SCAN:CST_SCAN + TT] = 1.0 * ((tt % LCH) != 0)
    m = np.zeros((128, NMSK), np.float32)
    t = np.arange(64)[None, :]
    for ch in range(NCH):
        m[:, MSK_USN + ch * 128:MSK_USN + (ch + 1) * 128] = -1.0 * (j > i)
        m[:, MSK_LSN + ch * 128:MSK_LSN + (ch + 1) * 128] = -1.0 * (i > j)
        m[:, MSK_USP + ch * 128:MSK_USP + (ch + 1) * 128] = 1.0 * (j > i)
        m[:, MSK_CI + ch * 64:MSK_CI + (ch + 1) * 64] = 1.0 * (t >= (i % 64))
    return c, m


def build_nc(TC, dbg=None):
    assert TC % TT == 0
    NT = TC // TT
    dbg = dbg or set()
    nc = bass.Bass("TRN2", target_bir_lowering=False)

    def din(name, shape, dt=F32):
        return nc.dram_tensor(name, list(shape), dt, kind="ExternalInput").ap()

    xT = din("xT", [D, TC])
    pT = din("pT", [2, DPLE, TC])
    w_in = din("w_in", [2, D, DIN])
    w_out = din("w_out", [2, DMIX, D])
    ple_w = din("ple_w", [2, DPLE, D])
    ple_gw = din("ple_gw", [2, D, D])
    glu_w = din("glu_w", [2, 512, 512])
    vec_d = din("vec", [128, NVEC])
    cst_d = din("cst", [128, NCST])
    msk_d = din("msk", [128, NMSK])
    w2a2_d = din("w2a2", [2, 128, 512])
    lruw_d = din("lruw", [2, 128, 8 * 128])
    s5s_d = din("s5s", [2, 128, 3 * 16])
    s5b_d = din("s5b", [2, 128, 2 * 16 * 16])
    s5c_d = din("s5c", [2, 128, 2 * 16 * 64])
    oT = nc.dram_tensor("oT", [D, TC], F32, kind="ExternalOutput").ap()
    dbg_out = {}

    def dram_int(name, shape, dt):
        return nc.dram_tensor(name, list(shape), dt, kind="Internal").ap()

    w_in_b = dram_int("w_in_b", [2, D, DIN], BF16)
    w_out_b = dram_int("w_out_b", [2, DMIX, D], BF16)
    ple_w_b = dram_int("ple_w_b", [2, DPLE, D], BF16)
    ple_gw_b = dram_int("ple_gw_b", [2, D, D], BF16)
    glu_w_b = dram_int("glu_w_b", [2, 512, 512], BF16)
    s5tab_d = dram_int("s5tab", [2, 128, 2 * 16 * TS5], F32)

    P = Prog(nc)
    wsb = P.wrap(None, "wscratch")
    tabsb = P.wrap(None, "s5tabscr")

    def dbg_dump(name, buf, ap, shape, dt=F32):
        if name not in dbg:
            return
        o = nc.dram_tensor("dbg_" + name, list(shape), dt, kind="ExternalOutput").ap()
        dbg_out[name] = P.dma(o, ap, reads=[buf])

    vec = P.sb([128, NVEC], F32, "vec")
    cst = P.sb([128, NCST], F32, "cst")
    P.dma(vec[:], vec_d, writes=[vec])
    P.dma(cst[:], cst_d, writes=[cst])
    cstb = P.sb([128, 128], BF16, "cstb")
    P.copy("dve", cstb[:], cst[:, CST_IDENT:CST_IDENT + 128], [cst], [cstb])
    mskb = P.sb([128, NMSK], BF16, "mskb")
    ident_f = cst[:, CST_IDENT:CST_IDENT + 128]
    ident_b = cstb[:, 0:128]
    onesbd_f = cst[:, CST_ONESBD:CST_ONESBD + 128]
    ones_f = P.sb([128, 128], F32, "ones_f")
    P.memset("pool", ones_f[:], 1.0, [ones_f])
    one_t = P.sb([128, 1], F32, "one_t")
    P.memset("pool", one_t[:], 1.0, [one_t])

    def V(l, name, i=0, n=1):
        c = vcol(l, name, i)
        return vec[:, c:c + n]

    for l in range(2):
        for (src, dst, rows) in ((w_in, w_in_b, D), (w_out, w_out_b, DMIX), (ple_w, ple_w_b, DPLE),
                                 (ple_gw, ple_gw_b, D), (glu_w, glu_w_b, 512)):
            for r0 in range(0, rows, 128):
                P.dma(dst[l, r0:r0 + 128, :], src[l, r0:r0 + 128, :], writes=[wsb], eng="pool",
                      max_dma_last_dim=4096)

    w2a2 = []
    lruw = []
    for l in range(2):
        w2a2.append(P.sb([128, 512], BF16, "w2a2_%d" % l))
        lruw.append(P.sb([128, 1024], BF16, "lruw_%d" % l))
    lru_c = P.sb([128, 2, 8], F32, "lru_c")
    s5B = [P.sb([128, 2 * 4 * 2 * 128], BF16, "s5B%d" % l) for l in range(2)]
    s5C = [P.sb([128, 2048], BF16, "s5C%d" % l) for l in range(2)]
    s5keep = [P.sb([128, 3, 16], F32, "s5keep%d" % l) for l in range(2)]
    s5rotb = [P.sb([128, 2, 16], F32, "s5rot%d" % l) for l in range(2)]
    ps_misc = P.ps("ps7")
    P.push()
    stage = P.sb([128, 1024], F32, "stage")
    mstage = P.sb([128, NMSK], F32, "mstage")
    P.dma(mstage[:], msk_d, writes=[mstage])
    P.copy("act", mskb[:], mstage[:], [mstage], [mskb])
    for l in range(2):
        P.dma(stage[:, 0:512], w2a2_d[l], writes=[stage])
        P.copy("act", w2a2[l][:], stage[:, 0:512], [stage], [w2a2[l]])
        P.dma(stage[:], lruw_d[l], writes=[stage])
        P.copy("act", lruw[l][:], stage[:], [stage], [lruw[l]])

    for l in range(2):
        tmp = P.sb([128, 4], F32, "lrutmp%d" % l)
        P.act(tmp[:], V(l, "lam", 0, 4), AF.Exp, [vec], [tmp], scale=-1.0)
        P.act(tmp[:], tmp[:], AF.Ln, [tmp, one_t], [tmp], bias=one_t[:, 0:1])
        P.ts("dve", lru_c[:, l, 0:4], tmp[:], -8.0, None, ALU.mult, None, [tmp], [lru_c])
        P.ts("dve", lru_c[:, l, 4:8], tmp[:], -16.0, None, ALU.mult, None, [tmp], [lru_c])

    s5rot = []
    for l in range(2):
        s5s = P.sb([128, 48], F32, "s5s%d" % l)
        P.dma(s5s[:], s5s_d[l], writes=[s5s])
        a_re = s5s[:, 0:16]
        a_im = s5s[:, 16:32]
        ldt = s5s[:, 32:48]
        w = P.sb([128, 16, 16], F32, "s5w%d" % l)
        R = [w]

        def row(i):
            return w[:, i, :]
        dt_, rho, th, cc, ss, t1, t2, lr, li, den, qre, qim, nr = [row(i) for i in range(13)]
        P.act(dt_, ldt, AF.Exp, [s5s], R)
        P.tt("dve", rho, a_re, dt_, ALU.mult, [s5s] + R, R)
        P.act(rho, rho, AF.Exp, R, R)
        P.tt("dve", th, a_im, dt_, ALU.mult, [s5s] + R, R)
        hp = P.sb([128, 1], F32, "halfpi%d" % l)
        P.memset("dve", hp[:], math.pi / 2, [hp])
        P.act(cc, th, AF.Sin, R + [hp], R, bias=hp[:, 0:1], scale=1.0 / 16)
        P.act(ss, th, AF.Sin, R, R, scale=1.0 / 16)

        def csq(c_, s_):
            P.tt("dve", t1, c_, c_, ALU.mult, R, R)
            P.tt("dve", t2, s_, s_, ALU.mult, R, R)
            P.stt(s_, c_, 2.0, s_, ALU.mult, ALU.mult, R, R)
            P.tt("dve", c_, t1, t2, ALU.subtract, R, R)
        for _ in range(4):
            csq(cc, ss)
        P.tt("dve", lr, rho, cc, ALU.mult, R, R)
        P.tt("dve", li, rho, ss, ALU.mult, R, R)
        P.tt("dve", t1, a_re, a_re, ALU.mult, [s5s] + R, R)
        P.tt("dve", t2, a_im, a_im, ALU.mult, [s5s] + R, R)
        P.tt("dve", den, t1, t2, ALU.add, R, R)
        P.recip(den, den, R, R)
        P.ts("dve", nr, lr, -1.0, None, ALU.add, None, R, R)
        P.tt("dve", t1, nr, a_re, ALU.mult, [s5s] + R, R)
        P.tt("dve", t2, li, a_im, ALU.mult, [s5s] + R, R)
        P.tt("dve", t1, t1, t2, ALU.add, R, R)
        P.tt("dve", qre, t1, den, ALU.mult, R, R)
        P.tt("dve", t1, li, a_re, ALU.mult, [s5s] + R, R)
        P.tt("dve", t2, nr, a_im, ALU.mult, [s5s] + R, R)
        P.tt("dve", t1, t1, t2, ALU.subtract, R, R)
        P.tt("dve", qim, t1, den, ALU.mult, R, R)
        keep = s5keep[l]
        P.copy("dve", keep[:, 0, :], rho, R, [keep])
        P.copy("dve", keep[:, 1, :], cc, R, [keep])
        P.copy("dve", keep[:, 2, :], ss, R, [keep])

        sbf = P.sb([128, 512], F32, "s5b_in%d" % l)
        P.dma(sbf[:], s5b_d[l], writes=[sbf])
        bre = sbf[:, 0:256].rearrange("p (j h) -> p j h", h=16)
        bim = sbf[:, 256:512].rearrange("p (j h) -> p j h", h=16)
        Bt = s5B[l]
        Btv = Bt[:, :].rearrange("p (r b q m) -> p r b q m", r=2, b=4, q=2)
        bpad = P.sb([128, 2, 128], BF16, "s5bpad%d" % l)
        tb = P.sb([128, 2, 16], F32, "s5tb%d" % l)
        for j in range(16):
            P.ts("dve", tb[:, 0, :], bim[:, j, :], qim[:, j:j + 1], None, ALU.mult, None, [sbf] + R, [tb])
            P.stt(tb[:, 0, :], bre[:, j, :], qre[:, j:j + 1], tb[:, 0, :], ALU.mult, ALU.subtract, [sbf, tb] + R, [tb])
            P.ts("dve", tb[:, 1, :], bre[:, j, :], qim[:, j:j + 1], None, ALU.mult, None, [sbf] + R, [tb])
            P.stt(tb[:, 1, :], bim[:, j, :], qre[:, j:j + 1], tb[:, 1, :], ALU.mult, ALU.add, [sbf, tb] + R, [tb])
            P.memset("pool", bpad[:], 0.0, [bpad])
            for gh in range(2):
                col0 = 32 * (j % 4) + gh * 16
                for ri in range(2):
                    P.copy("pool", bpad[gh * 64:(gh + 1) * 64, ri, col0:col0 + 16],
                           tb[gh * 64:(gh + 1) * 64, ri, :], [tb], [bpad])
            for ri in range(2):
                P.mm(ps_misc[:, ri * 128:(ri + 1) * 128], bpad[:, ri, :], ident_b, True, True, [bpad, cstb], [ps_misc])
            hf = (j % 4) // 2
            for ri in range(2):
                P.copy("act", Btv[64 * hf:64 * hf + 64, ri, j // 4, j % 2, :],
                       ps_misc[64 * hf:64 * hf + 64, ri * 128:(ri + 1) * 128], [ps_misc], [Bt])

        scf = P.sb([128, 2048], F32, "s5c_in%d" % l)
        P.dma(scf[:], s5c_d[l], writes=[scf])
        Ct = s5C[l]
        P.copy("act", Ct[:, 0:1024], scf[:, 0:1024], [scf], [Ct])
        P.ts("dve", Ct[:, 1024:2048], scf[:, 1024:2048], -1.0, None, ALU.mult, None, [scf], [Ct])

        tab = P.sb([128, 2, 16, TS5], F32, "s5tabb%d" % l)
        P.memset("pool", tab[:, 0, :, 0:1], 1.0, [tab])
        P.memset("pool", tab[:, 1, :, 0:1], 0.0, [tab])
        ec = P.sb([128, 2, 16], F32, "s5ec%d" % l)
        P.copy("dve", ec[:, 0, :], cc, R, [ec])
        P.copy("dve", ec[:, 1, :], ss, R, [ec])
        m = 1
        while m < TS5:
            for j in range(16):
                cj = ec[:, 0, j:j + 1]
                sj = ec[:, 1, j:j + 1]
                src_c = tab[:, 0, j, 0:m]
                src_s = tab[:, 1, j, 0:m]
                dst_c = tab[:, 0, j, m:2 * m]
                dst_s = tab[:, 1, j, m:2 * m]
                P.ts("dve", dst_c, src_s, sj, None, ALU.mult, None, [tab, ec], [tab])
                P.stt(dst_c, src_c, cj, dst_c, ALU.mult, ALU.subtract, [tab, ec], [tab])
                P.ts("dve", dst_s, src_c, sj, None, ALU.mult, None, [tab, ec], [tab])
                P.stt(dst_s, src_s, cj, dst_s, ALU.mult, ALU.add, [tab, ec], [tab])
            e_c = ec[:, 0, :]
            e_s = ec[:, 1, :]
            P.tt("dve", t1, e_c, e_c, ALU.mult, [ec] + R, R)
            P.tt("dve", t2, e_s, e_s, ALU.mult, [ec] + R, R)
            P.stt(e_s, e_c, 2.0, e_s, ALU.mult, ALU.mult, [ec], [ec])
            P.tt("dve", e_c, t1, t2, ALU.subtract, R, [ec])
            m *= 2
        rot = s5rotb[l]
        P.copy("dve", rot[:], ec[:], [ec], [rot])
        s5rot.append((rot, keep))
        P.dma(s5tab_d[l], tab[:, :, :, :].rearrange("p a j t -> p (a j t)"), reads=[tab], writes=[tabsb])
        dbg_dump("s5w%d" % l, w, w[:, :, :].rearrange("p a b -> p (a b)"), [128, 256])
        dbg_dump("s5keep%d" % l, keep, keep[:, :, :].rearrange("p a b -> p (a b)"), [128, 48])
        dbg_dump("s5tab%d" % l, tab, tab[:, :, :, :].rearrange("p a j t -> p (a j t)"), [128, 2 * 16 * TS5])
        dbg_dump("s5B%d" % l, Bt, Bt[:, :], [128, 2048], BF16)
    P.pop()

    hT = P.sb([128, 8, TT], F32, "hT")
    xn = P.sb([128, 8, TT], BF16, "xn")
    zst = [P.sb([128, 1 + TT], F32, "zst%d" % i) for i in range(2)]
    zs = P.sb([128, 13, TT], F32, "zs")
    zsv = P.views(zs, 13)
    zu = P.sb([128, 4, TT], F32, "zu")
    zx = P.sb([128, 4, 3 + TT], F32, "zx")
    sgate = P.sb([128, 12, TT], BF16, "sgate")
    sgv = P.views(sgate, 12)
    ycat = P.sb([128, 12, TT], BF16, "ycat")
    ycv = P.views(ycat, 12)
    pbf = P.sb([128, 2, TT], BF16, "pbf")
    ring = [P.sb([128, 4096], BF16, "ring%d" % i) for i in range(3)]
    ringi = [0]
    s5tab = P.sb([128, 2, 16, TS5], F32, "s5tab")
    banks = [P.ps("ps%d" % i) for i in range(7)] + [ps_misc]
    rot_i = {"proj": 0, "rw": 0}

    def bank(group):
        ids = (0, 1) if group == "proj" else (2, 3)
        i = rot_i[group]
        rot_i[group] = (i + 1) % len(ids)
        return banks[ids[i]]

    def next_ring():
        r = ring[ringi[0]]
        ringi[0] = (ringi[0] + 1) % 3
        return r

    cz = [P.sb([128, 13], F32, "cz%d" % l) for l in range(2)]
    cl = [P.sb([128, 4, 3], F32, "cl%d" % l) for l in range(2)]
    ch = [P.sb([128, 4], F32, "ch%d" % l) for l in range(2)]
    s5z = [P.sb([128, 2, 16], F32, "s5z%d" % l) for l in range(2)]
    s5zv = [P.views(s5z[l], 16) for l in range(2)]
    Tst = [[P.sb([128, 128], BF16, "T%d_%d" % (l, pb)) for pb in range(4)] for l in range(2)]
    for l in range(2):
        P.memset("pool", cz[l][:], 0.0, [cz[l]])
        P.memset("pool", cl[l][:], 0.0, [cl[l]])
        P.memset("pool", ch[l][:], 0.0, [ch[l]])
        P.memset("pool", s5z[l][:], 0.0, s5zv[l])
        for pb in range(4):
            P.memset("pool", Tst[l][pb][:], 0.0, [Tst[l][pb]])

    NF = 17
    fs = [P.sb([128, TT], F32, "rf%d" % i) for i in range(NF)]
    pad_names = ["RTp", "KTp", "CTp", "BTp", "VTp", "KGp", "BGp"]
    pads = {n: P.sb([128, NCH * 128], BF16, n) for n in pad_names}
    for n in pad_names:
        P.memset("pool", pads[n][:], 0.0, [pads[n]])
    RTc = P.sb([128, TT], BF16, "RTc")
    tanh_wd = P.sb([128, TT], BF16, "tanhwd")
    Blev_i = [[P.sb([128, NCH * 128], BF16, "Blev%d_%d" % (i, k)) for i in range(2)] for k in range(2)]
    BTlev_i = [[P.sb([128, NCH * 128], BF16, "BTlev%d_%d" % (i, k)) for i in range(2)] for k in range(2)]
    AkkT_i = [P.sb([128, NCH * 128], BF16, "AkkT_%d" % k) for k in range(2)]
    Xbf_i = [P.sb([128, NCH * 256], BF16, "Xbf_%d" % k) for k in range(2)]
    NPI = 2
    pp = []
    for i in range(NPI):
        d = {}
        for n in ("Vbd", "KGbd", "BGbd", "PT", "nU0", "Wbd"):
            d[n] = P.sb([128, NCH * 128], BF16, "%s_%d" % (n, i))
        for n in ("Rhat", "ArkT", "ArbT"):
            d[n] = P.sb([128, TT], BF16, "%s_%d" % (n, i))
        d["bonus"] = P.sb([128, TT], F32, "bonus_%d" % i)
        d["GL"] = P.sb([128, NCH], F32, "GL_%d" % i)
        d["rt32"] = P.sb([128, TT], F32, "rt32_%d" % i)
        pp.append(d)
    mix = P.sb([128, 4, TT], F32, "mix")
    mixv = P.views(mix, 4)

    NS5SET = 2
    s5f = [[P.sb([128, TS5], F32, "s5f%d_%d" % (k, i)) for i in range(8)] for k in range(NS5SET)]
    s5x = [P.sb([128, 2, TS5], BF16, "s5x%d" % k) for k in range(NS5SET)]
    s5zl = [P.sb([128, 2], F32, "s5zl%d" % k) for k in range(NS5SET)]
    spf = [P.sb([128, TT], F32, "spf%d" % i) for i in range(3)]
    lf = [P.sb([128, TT], F32, "lf%d" % i) for i in range(4)]
    ubf = P.sb([128, 4, TT], BF16, "ubf")
    lb = P.sb([128, TT], BF16, "lb")
    rstd = P.sb([128, TT], F32, "rstd")
    sq = P.sb([128, 4, TT], BF16, "sq")
    ones_b = P.sb([128, 128], BF16, "ones_b")
    P.memset("pool", ones_b[:], 1.0, [ones_b])

    out_toks = []
    eps_t = {}
    for e_ in (NORM_EPS, GN_EPS):
        t = P.sb([128, 1], F32, "eps%d" % len(eps_t))
        P.memset("pool", t[:], e_, [t])
        eps_t[e_] = t
    neg_half = -math.exp(-0.5)

    def rms_rstd(src_bufs, src_ap_fn, nblk, eps):
        b = bank("proj")
        for k in range(nblk):
            P.act(sq[:, k % 4, :], src_ap_fn(k), AF.Square, src_bufs, [sq])
            P.mm(b[:, 0:TT], ones_b[:], sq[:, k % 4, :], k == 0, k == nblk - 1, [ones_b, sq], [b])
        P.act(rstd[:], b[:, 0:TT], AF.Ln, [b, eps_t[eps]], [rstd], bias=eps_t[eps][:, 0:1], scale=1.0 / (nblk * 128))
        P.act(rstd[:], rstd[:], AF.Exp, [rstd], [rstd], scale=-0.5)

    def drive(items):
        active = list(items)
        while active:
            for item in list(active):
                g, w = item
                for _ in range(w):
                    try:
                        next(g)
                    except StopIteration:
                        active.remove(item)
                        break

    s5ctr = [0]

    for it in range(NT):
        t0 = it * TT
        first = (it == 0)
        P.dma(hT[:], xT[:, t0:t0 + TT].rearrange("(k p) t -> p k t", p=128), writes=[hT])
        for l in range(2):
            dbg_on = first and l == 0
            P.dma(pbf[:], pT[l, :, t0:t0 + TT].rearrange("(k p) t -> p k t", p=128), writes=[pbf], eng="pool")
            P.dma(s5tab[:, :, :, :].rearrange("p a j t -> p (a j t)"), s5tab_d[l], reads=[tabsb], writes=[s5tab])
            rms_rstd([hT], lambda k: hT[:, k, :], 8, NORM_EPS)
            for k in range(8):
                P.stt(xn[:, k, :], hT[:, k, :], V(l, "ng", k), rstd[:], ALU.mult, ALU.mult, [hT, vec, rstd], [xn])
            if dbg_on:
                dbg_dump("xn", xn, xn[:, :, :].rearrange("p a t -> p (a t)"), [128, 8 * TT], BF16)

            wchunk = {}

            def in_block(cb, l=l, wchunk=wchunk):
                ci = cb // 4
                if ci not in wchunk:
                    r = next_ring()
                    ncol = 512 if ci < 8 else 128
                    P.dma(r[:, 0:8 * ncol].rearrange("p (k n) -> p k n", k=8),
                          w_in_b[l].rearrange("(k p) n -> p k n", p=128)[:, :, ci * 512:ci * 512 + ncol],
                          reads=[wsb], writes=[r])
                    wchunk[ci] = (r, ncol)
                r, ncol = wchunk[ci]
                rv = r[:, 0:8 * ncol].rearrange("p (k n) -> p k n", k=8)
                c0 = (cb % 4) * 128
                b = bank("proj")
                for k in range(8):
                    P.mm(b[:, 0:TT], rv[:, k, c0:c0 + 128], xn[:, k, :], k == 0, k == 7, [r, xn], [b])
                return b

            for cb in range(13):
                st = zst[cb % 2]
                b = in_block(cb)
                P.copy("act", st[:, 0:1], cz[l][:, cb:cb + 1], [cz[l]], [st])
                P.copy("act", st[:, 1:1 + TT], b[:, 0:TT], [b], [st])
                P.copy("act", cz[l][:, cb:cb + 1], st[:, TT:TT + 1], [st], [cz[l]])
                d_ = lf[cb % 2]
                P.tt("pool", d_[:], st[:, 0:TT], st[:, 1:1 + TT], ALU.subtract, [st], [d_])
                P.stt(zs[:, cb, :], d_[:], V(l, "mu", cb), st[:, 1:1 + TT], ALU.mult, ALU.add, [d_, vec, st], [zsv[cb]])
            if dbg_on:
                dbg_dump("zs", zs, zs[:, :, :].rearrange("p a t -> p (a t)"), [128, 13 * TT])
            P.act(tanh_wd[0:64, :], zs[0:64, 12, :], AF.Tanh, [zsv[12]], [tanh_wd])
            P.copy("act", tanh_wd[64:128, :], zs[64:128, 12, :], [zsv[12]], [tanh_wd])
            for blk in range(4):
                b = in_block(13 + blk)
                P.act(sgate[:, blk, :], b[:, 0:TT], AF.Silu, [b], [sgv[blk]])
            for blk in range(4):
                b = in_block(17 + blk)
                P.copy("act", zu[:, blk, :], b[:, 0:TT], [b], [zu])
            P.copy("pool", ubf[:], zu[:], [zu], [ubf])
            for blk in range(4):
                b = in_block(21 + blk)
                P.act(sgate[:, 4 + blk, :], b[:, 0:TT], AF.Silu, [b], [sgv[4 + blk]])
            P.copy("act", zx[:, :, 0:3], cl[l][:, :, :], [cl[l]], [zx])
            for blk in range(4):
                b = in_block(25 + blk)
                P.copy("act", zx[:, blk, 3:3 + TT], b[:, 0:TT], [b], [zx])
            P.copy("act", cl[l][:, :, :], zx[:, :, TT:TT + 3], [zx], [cl[l]])
            for blk in range(4):
                b = in_block(29 + blk)
                P.act(sgate[:, 8 + blk, :], b[:, 0:TT], AF.Silu, [b], [sgv[8 + blk]])

            def prep(pb, inst, l=l, dbg_on=dbg_on):
                d = pp[inst]
                Blev, BTlev, AkkT = Blev_i[inst], BTlev_i[inst], AkkT_i[inst]
                r_ = zs[:, pb, :]
                k_ = zs[:, 4 + pb, :]
                v_ = zs[:, 8 + pb, :]
                zr_, zk_, zv_ = zsv[pb], zsv[4 + pb], zsv[8 + pb]
                (sg, ld, a_, kk_, kk2, sqk, kap, t1, kp, b_, lg, eg, ieg, eg1, dl, egl, rk) = fs[:17]
                rt32 = d["rt32"]
                cols = slice(pb * 128, (pb + 1) * 128)
                bw = bank("rw")
                P.mm(bw[:, 0:TT], w2a2[l][0:64, cols], tanh_wd[0:64, :], True, True, [w2a2[l], tanh_wd], [bw])
                P.act(sg[:], bw[:, 0:TT], AF.Sigmoid, [bw, vec], [sg], bias=V(l, "w0", pb))
                P.ts("dve", ld[:], sg[:], neg_half, None, ALU.mult, None, [sg], [ld])
                ba_ = bank("rw")
                P.mm(ba_[:, 0:TT], w2a2[l][64:128, cols], tanh_wd[64:128, :], True, True, [w2a2[l], tanh_wd], [ba_])
                P.act(a_[:], ba_[:, 0:TT], AF.Sigmoid, [ba_, vec], [a_], bias=V(l, "a0", pb))
                yield
                P.scan(lg[:], cst[:, CST_SCAN:CST_SCAN + TT], ld[:], 0.0, [cst, ld], [lg])
                P.ts("dve", kk_[:], k_, V(l, "kk", pb), None, ALU.mult, None, [zk_, vec], [kk_])
                P.tt("pool", kk2[:], kk_[:], kk_[:], ALU.mult, [kk_], [kk2])
                bs = bank("rw")
                P.mm(bs[:, 0:TT], onesbd_f, kk2[:], True, True, [cst, kk2], [bs])
                P.act(sqk[:], bs[:, 0:TT], AF.Sqrt, [bs], [sqk])
                yield
                P.act(eg[:], lg[:], AF.Exp, [lg], [eg])
                P.act(ieg[:], lg[:], AF.Exp, [lg], [ieg], scale=-1.0)
                P.tt("pool", eg1[:], lg[:], ld[:], ALU.subtract, [lg, ld], [eg1])
                P.act(eg1[:], eg1[:], AF.Exp, [eg1], [eg1])
                P.ts("dve", sqk[:], sqk[:], 1e-12, None, ALU.max, None, [sqk], [sqk])
                P.recip(sqk[:], sqk[:], [sqk], [sqk])
                P.tt("pool", kap[:], kk_[:], sqk[:], ALU.mult, [kk_, sqk], [kap])
                yield
                P.ts("dve", t1[:], a_[:], -1.0, V(l, "ka", pb), ALU.add, ALU.mult, [a_, vec], [t1])
                P.stt(kp[:], t1[:], 1.0, k_, ALU.add, ALU.mult, [t1, zk_], [kp])
                P.tt("pool", b_[:], kap[:], a_[:], ALU.mult, [kap, a_], [b_])
                lg3 = lg[:, :].rearrange("p (c t) -> p c t", t=LCH)
                P.tt("dve", dl[:, :].rearrange("p (c t) -> p c t", t=LCH), lg3[:, :, LCH - 1:LCH].to_broadcast([128, NCH, LCH]),
                     lg3, ALU.subtract, [lg], [dl])
                P.act(egl[:], dl[:], AF.Exp, [dl], [egl])
                P.copy("act", d["GL"][:, :], eg[:, :].rearrange("p (c t) -> p c t", t=LCH)[:, :, LCH - 1], [eg], [d["GL"]])
                yield
                P.tt("dve", rt32[:], r_, eg[:], ALU.mult, [zr_, eg], [rt32])
                P.copy("act", RTc[:], rt32[:], [rt32], [RTc])

                def padw(name, eng, in0, in1, rd):
                    t = pads[name]
                    tv = t[:, :].rearrange("p (c h t) -> p c h t", c=NCH, h=2)
                    for hh in range(2):
                        ps_ = slice(hh * 64, (hh + 1) * 64)
                        o = tv[ps_, :, hh, :]
                        i0 = in0[ps_, :].rearrange("p (c t) -> p c t", t=LCH)
                        if in1 is None:
                            P.copy(eng, o, i0, rd, [t])
                        else:
                            i1 = in1[ps_, :].rearrange("p (c t) -> p c t", t=LCH)
                            P.tt(eng, o, i0, i1, ALU.mult, rd, [t])
                padw("RTp", "act", rt32, None, [rt32])
                padw("KTp", "dve", kp, ieg, [kp, ieg])
                padw("CTp", "pool", kap, eg1, [kap, eg1])
                yield
                padw("BTp", "dve", b_, ieg, [b_, ieg])
                padw("VTp", "act", zs[:, 8 + pb, :], None, [zv_])
                padw("KGp", "pool", kp, egl, [kp, egl])
                padw("BGp", "dve", b_, egl, [b_, egl])
                yield "pre_pp"
                P.stt(rk[:], r_, V(l, "rk", pb), kp[:], ALU.mult, ALU.mult, [zr_, vec, kp], [rk])
                bb = bank("rw")
                P.mm(bb[:, 0:TT], onesbd_f, rk[:], True, True, [cst, rk], [bb])
                P.tt("dve", d["bonus"][:], bb[:, 0:TT], v_, ALU.mult, [bb, zv_], [d["bonus"]])
                if dbg_on and pb == 0:
                    dbg_dump("lg", lg, lg[:], [128, TT])
                    dbg_dump("kap", kap, kap[:], [128, TT])
                    dbg_dump("kp", kp, kp[:], [128, TT])
                    dbg_dump("a", a_, a_[:], [128, TT])
                yield

                def chunkmm(dst_bank, lname, rname, rbuf=None, rcols=128):
                    lt = pads[lname]
                    for c in range(NCH):
                        if rbuf is None:
                            rb_ = pads[rname]
                            rap = rb_[:, c * 128:(c + 1) * 128]
                        else:
                            rb_ = rbuf
                            rap = rbuf[:, c * rcols:(c + 1) * rcols]
                        P.mm(dst_bank[:, c * rcols:(c + 1) * rcols], lt[:, c * 128:(c + 1) * 128], rap, True, True,
                             [lt, rb_], [dst_bank])

                def masked(dst, src_bank, mcol, w):
                    n = NCH * w
                    P.tt("dve", dst[:, 0:n], src_bank[:, 0:n], mskb[:, mcol:mcol + n], ALU.mult, [src_bank, mskb], [dst])
                b1 = bank("rw")
                chunkmm(b1, "BTp", "CTp")
                masked(BTlev[0], b1, MSK_USN, 128)
                b2 = bank("rw")
                chunkmm(b2, "CTp", "BTp")
                masked(Blev[0], b2, MSK_LSN, 128)
                yield
                b3 = bank("rw")
                chunkmm(b3, "KTp", "CTp")
                masked(AkkT, b3, MSK_USP, 128)
                b4 = bank("rw")
                chunkmm(b4, "KTp", None, RTc, LCH)
                masked(d["ArkT"], b4, MSK_CI, LCH)
                b5 = bank("rw")
                chunkmm(b5, "BTp", None, RTc, LCH)
                masked(d["ArbT"], b5, MSK_CI, LCH)
                yield

            def tokmajor(src_name, dst_buf, dst_ap, eng):
                bt_ = bank("rw")
                lt = pads[src_name]
                for c in range(NCH):
                    P.mm(bt_[:, c * 128:(c + 1) * 128], lt[:, c * 128:(c + 1) * 128], ident_b, True, True,
                         [lt, cstb], [bt_])
                P.copy(eng, dst_ap, bt_[:, 0:NCH * 128] if len(dst_ap.shape) == 2 else
                       bt_[:, 0:NCH * 128].rearrange("p (c n) -> p c n", n=128), [bt_], [dst_buf])

            def solve(pb, inst, l=l):
                d = pp[inst]
                Blev, BTlev, AkkT, Xbf = Blev_i[inst], BTlev_i[inst], AkkT_i[inst], Xbf_i[inst]
                Xbfv = Xbf[:, :].rearrange("p (c n) -> p c n", n=256)
                tokmajor("VTp", d["Vbd"], d["Vbd"][:, :], "act")
                tokmajor("KGp", d["KGbd"], d["KGbd"][:, :], "act")
                yield
                tokmajor("BGp", d["BGbd"], d["BGbd"][:, :], "act")
                tokmajor("CTp", Xbf, Xbfv[:, :, 0:128], "act")
                bt_ = bank("rw")
                for c in range(NCH):
                    cs = slice(c * 128, (c + 1) * 128)
                    P.mm(bt_[:, cs], AkkT[:, cs], d["Vbd"][:, cs], True, True, [AkkT, d["Vbd"]], [bt_])
                P.copy("act", Xbfv[:, :, 128:256], bt_[:, 0:NCH * 128].rearrange("p (c n) -> p c n", n=128), [bt_], [Xbf])
                yield "tok_done"
                cur = 0
                NLEV = 6
                for lev in range(NLEV):
                    if lev < NLEV - 1:
                        nxt = 1 - cur
                        bq = bank("rw")
                        for c in range(NCH):
                            cs = slice(c * 128, (c + 1) * 128)
                            P.mm(bq[:, cs], Blev[cur][:, cs], BTlev[cur][:, cs], True, True, [Blev[cur], BTlev[cur]], [bq])
                        if lev < NLEV - 2:
                            bq2 = bank("rw")
                            for c in range(NCH):
                                cs = slice(c * 128, (c + 1) * 128)
                                P.mm(bq2[:, cs], BTlev[cur][:, cs], Blev[cur][:, cs], True, True,
                                     [Blev[cur], BTlev[cur]], [bq2])
                    for half in range(2):
                        bx_ = banks[4 + half]
                        for cc_ in range(2):
                            c = half * 2 + cc_
                            P.mm(bx_[:, cc_ * 256:(cc_ + 1) * 256], BTlev[cur][:, c * 128:(c + 1) * 128], Xbfv[:, c, :],
                                 True, False, [BTlev[cur], Xbf], [bx_])
                            P.mm(bx_[:, cc_ * 256:(cc_ + 1) * 256], ident_b, Xbfv[:, c, :],
                                 False, True, [cstb, Xbf], [bx_])
                    if lev < NLEV - 1:
                        P.copy("act", BTlev[nxt][:], bq[:, 0:512], [bq], [BTlev[nxt]])
                        if lev < NLEV - 2:
                            P.copy("dve", Blev[nxt][:], bq2[:, 0:512], [bq2], [Blev[nxt]])
                    P.copy("act", Xbf[:, 0:512], banks[4][:, 0:512], [banks[4]], [Xbf])
                    P.copy("dve", Xbf[:, 512:1024], banks[5][:, 0:512], [banks[5]], [Xbf])
                    if lev < NLEV - 1:
                        cur = nxt
                    yield
                nU0v = d["nU0"][:, :].rearrange("p (c n) -> p c n", n=128)
                Wbdv = d["Wbd"][:, :].rearrange("p (c n) -> p c n", n=128)
                P.ts("dve", nU0v, Xbfv[:, :, 128:256], -1.0, None, ALU.mult, None, [Xbf], [d["nU0"]])
                P.copy("pool", Wbdv, Xbfv[:, :, 0:128], [Xbf], [d["Wbd"]])
                br = bank("rw")
                for c in range(NCH):
                    P.mm(br[:, c * LCH:(c + 1) * LCH], d["Wbd"][:, c * 128:(c + 1) * 128],
                         d["ArbT"][:, c * LCH:(c + 1) * LCH], True, True, [d["Wbd"], d["ArbT"]], [br])
                P.tt("dve", d["Rhat"][:], d["rt32"][:], br[:, 0:TT], ALU.subtract, [d["rt32"], br], [d["Rhat"]])
                bp = bank("rw")
                for c in range(NCH):
                    cs = slice(c * 128, (c + 1) * 128)
                    P.mm(bp[:, cs], d["Wbd"][:, cs], d["BGbd"][:, cs], True, True, [d["Wbd"], d["BGbd"]], [bp])
                for c in range(NCH):
                    cs = slice(c * 128, (c + 1) * 128)
                    P.stt(d["PT"][:, cs], ident_f, d["GL"][:, c:c + 1], bp[:, cs], ALU.mult, ALU.subtract,
                          [cst, d["GL"], bp], [d["PT"]])
                yield

            def seq(pb, inst, c, l=l):
                d = pp[inst]
                T = Tst[l][pb]
                yb = banks[6]
                tb_ = banks[4 + inst]
                ycols = slice(inst * TT + c * LCH, inst * TT + (c + 1) * LCH)
                cs = slice(c * 128, (c + 1) * 128)
                cl_ = slice(c * LCH, (c + 1) * LCH)
                P.mm(yb[:, ycols], T[:], d["Rhat"][:, cl_], True, False, [T, d["Rhat"]], [yb])
                P.mm(yb[:, ycols], d["Vbd"][:, cs], d["ArkT"][:, cl_], False, False, [d["Vbd"], d["ArkT"]], [yb])
                P.mm(yb[:, ycols], d["nU0"][:, cs], d["ArbT"][:, cl_], False, True, [d["nU0"], d["ArbT"]], [yb])
                P.mm(tb_[:, 0:128], d["PT"][:, cs], T[:], True, False, [d["PT"], T], [tb_])
                P.mm(tb_[:, 0:128], d["KGbd"][:, cs], d["Vbd"][:, cs], False, False, [d["KGbd"], d["Vbd"]], [tb_])
                P.mm(tb_[:, 0:128], d["BGbd"][:, cs], d["nU0"][:, cs], False, True, [d["BGbd"], d["nU0"]], [tb_])
                P.copy("act", T[:], tb_[:, 0:128], [tb_], [T])

            def fin(pb, inst, l=l, dbg_on=dbg_on):
                d = pp[inst]
                yb = banks[6]
                y32, yc, ysq, rs = spf[0], spf[1], spf[2], lf[3]
                P.copy("act", y32[:], yb[:, inst * TT:(inst + 1) * TT], [yb], [y32])
                if dbg_on:
                    dbg_dump("y_rw%d" % pb, y32, y32[:], [128, TT])
                bm = bank("rw")
                P.mm(bm[:, 0:TT], onesbd_f, y32[:], True, True, [cst, y32], [bm])
                P.stt(yc[:], bm[:, 0:TT], -1.0 / 64, y32[:], ALU.mult, ALU.add, [bm, y32], [yc])
                P.act(ysq[:], yc[:], AF.Square, [yc], [ysq])
                bv = bank("rw")
                P.mm(bv[:, 0:TT], onesbd_f, ysq[:], True, True, [cst, ysq], [bv])
                P.act(rs[:], bv[:, 0:TT], AF.Ln, [bv, eps_t[GN_EPS]], [rs], bias=eps_t[GN_EPS][:, 0:1], scale=1.0 / 64)
                P.act(rs[:], rs[:], AF.Exp, [rs], [rs], scale=-0.5)
                P.tt("dve", yc[:], yc[:], rs[:], ALU.mult, [yc, rs], [yc])
                P.ts("dve", yc[:], yc[:], V(l, "lnw", pb), V(l, "lnb", pb), ALU.mult, ALU.add, [yc, vec], [yc])
                P.tt("pool", yc[:], yc[:], d["bonus"][:], ALU.add, [yc, d["bonus"]], [yc])
                P.tt("dve", ycat[:, pb, :], yc[:], sgate[:, pb, :], ALU.mult, [yc, sgv[pb]], [ycv[pb]])

            def rwkv_gen():
                def chain(pb, inst):
                    yield from prep(pb, inst)
                    yield from solve(pb, inst)

                def tail(pbs):
                    for c in range(NCH):
                        for inst, pb in enumerate(pbs):
                            seq(pb, inst, c)
                            yield
                    for inst, pb in enumerate(pbs):
                        fin(pb, inst)
                        yield

                def merge(ga, gb):
                    da = db = False
                    while not (da and db):
                        if not da:
                            try:
                                next(ga)
                                yield
                            except StopIteration:
                                da = True
                        if not db:
                            try:
                                next(gb)
                                yield
                            except StopIteration:
                                db = True

                def half_front(pbs):
                    gA, gB = chain(pbs[0], 0), chain(pbs[1], 1)
                    for v in gA:
                        yield
                        if v == "tok_done":
                            break
                    yield from merge(gA, gB)

                yield from half_front((0, 1))
                t0_ = tail((0, 1))
                gA, gB = chain(2, 0), chain(3, 1)

                def front2a():
                    for v in gA:
                        yield
                        if v == "pre_pp":
                            break
                yield from merge(t0_, front2a())
                for v in gA:
                    yield
                    if v == "tok_done":
                        break
                yield from merge(gA, gB)
                yield from tail((2, 3))

            def s5_gen(l=l, dbg_on=dbg_on):
                Bv = s5B[l][:, :].rearrange("p (r b q m) -> p r b q m", r=2, b=4, q=2)
                Cv = s5C[l][:, :].rearrange("p (r j m) -> p r j m", r=2, j=16)
                rotk, keep = s5rot[l]
                NS = TT // TS5
                yb5 = banks[7]

                def stage0(blk, s, jj, k):
                    hf, jh = jj // 2, jj % 2
                    hs = slice(64 * hf, 64 * hf + 64)
                    tsl = slice(s * TS5, (s + 1) * TS5)
                    bu = banks[0]
                    c0 = (k % 2) * 2 * TS5
                    bre = bu[:, c0:c0 + TS5]
                    bim = bu[:, c0 + TS5:c0 + 2 * TS5]
                    P.mm(bre, Bv[hs, 0, blk, jh, :], ubf[hs, blk, tsl], True, True, [s5B[l], ubf], [bu])
                    P.mm(bim, Bv[hs, 1, blk, jh, :], ubf[hs, blk, tsl], True, True, [s5B[l], ubf], [bu])

                def stage1(blk, s, jj, k):
                    j = blk * 4 + jj
                    (t1, t2, bzr, bzi, zr, zi, t3, t4) = s5f[k]
                    bu = banks[0]
                    c0 = (k % 2) * 2 * TS5
                    bre = bu[:, c0:c0 + TS5]
                    bim = bu[:, c0 + TS5:c0 + 2 * TS5]
                    cosT = s5tab[:, 0, j, :]
                    sinT = s5tab[:, 1, j, :]
                    P.tt("dve", t1[:], bre, cosT, ALU.mult, [bu, s5tab], [t1])
                    P.tt("dve", t2[:], bim, sinT, ALU.mult, [bu, s5tab], [t2])
                    P.tt("dve", t3[:], bim, cosT, ALU.mult, [bu, s5tab], [t3])
                    P.tt("dve", t4[:], bre, sinT, ALU.mult, [bu, s5tab], [t4])
                    P.tt(S5E, bzr[:], t1[:], t2[:], ALU.add, [t1, t2], [bzr])
                    P.tt(S5E, bzi[:], t3[:], t4[:], ALU.subtract, [t3, t4], [bzi])

                def stage2(blk, s, jj, k):
                    j = blk * 4 + jj
                    hf, jh = jj // 2, jj % 2
                    hs = slice(64 * hf, 64 * hf + 64)
                    tsl = slice(s * TS5, (s + 1) * TS5)
                    (t1, t2, bzr, bzi, zr, zi, t3, t4) = s5f[k]
                    sx = s5x[k]
                    zl = s5zl[k]
                    zv = s5zv[l][j]
                    cosT = s5tab[:, 0, j, :]
                    sinT = s5tab[:, 1, j, :]
                    rho_b = keep[:, 0, j:j + 1].to_broadcast([128, TS5])
                    P.scan(zr[:], rho_b, bzr[:], s5z[l][:, 0, j:j + 1], [keep, bzr, zv], [zr])
                    P.scan(zi[:], rho_b, bzi[:], s5z[l][:, 1, j:j + 1], [keep, bzi, zv], [zi])
                    P.tt("dve", t1[:], zr[:], cosT, ALU.mult, [zr, s5tab], [t1])
                    P.tt("dve", t2[:], zi[:], sinT, ALU.mult, [zi, s5tab], [t2])
                    P.tt(S5E, sx[:, 0, :], t1[:], t2[:], ALU.subtract, [t1, t2], [sx])
                    P.tt("dve", t3[:], zr[:], sinT, ALU.mult, [zr, s5tab], [t3])
                    P.tt(S5E, t4[:], zi[:], cosT, ALU.mult, [zi, s5tab], [t4])
                    P.tt(S5E, sx[:, 1, :], t3[:], t4[:], ALU.add, [t3, t4], [sx])
                    rc = rotk[:, 0, j:j + 1]
                    rs_ = rotk[:, 1, j:j + 1]
                    zlr = zr[:, TS5 - 1:TS5]
                    zli = zi[:, TS5 - 1:TS5]
                    P.ts("dve", zl[:, 0:1], zli, rs_, None, ALU.mult, None, [zi, rotk], [zl])
                    P.ts("dve", zl[:, 1:2], zlr, rs_, None, ALU.mult, None, [zr, rotk], [zl])
                    P.stt(s5z[l][:, 0, j:j + 1], zlr, rc, zl[:, 0:1], ALU.mult, ALU.subtract, [zr, rotk, zl], [zv])
                    P.stt(s5z[l][:, 1, j:j + 1], zli, rc, zl[:, 1:2], ALU.mult, ALU.add, [zi, rotk, zl], [zv])

                def stage3(blk, s, jj, k):
                    j = blk * 4 + jj
                    hf, jh = jj // 2, jj % 2
                    hs = slice(64 * hf, 64 * hf + 64)
                    tsl = slice(s * TS5, (s + 1) * TS5)
                    sx = s5x[k]
                    P.mm(yb5[hs, tsl], Cv[:, 0, j, :], sx[:, 0, :], jh == 0, False, [s5C[l], sx], [yb5])
                    P.mm(yb5[hs, tsl], Cv[:, 1, j, :], sx[:, 1, :], False, jh == 1, [s5C[l], sx], [yb5])

                for blk in range(4):
                    units = [(s, jj) for s in range(NS) for jj in range(4)]
                    ks = []
                    for (s, jj) in units:
                        ks.append(s5ctr[0] % NS5SET)
                        s5ctr[0] += 1
                    nU = len(units)
                    stage0(blk, units[0][0], units[0][1], ks[0])
                    stage0(blk, units[1][0], units[1][1], ks[1])
                    stage1(blk, units[0][0], units[0][1], ks[0])
                    yield
                    for u in range(nU):
                        if u + 1 < nU:
                            stage1(blk, units[u + 1][0], units[u + 1][1], ks[u + 1])
                            if u + 2 < nU:
                                stage0(blk, units[u + 2][0], units[u + 2][1], ks[u + 2])
                            yield
                        stage2(blk, units[u][0], units[u][1], ks[u])
                        if u >= 1:
                            stage3(blk, units[u - 1][0], units[u - 1][1], ks[u - 1])
                        yield
                    stage3(blk, units[nU - 1][0], units[nU - 1][1], ks[nU - 1])
                    ys, x2, q_ = spf
                    P.stt(ys[:], zu[:, blk, :], V(l, "s5d", blk), yb5[:, 0:TT], ALU.mult, ALU.add, [zu, vec, yb5], [ys])
                    if dbg_on:
                        dbg_dump("s5y%d" % blk, ys, ys[:], [128, TT])
                    P.act(x2[:], ys[:], AF.Square, [ys], [x2])
                    P.ts("dve", x2[:], x2[:], 0.044715, 1.0, ALU.mult, ALU.add, [x2], [x2])
                    P.tt("pool", q_[:], x2[:], ys[:], ALU.mult, [x2, ys], [q_])
                    P.act(x2[:], q_[:], AF.Sigmoid, [q_], [x2], scale=2.0 * math.sqrt(2.0 / math.pi))
                    P.tt("dve", mix[:, blk, :], ys[:], x2[:], ALU.mult, [ys, x2], [mixv[blk]])
                    yield
                zgb = ubf
                P.copy("pool", zgb[:], mix[:], mixv, [zgb])
                rg = next_ring()
                P.dma(rg[:, 0:2048].rearrange("p (k n) -> p k n", k=4), glu_w_b[l].rearrange("(k p) n -> p k n", p=128),
                      reads=[wsb], writes=[rg])
                rgv = rg[:, 0:2048].rearrange("p (k n) -> p k n", k=4)
                for ob in range(4):
                    b = banks[0]
                    for k in range(4):
                        P.mm(b[:, 0:TT], rgv[:, k, ob * 128:(ob + 1) * 128], zgb[:, k, :], k == 0, k == 3, [rg, zgb], [b])
                    sg_ = spf[ob % 2]
                    P.act(sg_[:], b[:, 0:TT], AF.Sigmoid, [b, vec], [sg_], bias=V(l, "glub", ob))
                    P.tt("pool", sg_[:], sg_[:], sgate[:, 4 + ob, :], ALU.mult, [sg_, sgv[4 + ob]], [sg_])
                    P.tt("dve", ycat[:, 4 + ob, :], mix[:, ob, :], sg_[:], ALU.mult, [mixv[ob], sg_], [ycv[4 + ob]])
                    yield

            def lru_gen(l=l, dbg_on=dbg_on):
                for blk in range(4):
                    A, B, C, Dd = lf
                    bl = banks[1]
                    P.ts("dve", A[:], zx[:, blk, 0:TT], V(l, "cw", 0 * 4 + blk), V(l, "cb", blk), ALU.mult, ALU.add,
                         [zx, vec], [A])
                    for j in range(1, 4):
                        P.stt(A[:], zx[:, blk, j:j + TT], V(l, "cw", j * 4 + blk), A[:], ALU.mult, ALU.add,
                              [zx, vec, A], [A])
                    P.copy("act", lb[:], A[:], [A], [lb])
                    yield
                    P.mm(bl[:, 0:TT], lruw[l][:, blk * 128:(blk + 1) * 128], lb[:], True, True, [lruw[l], lb], [bl])
                    P.mm(bl[:, TT:2 * TT], lruw[l][:, (4 + blk) * 128:(5 + blk) * 128], lb[:], True, True,
                         [lruw[l], lb], [bl])
                    P.act(B[:], bl[:, 0:TT], AF.Sigmoid, [bl, vec], [B], bias=V(l, "ba", blk))
                    P.act(C[:], bl[:, TT:2 * TT], AF.Sigmoid, [bl, vec], [C], bias=V(l, "bx", blk))
                    yield
                    P.act(Dd[:], B[:], AF.Exp, [B, lru_c], [Dd], scale=lru_c[:, l, blk:blk + 1])
                    P.act(B[:], B[:], AF.Exp, [B, lru_c], [B], scale=lru_c[:, l, 4 + blk:5 + blk])
                    P.act(B[:], B[:], AF.Ln, [B, one_t], [B], bias=one_t[:, 0:1], scale=-1.0)
                    P.act(B[:], B[:], AF.Exp, [B], [B], scale=0.5)
                    P.tt("pool", C[:], C[:], A[:], ALU.mult, [C, A], [C])
                    P.tt("pool", C[:], C[:], B[:], ALU.mult, [C, B], [C])
                    yield
                    P.scan(A[:], Dd[:], C[:], ch[l][:, blk:blk + 1], [Dd, C, ch[l]], [A])
                    P.copy("act", ch[l][:, blk:blk + 1], A[:, TT - 1:TT], [A], [ch[l]])
                    if dbg_on:
                        dbg_dump("lru%d" % blk, A, A[:], [128, TT])
                    P.tt("pool", ycat[:, 8 + blk, :], A[:], sgate[:, 8 + blk, :], ALU.mult, [A, sgv[8 + blk]], [ycv[8 + blk]])
                    yield

            gens = []
            if "rwkv" not in SKIP:
                gens.append((rwkv_gen(), GW[0]))
            if "s5" not in SKIP:
                gens.append((s5_gen(), GW[1]))
            if "lru" not in SKIP:
                gens.append((lru_gen(), GW[2]))
            if "serial" in SKIP:
                for g, w in gens:
                    drive([(g, 1)])
            else:
                drive(gens)
            if dbg_on:
                dbg_dump("ycat_rw", ycat, ycat[:, 0:4, :].rearrange("p a t -> p (a t)"), [128, 4 * TT], BF16)

            for oc in range(4):
                r = next_ring()
                P.dma(r[:, 0:12 * 256].rearrange("p (k n) -> p k n", k=12),
                      w_out_b[l].rearrange("(k p) n -> p k n", p=128)[:, :, oc * 256:(oc + 1) * 256],
                      reads=[wsb], writes=[r])
                rv = r[:, 0:12 * 256].rearrange("p (k n) -> p k n", k=12)
                for ob2 in range(2):
                    ob = oc * 2 + ob2
                    b = bank("proj")
                    for k in range(12):
                        P.mm(b[:, 0:TT], rv[:, k, ob2 * 128:(ob2 + 1) * 128], ycat[:, k, :], k == 0, k == 11,
                             [r] + ycv, [b])
                    P.tt("dve", hT[:, ob, :], hT[:, ob, :], b[:, 0:TT], ALU.add, [hT, b], [hT])
            r = next_ring()
            P.dma(r[:, 0:2048].rearrange("p (k n) -> p k n", k=2), ple_w_b[l].rearrange("(k p) n -> p k n", p=128),
                  reads=[wsb], writes=[r])
            rv = r[:, 0:2048].rearrange("p (k n) -> p k n", k=2)
            epre = zs
            for ob in range(8):
                b = bank("proj")
                for k in range(2):
                    P.mm(b[:, 0:TT], rv[:, k, ob * 128:(ob + 1) * 128], pbf[:, k, :], k == 0, k == 1, [r, pbf], [b])
                P.copy("act", epre[:, ob, :], b[:, 0:TT], [b], [zsv[ob]])
            rms_rstd(zsv[0:8], lambda k: epre[:, k, :], 8, NORM_EPS)
            for k in range(8):
                P.copy("act" if k % 2 == 0 else "dve", xn[:, k, :], hT[:, k, :], [hT], [xn])
            for gc in range(2):
                r = next_ring()
                P.dma(r[:, 0:4096].rearrange("p (k n) -> p k n", k=8),
                      ple_gw_b[l].rearrange("(k p) n -> p k n", p=128)[:, :, gc * 512:(gc + 1) * 512],
                      reads=[wsb], writes=[r])
                rv = r[:, 0:4096].rearrange("p (k n) -> p k n", k=8)
                for ob2 in range(4):
                    ob = gc * 4 + ob2
                    b = bank("proj")
                    for k in range(8):
                        P.mm(b[:, 0:TT], rv[:, k, ob2 * 128:(ob2 + 1) * 128], xn[:, k, :], k == 0, k == 7, [r, xn], [b])
                    sg_, e_ = fs[6 + 2 * (ob % 2)], fs[7 + 2 * (ob % 2)]
                    P.act(sg_[:], b[:, 0:TT], AF.Sigmoid, [b], [sg_])
                    P.stt(e_[:], epre[:, ob, :], V(l, "png", ob), rstd[:], ALU.mult, ALU.mult, [zsv[ob], vec, rstd], [e_])
                    P.tt("dve", e_[:], e_[:], sg_[:], ALU.mult, [e_, sg_], [e_])
                    P.tt("dve", hT[:, ob, :], hT[:, ob, :], e_[:], ALU.add, [hT, e_], [hT])
            if dbg_on:
                dbg_dump("h1", hT, hT[:, :, :].rearrange("p a t -> p (a t)"), [128, 8 * TT])
        rms_rstd([hT], lambda k: hT[:, k, :], 8, NORM_EPS)
        fo = 2 * VEC_PER_LAYER
        for k in range(8):
            P.stt(zs[:, k, :], hT[:, k, :], vec[:, fo + k:fo + k + 1], rstd[:], ALU.mult, ALU.mult, [hT, vec, rstd], [zsv[k]])
        out_toks.append(P.dma(oT[:, t0:t0 + TT].rearrange("(k p) t -> p k t", p=128), zs[:, 0:8, :], reads=zsv[0:8]))
    out_toks.extend(dbg_out.values())
    ninst = P.ninst
    P.finish(out_toks)
    return nc, ninst


def pack_shared(inp):
    f = lambda a: np.asarray(a, np.float32)
    vec = np.zeros((128, NVEC), np.float32)
    for l in range(2):
        def put(name, arr, n):
            c = vcol(l, name)
            vec[:, c:c + n] = _pp(arr, n)
        put("ng", f(inp["norm_g"])[l], 8)
        put("mu", f(inp["rwkv_mu"])[l], 13)
        put("w0", f(inp["rwkv_w0"])[l], 4)
        put("a0", f(inp["rwkv_a0"])[l], 4)
        put("kk", f(inp["rwkv_k_k"])[l], 4)
        put("ka", f(inp["rwkv_k_a"])[l], 4)
        put("rk", f(inp["rwkv_r_k"])[l].reshape(512), 4)
        put("lnw", f(inp["rwkv_ln_w"])[l], 4)
        put("lnb", f(inp["rwkv_ln_b"])[l], 4)
        put("s5d", f(inp["s5_d"])[l], 4)
        put("glub", f(inp["s5_glu_b"])[l], 4)
        cw = f(inp["lru_conv_w"])[l]
        c = vcol(l, "cw")
        for j in range(4):
            vec[:, c + 4 * j:c + 4 * j + 4] = _pp(cw[j], 4)
        put("cb", f(inp["lru_conv_b"])[l], 4)
        put("ba", f(inp["lru_ba"])[l], 4)
        put("bx", f(inp["lru_bx"])[l], 4)
        put("lam", f(inp["lru_lambda"])[l], 4)
        put("png", f(inp["ple_norm_g"])[l], 8)
    vec[:, 2 * VEC_PER_LAYER:2 * VEC_PER_LAYER + 8] = _pp(f(inp["final_norm_g"]), 8)

    w2a2 = np.zeros((2, 128, 512), np.float32)
    w2a2[:, 0:64] = f(inp["rwkv_w2"])
    w2a2[:, 64:128] = f(inp["rwkv_a2"])
    lruw = np.zeros((2, 128, 8, 128), np.float32)
    for l in range(2):
        for m, key in enumerate(("lru_wa", "lru_wx")):
            w = f(inp[key])[l]
            for q in range(4):
                for b2 in range(2):
                    lruw[l, b2 * 64:(b2 + 1) * 64, m * 4 + q, b2 * 64:(b2 + 1) * 64] = w[2 * q + b2]
    lruw = lruw.reshape(2, 128, 1024)
    def modes(a):
        a = f(a).reshape(2, 16, 2, 64)
        return np.ascontiguousarray(a.transpose(0, 2, 3, 1).reshape(2, 128, 16))
    s5s = np.zeros((2, 128, 3, 16), np.float32)
    s5s[:, :, 0] = modes(inp["s5_a_re"])
    s5s[:, :, 1] = modes(inp["s5_a_im"])
    ldt = np.broadcast_to(f(inp["s5_log_dt"])[:, :, None], (2, 32, 64))
    s5s[:, :, 2] = modes(ldt)
    s5s = s5s.reshape(2, 128, 48)
    def bmodes(a):
        a = f(a).reshape(2, 16, 2, 64, 16)
        return a.transpose(0, 2, 3, 1, 4).reshape(2, 128, 16, 16)
    s5b = np.stack([bmodes(inp["s5_b_re"]), bmodes(inp["s5_b_im"])], axis=2).reshape(2, 128, 512)
    s5c = np.zeros((2, 128, 2, 16, 64), np.float32)
    for ri, key in enumerate(("s5_c_re", "s5_c_im")):
        c = f(inp[key]).reshape(2, 16, 2, 16, 64)
        for gh in range(2):
            for jh in range(2):
                c0 = 32 * jh + 16 * gh
                s5c[:, gh * 64:(gh + 1) * 64, ri, jh::2, c0:c0 + 16] = c[:, jh::2, gh].transpose(0, 3, 1, 2)
    s5c = s5c.reshape(2, 128, 2048)
    return {
        "w_in": np.ascontiguousarray(f(inp["w_in"])), "w_out": np.ascontiguousarray(f(inp["w_out"])),
        "ple_w": np.ascontiguousarray(f(inp["ple_w"])), "ple_gw": np.ascontiguousarray(f(inp["ple_gate_w"])),
        "glu_w": np.ascontiguousarray(f(inp["s5_glu_w"])), "vec": vec, "cst": make_consts()[0], "msk": make_consts()[1],
        "w2a2": w2a2, "lruw": np.ascontiguousarray(lruw), "s5s": np.ascontiguousarray(s5s),
        "s5b": np.ascontiguousarray(s5b), "s5c": np.ascontiguousarray(s5c),
    }


_NC_CACHE = {}


def run_cores(inp, TC, batches, dbg=None):
    key = (TC, tuple(sorted(dbg)) if dbg else None)
    if key not in _NC_CACHE:
        _NC_CACHE[key] = build_nc(TC, dbg)
    nc, ninst = _NC_CACHE[key]
    shared = pack_shared(inp)
    x = np.asarray(inp["x"], np.float32)
    p = np.asarray(inp["p"], np.float32)
    in_maps = []
    for b in batches:
        m = dict(shared)
        m["xT"] = np.ascontiguousarray(x[b, :TC].T)
        m["pT"] = np.ascontiguousarray(p[:, b, :TC].transpose(0, 2, 1))
        in_maps.append(m)
    res = run_bass_kernel_spmd(nc, in_maps, core_ids=list(range(len(batches))))
    return res


def kernel(**inputs):
    x = np.asarray(inputs["x"])
    B, S, _ = x.shape
    batches = [c % B for c in range(8)]
    res = run_cores(inputs, S, batches)
    out = np.empty((B, S, D), np.float32)
    for b in range(B):
        out[b] = res.results[b]["oT"].T
    return out.astype(x.dtype)
```

```python
import contextlib
import math
import numpy as np
import concourse.bass as bass
import concourse.mybir as mybir
from concourse.bass_utils import run_bass_kernel_spmd

F32 = mybir.dt.float32
BF16 = mybir.dt.bfloat16
ALU = mybir.AluOpType
AF = mybir.ActivationFunctionType

D = 1024
DIN = 4224
DMIX = 1536
DPLE = 256
TT = 256
LCH = 64
NCH = TT // LCH
TS5 = 128
import os
SKIP = set(os.environ.get("KSKIP", "").split(","))
S5E = os.environ.get("KS5E", "pool")
GW = tuple(int(v) for v in os.environ.get("KGW", "1,1,1").split(","))
GN_EPS = 64e-5
NORM_EPS = 1e-6


class Tok:
    __slots__ = ("sem", "val", "eng", "dma")

    def __init__(self, sem, val, eng, dma):
        self.sem, self.val, self.eng, self.dma = sem, val, eng, dma


class Buf:
    def __init__(self, t, name):
        self.t = t
        self.name = name
        self.w = None
        self.r = []

    def __getitem__(self, idx):
        return self.t[idx]


class Prog:
    ENGS = ("pe", "act", "dve", "pool", "sp")

    def __init__(self, nc, n_dma_sems=32):
        self.nc = nc
        self.es = contextlib.ExitStack()
        self.ops = {e: [] for e in self.ENGS}
        self.cnt = {e: 0 for e in self.ENGS}
        self.sem = {e: self.es.enter_context(nc.semaphore("s_" + e)) for e in self.ENGS}
        self.dsem = [self.es.enter_context(nc.semaphore("d%d" % i)) for i in range(n_dma_sems)]
        self.duse = [0] * n_dma_sems
        self.dnext = 0
        self.seen = {e: {} for e in self.ENGS}
        self.nbuf = 0
        self.ninst = 0
        self.stack = [self.es]

    def push(self):
        st = contextlib.ExitStack()
        self.stack.append(st)

    def pop(self):
        self.barrier()
        self.stack.pop().close()

    def barrier(self):
        toks = []
        for f in self.ENGS:
            if self.cnt[f] > 0:
                toks.append(Tok(self.sem[f], self.cnt[f], f, False))
        for i, s in enumerate(self.dsem):
            if self.duse[i] > 0:
                toks.append(Tok(s, 16 * self.duse[i], "dma", True))
        for e in self.ENGS:
            wl = []
            for t in toks:
                if t.eng == e and not t.dma:
                    continue
                k = id(t.sem)
                if self.seen[e].get(k, 0) >= t.val:
                    continue
                self.seen[e][k] = t.val
                wl.append((t.sem, t.val))

            def run(en, wl=wl):
                for (s, v) in wl:
                    en.wait_ge(s, v)
            self.ops[e].append(run)

    def sb(self, shape, dt=F32, name=None):
        self.nbuf += 1
        name = name or ("b%d" % self.nbuf)
        t = self.stack[-1].enter_context(self.nc.sbuf_tensor("sb_" + name, list(shape), dt))
        return Buf(t, name)

    def ps(self, name, dt=F32, cols=512):
        t = self.es.enter_context(self.nc.psum_tensor(name, [128, cols], dt))
        return Buf(t, name)

    def wrap(self, t, name):
        return Buf(t, name)

    def views(self, buf, n):
        return [Buf(buf.t, "%s.v%d" % (buf.name, i)) for i in range(n)]

    def _need(self, eng, tok, waits, is_dma_issue):
        if tok is None:
            return
        if tok.eng == eng and not tok.dma and not is_dma_issue and eng == "pe":
            return
        k = id(tok.sem)
        if self.seen[eng].get(k, 0) >= tok.val:
            return
        cur = waits.get(k)
        if cur is None or cur[1] < tok.val:
            waits[k] = (tok.sem, tok.val)

    def emit(self, eng, fn, reads=(), writes=(), dma=False):
        waits = {}
        for b in reads:
            self._need(eng, b.w, waits, dma)
        for b in writes:
            self._need(eng, b.w, waits, dma)
            for t in b.r:
                self._need(eng, t, waits, dma)
        if dma:
            i = self.dnext
            self.dnext = (self.dnext + 1) % len(self.dsem)
            s = self.dsem[i]
            if self.duse[i] > 0:
                self._need(eng, Tok(s, 16 * self.duse[i], "dma", True), waits, True)
            self.duse[i] += 1
            tok = Tok(s, 16 * self.duse[i], "dma", True)
            inc = 16
        else:
            self.cnt[eng] += 1
            tok = Tok(self.sem[eng], self.cnt[eng], eng, False)
            inc = 1
        wl = list(waits.values())
        for (s, v) in wl:
            self.seen[eng][id(s)] = v
        tsem = tok.sem
        self.ninst += 1 + len(wl)

        def run(e, wl=wl, fn=fn, tsem=tsem, inc=inc):
            for (s, v) in wl:
                e.wait_ge(s, v)
            fn(e).then_inc(tsem, inc)

        self.ops[eng].append(run)
        for b in reads:
            b.r = [t for t in b.r if t.sem is not tok.sem]
            b.r.append(tok)
        for b in writes:
            b.w = tok
            b.r = []
        return tok

    def finish(self, out_toks):
        wl = [(t.sem, t.val) for t in out_toks]

        def run(e, wl=wl):
            for (s, v) in wl:
                e.wait_ge(s, v)

        self.ops["sp"].append(run)
        nc = self.nc
        ops = self.ops
        with nc.Block() as block:
            @block.tensor
            def _(e):
                for f in ops["pe"]:
                    f(e)

            @block.scalar
            def _(e):
                for f in ops["act"]:
                    f(e)

            @block.vector
            def _(e):
                for f in ops["dve"]:
                    f(e)

            @block.gpsimd
            def _(e):
                for f in ops["pool"]:
                    f(e)

            @block.sync
            def _(e):
                for f in ops["sp"]:
                    f(e)
        self.es.close()

    def dma(self, out, in_, reads=(), writes=(), eng="sp", **kw):
        return self.emit(eng, lambda e: e.dma_start(out=out, in_=in_, **kw), reads, writes, dma=True)

    def mm(self, out, lhsT, rhs, start, stop, reads, writes):
        return self.emit("pe", lambda e: e.matmul(out, lhsT, rhs, start=start, stop=stop), reads, writes)

    def act(self, out, in_, func, reads, writes, bias=None, scale=None):
        kw = {}
        if bias is not None:
            kw["bias"] = bias
        if scale is not None:
            kw["scale"] = scale
        return self.emit("act", lambda e: e.activation(out=out, in_=in_, func=func, **kw), reads, writes)

    def tt(self, eng, out, in0, in1, op, reads, writes):
        return self.emit(eng, lambda e: e.tensor_tensor(out=out, in0=in0, in1=in1, op=op), reads, writes)

    def ts(self, eng, out, in0, s1, s2, op0, op1, reads, writes):
        if op1 is None:
            return self.emit(eng, lambda e: e.tensor_scalar(out, in0, s1, None, op0), reads, writes)
        return self.emit(eng, lambda e: e.tensor_scalar(out, in0, s1, s2, op0, op1), reads, writes)

    def stt(self, out, in0, scalar, in1, op0, op1, reads, writes):
        return self.emit("dve", lambda e: e.scalar_tensor_tensor(out, in0, scalar, in1, op0, op1), reads, writes)

    def copy(self, eng, out, in_, reads, writes):
        if eng == "act":
            return self.emit("act", lambda e: e.activation(out=out, in_=in_, func=AF.Copy), reads, writes)
        return self.emit(eng, lambda e: e.tensor_copy(out, in_), reads, writes)

    def memset(self, eng, ap, val, writes):
        return self.emit(eng, lambda e: e.memset(ap, val), (), writes)

    def scan(self, out, d0, d1, init, reads, writes):
        return self.emit("dve", lambda e: e.tensor_tensor_scan(out, d0, d1, init, ALU.mult, ALU.add), reads, writes)

    def recip(self, out, in_, reads, writes):
        return self.emit("dve", lambda e: e.reciprocal(out, in_), reads, writes)


VEC_FIELDS = [("ng", 8), ("mu", 13), ("w0", 4), ("a0", 4), ("kk", 4), ("ka", 4), ("rk", 4), ("lnw", 4),
              ("lnb", 4), ("s5d", 4), ("glub", 4), ("cw", 16), ("cb", 4), ("ba", 4), ("bx", 4), ("lam", 4),
              ("png", 8)]
VEC_PER_LAYER = sum(n for _, n in VEC_FIELDS)
VEC_OFF = {}
_o = 0
for _n, _c in VEC_FIELDS:
    VEC_OFF[_n] = _o
    _o += _c
NVEC = 2 * VEC_PER_LAYER + 8

CST_IDENT = 0
CST_ONESBD = 128
CST_SCAN = 256
NCST = 256 + TT
MSK_USN = 0
MSK_LSN = NCH * 128
MSK_USP = 2 * NCH * 128
MSK_CI = 3 * NCH * 128
NMSK = 3 * NCH * 128 + NCH * LCH


def vcol(l, name, i=0):
    return l * VEC_PER_LAYER + VEC_OFF[name] + i


def _pp(v, n):
    return np.ascontiguousarray(np.asarray(v, np.float32).reshape(n, 128).T)


def make_consts():
    c = np.zeros((128, NCST), np.float32)
    i = np.arange(128)[:, None]
    j = np.arange(128)[None, :]
    c[:, CST_IDENT:CST_IDENT + 128] = (i == j)
    c[:, CST_ONESBD:CST_ONESBD + 128] = ((i // 64) == (j // 64))
    tt = np.arange(TT)[None, :]
    c[:, CST_SCAN:CST_SCAN + TT] = 1.0 * ((tt % LCH) != 0)
    m = np.zeros((128, NMSK), np.float32)
    t = np.arange(64)[None, :]
    for ch in range(NCH):
        m[:, MSK_USN + ch * 128:MSK_USN + (ch + 1) * 128] = -1.0 * (j > i)
        m[:, MSK_LSN + ch * 128:MSK_LSN + (ch + 1) * 128] = -1.0 * (i > j)
        m[:, MSK_USP + ch * 128:MSK_USP + (ch + 1) * 128] = 1.0 * (j > i)
        m[:, MSK_CI + ch * 64:MSK_CI + (ch + 1) * 64] = 1.0 * (t >= (i % 64))
    return c, m


def build_nc(TC, dbg=None):
    assert TC % TT == 0
    NT = TC // TT
    dbg = dbg or set()
    nc = bass.Bass("TRN2", target_bir_lowering=False)

    def din(name, shape, dt=F32):
        return nc.dram_tensor(name, list(shape), dt, kind="ExternalInput").ap()

    xT = din("xT", [D, TC])
    pT = din("pT", [2, DPLE, TC])
    w_in = din("w_in", [2, D, DIN])
    w_out = din("w_out", [2, DMIX, D])
    ple_w = din("ple_w", [2, DPLE, D])
    ple_gw = din("ple_gw", [2, D, D])
    glu_w = din("glu_w", [2, 512, 512])
    vec_d = din("vec", [128, NVEC])
    cst_d = din("cst", [128, NCST])
    msk_d = din("msk", [128, NMSK])
    w2a2_d = din("w2a2", [2, 128, 512])
    lruw_d = din("lruw", [2, 128, 8 * 128])
    s5s_d = din("s5s", [2, 128, 3 * 16])
    s5b_d = din("s5b", [2, 128, 2 * 16 * 16])
    s5c_d = din("s5c", [2, 128, 2 * 16 * 64])
    oT = nc.dram_tensor("oT", [D, TC], F32, kind="ExternalOutput").ap()
    dbg_out = {}

    def dram_int(name, shape, dt):
        return nc.dram_tensor(name, list(shape), dt, kind="Internal").ap()

    w_in_b = dram_int("w_in_b", [2, D, DIN], BF16)
    w_out_b = dram_int("w_out_b", [2, DMIX, D], BF16)
    ple_w_b = dram_int("ple_w_b", [2, DPLE, D], BF16)
    ple_gw_b = dram_int("ple_gw_b", [2, D, D], BF16)
    glu_w_b = dram_int("glu_w_b", [2, 512, 512], BF16)
    s5tab_d = dram_int("s5tab", [2, 128, 2 * 16 * TS5], F32)

    P = Prog(nc)
    wsb = P.wrap(None, "wscratch")
    tabsb = P.wrap(None, "s5tabscr")

    def dbg_dump(name, buf, ap, shape, dt=F32):
        if name not in dbg:
            return
        o = nc.dram_tensor("dbg_" + name, list(shape), dt, kind="ExternalOutput").ap()
        dbg_out[name] = P.dma(o, ap, reads=[buf])

    vec = P.sb([128, NVEC], F32, "vec")
    cst = P.sb([128, NCST], F32, "cst")
    P.dma(vec[:], vec_d, writes=[vec])
    P.dma(cst[:], cst_d, writes=[cst])
    cstb = P.sb([128, 128], BF16, "cstb")
    P.copy("dve", cstb[:], cst[:, CST_IDENT:CST_IDENT + 128], [cst], [cstb])
    mskb = P.sb([128, NMSK], BF16, "mskb")
    ident_f = cst[:, CST_IDENT:CST_IDENT + 128]
    ident_b = cstb[:, 0:128]
    onesbd_f = cst[:, CST_ONESBD:CST_ONESBD + 128]
    ones_f = P.sb([128, 128], F32, "ones_f")
    P.memset("pool", ones_f[:], 1.0, [ones_f])
    one_t = P.sb([128, 1], F32, "one_t")
    P.memset("pool", one_t[:], 1.0, [one_t])

    def V(l, name, i=0, n=1):
        c = vcol(l, name, i)
        return vec[:, c:c + n]

    for l in range(2):
        for (src, dst, rows) in ((w_in, w_in_b, D), (w_out, w_out_b, DMIX), (ple_w, ple_w_b, DPLE),
                                 (ple_gw, ple_gw_b, D), (glu_w, glu_w_b, 512)):
            for r0 in range(0, rows, 128):
                P.dma(dst[l, r0:r0 + 128, :], src[l, r0:r0 + 128, :], writes=[wsb], eng="pool",
                      max_dma_last_dim=4096)

    w2a2 = []
    lruw = []
    for l in range(2):
        w2a2.append(P.sb([128, 512], BF16, "w2a2_%d" % l))
        lruw.append(P.sb([128, 1024], BF16, "lruw_%d" % l))
    lru_c = P.sb([128, 2, 8], F32, "lru_c")
    s5B = [P.sb([128, 2 * 4 * 2 * 128], BF16, "s5B%d" % l) for l in range(2)]
    s5C = [P.sb([128, 2048], BF16, "s5C%d" % l) for l in range(2)]
    s5keep = [P.sb([128, 3, 16], F32, "s5keep%d" % l) for l in range(2)]
    s5rotb = [P.sb([128, 2, 16], F32, "s5rot%d" % l) for l in range(2)]
    ps_misc = P.ps("ps7")
    P.push()
    stage = P.sb([128, 1024], F32, "stage")
    mstage = P.sb([128, NMSK], F32, "mstage")
    P.dma(mstage[:], msk_d, writes=[mstage])
    P.copy("act", mskb[:], mstage[:], [mstage], [mskb])
    for l in range(2):
        P.dma(stage[:, 0:512], w2a2_d[l], writes=[stage])
        P.copy("act", w2a2[l][:], stage[:, 0:512], [stage], [w2a2[l]])
        P.dma(stage[:], lruw_d[l], writes=[stage])
        P.copy("act", lruw[l][:], stage[:], [stage], [lruw[l]])

    for l in range(2):
        tmp = P.sb([128, 4], F32, "lrutmp%d" % l)
        P.act(tmp[:], V(l, "lam", 0, 4), AF.Exp, [vec], [tmp], scale=-1.0)
        P.act(tmp[:], tmp[:], AF.Ln, [tmp, one_t], [tmp], bias=one_t[:, 0:1])
        P.ts("dve", lru_c[:, l, 0:4], tmp[:], -8.0, None, ALU.mult, None, [tmp], [lru_c])
        P.ts("dve", lru_c[:, l, 4:8], tmp[:], -16.0, None, ALU.mult, None, [tmp], [lru_c])

    s5rot = []
    for l in range(2):
        s5s = P.sb([128, 48], F32, "s5s%d" % l)
        P.dma(s5s[:], s5s_d[l], writes=[s5s])
        a_re = s5s[:, 0:16]
        a_im = s5s[:, 16:32]
        ldt = s5s[:, 32:48]
        w = P.sb([128, 16, 16], F32, "s5w%d" % l)
        R = [w]

        def row(i):
            return w[:, i, :]
        dt_, rho, th, cc, ss, t1, t2, lr, li, den, qre, qim, nr = [row(i) for i in range(13)]
        P.act(dt_, ldt, AF.Exp, [s5s], R)
        P.tt("dve", rho, a_re, dt_, ALU.mult, [s5s] + R, R)
        P.act(rho, rho, AF.Exp, R, R)
        P.tt("dve", th, a_im, dt_, ALU.mult, [s5s] + R, R)
        hp = P.sb([128, 1], F32, "halfpi%d" % l)
        P.memset("dve", hp[:], math.pi / 2, [hp])
        P.act(cc, th, AF.Sin, R + [hp], R, bias=hp[:, 0:1], scale=1.0 / 16)
        P.act(ss, th, AF.Sin, R, R, scale=1.0 / 16)

        def csq(c_, s_):
            P.tt("dve", t1, c_, c_, ALU.mult, R, R)
            P.tt("dve", t2, s_, s_, ALU.mult, R, R)
            P.stt(s_, c_, 2.0, s_, ALU.mult, ALU.mult, R, R)
            P.tt("dve", c_, t1, t2, ALU.subtract, R, R)
        for _ in range(4):
            csq(cc, ss)
        P.tt("dve", lr, rho, cc, ALU.mult, R, R)
        P.tt("dve", li, rho, ss, ALU.mult, R, R)
        P.tt("dve", t1, a_re, a_re, ALU.mult, [s5s] + R, R)
        P.tt("dve", t2, a_im, a_im, ALU.mult, [s5s] + R, R)
        P.tt("dve", den, t1, t2, ALU.add, R, R)
        P.recip(den, den, R, R)
        P.ts("dve", nr, lr, -1.0, None, ALU.add, None, R, R)
        P.tt("dve", t1, nr, a_re, ALU.mult, [s5s] + R, R)
        P.tt("dve", t2, li, a_im, ALU.mult, [s5s] + R, R)
        P.tt("dve", t1, t1, t2, ALU.add, R, R)
        P.tt("dve", qre, t1, den, ALU.mult, R, R)
        P.tt("dve", t1, li, a_re, ALU.mult, [s5s] + R, R)
        P.tt("dve", t2, nr, a_im, ALU.mult, [s5s] + R, R)
        P.tt("dve", t1, t1, t2, ALU.subtract, R, R)
        P.tt("dve", qim, t1, den, ALU.mult, R, R)
        keep = s5keep[l]
        P.copy("dve", keep[:, 0, :], rho, R, [keep])
        P.copy("dve", keep[:, 1, :], cc, R, [keep])
        P.copy("dve", keep[:, 2, :], ss, R, [keep])

        sbf = P.sb([128, 512], F32, "s5b_in%d" % l)
        P.dma(sbf[:], s5b_d[l], writes=[sbf])
        bre = sbf[:, 0:256].rearrange("p (j h) -> p j h", h=16)
        bim = sbf[:, 256:512].rearrange("p (j h) -> p j h", h=16)
        Bt = s5B[l]
        Btv = Bt[:, :].rearrange("p (r b q m) -> p r b q m", r=2, b=4, q=2)
        bpad = P.sb([128, 2, 128], BF16, "s5bpad%d" % l)
        tb = P.sb([128, 2, 16], F32, "s5tb%d" % l)
        for j in range(16):
            P.ts("dve", tb[:, 0, :], bim[:, j, :], qim[:, j:j + 1], None, ALU.mult, None, [sbf] + R, [tb])
            P.stt(tb[:, 0, :], bre[:, j, :], qre[:, j:j + 1], tb[:, 0, :], ALU.mult, ALU.subtract, [sbf, tb] + R, [tb])
            P.ts("dve", tb[:, 1, :], bre[:, j, :], qim[:, j:j + 1], None, ALU.mult, None, [sbf] + R, [tb])
            P.stt(tb[:, 1, :], bim[:, j, :], qre[:, j:j + 1], tb[:, 1, :], ALU.mult, ALU.add, [sbf, tb] + R, [tb])
            P.memset("pool", bpad[:], 0.0, [bpad])
            for gh in range(2):
                col0 = 32 * (j % 4) + gh * 16
                for ri in range(2):
                    P.copy("pool", bpad[gh * 64:(gh + 1) * 64, ri, col0:col0 + 16],
                           tb[gh * 64:(gh + 1) * 64, ri, :], [tb], [bpad])
            for ri in range(2):
                P.mm(ps_misc[:, ri * 128:(ri + 1) * 128], bpad[:, ri, :], ident_b, True, True, [bpad, cstb], [ps_misc])
            hf = (j % 4) // 2
            for ri in range(2):
                P.copy("act", Btv[64 * hf:64 * hf + 64, ri, j // 4, j % 2, :],
                       ps_misc[64 * hf:64 * hf + 64, ri * 128:(ri + 1) * 128], [ps_misc], [Bt])

        scf = P.sb([128, 2048], F32, "s5c_in%d" % l)
        P.dma(scf[:], s5c_d[l], writes=[scf])
        Ct = s5C[l]
        P.copy("act", Ct[:, 0:1024], scf[:, 0:1024], [scf], [Ct])
        P.ts("dve", Ct[:, 1024:2048], scf[:, 1024:2048], -1.0, None, ALU.mult, None, [scf], [Ct])

        tab = P.sb([128, 2, 16, TS5], F32, "s5tabb%d" % l)
        P.memset("pool", tab[:, 0, :, 0:1], 1.0, [tab])
        P.memset("pool", tab[:, 1, :, 0:1], 0.0, [tab])
        ec = P.sb([128, 2, 16], F32, "s5ec%d" % l)
        P.copy("dve", ec[:, 0, :], cc, R, [ec])
        P.copy("dve", ec[:, 1, :], ss, R, [ec])
        m = 1
        while m < TS5:
            for j in range(16):
                cj = ec[:, 0, j:j + 1]
                sj = ec[:, 1, j:j + 1]
                src_c = tab[:, 0, j, 0:m]
                src_s = tab[:, 1, j, 0:m]
                dst_c = tab[:, 0, j, m:2 * m]
                dst_s = tab[:, 1, j, m:2 * m]
                P.ts("dve", dst_c, src_s, sj, None, ALU.mult, None, [tab, ec], [tab])
                P.stt(dst_c, src_c, cj, dst_c, ALU.mult, ALU.subtract, [tab, ec], [tab])
                P.ts("dve", dst_s, src_c, sj, None, ALU.mult, None, [tab, ec], [tab])
                P.stt(dst_s, src_s, cj, dst_s, ALU.mult, ALU.add, [tab, ec], [tab])
            e_c = ec[:, 0, :]
            e_s = ec[:, 1, :]
            P.tt("dve", t1, e_c, e_c, ALU.mult, [ec] + R, R)
            P.tt("dve", t2, e_s, e_s, ALU.mult, [ec] + R, R)
            P.stt(e_s, e_c, 2.0, e_s, ALU.mult, ALU.mult, [ec], [ec])
            P.tt("dve", e_c, t1, t2, ALU.subtract, R, [ec])
            m *= 2
        rot = s5rotb[l]
        P.copy("dve", rot[:], ec[:], [ec], [rot])
        s5rot.append((rot, keep))
        P.dma(s5tab_d[l], tab[:, :, :, :].rearrange("p a j t -> p (a j t)"), reads=[tab], writes=[tabsb])
        dbg_dump("s5w%d" % l, w, w[:, :, :].rearrange("p a b -> p (a b)"), [128, 256])
        dbg_dump("s5keep%d" % l, keep, keep[:, :, :].rearrange("p a b -> p (a b)"), [128, 48])
        dbg_dump("s5tab%d" % l, tab, tab[:, :, :, :].rearrange("p a j t -> p (a j t)"), [128, 2 * 16 * TS5])
        dbg_dump("s5B%d" % l, Bt, Bt[:, :], [128, 2048], BF16)
    P.pop()

    hT = P.sb([128, 8, TT], F32, "hT")
    xn = P.sb([128, 8, TT], BF16, "xn")
    zst = [P.sb([128, 1 + TT], F32, "zst%d" % i) for i in range(2)]
    zs = P.sb([128, 13, TT], F32, "zs")
    zsv = P.views(zs, 13)
    zu = P.sb([128, 4, TT], F32, "zu")
    zx = P.sb([128, 4, 3 + TT], F32, "zx")
    sgate = P.sb([128, 12, TT], BF16, "sgate")
    sgv = P.views(sgate, 12)
    ycat = P.sb([128, 12, TT], BF16, "ycat")
    ycv = P.views(ycat, 12)
    pbf = P.sb([128, 2, TT], BF16, "pbf")
    ring = [P.sb([128, 4096], BF16, "ring%d" % i) for i in range(3)]
    ringi = [0]
    s5tab = P.sb([128, 2, 16, TS5], F32, "s5tab")
    banks = [P.ps("ps%d" % i) for i in range(7)] + [ps_misc]
    rot_i = {"proj": 0, "rw": 0}

    def bank(group):
        ids = (0, 1) if group == "proj" else (2, 3)
        i = rot_i[group]
        rot_i[group] = (i + 1) % len(ids)
        return banks[ids[i]]

    def next_ring():
        r = ring[ringi[0]]
        ringi[0] = (ringi[0] + 1) % 3
        return r

    cz = [P.sb([128, 13], F32, "cz%d" % l) for l in range(2)]
    cl = [P.sb([128, 4, 3], F32, "cl%d" % l) for l in range(2)]
    ch = [P.sb([128, 4], F32, "ch%d" % l) for l in range(2)]
    s5z = [P.sb([128, 2, 16], F32, "s5z%d" % l) for l in range(2)]
    s5zv = [P.views(s5z[l], 16) for l in range(2)]
    Tst = [[P.sb([128, 128], BF16, "T%d_%d" % (l, pb)) for pb in range(4)] for l in range(2)]
    for l in range(2):
        P.memset("pool", cz[l][:], 0.0, [cz[l]])
        P.memset("pool", cl[l][:], 0.0, [cl[l]])
        P.memset("pool", ch[l][:], 0.0, [ch[l]])
        P.memset("pool", s5z[l][:], 0.0, s5zv[l])
        for pb in range(4):
            P.memset("pool", Tst[l][pb][:], 0.0, [Tst[l][pb]])

    NF = 17
    fs = [P.sb([128, TT], F32, "rf%d" % i) for i in range(NF)]
    pad_names = ["RTp", "KTp", "CTp", "BTp", "VTp", "KGp", "BGp"]
    pads = {n: P.sb([128, NCH * 128], BF16, n) for n in pad_names}
    for n in pad_names:
        P.memset("pool", pads[n][:], 0.0, [pads[n]])
    RTc = P.sb([128, TT], BF16, "RTc")
    tanh_wd = P.sb([128, TT], BF16, "tanhwd")
    Blev_i = [[P.sb([128, NCH * 128], BF16, "Blev%d_%d" % (i, k)) for i in range(2)] for k in range(2)]
    BTlev_i = [[P.sb([128, NCH * 128], BF16, "BTlev%d_%d" % (i, k)) for i in range(2)] for k in range(2)]
    AkkT_i = [P.sb([128, NCH * 128], BF16, "AkkT_%d" % k) for k in range(2)]
    Xbf_i = [P.sb([128, NCH * 256], BF16, "Xbf_%d" % k) for k in range(2)]
    NPI = 2
    pp = []
    for i in range(NPI):
        d = {}
        for n in ("Vbd", "KGbd", "BGbd", "PT", "nU0", "Wbd"):
            d[n] = P.sb([128, NCH * 128], BF16, "%s_%d" % (n, i))
        for n in ("Rhat", "ArkT", "ArbT"):
            d[n] = P.sb([128, TT], BF16, "%s_%d" % (n, i))
        d["bonus"] = P.sb([128, TT], F32, "bonus_%d" % i)
        d["GL"] = P.sb([128, NCH], F32, "GL_%d" % i)
        d["rt32"] = P.sb([128, TT], F32, "rt32_%d" % i)
        pp.append(d)
    mix = P.sb([128, 4, TT], F32, "mix")
    mixv = P.views(mix, 4)

    NS5SET = 2
    s5f = [[P.sb([128, TS5], F32, "s5f%d_%d" % (k, i)) for i in range(8)] for k in range(NS5SET)]
    s5x = [P.sb([128, 2, TS5], BF16, "s5x%d" % k) for k in range(NS5SET)]
    s5zl = [P.sb([128, 2], F32, "s5zl%d" % k) for k in range(NS5SET)]
    spf = [P.sb([128, TT], F32, "spf%d" % i) for i in range(3)]
    lf = [P.sb([128, TT], F32, "lf%d" % i) for i in range(4)]
    ubf = P.sb([128, 4, TT], BF16, "ubf")
    lb = P.sb([128, TT], BF16, "lb")
    rstd = P.sb([128, TT], F32, "rstd")
    sq = P.sb([128, 4, TT], BF16, "sq")
    ones_b = P.sb([128, 128], BF16, "ones_b")
    P.memset("pool", ones_b[:], 1.0, [ones_b])

    out_toks = []
    eps_t = {}
    for e_ in (NORM_EPS, GN_EPS):
        t = P.sb([128, 1], F32, "eps%d" % len(eps_t))
        P.memset("pool", t[:], e_, [t])
        eps_t[e_] = t
    neg_half = -math.exp(-0.5)

    def rms_rstd(src_bufs, src_ap_fn, nblk, eps):
        b = bank("proj")
        for k in range(nblk):
            P.act(sq[:, k % 4, :], src_ap_fn(k), AF.Square, src_bufs, [sq])
            P.mm(b[:, 0:TT], ones_b[:], sq[:, k % 4, :], k == 0, k == nblk - 1, [ones_b, sq], [b])
        P.act(rstd[:], b[:, 0:TT], AF.Ln, [b, eps_t[eps]], [rstd], bias=eps_t[eps][:, 0:1], scale=1.0 / (nblk * 128))
        P.act(rstd[:], rstd[:], AF.Exp, [rstd], [rstd], scale=-0.5)

    def drive(items):
        active = list(items)
        while active:
            for item in list(active):
                g, w = item
                for _ in range(w):
                    try:
                        next(g)
                    except StopIteration:
                        active.remove(item)
                        break

    s5ctr = [0]

    for it in range(NT):
        t0 = it * TT
        first = (it == 0)
        P.dma(hT[:], xT[:, t0:t0 + TT].rearrange("(k p) t -> p k t", p=128), writes=[hT])
        for l in range(2):
            dbg_on = first and l == 0
            P.dma(pbf[:], pT[l, :, t0:t0 + TT].rearrange("(k p) t -> p k t", p=128), writes=[pbf], eng="pool")
            P.dma(s5tab[:, :, :, :].rearrange("p a j t -> p (a j t)"), s5tab_d[l], reads=[tabsb], writes=[s5tab])
            rms_rstd([hT], lambda k: hT[:, k, :], 8, NORM_EPS)
            for k in range(8):
                P.stt(xn[:, k, :], hT[:, k, :], V(l, "ng", k), rstd[:], ALU.mult, ALU.mult, [hT, vec, rstd], [xn])
            if dbg_on:
                dbg_dump("xn", xn, xn[:, :, :].rearrange("p a t -> p (a t)"), [128, 8 * TT], BF16)

            wchunk = {}

            def in_block(cb, l=l, wchunk=wchunk):
                ci = cb // 4
                if ci not in wchunk:
                    r = next_ring()
                    ncol = 512 if ci < 8 else 128
                    P.dma(r[:, 0:8 * ncol].rearrange("p (k n) -> p k n", k=8),
                          w_in_b[l].rearrange("(k p) n -> p k n", p=128)[:, :, ci * 512:ci * 512 + ncol],
                          reads=[wsb], writes=[r])
                    wchunk[ci] = (r, ncol)
                r, ncol = wchunk[ci]
                rv = r[:, 0:8 * ncol].rearrange("p (k n) -> p k n", k=8)
                c0 = (cb % 4) * 128
                b = bank("proj")
                for k in range(8):
                    P.mm(b[:, 0:TT], rv[:, k, c0:c0 + 128], xn[:, k, :], k == 0, k == 7, [r, xn], [b])
                return b

            for cb in range(13):
                st = zst[cb % 2]
                b = in_block(cb)
                P.copy("act", st[:, 0:1], cz[l][:, cb:cb + 1], [cz[l]], [st])
                P.copy("act", st[:, 1:1 + TT], b[:, 0:TT], [b], [st])
                P.copy("act", cz[l][:, cb:cb + 1], st[:, TT:TT + 1], [st], [cz[l]])
                d_ = lf[cb % 2]
                P.tt("pool", d_[:], st[:, 0:TT], st[:, 1:1 + TT], ALU.subtract, [st], [d_])
                P.stt(zs[:, cb, :], d_[:], V(l, "mu", cb), st[:, 1:1 + TT], ALU.mult, ALU.add, [d_, vec, st], [zsv[cb]])
            if dbg_on:
                dbg_dump("zs", zs, zs[:, :, :].rearrange("p a t -> p (a t)"), [128, 13 * TT])
            P.act(tanh_wd[0:64, :], zs[0:64, 12, :], AF.Tanh, [zsv[12]], [tanh_wd])
            P.copy("act", tanh_wd[64:128, :], zs[64:128, 12, :], [zsv[12]], [tanh_wd])
            for blk in range(4):
                b = in_block(13 + blk)
                P.act(sgate[:, blk, :], b[:, 0:TT], AF.Silu, [b], [sgv[blk]])
            for blk in range(4):
                b = in_block(17 + blk)
                P.copy("act", zu[:, blk, :], b[:, 0:TT], [b], [zu])
            P.copy("pool", ubf[:], zu[:], [zu], [ubf])
            for blk in range(4):
                b = in_block(21 + blk)
                P.act(sgate[:, 4 + blk, :], b[:, 0:TT], AF.Silu, [b], [sgv[4 + blk]])
            P.copy("act", zx[:, :, 0:3], cl[l][:, :, :], [cl[l]], [zx])
            for blk in range(4):
                b = in_block(25 + blk)
                P.copy("act", zx[:, blk, 3:3 + TT], b[:, 0:TT], [b], [zx])
            P.copy("act", cl[l][:, :, :], zx[:, :, TT:TT + 3], [zx], [cl[l]])
            for blk in range(4):
                b = in_block(29 + blk)
                P.act(sgate[:, 8 + blk, :], b[:, 0:TT], AF.Silu, [b], [sgv[8 + blk]])

            def prep(pb, inst, l=l, dbg_on=dbg_on):
                d = pp[inst]
                Blev, BTlev, AkkT = Blev_i[inst], BTlev_i[inst], AkkT_i[inst]
                r_ = zs[:, pb, :]
                k_ = zs[:, 4 + pb, :]
                v_ = zs[:, 8 + pb, :]
                zr_, zk_, zv_ = zsv[pb], zsv[4 + pb], zsv[8 + pb]
                (sg, ld, a_, kk_, kk2, sqk, kap, t1, kp, b_, lg, eg, ieg, eg1, dl, egl, rk) = fs[:17]
                rt32 = d["rt32"]
                cols = slice(pb * 128, (pb + 1) * 128)
                bw = bank("rw")
                P.mm(bw[:, 0:TT], w2a2[l][0:64, cols], tanh_wd[0:64, :], True, True, [w2a2[l], tanh_wd], [bw])
                P.act(sg[:], bw[:, 0:TT], AF.Sigmoid, [bw, vec], [sg], bias=V(l, "w0", pb))
                P.ts("dve", ld[:], sg[:], neg_half, None, ALU.mult, None, [sg], [ld])
                ba_ = bank("rw")
                P.mm(ba_[:, 0:TT], w2a2[l][64:128, cols], tanh_wd[64:128, :], True, True, [w2a2[l], tanh_wd], [ba_])
                P.act(a_[:], ba_[:, 0:TT], AF.Sigmoid, [ba_, vec], [a_], bias=V(l, "a0", pb))
                yield
                P.scan(lg[:], cst[:, CST_SCAN:CST_SCAN + TT], ld[:], 0.0, [cst, ld], [lg])
                P.ts("dve", kk_[:], k_, V(l, "kk", pb), None, ALU.mult, None, [zk_, vec], [kk_])
                P.tt("pool", kk2[:], kk_[:], kk_[:], ALU.mult, [kk_], [kk2])
                P.act(eg[:], lg[:], AF.Exp, [lg], [eg])
                yield
                bs = bank("rw")
                P.mm(bs[:, 0:TT], onesbd_f, kk2[:], True, True, [cst, kk2], [bs])
                P.act(sqk[:], bs[:, 0:TT], AF.Sqrt, [bs], [sqk])
                P.act(ieg[:], lg[:], AF.Exp, [lg], [ieg], scale=-1.0)
                P.tt("pool", eg1[:], lg[:], ld[:], ALU.subtract, [lg, ld], [eg1])
                P.act(eg1[:], eg1[:], AF.Exp, [eg1], [eg1])
                P.ts("dve", sqk[:], sqk[:], 1e-12, None, ALU.max, None, [sqk], [sqk])
                P.recip(sqk[:], sqk[:], [sqk], [sqk])
                P.tt("pool", kap[:], kk_[:], sqk[:], ALU.mult, [kk_, sqk], [kap])
                yield
                P.ts("dve", t1[:], a_[:], -1.0, V(l, "ka", pb), ALU.add, ALU.mult, [a_, vec], [t1])
                P.stt(kp[:], t1[:], 1.0, k_, ALU.add, ALU.mult, [t1, zk_], [kp])
                P.tt("pool", b_[:], kap[:], a_[:], ALU.mult, [kap, a_], [b_])
                lg3 = lg[:, :].rearrange("p (c t) -> p c t", t=LCH)
                P.tt("dve", dl[:, :].rearrange("p (c t) -> p c t", t=LCH), lg3[:, :, LCH - 1:LCH].to_broadcast([128, NCH, LCH]),
                     lg3, ALU.subtract, [lg], [dl])
                P.act(egl[:], dl[:], AF.Exp, [dl], [egl])
                P.copy("act", d["GL"][:, :], eg[:, :].rearrange("p (c t) -> p c t", t=LCH)[:, :, LCH - 1], [eg], [d["GL"]])
                yield
                P.tt("dve", rt32[:], r_, eg[:], ALU.mult, [zr_, eg], [rt32])
                P.copy("act", RTc[:], rt32[:], [rt32], [RTc])

                def padw(name, eng, in0, in1, rd):
                    t = pads[name]
                    tv = t[:, :].rearrange("p (c h t) -> p c h t", c=NCH, h=2)
                    for hh in range(2):
                        ps_ = slice(hh * 64, (hh + 1) * 64)
                        o = tv[ps_, :, hh, :]
                        i0 = in0[ps_, :].rearrange("p (c t) -> p c t", t=LCH)
                        if in1 is None:
                            P.copy(eng, o, i0, rd, [t])
                        else:
                            i1 = in1[ps_, :].rearrange("p (c t) -> p c t", t=LCH)
                            P.tt(eng, o, i0, i1, ALU.mult, rd, [t])
                padw("RTp", "act", rt32, None, [rt32])
                padw("KTp", "dve", kp, ieg, [kp, ieg])
                padw("CTp", "pool", kap, eg1, [kap, eg1])
                yield
                padw("BTp", "dve", b_, ieg, [b_, ieg])
                padw("VTp", "act", zs[:, 8 + pb, :], None, [zv_])
                padw("KGp", "pool", kp, egl, [kp, egl])
                padw("BGp", "dve", b_, egl, [b_, egl])
                yield "pre_pp"
                P.stt(rk[:], r_, V(l, "rk", pb), kp[:], ALU.mult, ALU.mult, [zr_, vec, kp], [rk])
                yield
                bb = bank("rw")
                P.mm(bb[:, 0:TT], onesbd_f, rk[:], True, True, [cst, rk], [bb])
                P.tt("dve", d["bonus"][:], bb[:, 0:TT], v_, ALU.mult, [bb, zv_], [d["bonus"]])
                if dbg_on and pb == 0:
                    dbg_dump("lg", lg, lg[:], [128, TT])
                    dbg_dump("kap", kap, kap[:], [128, TT])
                    dbg_dump("kp", kp, kp[:], [128, TT])
                    dbg_dump("a", a_, a_[:], [128, TT])
                yield

                def chunkmm(dst_bank, lname, rname, rbuf=None, rcols=128):
                    lt = pads[lname]
                    for c in range(NCH):
                        if rbuf is None:
                            rb_ = pads[rname]
                            rap = rb_[:, c * 128:(c + 1) * 128]
                        else:
                            rb_ = rbuf
                            rap = rbuf[:, c * rcols:(c + 1) * rcols]
                        P.mm(dst_bank[:, c * rcols:(c + 1) * rcols], lt[:, c * 128:(c + 1) * 128], rap, True, True,
                             [lt, rb_], [dst_bank])

                def masked(dst, src_bank, mcol, w):
                    n = NCH * w
                    P.tt("dve", dst[:, 0:n], src_bank[:, 0:n], mskb[:, mcol:mcol + n], ALU.mult, [src_bank, mskb], [dst])
                b1 = bank("rw")
                chunkmm(b1, "BTp", "CTp")
                masked(BTlev[0], b1, MSK_USN, 128)
                b2 = bank("rw")
                chunkmm(b2, "CTp", "BTp")
                masked(Blev[0], b2, MSK_LSN, 128)
                yield
                b3 = bank("rw")
                chunkmm(b3, "KTp", "CTp")
                masked(AkkT, b3, MSK_USP, 128)
                b4 = bank("rw")
                chunkmm(b4, "KTp", None, RTc, LCH)
                masked(d["ArkT"], b4, MSK_CI, LCH)
                b5 = bank("rw")
                chunkmm(b5, "BTp", None, RTc, LCH)
                masked(d["ArbT"], b5, MSK_CI, LCH)
                yield

            def tokmajor(src_name, dst_buf, dst_ap, eng):
                bt_ = bank("rw")
                lt = pads[src_name]
                for c in range(NCH):
                    P.mm(bt_[:, c * 128:(c + 1) * 128], lt[:, c * 128:(c + 1) * 128], ident_b, True, True,
                         [lt, cstb], [bt_])
                P.copy(eng, dst_ap, bt_[:, 0:NCH * 128] if len(dst_ap.shape) == 2 else
                       bt_[:, 0:NCH * 128].rearrange("p (c n) -> p c n", n=128), [bt_], [dst_buf])

            def solve(pb, inst, l=l):
                d = pp[inst]
                Blev, BTlev, AkkT, Xbf = Blev_i[inst], BTlev_i[inst], AkkT_i[inst], Xbf_i[inst]
                Xbfv = Xbf[:, :].rearrange("p (c n) -> p c n", n=256)
                tokmajor("VTp", d["Vbd"], d["Vbd"][:, :], "act")
                tokmajor("KGp", d["KGbd"], d["KGbd"][:, :], "act")
                yield
                tokmajor("BGp", d["BGbd"], d["BGbd"][:, :], "act")
                tokmajor("CTp", Xbf, Xbfv[:, :, 0:128], "act")
                bt_ = bank("rw")
                for c in range(NCH):
                    cs = slice(c * 128, (c + 1) * 128)
                    P.mm(bt_[:, cs], AkkT[:, cs], d["Vbd"][:, cs], True, True, [AkkT, d["Vbd"]], [bt_])
                P.copy("act", Xbfv[:, :, 128:256], bt_[:, 0:NCH * 128].rearrange("p (c n) -> p c n", n=128), [bt_], [Xbf])
                yield "tok_done"
                cur = 0
                NLEV = 6
                for lev in range(NLEV):
                    if lev < NLEV - 1:
                        nxt = 1 - cur
                        bq = bank("rw")
                        for c in range(NCH):
                            cs = slice(c * 128, (c + 1) * 128)
                            P.mm(bq[:, cs], Blev[cur][:, cs], BTlev[cur][:, cs], True, True, [Blev[cur], BTlev[cur]], [bq])
                        if lev < NLEV - 2:
                            bq2 = bank("rw")
                            for c in range(NCH):
                                cs = slice(c * 128, (c + 1) * 128)
                                P.mm(bq2[:, cs], BTlev[cur][:, cs], Blev[cur][:, cs], True, True,
                                     [Blev[cur], BTlev[cur]], [bq2])
                    for half in range(2):
                        bx_ = banks[4 + half]
                        for cc_ in range(2):
                            c = half * 2 + cc_
                            P.mm(bx_[:, cc_ * 256:(cc_ + 1) * 256], BTlev[cur][:, c * 128:(c + 1) * 128], Xbfv[:, c, :],
                                 True, False, [BTlev[cur], Xbf], [bx_])
                            P.mm(bx_[:, cc_ * 256:(cc_ + 1) * 256], ident_b, Xbfv[:, c, :],
                                 False, True, [cstb, Xbf], [bx_])
                    if lev < NLEV - 1:
                        P.copy("act", BTlev[nxt][:], bq[:, 0:512], [bq], [BTlev[nxt]])
                        if lev < NLEV - 2:
                            P.copy("dve", Blev[nxt][:], bq2[:, 0:512], [bq2], [Blev[nxt]])
                    P.copy("act", Xbf[:, 0:512], banks[4][:, 0:512], [banks[4]], [Xbf])
                    P.copy("dve", Xbf[:, 512:1024], banks[5][:, 0:512], [banks[5]], [Xbf])
                    if lev < NLEV - 1:
                        cur = nxt
                    yield
                nU0v = d["nU0"][:, :].rearrange("p (c n) -> p c n", n=128)
                Wbdv = d["Wbd"][:, :].rearrange("p (c n) -> p c n", n=128)
                P.ts("dve", nU0v, Xbfv[:, :, 128:256], -1.0, None, ALU.mult, None, [Xbf], [d["nU0"]])
                P.copy("pool", Wbdv, Xbfv[:, :, 0:128], [Xbf], [d["Wbd"]])
                yield
                br = bank("rw")
                for c in range(NCH):
                    P.mm(br[:, c * LCH:(c + 1) * LCH], d["Wbd"][:, c * 128:(c + 1) * 128],
                         d["ArbT"][:, c * LCH:(c + 1) * LCH], True, True, [d["Wbd"], d["ArbT"]], [br])
                P.tt("dve", d["Rhat"][:], d["rt32"][:], br[:, 0:TT], ALU.subtract, [d["rt32"], br], [d["Rhat"]])
                bp = bank("rw")
                for c in range(NCH):
                    cs = slice(c * 128, (c + 1) * 128)
                    P.mm(bp[:, cs], d["Wbd"][:, cs], d["BGbd"][:, cs], True, True, [d["Wbd"], d["BGbd"]], [bp])
                for c in range(NCH):
                    cs = slice(c * 128, (c + 1) * 128)
                    P.stt(d["PT"][:, cs], ident_f, d["GL"][:, c:c + 1], bp[:, cs], ALU.mult, ALU.subtract,
                          [cst, d["GL"], bp], [d["PT"]])
                yield

            def seq(pb, inst, c, l=l):
                d = pp[inst]
                T = Tst[l][pb]
                yb = banks[6]
                tb_ = banks[4 + inst]
                ycols = slice(inst * TT + c * LCH, inst * TT + (c + 1) * LCH)
                cs = slice(c * 128, (c + 1) * 128)
                cl_ = slice(c * LCH, (c + 1) * LCH)
                P.mm(yb[:, ycols], T[:], d["Rhat"][:, cl_], True, False, [T, d["Rhat"]], [yb])
                P.mm(yb[:, ycols], d["Vbd"][:, cs], d["ArkT"][:, cl_], False, False, [d["Vbd"], d["ArkT"]], [yb])
                P.mm(yb[:, ycols], d["nU0"][:, cs], d["ArbT"][:, cl_], False, True, [d["nU0"], d["ArbT"]], [yb])
                P.mm(tb_[:, 0:128], d["PT"][:, cs], T[:], True, False, [d["PT"], T], [tb_])
                P.mm(tb_[:, 0:128], d["KGbd"][:, cs], d["Vbd"][:, cs], False, False, [d["KGbd"], d["Vbd"]], [tb_])
                P.mm(tb_[:, 0:128], d["BGbd"][:, cs], d["nU0"][:, cs], False, True, [d["BGbd"], d["nU0"]], [tb_])
                P.copy("act", T[:], tb_[:, 0:128], [tb_], [T])

            def fin(pb, inst, l=l, dbg_on=dbg_on):
                assert lru_done[0], "fin emitted before the LRU chain finished (scratch lf[3] still live)"
                d = pp[inst]
                yb = banks[6]
                y32, yc, ysq, rs = spf[0], spf[1], spf[2], lf[3]
                P.copy("act", y32[:], yb[:, inst * TT:(inst + 1) * TT], [yb], [y32])
                if dbg_on:
                    dbg_dump("y_rw%d" % pb, y32, y32[:], [128, TT])
                bm = bank("rw")
                P.mm(bm[:, 0:TT], onesbd_f, y32[:], True, True, [cst, y32], [bm])
                P.stt(yc[:], bm[:, 0:TT], -1.0 / 64, y32[:], ALU.mult, ALU.add, [bm, y32], [yc])
                P.act(ysq[:], yc[:], AF.Square, [yc], [ysq])
                bv = bank("rw")
                P.mm(bv[:, 0:TT], onesbd_f, ysq[:], True, True, [cst, ysq], [bv])
                P.act(rs[:], bv[:, 0:TT], AF.Ln, [bv, eps_t[GN_EPS]], [rs], bias=eps_t[GN_EPS][:, 0:1], scale=1.0 / 64)
                P.act(rs[:], rs[:], AF.Exp, [rs], [rs], scale=-0.5)
                P.tt("dve", yc[:], yc[:], rs[:], ALU.mult, [yc, rs], [yc])
                P.ts("dve", yc[:], yc[:], V(l, "lnw", pb), V(l, "lnb", pb), ALU.mult, ALU.add, [yc, vec], [yc])
                P.tt("pool", yc[:], yc[:], d["bonus"][:], ALU.add, [yc, d["bonus"]], [yc])
                P.tt("dve", ycat[:, pb, :], yc[:], sgate[:, pb, :], ALU.mult, [yc, sgv[pb]], [ycv[pb]])

            def rwkv_gen():
                def chain(pb, inst):
                    yield from prep(pb, inst)
                    yield from solve(pb, inst)

                def tail(pbs):
                    for c in range(NCH):
                        for inst, pb in enumerate(pbs):
                            seq(pb, inst, c)
                            yield
                    for inst, pb in enumerate(pbs):
                        fin(pb, inst)
                        yield

                def merge(ga, gb):
                    da = db = False
                    while not (da and db):
                        if not da:
                            try:
                                next(ga)
                                yield
                            except StopIteration:
                                da = True
                        if not db:
                            try:
                                next(gb)
                                yield
                            except StopIteration:
                                db = True

                def half_front(pbs):
                    gA, gB = chain(pbs[0], 0), chain(pbs[1], 1)
                    for v in gA:
                        yield
                        if v == "tok_done":
                            break
                    yield from merge(gA, gB)

                yield from half_front((0, 1))
                t0_ = tail((0, 1))
                gA, gB = chain(2, 0), chain(3, 1)

                def front2a():
                    for v in gA:
                        yield
                        if v == "pre_pp":
                            break
                yield from merge(t0_, front2a())
                for v in gA:
                    yield
                    if v == "tok_done":
                        break
                yield from merge(gA, gB)
                yield from tail((2, 3))

            def s5_gen(l=l, dbg_on=dbg_on):
                Bv = s5B[l][:, :].rearrange("p (r b q m) -> p r b q m", r=2, b=4, q=2)
                Cv = s5C[l][:, :].rearrange("p (r j m) -> p r j m", r=2, j=16)
                rotk, keep = s5rot[l]
                NS = TT // TS5
                yb5 = banks[7]

                def stage0(blk, s, jj, k):
                    hf, jh = jj // 2, jj % 2
                    hs = slice(64 * hf, 64 * hf + 64)
                    tsl = slice(s * TS5, (s + 1) * TS5)
                    bu = banks[0]
                    c0 = (k % 2) * 2 * TS5
                    bre = bu[:, c0:c0 + TS5]
                    bim = bu[:, c0 + TS5:c0 + 2 * TS5]
                    P.mm(bre, Bv[hs, 0, blk, jh, :], ubf[hs, blk, tsl], True, True, [s5B[l], ubf], [bu])
                    P.mm(bim, Bv[hs, 1, blk, jh, :], ubf[hs, blk, tsl], True, True, [s5B[l], ubf], [bu])

                def stage1(blk, s, jj, k):
                    j = blk * 4 + jj
                    (t1, t2, bzr, bzi, zr, zi, t3, t4) = s5f[k]
                    bu = banks[0]
                    c0 = (k % 2) * 2 * TS5
                    bre = bu[:, c0:c0 + TS5]
                    bim = bu[:, c0 + TS5:c0 + 2 * TS5]
                    cosT = s5tab[:, 0, j, :]
                    sinT = s5tab[:, 1, j, :]
                    P.tt("dve", t1[:], bre, cosT, ALU.mult, [bu, s5tab], [t1])
                    P.tt("dve", t2[:], bim, sinT, ALU.mult, [bu, s5tab], [t2])
                    P.tt("dve", t3[:], bim, cosT, ALU.mult, [bu, s5tab], [t3])
                    P.tt("dve", t4[:], bre, sinT, ALU.mult, [bu, s5tab], [t4])
                    P.tt(S5E, bzr[:], t1[:], t2[:], ALU.add, [t1, t2], [bzr])
                    P.tt(S5E, bzi[:], t3[:], t4[:], ALU.subtract, [t3, t4], [bzi])

                def stage2(blk, s, jj, k):
                    j = blk * 4 + jj
                    hf, jh = jj // 2, jj % 2
                    hs = slice(64 * hf, 64 * hf + 64)
                    tsl = slice(s * TS5, (s + 1) * TS5)
                    (t1, t2, bzr, bzi, zr, zi, t3, t4) = s5f[k]
                    sx = s5x[k]
                    zl = s5zl[k]
                    zv = s5zv[l][j]
                    cosT = s5tab[:, 0, j, :]
                    sinT = s5tab[:, 1, j, :]
                    rho_b = keep[:, 0, j:j + 1].to_broadcast([128, TS5])
                    P.scan(zr[:], rho_b, bzr[:], s5z[l][:, 0, j:j + 1], [keep, bzr, zv], [zr])
                    P.scan(zi[:], rho_b, bzi[:], s5z[l][:, 1, j:j + 1], [keep, bzi, zv], [zi])
                    P.tt("dve", t1[:], zr[:], cosT, ALU.mult, [zr, s5tab], [t1])
                    P.tt("dve", t2[:], zi[:], sinT, ALU.mult, [zi, s5tab], [t2])
                    P.tt(S5E, sx[:, 0, :], t1[:], t2[:], ALU.subtract, [t1, t2], [sx])
                    P.tt("dve", t3[:], zr[:], sinT, ALU.mult, [zr, s5tab], [t3])
                    P.tt(S5E, t4[:], zi[:], cosT, ALU.mult, [zi, s5tab], [t4])
                    P.tt(S5E, sx[:, 1, :], t3[:], t4[:], ALU.add, [t3, t4], [sx])
                    rc = rotk[:, 0, j:j + 1]
                    rs_ = rotk[:, 1, j:j + 1]
                    zlr = zr[:, TS5 - 1:TS5]
                    zli = zi[:, TS5 - 1:TS5]
                    P.ts("dve", zl[:, 0:1], zli, rs_, None, ALU.mult, None, [zi, rotk], [zl])
                    P.ts("dve", zl[:, 1:2], zlr, rs_, None, ALU.mult, None, [zr, rotk], [zl])
                    P.stt(s5z[l][:, 0, j:j + 1], zlr, rc, zl[:, 0:1], ALU.mult, ALU.subtract, [zr, rotk, zl], [zv])
                    P.stt(s5z[l][:, 1, j:j + 1], zli, rc, zl[:, 1:2], ALU.mult, ALU.add, [zi, rotk, zl], [zv])

                def stage3(blk, s, jj, k):
                    j = blk * 4 + jj
                    hf, jh = jj // 2, jj % 2
                    hs = slice(64 * hf, 64 * hf + 64)
                    tsl = slice(s * TS5, (s + 1) * TS5)
                    sx = s5x[k]
                    P.mm(yb5[hs, tsl], Cv[:, 0, j, :], sx[:, 0, :], jh == 0, False, [s5C[l], sx], [yb5])
                    P.mm(yb5[hs, tsl], Cv[:, 1, j, :], sx[:, 1, :], False, jh == 1, [s5C[l], sx], [yb5])

                for blk in range(4):
                    units = [(s, jj) for s in range(NS) for jj in range(4)]
                    ks = []
                    for (s, jj) in units:
                        ks.append(s5ctr[0] % NS5SET)
                        s5ctr[0] += 1
                    nU = len(units)
                    stage0(blk, units[0][0], units[0][1], ks[0])
                    stage0(blk, units[1][0], units[1][1], ks[1])
                    stage1(blk, units[0][0], units[0][1], ks[0])
                    yield
                    for u in range(nU):
                        if u + 1 < nU:
                            stage1(blk, units[u + 1][0], units[u + 1][1], ks[u + 1])
                            if u + 2 < nU:
                                stage0(blk, units[u + 2][0], units[u + 2][1], ks[u + 2])
                            yield
                        stage2(blk, units[u][0], units[u][1], ks[u])
                        if u >= 1:
                            stage3(blk, units[u - 1][0], units[u - 1][1], ks[u - 1])
                        yield
                    stage3(blk, units[nU - 1][0], units[nU - 1][1], ks[nU - 1])
                    ys, x2, q_ = spf
                    P.stt(ys[:], zu[:, blk, :], V(l, "s5d", blk), yb5[:, 0:TT], ALU.mult, ALU.add, [zu, vec, yb5], [ys])
                    if dbg_on:
                        dbg_dump("s5y%d" % blk, ys, ys[:], [128, TT])
                    P.act(x2[:], ys[:], AF.Square, [ys], [x2])
                    P.ts("dve", x2[:], x2[:], 0.044715, 1.0, ALU.mult, ALU.add, [x2], [x2])
                    P.tt("pool", q_[:], x2[:], ys[:], ALU.mult, [x2, ys], [q_])
                    P.act(x2[:], q_[:], AF.Sigmoid, [q_], [x2], scale=2.0 * math.sqrt(2.0 / math.pi))
                    P.tt("dve", mix[:, blk, :], ys[:], x2[:], ALU.mult, [ys, x2], [mixv[blk]])
                    yield
                zgb = ubf
                P.copy("pool", zgb[:], mix[:], mixv, [zgb])
                rg = next_ring()
                P.dma(rg[:, 0:2048].rearrange("p (k n) -> p k n", k=4), glu_w_b[l].rearrange("(k p) n -> p k n", p=128),
                      reads=[wsb], writes=[rg])
                rgv = rg[:, 0:2048].rearrange("p (k n) -> p k n", k=4)
                for ob in range(4):
                    b = banks[0]
                    for k in range(4):
                        P.mm(b[:, 0:TT], rgv[:, k, ob * 128:(ob + 1) * 128], zgb[:, k, :], k == 0, k == 3, [rg, zgb], [b])
                    sg_ = spf[ob % 2]
                    P.act(sg_[:], b[:, 0:TT], AF.Sigmoid, [b, vec], [sg_], bias=V(l, "glub", ob))
                    P.tt("pool", sg_[:], sg_[:], sgate[:, 4 + ob, :], ALU.mult, [sg_, sgv[4 + ob]], [sg_])
                    P.tt("dve", ycat[:, 4 + ob, :], mix[:, ob, :], sg_[:], ALU.mult, [mixv[ob], sg_], [ycv[4 + ob]])
                    yield

            lru_done = [("lru" in SKIP)]

            def lru_gen(l=l, dbg_on=dbg_on):
                for blk in range(4):
                    A, B, C, Dd = lf
                    bl = banks[1]
                    P.ts("dve", A[:], zx[:, blk, 0:TT], V(l, "cw", 0 * 4 + blk), V(l, "cb", blk), ALU.mult, ALU.add,
                         [zx, vec], [A])
                    for j in range(1, 4):
                        P.stt(A[:], zx[:, blk, j:j + TT], V(l, "cw", j * 4 + blk), A[:], ALU.mult, ALU.add,
                              [zx, vec, A], [A])
                    P.copy("act", lb[:], A[:], [A], [lb])
                    yield
                    P.mm(bl[:, 0:TT], lruw[l][:, blk * 128:(blk + 1) * 128], lb[:], True, True, [lruw[l], lb], [bl])
                    P.mm(bl[:, TT:2 * TT], lruw[l][:, (4 + blk) * 128:(5 + blk) * 128], lb[:], True, True,
                         [lruw[l], lb], [bl])
                    P.act(B[:], bl[:, 0:TT], AF.Sigmoid, [bl, vec], [B], bias=V(l, "ba", blk))
                    P.act(C[:], bl[:, TT:2 * TT], AF.Sigmoid, [bl, vec], [C], bias=V(l, "bx", blk))
                    yield
                    P.act(Dd[:], B[:], AF.Exp, [B, lru_c], [Dd], scale=lru_c[:, l, blk:blk + 1])
                    P.act(B[:], B[:], AF.Exp, [B, lru_c], [B], scale=lru_c[:, l, 4 + blk:5 + blk])
                    P.act(B[:], B[:], AF.Ln, [B, one_t], [B], bias=one_t[:, 0:1], scale=-1.0)
                    P.act(B[:], B[:], AF.Exp, [B], [B], scale=0.5)
                    P.tt("pool", C[:], C[:], A[:], ALU.mult, [C, A], [C])
                    P.tt("pool", C[:], C[:], B[:], ALU.mult, [C, B], [C])
                    yield
                    P.scan(A[:], Dd[:], C[:], ch[l][:, blk:blk + 1], [Dd, C, ch[l]], [A])
                    P.copy("act", ch[l][:, blk:blk + 1], A[:, TT - 1:TT], [A], [ch[l]])
                    if dbg_on:
                        dbg_dump("lru%d" % blk, A, A[:], [128, TT])
                    P.tt("pool", ycat[:, 8 + blk, :], A[:], sgate[:, 8 + blk, :], ALU.mult, [A, sgv[8 + blk]], [ycv[8 + blk]])
                    if blk == 3:
                        lru_done[0] = True
                    yield

            gens = []
            if "rwkv" not in SKIP:
                gens.append((rwkv_gen(), GW[0]))
            if "s5" not in SKIP:
                gens.append((s5_gen(), GW[1]))
            if "lru" not in SKIP:
                gens.append((lru_gen(), GW[2]))
            if "serial" in SKIP:
                for g, w in gens:
                    drive([(g, 1)])
            else:
                drive(gens)
            if dbg_on:
                dbg_dump("ycat_rw", ycat, ycat[:, 0:4, :].rearrange("p a t -> p (a t)"), [128, 4 * TT], BF16)

            for oc in range(4):
                r = next_ring()
                P.dma(r[:, 0:12 * 256].rearrange("p (k n) -> p k n", k=12),
                      w_out_b[l].rearrange("(k p) n -> p k n", p=128)[:, :, oc * 256:(oc + 1) * 256],
                      reads=[wsb], writes=[r])
                rv = r[:, 0:12 * 256].rearrange("p (k n) -> p k n", k=12)
                for ob2 in range(2):
                    ob = oc * 2 + ob2
                    b = bank("proj")
                    for k in range(12):
                        P.mm(b[:, 0:TT], rv[:, k, ob2 * 128:(ob2 + 1) * 128], ycat[:, k, :], k == 0, k == 11,
                             [r] + ycv, [b])
                    P.tt("dve", hT[:, ob, :], hT[:, ob, :], b[:, 0:TT], ALU.add, [hT, b], [hT])
            for k in range(8):
                P.copy("act" if k % 2 == 0 else "dve", xn[:, k, :], hT[:, k, :], [hT], [xn])
            r = next_ring()
            P.dma(r[:, 0:2048].rearrange("p (k n) -> p k n", k=2), ple_w_b[l].rearrange("(k p) n -> p k n", p=128),
                  reads=[wsb], writes=[r])
            rv = r[:, 0:2048].rearrange("p (k n) -> p k n", k=2)
            epre = zs
            for ob in range(8):
                b = bank("proj")
                for k in range(2):
                    P.mm(b[:, 0:TT], rv[:, k, ob * 128:(ob + 1) * 128], pbf[:, k, :], k == 0, k == 1, [r, pbf], [b])
                P.copy("act", epre[:, ob, :], b[:, 0:TT], [b], [zsv[ob]])
            rms_rstd(zsv[0:8], lambda k: epre[:, k, :], 8, NORM_EPS)
            for gc in range(2):
                r = next_ring()
                P.dma(r[:, 0:4096].rearrange("p (k n) -> p k n", k=8),
                      ple_gw_b[l].rearrange("(k p) n -> p k n", p=128)[:, :, gc * 512:(gc + 1) * 512],
                      reads=[wsb], writes=[r])
                rv = r[:, 0:4096].rearrange("p (k n) -> p k n", k=8)
                for ob2 in range(4):
                    ob = gc * 4 + ob2
                    b = bank("proj")
                    for k in range(8):
                        P.mm(b[:, 0:TT], rv[:, k, ob2 * 128:(ob2 + 1) * 128], xn[:, k, :], k == 0, k == 7, [r, xn], [b])
                    sg_, e_ = fs[6 + 2 * (ob % 2)], fs[7 + 2 * (ob % 2)]
                    P.act(sg_[:], b[:, 0:TT], AF.Sigmoid, [b], [sg_])
                    P.stt(e_[:], epre[:, ob, :], V(l, "png", ob), rstd[:], ALU.mult, ALU.mult, [zsv[ob], vec, rstd], [e_])
                    P.tt("dve", e_[:], e_[:], sg_[:], ALU.mult, [e_, sg_], [e_])
                    P.tt("dve", hT[:, ob, :], hT[:, ob, :], e_[:], ALU.add, [hT, e_], [hT])
            if dbg_on:
                dbg_dump("h1", hT, hT[:, :, :].rearrange("p a t -> p (a t)"), [128, 8 * TT])
        rms_rstd([hT], lambda k: hT[:, k, :], 8, NORM_EPS)
        fo = 2 * VEC_PER_LAYER
        for k in range(8):
            P.stt(zs[:, k, :], hT[:, k, :], vec[:, fo + k:fo + k + 1], rstd[:], ALU.mult, ALU.mult, [hT, vec, rstd], [zsv[k]])
        out_toks.append(P.dma(oT[:, t0:t0 + TT].rearrange("(k p) t -> p k t", p=128), zs[:, 0:8, :], reads=zsv[0:8]))
    out_toks.extend(dbg_out.values())
    ninst = P.ninst
    P.finish(out_toks)
    return nc, ninst


def pack_shared(inp):
    f = lambda a: np.asarray(a, np.float32)
    vec = np.zeros((128, NVEC), np.float32)
    for l in range(2):
        def put(name, arr, n):
            c = vcol(l, name)
            vec[:, c:c + n] = _pp(arr, n)
        put("ng", f(inp["norm_g"])[l], 8)
        put("mu", f(inp["rwkv_mu"])[l], 13)
        put("w0", f(inp["rwkv_w0"])[l], 4)
        put("a0", f(inp["rwkv_a0"])[l], 4)
        put("kk", f(inp["rwkv_k_k"])[l], 4)
        put("ka", f(inp["rwkv_k_a"])[l], 4)
        put("rk", f(inp["rwkv_r_k"])[l].reshape(512), 4)
        put("lnw", f(inp["rwkv_ln_w"])[l], 4)
        put("lnb", f(inp["rwkv_ln_b"])[l], 4)
        put("s5d", f(inp["s5_d"])[l], 4)
        put("glub", f(inp["s5_glu_b"])[l], 4)
        cw = f(inp["lru_conv_w"])[l]
        c = vcol(l, "cw")
        for j in range(4):
            vec[:, c + 4 * j:c + 4 * j + 4] = _pp(cw[j], 4)
        put("cb", f(inp["lru_conv_b"])[l], 4)
        put("ba", f(inp["lru_ba"])[l], 4)
        put("bx", f(inp["lru_bx"])[l], 4)
        put("lam", f(inp["lru_lambda"])[l], 4)
        put("png", f(inp["ple_norm_g"])[l], 8)
    vec[:, 2 * VEC_PER_LAYER:2 * VEC_PER_LAYER + 8] = _pp(f(inp["final_norm_g"]), 8)

    w2a2 = np.zeros((2, 128, 512), np.float32)
    w2a2[:, 0:64] = f(inp["rwkv_w2"])
    w2a2[:, 64:128] = f(inp["rwkv_a2"])
    lruw = np.zeros((2, 128, 8, 128), np.float32)
    for l in range(2):
        for m, key in enumerate(("lru_wa", "lru_wx")):
            w = f(inp[key])[l]
            for q in range(4):
                for b2 in range(2):
                    lruw[l, b2 * 64:(b2 + 1) * 64, m * 4 + q, b2 * 64:(b2 + 1) * 64] = w[2 * q + b2]
    lruw = lruw.reshape(2, 128, 1024)
    def modes(a):
        a = f(a).reshape(2, 16, 2, 64)
        return np.ascontiguousarray(a.transpose(0, 2, 3, 1).reshape(2, 128, 16))
    s5s = np.zeros((2, 128, 3, 16), np.float32)
    s5s[:, :, 0] = modes(inp["s5_a_re"])
    s5s[:, :, 1] = modes(inp["s5_a_im"])
    ldt = np.broadcast_to(f(inp["s5_log_dt"])[:, :, None], (2, 32, 64))
    s5s[:, :, 2] = modes(ldt)
    s5s = s5s.reshape(2, 128, 48)
    def bmodes(a):
        a = f(a).reshape(2, 16, 2, 64, 16)
        return a.transpose(0, 2, 3, 1, 4).reshape(2, 128, 16, 16)
    s5b = np.stack([bmodes(inp["s5_b_re"]), bmodes(inp["s5_b_im"])], axis=2).reshape(2, 128, 512)
    s5c = np.zeros((2, 128, 2, 16, 64), np.float32)
    for ri, key in enumerate(("s5_c_re", "s5_c_im")):
        c = f(inp[key]).reshape(2, 16, 2, 16, 64)
        for gh in range(2):
            for jh in range(2):
                c0 = 32 * jh + 16 * gh
                s5c[:, gh * 64:(gh + 1) * 64, ri, jh::2, c0:c0 + 16] = c[:, jh::2, gh].transpose(0, 3, 1, 2)
    s5c = s5c.reshape(2, 128, 2048)
    return {
        "w_in": np.ascontiguousarray(f(inp["w_in"])), "w_out": np.ascontiguousarray(f(inp["w_out"])),
        "ple_w": np.ascontiguousarray(f(inp["ple_w"])), "ple_gw": np.ascontiguousarray(f(inp["ple_gate_w"])),
        "glu_w": np.ascontiguousarray(f(inp["s5_glu_w"])), "vec": vec, "cst": make_consts()[0], "msk": make_consts()[1],
        "w2a2": w2a2, "lruw": np.ascontiguousarray(lruw), "s5s": np.ascontiguousarray(s5s),
        "s5b": np.ascontiguousarray(s5b), "s5c": np.ascontiguousarray(s5c),
    }


_NC_CACHE = {}


def run_cores(inp, TC, batches, dbg=None):
    key = (TC, tuple(sorted(dbg)) if dbg else None)
    if key not in _NC_CACHE:
        _NC_CACHE[key] = build_nc(TC, dbg)
    nc, ninst = _NC_CACHE[key]
    shared = pack_shared(inp)
    x = np.asarray(inp["x"], np.float32)
    p = np.asarray(inp["p"], np.float32)
    in_maps = []
    for b in batches:
        m = dict(shared)
        m["xT"] = np.ascontiguousarray(x[b, :TC].T)
        m["pT"] = np.ascontiguousarray(p[:, b, :TC].transpose(0, 2, 1))
        in_maps.append(m)
    res = run_bass_kernel_spmd(nc, in_maps, core_ids=list(range(len(batches))))
    return res


def kernel(**inputs):
    x = np.asarray(inputs["x"])
    B, S, _ = x.shape
    batches = [c % B for c in range(8)]
    res = run_cores(inputs, S, batches)
    out = np.empty((B, S, D), np.float32)
    for b in range(B):
        out[b] = res.results[b]["oT"].T
    return out.astype(x.dtype)
```

```python
import contextlib
import math
import numpy as np
import concourse.bass as bass
import concourse.mybir as mybir
from concourse.bass_utils import run_bass_kernel_spmd

F32 = mybir.dt.float32
BF16 = mybir.dt.bfloat16
ALU = mybir.AluOpType
AF = mybir.ActivationFunctionType

D = 1024
DIN = 4224
DMIX = 1536
DPLE = 256
TT = 256
LCH = 64
NCH = TT // LCH
TS5 = 128
import os
SKIP = set(os.environ.get("KSKIP", "").split(","))
S5E = os.environ.get("KS5E", "pool")
GW = tuple(int(v) for v in os.environ.get("KGW", "1,1,1").split(","))
GN_EPS = 64e-5
NORM_EPS = 1e-6


class Tok:
    __slots__ = ("sem", "val", "eng", "dma")

    def __init__(self, sem, val, eng, dma):
        self.sem, self.val, self.eng, self.dma = sem, val, eng, dma


class Buf:
    def __init__(self, t, name):
        self.t = t
        self.name = name
        self.w = None
        self.r = []

    def __getitem__(self, idx):
        return self.t[idx]


class Prog:
    ENGS = ("pe", "act", "dve", "pool", "sp")

    def __init__(self, nc, n_dma_sems=32):
        self.nc = nc
        self.es = contextlib.ExitStack()
        self.ops = {e: [] for e in self.ENGS}
        self.cnt = {e: 0 for e in self.ENGS}
        self.sem = {e: self.es.enter_context(nc.semaphore("s_" + e)) for e in self.ENGS}
        self.dsem = [self.es.enter_context(nc.semaphore("d%d" % i)) for i in range(n_dma_sems)]
        self.duse = [0] * n_dma_sems
        self.dnext = 0
        self.seen = {e: {} for e in self.ENGS}
        self.nbuf = 0
        self.ninst = 0
        self.stack = [self.es]

    def push(self):
        st = contextlib.ExitStack()
        self.stack.append(st)

    def pop(self):
        self.barrier()
        self.stack.pop().close()

    def barrier(self):
        toks = []
        for f in self.ENGS:
            if self.cnt[f] > 0:
                toks.append(Tok(self.sem[f], self.cnt[f], f, False))
        for i, s in enumerate(self.dsem):
            if self.duse[i] > 0:
                toks.append(Tok(s, 16 * self.duse[i], "dma", True))
        for e in self.ENGS:
            wl = []
            for t in toks:
                if t.eng == e and not t.dma:
                    continue
                k = id(t.sem)
                if self.seen[e].get(k, 0) >= t.val:
                    continue
                self.seen[e][k] = t.val
                wl.append((t.sem, t.val))

            def run(en, wl=wl):
                for (s, v) in wl:
                    en.wait_ge(s, v)
            self.ops[e].append(run)

    def sb(self, shape, dt=F32, name=None):
        self.nbuf += 1
        name = name or ("b%d" % self.nbuf)
        t = self.stack[-1].enter_context(self.nc.sbuf_tensor("sb_" + name, list(shape), dt))
        return Buf(t, name)

    def ps(self, name, dt=F32, cols=512):
        t = self.es.enter_context(self.nc.psum_tensor(name, [128, cols], dt))
        return Buf(t, name)

    def wrap(self, t, name):
        return Buf(t, name)

    def views(self, buf, n):
        return [Buf(buf.t, "%s.v%d" % (buf.name, i)) for i in range(n)]

    def _need(self, eng, tok, waits, is_dma_issue):
        if tok is None:
            return
        if tok.eng == eng and not tok.dma and not is_dma_issue and eng == "pe":
            return
        k = id(tok.sem)
        if self.seen[eng].get(k, 0) >= tok.val:
            return
        cur = waits.get(k)
        if cur is None or cur[1] < tok.val:
            waits[k] = (tok.sem, tok.val)

    def emit(self, eng, fn, reads=(), writes=(), dma=False):
        waits = {}
        for b in reads:
            self._need(eng, b.w, waits, dma)
        for b in writes:
            self._need(eng, b.w, waits, dma)
            for t in b.r:
                self._need(eng, t, waits, dma)
        if dma:
            i = self.dnext
            self.dnext = (self.dnext + 1) % len(self.dsem)
            s = self.dsem[i]
            if self.duse[i] > 0:
                self._need(eng, Tok(s, 16 * self.duse[i], "dma", True), waits, True)
            self.duse[i] += 1
            tok = Tok(s, 16 * self.duse[i], "dma", True)
            inc = 16
        else:
            self.cnt[eng] += 1
            tok = Tok(self.sem[eng], self.cnt[eng], eng, False)
            inc = 1
        wl = list(waits.values())
        for (s, v) in wl:
            self.seen[eng][id(s)] = v
        tsem = tok.sem
        self.ninst += 1 + len(wl)

        def run(e, wl=wl, fn=fn, tsem=tsem, inc=inc):
            for (s, v) in wl:
                e.wait_ge(s, v)
            fn(e).then_inc(tsem, inc)

        self.ops[eng].append(run)
        for b in reads:
            b.r = [t for t in b.r if t.sem is not tok.sem]
            b.r.append(tok)
        for b in writes:
            b.w = tok
            b.r = []
        return tok

    def finish(self, out_toks):
        wl = [(t.sem, t.val) for t in out_toks]

        def run(e, wl=wl):
            for (s, v) in wl:
                e.wait_ge(s, v)

        self.ops["sp"].append(run)
        nc = self.nc
        ops = self.ops
        with nc.Block() as block:
            @block.tensor
            def _(e):
                for f in ops["pe"]:
                    f(e)

            @block.scalar
            def _(e):
                for f in ops["act"]:
                    f(e)

            @block.vector
            def _(e):
                for f in ops["dve"]:
                    f(e)

            @block.gpsimd
            def _(e):
                for f in ops["pool"]:
                    f(e)

            @block.sync
            def _(e):
                for f in ops["sp"]:
                    f(e)
        self.es.close()

    def dma(self, out, in_, reads=(), writes=(), eng="sp", **kw):
        return self.emit(eng, lambda e: e.dma_start(out=out, in_=in_, **kw), reads, writes, dma=True)

    def mm(self, out, lhsT, rhs, start, stop, reads, writes):
        return self.emit("pe", lambda e: e.matmul(out, lhsT, rhs, start=start, stop=stop), reads, writes)

    def act(self, out, in_, func, reads, writes, bias=None, scale=None):
        kw = {}
        if bias is not None:
            kw["bias"] = bias
        if scale is not None:
            kw["scale"] = scale
        return self.emit("act", lambda e: e.activation(out=out, in_=in_, func=func, **kw), reads, writes)

    def tt(self, eng, out, in0, in1, op, reads, writes):
        return self.emit(eng, lambda e: e.tensor_tensor(out=out, in0=in0, in1=in1, op=op), reads, writes)

    def ts(self, eng, out, in0, s1, s2, op0, op1, reads, writes):
        if op1 is None:
            return self.emit(eng, lambda e: e.tensor_scalar(out, in0, s1, None, op0), reads, writes)
        return self.emit(eng, lambda e: e.tensor_scalar(out, in0, s1, s2, op0, op1), reads, writes)

    def stt(self, out, in0, scalar, in1, op0, op1, reads, writes):
        return self.emit("dve", lambda e: e.scalar_tensor_tensor(out, in0, scalar, in1, op0, op1), reads, writes)

    def copy(self, eng, out, in_, reads, writes):
        if eng == "act":
            return self.emit("act", lambda e: e.activation(out=out, in_=in_, func=AF.Copy), reads, writes)
        return self.emit(eng, lambda e: e.tensor_copy(out, in_), reads, writes)

    def memset(self, eng, ap, val, writes):
        return self.emit(eng, lambda e: e.memset(ap, val), (), writes)

    def scan(self, out, d0, d1, init, reads, writes):
        return self.emit("dve", lambda e: e.tensor_tensor_scan(out, d0, d1, init, ALU.mult, ALU.add), reads, writes)

    def recip(self, out, in_, reads, writes):
        return self.emit("dve", lambda e: e.reciprocal(out, in_), reads, writes)


VEC_FIELDS = [("ng", 8), ("mu", 13), ("w0", 4), ("a0", 4), ("kk", 4), ("ka", 4), ("rk", 4), ("lnw", 4),
              ("lnb", 4), ("s5d", 4), ("glub", 4), ("cw", 16), ("cb", 4), ("ba", 4), ("bx", 4), ("lam", 4),
              ("png", 8)]
VEC_PER_LAYER = sum(n for _, n in VEC_FIELDS)
VEC_OFF = {}
_o = 0
for _n, _c in VEC_FIELDS:
    VEC_OFF[_n] = _o
    _o += _c
NVEC = 2 * VEC_PER_LAYER + 8

CST_IDENT = 0
CST_ONESBD = 128
CST_SCAN = 256
NCST = 256 + TT
MSK_USN = 0
MSK_LSN = NCH * 128
MSK_USP = 2 * NCH * 128
MSK_CI = 3 * NCH * 128
NMSK = 3 * NCH * 128 + NCH * LCH


def vcol(l, name, i=0):
    return l * VEC_PER_LAYER + VEC_OFF[name] + i


def _pp(v, n):
    return np.ascontiguousarray(np.asarray(v, np.float32).reshape(n, 128).T)


def make_consts():
    c = np.zeros((128, NCST), np.float32)
    i = np.arange(128)[:, None]
    j = np.arange(128)[None, :]
    c[:, CST_IDENT:CST_IDENT + 128] = (i == j)
    c[:, CST_ONESBD:CST_ONESBD + 128] = ((i // 64) == (j // 64))
    tt = np.arange(TT)[None, :]
    c[:, CST_SCAN:CST_SCAN + TT] = 1.0 * ((tt % LCH) != 0)
    m = np.zeros((128, NMSK), np.float32)
    t = np.arange(64)[None, :]
    for ch in range(NCH):
        m[:, MSK_USN + ch * 128:MSK_USN + (ch + 1) * 128] = -1.0 * (j > i)
        m[:, MSK_LSN + ch * 128:MSK_LSN + (ch + 1) * 128] = -1.0 * (i > j)
        m[:, MSK_USP + ch * 128:MSK_USP + (ch + 1) * 128] = 1.0 * (j > i)
        m[:, MSK_CI + ch * 64:MSK_CI + (ch + 1) * 64] = 1.0 * (t >= (i % 64))
    return c, m


def build_nc(TC, dbg=None):
    assert TC % TT == 0
    NT = TC // TT
    dbg = dbg or set()
    nc = bass.Bass("TRN2", target_bir_lowering=False)

    def din(name, shape, dt=F32):
        return nc.dram_tensor(name, list(shape), dt, kind="ExternalInput").ap()

    xT = din("xT", [D, TC])
    pT = din("pT", [2, DPLE, TC])
    w_in = din("w_in", [2, D, DIN])
    w_out = din("w_out", [2, DMIX, D])
    ple_w = din("ple_w", [2, DPLE, D])
    ple_gw = din("ple_gw", [2, D, D])
    glu_w = din("glu_w", [2, 512, 512])
    vec_d = din("vec", [128, NVEC])
    cst_d = din("cst", [128, NCST])
    msk_d = din("msk", [128, NMSK])
    w2a2_d = din("w2a2", [2, 128, 512])
    lruw_d = din("lruw", [2, 128, 8 * 128])
    s5s_d = din("s5s", [2, 128, 3 * 16])
    s5b_d = din("s5b", [2, 128, 2 * 16 * 16])
    s5c_d = din("s5c", [2, 128, 2 * 16 * 64])
    oT = nc.dram_tensor("oT", [D, TC], F32, kind="ExternalOutput").ap()
    dbg_out = {}

    def dram_int(name, shape, dt):
        return nc.dram_tensor(name, list(shape), dt, kind="Internal").ap()

    w_in_b = dram_int("w_in_b", [2, D, DIN], BF16)
    w_out_b = dram_int("w_out_b", [2, DMIX, D], BF16)
    ple_w_b = dram_int("ple_w_b", [2, DPLE, D], BF16)
    ple_gw_b = dram_int("ple_gw_b", [2, D, D], BF16)
    glu_w_b = dram_int("glu_w_b", [2, 512, 512], BF16)
    s5tab_d = dram_int("s5tab", [2, 128, 2 * 16 * TS5], F32)

    P = Prog(nc)
    wsb = P.wrap(None, "wscratch")
    tabsb = P.wrap(None, "s5tabscr")

    def dbg_dump(name, buf, ap, shape, dt=F32):
        if name not in dbg:
            return
        o = nc.dram_tensor("dbg_" + name, list(shape), dt, kind="ExternalOutput").ap()
        dbg_out[name] = P.dma(o, ap, reads=[buf])

    vec = P.sb([128, NVEC], F32, "vec")
    cst = P.sb([128, NCST], F32, "cst")
    P.dma(vec[:], vec_d, writes=[vec])
    P.dma(cst[:], cst_d, writes=[cst])
    cstb = P.sb([128, 128], BF16, "cstb")
    P.copy("dve", cstb[:], cst[:, CST_IDENT:CST_IDENT + 128], [cst], [cstb])
    mskb = P.sb([128, NMSK], BF16, "mskb")
    ident_f = cst[:, CST_IDENT:CST_IDENT + 128]
    ident_b = cstb[:, 0:128]
    onesbd_f = cst[:, CST_ONESBD:CST_ONESBD + 128]
    ones_f = P.sb([128, 128], F32, "ones_f")
    P.memset("pool", ones_f[:], 1.0, [ones_f])
    one_t = P.sb([128, 1], F32, "one_t")
    P.memset("pool", one_t[:], 1.0, [one_t])

    def V(l, name, i=0, n=1):
        c = vcol(l, name, i)
        return vec[:, c:c + n]

    for l in range(2):
        for (src, dst, rows) in ((w_in, w_in_b, D), (w_out, w_out_b, DMIX), (ple_w, ple_w_b, DPLE),
                                 (ple_gw, ple_gw_b, D), (glu_w, glu_w_b, 512)):
            for r0 in range(0, rows, 128):
                P.dma(dst[l, r0:r0 + 128, :], src[l, r0:r0 + 128, :], writes=[wsb], eng="pool",
                      max_dma_last_dim=4096)

    w2a2 = []
    lruw = []
    for l in range(2):
        w2a2.append(P.sb([128, 512], BF16, "w2a2_%d" % l))
        lruw.append(P.sb([128, 1024], BF16, "lruw_%d" % l))
    lru_c = P.sb([128, 2, 8], F32, "lru_c")
    s5B = [P.sb([128, 2 * 4 * 2 * 128], BF16, "s5B%d" % l) for l in range(2)]
    s5C = [P.sb([128, 2048], BF16, "s5C%d" % l) for l in range(2)]
    s5keep = [P.sb([128, 3, 16], F32, "s5keep%d" % l) for l in range(2)]
    s5rotb = [P.sb([128, 2, 16], F32, "s5rot%d" % l) for l in range(2)]
    ps_misc = P.ps("ps7")
    P.push()
    stage = P.sb([128, 1024], F32, "stage")
    mstage = P.sb([128, NMSK], F32, "mstage")
    P.dma(mstage[:], msk_d, writes=[mstage])
    P.copy("act", mskb[:], mstage[:], [mstage], [mskb])
    for l in range(2):
        P.dma(stage[:, 0:512], w2a2_d[l], writes=[stage])
        P.copy("act", w2a2[l][:], stage[:, 0:512], [stage], [w2a2[l]])
        P.dma(stage[:], lruw_d[l], writes=[stage])
        P.copy("act", lruw[l][:], stage[:], [stage], [lruw[l]])

    for l in range(2):
        tmp = P.sb([128, 4], F32, "lrutmp%d" % l)
        P.act(tmp[:], V(l, "lam", 0, 4), AF.Exp, [vec], [tmp], scale=-1.0)
        P.act(tmp[:], tmp[:], AF.Ln, [tmp, one_t], [tmp], bias=one_t[:, 0:1])
        P.ts("dve", lru_c[:, l, 0:4], tmp[:], -8.0, None, ALU.mult, None, [tmp], [lru_c])
        P.ts("dve", lru_c[:, l, 4:8], tmp[:], -16.0, None, ALU.mult, None, [tmp], [lru_c])

    s5rot = []
    for l in range(2):
        s5s = P.sb([128, 48], F32, "s5s%d" % l)
        P.dma(s5s[:], s5s_d[l], writes=[s5s])
        a_re = s5s[:, 0:16]
        a_im = s5s[:, 16:32]
        ldt = s5s[:, 32:48]
        w = P.sb([128, 16, 16], F32, "s5w%d" % l)
        R = [w]

        def row(i):
            return w[:, i, :]
        dt_, rho, th, cc, ss, t1, t2, lr, li, den, qre, qim, nr = [row(i) for i in range(13)]
        P.act(dt_, ldt, AF.Exp, [s5s], R)
        P.tt("dve", rho, a_re, dt_, ALU.mult, [s5s] + R, R)
        P.act(rho, rho, AF.Exp, R, R)
        P.tt("dve", th, a_im, dt_, ALU.mult, [s5s] + R, R)
        hp = P.sb([128, 1], F32, "halfpi%d" % l)
        P.memset("dve", hp[:], math.pi / 2, [hp])
        P.act(cc, th, AF.Sin, R + [hp], R, bias=hp[:, 0:1], scale=1.0 / 16)
        P.act(ss, th, AF.Sin, R, R, scale=1.0 / 16)

        def csq(c_, s_):
            P.tt("dve", t1, c_, c_, ALU.mult, R, R)
            P.tt("dve", t2, s_, s_, ALU.mult, R, R)
            P.stt(s_, c_, 2.0, s_, ALU.mult, ALU.mult, R, R)
            P.tt("dve", c_, t1, t2, ALU.subtract, R, R)
        for _ in range(4):
            csq(cc, ss)
        P.tt("dve", lr, rho, cc, ALU.mult, R, R)
        P.tt("dve", li, rho, ss, ALU.mult, R, R)
        P.tt("dve", t1, a_re, a_re, ALU.mult, [s5s] + R, R)
        P.tt("dve", t2, a_im, a_im, ALU.mult, [s5s] + R, R)
        P.tt("dve", den, t1, t2, ALU.add, R, R)
        P.recip(den, den, R, R)
        P.ts("dve", nr, lr, -1.0, None, ALU.add, None, R, R)
        P.tt("dve", t1, nr, a_re, ALU.mult, [s5s] + R, R)
        P.tt("dve", t2, li, a_im, ALU.mult, [s5s] + R, R)
        P.tt("dve", t1, t1, t2, ALU.add, R, R)
        P.tt("dve", qre, t1, den, ALU.mult, R, R)
        P.tt("dve", t1, li, a_re, ALU.mult, [s5s] + R, R)
        P.tt("dve", t2, nr, a_im, ALU.mult, [s5s] + R, R)
        P.tt("dve", t1, t1, t2, ALU.subtract, R, R)
        P.tt("dve", qim, t1, den, ALU.mult, R, R)
        keep = s5keep[l]
        P.copy("dve", keep[:, 0, :], rho, R, [keep])
        P.copy("dve", keep[:, 1, :], cc, R, [keep])
        P.copy("dve", keep[:, 2, :], ss, R, [keep])

        sbf = P.sb([128, 512], F32, "s5b_in%d" % l)
        P.dma(sbf[:], s5b_d[l], writes=[sbf])
        bre = sbf[:, 0:256].rearrange("p (j h) -> p j h", h=16)
        bim = sbf[:, 256:512].rearrange("p (j h) -> p j h", h=16)
        Bt = s5B[l]
        Btv = Bt[:, :].rearrange("p (r b q m) -> p r b q m", r=2, b=4, q=2)
        bpad = P.sb([128, 2, 128], BF16, "s5bpad%d" % l)
        tb = P.sb([128, 2, 16], F32, "s5tb%d" % l)
        for j in range(16):
            P.ts("dve", tb[:, 0, :], bim[:, j, :], qim[:, j:j + 1], None, ALU.mult, None, [sbf] + R, [tb])
            P.stt(tb[:, 0, :], bre[:, j, :], qre[:, j:j + 1], tb[:, 0, :], ALU.mult, ALU.subtract, [sbf, tb] + R, [tb])
            P.ts("dve", tb[:, 1, :], bre[:, j, :], qim[:, j:j + 1], None, ALU.mult, None, [sbf] + R, [tb])
            P.stt(tb[:, 1, :], bim[:, j, :], qre[:, j:j + 1], tb[:, 1, :], ALU.mult, ALU.add, [sbf, tb] + R, [tb])
            P.memset("pool", bpad[:], 0.0, [bpad])
            for gh in range(2):
                col0 = 32 * (j % 4) + gh * 16
                for ri in range(2):
                    P.copy("pool", bpad[gh * 64:(gh + 1) * 64, ri, col0:col0 + 16],
                           tb[gh * 64:(gh + 1) * 64, ri, :], [tb], [bpad])
            for ri in range(2):
                P.mm(ps_misc[:, ri * 128:(ri + 1) * 128], bpad[:, ri, :], ident_b, True, True, [bpad, cstb], [ps_misc])
            hf = (j % 4) // 2
            for ri in range(2):
                P.copy("act", Btv[64 * hf:64 * hf + 64, ri, j // 4, j % 2, :],
                       ps_misc[64 * hf:64 * hf + 64, ri * 128:(ri + 1) * 128], [ps_misc], [Bt])

        scf = P.sb([128, 2048], F32, "s5c_in%d" % l)
        P.dma(scf[:], s5c_d[l], writes=[scf])
        Ct = s5C[l]
        P.copy("act", Ct[:, 0:1024], scf[:, 0:1024], [scf], [Ct])
        P.ts("dve", Ct[:, 1024:2048], scf[:, 1024:2048], -1.0, None, ALU.mult, None, [scf], [Ct])

        tab = P.sb([128, 2, 16, TS5], F32, "s5tabb%d" % l)
        P.memset("pool", tab[:, 0, :, 0:1], 1.0, [tab])
        P.memset("pool", tab[:, 1, :, 0:1], 0.0, [tab])
        ec = P.sb([128, 2, 16], F32, "s5ec%d" % l)
        P.copy("dve", ec[:, 0, :], cc, R, [ec])
        P.copy("dve", ec[:, 1, :], ss, R, [ec])
        m = 1
        while m < TS5:
            for j in range(16):
                cj = ec[:, 0, j:j + 1]
                sj = ec[:, 1, j:j + 1]
                src_c = tab[:, 0, j, 0:m]
                src_s = tab[:, 1, j, 0:m]
                dst_c = tab[:, 0, j, m:2 * m]
                dst_s = tab[:, 1, j, m:2 * m]
                P.ts("dve", dst_c, src_s, sj, None, ALU.mult, None, [tab, ec], [tab])
                P.stt(dst_c, src_c, cj, dst_c, ALU.mult, ALU.subtract, [tab, ec], [tab])
                P.ts("dve", dst_s, src_c, sj, None, ALU.mult, None, [tab, ec], [tab])
                P.stt(dst_s, src_s, cj, dst_s, ALU.mult, ALU.add, [tab, ec], [tab])
            e_c = ec[:, 0, :]
            e_s = ec[:, 1, :]
            P.tt("dve", t1, e_c, e_c, ALU.mult, [ec] + R, R)
            P.tt("dve", t2, e_s, e_s, ALU.mult, [ec] + R, R)
            P.stt(e_s, e_c, 2.0, e_s, ALU.mult, ALU.mult, [ec], [ec])
            P.tt("dve", e_c, t1, t2, ALU.subtract, R, [ec])
            m *= 2
        rot = s5rotb[l]
        P.copy("dve", rot[:], ec[:], [ec], [rot])
        s5rot.append((rot, keep))
        P.dma(s5tab_d[l], tab[:, :, :, :].rearrange("p a j t -> p (a j t)"), reads=[tab], writes=[tabsb])
        dbg_dump("s5w%d" % l, w, w[:, :, :].rearrange("p a b -> p (a b)"), [128, 256])
        dbg_dump("s5keep%d" % l, keep, keep[:, :, :].rearrange("p a b -> p (a b)"), [128, 48])
        dbg_dump("s5tab%d" % l, tab, tab[:, :, :, :].rearrange("p a j t -> p (a j t)"), [128, 2 * 16 * TS5])
        dbg_dump("s5B%d" % l, Bt, Bt[:, :], [128, 2048], BF16)
    P.pop()

    hT = P.sb([128, 8, TT], F32, "hT")
    xn = P.sb([128, 8, TT], BF16, "xn")
    zst = [P.sb([128, 1 + TT], F32, "zst%d" % i) for i in range(2)]
    zs = P.sb([128, 13, TT], F32, "zs")
    zsv = P.views(zs, 13)
    zu = P.sb([128, 4, TT], F32, "zu")
    zx = P.sb([128, 4, 3 + TT], F32, "zx")
    sgate = P.sb([128, 12, TT], BF16, "sgate")
    sgv = P.views(sgate, 12)
    ycat = P.sb([128, 12, TT], BF16, "ycat")
    ycv = P.views(ycat, 12)
    pbf = P.sb([128, 2, TT], BF16, "pbf")
    ring = [P.sb([128, 4096], BF16, "ring%d" % i) for i in range(3)]
    ringi = [0]
    s5tab = P.sb([128, 2, 16, TS5], F32, "s5tab")
    banks = [P.ps("ps%d" % i) for i in range(7)] + [ps_misc]
    rot_i = {"proj": 0, "rw": 0}

    def bank(group):
        ids = (0, 1) if group == "proj" else (2, 3)
        i = rot_i[group]
        rot_i[group] = (i + 1) % len(ids)
        return banks[ids[i]]

    def next_ring():
        r = ring[ringi[0]]
        ringi[0] = (ringi[0] + 1) % 3
        return r

    cz = [P.sb([128, 13], F32, "cz%d" % l) for l in range(2)]
    cl = [P.sb([128, 4, 3], F32, "cl%d" % l) for l in range(2)]
    ch = [P.sb([128, 4], F32, "ch%d" % l) for l in range(2)]
    s5z = [P.sb([128, 2, 16], F32, "s5z%d" % l) for l in range(2)]
    s5zv = [P.views(s5z[l], 16) for l in range(2)]
    Tst = [[P.sb([128, 128], BF16, "T%d_%d" % (l, pb)) for pb in range(4)] for l in range(2)]
    for l in range(2):
        P.memset("pool", cz[l][:], 0.0, [cz[l]])
        P.memset("pool", cl[l][:], 0.0, [cl[l]])
        P.memset("pool", ch[l][:], 0.0, [ch[l]])
        P.memset("pool", s5z[l][:], 0.0, s5zv[l])
        for pb in range(4):
            P.memset("pool", Tst[l][pb][:], 0.0, [Tst[l][pb]])

    NF = 17
    fs = [P.sb([128, TT], F32, "rf%d" % i) for i in range(NF)]
    pad_names = ["RTp", "KTp", "CTp", "BTp", "VTp", "KGp", "BGp"]
    pads = {n: P.sb([128, NCH * 128], BF16, n) for n in pad_names}
    for n in pad_names:
        P.memset("pool", pads[n][:], 0.0, [pads[n]])
    RTc = P.sb([128, TT], BF16, "RTc")
    tanh_wd = P.sb([128, TT], BF16, "tanhwd")
    Blev_i = [[P.sb([128, NCH * 128], BF16, "Blev%d_%d" % (i, k)) for i in range(2)] for k in range(2)]
    BTlev_i = [[P.sb([128, NCH * 128], BF16, "BTlev%d_%d" % (i, k)) for i in range(2)] for k in range(2)]
    AkkT_i = [P.sb([128, NCH * 128], BF16, "AkkT_%d" % k) for k in range(2)]
    Xbf_i = [P.sb([128, NCH * 256], BF16, "Xbf_%d" % k) for k in range(2)]
    NPI = 2
    pp = []
    for i in range(NPI):
        d = {}
        for n in ("Vbd", "KGbd", "BGbd", "PT", "nU0", "Wbd"):
            d[n] = P.sb([128, NCH * 128], BF16, "%s_%d" % (n, i))
        for n in ("Rhat", "ArkT", "ArbT"):
            d[n] = P.sb([128, TT], BF16, "%s_%d" % (n, i))
        d["bonus"] = P.sb([128, TT], F32, "bonus_%d" % i)
        d["GL"] = P.sb([128, NCH], F32, "GL_%d" % i)
        d["rt32"] = P.sb([128, TT], F32, "rt32_%d" % i)
        pp.append(d)
    mix = P.sb([128, 4, TT], F32, "mix")
    mixv = P.views(mix, 4)

    NS5SET = 2
    s5f = [[P.sb([128, TS5], F32, "s5f%d_%d" % (k, i)) for i in range(8)] for k in range(NS5SET)]
    NS5X = 3
    s5x = [P.sb([128, 2, TS5], BF16, "s5x%d" % k) for k in range(NS5X)]
    s5zl = [P.sb([128, 2], F32, "s5zl%d" % k) for k in range(NS5SET)]
    spf = [P.sb([128, TT], F32, "spf%d" % i) for i in range(3)]
    lf = [P.sb([128, TT], F32, "lf%d" % i) for i in range(4)]
    ubf = P.sb([128, 4, TT], BF16, "ubf")
    lb = P.sb([128, TT], BF16, "lb")
    rstd = P.sb([128, TT], F32, "rstd")
    sq = P.sb([128, 4, TT], BF16, "sq")
    ones_b = P.sb([128, 128], BF16, "ones_b")
    P.memset("pool", ones_b[:], 1.0, [ones_b])

    out_toks = []
    eps_t = {}
    for e_ in (NORM_EPS, GN_EPS):
        t = P.sb([128, 1], F32, "eps%d" % len(eps_t))
        P.memset("pool", t[:], e_, [t])
        eps_t[e_] = t
    neg_half = -math.exp(-0.5)

    def rms_rstd(src_bufs, src_ap_fn, nblk, eps):
        b = bank("proj")
        for k in range(nblk):
            P.act(sq[:, k % 4, :], src_ap_fn(k), AF.Square, src_bufs, [sq])
            P.mm(b[:, 0:TT], ones_b[:], sq[:, k % 4, :], k == 0, k == nblk - 1, [ones_b, sq], [b])
        P.act(rstd[:], b[:, 0:TT], AF.Ln, [b, eps_t[eps]], [rstd], bias=eps_t[eps][:, 0:1], scale=1.0 / (nblk * 128))
        P.act(rstd[:], rstd[:], AF.Exp, [rstd], [rstd], scale=-0.5)

    def drive(items):
        active = list(items)
        while active:
            for item in list(active):
                g, w = item
                for _ in range(w):
                    try:
                        next(g)
                    except StopIteration:
                        active.remove(item)
                        break

    s5ctr = [0]

    for it in range(NT):
        t0 = it * TT
        first = (it == 0)
        P.dma(hT[:], xT[:, t0:t0 + TT].rearrange("(k p) t -> p k t", p=128), writes=[hT])
        for l in range(2):
            dbg_on = first and l == 0
            P.dma(pbf[:], pT[l, :, t0:t0 + TT].rearrange("(k p) t -> p k t", p=128), writes=[pbf], eng="pool")
            P.dma(s5tab[:, :, :, :].rearrange("p a j t -> p (a j t)"), s5tab_d[l], reads=[tabsb], writes=[s5tab])
            rms_rstd([hT], lambda k: hT[:, k, :], 8, NORM_EPS)
            for k in range(8):
                P.stt(xn[:, k, :], hT[:, k, :], V(l, "ng", k), rstd[:], ALU.mult, ALU.mult, [hT, vec, rstd], [xn])
            if dbg_on:
                dbg_dump("xn", xn, xn[:, :, :].rearrange("p a t -> p (a t)"), [128, 8 * TT], BF16)

            wchunk = {}

            def in_block(cb, l=l, wchunk=wchunk):
                ci = cb // 4
                if ci not in wchunk:
                    r = next_ring()
                    ncol = 512 if ci < 8 else 128
                    P.dma(r[:, 0:8 * ncol].rearrange("p (k n) -> p k n", k=8),
                          w_in_b[l].rearrange("(k p) n -> p k n", p=128)[:, :, ci * 512:ci * 512 + ncol],
                          reads=[wsb], writes=[r])
                    wchunk[ci] = (r, ncol)
                r, ncol = wchunk[ci]
                rv = r[:, 0:8 * ncol].rearrange("p (k n) -> p k n", k=8)
                c0 = (cb % 4) * 128
                b = bank("proj")
                for k in range(8):
                    P.mm(b[:, 0:TT], rv[:, k, c0:c0 + 128], xn[:, k, :], k == 0, k == 7, [r, xn], [b])
                return b

            for cb in range(13):
                st = zst[cb % 2]
                b = in_block(cb)
                P.copy("act", st[:, 0:1], cz[l][:, cb:cb + 1], [cz[l]], [st])
                P.copy("act", st[:, 1:1 + TT], b[:, 0:TT], [b], [st])
                P.copy("act", cz[l][:, cb:cb + 1], st[:, TT:TT + 1], [st], [cz[l]])
                d_ = lf[cb % 2]
                P.tt("pool", d_[:], st[:, 0:TT], st[:, 1:1 + TT], ALU.subtract, [st], [d_])
                P.stt(zs[:, cb, :], d_[:], V(l, "mu", cb), st[:, 1:1 + TT], ALU.mult, ALU.add, [d_, vec, st], [zsv[cb]])
            if dbg_on:
                dbg_dump("zs", zs, zs[:, :, :].rearrange("p a t -> p (a t)"), [128, 13 * TT])
            P.act(tanh_wd[0:64, :], zs[0:64, 12, :], AF.Tanh, [zsv[12]], [tanh_wd])
            P.copy("act", tanh_wd[64:128, :], zs[64:128, 12, :], [zsv[12]], [tanh_wd])
            for blk in range(4):
                b = in_block(13 + blk)
                P.act(sgate[:, blk, :], b[:, 0:TT], AF.Silu, [b], [sgv[blk]])
            for blk in range(4):
                b = in_block(17 + blk)
                P.copy("act", zu[:, blk, :], b[:, 0:TT], [b], [zu])
            P.copy("pool", ubf[:], zu[:], [zu], [ubf])
            for blk in range(4):
                b = in_block(21 + blk)
                P.act(sgate[:, 4 + blk, :], b[:, 0:TT], AF.Silu, [b], [sgv[4 + blk]])
            P.copy("act", zx[:, :, 0:3], cl[l][:, :, :], [cl[l]], [zx])
            for blk in range(4):
                b = in_block(25 + blk)
                P.copy("act", zx[:, blk, 3:3 + TT], b[:, 0:TT], [b], [zx])
            P.copy("act", cl[l][:, :, :], zx[:, :, TT:TT + 3], [zx], [cl[l]])
            for blk in range(4):
                b = in_block(29 + blk)
                P.act(sgate[:, 8 + blk, :], b[:, 0:TT], AF.Silu, [b], [sgv[8 + blk]])

            def prep(pb, inst, l=l, dbg_on=dbg_on):
                d = pp[inst]
                Blev, BTlev, AkkT = Blev_i[inst], BTlev_i[inst], AkkT_i[inst]
                r_ = zs[:, pb, :]
                k_ = zs[:, 4 + pb, :]
                v_ = zs[:, 8 + pb, :]
                zr_, zk_, zv_ = zsv[pb], zsv[4 + pb], zsv[8 + pb]
                (sg, ld, a_, kk_, kk2, sqk, kap, t1, kp, b_, lg, eg, ieg, eg1, dl, egl, rk) = fs[:17]
                rt32 = d["rt32"]
                cols = slice(pb * 128, (pb + 1) * 128)
                bw = bank("rw")
                P.mm(bw[:, 0:TT], w2a2[l][0:64, cols], tanh_wd[0:64, :], True, True, [w2a2[l], tanh_wd], [bw])
                P.act(sg[:], bw[:, 0:TT], AF.Sigmoid, [bw, vec], [sg], bias=V(l, "w0", pb))
                P.ts("dve", ld[:], sg[:], neg_half, None, ALU.mult, None, [sg], [ld])
                ba_ = bank("rw")
                P.mm(ba_[:, 0:TT], w2a2[l][64:128, cols], tanh_wd[64:128, :], True, True, [w2a2[l], tanh_wd], [ba_])
                P.act(a_[:], ba_[:, 0:TT], AF.Sigmoid, [ba_, vec], [a_], bias=V(l, "a0", pb))
                yield
                P.scan(lg[:], cst[:, CST_SCAN:CST_SCAN + TT], ld[:], 0.0, [cst, ld], [lg])
                P.ts("dve", kk_[:], k_, V(l, "kk", pb), None, ALU.mult, None, [zk_, vec], [kk_])
                P.tt("pool", kk2[:], kk_[:], kk_[:], ALU.mult, [kk_], [kk2])
                P.act(eg[:], lg[:], AF.Exp, [lg], [eg])
                yield
                bs = bank("rw")
                P.mm(bs[:, 0:TT], onesbd_f, kk2[:], True, True, [cst, kk2], [bs])
                P.act(sqk[:], bs[:, 0:TT], AF.Sqrt, [bs], [sqk])
                P.act(ieg[:], lg[:], AF.Exp, [lg], [ieg], scale=-1.0)
                P.tt("pool", eg1[:], lg[:], ld[:], ALU.subtract, [lg, ld], [eg1])
                P.act(eg1[:], eg1[:], AF.Exp, [eg1], [eg1])
                P.ts("dve", sqk[:], sqk[:], 1e-12, None, ALU.max, None, [sqk], [sqk])
                P.recip(sqk[:], sqk[:], [sqk], [sqk])
                P.tt("pool", kap[:], kk_[:], sqk[:], ALU.mult, [kk_, sqk], [kap])
                yield
                P.ts("dve", t1[:], a_[:], -1.0, V(l, "ka", pb), ALU.add, ALU.mult, [a_, vec], [t1])
                P.stt(kp[:], t1[:], 1.0, k_, ALU.add, ALU.mult, [t1, zk_], [kp])
                P.tt("pool", b_[:], kap[:], a_[:], ALU.mult, [kap, a_], [b_])
                lg3 = lg[:, :].rearrange("p (c t) -> p c t", t=LCH)
                P.tt("dve", dl[:, :].rearrange("p (c t) -> p c t", t=LCH), lg3[:, :, LCH - 1:LCH].to_broadcast([128, NCH, LCH]),
                     lg3, ALU.subtract, [lg], [dl])
                P.act(egl[:], dl[:], AF.Exp, [dl], [egl])
                P.copy("act", d["GL"][:, :], eg[:, :].rearrange("p (c t) -> p c t", t=LCH)[:, :, LCH - 1], [eg], [d["GL"]])
                yield
                P.tt("dve", rt32[:], r_, eg[:], ALU.mult, [zr_, eg], [rt32])
                P.copy("act", RTc[:], rt32[:], [rt32], [RTc])

                def padw(name, eng, in0, in1, rd):
                    t = pads[name]
                    tv = t[:, :].rearrange("p (c h t) -> p c h t", c=NCH, h=2)
                    for hh in range(2):
                        ps_ = slice(hh * 64, (hh + 1) * 64)
                        o = tv[ps_, :, hh, :]
                        i0 = in0[ps_, :].rearrange("p (c t) -> p c t", t=LCH)
                        if in1 is None:
                            P.copy(eng, o, i0, rd, [t])
                        else:
                            i1 = in1[ps_, :].rearrange("p (c t) -> p c t", t=LCH)
                            P.tt(eng, o, i0, i1, ALU.mult, rd, [t])
                padw("RTp", "act", rt32, None, [rt32])
                padw("KTp", "dve", kp, ieg, [kp, ieg])
                padw("CTp", "pool", kap, eg1, [kap, eg1])
                yield
                padw("BTp", "dve", b_, ieg, [b_, ieg])
                padw("VTp", "act", zs[:, 8 + pb, :], None, [zv_])
                padw("KGp", "pool", kp, egl, [kp, egl])
                padw("BGp", "dve", b_, egl, [b_, egl])
                yield "pre_pp"
                P.stt(rk[:], r_, V(l, "rk", pb), kp[:], ALU.mult, ALU.mult, [zr_, vec, kp], [rk])
                yield
                bb = bank("rw")
                P.mm(bb[:, 0:TT], onesbd_f, rk[:], True, True, [cst, rk], [bb])
                P.tt("dve", d["bonus"][:], bb[:, 0:TT], v_, ALU.mult, [bb, zv_], [d["bonus"]])
                if dbg_on and pb == 0:
                    dbg_dump("lg", lg, lg[:], [128, TT])
                    dbg_dump("kap", kap, kap[:], [128, TT])
                    dbg_dump("kp", kp, kp[:], [128, TT])
                    dbg_dump("a", a_, a_[:], [128, TT])
                yield

                def chunkmm(dst_bank, lname, rname, rbuf=None, rcols=128):
                    lt = pads[lname]
                    for c in range(NCH):
                        if rbuf is None:
                            rb_ = pads[rname]
                            rap = rb_[:, c * 128:(c + 1) * 128]
                        else:
                            rb_ = rbuf
                            rap = rbuf[:, c * rcols:(c + 1) * rcols]
                        P.mm(dst_bank[:, c * rcols:(c + 1) * rcols], lt[:, c * 128:(c + 1) * 128], rap, True, True,
                             [lt, rb_], [dst_bank])

                def masked(dst, src_bank, mcol, w):
                    n = NCH * w
                    P.tt("dve", dst[:, 0:n], src_bank[:, 0:n], mskb[:, mcol:mcol + n], ALU.mult, [src_bank, mskb], [dst])
                b1 = bank("rw")
                chunkmm(b1, "BTp", "CTp")
                masked(BTlev[0], b1, MSK_USN, 128)
                b2 = bank("rw")
                chunkmm(b2, "CTp", "BTp")
                masked(Blev[0], b2, MSK_LSN, 128)
                yield
                b3 = bank("rw")
                chunkmm(b3, "KTp", "CTp")
                masked(AkkT, b3, MSK_USP, 128)
                b4 = bank("rw")
                chunkmm(b4, "KTp", None, RTc, LCH)
                masked(d["ArkT"], b4, MSK_CI, LCH)
                b5 = bank("rw")
                chunkmm(b5, "BTp", None, RTc, LCH)
                masked(d["ArbT"], b5, MSK_CI, LCH)
                yield

            def tokmajor(src_name, dst_buf, dst_ap, eng):
                bt_ = bank("rw")
                lt = pads[src_name]
                for c in range(NCH):
                    P.mm(bt_[:, c * 128:(c + 1) * 128], lt[:, c * 128:(c + 1) * 128], ident_b, True, True,
                         [lt, cstb], [bt_])
                P.copy(eng, dst_ap, bt_[:, 0:NCH * 128] if len(dst_ap.shape) == 2 else
                       bt_[:, 0:NCH * 128].rearrange("p (c n) -> p c n", n=128), [bt_], [dst_buf])

            def solve(pb, inst, l=l):
                d = pp[inst]
                Blev, BTlev, AkkT, Xbf = Blev_i[inst], BTlev_i[inst], AkkT_i[inst], Xbf_i[inst]
                Xbfv = Xbf[:, :].rearrange("p (c n) -> p c n", n=256)
                tokmajor("VTp", d["Vbd"], d["Vbd"][:, :], "act")
                tokmajor("KGp", d["KGbd"], d["KGbd"][:, :], "act")
                yield
                tokmajor("BGp", d["BGbd"], d["BGbd"][:, :], "act")
                tokmajor("CTp", Xbf, Xbfv[:, :, 0:128], "act")
                bt_ = bank("rw")
                for c in range(NCH):
                    cs = slice(c * 128, (c + 1) * 128)
                    P.mm(bt_[:, cs], AkkT[:, cs], d["Vbd"][:, cs], True, True, [AkkT, d["Vbd"]], [bt_])
                P.copy("act", Xbfv[:, :, 128:256], bt_[:, 0:NCH * 128].rearrange("p (c n) -> p c n", n=128), [bt_], [Xbf])
                yield "tok_done"
                cur = 0
                NLEV = 6
                for lev in range(NLEV):
                    if lev < NLEV - 1:
                        nxt = 1 - cur
                        bq = bank("rw")
                        for c in range(NCH):
                            cs = slice(c * 128, (c + 1) * 128)
                            P.mm(bq[:, cs], Blev[cur][:, cs], BTlev[cur][:, cs], True, True, [Blev[cur], BTlev[cur]], [bq])
                        if lev < NLEV - 2:
                            bq2 = bank("rw")
                            for c in range(NCH):
                                cs = slice(c * 128, (c + 1) * 128)
                                P.mm(bq2[:, cs], BTlev[cur][:, cs], Blev[cur][:, cs], True, True,
                                     [Blev[cur], BTlev[cur]], [bq2])
                    for half in range(2):
                        bx_ = banks[4 + half]
                        for cc_ in range(2):
                            c = half * 2 + cc_
                            P.mm(bx_[:, cc_ * 256:(cc_ + 1) * 256], BTlev[cur][:, c * 128:(c + 1) * 128], Xbfv[:, c, :],
                                 True, False, [BTlev[cur], Xbf], [bx_])
                            P.mm(bx_[:, cc_ * 256:(cc_ + 1) * 256], ident_b, Xbfv[:, c, :],
                                 False, True, [cstb, Xbf], [bx_])
                    if lev < NLEV - 1:
                        P.copy("act", BTlev[nxt][:], bq[:, 0:512], [bq], [BTlev[nxt]])
                        if lev < NLEV - 2:
                            P.copy("dve", Blev[nxt][:], bq2[:, 0:512], [bq2], [Blev[nxt]])
                    P.copy("act", Xbf[:, 0:512], banks[4][:, 0:512], [banks[4]], [Xbf])
                    P.copy("dve", Xbf[:, 512:1024], banks[5][:, 0:512], [banks[5]], [Xbf])
                    if lev < NLEV - 1:
                        cur = nxt
                    yield
                nU0v = d["nU0"][:, :].rearrange("p (c n) -> p c n", n=128)
                Wbdv = d["Wbd"][:, :].rearrange("p (c n) -> p c n", n=128)
                P.ts("dve", nU0v, Xbfv[:, :, 128:256], -1.0, None, ALU.mult, None, [Xbf], [d["nU0"]])
                P.copy("pool", Wbdv, Xbfv[:, :, 0:128], [Xbf], [d["Wbd"]])
                yield
                br = bank("rw")
                for c in range(NCH):
                    P.mm(br[:, c * LCH:(c + 1) * LCH], d["Wbd"][:, c * 128:(c + 1) * 128],
                         d["ArbT"][:, c * LCH:(c + 1) * LCH], True, True, [d["Wbd"], d["ArbT"]], [br])
                P.tt("dve", d["Rhat"][:], d["rt32"][:], br[:, 0:TT], ALU.subtract, [d["rt32"], br], [d["Rhat"]])
                bp = bank("rw")
                for c in range(NCH):
                    cs = slice(c * 128, (c + 1) * 128)
                    P.mm(bp[:, cs], d["Wbd"][:, cs], d["BGbd"][:, cs], True, True, [d["Wbd"], d["BGbd"]], [bp])
                for c in range(NCH):
                    cs = slice(c * 128, (c + 1) * 128)
                    P.stt(d["PT"][:, cs], ident_f, d["GL"][:, c:c + 1], bp[:, cs], ALU.mult, ALU.subtract,
                          [cst, d["GL"], bp], [d["PT"]])
                yield

            def seq(pb, inst, c, l=l):
                d = pp[inst]
                T = Tst[l][pb]
                yb = banks[6]
                tb_ = banks[4 + inst]
                ycols = slice(inst * TT + c * LCH, inst * TT + (c + 1) * LCH)
                cs = slice(c * 128, (c + 1) * 128)
                cl_ = slice(c * LCH, (c + 1) * LCH)
                P.mm(yb[:, ycols], T[:], d["Rhat"][:, cl_], True, False, [T, d["Rhat"]], [yb])
                P.mm(yb[:, ycols], d["Vbd"][:, cs], d["ArkT"][:, cl_], False, False, [d["Vbd"], d["ArkT"]], [yb])
                P.mm(yb[:, ycols], d["nU0"][:, cs], d["ArbT"][:, cl_], False, True, [d["nU0"], d["ArbT"]], [yb])
                P.mm(tb_[:, 0:128], d["PT"][:, cs], T[:], True, False, [d["PT"], T], [tb_])
                P.mm(tb_[:, 0:128], d["KGbd"][:, cs], d["Vbd"][:, cs], False, False, [d["KGbd"], d["Vbd"]], [tb_])
                P.mm(tb_[:, 0:128], d["BGbd"][:, cs], d["nU0"][:, cs], False, True, [d["BGbd"], d["nU0"]], [tb_])
                P.copy("act", T[:], tb_[:, 0:128], [tb_], [T])

            def fin(pb, inst, l=l, dbg_on=dbg_on):
                assert lru_done[0], "fin emitted before the LRU chain finished (scratch lf[3] still live)"
                d = pp[inst]
                yb = banks[6]
                y32, yc, ysq, rs = spf[0], spf[1], spf[2], lf[3]
                P.copy("act", y32[:], yb[:, inst * TT:(inst + 1) * TT], [yb], [y32])
                if dbg_on:
                    dbg_dump("y_rw%d" % pb, y32, y32[:], [128, TT])
                bm = bank("rw")
                P.mm(bm[:, 0:TT], onesbd_f, y32[:], True, True, [cst, y32], [bm])
                P.stt(yc[:], bm[:, 0:TT], -1.0 / 64, y32[:], ALU.mult, ALU.add, [bm, y32], [yc])
                P.act(ysq[:], yc[:], AF.Square, [yc], [ysq])
                bv = bank("rw")
                P.mm(bv[:, 0:TT], onesbd_f, ysq[:], True, True, [cst, ysq], [bv])
                P.act(rs[:], bv[:, 0:TT], AF.Ln, [bv, eps_t[GN_EPS]], [rs], bias=eps_t[GN_EPS][:, 0:1], scale=1.0 / 64)
                P.act(rs[:], rs[:], AF.Exp, [rs], [rs], scale=-0.5)
                P.tt("dve", yc[:], yc[:], rs[:], ALU.mult, [yc, rs], [yc])
                P.ts("dve", yc[:], yc[:], V(l, "lnw", pb), V(l, "lnb", pb), ALU.mult, ALU.add, [yc, vec], [yc])
                P.tt("pool", yc[:], yc[:], d["bonus"][:], ALU.add, [yc, d["bonus"]], [yc])
                P.tt("dve", ycat[:, pb, :], yc[:], sgate[:, pb, :], ALU.mult, [yc, sgv[pb]], [ycv[pb]])

            def rwkv_gen():
                def chain(pb, inst):
                    yield from prep(pb, inst)
                    yield from solve(pb, inst)

                def tail(pbs):
                    for c in range(NCH):
                        for inst, pb in enumerate(pbs):
                            seq(pb, inst, c)
                            yield
                    for inst, pb in enumerate(pbs):
                        fin(pb, inst)
                        yield

                def merge(ga, gb):
                    da = db = False
                    while not (da and db):
                        if not da:
                            try:
                                next(ga)
                                yield
                            except StopIteration:
                                da = True
                        if not db:
                            try:
                                next(gb)
                                yield
                            except StopIteration:
                                db = True

                def half_front(pbs):
                    gA, gB = chain(pbs[0], 0), chain(pbs[1], 1)
                    for v in gA:
                        yield
                        if v == "tok_done":
                            break
                    yield from merge(gA, gB)

                yield from half_front((0, 1))
                t0_ = tail((0, 1))
                gA, gB = chain(2, 0), chain(3, 1)

                def front2a():
                    for v in gA:
                        yield
                        if v == "pre_pp":
                            break
                yield from merge(t0_, front2a())
                for v in gA:
                    yield
                    if v == "tok_done":
                        break
                yield from merge(gA, gB)
                yield from tail((2, 3))

            def s5_gen(l=l, dbg_on=dbg_on):
                Bv = s5B[l][:, :].rearrange("p (r b q m) -> p r b q m", r=2, b=4, q=2)
                Cv = s5C[l][:, :].rearrange("p (r j m) -> p r j m", r=2, j=16)
                rotk, keep = s5rot[l]
                NS = TT // TS5
                yb5 = banks[7]

                def stage0(blk, s, jj, k):
                    hf, jh = jj // 2, jj % 2
                    hs = slice(64 * hf, 64 * hf + 64)
                    tsl = slice(s * TS5, (s + 1) * TS5)
                    bu = banks[k % 2]
                    c0 = 0
                    bre = bu[:, c0:c0 + TS5]
                    bim = bu[:, c0 + TS5:c0 + 2 * TS5]
                    P.mm(bre, Bv[hs, 0, blk, jh, :], ubf[hs, blk, tsl], True, True, [s5B[l], ubf], [bu])
                    P.mm(bim, Bv[hs, 1, blk, jh, :], ubf[hs, blk, tsl], True, True, [s5B[l], ubf], [bu])

                def stage1(blk, s, jj, k):
                    j = blk * 4 + jj
                    (t1, t2, bzr, bzi, zr, zi, t3, t4) = s5f[k]
                    bu = banks[k % 2]
                    c0 = 0
                    bre = bu[:, c0:c0 + TS5]
                    bim = bu[:, c0 + TS5:c0 + 2 * TS5]
                    cosT = s5tab[:, 0, j, :]
                    sinT = s5tab[:, 1, j, :]
                    P.tt("dve", t1[:], bre, cosT, ALU.mult, [bu, s5tab], [t1])
                    P.tt("dve", t2[:], bim, sinT, ALU.mult, [bu, s5tab], [t2])
                    P.tt("dve", t3[:], bim, cosT, ALU.mult, [bu, s5tab], [t3])
                    P.tt("dve", t4[:], bre, sinT, ALU.mult, [bu, s5tab], [t4])
                    P.tt(S5E, bzr[:], t1[:], t2[:], ALU.add, [t1, t2], [bzr])
                    P.tt(S5E, bzi[:], t3[:], t4[:], ALU.subtract, [t3, t4], [bzi])

                def stage2(blk, s, jj, k, kx):
                    j = blk * 4 + jj
                    hf, jh = jj // 2, jj % 2
                    hs = slice(64 * hf, 64 * hf + 64)
                    tsl = slice(s * TS5, (s + 1) * TS5)
                    (t1, t2, bzr, bzi, zr, zi, t3, t4) = s5f[k]
                    sx = s5x[kx]
                    zl = s5zl[k]
                    zv = s5zv[l][j]
                    cosT = s5tab[:, 0, j, :]
                    sinT = s5tab[:, 1, j, :]
                    rho_b = keep[:, 0, j:j + 1].to_broadcast([128, TS5])
                    P.scan(zr[:], rho_b, bzr[:], s5z[l][:, 0, j:j + 1], [keep, bzr, zv], [zr])
                    P.scan(zi[:], rho_b, bzi[:], s5z[l][:, 1, j:j + 1], [keep, bzi, zv], [zi])
                    P.tt("dve", t1[:], zr[:], cosT, ALU.mult, [zr, s5tab], [t1])
                    P.tt("dve", t2[:], zi[:], sinT, ALU.mult, [zi, s5tab], [t2])
                    P.tt(S5E, sx[:, 0, :], t1[:], t2[:], ALU.subtract, [t1, t2], [sx])
                    P.tt("dve", t3[:], zr[:], sinT, ALU.mult, [zr, s5tab], [t3])
                    P.tt(S5E, t4[:], zi[:], cosT, ALU.mult, [zi, s5tab], [t4])
                    P.tt(S5E, sx[:, 1, :], t3[:], t4[:], ALU.add, [t3, t4], [sx])
                    rc = rotk[:, 0, j:j + 1]
                    rs_ = rotk[:, 1, j:j + 1]
                    zlr = zr[:, TS5 - 1:TS5]
                    zli = zi[:, TS5 - 1:TS5]
                    P.ts("dve", zl[:, 0:1], zli, rs_, None, ALU.mult, None, [zi, rotk], [zl])
                    P.ts("dve", zl[:, 1:2], zlr, rs_, None, ALU.mult, None, [zr, rotk], [zl])
                    P.stt(s5z[l][:, 0, j:j + 1], zlr, rc, zl[:, 0:1], ALU.mult, ALU.subtract, [zr, rotk, zl], [zv])
                    P.stt(s5z[l][:, 1, j:j + 1], zli, rc, zl[:, 1:2], ALU.mult, ALU.add, [zi, rotk, zl], [zv])

                def stage3(blk, s, jj, kx):
                    j = blk * 4 + jj
                    hf, jh = jj // 2, jj % 2
                    hs = slice(64 * hf, 64 * hf + 64)
                    tsl = slice(s * TS5, (s + 1) * TS5)
                    sx = s5x[kx]
                    P.mm(yb5[hs, tsl], Cv[:, 0, j, :], sx[:, 0, :], jh == 0, False, [s5C[l], sx], [yb5])
                    P.mm(yb5[hs, tsl], Cv[:, 1, j, :], sx[:, 1, :], False, jh == 1, [s5C[l], sx], [yb5])

                for blk in range(4):
                    units = [(s, jj) for s in range(NS) for jj in range(4)]
                    ks = []
                    for (s, jj) in units:
                        ks.append(s5ctr[0] % NS5SET)
                        s5ctr[0] += 1
                    nU = len(units)
                    stage0(blk, units[0][0], units[0][1], ks[0])
                    stage0(blk, units[1][0], units[1][1], ks[1])
                    stage1(blk, units[0][0], units[0][1], ks[0])
                    yield
                    for u in range(nU):
                        if u + 1 < nU:
                            stage1(blk, units[u + 1][0], units[u + 1][1], ks[u + 1])
                            if u + 2 < nU:
                                stage0(blk, units[u + 2][0], units[u + 2][1], ks[u + 2])
                            yield
                        stage2(blk, units[u][0], units[u][1], ks[u], u % NS5X)
                        if u >= 2:
                            stage3(blk, units[u - 2][0], units[u - 2][1], (u - 2) % NS5X)
                        yield
                    stage3(blk, units[nU - 2][0], units[nU - 2][1], (nU - 2) % NS5X)
                    stage3(blk, units[nU - 1][0], units[nU - 1][1], (nU - 1) % NS5X)
                    ys, x2, q_ = spf
                    P.stt(ys[:], zu[:, blk, :], V(l, "s5d", blk), yb5[:, 0:TT], ALU.mult, ALU.add, [zu, vec, yb5], [ys])
                    if dbg_on:
                        dbg_dump("s5y%d" % blk, ys, ys[:], [128, TT])
                    P.act(x2[:], ys[:], AF.Square, [ys], [x2])
                    P.ts("dve", x2[:], x2[:], 0.044715, 1.0, ALU.mult, ALU.add, [x2], [x2])
                    P.tt("pool", q_[:], x2[:], ys[:], ALU.mult, [x2, ys], [q_])
                    P.act(x2[:], q_[:], AF.Sigmoid, [q_], [x2], scale=2.0 * math.sqrt(2.0 / math.pi))
                    P.tt("dve", mix[:, blk, :], ys[:], x2[:], ALU.mult, [ys, x2], [mixv[blk]])
                    yield
                zgb = ubf
                P.copy("pool", zgb[:], mix[:], mixv, [zgb])
                rg = next_ring()
                P.dma(rg[:, 0:2048].rearrange("p (k n) -> p k n", k=4), glu_w_b[l].rearrange("(k p) n -> p k n", p=128),
                      reads=[wsb], writes=[rg])
                rgv = rg[:, 0:2048].rearrange("p (k n) -> p k n", k=4)
                for ob in range(4):
                    b = banks[0]
                    for k in range(4):
                        P.mm(b[:, 0:TT], rgv[:, k, ob * 128:(ob + 1) * 128], zgb[:, k, :], k == 0, k == 3, [rg, zgb], [b])
                    sg_ = spf[ob % 2]
                    P.act(sg_[:], b[:, 0:TT], AF.Sigmoid, [b, vec], [sg_], bias=V(l, "glub", ob))
                    P.tt("pool", sg_[:], sg_[:], sgate[:, 4 + ob, :], ALU.mult, [sg_, sgv[4 + ob]], [sg_])
                    P.tt("dve", ycat[:, 4 + ob, :], mix[:, ob, :], sg_[:], ALU.mult, [mixv[ob], sg_], [ycv[4 + ob]])
                    yield

            lru_done = [("lru" in SKIP)]

            def lru_gen(l=l, dbg_on=dbg_on):
                for blk in range(4):
                    A, B, C, Dd = lf
                    bl = banks[6]
                    P.ts("dve", A[:], zx[:, blk, 0:TT], V(l, "cw", 0 * 4 + blk), V(l, "cb", blk), ALU.mult, ALU.add,
                         [zx, vec], [A])
                    for j in range(1, 4):
                        P.stt(A[:], zx[:, blk, j:j + TT], V(l, "cw", j * 4 + blk), A[:], ALU.mult, ALU.add,
                              [zx, vec, A], [A])
                    P.copy("act", lb[:], A[:], [A], [lb])
                    yield
                    P.mm(bl[:, 0:TT], lruw[l][:, blk * 128:(blk + 1) * 128], lb[:], True, True, [lruw[l], lb], [bl])
                    P.mm(bl[:, TT:2 * TT], lruw[l][:, (4 + blk) * 128:(5 + blk) * 128], lb[:], True, True,
                         [lruw[l], lb], [bl])
                    P.act(B[:], bl[:, 0:TT], AF.Sigmoid, [bl, vec], [B], bias=V(l, "ba", blk))
                    P.act(C[:], bl[:, TT:2 * TT], AF.Sigmoid, [bl, vec], [C], bias=V(l, "bx", blk))
                    yield
                    P.act(Dd[:], B[:], AF.Exp, [B, lru_c], [Dd], scale=lru_c[:, l, blk:blk + 1])
                    P.act(B[:], B[:], AF.Exp, [B, lru_c], [B], scale=lru_c[:, l, 4 + blk:5 + blk])
                    P.act(B[:], B[:], AF.Ln, [B, one_t], [B], bias=one_t[:, 0:1], scale=-1.0)
                    P.act(B[:], B[:], AF.Exp, [B], [B], scale=0.5)
                    P.tt("pool", C[:], C[:], A[:], ALU.mult, [C, A], [C])
                    P.tt("pool", C[:], C[:], B[:], ALU.mult, [C, B], [C])
                    yield
                    P.scan(A[:], Dd[:], C[:], ch[l][:, blk:blk + 1], [Dd, C, ch[l]], [A])
                    P.copy("act", ch[l][:, blk:blk + 1], A[:, TT - 1:TT], [A], [ch[l]])
                    if dbg_on:
                        dbg_dump("lru%d" % blk, A, A[:], [128, TT])
                    P.tt("pool", ycat[:, 8 + blk, :], A[:], sgate[:, 8 + blk, :], ALU.mult, [A, sgv[8 + blk]], [ycv[8 + blk]])
                    if blk == 3:
                        lru_done[0] = True
                    yield

            gens = []
            if "rwkv" not in SKIP:
                gens.append((rwkv_gen(), GW[0]))
            if "s5" not in SKIP:
                gens.append((s5_gen(), GW[1]))
            if "lru" not in SKIP:
                gens.append((lru_gen(), GW[2]))
            if "serial" in SKIP:
                for g, w in gens:
                    drive([(g, 1)])
            else:
                drive(gens)
            if dbg_on:
                dbg_dump("ycat_rw", ycat, ycat[:, 0:4, :].rearrange("p a t -> p (a t)"), [128, 4 * TT], BF16)

            for oc in range(4):
                r = next_ring()
                P.dma(r[:, 0:12 * 256].rearrange("p (k n) -> p k n", k=12),
                      w_out_b[l].rearrange("(k p) n -> p k n", p=128)[:, :, oc * 256:(oc + 1) * 256],
                      reads=[wsb], writes=[r])
                rv = r[:, 0:12 * 256].rearrange("p (k n) -> p k n", k=12)
                for ob2 in range(2):
                    ob = oc * 2 + ob2
                    b = bank("proj")
                    for k in range(12):
                        P.mm(b[:, 0:TT], rv[:, k, ob2 * 128:(ob2 + 1) * 128], ycat[:, k, :], k == 0, k == 11,
                             [r] + ycv, [b])
                    P.tt("dve", hT[:, ob, :], hT[:, ob, :], b[:, 0:TT], ALU.add, [hT, b], [hT])
            for k in range(8):
                P.copy("act" if k % 2 == 0 else "dve", xn[:, k, :], hT[:, k, :], [hT], [xn])
            r = next_ring()
            P.dma(r[:, 0:2048].rearrange("p (k n) -> p k n", k=2), ple_w_b[l].rearrange("(k p) n -> p k n", p=128),
                  reads=[wsb], writes=[r])
            rv = r[:, 0:2048].rearrange("p (k n) -> p k n", k=2)
            epre = zs
            for ob in range(8):
                b = bank("proj")
                for k in range(2):
                    P.mm(b[:, 0:TT], rv[:, k, ob * 128:(ob + 1) * 128], pbf[:, k, :], k == 0, k == 1, [r, pbf], [b])
                P.copy("act", epre[:, ob, :], b[:, 0:TT], [b], [zsv[ob]])
            rms_rstd(zsv[0:8], lambda k: epre[:, k, :], 8, NORM_EPS)
            for gc in range(2):
                r = next_ring()
                P.dma(r[:, 0:4096].rearrange("p (k n) -> p k n", k=8),
                      ple_gw_b[l].rearrange("(k p) n -> p k n", p=128)[:, :, gc * 512:(gc + 1) * 512],
                      reads=[wsb], writes=[r])
                rv = r[:, 0:4096].rearrange("p (k n) -> p k n", k=8)
                for ob2 in range(4):
                    ob = gc * 4 + ob2
                    b = bank("proj")
                    for k in range(8):
                        P.mm(b[:, 0:TT], rv[:, k, ob2 * 128:(ob2 + 1) * 128], xn[:, k, :], k == 0, k == 7, [r, xn], [b])
                    sg_, e_ = fs[6 + 2 * (ob % 2)], fs[7 + 2 * (ob % 2)]
                    P.act(sg_[:], b[:, 0:TT], AF.Sigmoid, [b], [sg_])
                    P.stt(e_[:], epre[:, ob, :], V(l, "png", ob), rstd[:], ALU.mult, ALU.mult, [zsv[ob], vec, rstd], [e_])
                    P.tt("dve", e_[:], e_[:], sg_[:], ALU.mult, [e_, sg_], [e_])
                    P.tt("dve", hT[:, ob, :], hT[:, ob, :], e_[:], ALU.add, [hT, e_], [hT])
            if dbg_on:
                dbg_dump("h1", hT, hT[:, :, :].rearrange("p a t -> p (a t)"), [128, 8 * TT])
        rms_rstd([hT], lambda k: hT[:, k, :], 8, NORM_EPS)
        fo = 2 * VEC_PER_LAYER
        for k in range(8):
            P.stt(zs[:, k, :], hT[:, k, :], vec[:, fo + k:fo + k + 1], rstd[:], ALU.mult, ALU.mult, [hT, vec, rstd], [zsv[k]])
        out_toks.append(P.dma(oT[:, t0:t0 + TT].rearrange("(k p) t -> p k t", p=128), zs[:, 0:8, :], reads=zsv[0:8]))
    out_toks.extend(dbg_out.values())
    ninst = P.ninst
    P.finish(out_toks)
    return nc, ninst


def pack_shared(inp):
    f = lambda a: np.asarray(a, np.float32)
    vec = np.zeros((128, NVEC), np.float32)
    for l in range(2):
        def put(name, arr, n):
            c = vcol(l, name)
            vec[:, c:c + n] = _pp(arr, n)
        put("ng", f(inp["norm_g"])[l], 8)
        put("mu", f(inp["rwkv_mu"])[l], 13)
        put("w0", f(inp["rwkv_w0"])[l], 4)
        put("a0", f(inp["rwkv_a0"])[l], 4)
        put("kk", f(inp["rwkv_k_k"])[l], 4)
        put("ka", f(inp["rwkv_k_a"])[l], 4)
        put("rk", f(inp["rwkv_r_k"])[l].reshape(512), 4)
        put("lnw", f(inp["rwkv_ln_w"])[l], 4)
        put("lnb", f(inp["rwkv_ln_b"])[l], 4)
        put("s5d", f(inp["s5_d"])[l], 4)
        put("glub", f(inp["s5_glu_b"])[l], 4)
        cw = f(inp["lru_conv_w"])[l]
        c = vcol(l, "cw")
        for j in range(4):
            vec[:, c + 4 * j:c + 4 * j + 4] = _pp(cw[j], 4)
        put("cb", f(inp["lru_conv_b"])[l], 4)
        put("ba", f(inp["lru_ba"])[l], 4)
        put("bx", f(inp["lru_bx"])[l], 4)
        put("lam", f(inp["lru_lambda"])[l], 4)
        put("png", f(inp["ple_norm_g"])[l], 8)
    vec[:, 2 * VEC_PER_LAYER:2 * VEC_PER_LAYER + 8] = _pp(f(inp["final_norm_g"]), 8)

    w2a2 = np.zeros((2, 128, 512), np.float32)
    w2a2[:, 0:64] = f(inp["rwkv_w2"])
    w2a2[:, 64:128] = f(inp["rwkv_a2"])
    lruw = np.zeros((2, 128, 8, 128), np.float32)
    for l in range(2):
        for m, key in enumerate(("lru_wa", "lru_wx")):
            w = f(inp[key])[l]
            for q in range(4):
                for b2 in range(2):
                    lruw[l, b2 * 64:(b2 + 1) * 64, m * 4 + q, b2 * 64:(b2 + 1) * 64] = w[2 * q + b2]
    lruw = lruw.reshape(2, 128, 1024)
    def modes(a):
        a = f(a).reshape(2, 16, 2, 64)
        return np.ascontiguousarray(a.transpose(0, 2, 3, 1).reshape(2, 128, 16))
    s5s = np.zeros((2, 128, 3, 16), np.float32)
    s5s[:, :, 0] = modes(inp["s5_a_re"])
    s5s[:, :, 1] = modes(inp["s5_a_im"])
    ldt = np.broadcast_to(f(inp["s5_log_dt"])[:, :, None], (2, 32, 64))
    s5s[:, :, 2] = modes(ldt)
    s5s = s5s.reshape(2, 128, 48)
    def bmodes(a):
        a = f(a).reshape(2, 16, 2, 64, 16)
        return a.transpose(0, 2, 3, 1, 4).reshape(2, 128, 16, 16)
    s5b = np.stack([bmodes(inp["s5_b_re"]), bmodes(inp["s5_b_im"])], axis=2).reshape(2, 128, 512)
    s5c = np.zeros((2, 128, 2, 16, 64), np.float32)
    for ri, key in enumerate(("s5_c_re", "s5_c_im")):
        c = f(inp[key]).reshape(2, 16, 2, 16, 64)
        for gh in range(2):
            for jh in range(2):
                c0 = 32 * jh + 16 * gh
                s5c[:, gh * 64:(gh + 1) * 64, ri, jh::2, c0:c0 + 16] = c[:, jh::2, gh].transpose(0, 3, 1, 2)
    s5c = s5c.reshape(2, 128, 2048)
    return {
        "w_in": np.ascontiguousarray(f(inp["w_in"])), "w_out": np.ascontiguousarray(f(inp["w_out"])),
        "ple_w": np.ascontiguousarray(f(inp["ple_w"])), "ple_gw": np.ascontiguousarray(f(inp["ple_gate_w"])),
        "glu_w": np.ascontiguousarray(f(inp["s5_glu_w"])), "vec": vec, "cst": make_consts()[0], "msk": make_consts()[1],
        "w2a2": w2a2, "lruw": np.ascontiguousarray(lruw), "s5s": np.ascontiguousarray(s5s),
        "s5b": np.ascontiguousarray(s5b), "s5c": np.ascontiguousarray(s5c),
    }


_NC_CACHE = {}


def run_cores(inp, TC, batches, dbg=None):
    key = (TC, tuple(sorted(dbg)) if dbg else None)
    if key not in _NC_CACHE:
        _NC_CACHE[key] = build_nc(TC, dbg)
    nc, ninst = _NC_CACHE[key]
    shared = pack_shared(inp)
    x = np.asarray(inp["x"], np.float32)
    p = np.asarray(inp["p"], np.float32)
    in_maps = []
    for b in batches:
        m = dict(shared)
        m["xT"] = np.ascontiguousarray(x[b, :TC].T)
        m["pT"] = np.ascontiguousarray(p[:, b, :TC].transpose(0, 2, 1))
        in_maps.append(m)
    res = run_bass_kernel_spmd(nc, in_maps, core_ids=list(range(len(batches))))
    return res


def kernel(**inputs):
    x = np.asarray(inputs["x"])
    B, S, _ = x.shape
    batches = [c % B for c in range(8)]
    res = run_cores(inputs, S, batches)
    out = np.empty((B, S, D), np.float32)
    for b in range(B):
        out[b] = res.results[b]["oT"].T
    return out.astype(x.dtype)
```

```python
import contextlib
import math
import numpy as np
import concourse.bass as bass
import concourse.mybir as mybir
from concourse.bass_utils import run_bass_kernel_spmd

F32 = mybir.dt.float32
BF16 = mybir.dt.bfloat16
ALU = mybir.AluOpType
AF = mybir.ActivationFunctionType

D = 1024
DIN = 4224
DMIX = 1536
DPLE = 256
TT = 256
LCH = 64
NCH = TT // LCH
TS5 = 128
import os
SKIP = set(os.environ.get("KSKIP", "").split(","))
S5E = os.environ.get("KS5E", "pool")
GW = tuple(int(v) for v in os.environ.get("KGW", "1,1,1").split(","))
GN_EPS = 64e-5
NORM_EPS = 1e-6


class Tok:
    __slots__ = ("sem", "val", "eng", "dma")

    def __init__(self, sem, val, eng, dma):
        self.sem, self.val, self.eng, self.dma = sem, val, eng, dma


class Buf:
    def __init__(self, t, name):
        self.t = t
        self.name = name
        self.w = None
        self.r = []

    def __getitem__(self, idx):
        return self.t[idx]


class Prog:
    ENGS = ("pe", "act", "dve", "pool", "sp")

    def __init__(self, nc, n_dma_sems=32):
        self.nc = nc
        self.es = contextlib.ExitStack()
        self.ops = {e: [] for e in self.ENGS}
        self.cnt = {e: 0 for e in self.ENGS}
        self.sem = {e: self.es.enter_context(nc.semaphore("s_" + e)) for e in self.ENGS}
        self.dsem = [self.es.enter_context(nc.semaphore("d%d" % i)) for i in range(n_dma_sems)]
        self.duse = [0] * n_dma_sems
        self.dnext = 0
        self.seen = {e: {} for e in self.ENGS}
        self.nbuf = 0
        self.ninst = 0
        self.stack = [self.es]

    def push(self):
        st = contextlib.ExitStack()
        self.stack.append(st)

    def pop(self):
        self.barrier()
        self.stack.pop().close()

    def barrier(self):
        toks = []
        for f in self.ENGS:
            if self.cnt[f] > 0:
                toks.append(Tok(self.sem[f], self.cnt[f], f, False))
        for i, s in enumerate(self.dsem):
            if self.duse[i] > 0:
                toks.append(Tok(s, 16 * self.duse[i], "dma", True))
        for e in self.ENGS:
            wl = []
            for t in toks:
                if t.eng == e and not t.dma:
                    continue
                k = id(t.sem)
                if self.seen[e].get(k, 0) >= t.val:
                    continue
                self.seen[e][k] = t.val
                wl.append((t.sem, t.val))

            def run(en, wl=wl):
                for (s, v) in wl:
                    en.wait_ge(s, v)
            self.ops[e].append(run)

    def sb(self, shape, dt=F32, name=None):
        self.nbuf += 1
        name = name or ("b%d" % self.nbuf)
        t = self.stack[-1].enter_context(self.nc.sbuf_tensor("sb_" + name, list(shape), dt))
        return Buf(t, name)

    def ps(self, name, dt=F32, cols=512):
        t = self.es.enter_context(self.nc.psum_tensor(name, [128, cols], dt))
        return Buf(t, name)

    def wrap(self, t, name):
        return Buf(t, name)

    def views(self, buf, n):
        return [Buf(buf.t, "%s.v%d" % (buf.name, i)) for i in range(n)]

    def _need(self, eng, tok, waits, is_dma_issue):
        if tok is None:
            return
        if tok.eng == eng and not tok.dma and not is_dma_issue and eng == "pe":
            return
        k = id(tok.sem)
        if self.seen[eng].get(k, 0) >= tok.val:
            return
        cur = waits.get(k)
        if cur is None or cur[1] < tok.val:
            waits[k] = (tok.sem, tok.val)

    def emit(self, eng, fn, reads=(), writes=(), dma=False):
        waits = {}
        for b in reads:
            self._need(eng, b.w, waits, dma)
        for b in writes:
            self._need(eng, b.w, waits, dma)
            for t in b.r:
                self._need(eng, t, waits, dma)
        if dma:
            i = self.dnext
            self.dnext = (self.dnext + 1) % len(self.dsem)
            s = self.dsem[i]
            if self.duse[i] > 0:
                self._need(eng, Tok(s, 16 * self.duse[i], "dma", True), waits, True)
            self.duse[i] += 1
            tok = Tok(s, 16 * self.duse[i], "dma", True)
            inc = 16
        else:
            self.cnt[eng] += 1
            tok = Tok(self.sem[eng], self.cnt[eng], eng, False)
            inc = 1
        wl = list(waits.values())
        for (s, v) in wl:
            self.seen[eng][id(s)] = v
        tsem = tok.sem
        self.ninst += 1 + len(wl)

        def run(e, wl=wl, fn=fn, tsem=tsem, inc=inc):
            for (s, v) in wl:
                e.wait_ge(s, v)
            fn(e).then_inc(tsem, inc)

        self.ops[eng].append(run)
        for b in reads:
            b.r = [t for t in b.r if t.sem is not tok.sem]
            b.r.append(tok)
        for b in writes:
            b.w = tok
            b.r = []
        return tok

    def finish(self, out_toks):
        wl = [(t.sem, t.val) for t in out_toks]

        def run(e, wl=wl):
            for (s, v) in wl:
                e.wait_ge(s, v)

        self.ops["sp"].append(run)
        nc = self.nc
        ops = self.ops
        with nc.Block() as block:
            @block.tensor
            def _(e):
                for f in ops["pe"]:
                    f(e)

            @block.scalar
            def _(e):
                for f in ops["act"]:
                    f(e)

            @block.vector
            def _(e):
                for f in ops["dve"]:
                    f(e)

            @block.gpsimd
            def _(e):
                for f in ops["pool"]:
                    f(e)

            @block.sync
            def _(e):
                for f in ops["sp"]:
                    f(e)
        self.es.close()

    def dma(self, out, in_, reads=(), writes=(), eng="sp", **kw):
        return self.emit(eng, lambda e: e.dma_start(out=out, in_=in_, **kw), reads, writes, dma=True)

    def mm(self, out, lhsT, rhs, start, stop, reads, writes):
        return self.emit("pe", lambda e: e.matmul(out, lhsT, rhs, start=start, stop=stop), reads, writes)

    def act(self, out, in_, func, reads, writes, bias=None, scale=None):
        kw = {}
        if bias is not None:
            kw["bias"] = bias
        if scale is not None:
            kw["scale"] = scale
        return self.emit("act", lambda e: e.activation(out=out, in_=in_, func=func, **kw), reads, writes)

    def tt(self, eng, out, in0, in1, op, reads, writes):
        return self.emit(eng, lambda e: e.tensor_tensor(out=out, in0=in0, in1=in1, op=op), reads, writes)

    def ts(self, eng, out, in0, s1, s2, op0, op1, reads, writes):
        if op1 is None:
            return self.emit(eng, lambda e: e.tensor_scalar(out, in0, s1, None, op0), reads, writes)
        return self.emit(eng, lambda e: e.tensor_scalar(out, in0, s1, s2, op0, op1), reads, writes)

    def stt(self, out, in0, scalar, in1, op0, op1, reads, writes):
        return self.emit("dve", lambda e: e.scalar_tensor_tensor(out, in0, scalar, in1, op0, op1), reads, writes)

    def copy(self, eng, out, in_, reads, writes):
        if eng == "act":
            return self.emit("act", lambda e: e.activation(out=out, in_=in_, func=AF.Copy), reads, writes)
        return self.emit(eng, lambda e: e.tensor_copy(out, in_), reads, writes)

    def memset(self, eng, ap, val, writes):
        return self.emit(eng, lambda e: e.memset(ap, val), (), writes)

    def scan(self, out, d0, d1, init, reads, writes):
        return self.emit("dve", lambda e: e.tensor_tensor_scan(out, d0, d1, init, ALU.mult, ALU.add), reads, writes)

    def recip(self, out, in_, reads, writes):
        return self.emit("dve", lambda e: e.reciprocal(out, in_), reads, writes)


VEC_FIELDS = [("ng", 8), ("mu", 13), ("w0", 4), ("a0", 4), ("kk", 4), ("ka", 4), ("rk", 4), ("lnw", 4),
              ("lnb", 4), ("s5d", 4), ("glub", 4), ("cw", 16), ("cb", 4), ("ba", 4), ("bx", 4), ("lam", 4),
              ("png", 8)]
VEC_PER_LAYER = sum(n for _, n in VEC_FIELDS)
VEC_OFF = {}
_o = 0
for _n, _c in VEC_FIELDS:
    VEC_OFF[_n] = _o
    _o += _c
NVEC = 2 * VEC_PER_LAYER + 8

CST_IDENT = 0
CST_ONESBD = 128
CST_SCAN = 256
NCST = 256 + TT
MSK_USN = 0
MSK_LSN = NCH * 128
MSK_USP = 2 * NCH * 128
MSK_CI = 3 * NCH * 128
NMSK = 3 * NCH * 128 + NCH * LCH


def vcol(l, name, i=0):
    return l * VEC_PER_LAYER + VEC_OFF[name] + i


def _pp(v, n):
    return np.ascontiguousarray(np.asarray(v, np.float32).reshape(n, 128).T)


def make_consts():
    c = np.zeros((128, NCST), np.float32)
    i = np.arange(128)[:, None]
    j = np.arange(128)[None, :]
    c[:, CST_IDENT:CST_IDENT + 128] = (i == j)
    c[:, CST_ONESBD:CST_ONESBD + 128] = ((i // 64) == (j // 64))
    tt = np.arange(TT)[None, :]
    c[:, CST_SCAN:CST_SCAN + TT] = 1.0 * ((tt % LCH) != 0)
    m = np.zeros((128, NMSK), np.float32)
    t = np.arange(64)[None, :]
    for ch in range(NCH):
        m[:, MSK_USN + ch * 128:MSK_USN + (ch + 1) * 128] = -1.0 * (j > i)
        m[:, MSK_LSN + ch * 128:MSK_LSN + (ch + 1) * 128] = -1.0 * (i > j)
        m[:, MSK_USP + ch * 128:MSK_USP + (ch + 1) * 128] = 1.0 * (j > i)
        m[:, MSK_CI + ch * 64:MSK_CI + (ch + 1) * 64] = 1.0 * (t >= (i % 64))
    return c, m


def build_nc(TC, dbg=None):
    assert TC % TT == 0
    NT = TC // TT
    dbg = dbg or set()
    nc = bass.Bass("TRN2", target_bir_lowering=False)

    def din(name, shape, dt=F32):
        return nc.dram_tensor(name, list(shape), dt, kind="ExternalInput").ap()

    xT = din("xT", [D, TC])
    pT = din("pT", [2, DPLE, TC])
    w_in = din("w_in", [2, D, DIN])
    w_out = din("w_out", [2, DMIX, D])
    ple_w = din("ple_w", [2, DPLE, D])
    ple_gw = din("ple_gw", [2, D, D])
    glu_w = din("glu_w", [2, 512, 512])
    vec_d = din("vec", [128, NVEC])
    cst_d = din("cst", [128, NCST])
    msk_d = din("msk", [128, NMSK])
    w2a2_d = din("w2a2", [2, 128, 512])
    lruw_d = din("lruw", [2, 128, 8 * 128])
    s5s_d = din("s5s", [2, 128, 3 * 16])
    s5b_d = din("s5b", [2, 128, 2 * 16 * 16])
    s5c_d = din("s5c", [2, 128, 2 * 16 * 64])
    oT = nc.dram_tensor("oT", [D, TC], F32, kind="ExternalOutput").ap()
    dbg_out = {}

    def dram_int(name, shape, dt):
        return nc.dram_tensor(name, list(shape), dt, kind="Internal").ap()

    w_in_b = dram_int("w_in_b", [2, D, DIN], BF16)
    w_out_b = dram_int("w_out_b", [2, DMIX, D], BF16)
    ple_w_b = dram_int("ple_w_b", [2, DPLE, D], BF16)
    ple_gw_b = dram_int("ple_gw_b", [2, D, D], BF16)
    glu_w_b = dram_int("glu_w_b", [2, 512, 512], BF16)
    s5tab_d = dram_int("s5tab", [2, 128, 2 * 16 * TS5], F32)

    P = Prog(nc)
    wsb = P.wrap(None, "wscratch")
    tabsb = P.wrap(None, "s5tabscr")

    def dbg_dump(name, buf, ap, shape, dt=F32):
        if name not in dbg:
            return
        o = nc.dram_tensor("dbg_" + name, list(shape), dt, kind="ExternalOutput").ap()
        dbg_out[name] = P.dma(o, ap, reads=[buf])

    vec = P.sb([128, NVEC], F32, "vec")
    cst = P.sb([128, NCST], F32, "cst")
    P.dma(vec[:], vec_d, writes=[vec])
    P.dma(cst[:], cst_d, writes=[cst])
    cstb = P.sb([128, 128], BF16, "cstb")
    P.copy("dve", cstb[:], cst[:, CST_IDENT:CST_IDENT + 128], [cst], [cstb])
    mskb = P.sb([128, NMSK], BF16, "mskb")
    ident_f = cst[:, CST_IDENT:CST_IDENT + 128]
    ident_b = cstb[:, 0:128]
    onesbd_f = cst[:, CST_ONESBD:CST_ONESBD + 128]
    ones_f = P.sb([128, 128], F32, "ones_f")
    P.memset("pool", ones_f[:], 1.0, [ones_f])
    one_t = P.sb([128, 1], F32, "one_t")
    P.memset("pool", one_t[:], 1.0, [one_t])

    def V(l, name, i=0, n=1):
        c = vcol(l, name, i)
        return vec[:, c:c + n]

    for l in range(2):
        for (src, dst, rows) in ((w_in, w_in_b, D), (w_out, w_out_b, DMIX), (ple_w, ple_w_b, DPLE),
                                 (ple_gw, ple_gw_b, D), (glu_w, glu_w_b, 512)):
            for r0 in range(0, rows, 128):
                P.dma(dst[l, r0:r0 + 128, :], src[l, r0:r0 + 128, :], writes=[wsb], eng="pool",
                      max_dma_last_dim=4096)

    w2a2 = []
    lruw = []
    for l in range(2):
        w2a2.append(P.sb([128, 512], BF16, "w2a2_%d" % l))
        lruw.append(P.sb([128, 1024], BF16, "lruw_%d" % l))
    lru_c = P.sb([128, 2, 8], F32, "lru_c")
    s5B = [P.sb([128, 2 * 4 * 2 * 128], BF16, "s5B%d" % l) for l in range(2)]
    s5C = [P.sb([128, 2048], BF16, "s5C%d" % l) for l in range(2)]
    s5keep = [P.sb([128, 3, 16], F32, "s5keep%d" % l) for l in range(2)]
    s5rotb = [P.sb([128, 2, 16], F32, "s5rot%d" % l) for l in range(2)]
    ps_misc = P.ps("ps7")
    P.push()
    stage = P.sb([128, 1024], F32, "stage")
    mstage = P.sb([128, NMSK], F32, "mstage")
    P.dma(mstage[:], msk_d, writes=[mstage])
    P.copy("act", mskb[:], mstage[:], [mstage], [mskb])
    for l in range(2):
        P.dma(stage[:, 0:512], w2a2_d[l], writes=[stage])
        P.copy("act", w2a2[l][:], stage[:, 0:512], [stage], [w2a2[l]])
        P.dma(stage[:], lruw_d[l], writes=[stage])
        P.copy("act", lruw[l][:], stage[:], [stage], [lruw[l]])

    for l in range(2):
        tmp = P.sb([128, 4], F32, "lrutmp%d" % l)
        P.act(tmp[:], V(l, "lam", 0, 4), AF.Exp, [vec], [tmp], scale=-1.0)
        P.act(tmp[:], tmp[:], AF.Ln, [tmp, one_t], [tmp], bias=one_t[:, 0:1])
        P.ts("dve", lru_c[:, l, 0:4], tmp[:], -8.0, None, ALU.mult, None, [tmp], [lru_c])
        P.ts("dve", lru_c[:, l, 4:8], tmp[:], -16.0, None, ALU.mult, None, [tmp], [lru_c])

    s5rot = []
    for l in range(2):
        s5s = P.sb([128, 48], F32, "s5s%d" % l)
        P.dma(s5s[:], s5s_d[l], writes=[s5s])
        a_re = s5s[:, 0:16]
        a_im = s5s[:, 16:32]
        ldt = s5s[:, 32:48]
        w = P.sb([128, 16, 16], F32, "s5w%d" % l)
        R = [w]

        def row(i):
            return w[:, i, :]
        dt_, rho, th, cc, ss, t1, t2, lr, li, den, qre, qim, nr = [row(i) for i in range(13)]
        P.act(dt_, ldt, AF.Exp, [s5s], R)
        P.tt("dve", rho, a_re, dt_, ALU.mult, [s5s] + R, R)
        P.act(rho, rho, AF.Exp, R, R)
        P.tt("dve", th, a_im, dt_, ALU.mult, [s5s] + R, R)
        hp = P.sb([128, 1], F32, "halfpi%d" % l)
        P.memset("dve", hp[:], math.pi / 2, [hp])
        P.act(cc, th, AF.Sin, R + [hp], R, bias=hp[:, 0:1], scale=1.0 / 16)
        P.act(ss, th, AF.Sin, R, R, scale=1.0 / 16)

        def csq(c_, s_):
            P.tt("dve", t1, c_, c_, ALU.mult, R, R)
            P.tt("dve", t2, s_, s_, ALU.mult, R, R)
            P.stt(s_, c_, 2.0, s_, ALU.mult, ALU.mult, R, R)
            P.tt("dve", c_, t1, t2, ALU.subtract, R, R)
        for _ in range(4):
            csq(cc, ss)
        P.tt("dve", lr, rho, cc, ALU.mult, R, R)
        P.tt("dve", li, rho, ss, ALU.mult, R, R)
        P.tt("dve", t1, a_re, a_re, ALU.mult, [s5s] + R, R)
        P.tt("dve", t2, a_im, a_im, ALU.mult, [s5s] + R, R)
        P.tt("dve", den, t1, t2, ALU.add, R, R)
        P.recip(den, den, R, R)
        P.ts("dve", nr, lr, -1.0, None, ALU.add, None, R, R)
        P.tt("dve", t1, nr, a_re, ALU.mult, [s5s] + R, R)
        P.tt("dve", t2, li, a_im, ALU.mult, [s5s] + R, R)
        P.tt("dve", t1, t1, t2, ALU.add, R, R)
        P.tt("dve", qre, t1, den, ALU.mult, R, R)
        P.tt("dve", t1, li, a_re, ALU.mult, [s5s] + R, R)
        P.tt("dve", t2, nr, a_im, ALU.mult, [s5s] + R, R)
        P.tt("dve", t1, t1, t2, ALU.subtract, R, R)
        P.tt("dve", qim, t1, den, ALU.mult, R, R)
        keep = s5keep[l]
        P.copy("dve", keep[:, 0, :], rho, R, [keep])
        P.copy("dve", keep[:, 1, :], cc, R, [keep])
        P.copy("dve", keep[:, 2, :], ss, R, [keep])

        sbf = P.sb([128, 512], F32, "s5b_in%d" % l)
        P.dma(sbf[:], s5b_d[l], writes=[sbf])
        bre = sbf[:, 0:256].rearrange("p (j h) -> p j h", h=16)
        bim = sbf[:, 256:512].rearrange("p (j h) -> p j h", h=16)
        Bt = s5B[l]
        Btv = Bt[:, :].rearrange("p (r b q m) -> p r b q m", r=2, b=4, q=2)
        bpad = P.sb([128, 2, 128], BF16, "s5bpad%d" % l)
        tb = P.sb([128, 2, 16], F32, "s5tb%d" % l)
        for j in range(16):
            P.ts("dve", tb[:, 0, :], bim[:, j, :], qim[:, j:j + 1], None, ALU.mult, None, [sbf] + R, [tb])
            P.stt(tb[:, 0, :], bre[:, j, :], qre[:, j:j + 1], tb[:, 0, :], ALU.mult, ALU.subtract, [sbf, tb] + R, [tb])
            P.ts("dve", tb[:, 1, :], bre[:, j, :], qim[:, j:j + 1], None, ALU.mult, None, [sbf] + R, [tb])
            P.stt(tb[:, 1, :], bim[:, j, :], qre[:, j:j + 1], tb[:, 1, :], ALU.mult, ALU.add, [sbf, tb] + R, [tb])
            P.memset("pool", bpad[:], 0.0, [bpad])
            for gh in range(2):
                col0 = 32 * (j % 4) + gh * 16
                for ri in range(2):
                    P.copy("pool", bpad[gh * 64:(gh + 1) * 64, ri, col0:col0 + 16],
                           tb[gh * 64:(gh + 1) * 64, ri, :], [tb], [bpad])
            for ri in range(2):
                P.mm(ps_misc[:, ri * 128:(ri + 1) * 128], bpad[:, ri, :], ident_b, True, True, [bpad, cstb], [ps_misc])
            hf = (j % 4) // 2
            for ri in range(2):
                P.copy("act", Btv[64 * hf:64 * hf + 64, ri, j // 4, j % 2, :],
                       ps_misc[64 * hf:64 * hf + 64, ri * 128:(ri + 1) * 128], [ps_misc], [Bt])

        scf = P.sb([128, 2048], F32, "s5c_in%d" % l)
        P.dma(scf[:], s5c_d[l], writes=[scf])
        Ct = s5C[l]
        P.copy("act", Ct[:, 0:1024], scf[:, 0:1024], [scf], [Ct])
        P.ts("dve", Ct[:, 1024:2048], scf[:, 1024:2048], -1.0, None, ALU.mult, None, [scf], [Ct])

        tab = P.sb([128, 2, 16, TS5], F32, "s5tabb%d" % l)
        P.memset("pool", tab[:, 0, :, 0:1], 1.0, [tab])
        P.memset("pool", tab[:, 1, :, 0:1], 0.0, [tab])
        ec = P.sb([128, 2, 16], F32, "s5ec%d" % l)
        P.copy("dve", ec[:, 0, :], cc, R, [ec])
        P.copy("dve", ec[:, 1, :], ss, R, [ec])
        m = 1
        while m < TS5:
            for j in range(16):
                cj = ec[:, 0, j:j + 1]
                sj = ec[:, 1, j:j + 1]
                src_c = tab[:, 0, j, 0:m]
                src_s = tab[:, 1, j, 0:m]
                dst_c = tab[:, 0, j, m:2 * m]
                dst_s = tab[:, 1, j, m:2 * m]
                P.ts("dve", dst_c, src_s, sj, None, ALU.mult, None, [tab, ec], [tab])
                P.stt(dst_c, src_c, cj, dst_c, ALU.mult, ALU.subtract, [tab, ec], [tab])
                P.ts("dve", dst_s, src_c, sj, None, ALU.mult, None, [tab, ec], [tab])
                P.stt(dst_s, src_s, cj, dst_s, ALU.mult, ALU.add, [tab, ec], [tab])
            e_c = ec[:, 0, :]
            e_s = ec[:, 1, :]
            P.tt("dve", t1, e_c, e_c, ALU.mult, [ec] + R, R)
            P.tt("dve", t2, e_s, e_s, ALU.mult, [ec] + R, R)
            P.stt(e_s, e_c, 2.0, e_s, ALU.mult, ALU.mult, [ec], [ec])
            P.tt("dve", e_c, t1, t2, ALU.subtract, R, [ec])
            m *= 2
        rot = s5rotb[l]
        P.copy("dve", rot[:], ec[:], [ec], [rot])
        s5rot.append((rot, keep))
        P.dma(s5tab_d[l], tab[:, :, :, :].rearrange("p a j t -> p (a j t)"), reads=[tab], writes=[tabsb])
        dbg_dump("s5w%d" % l, w, w[:, :, :].rearrange("p a b -> p (a b)"), [128, 256])
        dbg_dump("s5keep%d" % l, keep, keep[:, :, :].rearrange("p a b -> p (a b)"), [128, 48])
        dbg_dump("s5tab%d" % l, tab, tab[:, :, :, :].rearrange("p a j t -> p (a j t)"), [128, 2 * 16 * TS5])
        dbg_dump("s5B%d" % l, Bt, Bt[:, :], [128, 2048], BF16)
    P.pop()

    hT = P.sb([128, 8, TT], F32, "hT")
    xn = P.sb([128, 8, TT], BF16, "xn")
    zst = [P.sb([128, 1 + TT], F32, "zst%d" % i) for i in range(2)]
    zs = P.sb([128, 13, TT], F32, "zs")
    zsv = P.views(zs, 13)
    zu = P.sb([128, 4, TT], F32, "zu")
    zx = P.sb([128, 4, 3 + TT], F32, "zx")
    sgate = P.sb([128, 12, TT], BF16, "sgate")
    sgv = P.views(sgate, 12)
    ycat = P.sb([128, 12, TT], BF16, "ycat")
    ycv = P.views(ycat, 12)
    pbf = P.sb([128, 2, TT], BF16, "pbf")
    ring = [P.sb([128, 4096], BF16, "ring%d" % i) for i in range(3)]
    ringi = [0]
    s5tab = P.sb([128, 2, 16, TS5], F32, "s5tab")
    banks = [P.ps("ps%d" % i) for i in range(7)] + [ps_misc]
    rot_i = {"proj": 0, "rw": 0}

    def bank(group):
        ids = (0, 1) if group == "proj" else (2, 3, 6)
        i = rot_i[group]
        rot_i[group] = (i + 1) % len(ids)
        return banks[ids[i]]

    def next_ring():
        r = ring[ringi[0]]
        ringi[0] = (ringi[0] + 1) % 3
        return r

    cz = [P.sb([128, 13], F32, "cz%d" % l) for l in range(2)]
    cl = [P.sb([128, 4, 3], F32, "cl%d" % l) for l in range(2)]
    ch = [P.sb([128, 4], F32, "ch%d" % l) for l in range(2)]
    s5z = [P.sb([128, 2, 16], F32, "s5z%d" % l) for l in range(2)]
    s5zv = [P.views(s5z[l], 16) for l in range(2)]
    Tst = [[P.sb([128, 128], BF16, "T%d_%d" % (l, pb)) for pb in range(4)] for l in range(2)]
    for l in range(2):
        P.memset("pool", cz[l][:], 0.0, [cz[l]])
        P.memset("pool", cl[l][:], 0.0, [cl[l]])
        P.memset("pool", ch[l][:], 0.0, [ch[l]])
        P.memset("pool", s5z[l][:], 0.0, s5zv[l])
        for pb in range(4):
            P.memset("pool", Tst[l][pb][:], 0.0, [Tst[l][pb]])

    NF = 17
    fs = [P.sb([128, TT], F32, "rf%d" % i) for i in range(NF)]
    pad_names = ["RTp", "KTp", "CTp", "BTp", "VTp", "KGp", "BGp"]
    pads = {n: P.sb([128, NCH * 128], BF16, n) for n in pad_names}
    for n in pad_names:
        P.memset("pool", pads[n][:], 0.0, [pads[n]])
    RTc = P.sb([128, TT], BF16, "RTc")
    tanh_wd = P.sb([128, TT], BF16, "tanhwd")
    Blev_i = [[P.sb([128, NCH * 128], BF16, "Blev%d_%d" % (i, k)) for i in range(2)] for k in range(2)]
    BTlev_i = [[P.sb([128, NCH * 128], BF16, "BTlev%d_%d" % (i, k)) for i in range(2)] for k in range(2)]
    AkkT_i = [P.sb([128, NCH * 128], BF16, "AkkT_%d" % k) for k in range(2)]
    Xbf_i = [P.sb([128, NCH * 256], BF16, "Xbf_%d" % k) for k in range(2)]
    NPI = 2
    pp = []
    for i in range(NPI):
        d = {}
        for n in ("Vbd", "KGbd", "BGbd", "PT", "nU0", "Wbd"):
            d[n] = P.sb([128, NCH * 128], BF16, "%s_%d" % (n, i))
        for n in ("Rhat", "ArkT", "ArbT"):
            d[n] = P.sb([128, TT], BF16, "%s_%d" % (n, i))
        d["bonus"] = P.sb([128, TT], F32, "bonus_%d" % i)
        d["GL"] = P.sb([128, NCH], F32, "GL_%d" % i)
        d["rt32"] = P.sb([128, TT], F32, "rt32_%d" % i)
        pp.append(d)
    mix = P.sb([128, 4, TT], F32, "mix")
    mixv = P.views(mix, 4)

    NS5SET = 2
    s5f = [[P.sb([128, TS5], F32, "s5f%d_%d" % (k, i)) for i in range(8)] for k in range(NS5SET)]
    NS5X = 3
    s5x = [P.sb([128, 2, TS5], BF16, "s5x%d" % k) for k in range(NS5X)]
    s5zl = [P.sb([128, 2], F32, "s5zl%d" % k) for k in range(NS5SET)]
    spf = [P.sb([128, TT], F32, "spf%d" % i) for i in range(3)]
    lf = [P.sb([128, TT], F32, "lf%d" % i) for i in range(4)]
    ubf = P.sb([128, 4, TT], BF16, "ubf")
    lb = P.sb([128, TT], BF16, "lb")
    rstd = P.sb([128, TT], F32, "rstd")
    sq = P.sb([128, 4, TT], BF16, "sq")
    ones_b = P.sb([128, 128], BF16, "ones_b")
    P.memset("pool", ones_b[:], 1.0, [ones_b])

    out_toks = []
    eps_t = {}
    for e_ in (NORM_EPS, GN_EPS):
        t = P.sb([128, 1], F32, "eps%d" % len(eps_t))
        P.memset("pool", t[:], e_, [t])
        eps_t[e_] = t
    neg_half = -math.exp(-0.5)

    def rms_rstd(src_bufs, src_ap_fn, nblk, eps):
        b = bank("proj")
        for k in range(nblk):
            P.act(sq[:, k % 4, :], src_ap_fn(k), AF.Square, src_bufs, [sq])
            P.mm(b[:, 0:TT], ones_b[:], sq[:, k % 4, :], k == 0, k == nblk - 1, [ones_b, sq], [b])
        P.act(rstd[:], b[:, 0:TT], AF.Ln, [b, eps_t[eps]], [rstd], bias=eps_t[eps][:, 0:1], scale=1.0 / (nblk * 128))
        P.act(rstd[:], rstd[:], AF.Exp, [rstd], [rstd], scale=-0.5)

    def drive(items):
        active = list(items)
        while active:
            for item in list(active):
                g, w = item
                for _ in range(w):
                    try:
                        next(g)
                    except StopIteration:
                        active.remove(item)
                        break

    s5ctr = [0]

    for it in range(NT):
        t0 = it * TT
        first = (it == 0)
        P.dma(hT[:], xT[:, t0:t0 + TT].rearrange("(k p) t -> p k t", p=128), writes=[hT])
        for l in range(2):
            dbg_on = first and l == 0
            P.dma(pbf[:], pT[l, :, t0:t0 + TT].rearrange("(k p) t -> p k t", p=128), writes=[pbf], eng="pool")
            P.dma(s5tab[:, :, :, :].rearrange("p a j t -> p (a j t)"), s5tab_d[l], reads=[tabsb], writes=[s5tab])
            rms_rstd([hT], lambda k: hT[:, k, :], 8, NORM_EPS)
            for k in range(8):
                P.stt(xn[:, k, :], hT[:, k, :], V(l, "ng", k), rstd[:], ALU.mult, ALU.mult, [hT, vec, rstd], [xn])
            if dbg_on:
                dbg_dump("xn", xn, xn[:, :, :].rearrange("p a t -> p (a t)"), [128, 8 * TT], BF16)

            wchunk = {}

            def in_block(cb, l=l, wchunk=wchunk):
                ci = cb // 4
                if ci not in wchunk:
                    r = next_ring()
                    ncol = 512 if ci < 8 else 128
                    P.dma(r[:, 0:8 * ncol].rearrange("p (k n) -> p k n", k=8),
                          w_in_b[l].rearrange("(k p) n -> p k n", p=128)[:, :, ci * 512:ci * 512 + ncol],
                          reads=[wsb], writes=[r])
                    wchunk[ci] = (r, ncol)
                r, ncol = wchunk[ci]
                rv = r[:, 0:8 * ncol].rearrange("p (k n) -> p k n", k=8)
                c0 = (cb % 4) * 128
                b = bank("proj")
                for k in range(8):
                    P.mm(b[:, 0:TT], rv[:, k, c0:c0 + 128], xn[:, k, :], k == 0, k == 7, [r, xn], [b])
                return b

            for cb in range(13):
                st = zst[cb % 2]
                b = in_block(cb)
                P.copy("act", st[:, 0:1], cz[l][:, cb:cb + 1], [cz[l]], [st])
                P.copy("act", st[:, 1:1 + TT], b[:, 0:TT], [b], [st])
                P.copy("act", cz[l][:, cb:cb + 1], st[:, TT:TT + 1], [st], [cz[l]])
                d_ = lf[cb % 2]
                P.tt("pool", d_[:], st[:, 0:TT], st[:, 1:1 + TT], ALU.subtract, [st], [d_])
                P.stt(zs[:, cb, :], d_[:], V(l, "mu", cb), st[:, 1:1 + TT], ALU.mult, ALU.add, [d_, vec, st], [zsv[cb]])
            if dbg_on:
                dbg_dump("zs", zs, zs[:, :, :].rearrange("p a t -> p (a t)"), [128, 13 * TT])
            P.act(tanh_wd[0:64, :], zs[0:64, 12, :], AF.Tanh, [zsv[12]], [tanh_wd])
            P.copy("act", tanh_wd[64:128, :], zs[64:128, 12, :], [zsv[12]], [tanh_wd])
            for blk in range(4):
                b = in_block(13 + blk)
                P.act(sgate[:, blk, :], b[:, 0:TT], AF.Silu, [b], [sgv[blk]])
            for blk in range(4):
                b = in_block(17 + blk)
                P.copy("act", zu[:, blk, :], b[:, 0:TT], [b], [zu])
            P.copy("pool", ubf[:], zu[:], [zu], [ubf])
            for blk in range(4):
                b = in_block(21 + blk)
                P.act(sgate[:, 4 + blk, :], b[:, 0:TT], AF.Silu, [b], [sgv[4 + blk]])
            P.copy("act", zx[:, :, 0:3], cl[l][:, :, :], [cl[l]], [zx])
            for blk in range(4):
                b = in_block(25 + blk)
                P.copy("act", zx[:, blk, 3:3 + TT], b[:, 0:TT], [b], [zx])
            P.copy("act", cl[l][:, :, :], zx[:, :, TT:TT + 3], [zx], [cl[l]])
            for blk in range(4):
                b = in_block(29 + blk)
                P.act(sgate[:, 8 + blk, :], b[:, 0:TT], AF.Silu, [b], [sgv[8 + blk]])

            def prep(pb, inst, l=l, dbg_on=dbg_on):
                d = pp[inst]
                Blev, BTlev, AkkT = Blev_i[inst], BTlev_i[inst], AkkT_i[inst]
                r_ = zs[:, pb, :]
                k_ = zs[:, 4 + pb, :]
                v_ = zs[:, 8 + pb, :]
                zr_, zk_, zv_ = zsv[pb], zsv[4 + pb], zsv[8 + pb]
                (sg, ld, a_, kk_, kk2, sqk, kap, t1, kp, b_, lg, eg, ieg, eg1, dl, egl, rk) = fs[:17]
                rt32 = d["rt32"]
                cols = slice(pb * 128, (pb + 1) * 128)
                bw = bank("rw")
                P.mm(bw[:, 0:TT], w2a2[l][0:64, cols], tanh_wd[0:64, :], True, True, [w2a2[l], tanh_wd], [bw])
                P.act(sg[:], bw[:, 0:TT], AF.Sigmoid, [bw, vec], [sg], bias=V(l, "w0", pb))
                P.ts("dve", ld[:], sg[:], neg_half, None, ALU.mult, None, [sg], [ld])
                ba_ = bank("rw")
                P.mm(ba_[:, 0:TT], w2a2[l][64:128, cols], tanh_wd[64:128, :], True, True, [w2a2[l], tanh_wd], [ba_])
                P.act(a_[:], ba_[:, 0:TT], AF.Sigmoid, [ba_, vec], [a_], bias=V(l, "a0", pb))
                yield
                P.scan(lg[:], cst[:, CST_SCAN:CST_SCAN + TT], ld[:], 0.0, [cst, ld], [lg])
                P.ts("dve", kk_[:], k_, V(l, "kk", pb), None, ALU.mult, None, [zk_, vec], [kk_])
                P.tt("pool", kk2[:], kk_[:], kk_[:], ALU.mult, [kk_], [kk2])
                P.act(eg[:], lg[:], AF.Exp, [lg], [eg])
                yield
                bs = bank("rw")
                P.mm(bs[:, 0:TT], onesbd_f, kk2[:], True, True, [cst, kk2], [bs])
                P.act(sqk[:], bs[:, 0:TT], AF.Sqrt, [bs], [sqk])
                P.act(ieg[:], lg[:], AF.Exp, [lg], [ieg], scale=-1.0)
                P.tt("pool", eg1[:], lg[:], ld[:], ALU.subtract, [lg, ld], [eg1])
                P.act(eg1[:], eg1[:], AF.Exp, [eg1], [eg1])
                P.ts("dve", sqk[:], sqk[:], 1e-12, None, ALU.max, None, [sqk], [sqk])
                P.recip(sqk[:], sqk[:], [sqk], [sqk])
                P.tt("pool", kap[:], kk_[:], sqk[:], ALU.mult, [kk_, sqk], [kap])
                yield
                P.ts("dve", t1[:], a_[:], -1.0, V(l, "ka", pb), ALU.add, ALU.mult, [a_, vec], [t1])
                P.stt(kp[:], t1[:], 1.0, k_, ALU.add, ALU.mult, [t1, zk_], [kp])
                P.tt("pool", b_[:], kap[:], a_[:], ALU.mult, [kap, a_], [b_])
                lg3 = lg[:, :].rearrange("p (c t) -> p c t", t=LCH)
                P.tt("dve", dl[:, :].rearrange("p (c t) -> p c t", t=LCH), lg3[:, :, LCH - 1:LCH].to_broadcast([128, NCH, LCH]),
                     lg3, ALU.subtract, [lg], [dl])
                P.act(egl[:], dl[:], AF.Exp, [dl], [egl])
                P.copy("act", d["GL"][:, :], eg[:, :].rearrange("p (c t) -> p c t", t=LCH)[:, :, LCH - 1], [eg], [d["GL"]])
                yield
                P.tt("dve", rt32[:], r_, eg[:], ALU.mult, [zr_, eg], [rt32])
                P.copy("act", RTc[:], rt32[:], [rt32], [RTc])

                def padw(name, eng, in0, in1, rd):
                    t = pads[name]
                    tv = t[:, :].rearrange("p (c h t) -> p c h t", c=NCH, h=2)
                    for hh in range(2):
                        ps_ = slice(hh * 64, (hh + 1) * 64)
                        o = tv[ps_, :, hh, :]
                        i0 = in0[ps_, :].rearrange("p (c t) -> p c t", t=LCH)
                        if in1 is None:
                            P.copy(eng, o, i0, rd, [t])
                        else:
                            i1 = in1[ps_, :].rearrange("p (c t) -> p c t", t=LCH)
                            P.tt(eng, o, i0, i1, ALU.mult, rd, [t])
                padw("RTp", "act", rt32, None, [rt32])
                padw("KTp", "dve", kp, ieg, [kp, ieg])
                padw("CTp", "pool", kap, eg1, [kap, eg1])
                yield
                padw("BTp", "dve", b_, ieg, [b_, ieg])
                padw("VTp", "act", zs[:, 8 + pb, :], None, [zv_])
                padw("KGp", "pool", kp, egl, [kp, egl])
                padw("BGp", "dve", b_, egl, [b_, egl])
                yield "pre_pp"
                P.stt(rk[:], r_, V(l, "rk", pb), kp[:], ALU.mult, ALU.mult, [zr_, vec, kp], [rk])
                yield
                bb = bank("rw")
                P.mm(bb[:, 0:TT], onesbd_f, rk[:], True, True, [cst, rk], [bb])
                P.tt("dve", d["bonus"][:], bb[:, 0:TT], v_, ALU.mult, [bb, zv_], [d["bonus"]])
                if dbg_on and pb == 0:
                    dbg_dump("lg", lg, lg[:], [128, TT])
                    dbg_dump("kap", kap, kap[:], [128, TT])
                    dbg_dump("kp", kp, kp[:], [128, TT])
                    dbg_dump("a", a_, a_[:], [128, TT])
                yield

                def chunkmm(dst_bank, lname, rname, rbuf=None, rcols=128):
                    lt = pads[lname]
                    for c in range(NCH):
                        if rbuf is None:
                            rb_ = pads[rname]
                            rap = rb_[:, c * 128:(c + 1) * 128]
                        else:
                            rb_ = rbuf
                            rap = rbuf[:, c * rcols:(c + 1) * rcols]
                        P.mm(dst_bank[:, c * rcols:(c + 1) * rcols], lt[:, c * 128:(c + 1) * 128], rap, True, True,
                             [lt, rb_], [dst_bank])

                def masked(dst, src_bank, mcol, w):
                    n = NCH * w
                    P.tt("dve", dst[:, 0:n], src_bank[:, 0:n], mskb[:, mcol:mcol + n], ALU.mult, [src_bank, mskb], [dst])
                b1 = bank("rw")
                chunkmm(b1, "BTp", "CTp")
                masked(BTlev[0], b1, MSK_USN, 128)
                b2 = bank("rw")
                chunkmm(b2, "CTp", "BTp")
                masked(Blev[0], b2, MSK_LSN, 128)
                yield
                b3 = bank("rw")
                chunkmm(b3, "KTp", "CTp")
                masked(AkkT, b3, MSK_USP, 128)
                b4 = bank("rw")
                chunkmm(b4, "KTp", None, RTc, LCH)
                masked(d["ArkT"], b4, MSK_CI, LCH)
                b5 = bank("rw")
                chunkmm(b5, "BTp", None, RTc, LCH)
                masked(d["ArbT"], b5, MSK_CI, LCH)
                yield

            def tokmajor(src_name, dst_buf, dst_ap, eng):
                bt_ = bank("rw")
                lt = pads[src_name]
                for c in range(NCH):
                    P.mm(bt_[:, c * 128:(c + 1) * 128], lt[:, c * 128:(c + 1) * 128], ident_b, True, True,
                         [lt, cstb], [bt_])
                P.copy(eng, dst_ap, bt_[:, 0:NCH * 128] if len(dst_ap.shape) == 2 else
                       bt_[:, 0:NCH * 128].rearrange("p (c n) -> p c n", n=128), [bt_], [dst_buf])

            def solve(pb, inst, l=l):
                d = pp[inst]
                Blev, BTlev, AkkT, Xbf = Blev_i[inst], BTlev_i[inst], AkkT_i[inst], Xbf_i[inst]
                Xbfv = Xbf[:, :].rearrange("p (c n) -> p c n", n=256)
                tokmajor("VTp", d["Vbd"], d["Vbd"][:, :], "act")
                tokmajor("KGp", d["KGbd"], d["KGbd"][:, :], "act")
                yield
                tokmajor("BGp", d["BGbd"], d["BGbd"][:, :], "act")
                tokmajor("CTp", Xbf, Xbfv[:, :, 0:128], "act")
                bt_ = bank("rw")
                for c in range(NCH):
                    cs = slice(c * 128, (c + 1) * 128)
                    P.mm(bt_[:, cs], AkkT[:, cs], d["Vbd"][:, cs], True, True, [AkkT, d["Vbd"]], [bt_])
                P.copy("act", Xbfv[:, :, 128:256], bt_[:, 0:NCH * 128].rearrange("p (c n) -> p c n", n=128), [bt_], [Xbf])
                yield "tok_done"
                cur = 0
                NLEV = 6
                for lev in range(NLEV):
                    if lev < NLEV - 1:
                        nxt = 1 - cur
                        bq = bank("rw")
                        for c in range(NCH):
                            cs = slice(c * 128, (c + 1) * 128)
                            P.mm(bq[:, cs], Blev[cur][:, cs], BTlev[cur][:, cs], True, True, [Blev[cur], BTlev[cur]], [bq])
                        if lev < NLEV - 2:
                            bq2 = bank("rw")
                            for c in range(NCH):
                                cs = slice(c * 128, (c + 1) * 128)
                                P.mm(bq2[:, cs], BTlev[cur][:, cs], Blev[cur][:, cs], True, True,
                                     [Blev[cur], BTlev[cur]], [bq2])
                    for half in range(2):
                        bx_ = banks[4 + half]
                        for cc_ in range(2):
                            c = half * 2 + cc_
                            P.mm(bx_[:, cc_ * 256:(cc_ + 1) * 256], BTlev[cur][:, c * 128:(c + 1) * 128], Xbfv[:, c, :],
                                 True, False, [BTlev[cur], Xbf], [bx_])
                            P.mm(bx_[:, cc_ * 256:(cc_ + 1) * 256], ident_b, Xbfv[:, c, :],
                                 False, True, [cstb, Xbf], [bx_])
                    if lev < NLEV - 1:
                        P.copy("act", BTlev[nxt][:], bq[:, 0:512], [bq], [BTlev[nxt]])
                        if lev < NLEV - 2:
                            P.copy("dve", Blev[nxt][:], bq2[:, 0:512], [bq2], [Blev[nxt]])
                    P.copy("act", Xbf[:, 0:512], banks[4][:, 0:512], [banks[4]], [Xbf])
                    P.copy("dve", Xbf[:, 512:1024], banks[5][:, 0:512], [banks[5]], [Xbf])
                    if lev < NLEV - 1:
                        cur = nxt
                    yield
                nU0v = d["nU0"][:, :].rearrange("p (c n) -> p c n", n=128)
                Wbdv = d["Wbd"][:, :].rearrange("p (c n) -> p c n", n=128)
                P.ts("dve", nU0v, Xbfv[:, :, 128:256], -1.0, None, ALU.mult, None, [Xbf], [d["nU0"]])
                P.copy("pool", Wbdv, Xbfv[:, :, 0:128], [Xbf], [d["Wbd"]])
                yield
                br = bank("rw")
                for c in range(NCH):
                    P.mm(br[:, c * LCH:(c + 1) * LCH], d["Wbd"][:, c * 128:(c + 1) * 128],
                         d["ArbT"][:, c * LCH:(c + 1) * LCH], True, True, [d["Wbd"], d["ArbT"]], [br])
                P.tt("dve", d["Rhat"][:], d["rt32"][:], br[:, 0:TT], ALU.subtract, [d["rt32"], br], [d["Rhat"]])
                bp = bank("rw")
                for c in range(NCH):
                    cs = slice(c * 128, (c + 1) * 128)
                    P.mm(bp[:, cs], d["Wbd"][:, cs], d["BGbd"][:, cs], True, True, [d["Wbd"], d["BGbd"]], [bp])
                for c in range(NCH):
                    cs = slice(c * 128, (c + 1) * 128)
                    P.stt(d["PT"][:, cs], ident_f, d["GL"][:, c:c + 1], bp[:, cs], ALU.mult, ALU.subtract,
                          [cst, d["GL"], bp], [d["PT"]])
                yield

            def seq(pb, inst, c, l=l):
                d = pp[inst]
                T = Tst[l][pb]
                tb_ = banks[4 + inst]
                yb = tb_
                ycols = slice(128 + c * LCH, 128 + (c + 1) * LCH)
                cs = slice(c * 128, (c + 1) * 128)
                cl_ = slice(c * LCH, (c + 1) * LCH)
                P.mm(yb[:, ycols], T[:], d["Rhat"][:, cl_], True, False, [T, d["Rhat"]], [yb])
                P.mm(yb[:, ycols], d["Vbd"][:, cs], d["ArkT"][:, cl_], False, False, [d["Vbd"], d["ArkT"]], [yb])
                P.mm(yb[:, ycols], d["nU0"][:, cs], d["ArbT"][:, cl_], False, True, [d["nU0"], d["ArbT"]], [yb])
                P.mm(tb_[:, 0:128], d["PT"][:, cs], T[:], True, False, [d["PT"], T], [tb_])
                P.mm(tb_[:, 0:128], d["KGbd"][:, cs], d["Vbd"][:, cs], False, False, [d["KGbd"], d["Vbd"]], [tb_])
                P.mm(tb_[:, 0:128], d["BGbd"][:, cs], d["nU0"][:, cs], False, True, [d["BGbd"], d["nU0"]], [tb_])
                P.copy("act", T[:], tb_[:, 0:128], [tb_], [T])

            def fin(pb, inst, l=l, dbg_on=dbg_on):
                assert lru_done[0], "fin emitted before the LRU chain finished (scratch lf[3] still live)"
                d = pp[inst]
                yb = banks[4 + inst]
                y32, yc, ysq, rs = spf[0], spf[1], spf[2], lf[3]
                P.copy("act", y32[:], yb[:, 128:128 + TT], [yb], [y32])
                if dbg_on:
                    dbg_dump("y_rw%d" % pb, y32, y32[:], [128, TT])
                bm = bank("rw")
                P.mm(bm[:, 0:TT], onesbd_f, y32[:], True, True, [cst, y32], [bm])
                P.stt(yc[:], bm[:, 0:TT], -1.0 / 64, y32[:], ALU.mult, ALU.add, [bm, y32], [yc])
                P.act(ysq[:], yc[:], AF.Square, [yc], [ysq])
                bv = bank("rw")
                P.mm(bv[:, 0:TT], onesbd_f, ysq[:], True, True, [cst, ysq], [bv])
                P.act(rs[:], bv[:, 0:TT], AF.Ln, [bv, eps_t[GN_EPS]], [rs], bias=eps_t[GN_EPS][:, 0:1], scale=1.0 / 64)
                P.act(rs[:], rs[:], AF.Exp, [rs], [rs], scale=-0.5)
                P.tt("dve", yc[:], yc[:], rs[:], ALU.mult, [yc, rs], [yc])
                P.ts("dve", yc[:], yc[:], V(l, "lnw", pb), V(l, "lnb", pb), ALU.mult, ALU.add, [yc, vec], [yc])
                P.tt("pool", yc[:], yc[:], d["bonus"][:], ALU.add, [yc, d["bonus"]], [yc])
                P.tt("dve", ycat[:, pb, :], yc[:], sgate[:, pb, :], ALU.mult, [yc, sgv[pb]], [ycv[pb]])

            def rwkv_gen():
                def chain(pb, inst):
                    yield from prep(pb, inst)
                    yield from solve(pb, inst)

                def tail(pbs):
                    for c in range(NCH):
                        for inst, pb in enumerate(pbs):
                            seq(pb, inst, c)
                            yield
                    for inst, pb in enumerate(pbs):
                        fin(pb, inst)
                        yield

                def merge(ga, gb):
                    da = db = False
                    while not (da and db):
                        if not da:
                            try:
                                next(ga)
                                yield
                            except StopIteration:
                                da = True
                        if not db:
                            try:
                                next(gb)
                                yield
                            except StopIteration:
                                db = True

                def half_front(pbs):
                    gA, gB = chain(pbs[0], 0), chain(pbs[1], 1)
                    for v in gA:
                        yield
                        if v == "tok_done":
                            break
                    yield from merge(gA, gB)

                yield from half_front((0, 1))
                t0_ = tail((0, 1))
                gA, gB = chain(2, 0), chain(3, 1)

                def front2a():
                    for v in gA:
                        yield
                        if v == "pre_pp":
                            break
                yield from merge(t0_, front2a())
                for v in gA:
                    yield
                    if v == "tok_done":
                        break
                yield from merge(gA, gB)
                yield from tail((2, 3))

            def s5_gen(l=l, dbg_on=dbg_on):
                Bv = s5B[l][:, :].rearrange("p (r b q m) -> p r b q m", r=2, b=4, q=2)
                Cv = s5C[l][:, :].rearrange("p (r j m) -> p r j m", r=2, j=16)
                rotk, keep = s5rot[l]
                NS = TT // TS5
                yb5 = banks[7]

                def stage0(blk, s, jj, k):
                    hf, jh = jj // 2, jj % 2
                    hs = slice(64 * hf, 64 * hf + 64)
                    tsl = slice(s * TS5, (s + 1) * TS5)
                    bu = banks[k % 2]
                    c0 = 0
                    bre = bu[:, c0:c0 + TS5]
                    bim = bu[:, c0 + TS5:c0 + 2 * TS5]
                    P.mm(bre, Bv[hs, 0, blk, jh, :], ubf[hs, blk, tsl], True, True, [s5B[l], ubf], [bu])
                    P.mm(bim, Bv[hs, 1, blk, jh, :], ubf[hs, blk, tsl], True, True, [s5B[l], ubf], [bu])

                def stage1(blk, s, jj, k):
                    j = blk * 4 + jj
                    (t1, t2, bzr, bzi, zr, zi, t3, t4) = s5f[k]
                    bu = banks[k % 2]
                    c0 = 0
                    bre = bu[:, c0:c0 + TS5]
                    bim = bu[:, c0 + TS5:c0 + 2 * TS5]
                    cosT = s5tab[:, 0, j, :]
                    sinT = s5tab[:, 1, j, :]
                    P.tt("dve", t1[:], bre, cosT, ALU.mult, [bu, s5tab], [t1])
                    P.tt("dve", t2[:], bim, sinT, ALU.mult, [bu, s5tab], [t2])
                    P.tt("dve", t3[:], bim, cosT, ALU.mult, [bu, s5tab], [t3])
                    P.tt("dve", t4[:], bre, sinT, ALU.mult, [bu, s5tab], [t4])
                    P.tt(S5E, bzr[:], t1[:], t2[:], ALU.add, [t1, t2], [bzr])
                    P.tt(S5E, bzi[:], t3[:], t4[:], ALU.subtract, [t3, t4], [bzi])

                def stage2(blk, s, jj, k, kx):
                    j = blk * 4 + jj
                    hf, jh = jj // 2, jj % 2
                    hs = slice(64 * hf, 64 * hf + 64)
                    tsl = slice(s * TS5, (s + 1) * TS5)
                    (t1, t2, bzr, bzi, zr, zi, t3, t4) = s5f[k]
                    sx = s5x[kx]
                    zl = s5zl[k]
                    zv = s5zv[l][j]
                    cosT = s5tab[:, 0, j, :]
                    sinT = s5tab[:, 1, j, :]
                    rho_b = keep[:, 0, j:j + 1].to_broadcast([128, TS5])
                    P.scan(zr[:], rho_b, bzr[:], s5z[l][:, 0, j:j + 1], [keep, bzr, zv], [zr])
                    P.scan(zi[:], rho_b, bzi[:], s5z[l][:, 1, j:j + 1], [keep, bzi, zv], [zi])
                    P.tt("dve", t1[:], zr[:], cosT, ALU.mult, [zr, s5tab], [t1])
                    P.tt("dve", t2[:], zi[:], sinT, ALU.mult, [zi, s5tab], [t2])
                    P.tt(S5E, sx[:, 0, :], t1[:], t2[:], ALU.subtract, [t1, t2], [sx])
                    P.tt("dve", t3[:], zr[:], sinT, ALU.mult, [zr, s5tab], [t3])
                    P.tt(S5E, t4[:], zi[:], cosT, ALU.mult, [zi, s5tab], [t4])
                    P.tt(S5E, sx[:, 1, :], t3[:], t4[:], ALU.add, [t3, t4], [sx])
                    rc = rotk[:, 0, j:j + 1]
                    rs_ = rotk[:, 1, j:j + 1]
                    zlr = zr[:, TS5 - 1:TS5]
                    zli = zi[:, TS5 - 1:TS5]
                    P.ts("dve", zl[:, 0:1], zli, rs_, None, ALU.mult, None, [zi, rotk], [zl])
                    P.ts("dve", zl[:, 1:2], zlr, rs_, None, ALU.mult, None, [zr, rotk], [zl])
                    P.stt(s5z[l][:, 0, j:j + 1], zlr, rc, zl[:, 0:1], ALU.mult, ALU.subtract, [zr, rotk, zl], [zv])
                    P.stt(s5z[l][:, 1, j:j + 1], zli, rc, zl[:, 1:2], ALU.mult, ALU.add, [zi, rotk, zl], [zv])

                def stage3(blk, s, jj, kx):
                    j = blk * 4 + jj
                    hf, jh = jj // 2, jj % 2
                    hs = slice(64 * hf, 64 * hf + 64)
                    tsl = slice(s * TS5, (s + 1) * TS5)
                    sx = s5x[kx]
                    P.mm(yb5[hs, tsl], Cv[:, 0, j, :], sx[:, 0, :], jh == 0, False, [s5C[l], sx], [yb5])
                    P.mm(yb5[hs, tsl], Cv[:, 1, j, :], sx[:, 1, :], False, jh == 1, [s5C[l], sx], [yb5])

                for blk in range(4):
                    units = [(s, jj) for s in range(NS) for jj in range(4)]
                    ks = []
                    for (s, jj) in units:
                        ks.append(s5ctr[0] % NS5SET)
                        s5ctr[0] += 1
                    nU = len(units)
                    stage0(blk, units[0][0], units[0][1], ks[0])
                    stage0(blk, units[1][0], units[1][1], ks[1])
                    stage1(blk, units[0][0], units[0][1], ks[0])
                    yield
                    for u in range(nU):
                        if u + 1 < nU:
                            stage1(blk, units[u + 1][0], units[u + 1][1], ks[u + 1])
                            if u + 2 < nU:
                                stage0(blk, units[u + 2][0], units[u + 2][1], ks[u + 2])
                            yield
                        stage2(blk, units[u][0], units[u][1], ks[u], u % NS5X)
                        if u >= 2:
                            stage3(blk, units[u - 2][0], units[u - 2][1], (u - 2) % NS5X)
                        yield
                    stage3(blk, units[nU - 2][0], units[nU - 2][1], (nU - 2) % NS5X)
                    stage3(blk, units[nU - 1][0], units[nU - 1][1], (nU - 1) % NS5X)
                    ys, x2, q_ = spf
                    P.stt(ys[:], zu[:, blk, :], V(l, "s5d", blk), yb5[:, 0:TT], ALU.mult, ALU.add, [zu, vec, yb5], [ys])
                    if dbg_on:
                        dbg_dump("s5y%d" % blk, ys, ys[:], [128, TT])
                    P.act(x2[:], ys[:], AF.Square, [ys], [x2])
                    P.ts("dve", x2[:], x2[:], 0.044715, 1.0, ALU.mult, ALU.add, [x2], [x2])
                    P.tt("pool", q_[:], x2[:], ys[:], ALU.mult, [x2, ys], [q_])
                    P.act(x2[:], q_[:], AF.Sigmoid, [q_], [x2], scale=2.0 * math.sqrt(2.0 / math.pi))
                    P.tt("dve", mix[:, blk, :], ys[:], x2[:], ALU.mult, [ys, x2], [mixv[blk]])
                    yield
                zgb = ubf
                P.copy("pool", zgb[:], mix[:], mixv, [zgb])
                rg = next_ring()
                P.dma(rg[:, 0:2048].rearrange("p (k n) -> p k n", k=4), glu_w_b[l].rearrange("(k p) n -> p k n", p=128),
                      reads=[wsb], writes=[rg])
                rgv = rg[:, 0:2048].rearrange("p (k n) -> p k n", k=4)
                for ob in range(4):
                    b = banks[0]
                    for k in range(4):
                        P.mm(b[:, 0:TT], rgv[:, k, ob * 128:(ob + 1) * 128], zgb[:, k, :], k == 0, k == 3, [rg, zgb], [b])
                    sg_ = spf[ob % 2]
                    P.act(sg_[:], b[:, 0:TT], AF.Sigmoid, [b, vec], [sg_], bias=V(l, "glub", ob))
                    P.tt("pool", sg_[:], sg_[:], sgate[:, 4 + ob, :], ALU.mult, [sg_, sgv[4 + ob]], [sg_])
                    P.tt("dve", ycat[:, 4 + ob, :], mix[:, ob, :], sg_[:], ALU.mult, [mixv[ob], sg_], [ycv[4 + ob]])
                    yield

            lru_done = [("lru" in SKIP)]

            def lru_gen(l=l, dbg_on=dbg_on):
                for blk in range(4):
                    A, B, C, Dd = lf
                    bl = banks[5]
                    P.ts("dve", A[:], zx[:, blk, 0:TT], V(l, "cw", 0 * 4 + blk), V(l, "cb", blk), ALU.mult, ALU.add,
                         [zx, vec], [A])
                    for j in range(1, 4):
                        P.stt(A[:], zx[:, blk, j:j + TT], V(l, "cw", j * 4 + blk), A[:], ALU.mult, ALU.add,
                              [zx, vec, A], [A])
                    P.copy("act", lb[:], A[:], [A], [lb])
                    yield
                    P.mm(bl[:, 0:TT], lruw[l][:, blk * 128:(blk + 1) * 128], lb[:], True, True, [lruw[l], lb], [bl])
                    P.mm(bl[:, TT:2 * TT], lruw[l][:, (4 + blk) * 128:(5 + blk) * 128], lb[:], True, True,
                         [lruw[l], lb], [bl])
                    P.act(B[:], bl[:, 0:TT], AF.Sigmoid, [bl, vec], [B], bias=V(l, "ba", blk))
                    P.act(C[:], bl[:, TT:2 * TT], AF.Sigmoid, [bl, vec], [C], bias=V(l, "bx", blk))
                    yield
                    P.act(Dd[:], B[:], AF.Exp, [B, lru_c], [Dd], scale=lru_c[:, l, blk:blk + 1])
                    P.act(B[:], B[:], AF.Exp, [B, lru_c], [B], scale=lru_c[:, l, 4 + blk:5 + blk])
                    P.act(B[:], B[:], AF.Ln, [B, one_t], [B], bias=one_t[:, 0:1], scale=-1.0)
                    P.act(B[:], B[:], AF.Exp, [B], [B], scale=0.5)
                    P.tt("pool", C[:], C[:], A[:], ALU.mult, [C, A], [C])
                    P.tt("pool", C[:], C[:], B[:], ALU.mult, [C, B], [C])
                    yield
                    P.scan(A[:], Dd[:], C[:], ch[l][:, blk:blk + 1], [Dd, C, ch[l]], [A])
                    P.copy("act", ch[l][:, blk:blk + 1], A[:, TT - 1:TT], [A], [ch[l]])
                    if dbg_on:
                        dbg_dump("lru%d" % blk, A, A[:], [128, TT])
                    P.tt("pool", ycat[:, 8 + blk, :], A[:], sgate[:, 8 + blk, :], ALU.mult, [A, sgv[8 + blk]], [ycv[8 + blk]])
                    if blk == 3:
                        lru_done[0] = True
                    yield

            gens = []
            if "rwkv" not in SKIP:
                gens.append((rwkv_gen(), GW[0]))
            if "s5" not in SKIP:
                gens.append((s5_gen(), GW[1]))
            if "lru" not in SKIP:
                gens.append((lru_gen(), GW[2]))
            if "serial" in SKIP:
                for g, w in gens:
                    drive([(g, 1)])
            else:
                drive(gens)
            if dbg_on:
                dbg_dump("ycat_rw", ycat, ycat[:, 0:4, :].rearrange("p a t -> p (a t)"), [128, 4 * TT], BF16)

            for oc in range(4):
                r = next_ring()
                P.dma(r[:, 0:12 * 256].rearrange("p (k n) -> p k n", k=12),
                      w_out_b[l].rearrange("(k p) n -> p k n", p=128)[:, :, oc * 256:(oc + 1) * 256],
                      reads=[wsb], writes=[r])
                rv = r[:, 0:12 * 256].rearrange("p (k n) -> p k n", k=12)
                for ob2 in range(2):
                    ob = oc * 2 + ob2
                    b = bank("proj")
                    for k in range(12):
                        P.mm(b[:, 0:TT], rv[:, k, ob2 * 128:(ob2 + 1) * 128], ycat[:, k, :], k == 0, k == 11,
                             [r] + ycv, [b])
                    P.tt("dve", hT[:, ob, :], hT[:, ob, :], b[:, 0:TT], ALU.add, [hT, b], [hT])
            for k in range(8):
                P.copy("act" if k % 2 == 0 else "dve", xn[:, k, :], hT[:, k, :], [hT], [xn])
            r = next_ring()
            P.dma(r[:, 0:2048].rearrange("p (k n) -> p k n", k=2), ple_w_b[l].rearrange("(k p) n -> p k n", p=128),
                  reads=[wsb], writes=[r])
            rv = r[:, 0:2048].rearrange("p (k n) -> p k n", k=2)
            epre = zs
            for ob in range(8):
                b = bank("proj")
                for k in range(2):
                    P.mm(b[:, 0:TT], rv[:, k, ob * 128:(ob + 1) * 128], pbf[:, k, :], k == 0, k == 1, [r, pbf], [b])
                P.copy("act", epre[:, ob, :], b[:, 0:TT], [b], [zsv[ob]])
            rms_rstd(zsv[0:8], lambda k: epre[:, k, :], 8, NORM_EPS)
            for gc in range(2):
                r = next_ring()
                P.dma(r[:, 0:4096].rearrange("p (k n) -> p k n", k=8),
                      ple_gw_b[l].rearrange("(k p) n -> p k n", p=128)[:, :, gc * 512:(gc + 1) * 512],
                      reads=[wsb], writes=[r])
                rv = r[:, 0:4096].rearrange("p (k n) -> p k n", k=8)
                for ob2 in range(4):
                    ob = gc * 4 + ob2
                    b = bank("proj")
                    for k in range(8):
                        P.mm(b[:, 0:TT], rv[:, k, ob2 * 128:(ob2 + 1) * 128], xn[:, k, :], k == 0, k == 7, [r, xn], [b])
                    sg_, e_ = fs[6 + 2 * (ob % 2)], fs[7 + 2 * (ob % 2)]
                    P.act(sg_[:], b[:, 0:TT], AF.Sigmoid, [b], [sg_])
                    P.stt(e_[:], epre[:, ob, :], V(l, "png", ob), rstd[:], ALU.mult, ALU.mult, [zsv[ob], vec, rstd], [e_])
                    P.tt("dve", e_[:], e_[:], sg_[:], ALU.mult, [e_, sg_], [e_])
                    P.tt("dve", hT[:, ob, :], hT[:, ob, :], e_[:], ALU.add, [hT, e_], [hT])
            if dbg_on:
                dbg_dump("h1", hT, hT[:, :, :].rearrange("p a t -> p (a t)"), [128, 8 * TT])
        rms_rstd([hT], lambda k: hT[:, k, :], 8, NORM_EPS)
        fo = 2 * VEC_PER_LAYER
        for k in range(8):
            P.stt(zs[:, k, :], hT[:, k, :], vec[:, fo + k:fo + k + 1], rstd[:], ALU.mult, ALU.mult, [hT, vec, rstd], [zsv[k]])
        out_toks.append(P.dma(oT[:, t0:t0 + TT].rearrange("(k p) t -> p k t", p=128), zs[:, 0:8, :], reads=zsv[0:8]))
    out_toks.extend(dbg_out.values())
    ninst = P.ninst
    P.finish(out_toks)
    return nc, ninst


def pack_shared(inp):
    f = lambda a: np.asarray(a, np.float32)
    vec = np.zeros((128, NVEC), np.float32)
    for l in range(2):
        def put(name, arr, n):
            c = vcol(l, name)
            vec[:, c:c + n] = _pp(arr, n)
        put("ng", f(inp["norm_g"])[l], 8)
        put("mu", f(inp["rwkv_mu"])[l], 13)
        put("w0", f(inp["rwkv_w0"])[l], 4)
        put("a0", f(inp["rwkv_a0"])[l], 4)
        put("kk", f(inp["rwkv_k_k"])[l], 4)
        put("ka", f(inp["rwkv_k_a"])[l], 4)
        put("rk", f(inp["rwkv_r_k"])[l].reshape(512), 4)
        put("lnw", f(inp["rwkv_ln_w"])[l], 4)
        put("lnb", f(inp["rwkv_ln_b"])[l], 4)
        put("s5d", f(inp["s5_d"])[l], 4)
        put("glub", f(inp["s5_glu_b"])[l], 4)
        cw = f(inp["lru_conv_w"])[l]
        c = vcol(l, "cw")
        for j in range(4):
            vec[:, c + 4 * j:c + 4 * j + 4] = _pp(cw[j], 4)
        put("cb", f(inp["lru_conv_b"])[l], 4)
        put("ba", f(inp["lru_ba"])[l], 4)
        put("bx", f(inp["lru_bx"])[l], 4)
        put("lam", f(inp["lru_lambda"])[l], 4)
        put("png", f(inp["ple_norm_g"])[l], 8)
    vec[:, 2 * VEC_PER_LAYER:2 * VEC_PER_LAYER + 8] = _pp(f(inp["final_norm_g"]), 8)

    w2a2 = np.zeros((2, 128, 512), np.float32)
    w2a2[:, 0:64] = f(inp["rwkv_w2"])
    w2a2[:, 64:128] = f(inp["rwkv_a2"])
    lruw = np.zeros((2, 128, 8, 128), np.float32)
    for l in range(2):
        for m, key in enumerate(("lru_wa", "lru_wx")):
            w = f(inp[key])[l]
            for q in range(4):
                for b2 in range(2):
                    lruw[l, b2 * 64:(b2 + 1) * 64, m * 4 + q, b2 * 64:(b2 + 1) * 64] = w[2 * q + b2]
    lruw = lruw.reshape(2, 128, 1024)
    def modes(a):
        a = f(a).reshape(2, 16, 2, 64)
        return np.ascontiguousarray(a.transpose(0, 2, 3, 1).reshape(2, 128, 16))
    s5s = np.zeros((2, 128, 3, 16), np.float32)
    s5s[:, :, 0] = modes(inp["s5_a_re"])
    s5s[:, :, 1] = modes(inp["s5_a_im"])
    ldt = np.broadcast_to(f(inp["s5_log_dt"])[:, :, None], (2, 32, 64))
    s5s[:, :, 2] = modes(ldt)
    s5s = s5s.reshape(2, 128, 48)
    def bmodes(a):
        a = f(a).reshape(2, 16, 2, 64, 16)
        return a.transpose(0, 2, 3, 1, 4).reshape(2, 128, 16, 16)
    s5b = np.stack([bmodes(inp["s5_b_re"]), bmodes(inp["s5_b_im"])], axis=2).reshape(2, 128, 512)
    s5c = np.zeros((2, 128, 2, 16, 64), np.float32)
    for ri, key in enumerate(("s5_c_re", "s5_c_im")):
        c = f(inp[key]).reshape(2, 16, 2, 16, 64)
        for gh in range(2):
            for jh in range(2):
                c0 = 32 * jh + 16 * gh
                s5c[:, gh * 64:(gh + 1) * 64, ri, jh::2, c0:c0 + 16] = c[:, jh::2, gh].transpose(0, 3, 1, 2)
    s5c = s5c.reshape(2, 128, 2048)
    return {
        "w_in": np.ascontiguousarray(f(inp["w_in"])), "w_out": np.ascontiguousarray(f(inp["w_out"])),
        "ple_w": np.ascontiguousarray(f(inp["ple_w"])), "ple_gw": np.ascontiguousarray(f(inp["ple_gate_w"])),
        "glu_w": np.ascontiguousarray(f(inp["s5_glu_w"])), "vec": vec, "cst": make_consts()[0], "msk": make_consts()[1],
        "w2a2": w2a2, "lruw": np.ascontiguousarray(lruw), "s5s": np.ascontiguousarray(s5s),
        "s5b": np.ascontiguousarray(s5b), "s5c": np.ascontiguousarray(s5c),
    }


_NC_CACHE = {}


def run_cores(inp, TC, batches, dbg=None):
    key = (TC, tuple(sorted(dbg)) if dbg else None)
    if key not in _NC_CACHE:
        _NC_CACHE[key] = build_nc(TC, dbg)
    nc, ninst = _NC_CACHE[key]
    shared = pack_shared(inp)
    x = np.asarray(inp["x"], np.float32)
    p = np.asarray(inp["p"], np.float32)
    in_maps = []
    for b in batches:
        m = dict(shared)
        m["xT"] = np.ascontiguousarray(x[b, :TC].T)
        m["pT"] = np.ascontiguousarray(p[:, b, :TC].transpose(0, 2, 1))
        in_maps.append(m)
    res = run_bass_kernel_spmd(nc, in_maps, core_ids=list(range(len(batches))))
    return res


def kernel(**inputs):
    x = np.asarray(inputs["x"])
    B, S, _ = x.shape
    batches = [c % B for c in range(8)]
    res = run_cores(inputs, S, batches)
    out = np.empty((B, S, D), np.float32)
    for b in range(B):
        out[b] = res.results[b]["oT"].T
    return out.astype(x.dtype)
```

```python
import contextlib
import math
import numpy as np
import concourse.bass as bass
import concourse.mybir as mybir
from concourse.bass_utils import run_bass_kernel_spmd

F32 = mybir.dt.float32
BF16 = mybir.dt.bfloat16
ALU = mybir.AluOpType
AF = mybir.ActivationFunctionType

D = 1024
DIN = 4224
DMIX = 1536
DPLE = 256
TT = 256
LCH = 64
NCH = TT // LCH
TS5 = 128
import os
SKIP = set(os.environ.get("KSKIP", "").split(","))
S5E = os.environ.get("KS5E", "pool")
GW = tuple(int(v) for v in os.environ.get("KGW", "1,1,1").split(","))
GN_EPS = 64e-5
NORM_EPS = 1e-6


class Tok:
    __slots__ = ("sem", "val", "eng", "dma")

    def __init__(self, sem, val, eng, dma):
        self.sem, self.val, self.eng, self.dma = sem, val, eng, dma


class Buf:
    def __init__(self, t, name):
        self.t = t
        self.name = name
        self.w = None
        self.r = []

    def __getitem__(self, idx):
        return self.t[idx]


class Prog:
    ENGS = ("pe", "act", "dve", "pool", "sp")

    def __init__(self, nc, n_dma_sems=32):
        self.nc = nc
        self.es = contextlib.ExitStack()
        self.ops = {e: [] for e in self.ENGS}
        self.cnt = {e: 0 for e in self.ENGS}
        self.sem = {e: self.es.enter_context(nc.semaphore("s_" + e)) for e in self.ENGS}
        self.dsem = [self.es.enter_context(nc.semaphore("d%d" % i)) for i in range(n_dma_sems)]
        self.duse = [0] * n_dma_sems
        self.dnext = 0
        self.seen = {e: {} for e in self.ENGS}
        self.nbuf = 0
        self.ninst = 0
        self.stack = [self.es]

    def push(self):
        st = contextlib.ExitStack()
        self.stack.append(st)

    def pop(self):
        self.barrier()
        self.stack.pop().close()

    def barrier(self):
        toks = []
        for f in self.ENGS:
            if self.cnt[f] > 0:
                toks.append(Tok(self.sem[f], self.cnt[f], f, False))
        for i, s in enumerate(self.dsem):
            if self.duse[i] > 0:
                toks.append(Tok(s, 16 * self.duse[i], "dma", True))
        for e in self.ENGS:
            wl = []
            for t in toks:
                if t.eng == e and not t.dma:
                    continue
                k = id(t.sem)
                if self.seen[e].get(k, 0) >= t.val:
                    continue
                self.seen[e][k] = t.val
                wl.append((t.sem, t.val))

            def run(en, wl=wl):
                for (s, v) in wl:
                    en.wait_ge(s, v)
            self.ops[e].append(run)

    def sb(self, shape, dt=F32, name=None):
        self.nbuf += 1
        name = name or ("b%d" % self.nbuf)
        t = self.stack[-1].enter_context(self.nc.sbuf_tensor("sb_" + name, list(shape), dt))
        return Buf(t, name)

    def ps(self, name, dt=F32, cols=512):
        t = self.es.enter_context(self.nc.psum_tensor(name, [128, cols], dt))
        return Buf(t, name)

    def wrap(self, t, name):
        return Buf(t, name)

    def views(self, buf, n):
        return [Buf(buf.t, "%s.v%d" % (buf.name, i)) for i in range(n)]

    def _need(self, eng, tok, waits, is_dma_issue):
        if tok is None:
            return
        if tok.eng == eng and not tok.dma and not is_dma_issue and eng == "pe":
            return
        k = id(tok.sem)
        if self.seen[eng].get(k, 0) >= tok.val:
            return
        cur = waits.get(k)
        if cur is None or cur[1] < tok.val:
            waits[k] = (tok.sem, tok.val)

    def emit(self, eng, fn, reads=(), writes=(), dma=False):
        waits = {}
        for b in reads:
            self._need(eng, b.w, waits, dma)
        for b in writes:
            self._need(eng, b.w, waits, dma)
            for t in b.r:
                self._need(eng, t, waits, dma)
        if dma:
            i = self.dnext
            self.dnext = (self.dnext + 1) % len(self.dsem)
            s = self.dsem[i]
            if self.duse[i] > 0:
                self._need(eng, Tok(s, 16 * self.duse[i], "dma", True), waits, True)
            self.duse[i] += 1
            tok = Tok(s, 16 * self.duse[i], "dma", True)
            inc = 16
        else:
            self.cnt[eng] += 1
            tok = Tok(self.sem[eng], self.cnt[eng], eng, False)
            inc = 1
        wl = list(waits.values())
        for (s, v) in wl:
            self.seen[eng][id(s)] = v
        tsem = tok.sem
        self.ninst += 1 + len(wl)

        def run(e, wl=wl, fn=fn, tsem=tsem, inc=inc):
            for (s, v) in wl:
                e.wait_ge(s, v)
            fn(e).then_inc(tsem, inc)

        self.ops[eng].append(run)
        for b in reads:
            b.r = [t for t in b.r if t.sem is not tok.sem]
            b.r.append(tok)
        for b in writes:
            b.w = tok
            b.r = []
        return tok

    def finish(self, out_toks):
        wl = [(t.sem, t.val) for t in out_toks]

        def run(e, wl=wl):
            for (s, v) in wl:
                e.wait_ge(s, v)

        self.ops["sp"].append(run)
        nc = self.nc
        ops = self.ops
        with nc.Block() as block:
            @block.tensor
            def _(e):
                for f in ops["pe"]:
                    f(e)

            @block.scalar
            def _(e):
                for f in ops["act"]:
                    f(e)

            @block.vector
            def _(e):
                for f in ops["dve"]:
                    f(e)

            @block.gpsimd
            def _(e):
                for f in ops["pool"]:
                    f(e)

            @block.sync
            def _(e):
                for f in ops["sp"]:
                    f(e)
        self.es.close()

    def dma(self, out, in_, reads=(), writes=(), eng="sp", **kw):
        return self.emit(eng, lambda e: e.dma_start(out=out, in_=in_, **kw), reads, writes, dma=True)

    def mm(self, out, lhsT, rhs, start, stop, reads, writes):
        return self.emit("pe", lambda e: e.matmul(out, lhsT, rhs, start=start, stop=stop), reads, writes)

    def act(self, out, in_, func, reads, writes, bias=None, scale=None):
        kw = {}
        if bias is not None:
            kw["bias"] = bias
        if scale is not None:
            kw["scale"] = scale
        return self.emit("act", lambda e: e.activation(out=out, in_=in_, func=func, **kw), reads, writes)

    def tt(self, eng, out, in0, in1, op, reads, writes):
        return self.emit(eng, lambda e: e.tensor_tensor(out=out, in0=in0, in1=in1, op=op), reads, writes)

    def ts(self, eng, out, in0, s1, s2, op0, op1, reads, writes):
        if op1 is None:
            return self.emit(eng, lambda e: e.tensor_scalar(out, in0, s1, None, op0), reads, writes)
        return self.emit(eng, lambda e: e.tensor_scalar(out, in0, s1, s2, op0, op1), reads, writes)

    def stt(self, out, in0, scalar, in1, op0, op1, reads, writes):
        return self.emit("dve", lambda e: e.scalar_tensor_tensor(out, in0, scalar, in1, op0, op1), reads, writes)

    def copy(self, eng, out, in_, reads, writes):
        if eng == "act":
            return self.emit("act", lambda e: e.activation(out=out, in_=in_, func=AF.Copy), reads, writes)
        return self.emit(eng, lambda e: e.tensor_copy(out, in_), reads, writes)

    def memset(self, eng, ap, val, writes):
        return self.emit(eng, lambda e: e.memset(ap, val), (), writes)

    def scan(self, out, d0, d1, init, reads, writes):
        return self.emit("dve", lambda e: e.tensor_tensor_scan(out, d0, d1, init, ALU.mult, ALU.add), reads, writes)

    def recip(self, out, in_, reads, writes):
        return self.emit("dve", lambda e: e.reciprocal(out, in_), reads, writes)


VEC_FIELDS = [("ng", 8), ("mu", 13), ("w0", 4), ("a0", 4), ("kk", 4), ("ka", 4), ("rk", 4), ("lnw", 4),
              ("lnb", 4), ("s5d", 4), ("glub", 4), ("cw", 16), ("cb", 4), ("ba", 4), ("bx", 4), ("lam", 4),
              ("png", 8)]
VEC_PER_LAYER = sum(n for _, n in VEC_FIELDS)
VEC_OFF = {}
_o = 0
for _n, _c in VEC_FIELDS:
    VEC_OFF[_n] = _o
    _o += _c
NVEC = 2 * VEC_PER_LAYER + 8

CST_IDENT = 0
CST_ONESBD = 128
CST_SCAN = 256
NCST = 256 + TT
MSK_USN = 0
MSK_LSN = NCH * 128
MSK_USP = 2 * NCH * 128
MSK_CI = 3 * NCH * 128
NMSK = 3 * NCH * 128 + NCH * LCH


def vcol(l, name, i=0):
    return l * VEC_PER_LAYER + VEC_OFF[name] + i


def _pp(v, n):
    return np.ascontiguousarray(np.asarray(v, np.float32).reshape(n, 128).T)


def make_consts():
    c = np.zeros((128, NCST), np.float32)
    i = np.arange(128)[:, None]
    j = np.arange(128)[None, :]
    c[:, CST_IDENT:CST_IDENT + 128] = (i == j)
    c[:, CST_ONESBD:CST_ONESBD + 128] = ((i // 64) == (j // 64))
    tt = np.arange(TT)[None, :]
    c[:, CST_SCAN:CST_SCAN + TT] = 1.0 * ((tt % LCH) != 0)
    m = np.zeros((128, NMSK), np.float32)
    t = np.arange(64)[None, :]
    for ch in range(NCH):
        m[:, MSK_USN + ch * 128:MSK_USN + (ch + 1) * 128] = -1.0 * (j > i)
        m[:, MSK_LSN + ch * 128:MSK_LSN + (ch + 1) * 128] = -1.0 * (i > j)
        m[:, MSK_USP + ch * 128:MSK_USP + (ch + 1) * 128] = 1.0 * (j > i)
        m[:, MSK_CI + ch * 64:MSK_CI + (ch + 1) * 64] = 1.0 * (t >= (i % 64))
    return c, m


def build_nc(TC, dbg=None):
    assert TC % TT == 0
    NT = TC // TT
    dbg = dbg or set()
    nc = bass.Bass("TRN2", target_bir_lowering=False)

    def din(name, shape, dt=F32):
        return nc.dram_tensor(name, list(shape), dt, kind="ExternalInput").ap()

    xT = din("xT", [D, TC])
    pT = din("pT", [2, DPLE, TC])
    w_in = din("w_in", [2, D, DIN])
    w_out = din("w_out", [2, DMIX, D])
    ple_w = din("ple_w", [2, DPLE, D])
    ple_gw = din("ple_gw", [2, D, D])
    glu_w = din("glu_w", [2, 512, 512])
    vec_d = din("vec", [128, NVEC])
    cst_d = din("cst", [128, NCST])
    msk_d = din("msk", [128, NMSK])
    w2a2_d = din("w2a2", [2, 128, 512])
    lruw_d = din("lruw", [2, 128, 8 * 128])
    s5s_d = din("s5s", [2, 128, 3 * 16])
    s5b_d = din("s5b", [2, 128, 2 * 16 * 16])
    s5c_d = din("s5c", [2, 128, 2 * 16 * 64])
    oT = nc.dram_tensor("oT", [D, TC], F32, kind="ExternalOutput").ap()
    dbg_out = {}

    def dram_int(name, shape, dt):
        return nc.dram_tensor(name, list(shape), dt, kind="Internal").ap()

    w_in_b = dram_int("w_in_b", [2, D, DIN], BF16)
    w_out_b = dram_int("w_out_b", [2, DMIX, D], BF16)
    ple_w_b = dram_int("ple_w_b", [2, DPLE, D], BF16)
    ple_gw_b = dram_int("ple_gw_b", [2, D, D], BF16)
    glu_w_b = dram_int("glu_w_b", [2, 512, 512], BF16)
    s5tab_d = dram_int("s5tab", [2, 128, 2 * 16 * TS5], F32)

    P = Prog(nc)
    wsb = P.wrap(None, "wscratch")
    tabsb = P.wrap(None, "s5tabscr")

    def dbg_dump(name, buf, ap, shape, dt=F32):
        if name not in dbg:
            return
        o = nc.dram_tensor("dbg_" + name, list(shape), dt, kind="ExternalOutput").ap()
        dbg_out[name] = P.dma(o, ap, reads=[buf])

    vec = P.sb([128, NVEC], F32, "vec")
    cst = P.sb([128, NCST], F32, "cst")
    P.dma(vec[:], vec_d, writes=[vec])
    P.dma(cst[:], cst_d, writes=[cst])
    cstb = P.sb([128, 128], BF16, "cstb")
    P.copy("dve", cstb[:], cst[:, CST_IDENT:CST_IDENT + 128], [cst], [cstb])
    mskb = P.sb([128, NMSK], BF16, "mskb")
    ident_f = cst[:, CST_IDENT:CST_IDENT + 128]
    ident_b = cstb[:, 0:128]
    onesbd_f = cst[:, CST_ONESBD:CST_ONESBD + 128]
    ones_f = P.sb([128, 128], F32, "ones_f")
    P.memset("pool", ones_f[:], 1.0, [ones_f])
    one_t = P.sb([128, 1], F32, "one_t")
    P.memset("pool", one_t[:], 1.0, [one_t])

    def V(l, name, i=0, n=1):
        c = vcol(l, name, i)
        return vec[:, c:c + n]

    for l in range(2):
        for (src, dst, rows) in ((w_in, w_in_b, D), (w_out, w_out_b, DMIX), (ple_w, ple_w_b, DPLE),
                                 (ple_gw, ple_gw_b, D), (glu_w, glu_w_b, 512)):
            for r0 in range(0, rows, 128):
                P.dma(dst[l, r0:r0 + 128, :], src[l, r0:r0 + 128, :], writes=[wsb], eng="pool",
                      max_dma_last_dim=4096)

    w2a2 = []
    lruw = []
    for l in range(2):
        w2a2.append(P.sb([128, 512], BF16, "w2a2_%d" % l))
        lruw.append(P.sb([128, 1024], BF16, "lruw_%d" % l))
    lru_c = P.sb([128, 2, 8], F32, "lru_c")
    s5B = [P.sb([128, 2 * 4 * 2 * 128], BF16, "s5B%d" % l) for l in range(2)]
    s5C = [P.sb([128, 2048], BF16, "s5C%d" % l) for l in range(2)]
    s5keep = [P.sb([128, 3, 16], F32, "s5keep%d" % l) for l in range(2)]
    s5rotb = [P.sb([128, 2, 16], F32, "s5rot%d" % l) for l in range(2)]
    ps_misc = P.ps("ps7")
    P.push()
    stage = P.sb([128, 1024], F32, "stage")
    mstage = P.sb([128, NMSK], F32, "mstage")
    P.dma(mstage[:], msk_d, writes=[mstage])
    P.copy("act", mskb[:], mstage[:], [mstage], [mskb])
    for l in range(2):
        P.dma(stage[:, 0:512], w2a2_d[l], writes=[stage])
        P.copy("act", w2a2[l][:], stage[:, 0:512], [stage], [w2a2[l]])
        P.dma(stage[:], lruw_d[l], writes=[stage])
        P.copy("act", lruw[l][:], stage[:], [stage], [lruw[l]])

    for l in range(2):
        tmp = P.sb([128, 4], F32, "lrutmp%d" % l)
        P.act(tmp[:], V(l, "lam", 0, 4), AF.Exp, [vec], [tmp], scale=-1.0)
        P.act(tmp[:], tmp[:], AF.Ln, [tmp, one_t], [tmp], bias=one_t[:, 0:1])
        P.ts("dve", lru_c[:, l, 0:4], tmp[:], -8.0, None, ALU.mult, None, [tmp], [lru_c])
        P.ts("dve", lru_c[:, l, 4:8], tmp[:], -16.0, None, ALU.mult, None, [tmp], [lru_c])

    s5rot = []
    for l in range(2):
        s5s = P.sb([128, 48], F32, "s5s%d" % l)
        P.dma(s5s[:], s5s_d[l], writes=[s5s])
        a_re = s5s[:, 0:16]
        a_im = s5s[:, 16:32]
        ldt = s5s[:, 32:48]
        w = P.sb([128, 16, 16], F32, "s5w%d" % l)
        R = [w]

        def row(i):
            return w[:, i, :]
        dt_, rho, th, cc, ss, t1, t2, lr, li, den, qre, qim, nr = [row(i) for i in range(13)]
        P.act(dt_, ldt, AF.Exp, [s5s], R)
        P.tt("dve", rho, a_re, dt_, ALU.mult, [s5s] + R, R)
        P.act(rho, rho, AF.Exp, R, R)
        P.tt("dve", th, a_im, dt_, ALU.mult, [s5s] + R, R)
        hp = P.sb([128, 1], F32, "halfpi%d" % l)
        P.memset("dve", hp[:], math.pi / 2, [hp])
        P.act(cc, th, AF.Sin, R + [hp], R, bias=hp[:, 0:1], scale=1.0 / 16)
        P.act(ss, th, AF.Sin, R, R, scale=1.0 / 16)

        def csq(c_, s_):
            P.tt("dve", t1, c_, c_, ALU.mult, R, R)
            P.tt("dve", t2, s_, s_, ALU.mult, R, R)
            P.stt(s_, c_, 2.0, s_, ALU.mult, ALU.mult, R, R)
            P.tt("dve", c_, t1, t2, ALU.subtract, R, R)
        for _ in range(4):
            csq(cc, ss)
        P.tt("dve", lr, rho, cc, ALU.mult, R, R)
        P.tt("dve", li, rho, ss, ALU.mult, R, R)
        P.tt("dve", t1, a_re, a_re, ALU.mult, [s5s] + R, R)
        P.tt("dve", t2, a_im, a_im, ALU.mult, [s5s] + R, R)
        P.tt("dve", den, t1, t2, ALU.add, R, R)
        P.recip(den, den, R, R)
        P.ts("dve", nr, lr, -1.0, None, ALU.add, None, R, R)
        P.tt("dve", t1, nr, a_re, ALU.mult, [s5s] + R, R)
        P.tt("dve", t2, li, a_im, ALU.mult, [s5s] + R, R)
        P.tt("dve", t1, t1, t2, ALU.add, R, R)
        P.tt("dve", qre, t1, den, ALU.mult, R, R)
        P.tt("dve", t1, li, a_re, ALU.mult, [s5s] + R, R)
        P.tt("dve", t2, nr, a_im, ALU.mult, [s5s] + R, R)
        P.tt("dve", t1, t1, t2, ALU.subtract, R, R)
        P.tt("dve", qim, t1, den, ALU.mult, R, R)
        keep = s5keep[l]
        P.copy("dve", keep[:, 0, :], rho, R, [keep])
        P.copy("dve", keep[:, 1, :], cc, R, [keep])
        P.copy("dve", keep[:, 2, :], ss, R, [keep])

        sbf = P.sb([128, 512], F32, "s5b_in%d" % l)
        P.dma(sbf[:], s5b_d[l], writes=[sbf])
        bre = sbf[:, 0:256].rearrange("p (j h) -> p j h", h=16)
        bim = sbf[:, 256:512].rearrange("p (j h) -> p j h", h=16)
        Bt = s5B[l]
        Btv = Bt[:, :].rearrange("p (r b q m) -> p r b q m", r=2, b=4, q=2)
        bpad = P.sb([128, 2, 128], BF16, "s5bpad%d" % l)
        tb = P.sb([128, 2, 16], F32, "s5tb%d" % l)
        for j in range(16):
            P.ts("dve", tb[:, 0, :], bim[:, j, :], qim[:, j:j + 1], None, ALU.mult, None, [sbf] + R, [tb])
            P.stt(tb[:, 0, :], bre[:, j, :], qre[:, j:j + 1], tb[:, 0, :], ALU.mult, ALU.subtract, [sbf, tb] + R, [tb])
            P.ts("dve", tb[:, 1, :], bre[:, j, :], qim[:, j:j + 1], None, ALU.mult, None, [sbf] + R, [tb])
            P.stt(tb[:, 1, :], bim[:, j, :], qre[:, j:j + 1], tb[:, 1, :], ALU.mult, ALU.add, [sbf, tb] + R, [tb])
            P.memset("pool", bpad[:], 0.0, [bpad])
            for gh in range(2):
                col0 = 32 * (j % 4) + gh * 16
                for ri in range(2):
                    P.copy("pool", bpad[gh * 64:(gh + 1) * 64, ri, col0:col0 + 16],
                           tb[gh * 64:(gh + 1) * 64, ri, :], [tb], [bpad])
            for ri in range(2):
                P.mm(ps_misc[:, ri * 128:(ri + 1) * 128], bpad[:, ri, :], ident_b, True, True, [bpad, cstb], [ps_misc])
            hf = (j % 4) // 2
            for ri in range(2):
                P.copy("act", Btv[64 * hf:64 * hf + 64, ri, j // 4, j % 2, :],
                       ps_misc[64 * hf:64 * hf + 64, ri * 128:(ri + 1) * 128], [ps_misc], [Bt])

        scf = P.sb([128, 2048], F32, "s5c_in%d" % l)
        P.dma(scf[:], s5c_d[l], writes=[scf])
        Ct = s5C[l]
        P.copy("act", Ct[:, 0:1024], scf[:, 0:1024], [scf], [Ct])
        P.ts("dve", Ct[:, 1024:2048], scf[:, 1024:2048], -1.0, None, ALU.mult, None, [scf], [Ct])

        tab = P.sb([128, 2, 16, TS5], F32, "s5tabb%d" % l)
        P.memset("pool", tab[:, 0, :, 0:1], 1.0, [tab])
        P.memset("pool", tab[:, 1, :, 0:1], 0.0, [tab])
        ec = P.sb([128, 2, 16], F32, "s5ec%d" % l)
        P.copy("dve", ec[:, 0, :], cc, R, [ec])
        P.copy("dve", ec[:, 1, :], ss, R, [ec])
        m = 1
        while m < TS5:
            for j in range(16):
                cj = ec[:, 0, j:j + 1]
                sj = ec[:, 1, j:j + 1]
                src_c = tab[:, 0, j, 0:m]
                src_s = tab[:, 1, j, 0:m]
                dst_c = tab[:, 0, j, m:2 * m]
                dst_s = tab[:, 1, j, m:2 * m]
                P.ts("dve", dst_c, src_s, sj, None, ALU.mult, None, [tab, ec], [tab])
                P.stt(dst_c, src_c, cj, dst_c, ALU.mult, ALU.subtract, [tab, ec], [tab])
                P.ts("dve", dst_s, src_c, sj, None, ALU.mult, None, [tab, ec], [tab])
                P.stt(dst_s, src_s, cj, dst_s, ALU.mult, ALU.add, [tab, ec], [tab])
            e_c = ec[:, 0, :]
            e_s = ec[:, 1, :]
            P.tt("dve", t1, e_c, e_c, ALU.mult, [ec] + R, R)
            P.tt("dve", t2, e_s, e_s, ALU.mult, [ec] + R, R)
            P.stt(e_s, e_c, 2.0, e_s, ALU.mult, ALU.mult, [ec], [ec])
            P.tt("dve", e_c, t1, t2, ALU.subtract, R, [ec])
            m *= 2
        rot = s5rotb[l]
        P.copy("dve", rot[:], ec[:], [ec], [rot])
        s5rot.append((rot, keep))
        P.dma(s5tab_d[l], tab[:, :, :, :].rearrange("p a j t -> p (a j t)"), reads=[tab], writes=[tabsb])
        dbg_dump("s5w%d" % l, w, w[:, :, :].rearrange("p a b -> p (a b)"), [128, 256])
        dbg_dump("s5keep%d" % l, keep, keep[:, :, :].rearrange("p a b -> p (a b)"), [128, 48])
        dbg_dump("s5tab%d" % l, tab, tab[:, :, :, :].rearrange("p a j t -> p (a j t)"), [128, 2 * 16 * TS5])
        dbg_dump("s5B%d" % l, Bt, Bt[:, :], [128, 2048], BF16)
    P.pop()

    hT = P.sb([128, 8, TT], F32, "hT")
    xn = P.sb([128, 8, TT], BF16, "xn")
    zst = [P.sb([128, 1 + TT], F32, "zst%d" % i) for i in range(2)]
    zs = P.sb([128, 13, TT], F32, "zs")
    zsv = P.views(zs, 13)
    zu = P.sb([128, 4, TT], F32, "zu")
    zx = P.sb([128, 4, 3 + TT], F32, "zx")
    sgate = P.sb([128, 12, TT], BF16, "sgate")
    sgv = P.views(sgate, 12)
    ycat = P.sb([128, 12, TT], BF16, "ycat")
    ycv = P.views(ycat, 12)
    pbf = P.sb([128, 2, TT], BF16, "pbf")
    ring = [P.sb([128, 4096], BF16, "ring%d" % i) for i in range(3)]
    ringi = [0]
    s5tab = P.sb([128, 2, 16, TS5], F32, "s5tab")
    banks = [P.ps("ps%d" % i) for i in range(7)] + [ps_misc]
    rot_i = {"proj": 0, "rw": 0}

    def bank(group):
        ids = (0, 1) if group == "proj" else (2, 3, 6)
        i = rot_i[group]
        rot_i[group] = (i + 1) % len(ids)
        return banks[ids[i]]

    def next_ring():
        r = ring[ringi[0]]
        ringi[0] = (ringi[0] + 1) % 3
        return r

    cz = [P.sb([128, 13], F32, "cz%d" % l) for l in range(2)]
    cl = [P.sb([128, 4, 3], F32, "cl%d" % l) for l in range(2)]
    ch = [P.sb([128, 4], F32, "ch%d" % l) for l in range(2)]
    s5z = [P.sb([128, 2, 16], F32, "s5z%d" % l) for l in range(2)]
    s5zv = [P.views(s5z[l], 16) for l in range(2)]
    Tst = [[P.sb([128, 128], BF16, "T%d_%d" % (l, pb)) for pb in range(4)] for l in range(2)]
    for l in range(2):
        P.memset("pool", cz[l][:], 0.0, [cz[l]])
        P.memset("pool", cl[l][:], 0.0, [cl[l]])
        P.memset("pool", ch[l][:], 0.0, [ch[l]])
        P.memset("pool", s5z[l][:], 0.0, s5zv[l])
        for pb in range(4):
            P.memset("pool", Tst[l][pb][:], 0.0, [Tst[l][pb]])

    NF = 17
    fs = [P.sb([128, TT], F32, "rf%d" % i) for i in range(NF)]
    pad_names = ["RTp", "KTp", "CTp", "BTp", "VTp", "KGp", "BGp"]
    pads = {n: P.sb([128, NCH * 128], BF16, n) for n in pad_names}
    for n in pad_names:
        P.memset("pool", pads[n][:], 0.0, [pads[n]])
    RTc = P.sb([128, TT], BF16, "RTc")
    tanh_wd = P.sb([128, TT], BF16, "tanhwd")
    Blev_i = [[P.sb([128, NCH * 128], BF16, "Blev%d_%d" % (i, k)) for i in range(2)] for k in range(2)]
    BTlev_i = [[P.sb([128, NCH * 128], BF16, "BTlev%d_%d" % (i, k)) for i in range(2)] for k in range(2)]
    AkkT_i = [P.sb([128, NCH * 128], BF16, "AkkT_%d" % k) for k in range(2)]
    Xbf_i = [P.sb([128, NCH * 256], BF16, "Xbf_%d" % k) for k in range(2)]
    NPI = 2
    pp = []
    for i in range(NPI):
        d = {}
        for n in ("Vbd", "KGbd", "BGbd", "PT", "nU0", "Wbd"):
            d[n] = P.sb([128, NCH * 128], BF16, "%s_%d" % (n, i))
        for n in ("Rhat", "ArkT", "ArbT"):
            d[n] = P.sb([128, TT], BF16, "%s_%d" % (n, i))
        d["bonus"] = P.sb([128, TT], F32, "bonus_%d" % i)
        d["GL"] = P.sb([128, NCH], F32, "GL_%d" % i)
        d["rt32"] = P.sb([128, TT], F32, "rt32_%d" % i)
        pp.append(d)
    mix = P.sb([128, 4, TT], F32, "mix")
    mixv = P.views(mix, 4)

    NS5SET = 2
    s5f = [[P.sb([128, TS5], F32, "s5f%d_%d" % (k, i)) for i in range(8)] for k in range(NS5SET)]
    NS5X = 3
    s5x = [P.sb([128, 2, TS5], BF16, "s5x%d" % k) for k in range(NS5X)]
    s5zl = [P.sb([128, 2], F32, "s5zl%d" % k) for k in range(NS5SET)]
    spf = [P.sb([128, TT], F32, "spf%d" % i) for i in range(3)]
    lf = [P.sb([128, TT], F32, "lf%d" % i) for i in range(4)]
    ubf = P.sb([128, 4, TT], BF16, "ubf")
    lb = P.sb([128, TT], BF16, "lb")
    rstd = P.sb([128, TT], F32, "rstd")
    sq = P.sb([128, 4, TT], BF16, "sq")
    ones_b = P.sb([128, 128], BF16, "ones_b")
    P.memset("pool", ones_b[:], 1.0, [ones_b])

    out_toks = []
    eps_t = {}
    for e_ in (NORM_EPS, GN_EPS):
        t = P.sb([128, 1], F32, "eps%d" % len(eps_t))
        P.memset("pool", t[:], e_, [t])
        eps_t[e_] = t
    neg_half = -math.exp(-0.5)

    def rms_rstd(src_bufs, src_ap_fn, nblk, eps):
        b = bank("proj")
        for k in range(nblk):
            P.act(sq[:, k % 4, :], src_ap_fn(k), AF.Square, src_bufs, [sq])
            P.mm(b[:, 0:TT], ones_b[:], sq[:, k % 4, :], k == 0, k == nblk - 1, [ones_b, sq], [b])
        P.act(rstd[:], b[:, 0:TT], AF.Ln, [b, eps_t[eps]], [rstd], bias=eps_t[eps][:, 0:1], scale=1.0 / (nblk * 128))
        P.act(rstd[:], rstd[:], AF.Exp, [rstd], [rstd], scale=-0.5)

    def drive(items):
        active = list(items)
        while active:
            for item in list(active):
                g, w = item
                for _ in range(w):
                    try:
                        next(g)
                    except StopIteration:
                        active.remove(item)
                        break

    s5ctr = [0]

    for it in range(NT):
        t0 = it * TT
        first = (it == 0)
        P.dma(hT[:], xT[:, t0:t0 + TT].rearrange("(k p) t -> p k t", p=128), writes=[hT])
        for l in range(2):
            dbg_on = first and l == 0
            P.dma(pbf[:], pT[l, :, t0:t0 + TT].rearrange("(k p) t -> p k t", p=128), writes=[pbf], eng="pool")
            P.dma(s5tab[:, :, :, :].rearrange("p a j t -> p (a j t)"), s5tab_d[l], reads=[tabsb], writes=[s5tab])
            rms_rstd([hT], lambda k: hT[:, k, :], 8, NORM_EPS)
            for k in range(8):
                P.stt(xn[:, k, :], hT[:, k, :], V(l, "ng", k), rstd[:], ALU.mult, ALU.mult, [hT, vec, rstd], [xn])
            if dbg_on:
                dbg_dump("xn", xn, xn[:, :, :].rearrange("p a t -> p (a t)"), [128, 8 * TT], BF16)

            wchunk = {}

            def in_block(cb, l=l, wchunk=wchunk):
                ci = cb // 4
                if ci not in wchunk:
                    r = next_ring()
                    ncol = 512 if ci < 8 else 128
                    P.dma(r[:, 0:8 * ncol].rearrange("p (k n) -> p k n", k=8),
                          w_in_b[l].rearrange("(k p) n -> p k n", p=128)[:, :, ci * 512:ci * 512 + ncol],
                          reads=[wsb], writes=[r])
                    wchunk[ci] = (r, ncol)
                r, ncol = wchunk[ci]
                rv = r[:, 0:8 * ncol].rearrange("p (k n) -> p k n", k=8)
                c0 = (cb % 4) * 128
                b = bank("proj")
                for k in range(8):
                    P.mm(b[:, 0:TT], rv[:, k, c0:c0 + 128], xn[:, k, :], k == 0, k == 7, [r, xn], [b])
                return b

            for cb in range(13):
                st = zst[cb % 2]
                b = in_block(cb)
                P.copy("act", st[:, 0:1], cz[l][:, cb:cb + 1], [cz[l]], [st])
                P.copy("act", st[:, 1:1 + TT], b[:, 0:TT], [b], [st])
                P.copy("act", cz[l][:, cb:cb + 1], st[:, TT:TT + 1], [st], [cz[l]])
                d_ = lf[cb % 2]
                P.tt("pool", d_[:], st[:, 0:TT], st[:, 1:1 + TT], ALU.subtract, [st], [d_])
                P.stt(zs[:, cb, :], d_[:], V(l, "mu", cb), st[:, 1:1 + TT], ALU.mult, ALU.add, [d_, vec, st], [zsv[cb]])
            if dbg_on:
                dbg_dump("zs", zs, zs[:, :, :].rearrange("p a t -> p (a t)"), [128, 13 * TT])
            P.act(tanh_wd[0:64, :], zs[0:64, 12, :], AF.Tanh, [zsv[12]], [tanh_wd])
            P.copy("act", tanh_wd[64:128, :], zs[64:128, 12, :], [zsv[12]], [tanh_wd])
            for blk in range(4):
                b = in_block(13 + blk)
                P.act(sgate[:, blk, :], b[:, 0:TT], AF.Silu, [b], [sgv[blk]])
            for blk in range(4):
                b = in_block(17 + blk)
                P.copy("act", zu[:, blk, :], b[:, 0:TT], [b], [zu])
            P.copy("pool", ubf[:], zu[:], [zu], [ubf])
            for blk in range(4):
                b = in_block(21 + blk)
                P.act(sgate[:, 4 + blk, :], b[:, 0:TT], AF.Silu, [b], [sgv[4 + blk]])
            P.copy("act", zx[:, :, 0:3], cl[l][:, :, :], [cl[l]], [zx])
            for blk in range(4):
                b = in_block(25 + blk)
                P.copy("act", zx[:, blk, 3:3 + TT], b[:, 0:TT], [b], [zx])
            P.copy("act", cl[l][:, :, :], zx[:, :, TT:TT + 3], [zx], [cl[l]])
            for blk in range(4):
                b = in_block(29 + blk)
                P.act(sgate[:, 8 + blk, :], b[:, 0:TT], AF.Silu, [b], [sgv[8 + blk]])

            def prep(pb, inst, l=l, dbg_on=dbg_on):
                d = pp[inst]
                Blev, BTlev, AkkT = Blev_i[inst], BTlev_i[inst], AkkT_i[inst]
                r_ = zs[:, pb, :]
                k_ = zs[:, 4 + pb, :]
                v_ = zs[:, 8 + pb, :]
                zr_, zk_, zv_ = zsv[pb], zsv[4 + pb], zsv[8 + pb]
                (sg, ld, a_, kk_, kk2, sqk, kap, t1, kp, b_, lg, eg, ieg, eg1, dl, egl, rk) = fs[:17]
                rt32 = d["rt32"]
                cols = slice(pb * 128, (pb + 1) * 128)
                bw = bank("rw")
                P.mm(bw[:, 0:TT], w2a2[l][0:64, cols], tanh_wd[0:64, :], True, True, [w2a2[l], tanh_wd], [bw])
                P.act(sg[:], bw[:, 0:TT], AF.Sigmoid, [bw, vec], [sg], bias=V(l, "w0", pb))
                P.ts("dve", ld[:], sg[:], neg_half, None, ALU.mult, None, [sg], [ld])
                ba_ = bank("rw")
                P.mm(ba_[:, 0:TT], w2a2[l][64:128, cols], tanh_wd[64:128, :], True, True, [w2a2[l], tanh_wd], [ba_])
                P.act(a_[:], ba_[:, 0:TT], AF.Sigmoid, [ba_, vec], [a_], bias=V(l, "a0", pb))
                yield
                P.scan(lg[:], cst[:, CST_SCAN:CST_SCAN + TT], ld[:], 0.0, [cst, ld], [lg])
                P.ts("dve", kk_[:], k_, V(l, "kk", pb), None, ALU.mult, None, [zk_, vec], [kk_])
                P.tt("pool", kk2[:], kk_[:], kk_[:], ALU.mult, [kk_], [kk2])
                P.act(eg[:], lg[:], AF.Exp, [lg], [eg])
                yield
                bs = bank("rw")
                P.mm(bs[:, 0:TT], onesbd_f, kk2[:], True, True, [cst, kk2], [bs])
                P.act(sqk[:], bs[:, 0:TT], AF.Sqrt, [bs], [sqk])
                P.act(ieg[:], lg[:], AF.Exp, [lg], [ieg], scale=-1.0)
                P.tt("pool", eg1[:], lg[:], ld[:], ALU.subtract, [lg, ld], [eg1])
                P.act(eg1[:], eg1[:], AF.Exp, [eg1], [eg1])
                P.ts("dve", sqk[:], sqk[:], 1e-12, None, ALU.max, None, [sqk], [sqk])
                P.recip(sqk[:], sqk[:], [sqk], [sqk])
                P.tt("pool", kap[:], kk_[:], sqk[:], ALU.mult, [kk_, sqk], [kap])
                yield
                P.ts("dve", t1[:], a_[:], -1.0, V(l, "ka", pb), ALU.add, ALU.mult, [a_, vec], [t1])
                P.stt(kp[:], t1[:], 1.0, k_, ALU.add, ALU.mult, [t1, zk_], [kp])
                P.tt("pool", b_[:], kap[:], a_[:], ALU.mult, [kap, a_], [b_])
                lg3 = lg[:, :].rearrange("p (c t) -> p c t", t=LCH)
                P.tt("dve", dl[:, :].rearrange("p (c t) -> p c t", t=LCH), lg3[:, :, LCH - 1:LCH].to_broadcast([128, NCH, LCH]),
                     lg3, ALU.subtract, [lg], [dl])
                P.act(egl[:], dl[:], AF.Exp, [dl], [egl])
                P.copy("act", d["GL"][:, :], eg[:, :].rearrange("p (c t) -> p c t", t=LCH)[:, :, LCH - 1], [eg], [d["GL"]])
                yield
                P.tt("dve", rt32[:], r_, eg[:], ALU.mult, [zr_, eg], [rt32])
                P.copy("act", RTc[:], rt32[:], [rt32], [RTc])

                def padw(name, eng, in0, in1, rd):
                    t = pads[name]
                    tv = t[:, :].rearrange("p (c h t) -> p c h t", c=NCH, h=2)
                    for hh in range(2):
                        ps_ = slice(hh * 64, (hh + 1) * 64)
                        o = tv[ps_, :, hh, :]
                        i0 = in0[ps_, :].rearrange("p (c t) -> p c t", t=LCH)
                        if in1 is None:
                            P.copy(eng, o, i0, rd, [t])
                        else:
                            i1 = in1[ps_, :].rearrange("p (c t) -> p c t", t=LCH)
                            P.tt(eng, o, i0, i1, ALU.mult, rd, [t])
                padw("RTp", "act", rt32, None, [rt32])
                padw("KTp", "dve", kp, ieg, [kp, ieg])
                padw("CTp", "pool", kap, eg1, [kap, eg1])
                yield
                padw("BTp", "dve", b_, ieg, [b_, ieg])
                padw("VTp", "act", zs[:, 8 + pb, :], None, [zv_])
                padw("KGp", "pool", kp, egl, [kp, egl])
                padw("BGp", "dve", b_, egl, [b_, egl])
                yield "pre_pp"
                P.stt(rk[:], r_, V(l, "rk", pb), kp[:], ALU.mult, ALU.mult, [zr_, vec, kp], [rk])
                yield
                bb = bank("rw")
                P.mm(bb[:, 0:TT], onesbd_f, rk[:], True, True, [cst, rk], [bb])
                P.tt("dve", d["bonus"][:], bb[:, 0:TT], v_, ALU.mult, [bb, zv_], [d["bonus"]])
                if dbg_on and pb == 0:
                    dbg_dump("lg", lg, lg[:], [128, TT])
                    dbg_dump("kap", kap, kap[:], [128, TT])
                    dbg_dump("kp", kp, kp[:], [128, TT])
                    dbg_dump("a", a_, a_[:], [128, TT])
                yield

                def chunkmm(dst_bank, lname, rname, rbuf=None, rcols=128):
                    lt = pads[lname]
                    for c in range(NCH):
                        if rbuf is None:
                            rb_ = pads[rname]
                            rap = rb_[:, c * 128:(c + 1) * 128]
                        else:
                            rb_ = rbuf
                            rap = rbuf[:, c * rcols:(c + 1) * rcols]
                        P.mm(dst_bank[:, c * rcols:(c + 1) * rcols], lt[:, c * 128:(c + 1) * 128], rap, True, True,
                             [lt, rb_], [dst_bank])

                def masked(dst, src_bank, mcol, w):
                    n = NCH * w
                    P.tt("dve", dst[:, 0:n], src_bank[:, 0:n], mskb[:, mcol:mcol + n], ALU.mult, [src_bank, mskb], [dst])
                b1 = bank("rw")
                chunkmm(b1, "BTp", "CTp")
                masked(BTlev[0], b1, MSK_USN, 128)
                b2 = bank("rw")
                chunkmm(b2, "CTp", "BTp")
                masked(Blev[0], b2, MSK_LSN, 128)
                yield
                b3 = bank("rw")
                chunkmm(b3, "KTp", "CTp")
                masked(AkkT, b3, MSK_USP, 128)
                b4 = bank("rw")
                chunkmm(b4, "KTp", None, RTc, LCH)
                masked(d["ArkT"], b4, MSK_CI, LCH)
                b5 = bank("rw")
                chunkmm(b5, "BTp", None, RTc, LCH)
                masked(d["ArbT"], b5, MSK_CI, LCH)
                yield

            def tokmajor(src_name, dst_buf, dst_ap, eng):
                bt_ = bank("rw")
                lt = pads[src_name]
                for c in range(NCH):
                    P.mm(bt_[:, c * 128:(c + 1) * 128], lt[:, c * 128:(c + 1) * 128], ident_b, True, True,
                         [lt, cstb], [bt_])
                P.copy(eng, dst_ap, bt_[:, 0:NCH * 128] if len(dst_ap.shape) == 2 else
                       bt_[:, 0:NCH * 128].rearrange("p (c n) -> p c n", n=128), [bt_], [dst_buf])

            def solve(pb, inst, l=l):
                d = pp[inst]
                Blev, BTlev, AkkT, Xbf = Blev_i[inst], BTlev_i[inst], AkkT_i[inst], Xbf_i[inst]
                Xbfv = Xbf[:, :].rearrange("p (c n) -> p c n", n=256)
                tokmajor("VTp", d["Vbd"], d["Vbd"][:, :], "act")
                tokmajor("KGp", d["KGbd"], d["KGbd"][:, :], "act")
                yield
                tokmajor("BGp", d["BGbd"], d["BGbd"][:, :], "act")
                tokmajor("CTp", Xbf, Xbfv[:, :, 0:128], "act")
                bt_ = bank("rw")
                for c in range(NCH):
                    cs = slice(c * 128, (c + 1) * 128)
                    P.mm(bt_[:, cs], AkkT[:, cs], d["Vbd"][:, cs], True, True, [AkkT, d["Vbd"]], [bt_])
                P.copy("act", Xbfv[:, :, 128:256], bt_[:, 0:NCH * 128].rearrange("p (c n) -> p c n", n=128), [bt_], [Xbf])
                yield "tok_done"
                cur = 0
                NLEV = 6
                for lev in range(NLEV):
                    if lev < NLEV - 1:
                        nxt = 1 - cur
                        bq = bank("rw")
                        for c in range(NCH):
                            cs = slice(c * 128, (c + 1) * 128)
                            P.mm(bq[:, cs], Blev[cur][:, cs], BTlev[cur][:, cs], True, True, [Blev[cur], BTlev[cur]], [bq])
                        if lev < NLEV - 2:
                            bq2 = bank("rw")
                            for c in range(NCH):
                                cs = slice(c * 128, (c + 1) * 128)
                                P.mm(bq2[:, cs], BTlev[cur][:, cs], Blev[cur][:, cs], True, True,
                                     [Blev[cur], BTlev[cur]], [bq2])
                    for half in range(2):
                        bx_ = banks[4 + half]
                        for cc_ in range(2):
                            c = half * 2 + cc_
                            P.mm(bx_[:, cc_ * 256:(cc_ + 1) * 256], BTlev[cur][:, c * 128:(c + 1) * 128], Xbfv[:, c, :],
                                 True, False, [BTlev[cur], Xbf], [bx_])
                            P.mm(bx_[:, cc_ * 256:(cc_ + 1) * 256], ident_b, Xbfv[:, c, :],
                                 False, True, [cstb, Xbf], [bx_])
                    if lev < NLEV - 1:
                        P.copy("act", BTlev[nxt][:], bq[:, 0:512], [bq], [BTlev[nxt]])
                        if lev < NLEV - 2:
                            P.copy("dve", Blev[nxt][:], bq2[:, 0:512], [bq2], [Blev[nxt]])
                    P.copy("act", Xbf[:, 0:512], banks[4][:, 0:512], [banks[4]], [Xbf])
                    P.copy("dve", Xbf[:, 512:1024], banks[5][:, 0:512], [banks[5]], [Xbf])
                    if lev < NLEV - 1:
                        cur = nxt
                    yield
                nU0v = d["nU0"][:, :].rearrange("p (c n) -> p c n", n=128)
                Wbdv = d["Wbd"][:, :].rearrange("p (c n) -> p c n", n=128)
                P.ts("dve", nU0v, Xbfv[:, :, 128:256], -1.0, None, ALU.mult, None, [Xbf], [d["nU0"]])
                P.copy("pool", Wbdv, Xbfv[:, :, 0:128], [Xbf], [d["Wbd"]])
                yield
                br = bank("rw")
                for c in range(NCH):
                    P.mm(br[:, c * LCH:(c + 1) * LCH], d["Wbd"][:, c * 128:(c + 1) * 128],
                         d["ArbT"][:, c * LCH:(c + 1) * LCH], True, True, [d["Wbd"], d["ArbT"]], [br])
                P.tt("dve", d["Rhat"][:], d["rt32"][:], br[:, 0:TT], ALU.subtract, [d["rt32"], br], [d["Rhat"]])
                bp = bank("rw")
                for c in range(NCH):
                    cs = slice(c * 128, (c + 1) * 128)
                    P.mm(bp[:, cs], d["Wbd"][:, cs], d["BGbd"][:, cs], True, True, [d["Wbd"], d["BGbd"]], [bp])
                for c in range(NCH):
                    cs = slice(c * 128, (c + 1) * 128)
                    P.stt(d["PT"][:, cs], ident_f, d["GL"][:, c:c + 1], bp[:, cs], ALU.mult, ALU.subtract,
                          [cst, d["GL"], bp], [d["PT"]])
                yield

            def seq(pb, inst, c, l=l):
                d = pp[inst]
                T = Tst[l][pb]
                tb_ = banks[4 + inst]
                yb = tb_
                ycols = slice(128 + c * LCH, 128 + (c + 1) * LCH)
                cs = slice(c * 128, (c + 1) * 128)
                cl_ = slice(c * LCH, (c + 1) * LCH)
                P.mm(yb[:, ycols], T[:], d["Rhat"][:, cl_], True, False, [T, d["Rhat"]], [yb])
                P.mm(yb[:, ycols], d["Vbd"][:, cs], d["ArkT"][:, cl_], False, False, [d["Vbd"], d["ArkT"]], [yb])
                P.mm(yb[:, ycols], d["nU0"][:, cs], d["ArbT"][:, cl_], False, True, [d["nU0"], d["ArbT"]], [yb])
                P.mm(tb_[:, 0:128], d["PT"][:, cs], T[:], True, False, [d["PT"], T], [tb_])
                P.mm(tb_[:, 0:128], d["KGbd"][:, cs], d["Vbd"][:, cs], False, False, [d["KGbd"], d["Vbd"]], [tb_])
                P.mm(tb_[:, 0:128], d["BGbd"][:, cs], d["nU0"][:, cs], False, True, [d["BGbd"], d["nU0"]], [tb_])
                P.copy("act", T[:], tb_[:, 0:128], [tb_], [T])

            def fin(pb, inst, l=l, dbg_on=dbg_on):
                assert lru_done[0], "fin emitted before the LRU chain finished (scratch lf[3] still live)"
                d = pp[inst]
                yb = banks[4 + inst]
                y32, yc, ysq, rs = spf[0], spf[1], spf[2], lf[3]
                P.copy("act", y32[:], yb[:, 128:128 + TT], [yb], [y32])
                if dbg_on:
                    dbg_dump("y_rw%d" % pb, y32, y32[:], [128, TT])
                bm = bank("rw")
                P.mm(bm[:, 0:TT], onesbd_f, y32[:], True, True, [cst, y32], [bm])
                P.stt(yc[:], bm[:, 0:TT], -1.0 / 64, y32[:], ALU.mult, ALU.add, [bm, y32], [yc])
                P.act(ysq[:], yc[:], AF.Square, [yc], [ysq])
                bv = bank("rw")
                P.mm(bv[:, 0:TT], onesbd_f, ysq[:], True, True, [cst, ysq], [bv])
                P.act(rs[:], bv[:, 0:TT], AF.Ln, [bv, eps_t[GN_EPS]], [rs], bias=eps_t[GN_EPS][:, 0:1], scale=1.0 / 64)
                P.act(rs[:], rs[:], AF.Exp, [rs], [rs], scale=-0.5)
                P.tt("dve", yc[:], yc[:], rs[:], ALU.mult, [yc, rs], [yc])
                P.ts("dve", yc[:], yc[:], V(l, "lnw", pb), V(l, "lnb", pb), ALU.mult, ALU.add, [yc, vec], [yc])
                P.tt("pool", yc[:], yc[:], d["bonus"][:], ALU.add, [yc, d["bonus"]], [yc])
                P.tt("dve", ycat[:, pb, :], yc[:], sgate[:, pb, :], ALU.mult, [yc, sgv[pb]], [ycv[pb]])

            def rwkv_gen():
                def chain(pb, inst):
                    yield from prep(pb, inst)
                    yield from solve(pb, inst)

                def tail(pbs):
                    for c in range(NCH):
                        for inst, pb in enumerate(pbs):
                            seq(pb, inst, c)
                            yield
                    for inst, pb in enumerate(pbs):
                        fin(pb, inst)
                        yield

                def merge(ga, gb):
                    da = db = False
                    while not (da and db):
                        if not da:
                            try:
                                next(ga)
                                yield
                            except StopIteration:
                                da = True
                        if not db:
                            try:
                                next(gb)
                                yield
                            except StopIteration:
                                db = True

                def half_front(pbs):
                    gA, gB = chain(pbs[0], 0), chain(pbs[1], 1)
                    for v in gA:
                        yield
                        if v == "tok_done":
                            break
                    yield from merge(gA, gB)

                yield from half_front((0, 1))
                t0_ = tail((0, 1))
                gA, gB = chain(2, 0), chain(3, 1)

                def front2a():
                    for v in gA:
                        yield
                        if v == "pre_pp":
                            break
                yield from merge(t0_, front2a())
                half0_done[0] = True
                for v in gA:
                    yield
                    if v == "tok_done":
                        break
                yield from merge(gA, gB)
                yield from tail((2, 3))

            def s5_gen(l=l, dbg_on=dbg_on):
                Bv = s5B[l][:, :].rearrange("p (r b q m) -> p r b q m", r=2, b=4, q=2)
                Cv = s5C[l][:, :].rearrange("p (r j m) -> p r j m", r=2, j=16)
                rotk, keep = s5rot[l]
                NS = TT // TS5
                yb5 = banks[7]

                def stage0(blk, s, jj, k):
                    hf, jh = jj // 2, jj % 2
                    hs = slice(64 * hf, 64 * hf + 64)
                    tsl = slice(s * TS5, (s + 1) * TS5)
                    bu = banks[k % 2]
                    c0 = 0
                    bre = bu[:, c0:c0 + TS5]
                    bim = bu[:, c0 + TS5:c0 + 2 * TS5]
                    P.mm(bre, Bv[hs, 0, blk, jh, :], ubf[hs, blk, tsl], True, True, [s5B[l], ubf], [bu])
                    P.mm(bim, Bv[hs, 1, blk, jh, :], ubf[hs, blk, tsl], True, True, [s5B[l], ubf], [bu])

                def stage1(blk, s, jj, k):
                    j = blk * 4 + jj
                    (t1, t2, bzr, bzi, zr, zi, t3, t4) = s5f[k]
                    bu = banks[k % 2]
                    c0 = 0
                    bre = bu[:, c0:c0 + TS5]
                    bim = bu[:, c0 + TS5:c0 + 2 * TS5]
                    cosT = s5tab[:, 0, j, :]
                    sinT = s5tab[:, 1, j, :]
                    P.tt("dve", t1[:], bre, cosT, ALU.mult, [bu, s5tab], [t1])
                    P.tt("dve", t2[:], bim, sinT, ALU.mult, [bu, s5tab], [t2])
                    P.tt("dve", t3[:], bim, cosT, ALU.mult, [bu, s5tab], [t3])
                    P.tt("dve", t4[:], bre, sinT, ALU.mult, [bu, s5tab], [t4])
                    P.tt(S5E, bzr[:], t1[:], t2[:], ALU.add, [t1, t2], [bzr])
                    P.tt(S5E, bzi[:], t3[:], t4[:], ALU.subtract, [t3, t4], [bzi])

                def stage2(blk, s, jj, k, kx):
                    j = blk * 4 + jj
                    hf, jh = jj // 2, jj % 2
                    hs = slice(64 * hf, 64 * hf + 64)
                    tsl = slice(s * TS5, (s + 1) * TS5)
                    (t1, t2, bzr, bzi, zr, zi, t3, t4) = s5f[k]
                    sx = s5x[kx]
                    zl = s5zl[k]
                    zv = s5zv[l][j]
                    cosT = s5tab[:, 0, j, :]
                    sinT = s5tab[:, 1, j, :]
                    rho_b = keep[:, 0, j:j + 1].to_broadcast([128, TS5])
                    P.scan(zr[:], rho_b, bzr[:], s5z[l][:, 0, j:j + 1], [keep, bzr, zv], [zr])
                    P.scan(zi[:], rho_b, bzi[:], s5z[l][:, 1, j:j + 1], [keep, bzi, zv], [zi])
                    P.tt("dve", t1[:], zr[:], cosT, ALU.mult, [zr, s5tab], [t1])
                    P.tt("dve", t2[:], zi[:], sinT, ALU.mult, [zi, s5tab], [t2])
                    P.tt(S5E, sx[:, 0, :], t1[:], t2[:], ALU.subtract, [t1, t2], [sx])
                    P.tt("dve", t3[:], zr[:], sinT, ALU.mult, [zr, s5tab], [t3])
                    P.tt(S5E, t4[:], zi[:], cosT, ALU.mult, [zi, s5tab], [t4])
                    P.tt(S5E, sx[:, 1, :], t3[:], t4[:], ALU.add, [t3, t4], [sx])
                    rc = rotk[:, 0, j:j + 1]
                    rs_ = rotk[:, 1, j:j + 1]
                    zlr = zr[:, TS5 - 1:TS5]
                    zli = zi[:, TS5 - 1:TS5]
                    P.ts("dve", zl[:, 0:1], zli, rs_, None, ALU.mult, None, [zi, rotk], [zl])
                    P.ts("dve", zl[:, 1:2], zlr, rs_, None, ALU.mult, None, [zr, rotk], [zl])
                    P.stt(s5z[l][:, 0, j:j + 1], zlr, rc, zl[:, 0:1], ALU.mult, ALU.subtract, [zr, rotk, zl], [zv])
                    P.stt(s5z[l][:, 1, j:j + 1], zli, rc, zl[:, 1:2], ALU.mult, ALU.add, [zi, rotk, zl], [zv])

                def stage3(blk, s, jj, kx):
                    j = blk * 4 + jj
                    hf, jh = jj // 2, jj % 2
                    hs = slice(64 * hf, 64 * hf + 64)
                    tsl = slice(s * TS5, (s + 1) * TS5)
                    sx = s5x[kx]
                    P.mm(yb5[hs, tsl], Cv[:, 0, j, :], sx[:, 0, :], jh == 0, False, [s5C[l], sx], [yb5])
                    P.mm(yb5[hs, tsl], Cv[:, 1, j, :], sx[:, 1, :], False, jh == 1, [s5C[l], sx], [yb5])

                for blk in range(4):
                    units = [(s, jj) for s in range(NS) for jj in range(4)]
                    ks = []
                    for (s, jj) in units:
                        ks.append(s5ctr[0] % NS5SET)
                        s5ctr[0] += 1
                    nU = len(units)
                    stage0(blk, units[0][0], units[0][1], ks[0])
                    stage0(blk, units[1][0], units[1][1], ks[1])
                    stage1(blk, units[0][0], units[0][1], ks[0])
                    yield
                    for u in range(nU):
                        if u + 1 < nU:
                            stage1(blk, units[u + 1][0], units[u + 1][1], ks[u + 1])
                            if u + 2 < nU:
                                stage0(blk, units[u + 2][0], units[u + 2][1], ks[u + 2])
                            yield
                        stage2(blk, units[u][0], units[u][1], ks[u], u % NS5X)
                        if u >= 2:
                            stage3(blk, units[u - 2][0], units[u - 2][1], (u - 2) % NS5X)
                        yield
                    stage3(blk, units[nU - 2][0], units[nU - 2][1], (nU - 2) % NS5X)
                    stage3(blk, units[nU - 1][0], units[nU - 1][1], (nU - 1) % NS5X)
                    ys, x2, q_ = spf
                    P.stt(ys[:], zu[:, blk, :], V(l, "s5d", blk), yb5[:, 0:TT], ALU.mult, ALU.add, [zu, vec, yb5], [ys])
                    if dbg_on:
                        dbg_dump("s5y%d" % blk, ys, ys[:], [128, TT])
                    P.act(x2[:], ys[:], AF.Square, [ys], [x2])
                    P.ts("dve", x2[:], x2[:], 0.044715, 1.0, ALU.mult, ALU.add, [x2], [x2])
                    P.tt("pool", q_[:], x2[:], ys[:], ALU.mult, [x2, ys], [q_])
                    P.act(x2[:], q_[:], AF.Sigmoid, [q_], [x2], scale=2.0 * math.sqrt(2.0 / math.pi))
                    P.tt("dve", mix[:, blk, :], ys[:], x2[:], ALU.mult, [ys, x2], [mixv[blk]])
                    yield
                zgb = ubf
                P.copy("pool", zgb[:], mix[:], mixv, [zgb])
                rg = next_ring()
                P.dma(rg[:, 0:2048].rearrange("p (k n) -> p k n", k=4), glu_w_b[l].rearrange("(k p) n -> p k n", p=128),
                      reads=[wsb], writes=[rg])
                rgv = rg[:, 0:2048].rearrange("p (k n) -> p k n", k=4)
                for ob in range(4):
                    b = banks[0]
                    for k in range(4):
                        P.mm(b[:, 0:TT], rgv[:, k, ob * 128:(ob + 1) * 128], zgb[:, k, :], k == 0, k == 3, [rg, zgb], [b])
                    sg_ = spf[ob % 2]
                    P.act(sg_[:], b[:, 0:TT], AF.Sigmoid, [b, vec], [sg_], bias=V(l, "glub", ob))
                    P.tt("pool", sg_[:], sg_[:], sgate[:, 4 + ob, :], ALU.mult, [sg_, sgv[4 + ob]], [sg_])
                    P.tt("dve", ycat[:, 4 + ob, :], mix[:, ob, :], sg_[:], ALU.mult, [mixv[ob], sg_], [ycv[4 + ob]])
                    if ob == 3:
                        s5_done[0] = True
                    yield

            lru_done = [("lru" in SKIP)]
            s5_done = [False]
            half0_done = [False]

            def lru_gen(l=l, dbg_on=dbg_on):
                for blk in range(4):
                    A, B, C, Dd = lf
                    bl = banks[5]
                    P.ts("dve", A[:], zx[:, blk, 0:TT], V(l, "cw", 0 * 4 + blk), V(l, "cb", blk), ALU.mult, ALU.add,
                         [zx, vec], [A])
                    for j in range(1, 4):
                        P.stt(A[:], zx[:, blk, j:j + TT], V(l, "cw", j * 4 + blk), A[:], ALU.mult, ALU.add,
                              [zx, vec, A], [A])
                    P.copy("act", lb[:], A[:], [A], [lb])
                    yield
                    P.mm(bl[:, 0:TT], lruw[l][:, blk * 128:(blk + 1) * 128], lb[:], True, True, [lruw[l], lb], [bl])
                    P.mm(bl[:, TT:2 * TT], lruw[l][:, (4 + blk) * 128:(5 + blk) * 128], lb[:], True, True,
                         [lruw[l], lb], [bl])
                    P.act(B[:], bl[:, 0:TT], AF.Sigmoid, [bl, vec], [B], bias=V(l, "ba", blk))
                    P.act(C[:], bl[:, TT:2 * TT], AF.Sigmoid, [bl, vec], [C], bias=V(l, "bx", blk))
                    yield
                    P.act(Dd[:], B[:], AF.Exp, [B, lru_c], [Dd], scale=lru_c[:, l, blk:blk + 1])
                    P.act(B[:], B[:], AF.Exp, [B, lru_c], [B], scale=lru_c[:, l, 4 + blk:5 + blk])
                    P.act(B[:], B[:], AF.Ln, [B, one_t], [B], bias=one_t[:, 0:1], scale=-1.0)
                    P.act(B[:], B[:], AF.Exp, [B], [B], scale=0.5)
                    P.tt("pool", C[:], C[:], A[:], ALU.mult, [C, A], [C])
                    P.tt("pool", C[:], C[:], B[:], ALU.mult, [C, B], [C])
                    yield
                    P.scan(A[:], Dd[:], C[:], ch[l][:, blk:blk + 1], [Dd, C, ch[l]], [A])
                    P.copy("act", ch[l][:, blk:blk + 1], A[:, TT - 1:TT], [A], [ch[l]])
                    if dbg_on:
                        dbg_dump("lru%d" % blk, A, A[:], [128, TT])
                    P.tt("pool", ycat[:, 8 + blk, :], A[:], sgate[:, 8 + blk, :], ALU.mult, [A, sgv[8 + blk]], [ycv[8 + blk]])
                    if blk == 3:
                        lru_done[0] = True
                    yield

            KT_EARLY = (0, 1, 4, 5, 6, 7, 8, 9, 10, 11)
            KT_LATE = (2, 3)

            def oproj_early(l=l):
                while not (s5_done[0] and lru_done[0] and half0_done[0]):
                    yield
                for oc in range(4):
                    r = next_ring()
                    P.dma(r[:, 0:12 * 256].rearrange("p (k n) -> p k n", k=12),
                          w_out_b[l].rearrange("(k p) n -> p k n", p=128)[:, :, oc * 256:(oc + 1) * 256],
                          reads=[wsb], writes=[r])
                    rv = r[:, 0:12 * 256].rearrange("p (k n) -> p k n", k=12)
                    yield
                    for ob2 in range(2):
                        ob = oc * 2 + ob2
                        b = bank("proj")
                        for i_, k in enumerate(KT_EARLY):
                            P.mm(b[:, 0:TT], rv[:, k, ob2 * 128:(ob2 + 1) * 128], ycat[:, k, :], i_ == 0,
                                 i_ == len(KT_EARLY) - 1, [r, ycv[k]], [b])
                        P.tt("dve", hT[:, ob, :], hT[:, ob, :], b[:, 0:TT], ALU.add, [hT, b], [hT])
                        yield

            early_ok = ("rwkv" not in SKIP) and ("s5" not in SKIP) and ("lru" not in SKIP)
            gens = []
            if early_ok:
                gens.append((oproj_early(), 1))
            if "rwkv" not in SKIP:
                gens.append((rwkv_gen(), GW[0]))
            if "s5" not in SKIP:
                gens.append((s5_gen(), GW[1]))
            if "lru" not in SKIP:
                gens.append((lru_gen(), GW[2]))
            if "serial" in SKIP:
                for g, w in gens:
                    drive([(g, 1)])
            else:
                drive(gens)
            if dbg_on:
                dbg_dump("ycat_rw", ycat, ycat[:, 0:4, :].rearrange("p a t -> p (a t)"), [128, 4 * TT], BF16)

            if early_ok:
                r = next_ring()
                P.dma(r[:, 0:2 * 1024].rearrange("p (k n) -> p k n", k=2),
                      w_out_b[l].rearrange("(k p) n -> p k n", p=128)[:, 2:4, :],
                      reads=[wsb], writes=[r])
                rv = r[:, 0:2 * 1024].rearrange("p (k n) -> p k n", k=2)
                for ob in range(8):
                    b = bank("proj")
                    for i_, k in enumerate(KT_LATE):
                        P.mm(b[:, 0:TT], rv[:, i_, ob * 128:(ob + 1) * 128], ycat[:, k, :], i_ == 0, i_ == 1,
                             [r, ycv[k]], [b])
                    P.tt("dve", hT[:, ob, :], hT[:, ob, :], b[:, 0:TT], ALU.add, [hT, b], [hT])
            else:
                for oc in range(4):
                    r = next_ring()
                    P.dma(r[:, 0:12 * 256].rearrange("p (k n) -> p k n", k=12),
                          w_out_b[l].rearrange("(k p) n -> p k n", p=128)[:, :, oc * 256:(oc + 1) * 256],
                          reads=[wsb], writes=[r])
                    rv = r[:, 0:12 * 256].rearrange("p (k n) -> p k n", k=12)
                    for ob2 in range(2):
                        ob = oc * 2 + ob2
                        b = bank("proj")
                        for k in range(12):
                            P.mm(b[:, 0:TT], rv[:, k, ob2 * 128:(ob2 + 1) * 128], ycat[:, k, :], k == 0, k == 11,
                                 [r] + ycv, [b])
                        P.tt("dve", hT[:, ob, :], hT[:, ob, :], b[:, 0:TT], ALU.add, [hT, b], [hT])
            for k in range(8):
                P.copy("act" if k % 2 == 0 else "dve", xn[:, k, :], hT[:, k, :], [hT], [xn])
            r = next_ring()
            P.dma(r[:, 0:2048].rearrange("p (k n) -> p k n", k=2), ple_w_b[l].rearrange("(k p) n -> p k n", p=128),
                  reads=[wsb], writes=[r])
            rv = r[:, 0:2048].rearrange("p (k n) -> p k n", k=2)
            epre = zs
            for ob in range(8):
                b = bank("proj")
                for k in range(2):
                    P.mm(b[:, 0:TT], rv[:, k, ob * 128:(ob + 1) * 128], pbf[:, k, :], k == 0, k == 1, [r, pbf], [b])
                P.copy("act", epre[:, ob, :], b[:, 0:TT], [b], [zsv[ob]])
            rms_rstd(zsv[0:8], lambda k: epre[:, k, :], 8, NORM_EPS)
            for gc in range(2):
                r = next_ring()
                P.dma(r[:, 0:4096].rearrange("p (k n) -> p k n", k=8),
                      ple_gw_b[l].rearrange("(k p) n -> p k n", p=128)[:, :, gc * 512:(gc + 1) * 512],
                      reads=[wsb], writes=[r])
                rv = r[:, 0:4096].rearrange("p (k n) -> p k n", k=8)
                for ob2 in range(4):
                    ob = gc * 4 + ob2
                    b = bank("proj")
                    for k in range(8):
                        P.mm(b[:, 0:TT], rv[:, k, ob2 * 128:(ob2 + 1) * 128], xn[:, k, :], k == 0, k == 7, [r, xn], [b])
                    sg_, e_ = fs[6 + 2 * (ob % 2)], fs[7 + 2 * (ob % 2)]
                    P.act(sg_[:], b[:, 0:TT], AF.Sigmoid, [b], [sg_])
                    P.stt(e_[:], epre[:, ob, :], V(l, "png", ob), rstd[:], ALU.mult, ALU.mult, [zsv[ob], vec, rstd], [e_])
                    P.tt("dve", e_[:], e_[:], sg_[:], ALU.mult, [e_, sg_], [e_])
                    P.tt("dve", hT[:, ob, :], hT[:, ob, :], e_[:], ALU.add, [hT, e_], [hT])
            if dbg_on:
                dbg_dump("h1", hT, hT[:, :, :].rearrange("p a t -> p (a t)"), [128, 8 * TT])
        rms_rstd([hT], lambda k: hT[:, k, :], 8, NORM_EPS)
        fo = 2 * VEC_PER_LAYER
        for k in range(8):
            P.stt(zs[:, k, :], hT[:, k, :], vec[:, fo + k:fo + k + 1], rstd[:], ALU.mult, ALU.mult, [hT, vec, rstd], [zsv[k]])
        out_toks.append(P.dma(oT[:, t0:t0 + TT].rearrange("(k p) t -> p k t", p=128), zs[:, 0:8, :], reads=zsv[0:8]))
    out_toks.extend(dbg_out.values())
    ninst = P.ninst
    P.finish(out_toks)
    return nc, ninst


def pack_shared(inp):
    f = lambda a: np.asarray(a, np.float32)
    vec = np.zeros((128, NVEC), np.float32)
    for l in range(2):
        def put(name, arr, n):
            c = vcol(l, name)
            vec[:, c:c + n] = _pp(arr, n)
        put("ng", f(inp["norm_g"])[l], 8)
        put("mu", f(inp["rwkv_mu"])[l], 13)
        put("w0", f(inp["rwkv_w0"])[l], 4)
        put("a0", f(inp["rwkv_a0"])[l], 4)
        put("kk", f(inp["rwkv_k_k"])[l], 4)
        put("ka", f(inp["rwkv_k_a"])[l], 4)
        put("rk", f(inp["rwkv_r_k"])[l].reshape(512), 4)
        put("lnw", f(inp["rwkv_ln_w"])[l], 4)
        put("lnb", f(inp["rwkv_ln_b"])[l], 4)
        put("s5d", f(inp["s5_d"])[l], 4)
        put("glub", f(inp["s5_glu_b"])[l], 4)
        cw = f(inp["lru_conv_w"])[l]
        c = vcol(l, "cw")
        for j in range(4):
            vec[:, c + 4 * j:c + 4 * j + 4] = _pp(cw[j], 4)
        put("cb", f(inp["lru_conv_b"])[l], 4)
        put("ba", f(inp["lru_ba"])[l], 4)
        put("bx", f(inp["lru_bx"])[l], 4)
        put("lam", f(inp["lru_lambda"])[l], 4)
        put("png", f(inp["ple_norm_g"])[l], 8)
    vec[:, 2 * VEC_PER_LAYER:2 * VEC_PER_LAYER + 8] = _pp(f(inp["final_norm_g"]), 8)

    w2a2 = np.zeros((2, 128, 512), np.float32)
    w2a2[:, 0:64] = f(inp["rwkv_w2"])
    w2a2[:, 64:128] = f(inp["rwkv_a2"])
    lruw = np.zeros((2, 128, 8, 128), np.float32)
    for l in range(2):
        for m, key in enumerate(("lru_wa", "lru_wx")):
            w = f(inp[key])[l]
            for q in range(4):
                for b2 in range(2):
                    lruw[l, b2 * 64:(b2 + 1) * 64, m * 4 + q, b2 * 64:(b2 + 1) * 64] = w[2 * q + b2]
    lruw = lruw.reshape(2, 128, 1024)
    def modes(a):
        a = f(a).reshape(2, 16, 2, 64)
        return np.ascontiguousarray(a.transpose(0, 2, 3, 1).reshape(2, 128, 16))
    s5s = np.zeros((2, 128, 3, 16), np.float32)
    s5s[:, :, 0] = modes(inp["s5_a_re"])
    s5s[:, :, 1] = modes(inp["s5_a_im"])
    ldt = np.broadcast_to(f(inp["s5_log_dt"])[:, :, None], (2, 32, 64))
    s5s[:, :, 2] = modes(ldt)
    s5s = s5s.reshape(2, 128, 48)
    def bmodes(a):
        a = f(a).reshape(2, 16, 2, 64, 16)
        return a.transpose(0, 2, 3, 1, 4).reshape(2, 128, 16, 16)
    s5b = np.stack([bmodes(inp["s5_b_re"]), bmodes(inp["s5_b_im"])], axis=2).reshape(2, 128, 512)
    s5c = np.zeros((2, 128, 2, 16, 64), np.float32)
    for ri, key in enumerate(("s5_c_re", "s5_c_im")):
        c = f(inp[key]).reshape(2, 16, 2, 16, 64)
        for gh in range(2):
            for jh in range(2):
                c0 = 32 * jh + 16 * gh
                s5c[:, gh * 64:(gh + 1) * 64, ri, jh::2, c0:c0 + 16] = c[:, jh::2, gh].transpose(0, 3, 1, 2)
    s5c = s5c.reshape(2, 128, 2048)
    return {
        "w_in": np.ascontiguousarray(f(inp["w_in"])), "w_out": np.ascontiguousarray(f(inp["w_out"])),
        "ple_w": np.ascontiguousarray(f(inp["ple_w"])), "ple_gw": np.ascontiguousarray(f(inp["ple_gate_w"])),
        "glu_w": np.ascontiguousarray(f(inp["s5_glu_w"])), "vec": vec, "cst": make_consts()[0], "msk": make_consts()[1],
        "w2a2": w2a2, "lruw": np.ascontiguousarray(lruw), "s5s": np.ascontiguousarray(s5s),
        "s5b": np.ascontiguousarray(s5b), "s5c": np.ascontiguousarray(s5c),
    }


_NC_CACHE = {}


def run_cores(inp, TC, batches, dbg=None):
    key = (TC, tuple(sorted(dbg)) if dbg else None)
    if key not in _NC_CACHE:
        _NC_CACHE[key] = build_nc(TC, dbg)
    nc, ninst = _NC_CACHE[key]
    shared = pack_shared(inp)
    x = np.asarray(inp["x"], np.float32)
    p = np.asarray(inp["p"], np.float32)
    in_maps = []
    for b in batches:
        m = dict(shared)
        m["xT"] = np.ascontiguousarray(x[b, :TC].T)
        m["pT"] = np.ascontiguousarray(p[:, b, :TC].transpose(0, 2, 1))
        in_maps.append(m)
    res = run_bass_kernel_spmd(nc, in_maps, core_ids=list(range(len(batches))))
    return res


def kernel(**inputs):
    x = np.asarray(inputs["x"])
    B, S, _ = x.shape
    batches = [c % B for c in range(8)]
    res = run_cores(inputs, S, batches)
    out = np.empty((B, S, D), np.float32)
    for b in range(B):
        out[b] = res.results[b]["oT"].T
    return out.astype(x.dtype)
```

```python
import contextlib
import math
import numpy as np
import concourse.bass as bass
import concourse.mybir as mybir
from concourse.bass_utils import run_bass_kernel_spmd

F32 = mybir.dt.float32
BF16 = mybir.dt.bfloat16
ALU = mybir.AluOpType
AF = mybir.ActivationFunctionType

D = 1024
DIN = 4224
DMIX = 1536
DPLE = 256
TT = 256
LCH = 64
NCH = TT // LCH
TS5 = 128
import os
SKIP = set(os.environ.get("KSKIP", "").split(","))
S5E = os.environ.get("KS5E", "pool")
GW = tuple(int(v) for v in os.environ.get("KGW", "1,1,1").split(","))
GN_EPS = 64e-5
NORM_EPS = 1e-6


class Tok:
    __slots__ = ("sem", "val", "eng", "dma")

    def __init__(self, sem, val, eng, dma):
        self.sem, self.val, self.eng, self.dma = sem, val, eng, dma


class Buf:
    def __init__(self, t, name):
        self.t = t
        self.name = name
        self.w = None
        self.r = []

    def __getitem__(self, idx):
        return self.t[idx]


class Prog:
    ENGS = ("pe", "act", "dve", "pool", "sp")

    def __init__(self, nc, n_dma_sems=32):
        self.nc = nc
        self.es = contextlib.ExitStack()
        self.ops = {e: [] for e in self.ENGS}
        self.cnt = {e: 0 for e in self.ENGS}
        self.sem = {e: self.es.enter_context(nc.semaphore("s_" + e)) for e in self.ENGS}
        self.dsem = [self.es.enter_context(nc.semaphore("d%d" % i)) for i in range(n_dma_sems)]
        self.duse = [0] * n_dma_sems
        self.dnext = 0
        self.seen = {e: {} for e in self.ENGS}
        self.nbuf = 0
        self.ninst = 0
        self.stack = [self.es]

    def push(self):
        st = contextlib.ExitStack()
        self.stack.append(st)

    def pop(self):
        self.barrier()
        self.stack.pop().close()

    def barrier(self):
        toks = []
        for f in self.ENGS:
            if self.cnt[f] > 0:
                toks.append(Tok(self.sem[f], self.cnt[f], f, False))
        for i, s in enumerate(self.dsem):
            if self.duse[i] > 0:
                toks.append(Tok(s, 16 * self.duse[i], "dma", True))
        for e in self.ENGS:
            wl = []
            for t in toks:
                if t.eng == e and not t.dma:
                    continue
                k = id(t.sem)
                if self.seen[e].get(k, 0) >= t.val:
                    continue
                self.seen[e][k] = t.val
                wl.append((t.sem, t.val))

            def run(en, wl=wl):
                for (s, v) in wl:
                    en.wait_ge(s, v)
            self.ops[e].append(run)

    def sb(self, shape, dt=F32, name=None):
        self.nbuf += 1
        name = name or ("b%d" % self.nbuf)
        t = self.stack[-1].enter_context(self.nc.sbuf_tensor("sb_" + name, list(shape), dt))
        return Buf(t, name)

    def ps(self, name, dt=F32, cols=512):
        t = self.es.enter_context(self.nc.psum_tensor(name, [128, cols], dt))
        return Buf(t, name)

    def wrap(self, t, name):
        return Buf(t, name)

    def views(self, buf, n):
        return [Buf(buf.t, "%s.v%d" % (buf.name, i)) for i in range(n)]

    def _need(self, eng, tok, waits, is_dma_issue):
        if tok is None:
            return
        if tok.eng == eng and not tok.dma and not is_dma_issue and eng == "pe":
            return
        k = id(tok.sem)
        if self.seen[eng].get(k, 0) >= tok.val:
            return
        cur = waits.get(k)
        if cur is None or cur[1] < tok.val:
            waits[k] = (tok.sem, tok.val)

    def emit(self, eng, fn, reads=(), writes=(), dma=False):
        waits = {}
        for b in reads:
            self._need(eng, b.w, waits, dma)
        for b in writes:
            self._need(eng, b.w, waits, dma)
            for t in b.r:
                self._need(eng, t, waits, dma)
        if dma:
            i = self.dnext
            self.dnext = (self.dnext + 1) % len(self.dsem)
            s = self.dsem[i]
            if self.duse[i] > 0:
                self._need(eng, Tok(s, 16 * self.duse[i], "dma", True), waits, True)
            self.duse[i] += 1
            tok = Tok(s, 16 * self.duse[i], "dma", True)
            inc = 16
        else:
            self.cnt[eng] += 1
            tok = Tok(self.sem[eng], self.cnt[eng], eng, False)
            inc = 1
        wl = list(waits.values())
        for (s, v) in wl:
            self.seen[eng][id(s)] = v
        tsem = tok.sem
        self.ninst += 1 + len(wl)

        def run(e, wl=wl, fn=fn, tsem=tsem, inc=inc):
            for (s, v) in wl:
                e.wait_ge(s, v)
            fn(e).then_inc(tsem, inc)

        self.ops[eng].append(run)
        for b in reads:
            b.r = [t for t in b.r if t.sem is not tok.sem]
            b.r.append(tok)
        for b in writes:
            b.w = tok
            b.r = []
        return tok

    def finish(self, out_toks):
        wl = [(t.sem, t.val) for t in out_toks]

        def run(e, wl=wl):
            for (s, v) in wl:
                e.wait_ge(s, v)

        self.ops["sp"].append(run)
        nc = self.nc
        ops = self.ops
        with nc.Block() as block:
            @block.tensor
            def _(e):
                for f in ops["pe"]:
                    f(e)

            @block.scalar
            def _(e):
                for f in ops["act"]:
                    f(e)

            @block.vector
            def _(e):
                for f in ops["dve"]:
                    f(e)

            @block.gpsimd
            def _(e):
                for f in ops["pool"]:
                    f(e)

            @block.sync
            def _(e):
                for f in ops["sp"]:
                    f(e)
        self.es.close()

    def dma(self, out, in_, reads=(), writes=(), eng="sp", **kw):
        return self.emit(eng, lambda e: e.dma_start(out=out, in_=in_, **kw), reads, writes, dma=True)

    def mm(self, out, lhsT, rhs, start, stop, reads, writes):
        return self.emit("pe", lambda e: e.matmul(out, lhsT, rhs, start=start, stop=stop), reads, writes)

    def act(self, out, in_, func, reads, writes, bias=None, scale=None):
        kw = {}
        if bias is not None:
            kw["bias"] = bias
        if scale is not None:
            kw["scale"] = scale
        return self.emit("act", lambda e: e.activation(out=out, in_=in_, func=func, **kw), reads, writes)

    def tt(self, eng, out, in0, in1, op, reads, writes):
        return self.emit(eng, lambda e: e.tensor_tensor(out=out, in0=in0, in1=in1, op=op), reads, writes)

    def ts(self, eng, out, in0, s1, s2, op0, op1, reads, writes):
        if op1 is None:
            return self.emit(eng, lambda e: e.tensor_scalar(out, in0, s1, None, op0), reads, writes)
        return self.emit(eng, lambda e: e.tensor_scalar(out, in0, s1, s2, op0, op1), reads, writes)

    def stt(self, out, in0, scalar, in1, op0, op1, reads, writes):
        return self.emit("dve", lambda e: e.scalar_tensor_tensor(out, in0, scalar, in1, op0, op1), reads, writes)

    def copy(self, eng, out, in_, reads, writes):
        if eng == "act":
            return self.emit("act", lambda e: e.activation(out=out, in_=in_, func=AF.Copy), reads, writes)
        return self.emit(eng, lambda e: e.tensor_copy(out, in_), reads, writes)

    def memset(self, eng, ap, val, writes):
        return self.emit(eng, lambda e: e.memset(ap, val), (), writes)

    def scan(self, out, d0, d1, init, reads, writes):
        return self.emit("dve", lambda e: e.tensor_tensor_scan(out, d0, d1, init, ALU.mult, ALU.add), reads, writes)

    def recip(self, out, in_, reads, writes):
        return self.emit("dve", lambda e: e.reciprocal(out, in_), reads, writes)


VEC_FIELDS = [("ng", 8), ("mu", 13), ("w0", 4), ("a0", 4), ("kk", 4), ("ka", 4), ("rk", 4), ("lnw", 4),
              ("lnb", 4), ("s5d", 4), ("glub", 4), ("cw", 16), ("cb", 4), ("ba", 4), ("bx", 4), ("lam", 4),
              ("png", 8)]
VEC_PER_LAYER = sum(n for _, n in VEC_FIELDS)
VEC_OFF = {}
_o = 0
for _n, _c in VEC_FIELDS:
    VEC_OFF[_n] = _o
    _o += _c
NVEC = 2 * VEC_PER_LAYER + 8

CST_IDENT = 0
CST_ONESBD = 128
CST_SCAN = 256
NCST = 256 + TT
MSK_USN = 0
MSK_LSN = NCH * 128
MSK_USP = 2 * NCH * 128
MSK_CI = 3 * NCH * 128
NMSK = 3 * NCH * 128 + NCH * LCH


def vcol(l, name, i=0):
    return l * VEC_PER_LAYER + VEC_OFF[name] + i


def _pp(v, n):
    return np.ascontiguousarray(np.asarray(v, np.float32).reshape(n, 128).T)


def make_consts():
    c = np.zeros((128, NCST), np.float32)
    i = np.arange(128)[:, None]
    j = np.arange(128)[None, :]
    c[:, CST_IDENT:CST_IDENT + 128] = (i == j)
    c[:, CST_ONESBD:CST_ONESBD + 128] = ((i // 64) == (j // 64))
    tt = np.arange(TT)[None, :]
    c[:, CST_SCAN:CST_SCAN + TT] = 1.0 * ((tt % LCH) != 0)
    m = np.zeros((128, NMSK), np.float32)
    t = np.arange(64)[None, :]
    for ch in range(NCH):
        m[:, MSK_USN + ch * 128:MSK_USN + (ch + 1) * 128] = -1.0 * (j > i)
        m[:, MSK_LSN + ch * 128:MSK_LSN + (ch + 1) * 128] = -1.0 * (i > j)
        m[:, MSK_USP + ch * 128:MSK_USP + (ch + 1) * 128] = 1.0 * (j > i)
        m[:, MSK_CI + ch * 64:MSK_CI + (ch + 1) * 64] = 1.0 * (t >= (i % 64))
    return c, m


def build_nc(TC, dbg=None):
    assert TC % TT == 0
    NT = TC // TT
    dbg = dbg or set()
    nc = bass.Bass("TRN2", target_bir_lowering=False)

    def din(name, shape, dt=F32):
        return nc.dram_tensor(name, list(shape), dt, kind="ExternalInput").ap()

    xT = din("xT", [D, TC])
    pT = din("pT", [2, DPLE, TC])
    w_in = din("w_in", [2, D, DIN])
    w_out = din("w_out", [2, DMIX, D])
    ple_w = din("ple_w", [2, DPLE, D])
    ple_gw = din("ple_gw", [2, D, D])
    glu_w = din("glu_w", [2, 512, 512])
    vec_d = din("vec", [128, NVEC])
    cst_d = din("cst", [128, NCST])
    msk_d = din("msk", [128, NMSK])
    w2a2_d = din("w2a2", [2, 128, 512])
    lruw_d = din("lruw", [2, 128, 8 * 128])
    s5s_d = din("s5s", [2, 128, 3 * 16])
    s5b_d = din("s5b", [2, 128, 2 * 16 * 16])
    s5c_d = din("s5c", [2, 128, 2 * 16 * 64])
    oT = nc.dram_tensor("oT", [D, TC], F32, kind="ExternalOutput").ap()
    dbg_out = {}

    def dram_int(name, shape, dt):
        return nc.dram_tensor(name, list(shape), dt, kind="Internal").ap()

    w_in_b = dram_int("w_in_b", [2, D, DIN], BF16)
    w_out_b = dram_int("w_out_b", [2, DMIX, D], BF16)
    ple_w_b = dram_int("ple_w_b", [2, DPLE, D], BF16)
    ple_gw_b = dram_int("ple_gw_b", [2, D, D], BF16)
    glu_w_b = dram_int("glu_w_b", [2, 512, 512], BF16)
    s5tab_d = dram_int("s5tab", [2, 128, 2 * 16 * TS5], F32)

    P = Prog(nc)
    wsb = P.wrap(None, "wscratch")
    tabsb = P.wrap(None, "s5tabscr")

    def dbg_dump(name, buf, ap, shape, dt=F32):
        if name not in dbg:
            return
        o = nc.dram_tensor("dbg_" + name, list(shape), dt, kind="ExternalOutput").ap()
        dbg_out[name] = P.dma(o, ap, reads=[buf])

    vec = P.sb([128, NVEC], F32, "vec")
    cst = P.sb([128, NCST], F32, "cst")
    P.dma(vec[:], vec_d, writes=[vec])
    P.dma(cst[:], cst_d, writes=[cst])
    cstb = P.sb([128, 128], BF16, "cstb")
    P.copy("dve", cstb[:], cst[:, CST_IDENT:CST_IDENT + 128], [cst], [cstb])
    mskb = P.sb([128, NMSK], BF16, "mskb")
    ident_f = cst[:, CST_IDENT:CST_IDENT + 128]
    ident_b = cstb[:, 0:128]
    onesbd_f = cst[:, CST_ONESBD:CST_ONESBD + 128]
    ones_f = P.sb([128, 128], F32, "ones_f")
    P.memset("pool", ones_f[:], 1.0, [ones_f])
    one_t = P.sb([128, 1], F32, "one_t")
    P.memset("pool", one_t[:], 1.0, [one_t])

    def V(l, name, i=0, n=1):
        c = vcol(l, name, i)
        return vec[:, c:c + n]

    for l in range(2):
        for (src, dst, rows) in ((w_in, w_in_b, D), (w_out, w_out_b, DMIX), (ple_w, ple_w_b, DPLE),
                                 (ple_gw, ple_gw_b, D), (glu_w, glu_w_b, 512)):
            for r0 in range(0, rows, 128):
                P.dma(dst[l, r0:r0 + 128, :], src[l, r0:r0 + 128, :], writes=[wsb], eng="pool",
                      max_dma_last_dim=4096)

    w2a2 = []
    lruw = []
    for l in range(2):
        w2a2.append(P.sb([128, 512], BF16, "w2a2_%d" % l))
        lruw.append(P.sb([128, 1024], BF16, "lruw_%d" % l))
    lru_c = P.sb([128, 2, 8], F32, "lru_c")
    s5B = [P.sb([128, 2 * 4 * 2 * 128], BF16, "s5B%d" % l) for l in range(2)]
    s5C = [P.sb([128, 2048], BF16, "s5C%d" % l) for l in range(2)]
    s5keep = [P.sb([128, 3, 16], F32, "s5keep%d" % l) for l in range(2)]
    s5rotb = [P.sb([128, 2, 16], F32, "s5rot%d" % l) for l in range(2)]
    ps_misc = P.ps("ps7")
    P.push()
    stage = P.sb([128, 1024], F32, "stage")
    mstage = P.sb([128, NMSK], F32, "mstage")
    P.dma(mstage[:], msk_d, writes=[mstage])
    P.copy("act", mskb[:], mstage[:], [mstage], [mskb])
    for l in range(2):
        P.dma(stage[:, 0:512], w2a2_d[l], writes=[stage])
        P.copy("act", w2a2[l][:], stage[:, 0:512], [stage], [w2a2[l]])
        P.dma(stage[:], lruw_d[l], writes=[stage])
        P.copy("act", lruw[l][:], stage[:], [stage], [lruw[l]])

    for l in range(2):
        tmp = P.sb([128, 4], F32, "lrutmp%d" % l)
        P.act(tmp[:], V(l, "lam", 0, 4), AF.Exp, [vec], [tmp], scale=-1.0)
        P.act(tmp[:], tmp[:], AF.Ln, [tmp, one_t], [tmp], bias=one_t[:, 0:1])
        P.ts("dve", lru_c[:, l, 0:4], tmp[:], -8.0, None, ALU.mult, None, [tmp], [lru_c])
        P.ts("dve", lru_c[:, l, 4:8], tmp[:], -16.0, None, ALU.mult, None, [tmp], [lru_c])

    s5rot = []
    for l in range(2):
        s5s = P.sb([128, 48], F32, "s5s%d" % l)
        P.dma(s5s[:], s5s_d[l], writes=[s5s])
        a_re = s5s[:, 0:16]
        a_im = s5s[:, 16:32]
        ldt = s5s[:, 32:48]
        w = P.sb([128, 16, 16], F32, "s5w%d" % l)
        R = [w]

        def row(i):
            return w[:, i, :]
        dt_, rho, th, cc, ss, t1, t2, lr, li, den, qre, qim, nr = [row(i) for i in range(13)]
        P.act(dt_, ldt, AF.Exp, [s5s], R)
        P.tt("dve", rho, a_re, dt_, ALU.mult, [s5s] + R, R)
        P.act(rho, rho, AF.Exp, R, R)
        P.tt("dve", th, a_im, dt_, ALU.mult, [s5s] + R, R)
        hp = P.sb([128, 1], F32, "halfpi%d" % l)
        P.memset("dve", hp[:], math.pi / 2, [hp])
        P.act(cc, th, AF.Sin, R + [hp], R, bias=hp[:, 0:1], scale=1.0 / 16)
        P.act(ss, th, AF.Sin, R, R, scale=1.0 / 16)

        def csq(c_, s_):
            P.tt("dve", t1, c_, c_, ALU.mult, R, R)
            P.tt("dve", t2, s_, s_, ALU.mult, R, R)
            P.stt(s_, c_, 2.0, s_, ALU.mult, ALU.mult, R, R)
            P.tt("dve", c_, t1, t2, ALU.subtract, R, R)
        for _ in range(4):
            csq(cc, ss)
        P.tt("dve", lr, rho, cc, ALU.mult, R, R)
        P.tt("dve", li, rho, ss, ALU.mult, R, R)
        P.tt("dve", t1, a_re, a_re, ALU.mult, [s5s] + R, R)
        P.tt("dve", t2, a_im, a_im, ALU.mult, [s5s] + R, R)
        P.tt("dve", den, t1, t2, ALU.add, R, R)
        P.recip(den, den, R, R)
        P.ts("dve", nr, lr, -1.0, None, ALU.add, None, R, R)
        P.tt("dve", t1, nr, a_re, ALU.mult, [s5s] + R, R)
        P.tt("dve", t2, li, a_im, ALU.mult, [s5s] + R, R)
        P.tt("dve", t1, t1, t2, ALU.add, R, R)
        P.tt("dve", qre, t1, den, ALU.mult, R, R)
        P.tt("dve", t1, li, a_re, ALU.mult, [s5s] + R, R)
        P.tt("dve", t2, nr, a_im, ALU.mult, [s5s] + R, R)
        P.tt("dve", t1, t1, t2, ALU.subtract, R, R)
        P.tt("dve", qim, t1, den, ALU.mult, R, R)
        keep = s5keep[l]
        P.copy("dve", keep[:, 0, :], rho, R, [keep])
        P.copy("dve", keep[:, 1, :], cc, R, [keep])
        P.copy("dve", keep[:, 2, :], ss, R, [keep])

        sbf = P.sb([128, 512], F32, "s5b_in%d" % l)
        P.dma(sbf[:], s5b_d[l], writes=[sbf])
        bre = sbf[:, 0:256].rearrange("p (j h) -> p j h", h=16)
        bim = sbf[:, 256:512].rearrange("p (j h) -> p j h", h=16)
        Bt = s5B[l]
        Btv = Bt[:, :].rearrange("p (r b q m) -> p r b q m", r=2, b=4, q=2)
        bpad = P.sb([128, 2, 128], BF16, "s5bpad%d" % l)
        tb = P.sb([128, 2, 16], F32, "s5tb%d" % l)
        for j in range(16):
            P.ts("dve", tb[:, 0, :], bim[:, j, :], qim[:, j:j + 1], None, ALU.mult, None, [sbf] + R, [tb])
            P.stt(tb[:, 0, :], bre[:, j, :], qre[:, j:j + 1], tb[:, 0, :], ALU.mult, ALU.subtract, [sbf, tb] + R, [tb])
            P.ts("dve", tb[:, 1, :], bre[:, j, :], qim[:, j:j + 1], None, ALU.mult, None, [sbf] + R, [tb])
            P.stt(tb[:, 1, :], bim[:, j, :], qre[:, j:j + 1], tb[:, 1, :], ALU.mult, ALU.add, [sbf, tb] + R, [tb])
            P.memset("pool", bpad[:], 0.0, [bpad])
            for gh in range(2):
                col0 = 32 * (j % 4) + gh * 16
                for ri in range(2):
                    P.copy("pool", bpad[gh * 64:(gh + 1) * 64, ri, col0:col0 + 16],
                           tb[gh * 64:(gh + 1) * 64, ri, :], [tb], [bpad])
            for ri in range(2):
                P.mm(ps_misc[:, ri * 128:(ri + 1) * 128], bpad[:, ri, :], ident_b, True, True, [bpad, cstb], [ps_misc])
            hf = (j % 4) // 2
            for ri in range(2):
                P.copy("act", Btv[64 * hf:64 * hf + 64, ri, j // 4, j % 2, :],
                       ps_misc[64 * hf:64 * hf + 64, ri * 128:(ri + 1) * 128], [ps_misc], [Bt])

        scf = P.sb([128, 2048], F32, "s5c_in%d" % l)
        P.dma(scf[:], s5c_d[l], writes=[scf])
        Ct = s5C[l]
        P.copy("act", Ct[:, 0:1024], scf[:, 0:1024], [scf], [Ct])
        P.ts("dve", Ct[:, 1024:2048], scf[:, 1024:2048], -1.0, None, ALU.mult, None, [scf], [Ct])

        tab = P.sb([128, 2, 16, TS5], F32, "s5tabb%d" % l)
        P.memset("pool", tab[:, 0, :, 0:1], 1.0, [tab])
        P.memset("pool", tab[:, 1, :, 0:1], 0.0, [tab])
        ec = P.sb([128, 2, 16], F32, "s5ec%d" % l)
        P.copy("dve", ec[:, 0, :], cc, R, [ec])
        P.copy("dve", ec[:, 1, :], ss, R, [ec])
        m = 1
        while m < TS5:
            for j in range(16):
                cj = ec[:, 0, j:j + 1]
                sj = ec[:, 1, j:j + 1]
                src_c = tab[:, 0, j, 0:m]
                src_s = tab[:, 1, j, 0:m]
                dst_c = tab[:, 0, j, m:2 * m]
                dst_s = tab[:, 1, j, m:2 * m]
                P.ts("dve", dst_c, src_s, sj, None, ALU.mult, None, [tab, ec], [tab])
                P.stt(dst_c, src_c, cj, dst_c, ALU.mult, ALU.subtract, [tab, ec], [tab])
                P.ts("dve", dst_s, src_c, sj, None, ALU.mult, None, [tab, ec], [tab])
                P.stt(dst_s, src_s, cj, dst_s, ALU.mult, ALU.add, [tab, ec], [tab])
            e_c = ec[:, 0, :]
            e_s = ec[:, 1, :]
            P.tt("dve", t1, e_c, e_c, ALU.mult, [ec] + R, R)
            P.tt("dve", t2, e_s, e_s, ALU.mult, [ec] + R, R)
            P.stt(e_s, e_c, 2.0, e_s, ALU.mult, ALU.mult, [ec], [ec])
            P.tt("dve", e_c, t1, t2, ALU.subtract, R, [ec])
            m *= 2
        rot = s5rotb[l]
        P.copy("dve", rot[:], ec[:], [ec], [rot])
        s5rot.append((rot, keep))
        P.dma(s5tab_d[l], tab[:, :, :, :].rearrange("p a j t -> p (a j t)"), reads=[tab], writes=[tabsb])
        dbg_dump("s5w%d" % l, w, w[:, :, :].rearrange("p a b -> p (a b)"), [128, 256])
        dbg_dump("s5keep%d" % l, keep, keep[:, :, :].rearrange("p a b -> p (a b)"), [128, 48])
        dbg_dump("s5tab%d" % l, tab, tab[:, :, :, :].rearrange("p a j t -> p (a j t)"), [128, 2 * 16 * TS5])
        dbg_dump("s5B%d" % l, Bt, Bt[:, :], [128, 2048], BF16)
    P.pop()

    hT = P.sb([128, 8, TT], F32, "hT")
    xn = P.sb([128, 8, TT], BF16, "xn")
    zst = [P.sb([128, 1 + TT], F32, "zst%d" % i) for i in range(2)]
    zs = P.sb([128, 13, TT], F32, "zs")
    zsv = P.views(zs, 13)
    zu = P.sb([128, 4, TT], F32, "zu")
    zx = P.sb([128, 4, 3 + TT], F32, "zx")
    sgate = P.sb([128, 12, TT], BF16, "sgate")
    sgv = P.views(sgate, 12)
    ycat = P.sb([128, 12, TT], BF16, "ycat")
    ycv = P.views(ycat, 12)
    pbf = P.sb([128, 2, TT], BF16, "pbf")
    ring = [P.sb([128, 4096], BF16, "ring%d" % i) for i in range(3)]
    ringi = [0]
    s5tab = P.sb([128, 2, 16, TS5], F32, "s5tab")
    banks = [P.ps("ps%d" % i) for i in range(7)] + [ps_misc]
    rot_i = {"proj": 0, "rw": 0}

    def bank(group):
        ids = (0, 1) if group == "proj" else (2, 3, 6)
        i = rot_i[group]
        rot_i[group] = (i + 1) % len(ids)
        return banks[ids[i]]

    def next_ring():
        r = ring[ringi[0]]
        ringi[0] = (ringi[0] + 1) % 3
        return r

    cz = [P.sb([128, 13], F32, "cz%d" % l) for l in range(2)]
    cl = [P.sb([128, 4, 3], F32, "cl%d" % l) for l in range(2)]
    ch = [P.sb([128, 4], F32, "ch%d" % l) for l in range(2)]
    s5z = [P.sb([128, 2, 16], F32, "s5z%d" % l) for l in range(2)]
    s5zv = [P.views(s5z[l], 16) for l in range(2)]
    Tst = [[P.sb([128, 128], BF16, "T%d_%d" % (l, pb)) for pb in range(4)] for l in range(2)]
    for l in range(2):
        P.memset("pool", cz[l][:], 0.0, [cz[l]])
        P.memset("pool", cl[l][:], 0.0, [cl[l]])
        P.memset("pool", ch[l][:], 0.0, [ch[l]])
        P.memset("pool", s5z[l][:], 0.0, s5zv[l])
        for pb in range(4):
            P.memset("pool", Tst[l][pb][:], 0.0, [Tst[l][pb]])

    NF = 17
    fs = [P.sb([128, TT], F32, "rf%d" % i) for i in range(NF)]
    pad_names = ["RTp", "KTp", "CTp", "BTp", "VTp", "KGp", "BGp"]
    pads = {n: P.sb([128, NCH * 128], BF16, n) for n in pad_names}
    for n in pad_names:
        P.memset("pool", pads[n][:], 0.0, [pads[n]])
    RTc = P.sb([128, TT], BF16, "RTc")
    tanh_wd = P.sb([128, TT], BF16, "tanhwd")
    Blev_i = [[P.sb([128, NCH * 128], BF16, "Blev%d_%d" % (i, k)) for i in range(2)] for k in range(2)]
    BTlev_i = [[P.sb([128, NCH * 128], BF16, "BTlev%d_%d" % (i, k)) for i in range(2)] for k in range(2)]
    AkkT_i = [P.sb([128, NCH * 128], BF16, "AkkT_%d" % k) for k in range(2)]
    Xbf_i = [P.sb([128, NCH * 256], BF16, "Xbf_%d" % k) for k in range(2)]
    NPI = 2
    pp = []
    for i in range(NPI):
        d = {}
        for n in ("Vbd", "KGbd", "BGbd", "PT", "nU0", "Wbd"):
            d[n] = P.sb([128, NCH * 128], BF16, "%s_%d" % (n, i))
        for n in ("Rhat", "ArkT", "ArbT"):
            d[n] = P.sb([128, TT], BF16, "%s_%d" % (n, i))
        d["bonus"] = P.sb([128, TT], F32, "bonus_%d" % i)
        d["GL"] = P.sb([128, NCH], F32, "GL_%d" % i)
        d["rt32"] = P.sb([128, TT], F32, "rt32_%d" % i)
        pp.append(d)
    mix = P.sb([128, 4, TT], F32, "mix")
    mixv = P.views(mix, 4)

    NS5SET = 2
    s5f = [[P.sb([128, TS5], F32, "s5f%d_%d" % (k, i)) for i in range(8)] for k in range(NS5SET)]
    NS5X = 3
    s5x = [P.sb([128, 2, TS5], BF16, "s5x%d" % k) for k in range(NS5X)]
    s5zl = [P.sb([128, 2], F32, "s5zl%d" % k) for k in range(NS5SET)]
    spf = [P.sb([128, TT], F32, "spf%d" % i) for i in range(3)]
    lf = [P.sb([128, TT], F32, "lf%d" % i) for i in range(4)]
    ubf = P.sb([128, 4, TT], BF16, "ubf")
    lb = P.sb([128, TT], BF16, "lb")
    rstd = P.sb([128, TT], F32, "rstd")
    sq = P.sb([128, 4, TT], BF16, "sq")
    ones_b = P.sb([128, 128], BF16, "ones_b")
    P.memset("pool", ones_b[:], 1.0, [ones_b])

    out_toks = []
    eps_t = {}
    for e_ in (NORM_EPS, GN_EPS):
        t = P.sb([128, 1], F32, "eps%d" % len(eps_t))
        P.memset("pool", t[:], e_, [t])
        eps_t[e_] = t
    neg_half = -math.exp(-0.5)

    def rms_rstd(src_bufs, src_ap_fn, nblk, eps):
        b = bank("proj")
        for k in range(nblk):
            P.act(sq[:, k % 4, :], src_ap_fn(k), AF.Square, src_bufs, [sq])
            P.mm(b[:, 0:TT], ones_b[:], sq[:, k % 4, :], k == 0, k == nblk - 1, [ones_b, sq], [b])
        P.act(rstd[:], b[:, 0:TT], AF.Ln, [b, eps_t[eps]], [rstd], bias=eps_t[eps][:, 0:1], scale=1.0 / (nblk * 128))
        P.act(rstd[:], rstd[:], AF.Exp, [rstd], [rstd], scale=-0.5)

    def drive(items):
        active = list(items)
        while active:
            for item in list(active):
                g, w = item
                for _ in range(w):
                    try:
                        next(g)
                    except StopIteration:
                        active.remove(item)
                        break

    s5ctr = [0]

    for it in range(NT):
        t0 = it * TT
        first = (it == 0)
        P.dma(hT[:], xT[:, t0:t0 + TT].rearrange("(k p) t -> p k t", p=128), writes=[hT])
        for l in range(2):
            dbg_on = first and l == 0
            P.dma(pbf[:], pT[l, :, t0:t0 + TT].rearrange("(k p) t -> p k t", p=128), writes=[pbf], eng="pool")
            P.dma(s5tab[:, :, :, :].rearrange("p a j t -> p (a j t)"), s5tab_d[l], reads=[tabsb], writes=[s5tab])
            rms_rstd([hT], lambda k: hT[:, k, :], 8, NORM_EPS)
            for k in range(8):
                P.stt(xn[:, k, :], hT[:, k, :], V(l, "ng", k), rstd[:], ALU.mult, ALU.mult, [hT, vec, rstd], [xn])
            if dbg_on:
                dbg_dump("xn", xn, xn[:, :, :].rearrange("p a t -> p (a t)"), [128, 8 * TT], BF16)

            wchunk = {}

            def in_block(cb, l=l, wchunk=wchunk):
                ci = cb // 4
                if ci not in wchunk:
                    r = next_ring()
                    ncol = 512 if ci < 8 else 128
                    P.dma(r[:, 0:8 * ncol].rearrange("p (k n) -> p k n", k=8),
                          w_in_b[l].rearrange("(k p) n -> p k n", p=128)[:, :, ci * 512:ci * 512 + ncol],
                          reads=[wsb], writes=[r])
                    wchunk[ci] = (r, ncol)
                r, ncol = wchunk[ci]
                rv = r[:, 0:8 * ncol].rearrange("p (k n) -> p k n", k=8)
                c0 = (cb % 4) * 128
                b = bank("proj")
                for k in range(8):
                    P.mm(b[:, 0:TT], rv[:, k, c0:c0 + 128], xn[:, k, :], k == 0, k == 7, [r, xn], [b])
                return b

            for cb in range(13):
                st = zst[cb % 2]
                b = in_block(cb)
                P.copy("act", st[:, 0:1], cz[l][:, cb:cb + 1], [cz[l]], [st])
                P.copy("act", st[:, 1:1 + TT], b[:, 0:TT], [b], [st])
                P.copy("act", cz[l][:, cb:cb + 1], st[:, TT:TT + 1], [st], [cz[l]])
                d_ = lf[cb % 2]
                P.tt("pool", d_[:], st[:, 0:TT], st[:, 1:1 + TT], ALU.subtract, [st], [d_])
                P.stt(zs[:, cb, :], d_[:], V(l, "mu", cb), st[:, 1:1 + TT], ALU.mult, ALU.add, [d_, vec, st], [zsv[cb]])
            if dbg_on:
                dbg_dump("zs", zs, zs[:, :, :].rearrange("p a t -> p (a t)"), [128, 13 * TT])
            P.act(tanh_wd[0:64, :], zs[0:64, 12, :], AF.Tanh, [zsv[12]], [tanh_wd])
            P.copy("act", tanh_wd[64:128, :], zs[64:128, 12, :], [zsv[12]], [tanh_wd])
            for blk in range(4):
                b = in_block(13 + blk)
                P.act(sgate[:, blk, :], b[:, 0:TT], AF.Silu, [b], [sgv[blk]])
            for blk in range(4):
                b = in_block(17 + blk)
                P.copy("act", zu[:, blk, :], b[:, 0:TT], [b], [zu])
            P.copy("pool", ubf[:], zu[:], [zu], [ubf])
            for blk in range(4):
                b = in_block(21 + blk)
                P.act(sgate[:, 4 + blk, :], b[:, 0:TT], AF.Silu, [b], [sgv[4 + blk]])
            P.copy("act", zx[:, :, 0:3], cl[l][:, :, :], [cl[l]], [zx])
            for blk in range(4):
                b = in_block(25 + blk)
                P.copy("act", zx[:, blk, 3:3 + TT], b[:, 0:TT], [b], [zx])
            P.copy("act", cl[l][:, :, :], zx[:, :, TT:TT + 3], [zx], [cl[l]])
            for blk in range(4):
                b = in_block(29 + blk)
                P.act(sgate[:, 8 + blk, :], b[:, 0:TT], AF.Silu, [b], [sgv[8 + blk]])

            def prep(pb, inst, l=l, dbg_on=dbg_on):
                d = pp[inst]
                Blev, BTlev, AkkT = Blev_i[inst], BTlev_i[inst], AkkT_i[inst]
                r_ = zs[:, pb, :]
                k_ = zs[:, 4 + pb, :]
                v_ = zs[:, 8 + pb, :]
                zr_, zk_, zv_ = zsv[pb], zsv[4 + pb], zsv[8 + pb]
                (sg, ld, a_, kk_, kk2, sqk, kap, t1, kp, b_, lg, eg, ieg, eg1, dl, egl, rk) = fs[:17]
                rt32 = d["rt32"]
                cols = slice(pb * 128, (pb + 1) * 128)
                bw = bank("rw")
                P.mm(bw[:, 0:TT], w2a2[l][0:64, cols], tanh_wd[0:64, :], True, True, [w2a2[l], tanh_wd], [bw])
                P.act(sg[:], bw[:, 0:TT], AF.Sigmoid, [bw, vec], [sg], bias=V(l, "w0", pb))
                P.ts("dve", ld[:], sg[:], neg_half, None, ALU.mult, None, [sg], [ld])
                ba_ = bank("rw")
                P.mm(ba_[:, 0:TT], w2a2[l][64:128, cols], tanh_wd[64:128, :], True, True, [w2a2[l], tanh_wd], [ba_])
                P.act(a_[:], ba_[:, 0:TT], AF.Sigmoid, [ba_, vec], [a_], bias=V(l, "a0", pb))
                yield
                P.scan(lg[:], cst[:, CST_SCAN:CST_SCAN + TT], ld[:], 0.0, [cst, ld], [lg])
                P.ts("dve", kk_[:], k_, V(l, "kk", pb), None, ALU.mult, None, [zk_, vec], [kk_])
                P.tt("pool", kk2[:], kk_[:], kk_[:], ALU.mult, [kk_], [kk2])
                P.act(eg[:], lg[:], AF.Exp, [lg], [eg])
                yield
                bs = bank("rw")
                P.mm(bs[:, 0:TT], onesbd_f, kk2[:], True, True, [cst, kk2], [bs])
                P.act(sqk[:], bs[:, 0:TT], AF.Sqrt, [bs], [sqk])
                P.act(ieg[:], lg[:], AF.Exp, [lg], [ieg], scale=-1.0)
                P.tt("pool", eg1[:], lg[:], ld[:], ALU.subtract, [lg, ld], [eg1])
                P.act(eg1[:], eg1[:], AF.Exp, [eg1], [eg1])
                P.ts("dve", sqk[:], sqk[:], 1e-12, None, ALU.max, None, [sqk], [sqk])
                P.recip(sqk[:], sqk[:], [sqk], [sqk])
                P.tt("pool", kap[:], kk_[:], sqk[:], ALU.mult, [kk_, sqk], [kap])
                yield
                P.ts("dve", t1[:], a_[:], -1.0, V(l, "ka", pb), ALU.add, ALU.mult, [a_, vec], [t1])
                P.stt(kp[:], t1[:], 1.0, k_, ALU.add, ALU.mult, [t1, zk_], [kp])
                P.tt("pool", b_[:], kap[:], a_[:], ALU.mult, [kap, a_], [b_])
                lg3 = lg[:, :].rearrange("p (c t) -> p c t", t=LCH)
                P.tt("dve", dl[:, :].rearrange("p (c t) -> p c t", t=LCH), lg3[:, :, LCH - 1:LCH].to_broadcast([128, NCH, LCH]),
                     lg3, ALU.subtract, [lg], [dl])
                P.act(egl[:], dl[:], AF.Exp, [dl], [egl])
                P.copy("act", d["GL"][:, :], eg[:, :].rearrange("p (c t) -> p c t", t=LCH)[:, :, LCH - 1], [eg], [d["GL"]])
                yield
                P.tt("dve", rt32[:], r_, eg[:], ALU.mult, [zr_, eg], [rt32])
                P.copy("act", RTc[:], rt32[:], [rt32], [RTc])

                def padw(name, eng, in0, in1, rd):
                    t = pads[name]
                    tv = t[:, :].rearrange("p (c h t) -> p c h t", c=NCH, h=2)
                    for hh in range(2):
                        ps_ = slice(hh * 64, (hh + 1) * 64)
                        o = tv[ps_, :, hh, :]
                        i0 = in0[ps_, :].rearrange("p (c t) -> p c t", t=LCH)
                        if in1 is None:
                            P.copy(eng, o, i0, rd, [t])
                        else:
                            i1 = in1[ps_, :].rearrange("p (c t) -> p c t", t=LCH)
                            P.tt(eng, o, i0, i1, ALU.mult, rd, [t])
                padw("RTp", "act", rt32, None, [rt32])
                padw("KTp", "dve", kp, ieg, [kp, ieg])
                padw("CTp", "pool", kap, eg1, [kap, eg1])
                yield
                padw("BTp", "dve", b_, ieg, [b_, ieg])
                padw("VTp", "act", zs[:, 8 + pb, :], None, [zv_])
                padw("KGp", "pool", kp, egl, [kp, egl])
                padw("BGp", "dve", b_, egl, [b_, egl])
                yield "pre_pp"
                P.stt(rk[:], r_, V(l, "rk", pb), kp[:], ALU.mult, ALU.mult, [zr_, vec, kp], [rk])
                yield
                bb = bank("rw")
                P.mm(bb[:, 0:TT], onesbd_f, rk[:], True, True, [cst, rk], [bb])
                P.tt("dve", d["bonus"][:], bb[:, 0:TT], v_, ALU.mult, [bb, zv_], [d["bonus"]])
                if dbg_on and pb == 0:
                    dbg_dump("lg", lg, lg[:], [128, TT])
                    dbg_dump("kap", kap, kap[:], [128, TT])
                    dbg_dump("kp", kp, kp[:], [128, TT])
                    dbg_dump("a", a_, a_[:], [128, TT])
                yield

                def chunkmm(dst_bank, lname, rname, rbuf=None, rcols=128):
                    lt = pads[lname]
                    for c in range(NCH):
                        if rbuf is None:
                            rb_ = pads[rname]
                            rap = rb_[:, c * 128:(c + 1) * 128]
                        else:
                            rb_ = rbuf
                            rap = rbuf[:, c * rcols:(c + 1) * rcols]
                        P.mm(dst_bank[:, c * rcols:(c + 1) * rcols], lt[:, c * 128:(c + 1) * 128], rap, True, True,
                             [lt, rb_], [dst_bank])

                def masked(dst, src_bank, mcol, w):
                    n = NCH * w
                    P.tt("dve", dst[:, 0:n], src_bank[:, 0:n], mskb[:, mcol:mcol + n], ALU.mult, [src_bank, mskb], [dst])
                b1 = bank("rw")
                chunkmm(b1, "BTp", "CTp")
                masked(BTlev[0], b1, MSK_USN, 128)
                b2 = bank("rw")
                chunkmm(b2, "CTp", "BTp")
                masked(Blev[0], b2, MSK_LSN, 128)
                yield
                b3 = bank("rw")
                chunkmm(b3, "KTp", "CTp")
                masked(AkkT, b3, MSK_USP, 128)
                b4 = bank("rw")
                chunkmm(b4, "KTp", None, RTc, LCH)
                masked(d["ArkT"], b4, MSK_CI, LCH)
                b5 = bank("rw")
                chunkmm(b5, "BTp", None, RTc, LCH)
                masked(d["ArbT"], b5, MSK_CI, LCH)
                yield

            def tokmajor(src_name, dst_buf, dst_ap, eng):
                bt_ = bank("rw")
                lt = pads[src_name]
                for c in range(NCH):
                    P.mm(bt_[:, c * 128:(c + 1) * 128], lt[:, c * 128:(c + 1) * 128], ident_b, True, True,
                         [lt, cstb], [bt_])
                P.copy(eng, dst_ap, bt_[:, 0:NCH * 128] if len(dst_ap.shape) == 2 else
                       bt_[:, 0:NCH * 128].rearrange("p (c n) -> p c n", n=128), [bt_], [dst_buf])

            def solve(pb, inst, l=l):
                d = pp[inst]
                Blev, BTlev, AkkT, Xbf = Blev_i[inst], BTlev_i[inst], AkkT_i[inst], Xbf_i[inst]
                Xbfv = Xbf[:, :].rearrange("p (c n) -> p c n", n=256)
                tokmajor("VTp", d["Vbd"], d["Vbd"][:, :], "act")
                tokmajor("KGp", d["KGbd"], d["KGbd"][:, :], "act")
                yield
                tokmajor("BGp", d["BGbd"], d["BGbd"][:, :], "act")
                tokmajor("CTp", Xbf, Xbfv[:, :, 0:128], "act")
                bt_ = bank("rw")
                for c in range(NCH):
                    cs = slice(c * 128, (c + 1) * 128)
                    P.mm(bt_[:, cs], AkkT[:, cs], d["Vbd"][:, cs], True, True, [AkkT, d["Vbd"]], [bt_])
                P.copy("act", Xbfv[:, :, 128:256], bt_[:, 0:NCH * 128].rearrange("p (c n) -> p c n", n=128), [bt_], [Xbf])
                yield "tok_done"
                cur = 0
                NLEV = 6
                for lev in range(NLEV):
                    if lev < NLEV - 1:
                        nxt = 1 - cur
                        bq = bank("rw")
                        for c in range(NCH):
                            cs = slice(c * 128, (c + 1) * 128)
                            P.mm(bq[:, cs], Blev[cur][:, cs], BTlev[cur][:, cs], True, True, [Blev[cur], BTlev[cur]], [bq])
                        if lev < NLEV - 2:
                            bq2 = bank("rw")
                            for c in range(NCH):
                                cs = slice(c * 128, (c + 1) * 128)
                                P.mm(bq2[:, cs], BTlev[cur][:, cs], Blev[cur][:, cs], True, True,
                                     [Blev[cur], BTlev[cur]], [bq2])
                    for half in range(2):
                        bx_ = banks[4 + half]
                        for cc_ in range(2):
                            c = half * 2 + cc_
                            P.mm(bx_[:, cc_ * 256:(cc_ + 1) * 256], BTlev[cur][:, c * 128:(c + 1) * 128], Xbfv[:, c, :],
                                 True, False, [BTlev[cur], Xbf], [bx_])
                            P.mm(bx_[:, cc_ * 256:(cc_ + 1) * 256], ident_b, Xbfv[:, c, :],
                                 False, True, [cstb, Xbf], [bx_])
                    if lev < NLEV - 1:
                        P.copy("act", BTlev[nxt][:], bq[:, 0:512], [bq], [BTlev[nxt]])
                        if lev < NLEV - 2:
                            P.copy("dve", Blev[nxt][:], bq2[:, 0:512], [bq2], [Blev[nxt]])
                    P.copy("act", Xbf[:, 0:512], banks[4][:, 0:512], [banks[4]], [Xbf])
                    P.copy("dve", Xbf[:, 512:1024], banks[5][:, 0:512], [banks[5]], [Xbf])
                    if lev < NLEV - 1:
                        cur = nxt
                    yield
                nU0v = d["nU0"][:, :].rearrange("p (c n) -> p c n", n=128)
                Wbdv = d["Wbd"][:, :].rearrange("p (c n) -> p c n", n=128)
                P.ts("dve", nU0v, Xbfv[:, :, 128:256], -1.0, None, ALU.mult, None, [Xbf], [d["nU0"]])
                P.copy("pool", Wbdv, Xbfv[:, :, 0:128], [Xbf], [d["Wbd"]])
                yield
                br = bank("rw")
                for c in range(NCH):
                    P.mm(br[:, c * LCH:(c + 1) * LCH], d["Wbd"][:, c * 128:(c + 1) * 128],
                         d["ArbT"][:, c * LCH:(c + 1) * LCH], True, True, [d["Wbd"], d["ArbT"]], [br])
                P.tt("dve", d["Rhat"][:], d["rt32"][:], br[:, 0:TT], ALU.subtract, [d["rt32"], br], [d["Rhat"]])
                bp = bank("rw")
                for c in range(NCH):
                    cs = slice(c * 128, (c + 1) * 128)
                    P.mm(bp[:, cs], d["Wbd"][:, cs], d["BGbd"][:, cs], True, True, [d["Wbd"], d["BGbd"]], [bp])
                for c in range(NCH):
                    cs = slice(c * 128, (c + 1) * 128)
                    P.stt(d["PT"][:, cs], ident_f, d["GL"][:, c:c + 1], bp[:, cs], ALU.mult, ALU.subtract,
                          [cst, d["GL"], bp], [d["PT"]])
                yield

            def seq(pb, inst, c, l=l):
                d = pp[inst]
                T = Tst[l][pb]
                tb_ = banks[4 + inst]
                yb = tb_
                ycols = slice(128 + c * LCH, 128 + (c + 1) * LCH)
                cs = slice(c * 128, (c + 1) * 128)
                cl_ = slice(c * LCH, (c + 1) * LCH)
                P.mm(yb[:, ycols], T[:], d["Rhat"][:, cl_], True, False, [T, d["Rhat"]], [yb])
                P.mm(yb[:, ycols], d["Vbd"][:, cs], d["ArkT"][:, cl_], False, False, [d["Vbd"], d["ArkT"]], [yb])
                P.mm(yb[:, ycols], d["nU0"][:, cs], d["ArbT"][:, cl_], False, True, [d["nU0"], d["ArbT"]], [yb])
                P.mm(tb_[:, 0:128], d["PT"][:, cs], T[:], True, False, [d["PT"], T], [tb_])
                P.mm(tb_[:, 0:128], d["KGbd"][:, cs], d["Vbd"][:, cs], False, False, [d["KGbd"], d["Vbd"]], [tb_])
                P.mm(tb_[:, 0:128], d["BGbd"][:, cs], d["nU0"][:, cs], False, True, [d["BGbd"], d["nU0"]], [tb_])
                P.copy("act", T[:], tb_[:, 0:128], [tb_], [T])

            def fin(pb, inst, l=l, dbg_on=dbg_on):
                assert lru_done[0], "fin emitted before the LRU chain finished (scratch lf[3] still live)"
                d = pp[inst]
                yb = banks[4 + inst]
                y32, yc, ysq, rs = spf[0], spf[1], spf[2], lf[3]
                P.copy("act", y32[:], yb[:, 128:128 + TT], [yb], [y32])
                if dbg_on:
                    dbg_dump("y_rw%d" % pb, y32, y32[:], [128, TT])
                bm = bank("rw")
                P.mm(bm[:, 0:TT], onesbd_f, y32[:], True, True, [cst, y32], [bm])
                P.stt(yc[:], bm[:, 0:TT], -1.0 / 64, y32[:], ALU.mult, ALU.add, [bm, y32], [yc])
                P.act(ysq[:], yc[:], AF.Square, [yc], [ysq])
                bv = bank("rw")
                P.mm(bv[:, 0:TT], onesbd_f, ysq[:], True, True, [cst, ysq], [bv])
                P.act(rs[:], bv[:, 0:TT], AF.Ln, [bv, eps_t[GN_EPS]], [rs], bias=eps_t[GN_EPS][:, 0:1], scale=1.0 / 64)
                P.act(rs[:], rs[:], AF.Exp, [rs], [rs], scale=-0.5)
                P.tt("dve", yc[:], yc[:], rs[:], ALU.mult, [yc, rs], [yc])
                P.ts("dve", yc[:], yc[:], V(l, "lnw", pb), V(l, "lnb", pb), ALU.mult, ALU.add, [yc, vec], [yc])
                P.tt("pool", yc[:], yc[:], d["bonus"][:], ALU.add, [yc, d["bonus"]], [yc])
                P.tt("dve", ycat[:, pb, :], yc[:], sgate[:, pb, :], ALU.mult, [yc, sgv[pb]], [ycv[pb]])

            def rwkv_gen():
                def chain(pb, inst):
                    yield from prep(pb, inst)
                    if pb == 3:
                        prep3_done[0] = True
                    yield from solve(pb, inst)

                def tail(pbs):
                    for c in range(NCH):
                        for inst, pb in enumerate(pbs):
                            seq(pb, inst, c)
                            yield
                    for inst, pb in enumerate(pbs):
                        fin(pb, inst)
                        yield

                def merge(ga, gb):
                    da = db = False
                    while not (da and db):
                        if not da:
                            try:
                                next(ga)
                                yield
                            except StopIteration:
                                da = True
                        if not db:
                            try:
                                next(gb)
                                yield
                            except StopIteration:
                                db = True

                def half_front(pbs):
                    gA, gB = chain(pbs[0], 0), chain(pbs[1], 1)
                    for v in gA:
                        yield
                        if v == "tok_done":
                            break
                    yield from merge(gA, gB)

                yield from half_front((0, 1))
                t0_ = tail((0, 1))
                gA, gB = chain(2, 0), chain(3, 1)

                def front2a():
                    for v in gA:
                        yield
                        if v == "pre_pp":
                            break
                yield from merge(t0_, front2a())
                half0_done[0] = True
                for v in gA:
                    yield
                    if v == "tok_done":
                        break
                yield from merge(gA, gB)
                yield from tail((2, 3))

            def s5_gen(l=l, dbg_on=dbg_on):
                Bv = s5B[l][:, :].rearrange("p (r b q m) -> p r b q m", r=2, b=4, q=2)
                Cv = s5C[l][:, :].rearrange("p (r j m) -> p r j m", r=2, j=16)
                rotk, keep = s5rot[l]
                NS = TT // TS5
                yb5 = banks[7]

                def stage0(blk, s, jj, k):
                    hf, jh = jj // 2, jj % 2
                    hs = slice(64 * hf, 64 * hf + 64)
                    tsl = slice(s * TS5, (s + 1) * TS5)
                    bu = banks[k % 2]
                    c0 = 0
                    bre = bu[:, c0:c0 + TS5]
                    bim = bu[:, c0 + TS5:c0 + 2 * TS5]
                    P.mm(bre, Bv[hs, 0, blk, jh, :], ubf[hs, blk, tsl], True, True, [s5B[l], ubf], [bu])
                    P.mm(bim, Bv[hs, 1, blk, jh, :], ubf[hs, blk, tsl], True, True, [s5B[l], ubf], [bu])

                def stage1(blk, s, jj, k):
                    j = blk * 4 + jj
                    (t1, t2, bzr, bzi, zr, zi, t3, t4) = s5f[k]
                    bu = banks[k % 2]
                    c0 = 0
                    bre = bu[:, c0:c0 + TS5]
                    bim = bu[:, c0 + TS5:c0 + 2 * TS5]
                    cosT = s5tab[:, 0, j, :]
                    sinT = s5tab[:, 1, j, :]
                    P.tt("dve", t1[:], bre, cosT, ALU.mult, [bu, s5tab], [t1])
                    P.tt("dve", t2[:], bim, sinT, ALU.mult, [bu, s5tab], [t2])
                    P.tt("dve", t3[:], bim, cosT, ALU.mult, [bu, s5tab], [t3])
                    P.tt("dve", t4[:], bre, sinT, ALU.mult, [bu, s5tab], [t4])
                    P.tt(S5E, bzr[:], t1[:], t2[:], ALU.add, [t1, t2], [bzr])
                    P.tt(S5E, bzi[:], t3[:], t4[:], ALU.subtract, [t3, t4], [bzi])

                def stage2(blk, s, jj, k, kx):
                    j = blk * 4 + jj
                    hf, jh = jj // 2, jj % 2
                    hs = slice(64 * hf, 64 * hf + 64)
                    tsl = slice(s * TS5, (s + 1) * TS5)
                    (t1, t2, bzr, bzi, zr, zi, t3, t4) = s5f[k]
                    sx = s5x[kx]
                    zl = s5zl[k]
                    zv = s5zv[l][j]
                    cosT = s5tab[:, 0, j, :]
                    sinT = s5tab[:, 1, j, :]
                    rho_b = keep[:, 0, j:j + 1].to_broadcast([128, TS5])
                    P.scan(zr[:], rho_b, bzr[:], s5z[l][:, 0, j:j + 1], [keep, bzr, zv], [zr])
                    P.scan(zi[:], rho_b, bzi[:], s5z[l][:, 1, j:j + 1], [keep, bzi, zv], [zi])
                    P.tt("dve", t1[:], zr[:], cosT, ALU.mult, [zr, s5tab], [t1])
                    P.tt("dve", t2[:], zi[:], sinT, ALU.mult, [zi, s5tab], [t2])
                    P.tt(S5E, sx[:, 0, :], t1[:], t2[:], ALU.subtract, [t1, t2], [sx])
                    P.tt("dve", t3[:], zr[:], sinT, ALU.mult, [zr, s5tab], [t3])
                    P.tt(S5E, t4[:], zi[:], cosT, ALU.mult, [zi, s5tab], [t4])
                    P.tt(S5E, sx[:, 1, :], t3[:], t4[:], ALU.add, [t3, t4], [sx])
                    rc = rotk[:, 0, j:j + 1]
                    rs_ = rotk[:, 1, j:j + 1]
                    zlr = zr[:, TS5 - 1:TS5]
                    zli = zi[:, TS5 - 1:TS5]
                    P.ts("dve", zl[:, 0:1], zli, rs_, None, ALU.mult, None, [zi, rotk], [zl])
                    P.ts("dve", zl[:, 1:2], zlr, rs_, None, ALU.mult, None, [zr, rotk], [zl])
                    P.stt(s5z[l][:, 0, j:j + 1], zlr, rc, zl[:, 0:1], ALU.mult, ALU.subtract, [zr, rotk, zl], [zv])
                    P.stt(s5z[l][:, 1, j:j + 1], zli, rc, zl[:, 1:2], ALU.mult, ALU.add, [zi, rotk, zl], [zv])

                def stage3(blk, s, jj, kx):
                    j = blk * 4 + jj
                    hf, jh = jj // 2, jj % 2
                    hs = slice(64 * hf, 64 * hf + 64)
                    tsl = slice(s * TS5, (s + 1) * TS5)
                    sx = s5x[kx]
                    P.mm(yb5[hs, tsl], Cv[:, 0, j, :], sx[:, 0, :], jh == 0, False, [s5C[l], sx], [yb5])
                    P.mm(yb5[hs, tsl], Cv[:, 1, j, :], sx[:, 1, :], False, jh == 1, [s5C[l], sx], [yb5])

                for blk in range(4):
                    units = [(s, jj) for s in range(NS) for jj in range(4)]
                    ks = []
                    for (s, jj) in units:
                        ks.append(s5ctr[0] % NS5SET)
                        s5ctr[0] += 1
                    nU = len(units)
                    stage0(blk, units[0][0], units[0][1], ks[0])
                    stage0(blk, units[1][0], units[1][1], ks[1])
                    stage1(blk, units[0][0], units[0][1], ks[0])
                    yield
                    for u in range(nU):
                        if u + 1 < nU:
                            stage1(blk, units[u + 1][0], units[u + 1][1], ks[u + 1])
                            if u + 2 < nU:
                                stage0(blk, units[u + 2][0], units[u + 2][1], ks[u + 2])
                            yield
                        stage2(blk, units[u][0], units[u][1], ks[u], u % NS5X)
                        if u >= 2:
                            stage3(blk, units[u - 2][0], units[u - 2][1], (u - 2) % NS5X)
                        yield
                    stage3(blk, units[nU - 2][0], units[nU - 2][1], (nU - 2) % NS5X)
                    stage3(blk, units[nU - 1][0], units[nU - 1][1], (nU - 1) % NS5X)
                    ys, x2, q_ = spf
                    P.stt(ys[:], zu[:, blk, :], V(l, "s5d", blk), yb5[:, 0:TT], ALU.mult, ALU.add, [zu, vec, yb5], [ys])
                    if dbg_on:
                        dbg_dump("s5y%d" % blk, ys, ys[:], [128, TT])
                    P.act(x2[:], ys[:], AF.Square, [ys], [x2])
                    P.ts("dve", x2[:], x2[:], 0.044715, 1.0, ALU.mult, ALU.add, [x2], [x2])
                    P.tt("pool", q_[:], x2[:], ys[:], ALU.mult, [x2, ys], [q_])
                    P.act(x2[:], q_[:], AF.Sigmoid, [q_], [x2], scale=2.0 * math.sqrt(2.0 / math.pi))
                    P.tt("dve", mix[:, blk, :], ys[:], x2[:], ALU.mult, [ys, x2], [mixv[blk]])
                    yield
                zgb = ubf
                P.copy("pool", zgb[:], mix[:], mixv, [zgb])
                rg = next_ring()
                P.dma(rg[:, 0:2048].rearrange("p (k n) -> p k n", k=4), glu_w_b[l].rearrange("(k p) n -> p k n", p=128),
                      reads=[wsb], writes=[rg])
                rgv = rg[:, 0:2048].rearrange("p (k n) -> p k n", k=4)
                for ob in range(4):
                    b = banks[0]
                    for k in range(4):
                        P.mm(b[:, 0:TT], rgv[:, k, ob * 128:(ob + 1) * 128], zgb[:, k, :], k == 0, k == 3, [rg, zgb], [b])
                    sg_ = spf[ob % 2]
                    P.act(sg_[:], b[:, 0:TT], AF.Sigmoid, [b, vec], [sg_], bias=V(l, "glub", ob))
                    P.tt("pool", sg_[:], sg_[:], sgate[:, 4 + ob, :], ALU.mult, [sg_, sgv[4 + ob]], [sg_])
                    P.tt("dve", ycat[:, 4 + ob, :], mix[:, ob, :], sg_[:], ALU.mult, [mixv[ob], sg_], [ycv[4 + ob]])
                    if ob == 3:
                        s5_done[0] = True
                    yield

            lru_done = [("lru" in SKIP)]
            s5_done = [False]
            half0_done = [False]
            prep3_done = [False]

            def lru_gen(l=l, dbg_on=dbg_on):
                for blk in range(4):
                    A, B, C, Dd = lf
                    bl = banks[5]
                    P.ts("dve", A[:], zx[:, blk, 0:TT], V(l, "cw", 0 * 4 + blk), V(l, "cb", blk), ALU.mult, ALU.add,
                         [zx, vec], [A])
                    for j in range(1, 4):
                        P.stt(A[:], zx[:, blk, j:j + TT], V(l, "cw", j * 4 + blk), A[:], ALU.mult, ALU.add,
                              [zx, vec, A], [A])
                    P.copy("act", lb[:], A[:], [A], [lb])
                    yield
                    P.mm(bl[:, 0:TT], lruw[l][:, blk * 128:(blk + 1) * 128], lb[:], True, True, [lruw[l], lb], [bl])
                    P.mm(bl[:, TT:2 * TT], lruw[l][:, (4 + blk) * 128:(5 + blk) * 128], lb[:], True, True,
                         [lruw[l], lb], [bl])
                    P.act(B[:], bl[:, 0:TT], AF.Sigmoid, [bl, vec], [B], bias=V(l, "ba", blk))
                    P.act(C[:], bl[:, TT:2 * TT], AF.Sigmoid, [bl, vec], [C], bias=V(l, "bx", blk))
                    yield
                    P.act(Dd[:], B[:], AF.Exp, [B, lru_c], [Dd], scale=lru_c[:, l, blk:blk + 1])
                    P.act(B[:], B[:], AF.Exp, [B, lru_c], [B], scale=lru_c[:, l, 4 + blk:5 + blk])
                    P.act(B[:], B[:], AF.Ln, [B, one_t], [B], bias=one_t[:, 0:1], scale=-1.0)
                    P.act(B[:], B[:], AF.Exp, [B], [B], scale=0.5)
                    P.tt("pool", C[:], C[:], A[:], ALU.mult, [C, A], [C])
                    P.tt("pool", C[:], C[:], B[:], ALU.mult, [C, B], [C])
                    yield
                    P.scan(A[:], Dd[:], C[:], ch[l][:, blk:blk + 1], [Dd, C, ch[l]], [A])
                    P.copy("act", ch[l][:, blk:blk + 1], A[:, TT - 1:TT], [A], [ch[l]])
                    if dbg_on:
                        dbg_dump("lru%d" % blk, A, A[:], [128, TT])
                    P.tt("pool", ycat[:, 8 + blk, :], A[:], sgate[:, 8 + blk, :], ALU.mult, [A, sgv[8 + blk]], [ycv[8 + blk]])
                    if blk == 3:
                        lru_done[0] = True
                    yield

            KT_EARLY = (0, 1, 4, 5, 6, 7, 8, 9, 10, 11)
            KT_LATE = (2, 3)

            def oproj_early(l=l):
                while not (s5_done[0] and lru_done[0] and half0_done[0]):
                    yield
                for oc in range(4):
                    r = next_ring()
                    P.dma(r[:, 0:12 * 256].rearrange("p (k n) -> p k n", k=12),
                          w_out_b[l].rearrange("(k p) n -> p k n", p=128)[:, :, oc * 256:(oc + 1) * 256],
                          reads=[wsb], writes=[r])
                    rv = r[:, 0:12 * 256].rearrange("p (k n) -> p k n", k=12)
                    yield
                    for ob2 in range(2):
                        ob = oc * 2 + ob2
                        b = bank("proj")
                        for i_, k in enumerate(KT_EARLY):
                            P.mm(b[:, 0:TT], rv[:, k, ob2 * 128:(ob2 + 1) * 128], ycat[:, k, :], i_ == 0,
                                 i_ == len(KT_EARLY) - 1, [r, ycv[k]], [b])
                        P.tt("dve", hT[:, ob, :], hT[:, ob, :], b[:, 0:TT], ALU.add, [hT, b], [hT])
                        yield

            def ple_early(l=l):
                while not (prep3_done[0] and s5_done[0]):
                    yield
                r = next_ring()
                P.dma(r[:, 0:2048].rearrange("p (k n) -> p k n", k=2), ple_w_b[l].rearrange("(k p) n -> p k n", p=128),
                      reads=[wsb], writes=[r])
                rv = r[:, 0:2048].rearrange("p (k n) -> p k n", k=2)
                yield
                for ob in range(8):
                    b = bank("proj")
                    for k in range(2):
                        P.mm(b[:, 0:TT], rv[:, k, ob * 128:(ob + 1) * 128], pbf[:, k, :], k == 0, k == 1, [r, pbf], [b])
                    P.copy("act", zs[:, ob, :], b[:, 0:TT], [b], [zsv[ob]])
                    yield
                rms_rstd(zsv[0:8], lambda k: zs[:, k, :], 8, NORM_EPS)
                yield

            early_ok = ("rwkv" not in SKIP) and ("s5" not in SKIP) and ("lru" not in SKIP)
            gens = []
            if early_ok:
                gens.append((oproj_early(), 1))
                gens.append((ple_early(), 1))
            if "rwkv" not in SKIP:
                gens.append((rwkv_gen(), GW[0]))
            if "s5" not in SKIP:
                gens.append((s5_gen(), GW[1]))
            if "lru" not in SKIP:
                gens.append((lru_gen(), GW[2]))
            if "serial" in SKIP:
                for g, w in gens:
                    drive([(g, 1)])
            else:
                drive(gens)
            if dbg_on:
                dbg_dump("ycat_rw", ycat, ycat[:, 0:4, :].rearrange("p a t -> p (a t)"), [128, 4 * TT], BF16)

            if early_ok:
                r = next_ring()
                P.dma(r[:, 0:2 * 1024].rearrange("p (k n) -> p k n", k=2),
                      w_out_b[l].rearrange("(k p) n -> p k n", p=128)[:, 2:4, :],
                      reads=[wsb], writes=[r])
                rv = r[:, 0:2 * 1024].rearrange("p (k n) -> p k n", k=2)
                for ob in range(8):
                    b = bank("proj")
                    for i_, k in enumerate(KT_LATE):
                        P.mm(b[:, 0:TT], rv[:, i_, ob * 128:(ob + 1) * 128], ycat[:, k, :], i_ == 0, i_ == 1,
                             [r, ycv[k]], [b])
                    P.tt("dve", hT[:, ob, :], hT[:, ob, :], b[:, 0:TT], ALU.add, [hT, b], [hT])
            else:
                for oc in range(4):
                    r = next_ring()
                    P.dma(r[:, 0:12 * 256].rearrange("p (k n) -> p k n", k=12),
                          w_out_b[l].rearrange("(k p) n -> p k n", p=128)[:, :, oc * 256:(oc + 1) * 256],
                          reads=[wsb], writes=[r])
                    rv = r[:, 0:12 * 256].rearrange("p (k n) -> p k n", k=12)
                    for ob2 in range(2):
                        ob = oc * 2 + ob2
                        b = bank("proj")
                        for k in range(12):
                            P.mm(b[:, 0:TT], rv[:, k, ob2 * 128:(ob2 + 1) * 128], ycat[:, k, :], k == 0, k == 11,
                                 [r] + ycv, [b])
                        P.tt("dve", hT[:, ob, :], hT[:, ob, :], b[:, 0:TT], ALU.add, [hT, b], [hT])
            for k in range(8):
                P.copy("act" if k % 2 == 0 else "dve", xn[:, k, :], hT[:, k, :], [hT], [xn])
            epre = zs
            if not early_ok:
                r = next_ring()
                P.dma(r[:, 0:2048].rearrange("p (k n) -> p k n", k=2), ple_w_b[l].rearrange("(k p) n -> p k n", p=128),
                      reads=[wsb], writes=[r])
                rv = r[:, 0:2048].rearrange("p (k n) -> p k n", k=2)
                for ob in range(8):
                    b = bank("proj")
                    for k in range(2):
                        P.mm(b[:, 0:TT], rv[:, k, ob * 128:(ob + 1) * 128], pbf[:, k, :], k == 0, k == 1, [r, pbf], [b])
                    P.copy("act", epre[:, ob, :], b[:, 0:TT], [b], [zsv[ob]])
                rms_rstd(zsv[0:8], lambda k: epre[:, k, :], 8, NORM_EPS)
            for gc in range(2):
                r = next_ring()
                P.dma(r[:, 0:4096].rearrange("p (k n) -> p k n", k=8),
                      ple_gw_b[l].rearrange("(k p) n -> p k n", p=128)[:, :, gc * 512:(gc + 1) * 512],
                      reads=[wsb], writes=[r])
                rv = r[:, 0:4096].rearrange("p (k n) -> p k n", k=8)
                for ob2 in range(4):
                    ob = gc * 4 + ob2
                    b = bank("proj")
                    for k in range(8):
                        P.mm(b[:, 0:TT], rv[:, k, ob2 * 128:(ob2 + 1) * 128], xn[:, k, :], k == 0, k == 7, [r, xn], [b])
                    sg_, e_ = fs[6 + 2 * (ob % 2)], fs[7 + 2 * (ob % 2)]
                    P.act(sg_[:], b[:, 0:TT], AF.Sigmoid, [b], [sg_])
                    P.stt(e_[:], epre[:, ob, :], V(l, "png", ob), rstd[:], ALU.mult, ALU.mult, [zsv[ob], vec, rstd], [e_])
                    P.tt("dve", e_[:], e_[:], sg_[:], ALU.mult, [e_, sg_], [e_])
                    P.tt("dve", hT[:, ob, :], hT[:, ob, :], e_[:], ALU.add, [hT, e_], [hT])
            if dbg_on:
                dbg_dump("h1", hT, hT[:, :, :].rearrange("p a t -> p (a t)"), [128, 8 * TT])
        rms_rstd([hT], lambda k: hT[:, k, :], 8, NORM_EPS)
        fo = 2 * VEC_PER_LAYER
        for k in range(8):
            P.stt(zs[:, k, :], hT[:, k, :], vec[:, fo + k:fo + k + 1], rstd[:], ALU.mult, ALU.mult, [hT, vec, rstd], [zsv[k]])
        out_toks.append(P.dma(oT[:, t0:t0 + TT].rearrange("(k p) t -> p k t", p=128), zs[:, 0:8, :], reads=zsv[0:8]))
    out_toks.extend(dbg_out.values())
    ninst = P.ninst
    P.finish(out_toks)
    return nc, ninst


def pack_shared(inp):
    f = lambda a: np.asarray(a, np.float32)
    vec = np.zeros((128, NVEC), np.float32)
    for l in range(2):
        def put(name, arr, n):
            c = vcol(l, name)
            vec[:, c:c + n] = _pp(arr, n)
        put("ng", f(inp["norm_g"])[l], 8)
        put("mu", f(inp["rwkv_mu"])[l], 13)
        put("w0", f(inp["rwkv_w0"])[l], 4)
        put("a0", f(inp["rwkv_a0"])[l], 4)
        put("kk", f(inp["rwkv_k_k"])[l], 4)
        put("ka", f(inp["rwkv_k_a"])[l], 4)
        put("rk", f(inp["rwkv_r_k"])[l].reshape(512), 4)
        put("lnw", f(inp["rwkv_ln_w"])[l], 4)
        put("lnb", f(inp["rwkv_ln_b"])[l], 4)
        put("s5d", f(inp["s5_d"])[l], 4)
        put("glub", f(inp["s5_glu_b"])[l], 4)
        cw = f(inp["lru_conv_w"])[l]
        c = vcol(l, "cw")
        for j in range(4):
            vec[:, c + 4 * j:c + 4 * j + 4] = _pp(cw[j], 4)
        put("cb", f(inp["lru_conv_b"])[l], 4)
        put("ba", f(inp["lru_ba"])[l], 4)
        put("bx", f(inp["lru_bx"])[l], 4)
        put("lam", f(inp["lru_lambda"])[l], 4)
        put("png", f(inp["ple_norm_g"])[l], 8)
    vec[:, 2 * VEC_PER_LAYER:2 * VEC_PER_LAYER + 8] = _pp(f(inp["final_norm_g"]), 8)

    w2a2 = np.zeros((2, 128, 512), np.float32)
    w2a2[:, 0:64] = f(inp["rwkv_w2"])
    w2a2[:, 64:128] = f(inp["rwkv_a2"])
    lruw = np.zeros((2, 128, 8, 128), np.float32)
    for l in range(2):
        for m, key in enumerate(("lru_wa", "lru_wx")):
            w = f(inp[key])[l]
            for q in range(4):
                for b2 in range(2):
                    lruw[l, b2 * 64:(b2 + 1) * 64, m * 4 + q, b2 * 64:(b2 + 1) * 64] = w[2 * q + b2]
    lruw = lruw.reshape(2, 128, 1024)
    def modes(a):
        a = f(a).reshape(2, 16, 2, 64)
        return np.ascontiguousarray(a.transpose(0, 2, 3, 1).reshape(2, 128, 16))
    s5s = np.zeros((2, 128, 3, 16), np.float32)
    s5s[:, :, 0] = modes(inp["s5_a_re"])
    s5s[:, :, 1] = modes(inp["s5_a_im"])
    ldt = np.broadcast_to(f(inp["s5_log_dt"])[:, :, None], (2, 32, 64))
    s5s[:, :, 2] = modes(ldt)
    s5s = s5s.reshape(2, 128, 48)
    def bmodes(a):
        a = f(a).reshape(2, 16, 2, 64, 16)
        return a.transpose(0, 2, 3, 1, 4).reshape(2, 128, 16, 16)
    s5b = np.stack([bmodes(inp["s5_b_re"]), bmodes(inp["s5_b_im"])], axis=2).reshape(2, 128, 512)
    s5c = np.zeros((2, 128, 2, 16, 64), np.float32)
    for ri, key in enumerate(("s5_c_re", "s5_c_im")):
        c = f(inp[key]).reshape(2, 16, 2, 16, 64)
        for gh in range(2):
            for jh in range(2):
                c0 = 32 * jh + 16 * gh
                s5c[:, gh * 64:(gh + 1) * 64, ri, jh::2, c0:c0 + 16] = c[:, jh::2, gh].transpose(0, 3, 1, 2)
    s5c = s5c.reshape(2, 128, 2048)
    return {
        "w_in": np.ascontiguousarray(f(inp["w_in"])), "w_out": np.ascontiguousarray(f(inp["w_out"])),
        "ple_w": np.ascontiguousarray(f(inp["ple_w"])), "ple_gw": np.ascontiguousarray(f(inp["ple_gate_w"])),
        "glu_w": np.ascontiguousarray(f(inp["s5_glu_w"])), "vec": vec, "cst": make_consts()[0], "msk": make_consts()[1],
        "w2a2": w2a2, "lruw": np.ascontiguousarray(lruw), "s5s": np.ascontiguousarray(s5s),
        "s5b": np.ascontiguousarray(s5b), "s5c": np.ascontiguousarray(s5c),
    }


_NC_CACHE = {}


def run_cores(inp, TC, batches, dbg=None):
    key = (TC, tuple(sorted(dbg)) if dbg else None)
    if key not in _NC_CACHE:
        _NC_CACHE[key] = build_nc(TC, dbg)
    nc, ninst = _NC_CACHE[key]
    shared = pack_shared(inp)
    x = np.asarray(inp["x"], np.float32)
    p = np.asarray(inp["p"], np.float32)
    in_maps = []
    for b in batches:
        m = dict(shared)
        m["xT"] = np.ascontiguousarray(x[b, :TC].T)
        m["pT"] = np.ascontiguousarray(p[:, b, :TC].transpose(0, 2, 1))
        in_maps.append(m)
    res = run_bass_kernel_spmd(nc, in_maps, core_ids=list(range(len(batches))))
    return res


def kernel(**inputs):
    x = np.asarray(inputs["x"])
    B, S, _ = x.shape
    batches = [c % B for c in range(8)]
    res = run_cores(inputs, S, batches)
    out = np.empty((B, S, D), np.float32)
    for b in range(B):
        out[b] = res.results[b]["oT"].T
    return out.astype(x.dtype)
```

```python
import contextlib
import math
import numpy as np
import concourse.bass as bass
import concourse.mybir as mybir
from concourse.bass_utils import run_bass_kernel_spmd

F32 = mybir.dt.float32
BF16 = mybir.dt.bfloat16
ALU = mybir.AluOpType
AF = mybir.ActivationFunctionType

D = 1024
DIN = 4224
DMIX = 1536
DPLE = 256
TT = 256
LCH = 64
NCH = TT // LCH
TS5 = 128
import os
SKIP = set(os.environ.get("KSKIP", "").split(","))
S5E = os.environ.get("KS5E", "pool")
GW = tuple(int(v) for v in os.environ.get("KGW", "1,1,1").split(","))
GN_EPS = 64e-5
NORM_EPS = 1e-6


class Tok:
    __slots__ = ("sem", "val", "eng", "dma")

    def __init__(self, sem, val, eng, dma):
        self.sem, self.val, self.eng, self.dma = sem, val, eng, dma


class Buf:
    def __init__(self, t, name):
        self.t = t
        self.name = name
        self.w = None
        self.r = []

    def __getitem__(self, idx):
        return self.t[idx]


class Prog:
    ENGS = ("pe", "act", "dve", "pool", "sp")

    def __init__(self, nc, n_dma_sems=32):
        self.nc = nc
        self.es = contextlib.ExitStack()
        self.ops = {e: [] for e in self.ENGS}
        self.cnt = {e: 0 for e in self.ENGS}
        self.sem = {e: self.es.enter_context(nc.semaphore("s_" + e)) for e in self.ENGS}
        self.dsem = [self.es.enter_context(nc.semaphore("d%d" % i)) for i in range(n_dma_sems)]
        self.duse = [0] * n_dma_sems
        self.dnext = 0
        self.seen = {e: {} for e in self.ENGS}
        self.nbuf = 0
        self.ninst = 0
        self.stack = [self.es]

    def push(self):
        st = contextlib.ExitStack()
        self.stack.append(st)

    def pop(self):
        self.barrier()
        self.stack.pop().close()

    def barrier(self):
        toks = []
        for f in self.ENGS:
            if self.cnt[f] > 0:
                toks.append(Tok(self.sem[f], self.cnt[f], f, False))
        for i, s in enumerate(self.dsem):
            if self.duse[i] > 0:
                toks.append(Tok(s, 16 * self.duse[i], "dma", True))
        for e in self.ENGS:
            wl = []
            for t in toks:
                if t.eng == e and not t.dma:
                    continue
                k = id(t.sem)
                if self.seen[e].get(k, 0) >= t.val:
                    continue
                self.seen[e][k] = t.val
                wl.append((t.sem, t.val))

            def run(en, wl=wl):
                for (s, v) in wl:
                    en.wait_ge(s, v)
            self.ops[e].append(run)

    def sb(self, shape, dt=F32, name=None):
        self.nbuf += 1
        name = name or ("b%d" % self.nbuf)
        t = self.stack[-1].enter_context(self.nc.sbuf_tensor("sb_" + name, list(shape), dt))
        return Buf(t, name)

    def ps(self, name, dt=F32, cols=512):
        t = self.es.enter_context(self.nc.psum_tensor(name, [128, cols], dt))
        return Buf(t, name)

    def wrap(self, t, name):
        return Buf(t, name)

    def views(self, buf, n):
        return [Buf(buf.t, "%s.v%d" % (buf.name, i)) for i in range(n)]

    def _need(self, eng, tok, waits, is_dma_issue):
        if tok is None:
            return
        if tok.eng == eng and not tok.dma and not is_dma_issue and eng == "pe":
            return
        k = id(tok.sem)
        if self.seen[eng].get(k, 0) >= tok.val:
            return
        cur = waits.get(k)
        if cur is None or cur[1] < tok.val:
            waits[k] = (tok.sem, tok.val)

    def emit(self, eng, fn, reads=(), writes=(), dma=False):
        waits = {}
        for b in reads:
            self._need(eng, b.w, waits, dma)
        for b in writes:
            self._need(eng, b.w, waits, dma)
            for t in b.r:
                self._need(eng, t, waits, dma)
        if dma:
            i = self.dnext
            self.dnext = (self.dnext + 1) % len(self.dsem)
            s = self.dsem[i]
            if self.duse[i] > 0:
                self._need(eng, Tok(s, 16 * self.duse[i], "dma", True), waits, True)
            self.duse[i] += 1
            tok = Tok(s, 16 * self.duse[i], "dma", True)
            inc = 16
        else:
            self.cnt[eng] += 1
            tok = Tok(self.sem[eng], self.cnt[eng], eng, False)
            inc = 1
        wl = list(waits.values())
        for (s, v) in wl:
            self.seen[eng][id(s)] = v
        tsem = tok.sem
        self.ninst += 1 + len(wl)

        def run(e, wl=wl, fn=fn, tsem=tsem, inc=inc):
            for (s, v) in wl:
                e.wait_ge(s, v)
            fn(e).then_inc(tsem, inc)

        self.ops[eng].append(run)
        for b in reads:
            b.r = [t for t in b.r if t.sem is not tok.sem]
            b.r.append(tok)
        for b in writes:
            b.w = tok
            b.r = []
        return tok

    def finish(self, out_toks):
        wl = [(t.sem, t.val) for t in out_toks]

        def run(e, wl=wl):
            for (s, v) in wl:
                e.wait_ge(s, v)

        self.ops["sp"].append(run)
        nc = self.nc
        ops = self.ops
        with nc.Block() as block:
            @block.tensor
            def _(e):
                for f in ops["pe"]:
                    f(e)

            @block.scalar
            def _(e):
                for f in ops["act"]:
                    f(e)

            @block.vector
            def _(e):
                for f in ops["dve"]:
                    f(e)

            @block.gpsimd
            def _(e):
                for f in ops["pool"]:
                    f(e)

            @block.sync
            def _(e):
                for f in ops["sp"]:
                    f(e)
        self.es.close()

    def dma(self, out, in_, reads=(), writes=(), eng="sp", **kw):
        return self.emit(eng, lambda e: e.dma_start(out=out, in_=in_, **kw), reads, writes, dma=True)

    def mm(self, out, lhsT, rhs, start, stop, reads, writes):
        return self.emit("pe", lambda e: e.matmul(out, lhsT, rhs, start=start, stop=stop), reads, writes)

    def act(self, out, in_, func, reads, writes, bias=None, scale=None):
        kw = {}
        if bias is not None:
            kw["bias"] = bias
        if scale is not None:
            kw["scale"] = scale
        return self.emit("act", lambda e: e.activation(out=out, in_=in_, func=func, **kw), reads, writes)

    def tt(self, eng, out, in0, in1, op, reads, writes):
        return self.emit(eng, lambda e: e.tensor_tensor(out=out, in0=in0, in1=in1, op=op), reads, writes)

    def ts(self, eng, out, in0, s1, s2, op0, op1, reads, writes):
        if op1 is None:
            return self.emit(eng, lambda e: e.tensor_scalar(out, in0, s1, None, op0), reads, writes)
        return self.emit(eng, lambda e: e.tensor_scalar(out, in0, s1, s2, op0, op1), reads, writes)

    def stt(self, out, in0, scalar, in1, op0, op1, reads, writes):
        return self.emit("dve", lambda e: e.scalar_tensor_tensor(out, in0, scalar, in1, op0, op1), reads, writes)

    def copy(self, eng, out, in_, reads, writes):
        if eng == "act":
            return self.emit("act", lambda e: e.activation(out=out, in_=in_, func=AF.Copy), reads, writes)
        return self.emit(eng, lambda e: e.tensor_copy(out, in_), reads, writes)

    def memset(self, eng, ap, val, writes):
        return self.emit(eng, lambda e: e.memset(ap, val), (), writes)

    def scan(self, out, d0, d1, init, reads, writes):
        return self.emit("dve", lambda e: e.tensor_tensor_scan(out, d0, d1, init, ALU.mult, ALU.add), reads, writes)

    def recip(self, out, in_, reads, writes):
        return self.emit("dve", lambda e: e.reciprocal(out, in_), reads, writes)


VEC_FIELDS = [("ng", 8), ("mu", 13), ("w0", 4), ("a0", 4), ("kk", 4), ("ka", 4), ("rk", 4), ("lnw", 4),
              ("lnb", 4), ("s5d", 4), ("glub", 4), ("cw", 16), ("cb", 4), ("ba", 4), ("bx", 4), ("lam", 4),
              ("png", 8)]
VEC_PER_LAYER = sum(n for _, n in VEC_FIELDS)
VEC_OFF = {}
_o = 0
for _n, _c in VEC_FIELDS:
    VEC_OFF[_n] = _o
    _o += _c
NVEC = 2 * VEC_PER_LAYER + 8

CST_IDENT = 0
CST_ONESBD = 128
CST_SCAN = 256
NCST = 256 + TT
MSK_USN = 0
MSK_LSN = NCH * 128
MSK_USP = 2 * NCH * 128
MSK_CI = 3 * NCH * 128
NMSK = 3 * NCH * 128 + NCH * LCH


def vcol(l, name, i=0):
    return l * VEC_PER_LAYER + VEC_OFF[name] + i


def _pp(v, n):
    return np.ascontiguousarray(np.asarray(v, np.float32).reshape(n, 128).T)


def make_consts():
    c = np.zeros((128, NCST), np.float32)
    i = np.arange(128)[:, None]
    j = np.arange(128)[None, :]
    c[:, CST_IDENT:CST_IDENT + 128] = (i == j)
    c[:, CST_ONESBD:CST_ONESBD + 128] = ((i // 64) == (j // 64))
    tt = np.arange(TT)[None, :]
    c[:, CST_SCAN:CST_SCAN + TT] = 1.0 * ((tt % LCH) != 0)
    m = np.zeros((128, NMSK), np.float32)
    t = np.arange(64)[None, :]
    for ch in range(NCH):
        m[:, MSK_USN + ch * 128:MSK_USN + (ch + 1) * 128] = -1.0 * (j > i)
        m[:, MSK_LSN + ch * 128:MSK_LSN + (ch + 1) * 128] = -1.0 * (i > j)
        m[:, MSK_USP + ch * 128:MSK_USP + (ch + 1) * 128] = 1.0 * (j > i)
        m[:, MSK_CI + ch * 64:MSK_CI + (ch + 1) * 64] = 1.0 * (t >= (i % 64))
    return c, m


def build_nc(TC, dbg=None):
    assert TC % TT == 0
    NT = TC // TT
    dbg = dbg or set()
    nc = bass.Bass("TRN2", target_bir_lowering=False)

    def din(name, shape, dt=F32):
        return nc.dram_tensor(name, list(shape), dt, kind="ExternalInput").ap()

    xT = din("xT", [D, TC])
    pT = din("pT", [2, DPLE, TC])
    w_in = din("w_in", [2, D, DIN])
    w_out = din("w_out", [2, DMIX, D])
    ple_w = din("ple_w", [2, DPLE, D])
    ple_gw = din("ple_gw", [2, D, D])
    glu_w = din("glu_w", [2, 512, 512])
    vec_d = din("vec", [128, NVEC])
    cst_d = din("cst", [128, NCST])
    msk_d = din("msk", [128, NMSK])
    w2a2_d = din("w2a2", [2, 128, 512])
    lruw_d = din("lruw", [2, 128, 8 * 128])
    s5s_d = din("s5s", [2, 128, 3 * 16])
    s5b_d = din("s5b", [2, 128, 2 * 16 * 16])
    s5c_d = din("s5c", [2, 128, 2 * 16 * 64])
    oT = nc.dram_tensor("oT", [D, TC], F32, kind="ExternalOutput").ap()
    dbg_out = {}

    def dram_int(name, shape, dt):
        return nc.dram_tensor(name, list(shape), dt, kind="Internal").ap()

    w_in_b = dram_int("w_in_b", [2, D, DIN], BF16)
    w_out_b = dram_int("w_out_b", [2, DMIX, D], BF16)
    ple_w_b = dram_int("ple_w_b", [2, DPLE, D], BF16)
    ple_gw_b = dram_int("ple_gw_b", [2, D, D], BF16)
    glu_w_b = dram_int("glu_w_b", [2, 512, 512], BF16)
    s5tab_d = dram_int("s5tab", [2, 128, 2 * 16 * TS5], F32)

    P = Prog(nc)
    wsb = P.wrap(None, "wscratch")
    tabsb = P.wrap(None, "s5tabscr")

    def dbg_dump(name, buf, ap, shape, dt=F32):
        if name not in dbg:
            return
        o = nc.dram_tensor("dbg_" + name, list(shape), dt, kind="ExternalOutput").ap()
        dbg_out[name] = P.dma(o, ap, reads=[buf])

    vec = P.sb([128, NVEC], F32, "vec")
    cst = P.sb([128, NCST], F32, "cst")
    P.dma(vec[:], vec_d, writes=[vec])
    P.dma(cst[:], cst_d, writes=[cst])
    cstb = P.sb([128, 128], BF16, "cstb")
    P.copy("dve", cstb[:], cst[:, CST_IDENT:CST_IDENT + 128], [cst], [cstb])
    mskb = P.sb([128, NMSK], BF16, "mskb")
    ident_f = cst[:, CST_IDENT:CST_IDENT + 128]
    ident_b = cstb[:, 0:128]
    onesbd_f = cst[:, CST_ONESBD:CST_ONESBD + 128]
    ones_f = P.sb([128, 128], F32, "ones_f")
    P.memset("pool", ones_f[:], 1.0, [ones_f])
    one_t = P.sb([128, 1], F32, "one_t")
    P.memset("pool", one_t[:], 1.0, [one_t])

    def V(l, name, i=0, n=1):
        c = vcol(l, name, i)
        return vec[:, c:c + n]

    for l in range(2):
        for (src, dst, rows) in ((w_in, w_in_b, D), (w_out, w_out_b, DMIX), (ple_w, ple_w_b, DPLE),
                                 (ple_gw, ple_gw_b, D), (glu_w, glu_w_b, 512)):
            for r0 in range(0, rows, 128):
                P.dma(dst[l, r0:r0 + 128, :], src[l, r0:r0 + 128, :], writes=[wsb], eng="pool",
                      max_dma_last_dim=4096)

    w2a2 = []
    lruw = []
    for l in range(2):
        w2a2.append(P.sb([128, 512], BF16, "w2a2_%d" % l))
        lruw.append(P.sb([128, 1024], BF16, "lruw_%d" % l))
    lru_c = P.sb([128, 2, 8], F32, "lru_c")
    s5B = [P.sb([128, 2 * 4 * 2 * 128], BF16, "s5B%d" % l) for l in range(2)]
    s5C = [P.sb([128, 2048], BF16, "s5C%d" % l) for l in range(2)]
    s5keep = [P.sb([128, 3, 16], F32, "s5keep%d" % l) for l in range(2)]
    s5rotb = [P.sb([128, 2, 16], F32, "s5rot%d" % l) for l in range(2)]
    ps_misc = P.ps("ps7")
    P.push()
    stage = P.sb([128, 1024], F32, "stage")
    mstage = P.sb([128, NMSK], F32, "mstage")
    P.dma(mstage[:], msk_d, writes=[mstage])
    P.copy("act", mskb[:], mstage[:], [mstage], [mskb])
    for l in range(2):
        P.dma(stage[:, 0:512], w2a2_d[l], writes=[stage])
        P.copy("act", w2a2[l][:], stage[:, 0:512], [stage], [w2a2[l]])
        P.dma(stage[:], lruw_d[l], writes=[stage])
        P.copy("act", lruw[l][:], stage[:], [stage], [lruw[l]])

    for l in range(2):
        tmp = P.sb([128, 4], F32, "lrutmp%d" % l)
        P.act(tmp[:], V(l, "lam", 0, 4), AF.Exp, [vec], [tmp], scale=-1.0)
        P.act(tmp[:], tmp[:], AF.Ln, [tmp, one_t], [tmp], bias=one_t[:, 0:1])
        P.ts("dve", lru_c[:, l, 0:4], tmp[:], -8.0, None, ALU.mult, None, [tmp], [lru_c])
        P.ts("dve", lru_c[:, l, 4:8], tmp[:], -16.0, None, ALU.mult, None, [tmp], [lru_c])

    s5rot = []
    for l in range(2):
        s5s = P.sb([128, 48], F32, "s5s%d" % l)
        P.dma(s5s[:], s5s_d[l], writes=[s5s])
        a_re = s5s[:, 0:16]
        a_im = s5s[:, 16:32]
        ldt = s5s[:, 32:48]
        w = P.sb([128, 16, 16], F32, "s5w%d" % l)
        R = [w]

        def row(i):
            return w[:, i, :]
        dt_, rho, th, cc, ss, t1, t2, lr, li, den, qre, qim, nr = [row(i) for i in range(13)]
        P.act(dt_, ldt, AF.Exp, [s5s], R)
        P.tt("dve", rho, a_re, dt_, ALU.mult, [s5s] + R, R)
        P.act(rho, rho, AF.Exp, R, R)
        P.tt("dve", th, a_im, dt_, ALU.mult, [s5s] + R, R)
        hp = P.sb([128, 1], F32, "halfpi%d" % l)
        P.memset("dve", hp[:], math.pi / 2, [hp])
        P.act(cc, th, AF.Sin, R + [hp], R, bias=hp[:, 0:1], scale=1.0 / 16)
        P.act(ss, th, AF.Sin, R, R, scale=1.0 / 16)

        def csq(c_, s_):
            P.tt("dve", t1, c_, c_, ALU.mult, R, R)
            P.tt("dve", t2, s_, s_, ALU.mult, R, R)
            P.stt(s_, c_, 2.0, s_, ALU.mult, ALU.mult, R, R)
            P.tt("dve", c_, t1, t2, ALU.subtract, R, R)
        for _ in range(4):
            csq(cc, ss)
        P.tt("dve", lr, rho, cc, ALU.mult, R, R)
        P.tt("dve", li, rho, ss, ALU.mult, R, R)
        P.tt("dve", t1, a_re, a_re, ALU.mult, [s5s] + R, R)
        P.tt("dve", t2, a_im, a_im, ALU.mult, [s5s] + R, R)
        P.tt("dve", den, t1, t2, ALU.add, R, R)
        P.recip(den, den, R, R)
        P.ts("dve", nr, lr, -1.0, None, ALU.add, None, R, R)
        P.tt("dve", t1, nr, a_re, ALU.mult, [s5s] + R, R)
        P.tt("dve", t2, li, a_im, ALU.mult, [s5s] + R, R)
        P.tt("dve", t1, t1, t2, ALU.add, R, R)
        P.tt("dve", qre, t1, den, ALU.mult, R, R)
        P.tt("dve", t1, li, a_re, ALU.mult, [s5s] + R, R)
        P.tt("dve", t2, nr, a_im, ALU.mult, [s5s] + R, R)
        P.tt("dve", t1, t1, t2, ALU.subtract, R, R)
        P.tt("dve", qim, t1, den, ALU.mult, R, R)
        keep = s5keep[l]
        P.copy("dve", keep[:, 0, :], rho, R, [keep])
        P.copy("dve", keep[:, 1, :], cc, R, [keep])
        P.copy("dve", keep[:, 2, :], ss, R, [keep])

        sbf = P.sb([128, 512], F32, "s5b_in%d" % l)
        P.dma(sbf[:], s5b_d[l], writes=[sbf])
        bre = sbf[:, 0:256].rearrange("p (j h) -> p j h", h=16)
        bim = sbf[:, 256:512].rearrange("p (j h) -> p j h", h=16)
        Bt = s5B[l]
        Btv = Bt[:, :].rearrange("p (r b q m) -> p r b q m", r=2, b=4, q=2)
        bpad = P.sb([128, 2, 128], BF16, "s5bpad%d" % l)
        tb = P.sb([128, 2, 16], F32, "s5tb%d" % l)
        for j in range(16):
            P.ts("dve", tb[:, 0, :], bim[:, j, :], qim[:, j:j + 1], None, ALU.mult, None, [sbf] + R, [tb])
            P.stt(tb[:, 0, :], bre[:, j, :], qre[:, j:j + 1], tb[:, 0, :], ALU.mult, ALU.subtract, [sbf, tb] + R, [tb])
            P.ts("dve", tb[:, 1, :], bre[:, j, :], qim[:, j:j + 1], None, ALU.mult, None, [sbf] + R, [tb])
            P.stt(tb[:, 1, :], bim[:, j, :], qre[:, j:j + 1], tb[:, 1, :], ALU.mult, ALU.add, [sbf, tb] + R, [tb])
            P.memset("pool", bpad[:], 0.0, [bpad])
            for gh in range(2):
                col0 = 32 * (j % 4) + gh * 16
                for ri in range(2):
                    P.copy("pool", bpad[gh * 64:(gh + 1) * 64, ri, col0:col0 + 16],
                           tb[gh * 64:(gh + 1) * 64, ri, :], [tb], [bpad])
            for ri in range(2):
                P.mm(ps_misc[:, ri * 128:(ri + 1) * 128], bpad[:, ri, :], ident_b, True, True, [bpad, cstb], [ps_misc])
            hf = (j % 4) // 2
            for ri in range(2):
                P.copy("act", Btv[64 * hf:64 * hf + 64, ri, j // 4, j % 2, :],
                       ps_misc[64 * hf:64 * hf + 64, ri * 128:(ri + 1) * 128], [ps_misc], [Bt])

        scf = P.sb([128, 2048], F32, "s5c_in%d" % l)
        P.dma(scf[:], s5c_d[l], writes=[scf])
        Ct = s5C[l]
        P.copy("act", Ct[:, 0:1024], scf[:, 0:1024], [scf], [Ct])
        P.ts("dve", Ct[:, 1024:2048], scf[:, 1024:2048], -1.0, None, ALU.mult, None, [scf], [Ct])

        tab = P.sb([128, 2, 16, TS5], F32, "s5tabb%d" % l)
        P.memset("pool", tab[:, 0, :, 0:1], 1.0, [tab])
        P.memset("pool", tab[:, 1, :, 0:1], 0.0, [tab])
        ec = P.sb([128, 2, 16], F32, "s5ec%d" % l)
        P.copy("dve", ec[:, 0, :], cc, R, [ec])
        P.copy("dve", ec[:, 1, :], ss, R, [ec])
        m = 1
        while m < TS5:
            for j in range(16):
                cj = ec[:, 0, j:j + 1]
                sj = ec[:, 1, j:j + 1]
                src_c = tab[:, 0, j, 0:m]
                src_s = tab[:, 1, j, 0:m]
                dst_c = tab[:, 0, j, m:2 * m]
                dst_s = tab[:, 1, j, m:2 * m]
                P.ts("dve", dst_c, src_s, sj, None, ALU.mult, None, [tab, ec], [tab])
                P.stt(dst_c, src_c, cj, dst_c, ALU.mult, ALU.subtract, [tab, ec], [tab])
                P.ts("dve", dst_s, src_c, sj, None, ALU.mult, None, [tab, ec], [tab])
                P.stt(dst_s, src_s, cj, dst_s, ALU.mult, ALU.add, [tab, ec], [tab])
            e_c = ec[:, 0, :]
            e_s = ec[:, 1, :]
            P.tt("dve", t1, e_c, e_c, ALU.mult, [ec] + R, R)
            P.tt("dve", t2, e_s, e_s, ALU.mult, [ec] + R, R)
            P.stt(e_s, e_c, 2.0, e_s, ALU.mult, ALU.mult, [ec], [ec])
            P.tt("dve", e_c, t1, t2, ALU.subtract, R, [ec])
            m *= 2
        rot = s5rotb[l]
        P.copy("dve", rot[:], ec[:], [ec], [rot])
        s5rot.append((rot, keep))
        P.dma(s5tab_d[l], tab[:, :, :, :].rearrange("p a j t -> p (a j t)"), reads=[tab], writes=[tabsb])
        dbg_dump("s5w%d" % l, w, w[:, :, :].rearrange("p a b -> p (a b)"), [128, 256])
        dbg_dump("s5keep%d" % l, keep, keep[:, :, :].rearrange("p a b -> p (a b)"), [128, 48])
        dbg_dump("s5tab%d" % l, tab, tab[:, :, :, :].rearrange("p a j t -> p (a j t)"), [128, 2 * 16 * TS5])
        dbg_dump("s5B%d" % l, Bt, Bt[:, :], [128, 2048], BF16)
    P.pop()

    hT = P.sb([128, 8, TT], F32, "hT")
    xn = P.sb([128, 8, TT], BF16, "xn")
    xnv = P.views(xn, 8)
    zst = [P.sb([128, 1 + TT], F32, "zst%d" % i) for i in range(2)]
    zs = P.sb([128, 13, TT], F32, "zs")
    zsv = P.views(zs, 13)
    zu = P.sb([128, 4, TT], F32, "zu")
    zx = P.sb([128, 4, 3 + TT], F32, "zx")
    sgate = P.sb([128, 12, TT], BF16, "sgate")
    sgv = P.views(sgate, 12)
    ycat = P.sb([128, 12, TT], BF16, "ycat")
    ycv = P.views(ycat, 12)
    pbf = P.sb([128, 2, TT], BF16, "pbf")
    ring = [P.sb([128, 4096], BF16, "ring%d" % i) for i in range(3)]
    ringi = [0]
    s5tab = P.sb([128, 2, 16, TS5], F32, "s5tab")
    banks = [P.ps("ps%d" % i) for i in range(7)] + [ps_misc]
    rot_i = {"proj": 0, "rw": 0}

    def bank(group):
        ids = (0, 1) if group == "proj" else (2, 3, 6)
        i = rot_i[group]
        rot_i[group] = (i + 1) % len(ids)
        return banks[ids[i]]

    def next_ring():
        r = ring[ringi[0]]
        ringi[0] = (ringi[0] + 1) % 3
        return r

    cz = [P.sb([128, 13], F32, "cz%d" % l) for l in range(2)]
    cl = [P.sb([128, 4, 3], F32, "cl%d" % l) for l in range(2)]
    ch = [P.sb([128, 4], F32, "ch%d" % l) for l in range(2)]
    s5z = [P.sb([128, 2, 16], F32, "s5z%d" % l) for l in range(2)]
    s5zv = [P.views(s5z[l], 16) for l in range(2)]
    Tst = [[P.sb([128, 128], BF16, "T%d_%d" % (l, pb)) for pb in range(4)] for l in range(2)]
    for l in range(2):
        P.memset("pool", cz[l][:], 0.0, [cz[l]])
        P.memset("pool", cl[l][:], 0.0, [cl[l]])
        P.memset("pool", ch[l][:], 0.0, [ch[l]])
        P.memset("pool", s5z[l][:], 0.0, s5zv[l])
        for pb in range(4):
            P.memset("pool", Tst[l][pb][:], 0.0, [Tst[l][pb]])

    NF = 17
    fs = [P.sb([128, TT], F32, "rf%d" % i) for i in range(NF)]
    pad_names = ["RTp", "KTp", "CTp", "BTp", "VTp", "KGp", "BGp"]
    pads = {n: P.sb([128, NCH * 128], BF16, n) for n in pad_names}
    for n in pad_names:
        P.memset("pool", pads[n][:], 0.0, [pads[n]])
    RTc = P.sb([128, TT], BF16, "RTc")
    tanh_wd = P.sb([128, TT], BF16, "tanhwd")
    Blev_i = [[P.sb([128, NCH * 128], BF16, "Blev%d_%d" % (i, k)) for i in range(2)] for k in range(2)]
    BTlev_i = [[P.sb([128, NCH * 128], BF16, "BTlev%d_%d" % (i, k)) for i in range(2)] for k in range(2)]
    AkkT_i = [P.sb([128, NCH * 128], BF16, "AkkT_%d" % k) for k in range(2)]
    Xbf_i = [P.sb([128, NCH * 256], BF16, "Xbf_%d" % k) for k in range(2)]
    NPI = 2
    pp = []
    for i in range(NPI):
        d = {}
        for n in ("Vbd", "KGbd", "BGbd", "PT", "nU0", "Wbd"):
            d[n] = P.sb([128, NCH * 128], BF16, "%s_%d" % (n, i))
        for n in ("Rhat", "ArkT", "ArbT"):
            d[n] = P.sb([128, TT], BF16, "%s_%d" % (n, i))
        d["bonus"] = P.sb([128, TT], F32, "bonus_%d" % i)
        d["GL"] = P.sb([128, NCH], F32, "GL_%d" % i)
        d["rt32"] = P.sb([128, TT], F32, "rt32_%d" % i)
        pp.append(d)
    mix = P.sb([128, 4, TT], F32, "mix")
    mixv = P.views(mix, 4)

    NS5SET = 2
    s5f = [[P.sb([128, TS5], F32, "s5f%d_%d" % (k, i)) for i in range(8)] for k in range(NS5SET)]
    NS5X = 3
    s5x = [P.sb([128, 2, TS5], BF16, "s5x%d" % k) for k in range(NS5X)]
    s5zl = [P.sb([128, 2], F32, "s5zl%d" % k) for k in range(NS5SET)]
    spf = [P.sb([128, TT], F32, "spf%d" % i) for i in range(3)]
    lf = [P.sb([128, TT], F32, "lf%d" % i) for i in range(4)]
    ubf = P.sb([128, 4, TT], BF16, "ubf")
    lb = P.sb([128, TT], BF16, "lb")
    rstd = P.sb([128, TT], F32, "rstd")
    sq = P.sb([128, 4, TT], BF16, "sq")
    ones_b = P.sb([128, 128], BF16, "ones_b")
    P.memset("pool", ones_b[:], 1.0, [ones_b])

    out_toks = []
    eps_t = {}
    for e_ in (NORM_EPS, GN_EPS):
        t = P.sb([128, 1], F32, "eps%d" % len(eps_t))
        P.memset("pool", t[:], e_, [t])
        eps_t[e_] = t
    neg_half = -math.exp(-0.5)

    def rms_rstd(src_bufs, src_ap_fn, nblk, eps):
        b = bank("proj")
        for k in range(nblk):
            if k % 2 == 0:
                P.act(sq[:, k % 4, :], src_ap_fn(k), AF.Square, src_bufs, [sq])
            else:
                P.tt("dve", sq[:, k % 4, :], src_ap_fn(k), src_ap_fn(k), ALU.mult, src_bufs, [sq])
            P.mm(b[:, 0:TT], ones_b[:], sq[:, k % 4, :], k == 0, k == nblk - 1, [ones_b, sq], [b])
        P.act(rstd[:], b[:, 0:TT], AF.Ln, [b, eps_t[eps]], [rstd], bias=eps_t[eps][:, 0:1], scale=1.0 / (nblk * 128))
        P.act(rstd[:], rstd[:], AF.Exp, [rstd], [rstd], scale=-0.5)

    def drive(items):
        active = list(items)
        while active:
            for item in list(active):
                g, w = item
                for _ in range(w):
                    try:
                        next(g)
                    except StopIteration:
                        active.remove(item)
                        break

    s5ctr = [0]

    for it in range(NT):
        t0 = it * TT
        first = (it == 0)
        P.dma(hT[:], xT[:, t0:t0 + TT].rearrange("(k p) t -> p k t", p=128), writes=[hT])
        for l in range(2):
            dbg_on = first and l == 0
            P.dma(pbf[:], pT[l, :, t0:t0 + TT].rearrange("(k p) t -> p k t", p=128), writes=[pbf], eng="pool")
            P.dma(s5tab[:, :, :, :].rearrange("p a j t -> p (a j t)"), s5tab_d[l], reads=[tabsb], writes=[s5tab])
            rms_rstd([hT], lambda k: hT[:, k, :], 8, NORM_EPS)
            for k in range(8):
                P.stt(xn[:, k, :], hT[:, k, :], V(l, "ng", k), rstd[:], ALU.mult, ALU.mult, [hT, vec, rstd], [xnv[k]])
            if dbg_on:
                dbg_dump("xn", xnv[7], xn[:, :, :].rearrange("p a t -> p (a t)"), [128, 8 * TT], BF16)

            wchunk = {}

            def in_block(cb, l=l, wchunk=wchunk):
                ci = cb // 4
                if ci not in wchunk:
                    r = next_ring()
                    ncol = 512 if ci < 8 else 128
                    P.dma(r[:, 0:8 * ncol].rearrange("p (k n) -> p k n", k=8),
                          w_in_b[l].rearrange("(k p) n -> p k n", p=128)[:, :, ci * 512:ci * 512 + ncol],
                          reads=[wsb], writes=[r])
                    wchunk[ci] = (r, ncol)
                r, ncol = wchunk[ci]
                rv = r[:, 0:8 * ncol].rearrange("p (k n) -> p k n", k=8)
                c0 = (cb % 4) * 128
                b = bank("proj")
                for k in range(8):
                    P.mm(b[:, 0:TT], rv[:, k, c0:c0 + 128], xn[:, k, :], k == 0, k == 7, [r, xnv[k]], [b])
                return b

            for cb in range(13):
                st = zst[cb % 2]
                b = in_block(cb)
                P.copy("act", st[:, 0:1], cz[l][:, cb:cb + 1], [cz[l]], [st])
                P.copy("act", st[:, 1:1 + TT], b[:, 0:TT], [b], [st])
                P.copy("act", cz[l][:, cb:cb + 1], st[:, TT:TT + 1], [st], [cz[l]])
                d_ = lf[cb % 2]
                P.tt("pool", d_[:], st[:, 0:TT], st[:, 1:1 + TT], ALU.subtract, [st], [d_])
                P.stt(zs[:, cb, :], d_[:], V(l, "mu", cb), st[:, 1:1 + TT], ALU.mult, ALU.add, [d_, vec, st], [zsv[cb]])
            if dbg_on:
                dbg_dump("zs", zs, zs[:, :, :].rearrange("p a t -> p (a t)"), [128, 13 * TT])
            P.act(tanh_wd[0:64, :], zs[0:64, 12, :], AF.Tanh, [zsv[12]], [tanh_wd])
            P.copy("act", tanh_wd[64:128, :], zs[64:128, 12, :], [zsv[12]], [tanh_wd])
            for blk in range(4):
                b = in_block(13 + blk)
                P.act(sgate[:, blk, :], b[:, 0:TT], AF.Silu, [b], [sgv[blk]])
            for blk in range(4):
                b = in_block(17 + blk)
                P.copy("act", zu[:, blk, :], b[:, 0:TT], [b], [zu])
            P.copy("pool", ubf[:], zu[:], [zu], [ubf])
            for blk in range(4):
                b = in_block(21 + blk)
                P.act(sgate[:, 4 + blk, :], b[:, 0:TT], AF.Silu, [b], [sgv[4 + blk]])
            P.copy("act", zx[:, :, 0:3], cl[l][:, :, :], [cl[l]], [zx])
            for blk in range(4):
                b = in_block(25 + blk)
                P.copy("act", zx[:, blk, 3:3 + TT], b[:, 0:TT], [b], [zx])
            P.copy("act", cl[l][:, :, :], zx[:, :, TT:TT + 3], [zx], [cl[l]])
            for blk in range(4):
                b = in_block(29 + blk)
                P.act(sgate[:, 8 + blk, :], b[:, 0:TT], AF.Silu, [b], [sgv[8 + blk]])

            def prep(pb, inst, l=l, dbg_on=dbg_on):
                d = pp[inst]
                Blev, BTlev, AkkT = Blev_i[inst], BTlev_i[inst], AkkT_i[inst]
                r_ = zs[:, pb, :]
                k_ = zs[:, 4 + pb, :]
                v_ = zs[:, 8 + pb, :]
                zr_, zk_, zv_ = zsv[pb], zsv[4 + pb], zsv[8 + pb]
                (sg, ld, a_, kk_, kk2, sqk, kap, t1, kp, b_, lg, eg, ieg, eg1, dl, egl, rk) = fs[:17]
                rt32 = d["rt32"]
                cols = slice(pb * 128, (pb + 1) * 128)
                bw = bank("rw")
                P.mm(bw[:, 0:TT], w2a2[l][0:64, cols], tanh_wd[0:64, :], True, True, [w2a2[l], tanh_wd], [bw])
                P.act(sg[:], bw[:, 0:TT], AF.Sigmoid, [bw, vec], [sg], bias=V(l, "w0", pb))
                P.ts("dve", ld[:], sg[:], neg_half, None, ALU.mult, None, [sg], [ld])
                ba_ = bank("rw")
                P.mm(ba_[:, 0:TT], w2a2[l][64:128, cols], tanh_wd[64:128, :], True, True, [w2a2[l], tanh_wd], [ba_])
                P.act(a_[:], ba_[:, 0:TT], AF.Sigmoid, [ba_, vec], [a_], bias=V(l, "a0", pb))
                yield
                P.scan(lg[:], cst[:, CST_SCAN:CST_SCAN + TT], ld[:], 0.0, [cst, ld], [lg])
                P.ts("dve", kk_[:], k_, V(l, "kk", pb), None, ALU.mult, None, [zk_, vec], [kk_])
                P.tt("pool", kk2[:], kk_[:], kk_[:], ALU.mult, [kk_], [kk2])
                P.act(eg[:], lg[:], AF.Exp, [lg], [eg])
                yield
                bs = bank("rw")
                P.mm(bs[:, 0:TT], onesbd_f, kk2[:], True, True, [cst, kk2], [bs])
                P.act(sqk[:], bs[:, 0:TT], AF.Sqrt, [bs], [sqk])
                P.act(ieg[:], lg[:], AF.Exp, [lg], [ieg], scale=-1.0)
                P.tt("pool", eg1[:], lg[:], ld[:], ALU.subtract, [lg, ld], [eg1])
                P.act(eg1[:], eg1[:], AF.Exp, [eg1], [eg1])
                P.ts("dve", sqk[:], sqk[:], 1e-12, None, ALU.max, None, [sqk], [sqk])
                P.recip(sqk[:], sqk[:], [sqk], [sqk])
                P.tt("pool", kap[:], kk_[:], sqk[:], ALU.mult, [kk_, sqk], [kap])
                yield
                P.ts("dve", t1[:], a_[:], -1.0, V(l, "ka", pb), ALU.add, ALU.mult, [a_, vec], [t1])
                P.stt(kp[:], t1[:], 1.0, k_, ALU.add, ALU.mult, [t1, zk_], [kp])
                P.tt("pool", b_[:], kap[:], a_[:], ALU.mult, [kap, a_], [b_])
                lg3 = lg[:, :].rearrange("p (c t) -> p c t", t=LCH)
                P.tt("dve", dl[:, :].rearrange("p (c t) -> p c t", t=LCH), lg3[:, :, LCH - 1:LCH].to_broadcast([128, NCH, LCH]),
                     lg3, ALU.subtract, [lg], [dl])
                P.act(egl[:], dl[:], AF.Exp, [dl], [egl])
                P.copy("act", d["GL"][:, :], eg[:, :].rearrange("p (c t) -> p c t", t=LCH)[:, :, LCH - 1], [eg], [d["GL"]])
                yield
                P.tt("dve", rt32[:], r_, eg[:], ALU.mult, [zr_, eg], [rt32])
                P.copy("act", RTc[:], rt32[:], [rt32], [RTc])

                def padw(name, eng, in0, in1, rd):
                    t = pads[name]
                    tv = t[:, :].rearrange("p (c h t) -> p c h t", c=NCH, h=2)
                    for hh in range(2):
                        ps_ = slice(hh * 64, (hh + 1) * 64)
                        o = tv[ps_, :, hh, :]
                        i0 = in0[ps_, :].rearrange("p (c t) -> p c t", t=LCH)
                        if in1 is None:
                            P.copy(eng, o, i0, rd, [t])
                        else:
                            i1 = in1[ps_, :].rearrange("p (c t) -> p c t", t=LCH)
                            P.tt(eng, o, i0, i1, ALU.mult, rd, [t])
                padw("RTp", "act", rt32, None, [rt32])
                padw("KTp", "dve", kp, ieg, [kp, ieg])
                padw("CTp", "pool", kap, eg1, [kap, eg1])
                yield
                padw("BTp", "dve", b_, ieg, [b_, ieg])
                padw("VTp", "act", zs[:, 8 + pb, :], None, [zv_])
                padw("KGp", "pool", kp, egl, [kp, egl])
                padw("BGp", "dve", b_, egl, [b_, egl])
                yield "pre_pp"
                P.stt(rk[:], r_, V(l, "rk", pb), kp[:], ALU.mult, ALU.mult, [zr_, vec, kp], [rk])
                yield
                bb = bank("rw")
                P.mm(bb[:, 0:TT], onesbd_f, rk[:], True, True, [cst, rk], [bb])
                P.tt("dve", d["bonus"][:], bb[:, 0:TT], v_, ALU.mult, [bb, zv_], [d["bonus"]])
                if dbg_on and pb == 0:
                    dbg_dump("lg", lg, lg[:], [128, TT])
                    dbg_dump("kap", kap, kap[:], [128, TT])
                    dbg_dump("kp", kp, kp[:], [128, TT])
                    dbg_dump("a", a_, a_[:], [128, TT])
                yield

                def chunkmm(dst_bank, lname, rname, rbuf=None, rcols=128):
                    lt = pads[lname]
                    for c in range(NCH):
                        if rbuf is None:
                            rb_ = pads[rname]
                            rap = rb_[:, c * 128:(c + 1) * 128]
                        else:
                            rb_ = rbuf
                            rap = rbuf[:, c * rcols:(c + 1) * rcols]
                        P.mm(dst_bank[:, c * rcols:(c + 1) * rcols], lt[:, c * 128:(c + 1) * 128], rap, True, True,
                             [lt, rb_], [dst_bank])

                def masked(dst, src_bank, mcol, w):
                    n = NCH * w
                    P.tt("dve", dst[:, 0:n], src_bank[:, 0:n], mskb[:, mcol:mcol + n], ALU.mult, [src_bank, mskb], [dst])
                b1 = bank("rw")
                chunkmm(b1, "BTp", "CTp")
                masked(BTlev[0], b1, MSK_USN, 128)
                b2 = bank("rw")
                chunkmm(b2, "CTp", "BTp")
                masked(Blev[0], b2, MSK_LSN, 128)
                yield
                b3 = bank("rw")
                chunkmm(b3, "KTp", "CTp")
                masked(AkkT, b3, MSK_USP, 128)
                b4 = bank("rw")
                chunkmm(b4, "KTp", None, RTc, LCH)
                masked(d["ArkT"], b4, MSK_CI, LCH)
                b5 = bank("rw")
                chunkmm(b5, "BTp", None, RTc, LCH)
                masked(d["ArbT"], b5, MSK_CI, LCH)
                yield

            def tokmajor(src_name, dst_buf, dst_ap, eng):
                bt_ = bank("rw")
                lt = pads[src_name]
                for c in range(NCH):
                    P.mm(bt_[:, c * 128:(c + 1) * 128], lt[:, c * 128:(c + 1) * 128], ident_b, True, True,
                         [lt, cstb], [bt_])
                P.copy(eng, dst_ap, bt_[:, 0:NCH * 128] if len(dst_ap.shape) == 2 else
                       bt_[:, 0:NCH * 128].rearrange("p (c n) -> p c n", n=128), [bt_], [dst_buf])

            def solve(pb, inst, l=l):
                d = pp[inst]
                Blev, BTlev, AkkT, Xbf = Blev_i[inst], BTlev_i[inst], AkkT_i[inst], Xbf_i[inst]
                Xbfv = Xbf[:, :].rearrange("p (c n) -> p c n", n=256)
                tokmajor("VTp", d["Vbd"], d["Vbd"][:, :], "act")
                tokmajor("KGp", d["KGbd"], d["KGbd"][:, :], "act")
                yield
                tokmajor("BGp", d["BGbd"], d["BGbd"][:, :], "act")
                tokmajor("CTp", Xbf, Xbfv[:, :, 0:128], "act")
                bt_ = bank("rw")
                for c in range(NCH):
                    cs = slice(c * 128, (c + 1) * 128)
                    P.mm(bt_[:, cs], AkkT[:, cs], d["Vbd"][:, cs], True, True, [AkkT, d["Vbd"]], [bt_])
                P.copy("act", Xbfv[:, :, 128:256], bt_[:, 0:NCH * 128].rearrange("p (c n) -> p c n", n=128), [bt_], [Xbf])
                yield "tok_done"
                cur = 0
                NLEV = 6
                for lev in range(NLEV):
                    if lev < NLEV - 1:
                        nxt = 1 - cur
                        bq = bank("rw")
                        for c in range(NCH):
                            cs = slice(c * 128, (c + 1) * 128)
                            P.mm(bq[:, cs], Blev[cur][:, cs], BTlev[cur][:, cs], True, True, [Blev[cur], BTlev[cur]], [bq])
                        if lev < NLEV - 2:
                            bq2 = bank("rw")
                            for c in range(NCH):
                                cs = slice(c * 128, (c + 1) * 128)
                                P.mm(bq2[:, cs], BTlev[cur][:, cs], Blev[cur][:, cs], True, True,
                                     [Blev[cur], BTlev[cur]], [bq2])
                    for half in range(2):
                        bx_ = banks[4 + half]
                        for cc_ in range(2):
                            c = half * 2 + cc_
                            P.mm(bx_[:, cc_ * 256:(cc_ + 1) * 256], BTlev[cur][:, c * 128:(c + 1) * 128], Xbfv[:, c, :],
                                 True, False, [BTlev[cur], Xbf], [bx_])
                            P.mm(bx_[:, cc_ * 256:(cc_ + 1) * 256], ident_b, Xbfv[:, c, :],
                                 False, True, [cstb, Xbf], [bx_])
                    if lev < NLEV - 1:
                        P.copy("act", BTlev[nxt][:], bq[:, 0:512], [bq], [BTlev[nxt]])
                        if lev < NLEV - 2:
                            P.copy("dve", Blev[nxt][:], bq2[:, 0:512], [bq2], [Blev[nxt]])
                    P.copy("act", Xbf[:, 0:512], banks[4][:, 0:512], [banks[4]], [Xbf])
                    P.copy("dve", Xbf[:, 512:1024], banks[5][:, 0:512], [banks[5]], [Xbf])
                    if lev < NLEV - 1:
                        cur = nxt
                    yield
                nU0v = d["nU0"][:, :].rearrange("p (c n) -> p c n", n=128)
                Wbdv = d["Wbd"][:, :].rearrange("p (c n) -> p c n", n=128)
                P.ts("dve", nU0v, Xbfv[:, :, 128:256], -1.0, None, ALU.mult, None, [Xbf], [d["nU0"]])
                P.copy("pool", Wbdv, Xbfv[:, :, 0:128], [Xbf], [d["Wbd"]])
                yield
                br = bank("rw")
                for c in range(NCH):
                    P.mm(br[:, c * LCH:(c + 1) * LCH], d["Wbd"][:, c * 128:(c + 1) * 128],
                         d["ArbT"][:, c * LCH:(c + 1) * LCH], True, True, [d["Wbd"], d["ArbT"]], [br])
                P.tt("dve", d["Rhat"][:], d["rt32"][:], br[:, 0:TT], ALU.subtract, [d["rt32"], br], [d["Rhat"]])
                bp = bank("rw")
                for c in range(NCH):
                    cs = slice(c * 128, (c + 1) * 128)
                    P.mm(bp[:, cs], d["Wbd"][:, cs], d["BGbd"][:, cs], True, True, [d["Wbd"], d["BGbd"]], [bp])
                for c in range(NCH):
                    cs = slice(c * 128, (c + 1) * 128)
                    P.stt(d["PT"][:, cs], ident_f, d["GL"][:, c:c + 1], bp[:, cs], ALU.mult, ALU.subtract,
                          [cst, d["GL"], bp], [d["PT"]])
                yield

            def seq(pb, inst, c, l=l):
                d = pp[inst]
                T = Tst[l][pb]
                tb_ = banks[4 + inst]
                yb = tb_
                ycols = slice(128 + c * LCH, 128 + (c + 1) * LCH)
                cs = slice(c * 128, (c + 1) * 128)
                cl_ = slice(c * LCH, (c + 1) * LCH)
                P.mm(yb[:, ycols], T[:], d["Rhat"][:, cl_], True, False, [T, d["Rhat"]], [yb])
                P.mm(yb[:, ycols], d["Vbd"][:, cs], d["ArkT"][:, cl_], False, False, [d["Vbd"], d["ArkT"]], [yb])
                P.mm(yb[:, ycols], d["nU0"][:, cs], d["ArbT"][:, cl_], False, True, [d["nU0"], d["ArbT"]], [yb])
                P.mm(tb_[:, 0:128], d["PT"][:, cs], T[:], True, False, [d["PT"], T], [tb_])
                P.mm(tb_[:, 0:128], d["KGbd"][:, cs], d["Vbd"][:, cs], False, False, [d["KGbd"], d["Vbd"]], [tb_])
                P.mm(tb_[:, 0:128], d["BGbd"][:, cs], d["nU0"][:, cs], False, True, [d["BGbd"], d["nU0"]], [tb_])
                P.copy("act", T[:], tb_[:, 0:128], [tb_], [T])

            def fin(pb, inst, l=l, dbg_on=dbg_on):
                assert lru_done[0], "fin emitted before the LRU chain finished (scratch lf[3] still live)"
                d = pp[inst]
                yb = banks[4 + inst]
                y32, yc, ysq, rs = spf[0], spf[1], spf[2], lf[3]
                P.copy("act", y32[:], yb[:, 128:128 + TT], [yb], [y32])
                if dbg_on:
                    dbg_dump("y_rw%d" % pb, y32, y32[:], [128, TT])
                bm = bank("rw")
                P.mm(bm[:, 0:TT], onesbd_f, y32[:], True, True, [cst, y32], [bm])
                P.stt(yc[:], bm[:, 0:TT], -1.0 / 64, y32[:], ALU.mult, ALU.add, [bm, y32], [yc])
                P.act(ysq[:], yc[:], AF.Square, [yc], [ysq])
                bv = bank("rw")
                P.mm(bv[:, 0:TT], onesbd_f, ysq[:], True, True, [cst, ysq], [bv])
                P.act(rs[:], bv[:, 0:TT], AF.Ln, [bv, eps_t[GN_EPS]], [rs], bias=eps_t[GN_EPS][:, 0:1], scale=1.0 / 64)
                P.act(rs[:], rs[:], AF.Exp, [rs], [rs], scale=-0.5)
                P.tt("dve", yc[:], yc[:], rs[:], ALU.mult, [yc, rs], [yc])
                P.ts("dve", yc[:], yc[:], V(l, "lnw", pb), V(l, "lnb", pb), ALU.mult, ALU.add, [yc, vec], [yc])
                P.tt("pool", yc[:], yc[:], d["bonus"][:], ALU.add, [yc, d["bonus"]], [yc])
                P.tt("dve", ycat[:, pb, :], yc[:], sgate[:, pb, :], ALU.mult, [yc, sgv[pb]], [ycv[pb]])

            def rwkv_gen():
                def chain(pb, inst):
                    yield from prep(pb, inst)
                    if pb == 3:
                        prep3_done[0] = True
                    yield from solve(pb, inst)

                def tail(pbs):
                    for c in range(NCH):
                        for inst, pb in enumerate(pbs):
                            seq(pb, inst, c)
                            yield
                    for inst, pb in enumerate(pbs):
                        fin(pb, inst)
                        yield

                def merge(ga, gb):
                    da = db = False
                    while not (da and db):
                        if not da:
                            try:
                                next(ga)
                                yield
                            except StopIteration:
                                da = True
                        if not db:
                            try:
                                next(gb)
                                yield
                            except StopIteration:
                                db = True

                def half_front(pbs):
                    gA, gB = chain(pbs[0], 0), chain(pbs[1], 1)
                    for v in gA:
                        yield
                        if v == "tok_done":
                            break
                    yield from merge(gA, gB)

                yield from half_front((0, 1))
                t0_ = tail((0, 1))
                gA, gB = chain(2, 0), chain(3, 1)

                def front2a():
                    for v in gA:
                        yield
                        if v == "pre_pp":
                            break
                yield from merge(t0_, front2a())
                half0_done[0] = True
                for v in gA:
                    yield
                    if v == "tok_done":
                        break
                yield from merge(gA, gB)
                yield from tail((2, 3))

            def s5_gen(l=l, dbg_on=dbg_on):
                Bv = s5B[l][:, :].rearrange("p (r b q m) -> p r b q m", r=2, b=4, q=2)
                Cv = s5C[l][:, :].rearrange("p (r j m) -> p r j m", r=2, j=16)
                rotk, keep = s5rot[l]
                NS = TT // TS5
                yb5 = banks[7]

                def stage0(blk, s, jj, k):
                    hf, jh = jj // 2, jj % 2
                    hs = slice(64 * hf, 64 * hf + 64)
                    tsl = slice(s * TS5, (s + 1) * TS5)
                    bu = banks[k % 2]
                    c0 = 0
                    bre = bu[:, c0:c0 + TS5]
                    bim = bu[:, c0 + TS5:c0 + 2 * TS5]
                    P.mm(bre, Bv[hs, 0, blk, jh, :], ubf[hs, blk, tsl], True, True, [s5B[l], ubf], [bu])
                    P.mm(bim, Bv[hs, 1, blk, jh, :], ubf[hs, blk, tsl], True, True, [s5B[l], ubf], [bu])

                def stage1(blk, s, jj, k):
                    j = blk * 4 + jj
                    (t1, t2, bzr, bzi, zr, zi, t3, t4) = s5f[k]
                    bu = banks[k % 2]
                    c0 = 0
                    bre = bu[:, c0:c0 + TS5]
                    bim = bu[:, c0 + TS5:c0 + 2 * TS5]
                    cosT = s5tab[:, 0, j, :]
                    sinT = s5tab[:, 1, j, :]
                    P.tt("dve", t1[:], bre, cosT, ALU.mult, [bu, s5tab], [t1])
                    P.tt("dve", t2[:], bim, sinT, ALU.mult, [bu, s5tab], [t2])
                    P.tt("dve", t3[:], bim, cosT, ALU.mult, [bu, s5tab], [t3])
                    P.tt("dve", t4[:], bre, sinT, ALU.mult, [bu, s5tab], [t4])
                    P.tt(S5E, bzr[:], t1[:], t2[:], ALU.add, [t1, t2], [bzr])
                    P.tt(S5E, bzi[:], t3[:], t4[:], ALU.subtract, [t3, t4], [bzi])

                def stage2(blk, s, jj, k, kx):
                    j = blk * 4 + jj
                    hf, jh = jj // 2, jj % 2
                    hs = slice(64 * hf, 64 * hf + 64)
                    tsl = slice(s * TS5, (s + 1) * TS5)
                    (t1, t2, bzr, bzi, zr, zi, t3, t4) = s5f[k]
                    sx = s5x[kx]
                    zl = s5zl[k]
                    zv = s5zv[l][j]
                    cosT = s5tab[:, 0, j, :]
                    sinT = s5tab[:, 1, j, :]
                    rho_b = keep[:, 0, j:j + 1].to_broadcast([128, TS5])
                    P.scan(zr[:], rho_b, bzr[:], s5z[l][:, 0, j:j + 1], [keep, bzr, zv], [zr])
                    P.scan(zi[:], rho_b, bzi[:], s5z[l][:, 1, j:j + 1], [keep, bzi, zv], [zi])
                    P.tt("dve", t1[:], zr[:], cosT, ALU.mult, [zr, s5tab], [t1])
                    P.tt("dve", t2[:], zi[:], sinT, ALU.mult, [zi, s5tab], [t2])
                    P.tt(S5E, sx[:, 0, :], t1[:], t2[:], ALU.subtract, [t1, t2], [sx])
                    P.tt("dve", t3[:], zr[:], sinT, ALU.mult, [zr, s5tab], [t3])
                    P.tt(S5E, t4[:], zi[:], cosT, ALU.mult, [zi, s5tab], [t4])
                    P.tt(S5E, sx[:, 1, :], t3[:], t4[:], ALU.add, [t3, t4], [sx])
                    rc = rotk[:, 0, j:j + 1]
                    rs_ = rotk[:, 1, j:j + 1]
                    zlr = zr[:, TS5 - 1:TS5]
                    zli = zi[:, TS5 - 1:TS5]
                    P.ts("dve", zl[:, 0:1], zli, rs_, None, ALU.mult, None, [zi, rotk], [zl])
                    P.ts("dve", zl[:, 1:2], zlr, rs_, None, ALU.mult, None, [zr, rotk], [zl])
                    P.stt(s5z[l][:, 0, j:j + 1], zlr, rc, zl[:, 0:1], ALU.mult, ALU.subtract, [zr, rotk, zl], [zv])
                    P.stt(s5z[l][:, 1, j:j + 1], zli, rc, zl[:, 1:2], ALU.mult, ALU.add, [zi, rotk, zl], [zv])

                def stage3(blk, s, jj, kx):
                    j = blk * 4 + jj
                    hf, jh = jj // 2, jj % 2
                    hs = slice(64 * hf, 64 * hf + 64)
                    tsl = slice(s * TS5, (s + 1) * TS5)
                    sx = s5x[kx]
                    P.mm(yb5[hs, tsl], Cv[:, 0, j, :], sx[:, 0, :], jh == 0, False, [s5C[l], sx], [yb5])
                    P.mm(yb5[hs, tsl], Cv[:, 1, j, :], sx[:, 1, :], False, jh == 1, [s5C[l], sx], [yb5])

                for blk in range(4):
                    units = [(s, jj) for s in range(NS) for jj in range(4)]
                    ks = []
                    for (s, jj) in units:
                        ks.append(s5ctr[0] % NS5SET)
                        s5ctr[0] += 1
                    nU = len(units)
                    stage0(blk, units[0][0], units[0][1], ks[0])
                    stage0(blk, units[1][0], units[1][1], ks[1])
                    stage1(blk, units[0][0], units[0][1], ks[0])
                    yield
                    for u in range(nU):
                        if u + 1 < nU:
                            stage1(blk, units[u + 1][0], units[u + 1][1], ks[u + 1])
                            if u + 2 < nU:
                                stage0(blk, units[u + 2][0], units[u + 2][1], ks[u + 2])
                            yield
                        stage2(blk, units[u][0], units[u][1], ks[u], u % NS5X)
                        if u >= 2:
                            stage3(blk, units[u - 2][0], units[u - 2][1], (u - 2) % NS5X)
                        yield
                    stage3(blk, units[nU - 2][0], units[nU - 2][1], (nU - 2) % NS5X)
                    stage3(blk, units[nU - 1][0], units[nU - 1][1], (nU - 1) % NS5X)
                    ys, x2, q_ = spf
                    P.stt(ys[:], zu[:, blk, :], V(l, "s5d", blk), yb5[:, 0:TT], ALU.mult, ALU.add, [zu, vec, yb5], [ys])
                    if dbg_on:
                        dbg_dump("s5y%d" % blk, ys, ys[:], [128, TT])
                    P.act(x2[:], ys[:], AF.Square, [ys], [x2])
                    P.ts("dve", x2[:], x2[:], 0.044715, 1.0, ALU.mult, ALU.add, [x2], [x2])
                    P.tt("pool", q_[:], x2[:], ys[:], ALU.mult, [x2, ys], [q_])
                    P.act(x2[:], q_[:], AF.Sigmoid, [q_], [x2], scale=2.0 * math.sqrt(2.0 / math.pi))
                    P.tt("dve", mix[:, blk, :], ys[:], x2[:], ALU.mult, [ys, x2], [mixv[blk]])
                    yield
                zgb = ubf
                P.copy("pool", zgb[:], mix[:], mixv, [zgb])
                rg = next_ring()
                P.dma(rg[:, 0:2048].rearrange("p (k n) -> p k n", k=4), glu_w_b[l].rearrange("(k p) n -> p k n", p=128),
                      reads=[wsb], writes=[rg])
                rgv = rg[:, 0:2048].rearrange("p (k n) -> p k n", k=4)
                for ob in range(4):
                    b = banks[0]
                    for k in range(4):
                        P.mm(b[:, 0:TT], rgv[:, k, ob * 128:(ob + 1) * 128], zgb[:, k, :], k == 0, k == 3, [rg, zgb], [b])
                    sg_ = spf[ob % 2]
                    P.act(sg_[:], b[:, 0:TT], AF.Sigmoid, [b, vec], [sg_], bias=V(l, "glub", ob))
                    P.tt("pool", sg_[:], sg_[:], sgate[:, 4 + ob, :], ALU.mult, [sg_, sgv[4 + ob]], [sg_])
                    P.tt("dve", ycat[:, 4 + ob, :], mix[:, ob, :], sg_[:], ALU.mult, [mixv[ob], sg_], [ycv[4 + ob]])
                    if ob == 3:
                        s5_done[0] = True
                    yield

            lru_done = [("lru" in SKIP)]
            s5_done = [False]
            half0_done = [False]
            prep3_done = [False]

            def lru_gen(l=l, dbg_on=dbg_on):
                for blk in range(4):
                    A, B, C, Dd = lf
                    bl = banks[5]
                    P.ts("dve", A[:], zx[:, blk, 0:TT], V(l, "cw", 0 * 4 + blk), V(l, "cb", blk), ALU.mult, ALU.add,
                         [zx, vec], [A])
                    for j in range(1, 4):
                        P.stt(A[:], zx[:, blk, j:j + TT], V(l, "cw", j * 4 + blk), A[:], ALU.mult, ALU.add,
                              [zx, vec, A], [A])
                    P.copy("act", lb[:], A[:], [A], [lb])
                    yield
                    P.mm(bl[:, 0:TT], lruw[l][:, blk * 128:(blk + 1) * 128], lb[:], True, True, [lruw[l], lb], [bl])
                    P.mm(bl[:, TT:2 * TT], lruw[l][:, (4 + blk) * 128:(5 + blk) * 128], lb[:], True, True,
                         [lruw[l], lb], [bl])
                    P.act(B[:], bl[:, 0:TT], AF.Sigmoid, [bl, vec], [B], bias=V(l, "ba", blk))
                    P.act(C[:], bl[:, TT:2 * TT], AF.Sigmoid, [bl, vec], [C], bias=V(l, "bx", blk))
                    yield
                    P.act(Dd[:], B[:], AF.Exp, [B, lru_c], [Dd], scale=lru_c[:, l, blk:blk + 1])
                    P.act(B[:], B[:], AF.Exp, [B, lru_c], [B], scale=lru_c[:, l, 4 + blk:5 + blk])
                    P.act(B[:], B[:], AF.Ln, [B, one_t], [B], bias=one_t[:, 0:1], scale=-1.0)
                    P.act(B[:], B[:], AF.Exp, [B], [B], scale=0.5)
                    P.tt("pool", C[:], C[:], A[:], ALU.mult, [C, A], [C])
                    P.tt("pool", C[:], C[:], B[:], ALU.mult, [C, B], [C])
                    yield
                    P.scan(A[:], Dd[:], C[:], ch[l][:, blk:blk + 1], [Dd, C, ch[l]], [A])
                    P.copy("act", ch[l][:, blk:blk + 1], A[:, TT - 1:TT], [A], [ch[l]])
                    if dbg_on:
                        dbg_dump("lru%d" % blk, A, A[:], [128, TT])
                    P.tt("pool", ycat[:, 8 + blk, :], A[:], sgate[:, 8 + blk, :], ALU.mult, [A, sgv[8 + blk]], [ycv[8 + blk]])
                    if blk == 3:
                        lru_done[0] = True
                    yield

            KT_EARLY = (0, 1, 4, 5, 6, 7, 8, 9, 10, 11)
            KT_LATE = (2, 3)

            def oproj_early(l=l):
                while not (s5_done[0] and lru_done[0] and half0_done[0]):
                    yield
                for oc in range(4):
                    r = next_ring()
                    P.dma(r[:, 0:12 * 256].rearrange("p (k n) -> p k n", k=12),
                          w_out_b[l].rearrange("(k p) n -> p k n", p=128)[:, :, oc * 256:(oc + 1) * 256],
                          reads=[wsb], writes=[r])
                    rv = r[:, 0:12 * 256].rearrange("p (k n) -> p k n", k=12)
                    yield
                    for ob2 in range(2):
                        ob = oc * 2 + ob2
                        b = bank("proj")
                        for i_, k in enumerate(KT_EARLY):
                            P.mm(b[:, 0:TT], rv[:, k, ob2 * 128:(ob2 + 1) * 128], ycat[:, k, :], i_ == 0,
                                 i_ == len(KT_EARLY) - 1, [r, ycv[k]], [b])
                        P.tt("dve", hT[:, ob, :], hT[:, ob, :], b[:, 0:TT], ALU.add, [hT, b], [hT])
                        yield

            def ple_early(l=l):
                while not (prep3_done[0] and s5_done[0]):
                    yield
                r = next_ring()
                P.dma(r[:, 0:2048].rearrange("p (k n) -> p k n", k=2), ple_w_b[l].rearrange("(k p) n -> p k n", p=128),
                      reads=[wsb], writes=[r])
                rv = r[:, 0:2048].rearrange("p (k n) -> p k n", k=2)
                yield
                for ob in range(8):
                    b = bank("proj")
                    for k in range(2):
                        P.mm(b[:, 0:TT], rv[:, k, ob * 128:(ob + 1) * 128], pbf[:, k, :], k == 0, k == 1, [r, pbf], [b])
                    P.copy("act", zs[:, ob, :], b[:, 0:TT], [b], [zsv[ob]])
                    yield
                rms_rstd(zsv[0:8], lambda k: zs[:, k, :], 8, NORM_EPS)
                yield

            early_ok = ("rwkv" not in SKIP) and ("s5" not in SKIP) and ("lru" not in SKIP)
            gens = []
            if early_ok:
                gens.append((oproj_early(), 1))
                gens.append((ple_early(), 1))
            if "rwkv" not in SKIP:
                gens.append((rwkv_gen(), GW[0]))
            if "s5" not in SKIP:
                gens.append((s5_gen(), GW[1]))
            if "lru" not in SKIP:
                gens.append((lru_gen(), GW[2]))
            if "serial" in SKIP:
                for g, w in gens:
                    drive([(g, 1)])
            else:
                drive(gens)
            if dbg_on:
                dbg_dump("ycat_rw", ycat, ycat[:, 0:4, :].rearrange("p a t -> p (a t)"), [128, 4 * TT], BF16)

            if early_ok:
                r = next_ring()
                P.dma(r[:, 0:2 * 1024].rearrange("p (k n) -> p k n", k=2),
                      w_out_b[l].rearrange("(k p) n -> p k n", p=128)[:, 2:4, :],
                      reads=[wsb], writes=[r])
                rv = r[:, 0:2 * 1024].rearrange("p (k n) -> p k n", k=2)
                for ob in range(8):
                    b = bank("proj")
                    for i_, k in enumerate(KT_LATE):
                        P.mm(b[:, 0:TT], rv[:, i_, ob * 128:(ob + 1) * 128], ycat[:, k, :], i_ == 0, i_ == 1,
                             [r, ycv[k]], [b])
                    P.tt("dve", hT[:, ob, :], hT[:, ob, :], b[:, 0:TT], ALU.add, [hT, b], [hT])
            else:
                for oc in range(4):
                    r = next_ring()
                    P.dma(r[:, 0:12 * 256].rearrange("p (k n) -> p k n", k=12),
                          w_out_b[l].rearrange("(k p) n -> p k n", p=128)[:, :, oc * 256:(oc + 1) * 256],
                          reads=[wsb], writes=[r])
                    rv = r[:, 0:12 * 256].rearrange("p (k n) -> p k n", k=12)
                    for ob2 in range(2):
                        ob = oc * 2 + ob2
                        b = bank("proj")
                        for k in range(12):
                            P.mm(b[:, 0:TT], rv[:, k, ob2 * 128:(ob2 + 1) * 128], ycat[:, k, :], k == 0, k == 11,
                                 [r] + ycv, [b])
                        P.tt("dve", hT[:, ob, :], hT[:, ob, :], b[:, 0:TT], ALU.add, [hT, b], [hT])
            for k in range(8):
                P.copy("act" if k % 2 == 0 else "dve", xn[:, k, :], hT[:, k, :], [hT], [xnv[k]])
            epre = zs
            if not early_ok:
                r = next_ring()
                P.dma(r[:, 0:2048].rearrange("p (k n) -> p k n", k=2), ple_w_b[l].rearrange("(k p) n -> p k n", p=128),
                      reads=[wsb], writes=[r])
                rv = r[:, 0:2048].rearrange("p (k n) -> p k n", k=2)
                for ob in range(8):
                    b = bank("proj")
                    for k in range(2):
                        P.mm(b[:, 0:TT], rv[:, k, ob * 128:(ob + 1) * 128], pbf[:, k, :], k == 0, k == 1, [r, pbf], [b])
                    P.copy("act", epre[:, ob, :], b[:, 0:TT], [b], [zsv[ob]])
                rms_rstd(zsv[0:8], lambda k: epre[:, k, :], 8, NORM_EPS)
            for gc in range(2):
                r = next_ring()
                P.dma(r[:, 0:4096].rearrange("p (k n) -> p k n", k=8),
                      ple_gw_b[l].rearrange("(k p) n -> p k n", p=128)[:, :, gc * 512:(gc + 1) * 512],
                      reads=[wsb], writes=[r])
                rv = r[:, 0:4096].rearrange("p (k n) -> p k n", k=8)
                for ob2 in range(4):
                    ob = gc * 4 + ob2
                    b = bank("proj")
                    for k in range(8):
                        P.mm(b[:, 0:TT], rv[:, k, ob2 * 128:(ob2 + 1) * 128], xn[:, k, :], k == 0, k == 7, [r, xnv[k]], [b])
                    sg_, e_ = fs[6 + 2 * (ob % 2)], fs[7 + 2 * (ob % 2)]
                    P.act(sg_[:], b[:, 0:TT], AF.Sigmoid, [b], [sg_])
                    P.stt(e_[:], epre[:, ob, :], V(l, "png", ob), rstd[:], ALU.mult, ALU.mult, [zsv[ob], vec, rstd], [e_])
                    P.tt("dve", e_[:], e_[:], sg_[:], ALU.mult, [e_, sg_], [e_])
                    P.tt("dve", hT[:, ob, :], hT[:, ob, :], e_[:], ALU.add, [hT, e_], [hT])
            if dbg_on:
                dbg_dump("h1", hT, hT[:, :, :].rearrange("p a t -> p (a t)"), [128, 8 * TT])
        rms_rstd([hT], lambda k: hT[:, k, :], 8, NORM_EPS)
        fo = 2 * VEC_PER_LAYER
        for k in range(8):
            P.stt(zs[:, k, :], hT[:, k, :], vec[:, fo + k:fo + k + 1], rstd[:], ALU.mult, ALU.mult, [hT, vec, rstd], [zsv[k]])
        out_toks.append(P.dma(oT[:, t0:t0 + TT].rearrange("(k p) t -> p k t", p=128), zs[:, 0:8, :], reads=zsv[0:8]))
    out_toks.extend(dbg_out.values())
    ninst = P.ninst
    P.finish(out_toks)
    return nc, ninst


def pack_shared(inp):
    f = lambda a: np.asarray(a, np.float32)
    vec = np.zeros((128, NVEC), np.float32)
    for l in range(2):
        def put(name, arr, n):
            c = vcol(l, name)
            vec[:, c:c + n] = _pp(arr, n)
        put("ng", f(inp["norm_g"])[l], 8)
        put("mu", f(inp["rwkv_mu"])[l], 13)
        put("w0", f(inp["rwkv_w0"])[l], 4)
        put("a0", f(inp["rwkv_a0"])[l], 4)
        put("kk", f(inp["rwkv_k_k"])[l], 4)
        put("ka", f(inp["rwkv_k_a"])[l], 4)
        put("rk", f(inp["rwkv_r_k"])[l].reshape(512), 4)
        put("lnw", f(inp["rwkv_ln_w"])[l], 4)
        put("lnb", f(inp["rwkv_ln_b"])[l], 4)
        put("s5d", f(inp["s5_d"])[l], 4)
        put("glub", f(inp["s5_glu_b"])[l], 4)
        cw = f(inp["lru_conv_w"])[l]
        c = vcol(l, "cw")
        for j in range(4):
            vec[:, c + 4 * j:c + 4 * j + 4] = _pp(cw[j], 4)
        put("cb", f(inp["lru_conv_b"])[l], 4)
        put("ba", f(inp["lru_ba"])[l], 4)
        put("bx", f(inp["lru_bx"])[l], 4)
        put("lam", f(inp["lru_lambda"])[l], 4)
        put("png", f(inp["ple_norm_g"])[l], 8)
    vec[:, 2 * VEC_PER_LAYER:2 * VEC_PER_LAYER + 8] = _pp(f(inp["final_norm_g"]), 8)

    w2a2 = np.zeros((2, 128, 512), np.float32)
    w2a2[:, 0:64] = f(inp["rwkv_w2"])
    w2a2[:, 64:128] = f(inp["rwkv_a2"])
    lruw = np.zeros((2, 128, 8, 128), np.float32)
    for l in range(2):
        for m, key in enumerate(("lru_wa", "lru_wx")):
            w = f(inp[key])[l]
            for q in range(4):
                for b2 in range(2):
                    lruw[l, b2 * 64:(b2 + 1) * 64, m * 4 + q, b2 * 64:(b2 + 1) * 64] = w[2 * q + b2]
    lruw = lruw.reshape(2, 128, 1024)
    def modes(a):
        a = f(a).reshape(2, 16, 2, 64)
        return np.ascontiguousarray(a.transpose(0, 2, 3, 1).reshape(2, 128, 16))
    s5s = np.zeros((2, 128, 3, 16), np.float32)
    s5s[:, :, 0] = modes(inp["s5_a_re"])
    s5s[:, :, 1] = modes(inp["s5_a_im"])
    ldt = np.broadcast_to(f(inp["s5_log_dt"])[:, :, None], (2, 32, 64))
    s5s[:, :, 2] = modes(ldt)
    s5s = s5s.reshape(2, 128, 48)
    def bmodes(a):
        a = f(a).reshape(2, 16, 2, 64, 16)
        return a.transpose(0, 2, 3, 1, 4).reshape(2, 128, 16, 16)
    s5b = np.stack([bmodes(inp["s5_b_re"]), bmodes(inp["s5_b_im"])], axis=2).reshape(2, 128, 512)
    s5c = np.zeros((2, 128, 2, 16, 64), np.float32)
    for ri, key in enumerate(("s5_c_re", "s5_c_im")):
        c = f(inp[key]).reshape(2, 16, 2, 16, 64)
        for gh in range(2):
            for jh in range(2):
                c0 = 32 * jh + 16 * gh
                s5c[:, gh * 64:(gh + 1) * 64, ri, jh::2, c0:c0 + 16] = c[:, jh::2, gh].transpose(0, 3, 1, 2)
    s5c = s5c.reshape(2, 128, 2048)
    return {
        "w_in": np.ascontiguousarray(f(inp["w_in"])), "w_out": np.ascontiguousarray(f(inp["w_out"])),
        "ple_w": np.ascontiguousarray(f(inp["ple_w"])), "ple_gw": np.ascontiguousarray(f(inp["ple_gate_w"])),
        "glu_w": np.ascontiguousarray(f(inp["s5_glu_w"])), "vec": vec, "cst": make_consts()[0], "msk": make_consts()[1],
        "w2a2": w2a2, "lruw": np.ascontiguousarray(lruw), "s5s": np.ascontiguousarray(s5s),
        "s5b": np.ascontiguousarray(s5b), "s5c": np.ascontiguousarray(s5c),
    }


_NC_CACHE = {}


def run_cores(inp, TC, batches, dbg=None):
    key = (TC, tuple(sorted(dbg)) if dbg else None)
    if key not in _NC_CACHE:
        _NC_CACHE[key] = build_nc(TC, dbg)
    nc, ninst = _NC_CACHE[key]
    shared = pack_shared(inp)
    x = np.asarray(inp["x"], np.float32)
    p = np.asarray(inp["p"], np.float32)
    in_maps = []
    for b in batches:
        m = dict(shared)
        m["xT"] = np.ascontiguousarray(x[b, :TC].T)
        m["pT"] = np.ascontiguousarray(p[:, b, :TC].transpose(0, 2, 1))
        in_maps.append(m)
    res = run_bass_kernel_spmd(nc, in_maps, core_ids=list(range(len(batches))))
    return res


def kernel(**inputs):
    x = np.asarray(inputs["x"])
    B, S, _ = x.shape
    batches = [c % B for c in range(8)]
    res = run_cores(inputs, S, batches)
    out = np.empty((B, S, D), np.float32)
    for b in range(B):
        out[b] = res.results[b]["oT"].T
    return out.astype(x.dtype)
```

```python
import contextlib
import math
import numpy as np
import concourse.bass as bass
import concourse.mybir as mybir
from concourse.bass_utils import run_bass_kernel_spmd

F32 = mybir.dt.float32
BF16 = mybir.dt.bfloat16
ALU = mybir.AluOpType
AF = mybir.ActivationFunctionType

D = 1024
DIN = 4224
DMIX = 1536
DPLE = 256
TT = 256
LCH = 64
NCH = TT // LCH
TS5 = 128
import os
SKIP = set(os.environ.get("KSKIP", "").split(","))
S5E = os.environ.get("KS5E", "pool")
GW = tuple(int(v) for v in os.environ.get("KGW", "1,1,1").split(","))
GN_EPS = 64e-5
NORM_EPS = 1e-6


class Tok:
    __slots__ = ("sem", "val", "eng", "dma")

    def __init__(self, sem, val, eng, dma):
        self.sem, self.val, self.eng, self.dma = sem, val, eng, dma


class Buf:
    def __init__(self, t, name):
        self.t = t
        self.name = name
        self.w = None
        self.r = []

    def __getitem__(self, idx):
        return self.t[idx]


class Prog:
    ENGS = ("pe", "act", "dve", "pool", "sp")

    def __init__(self, nc, n_dma_sems=32):
        self.nc = nc
        self.es = contextlib.ExitStack()
        self.ops = {e: [] for e in self.ENGS}
        self.cnt = {e: 0 for e in self.ENGS}
        self.sem = {e: self.es.enter_context(nc.semaphore("s_" + e)) for e in self.ENGS}
        self.dsem = [self.es.enter_context(nc.semaphore("d%d" % i)) for i in range(n_dma_sems)]
        self.duse = [0] * n_dma_sems
        self.dnext = 0
        self.seen = {e: {} for e in self.ENGS}
        self.nbuf = 0
        self.ninst = 0
        self.stack = [self.es]

    def push(self):
        st = contextlib.ExitStack()
        self.stack.append(st)

    def pop(self):
        self.barrier()
        self.stack.pop().close()

    def barrier(self):
        toks = []
        for f in self.ENGS:
            if self.cnt[f] > 0:
                toks.append(Tok(self.sem[f], self.cnt[f], f, False))
        for i, s in enumerate(self.dsem):
            if self.duse[i] > 0:
                toks.append(Tok(s, 16 * self.duse[i], "dma", True))
        for e in self.ENGS:
            wl = []
            for t in toks:
                if t.eng == e and not t.dma:
                    continue
                k = id(t.sem)
                if self.seen[e].get(k, 0) >= t.val:
                    continue
                self.seen[e][k] = t.val
                wl.append((t.sem, t.val))

            def run(en, wl=wl):
                for (s, v) in wl:
                    en.wait_ge(s, v)
            self.ops[e].append(run)

    def sb(self, shape, dt=F32, name=None):
        self.nbuf += 1
        name = name or ("b%d" % self.nbuf)
        t = self.stack[-1].enter_context(self.nc.sbuf_tensor("sb_" + name, list(shape), dt))
        return Buf(t, name)

    def ps(self, name, dt=F32, cols=512):
        t = self.es.enter_context(self.nc.psum_tensor(name, [128, cols], dt))
        return Buf(t, name)

    def wrap(self, t, name):
        return Buf(t, name)

    def views(self, buf, n):
        return [Buf(buf.t, "%s.v%d" % (buf.name, i)) for i in range(n)]

    def _need(self, eng, tok, waits, is_dma_issue):
        if tok is None:
            return
        if tok.eng == eng and not tok.dma and not is_dma_issue and eng == "pe":
            return
        k = id(tok.sem)
        if self.seen[eng].get(k, 0) >= tok.val:
            return
        cur = waits.get(k)
        if cur is None or cur[1] < tok.val:
            waits[k] = (tok.sem, tok.val)

    def emit(self, eng, fn, reads=(), writes=(), dma=False):
        waits = {}
        for b in reads:
            self._need(eng, b.w, waits, dma)
        for b in writes:
            self._need(eng, b.w, waits, dma)
            for t in b.r:
                self._need(eng, t, waits, dma)
        if dma:
            i = self.dnext
            self.dnext = (self.dnext + 1) % len(self.dsem)
            s = self.dsem[i]
            if self.duse[i] > 0:
                self._need(eng, Tok(s, 16 * self.duse[i], "dma", True), waits, True)
            self.duse[i] += 1
            tok = Tok(s, 16 * self.duse[i], "dma", True)
            inc = 16
        else:
            self.cnt[eng] += 1
            tok = Tok(self.sem[eng], self.cnt[eng], eng, False)
            inc = 1
        wl = list(waits.values())
        for (s, v) in wl:
            self.seen[eng][id(s)] = v
        tsem = tok.sem
        self.ninst += 1 + len(wl)

        def run(e, wl=wl, fn=fn, tsem=tsem, inc=inc):
            for (s, v) in wl:
                e.wait_ge(s, v)
            fn(e).then_inc(tsem, inc)

        self.ops[eng].append(run)
        for b in reads:
            b.r = [t for t in b.r if t.sem is not tok.sem]
            b.r.append(tok)
        for b in writes:
            b.w = tok
            b.r = []
        return tok

    def finish(self, out_toks):
        wl = [(t.sem, t.val) for t in out_toks]

        def run(e, wl=wl):
            for (s, v) in wl:
                e.wait_ge(s, v)

        self.ops["sp"].append(run)
        nc = self.nc
        ops = self.ops
        with nc.Block() as block:
            @block.tensor
            def _(e):
                for f in ops["pe"]:
                    f(e)

            @block.scalar
            def _(e):
                for f in ops["act"]:
                    f(e)

            @block.vector
            def _(e):
                for f in ops["dve"]:
                    f(e)

            @block.gpsimd
            def _(e):
                for f in ops["pool"]:
                    f(e)

            @block.sync
            def _(e):
                for f in ops["sp"]:
                    f(e)
        self.es.close()

    def dma(self, out, in_, reads=(), writes=(), eng="sp", **kw):
        return self.emit(eng, lambda e: e.dma_start(out=out, in_=in_, **kw), reads, writes, dma=True)

    def mm(self, out, lhsT, rhs, start, stop, reads, writes):
        return self.emit("pe", lambda e: e.matmul(out, lhsT, rhs, start=start, stop=stop), reads, writes)

    def act(self, out, in_, func, reads, writes, bias=None, scale=None):
        kw = {}
        if bias is not None:
            kw["bias"] = bias
        if scale is not None:
            kw["scale"] = scale
        return self.emit("act", lambda e: e.activation(out=out, in_=in_, func=func, **kw), reads, writes)

    def tt(self, eng, out, in0, in1, op, reads, writes):
        return self.emit(eng, lambda e: e.tensor_tensor(out=out, in0=in0, in1=in1, op=op), reads, writes)

    def ts(self, eng, out, in0, s1, s2, op0, op1, reads, writes):
        if op1 is None:
            return self.emit(eng, lambda e: e.tensor_scalar(out, in0, s1, None, op0), reads, writes)
        return self.emit(eng, lambda e: e.tensor_scalar(out, in0, s1, s2, op0, op1), reads, writes)

    def stt(self, out, in0, scalar, in1, op0, op1, reads, writes):
        return self.emit("dve", lambda e: e.scalar_tensor_tensor(out, in0, scalar, in1, op0, op1), reads, writes)

    def copy(self, eng, out, in_, reads, writes):
        if eng == "act":
            return self.emit("act", lambda e: e.activation(out=out, in_=in_, func=AF.Copy), reads, writes)
        return self.emit(eng, lambda e: e.tensor_copy(out, in_), reads, writes)

    def memset(self, eng, ap, val, writes):
        return self.emit(eng, lambda e: e.memset(ap, val), (), writes)

    def scan(self, out, d0, d1, init, reads, writes):
        return self.emit("dve", lambda e: e.tensor_tensor_scan(out, d0, d1, init, ALU.mult, ALU.add), reads, writes)

    def recip(self, out, in_, reads, writes):
        return self.emit("dve", lambda e: e.reciprocal(out, in_), reads, writes)


VEC_FIELDS = [("ng", 8), ("mu", 13), ("w0", 4), ("a0", 4), ("kk", 4), ("ka", 4), ("rk", 4), ("lnw", 4),
              ("lnb", 4), ("s5d", 4), ("glub", 4), ("cw", 16), ("cb", 4), ("ba", 4), ("bx", 4), ("lam", 4),
              ("png", 8)]
VEC_PER_LAYER = sum(n for _, n in VEC_FIELDS)
VEC_OFF = {}
_o = 0
for _n, _c in VEC_FIELDS:
    VEC_OFF[_n] = _o
    _o += _c
NVEC = 2 * VEC_PER_LAYER + 8

CST_IDENT = 0
CST_ONESBD = 128
CST_SCAN = 256
NCST = 256 + TT
MSK_USN = 0
MSK_LSN = NCH * 128
MSK_USP = 2 * NCH * 128
MSK_CI = 3 * NCH * 128
NMSK = 3 * NCH * 128 + NCH * LCH


def vcol(l, name, i=0):
    return l * VEC_PER_LAYER + VEC_OFF[name] + i


def _pp(v, n):
    return np.ascontiguousarray(np.asarray(v, np.float32).reshape(n, 128).T)


def make_consts():
    c = np.zeros((128, NCST), np.float32)
    i = np.arange(128)[:, None]
    j = np.arange(128)[None, :]
    c[:, CST_IDENT:CST_IDENT + 128] = (i == j)
    c[:, CST_ONESBD:CST_ONESBD + 128] = ((i // 64) == (j // 64))
    tt = np.arange(TT)[None, :]
    c[:, CST_SCAN:CST_SCAN + TT] = 1.0 * ((tt % LCH) != 0)
    m = np.zeros((128, NMSK), np.float32)
    t = np.arange(64)[None, :]
    for ch in range(NCH):
        m[:, MSK_USN + ch * 128:MSK_USN + (ch + 1) * 128] = -1.0 * (j > i)
        m[:, MSK_LSN + ch * 128:MSK_LSN + (ch + 1) * 128] = -1.0 * (i > j)
        m[:, MSK_USP + ch * 128:MSK_USP + (ch + 1) * 128] = 1.0 * (j > i)
        m[:, MSK_CI + ch * 64:MSK_CI + (ch + 1) * 64] = 1.0 * (t >= (i % 64))
    return c, m


def build_nc(TC, dbg=None):
    assert TC % TT == 0
    NT = TC // TT
    dbg = dbg or set()
    nc = bass.Bass("TRN2", target_bir_lowering=False)

    def din(name, shape, dt=F32):
        return nc.dram_tensor(name, list(shape), dt, kind="ExternalInput").ap()

    xT = din("xT", [D, TC])
    pT = din("pT", [2, DPLE, TC])
    w_in = din("w_in", [2, D, DIN])
    w_out = din("w_out", [2, DMIX, D])
    ple_w = din("ple_w", [2, DPLE, D])
    ple_gw = din("ple_gw", [2, D, D])
    glu_w = din("glu_w", [2, 512, 512])
    vec_d = din("vec", [128, NVEC])
    cst_d = din("cst", [128, NCST])
    msk_d = din("msk", [128, NMSK])
    w2a2_d = din("w2a2", [2, 128, 512])
    lruw_d = din("lruw", [2, 128, 8 * 128])
    s5s_d = din("s5s", [2, 128, 3 * 16])
    s5b_d = din("s5b", [2, 128, 2 * 16 * 16])
    s5c_d = din("s5c", [2, 128, 2 * 16 * 64])
    oT = nc.dram_tensor("oT", [D, TC], F32, kind="ExternalOutput").ap()
    dbg_out = {}

    def dram_int(name, shape, dt):
        return nc.dram_tensor(name, list(shape), dt, kind="Internal").ap()

    w_in_b = dram_int("w_in_b", [2, D, DIN], BF16)
    w_out_b = dram_int("w_out_b", [2, DMIX, D], BF16)
    ple_w_b = dram_int("ple_w_b", [2, DPLE, D], BF16)
    ple_gw_b = dram_int("ple_gw_b", [2, D, D], BF16)
    glu_w_b = dram_int("glu_w_b", [2, 512, 512], BF16)
    s5tab_d = dram_int("s5tab", [2, 128, 2 * 16 * TS5], F32)

    P = Prog(nc)
    wsb = P.wrap(None, "wscratch")
    tabsb = P.wrap(None, "s5tabscr")

    def dbg_dump(name, buf, ap, shape, dt=F32):
        if name not in dbg:
            return
        o = nc.dram_tensor("dbg_" + name, list(shape), dt, kind="ExternalOutput").ap()
        dbg_out[name] = P.dma(o, ap, reads=[buf])

    vec = P.sb([128, NVEC], F32, "vec")
    cst = P.sb([128, NCST], F32, "cst")
    P.dma(vec[:], vec_d, writes=[vec])
    P.dma(cst[:], cst_d, writes=[cst])
    cstb = P.sb([128, 128], BF16, "cstb")
    P.copy("dve", cstb[:], cst[:, CST_IDENT:CST_IDENT + 128], [cst], [cstb])
    mskb = P.sb([128, NMSK], BF16, "mskb")
    ident_f = cst[:, CST_IDENT:CST_IDENT + 128]
    ident_b = cstb[:, 0:128]
    onesbd_f = cst[:, CST_ONESBD:CST_ONESBD + 128]
    ones_f = P.sb([128, 128], F32, "ones_f")
    P.memset("pool", ones_f[:], 1.0, [ones_f])
    one_t = P.sb([128, 1], F32, "one_t")
    P.memset("pool", one_t[:], 1.0, [one_t])

    def V(l, name, i=0, n=1):
        c = vcol(l, name, i)
        return vec[:, c:c + n]

    for l in range(2):
        for (src, dst, rows) in ((w_in, w_in_b, D), (w_out, w_out_b, DMIX), (ple_w, ple_w_b, DPLE),
                                 (ple_gw, ple_gw_b, D), (glu_w, glu_w_b, 512)):
            for r0 in range(0, rows, 128):
                P.dma(dst[l, r0:r0 + 128, :], src[l, r0:r0 + 128, :], writes=[wsb], eng="pool",
                      max_dma_last_dim=4096)

    w2a2 = []
    lruw = []
    for l in range(2):
        w2a2.append(P.sb([128, 512], BF16, "w2a2_%d" % l))
        lruw.append(P.sb([128, 1024], BF16, "lruw_%d" % l))
    lru_c = P.sb([128, 2, 8], F32, "lru_c")
    s5B = [P.sb([128, 2 * 4 * 2 * 128], BF16, "s5B%d" % l) for l in range(2)]
    s5C = [P.sb([128, 2048], BF16, "s5C%d" % l) for l in range(2)]
    s5keep = [P.sb([128, 3, 16], F32, "s5keep%d" % l) for l in range(2)]
    s5rotb = [P.sb([128, 2, 16], F32, "s5rot%d" % l) for l in range(2)]
    ps_misc = P.ps("ps7")
    P.push()
    stage = P.sb([128, 1024], F32, "stage")
    mstage = P.sb([128, NMSK], F32, "mstage")
    P.dma(mstage[:], msk_d, writes=[mstage])
    P.copy("act", mskb[:], mstage[:], [mstage], [mskb])
    for l in range(2):
        P.dma(stage[:, 0:512], w2a2_d[l], writes=[stage])
        P.copy("act", w2a2[l][:], stage[:, 0:512], [stage], [w2a2[l]])
        P.dma(stage[:], lruw_d[l], writes=[stage])
        P.copy("act", lruw[l][:], stage[:], [stage], [lruw[l]])

    for l in range(2):
        tmp = P.sb([128, 4], F32, "lrutmp%d" % l)
        P.act(tmp[:], V(l, "lam", 0, 4), AF.Exp, [vec], [tmp], scale=-1.0)
        P.act(tmp[:], tmp[:], AF.Ln, [tmp, one_t], [tmp], bias=one_t[:, 0:1])
        P.ts("dve", lru_c[:, l, 0:4], tmp[:], -8.0, None, ALU.mult, None, [tmp], [lru_c])
        P.ts("dve", lru_c[:, l, 4:8], tmp[:], -16.0, None, ALU.mult, None, [tmp], [lru_c])

    s5rot = []
    for l in range(2):
        s5s = P.sb([128, 48], F32, "s5s%d" % l)
        P.dma(s5s[:], s5s_d[l], writes=[s5s])
        a_re = s5s[:, 0:16]
        a_im = s5s[:, 16:32]
        ldt = s5s[:, 32:48]
        w = P.sb([128, 16, 16], F32, "s5w%d" % l)
        R = [w]

        def row(i):
            return w[:, i, :]
        dt_, rho, th, cc, ss, t1, t2, lr, li, den, qre, qim, nr = [row(i) for i in range(13)]
        P.act(dt_, ldt, AF.Exp, [s5s], R)
        P.tt("dve", rho, a_re, dt_, ALU.mult, [s5s] + R, R)
        P.act(rho, rho, AF.Exp, R, R)
        P.tt("dve", th, a_im, dt_, ALU.mult, [s5s] + R, R)
        hp = P.sb([128, 1], F32, "halfpi%d" % l)
        P.memset("dve", hp[:], math.pi / 2, [hp])
        P.act(cc, th, AF.Sin, R + [hp], R, bias=hp[:, 0:1], scale=1.0 / 16)
        P.act(ss, th, AF.Sin, R, R, scale=1.0 / 16)

        def csq(c_, s_):
            P.tt("dve", t1, c_, c_, ALU.mult, R, R)
            P.tt("dve", t2, s_, s_, ALU.mult, R, R)
            P.stt(s_, c_, 2.0, s_, ALU.mult, ALU.mult, R, R)
            P.tt("dve", c_, t1, t2, ALU.subtract, R, R)
        for _ in range(4):
            csq(cc, ss)
        P.tt("dve", lr, rho, cc, ALU.mult, R, R)
        P.tt("dve", li, rho, ss, ALU.mult, R, R)
        P.tt("dve", t1, a_re, a_re, ALU.mult, [s5s] + R, R)
        P.tt("dve", t2, a_im, a_im, ALU.mult, [s5s] + R, R)
        P.tt("dve", den, t1, t2, ALU.add, R, R)
        P.recip(den, den, R, R)
        P.ts("dve", nr, lr, -1.0, None, ALU.add, None, R, R)
        P.tt("dve", t1, nr, a_re, ALU.mult, [s5s] + R, R)
        P.tt("dve", t2, li, a_im, ALU.mult, [s5s] + R, R)
        P.tt("dve", t1, t1, t2, ALU.add, R, R)
        P.tt("dve", qre, t1, den, ALU.mult, R, R)
        P.tt("dve", t1, li, a_re, ALU.mult, [s5s] + R, R)
        P.tt("dve", t2, nr, a_im, ALU.mult, [s5s] + R, R)
        P.tt("dve", t1, t1, t2, ALU.subtract, R, R)
        P.tt("dve", qim, t1, den, ALU.mult, R, R)
        keep = s5keep[l]
        P.copy("dve", keep[:, 0, :], rho, R, [keep])
        P.copy("dve", keep[:, 1, :], cc, R, [keep])
        P.copy("dve", keep[:, 2, :], ss, R, [keep])

        sbf = P.sb([128, 512], F32, "s5b_in%d" % l)
        P.dma(sbf[:], s5b_d[l], writes=[sbf])
        bre = sbf[:, 0:256].rearrange("p (j h) -> p j h", h=16)
        bim = sbf[:, 256:512].rearrange("p (j h) -> p j h", h=16)
        Bt = s5B[l]
        Btv = Bt[:, :].rearrange("p (r b q m) -> p r b q m", r=2, b=4, q=2)
        bpad = P.sb([128, 2, 128], BF16, "s5bpad%d" % l)
        tb = P.sb([128, 2, 16], F32, "s5tb%d" % l)
        for j in range(16):
            P.ts("dve", tb[:, 0, :], bim[:, j, :], qim[:, j:j + 1], None, ALU.mult, None, [sbf] + R, [tb])
            P.stt(tb[:, 0, :], bre[:, j, :], qre[:, j:j + 1], tb[:, 0, :], ALU.mult, ALU.subtract, [sbf, tb] + R, [tb])
            P.ts("dve", tb[:, 1, :], bre[:, j, :], qim[:, j:j + 1], None, ALU.mult, None, [sbf] + R, [tb])
            P.stt(tb[:, 1, :], bim[:, j, :], qre[:, j:j + 1], tb[:, 1, :], ALU.mult, ALU.add, [sbf, tb] + R, [tb])
            P.memset("pool", bpad[:], 0.0, [bpad])
            for gh in range(2):
                col0 = 32 * (j % 4) + gh * 16
                for ri in range(2):
                    P.copy("pool", bpad[gh * 64:(gh + 1) * 64, ri, col0:col0 + 16],
                           tb[gh * 64:(gh + 1) * 64, ri, :], [tb], [bpad])
            for ri in range(2):
                P.mm(ps_misc[:, ri * 128:(ri + 1) * 128], bpad[:, ri, :], ident_b, True, True, [bpad, cstb], [ps_misc])
            hf = (j % 4) // 2
            for ri in range(2):
                P.copy("act", Btv[64 * hf:64 * hf + 64, ri, j // 4, j % 2, :],
                       ps_misc[64 * hf:64 * hf + 64, ri * 128:(ri + 1) * 128], [ps_misc], [Bt])

        scf = P.sb([128, 2048], F32, "s5c_in%d" % l)
        P.dma(scf[:], s5c_d[l], writes=[scf])
        Ct = s5C[l]
        P.copy("act", Ct[:, 0:1024], scf[:, 0:1024], [scf], [Ct])
        P.ts("dve", Ct[:, 1024:2048], scf[:, 1024:2048], -1.0, None, ALU.mult, None, [scf], [Ct])

        tab = P.sb([128, 2, 16, TS5], F32, "s5tabb%d" % l)
        P.memset("pool", tab[:, 0, :, 0:1], 1.0, [tab])
        P.memset("pool", tab[:, 1, :, 0:1], 0.0, [tab])
        ec = P.sb([128, 2, 16], F32, "s5ec%d" % l)
        P.copy("dve", ec[:, 0, :], cc, R, [ec])
        P.copy("dve", ec[:, 1, :], ss, R, [ec])
        m = 1
        while m < TS5:
            for j in range(16):
                cj = ec[:, 0, j:j + 1]
                sj = ec[:, 1, j:j + 1]
                src_c = tab[:, 0, j, 0:m]
                src_s = tab[:, 1, j, 0:m]
                dst_c = tab[:, 0, j, m:2 * m]
                dst_s = tab[:, 1, j, m:2 * m]
                P.ts("dve", dst_c, src_s, sj, None, ALU.mult, None, [tab, ec], [tab])
                P.stt(dst_c, src_c, cj, dst_c, ALU.mult, ALU.subtract, [tab, ec], [tab])
                P.ts("dve", dst_s, src_c, sj, None, ALU.mult, None, [tab, ec], [tab])
                P.stt(dst_s, src_s, cj, dst_s, ALU.mult, ALU.add, [tab, ec], [tab])
            e_c = ec[:, 0, :]
            e_s = ec[:, 1, :]
            P.tt("dve", t1, e_c, e_c, ALU.mult, [ec] + R, R)
            P.tt("dve", t2, e_s, e_s, ALU.mult, [ec] + R, R)
            P.stt(e_s, e_c, 2.0, e_s, ALU.mult, ALU.mult, [ec], [ec])
            P.tt("dve", e_c, t1, t2, ALU.subtract, R, [ec])
            m *= 2
        rot = s5rotb[l]
        P.copy("dve", rot[:], ec[:], [ec], [rot])
        s5rot.append((rot, keep))
        P.dma(s5tab_d[l], tab[:, :, :, :].rearrange("p a j t -> p (a j t)"), reads=[tab], writes=[tabsb])
        dbg_dump("s5w%d" % l, w, w[:, :, :].rearrange("p a b -> p (a b)"), [128, 256])
        dbg_dump("s5keep%d" % l, keep, keep[:, :, :].rearrange("p a b -> p (a b)"), [128, 48])
        dbg_dump("s5tab%d" % l, tab, tab[:, :, :, :].rearrange("p a j t -> p (a j t)"), [128, 2 * 16 * TS5])
        dbg_dump("s5B%d" % l, Bt, Bt[:, :], [128, 2048], BF16)
    P.pop()

    hT = P.sb([128, 8, TT], F32, "hT")
    hTv = P.views(hT, 8)
    xn = P.sb([128, 8, TT], BF16, "xn")
    xnv = P.views(xn, 8)
    zst = [P.sb([128, 1 + TT], F32, "zst%d" % i) for i in range(2)]
    zs = P.sb([128, 13, TT], F32, "zs")
    zsv = P.views(zs, 13)
    zu = P.sb([128, 4, TT], F32, "zu")
    zx = P.sb([128, 4, 3 + TT], F32, "zx")
    sgate = P.sb([128, 12, TT], BF16, "sgate")
    sgv = P.views(sgate, 12)
    ycat = P.sb([128, 12, TT], BF16, "ycat")
    ycv = P.views(ycat, 12)
    pbf = P.sb([128, 2, TT], BF16, "pbf")
    ring = [P.sb([128, 4096], BF16, "ring%d" % i) for i in range(3)]
    ringi = [0]
    s5tab = P.sb([128, 2, 16, TS5], F32, "s5tab")
    banks = [P.ps("ps%d" % i) for i in range(7)] + [ps_misc]
    rot_i = {"proj": 0, "rw": 0}

    def bank(group):
        ids = (0, 1) if group == "proj" else (2, 3, 6)
        i = rot_i[group]
        rot_i[group] = (i + 1) % len(ids)
        return banks[ids[i]]

    def next_ring():
        r = ring[ringi[0]]
        ringi[0] = (ringi[0] + 1) % 3
        return r

    cz = [P.sb([128, 13], F32, "cz%d" % l) for l in range(2)]
    cl = [P.sb([128, 4, 3], F32, "cl%d" % l) for l in range(2)]
    ch = [P.sb([128, 4], F32, "ch%d" % l) for l in range(2)]
    s5z = [P.sb([128, 2, 16], F32, "s5z%d" % l) for l in range(2)]
    s5zv = [P.views(s5z[l], 16) for l in range(2)]
    Tst = [[P.sb([128, 128], BF16, "T%d_%d" % (l, pb)) for pb in range(4)] for l in range(2)]
    for l in range(2):
        P.memset("pool", cz[l][:], 0.0, [cz[l]])
        P.memset("pool", cl[l][:], 0.0, [cl[l]])
        P.memset("pool", ch[l][:], 0.0, [ch[l]])
        P.memset("pool", s5z[l][:], 0.0, s5zv[l])
        for pb in range(4):
            P.memset("pool", Tst[l][pb][:], 0.0, [Tst[l][pb]])

    NF = 17
    fs = [P.sb([128, TT], F32, "rf%d" % i) for i in range(NF)]
    pad_names = ["RTp", "KTp", "CTp", "BTp", "VTp", "KGp", "BGp"]
    pads = {n: P.sb([128, NCH * 128], BF16, n) for n in pad_names}
    for n in pad_names:
        P.memset("pool", pads[n][:], 0.0, [pads[n]])
    RTc = P.sb([128, TT], BF16, "RTc")
    tanh_wd = P.sb([128, TT], BF16, "tanhwd")
    Blev_i = [[P.sb([128, NCH * 128], BF16, "Blev%d_%d" % (i, k)) for i in range(2)] for k in range(2)]
    BTlev_i = [[P.sb([128, NCH * 128], BF16, "BTlev%d_%d" % (i, k)) for i in range(2)] for k in range(2)]
    AkkT_i = [P.sb([128, NCH * 128], BF16, "AkkT_%d" % k) for k in range(2)]
    Xbf_i = [P.sb([128, NCH * 256], BF16, "Xbf_%d" % k) for k in range(2)]
    NPI = 2
    pp = []
    for i in range(NPI):
        d = {}
        for n in ("Vbd", "KGbd", "BGbd", "PT", "nU0", "Wbd"):
            d[n] = P.sb([128, NCH * 128], BF16, "%s_%d" % (n, i))
        for n in ("Rhat", "ArkT", "ArbT"):
            d[n] = P.sb([128, TT], BF16, "%s_%d" % (n, i))
        d["bonus"] = P.sb([128, TT], F32, "bonus_%d" % i)
        d["GL"] = P.sb([128, NCH], F32, "GL_%d" % i)
        d["rt32"] = P.sb([128, TT], F32, "rt32_%d" % i)
        pp.append(d)
    mix = P.sb([128, 4, TT], F32, "mix")
    mixv = P.views(mix, 4)

    NS5SET = 2
    s5f = [[P.sb([128, TS5], F32, "s5f%d_%d" % (k, i)) for i in range(8)] for k in range(NS5SET)]
    NS5X = 3
    s5x = [P.sb([128, 2, TS5], BF16, "s5x%d" % k) for k in range(NS5X)]
    s5zl = [P.sb([128, 2], F32, "s5zl%d" % k) for k in range(NS5SET)]
    spf = [P.sb([128, TT], F32, "spf%d" % i) for i in range(3)]
    lf = [P.sb([128, TT], F32, "lf%d" % i) for i in range(4)]
    ubf = P.sb([128, 4, TT], BF16, "ubf")
    lb = P.sb([128, TT], BF16, "lb")
    rstd = P.sb([128, TT], F32, "rstd")
    sq = P.sb([128, 4, TT], BF16, "sq")
    ones_b = P.sb([128, 128], BF16, "ones_b")
    P.memset("pool", ones_b[:], 1.0, [ones_b])

    out_toks = []
    eps_t = {}
    for e_ in (NORM_EPS, GN_EPS):
        t = P.sb([128, 1], F32, "eps%d" % len(eps_t))
        P.memset("pool", t[:], e_, [t])
        eps_t[e_] = t
    neg_half = -math.exp(-0.5)

    def rms_rstd(src_bufs, src_ap_fn, nblk, eps):
        b = bank("proj")
        for k in range(nblk):
            sb_ = [src_bufs[k]] if len(src_bufs) == nblk else src_bufs
            if k % 2 == 0:
                P.act(sq[:, k % 4, :], src_ap_fn(k), AF.Square, sb_, [sq])
            else:
                P.tt("dve", sq[:, k % 4, :], src_ap_fn(k), src_ap_fn(k), ALU.mult, sb_, [sq])
            P.mm(b[:, 0:TT], ones_b[:], sq[:, k % 4, :], k == 0, k == nblk - 1, [ones_b, sq], [b])
        P.act(rstd[:], b[:, 0:TT], AF.Ln, [b, eps_t[eps]], [rstd], bias=eps_t[eps][:, 0:1], scale=1.0 / (nblk * 128))
        P.act(rstd[:], rstd[:], AF.Exp, [rstd], [rstd], scale=-0.5)

    def drive(items):
        active = list(items)
        while active:
            for item in list(active):
                g, w = item
                for _ in range(w):
                    try:
                        next(g)
                    except StopIteration:
                        active.remove(item)
                        break

    s5ctr = [0]

    for it in range(NT):
        t0 = it * TT
        first = (it == 0)
        P.dma(hT[:], xT[:, t0:t0 + TT].rearrange("(k p) t -> p k t", p=128), writes=hTv)
        for l in range(2):
            dbg_on = first and l == 0
            P.dma(pbf[:], pT[l, :, t0:t0 + TT].rearrange("(k p) t -> p k t", p=128), writes=[pbf], eng="pool")
            P.dma(s5tab[:, :, :, :].rearrange("p a j t -> p (a j t)"), s5tab_d[l], reads=[tabsb], writes=[s5tab])
            rms_rstd(hTv, lambda k: hT[:, k, :], 8, NORM_EPS)
            for k in range(8):
                P.stt(xn[:, k, :], hT[:, k, :], V(l, "ng", k), rstd[:], ALU.mult, ALU.mult, [hTv[k], vec, rstd], [xnv[k]])
            if dbg_on:
                dbg_dump("xn", xnv[7], xn[:, :, :].rearrange("p a t -> p (a t)"), [128, 8 * TT], BF16)

            wchunk = {}

            def in_block(cb, l=l, wchunk=wchunk):
                ci = cb // 4
                if ci not in wchunk:
                    r = next_ring()
                    ncol = 512 if ci < 8 else 128
                    P.dma(r[:, 0:8 * ncol].rearrange("p (k n) -> p k n", k=8),
                          w_in_b[l].rearrange("(k p) n -> p k n", p=128)[:, :, ci * 512:ci * 512 + ncol],
                          reads=[wsb], writes=[r])
                    wchunk[ci] = (r, ncol)
                r, ncol = wchunk[ci]
                rv = r[:, 0:8 * ncol].rearrange("p (k n) -> p k n", k=8)
                c0 = (cb % 4) * 128
                b = bank("proj")
                for k in range(8):
                    P.mm(b[:, 0:TT], rv[:, k, c0:c0 + 128], xn[:, k, :], k == 0, k == 7, [r, xnv[k]], [b])
                return b

            for cb in range(13):
                st = zst[cb % 2]
                b = in_block(cb)
                P.copy("act", st[:, 0:1], cz[l][:, cb:cb + 1], [cz[l]], [st])
                P.copy("act", st[:, 1:1 + TT], b[:, 0:TT], [b], [st])
                P.copy("act", cz[l][:, cb:cb + 1], st[:, TT:TT + 1], [st], [cz[l]])
                d_ = lf[cb % 2]
                P.tt("pool", d_[:], st[:, 0:TT], st[:, 1:1 + TT], ALU.subtract, [st], [d_])
                P.stt(zs[:, cb, :], d_[:], V(l, "mu", cb), st[:, 1:1 + TT], ALU.mult, ALU.add, [d_, vec, st], [zsv[cb]])
            if dbg_on:
                dbg_dump("zs", zs, zs[:, :, :].rearrange("p a t -> p (a t)"), [128, 13 * TT])
            P.act(tanh_wd[0:64, :], zs[0:64, 12, :], AF.Tanh, [zsv[12]], [tanh_wd])
            P.copy("act", tanh_wd[64:128, :], zs[64:128, 12, :], [zsv[12]], [tanh_wd])
            for blk in range(4):
                b = in_block(13 + blk)
                P.act(sgate[:, blk, :], b[:, 0:TT], AF.Silu, [b], [sgv[blk]])
            for blk in range(4):
                b = in_block(17 + blk)
                P.copy("act", zu[:, blk, :], b[:, 0:TT], [b], [zu])
            P.copy("pool", ubf[:], zu[:], [zu], [ubf])
            for blk in range(4):
                b = in_block(21 + blk)
                P.act(sgate[:, 4 + blk, :], b[:, 0:TT], AF.Silu, [b], [sgv[4 + blk]])
            P.copy("act", zx[:, :, 0:3], cl[l][:, :, :], [cl[l]], [zx])
            for blk in range(4):
                b = in_block(25 + blk)
                P.copy("act", zx[:, blk, 3:3 + TT], b[:, 0:TT], [b], [zx])
            P.copy("act", cl[l][:, :, :], zx[:, :, TT:TT + 3], [zx], [cl[l]])
            for blk in range(4):
                b = in_block(29 + blk)
                P.act(sgate[:, 8 + blk, :], b[:, 0:TT], AF.Silu, [b], [sgv[8 + blk]])

            def prep(pb, inst, l=l, dbg_on=dbg_on):
                d = pp[inst]
                Blev, BTlev, AkkT = Blev_i[inst], BTlev_i[inst], AkkT_i[inst]
                r_ = zs[:, pb, :]
                k_ = zs[:, 4 + pb, :]
                v_ = zs[:, 8 + pb, :]
                zr_, zk_, zv_ = zsv[pb], zsv[4 + pb], zsv[8 + pb]
                (sg, ld, a_, kk_, kk2, sqk, kap, t1, kp, b_, lg, eg, ieg, eg1, dl, egl, rk) = fs[:17]
                rt32 = d["rt32"]
                cols = slice(pb * 128, (pb + 1) * 128)
                bw = bank("rw")
                P.mm(bw[:, 0:TT], w2a2[l][0:64, cols], tanh_wd[0:64, :], True, True, [w2a2[l], tanh_wd], [bw])
                P.act(sg[:], bw[:, 0:TT], AF.Sigmoid, [bw, vec], [sg], bias=V(l, "w0", pb))
                P.ts("dve", ld[:], sg[:], neg_half, None, ALU.mult, None, [sg], [ld])
                ba_ = bank("rw")
                P.mm(ba_[:, 0:TT], w2a2[l][64:128, cols], tanh_wd[64:128, :], True, True, [w2a2[l], tanh_wd], [ba_])
                P.act(a_[:], ba_[:, 0:TT], AF.Sigmoid, [ba_, vec], [a_], bias=V(l, "a0", pb))
                yield
                P.scan(lg[:], cst[:, CST_SCAN:CST_SCAN + TT], ld[:], 0.0, [cst, ld], [lg])
                P.ts("dve", kk_[:], k_, V(l, "kk", pb), None, ALU.mult, None, [zk_, vec], [kk_])
                P.tt("pool", kk2[:], kk_[:], kk_[:], ALU.mult, [kk_], [kk2])
                P.act(eg[:], lg[:], AF.Exp, [lg], [eg])
                yield
                bs = bank("rw")
                P.mm(bs[:, 0:TT], onesbd_f, kk2[:], True, True, [cst, kk2], [bs])
                P.act(sqk[:], bs[:, 0:TT], AF.Sqrt, [bs], [sqk])
                P.act(ieg[:], lg[:], AF.Exp, [lg], [ieg], scale=-1.0)
                P.tt("pool", eg1[:], lg[:], ld[:], ALU.subtract, [lg, ld], [eg1])
                P.act(eg1[:], eg1[:], AF.Exp, [eg1], [eg1])
                P.ts("dve", sqk[:], sqk[:], 1e-12, None, ALU.max, None, [sqk], [sqk])
                P.recip(sqk[:], sqk[:], [sqk], [sqk])
                P.tt("pool", kap[:], kk_[:], sqk[:], ALU.mult, [kk_, sqk], [kap])
                yield
                P.ts("dve", t1[:], a_[:], -1.0, V(l, "ka", pb), ALU.add, ALU.mult, [a_, vec], [t1])
                P.stt(kp[:], t1[:], 1.0, k_, ALU.add, ALU.mult, [t1, zk_], [kp])
                P.tt("pool", b_[:], kap[:], a_[:], ALU.mult, [kap, a_], [b_])
                lg3 = lg[:, :].rearrange("p (c t) -> p c t", t=LCH)
                P.tt("dve", dl[:, :].rearrange("p (c t) -> p c t", t=LCH), lg3[:, :, LCH - 1:LCH].to_broadcast([128, NCH, LCH]),
                     lg3, ALU.subtract, [lg], [dl])
                P.act(egl[:], dl[:], AF.Exp, [dl], [egl])
                P.copy("act", d["GL"][:, :], eg[:, :].rearrange("p (c t) -> p c t", t=LCH)[:, :, LCH - 1], [eg], [d["GL"]])
                yield
                P.tt("dve", rt32[:], r_, eg[:], ALU.mult, [zr_, eg], [rt32])
                P.copy("act", RTc[:], rt32[:], [rt32], [RTc])

                def padw(name, eng, in0, in1, rd):
                    t = pads[name]
                    tv = t[:, :].rearrange("p (c h t) -> p c h t", c=NCH, h=2)
                    for hh in range(2):
                        ps_ = slice(hh * 64, (hh + 1) * 64)
                        o = tv[ps_, :, hh, :]
                        i0 = in0[ps_, :].rearrange("p (c t) -> p c t", t=LCH)
                        if in1 is None:
                            P.copy(eng, o, i0, rd, [t])
                        else:
                            i1 = in1[ps_, :].rearrange("p (c t) -> p c t", t=LCH)
                            P.tt(eng, o, i0, i1, ALU.mult, rd, [t])
                padw("RTp", "act", rt32, None, [rt32])
                padw("KTp", "dve", kp, ieg, [kp, ieg])
                padw("CTp", "pool", kap, eg1, [kap, eg1])
                yield
                padw("BTp", "dve", b_, ieg, [b_, ieg])
                padw("VTp", "act", zs[:, 8 + pb, :], None, [zv_])
                padw("KGp", "pool", kp, egl, [kp, egl])
                padw("BGp", "dve", b_, egl, [b_, egl])
                yield "pre_pp"
                P.stt(rk[:], r_, V(l, "rk", pb), kp[:], ALU.mult, ALU.mult, [zr_, vec, kp], [rk])
                yield
                bb = bank("rw")
                P.mm(bb[:, 0:TT], onesbd_f, rk[:], True, True, [cst, rk], [bb])
                P.tt("dve", d["bonus"][:], bb[:, 0:TT], v_, ALU.mult, [bb, zv_], [d["bonus"]])
                if dbg_on and pb == 0:
                    dbg_dump("lg", lg, lg[:], [128, TT])
                    dbg_dump("kap", kap, kap[:], [128, TT])
                    dbg_dump("kp", kp, kp[:], [128, TT])
                    dbg_dump("a", a_, a_[:], [128, TT])
                yield

                def chunkmm(dst_bank, lname, rname, rbuf=None, rcols=128):
                    lt = pads[lname]
                    for c in range(NCH):
                        if rbuf is None:
                            rb_ = pads[rname]
                            rap = rb_[:, c * 128:(c + 1) * 128]
                        else:
                            rb_ = rbuf
                            rap = rbuf[:, c * rcols:(c + 1) * rcols]
                        P.mm(dst_bank[:, c * rcols:(c + 1) * rcols], lt[:, c * 128:(c + 1) * 128], rap, True, True,
                             [lt, rb_], [dst_bank])

                def masked(dst, src_bank, mcol, w):
                    n = NCH * w
                    P.tt("dve", dst[:, 0:n], src_bank[:, 0:n], mskb[:, mcol:mcol + n], ALU.mult, [src_bank, mskb], [dst])
                b1 = bank("rw")
                chunkmm(b1, "BTp", "CTp")
                masked(BTlev[0], b1, MSK_USN, 128)
                b2 = bank("rw")
                chunkmm(b2, "CTp", "BTp")
                masked(Blev[0], b2, MSK_LSN, 128)
                yield
                b3 = bank("rw")
                chunkmm(b3, "KTp", "CTp")
                masked(AkkT, b3, MSK_USP, 128)
                b4 = bank("rw")
                chunkmm(b4, "KTp", None, RTc, LCH)
                masked(d["ArkT"], b4, MSK_CI, LCH)
                b5 = bank("rw")
                chunkmm(b5, "BTp", None, RTc, LCH)
                masked(d["ArbT"], b5, MSK_CI, LCH)
                yield

            def tokmajor(src_name, dst_buf, dst_ap, eng):
                bt_ = bank("rw")
                lt = pads[src_name]
                for c in range(NCH):
                    P.mm(bt_[:, c * 128:(c + 1) * 128], lt[:, c * 128:(c + 1) * 128], ident_b, True, True,
                         [lt, cstb], [bt_])
                P.copy(eng, dst_ap, bt_[:, 0:NCH * 128] if len(dst_ap.shape) == 2 else
                       bt_[:, 0:NCH * 128].rearrange("p (c n) -> p c n", n=128), [bt_], [dst_buf])

            def solve(pb, inst, l=l):
                d = pp[inst]
                Blev, BTlev, AkkT, Xbf = Blev_i[inst], BTlev_i[inst], AkkT_i[inst], Xbf_i[inst]
                Xbfv = Xbf[:, :].rearrange("p (c n) -> p c n", n=256)
                tokmajor("VTp", d["Vbd"], d["Vbd"][:, :], "act")
                tokmajor("KGp", d["KGbd"], d["KGbd"][:, :], "act")
                yield
                tokmajor("BGp", d["BGbd"], d["BGbd"][:, :], "act")
                tokmajor("CTp", Xbf, Xbfv[:, :, 0:128], "act")
                bt_ = bank("rw")
                for c in range(NCH):
                    cs = slice(c * 128, (c + 1) * 128)
                    P.mm(bt_[:, cs], AkkT[:, cs], d["Vbd"][:, cs], True, True, [AkkT, d["Vbd"]], [bt_])
                P.copy("act", Xbfv[:, :, 128:256], bt_[:, 0:NCH * 128].rearrange("p (c n) -> p c n", n=128), [bt_], [Xbf])
                yield "tok_done"
                cur = 0
                NLEV = 6
                for lev in range(NLEV):
                    if lev < NLEV - 1:
                        nxt = 1 - cur
                        bq = bank("rw")
                        for c in range(NCH):
                            cs = slice(c * 128, (c + 1) * 128)
                            P.mm(bq[:, cs], Blev[cur][:, cs], BTlev[cur][:, cs], True, True, [Blev[cur], BTlev[cur]], [bq])
                        if lev < NLEV - 2:
                            bq2 = bank("rw")
                            for c in range(NCH):
                                cs = slice(c * 128, (c + 1) * 128)
                                P.mm(bq2[:, cs], BTlev[cur][:, cs], Blev[cur][:, cs], True, True,
                                     [Blev[cur], BTlev[cur]], [bq2])
                    for half in range(2):
                        bx_ = banks[4 + half]
                        for cc_ in range(2):
                            c = half * 2 + cc_
                            P.mm(bx_[:, cc_ * 256:(cc_ + 1) * 256], BTlev[cur][:, c * 128:(c + 1) * 128], Xbfv[:, c, :],
                                 True, False, [BTlev[cur], Xbf], [bx_])
                            P.mm(bx_[:, cc_ * 256:(cc_ + 1) * 256], ident_b, Xbfv[:, c, :],
                                 False, True, [cstb, Xbf], [bx_])
                    if lev < NLEV - 1:
                        P.copy("act", BTlev[nxt][:], bq[:, 0:512], [bq], [BTlev[nxt]])
                        if lev < NLEV - 2:
                            P.copy("dve", Blev[nxt][:], bq2[:, 0:512], [bq2], [Blev[nxt]])
                    P.copy("act", Xbf[:, 0:512], banks[4][:, 0:512], [banks[4]], [Xbf])
                    P.copy("dve", Xbf[:, 512:1024], banks[5][:, 0:512], [banks[5]], [Xbf])
                    if lev < NLEV - 1:
                        cur = nxt
                    yield
                nU0v = d["nU0"][:, :].rearrange("p (c n) -> p c n", n=128)
                Wbdv = d["Wbd"][:, :].rearrange("p (c n) -> p c n", n=128)
                P.ts("dve", nU0v, Xbfv[:, :, 128:256], -1.0, None, ALU.mult, None, [Xbf], [d["nU0"]])
                P.copy("pool", Wbdv, Xbfv[:, :, 0:128], [Xbf], [d["Wbd"]])
                yield
                br = bank("rw")
                for c in range(NCH):
                    P.mm(br[:, c * LCH:(c + 1) * LCH], d["Wbd"][:, c * 128:(c + 1) * 128],
                         d["ArbT"][:, c * LCH:(c + 1) * LCH], True, True, [d["Wbd"], d["ArbT"]], [br])
                P.tt("dve", d["Rhat"][:], d["rt32"][:], br[:, 0:TT], ALU.subtract, [d["rt32"], br], [d["Rhat"]])
                bp = bank("rw")
                for c in range(NCH):
                    cs = slice(c * 128, (c + 1) * 128)
                    P.mm(bp[:, cs], d["Wbd"][:, cs], d["BGbd"][:, cs], True, True, [d["Wbd"], d["BGbd"]], [bp])
                for c in range(NCH):
                    cs = slice(c * 128, (c + 1) * 128)
                    P.stt(d["PT"][:, cs], ident_f, d["GL"][:, c:c + 1], bp[:, cs], ALU.mult, ALU.subtract,
                          [cst, d["GL"], bp], [d["PT"]])
                yield

            def seq(pb, inst, c, l=l):
                d = pp[inst]
                T = Tst[l][pb]
                tb_ = banks[4 + inst]
                yb = tb_
                ycols = slice(128 + c * LCH, 128 + (c + 1) * LCH)
                cs = slice(c * 128, (c + 1) * 128)
                cl_ = slice(c * LCH, (c + 1) * LCH)
                P.mm(yb[:, ycols], T[:], d["Rhat"][:, cl_], True, False, [T, d["Rhat"]], [yb])
                P.mm(yb[:, ycols], d["Vbd"][:, cs], d["ArkT"][:, cl_], False, False, [d["Vbd"], d["ArkT"]], [yb])
                P.mm(yb[:, ycols], d["nU0"][:, cs], d["ArbT"][:, cl_], False, True, [d["nU0"], d["ArbT"]], [yb])
                P.mm(tb_[:, 0:128], d["PT"][:, cs], T[:], True, False, [d["PT"], T], [tb_])
                P.mm(tb_[:, 0:128], d["KGbd"][:, cs], d["Vbd"][:, cs], False, False, [d["KGbd"], d["Vbd"]], [tb_])
                P.mm(tb_[:, 0:128], d["BGbd"][:, cs], d["nU0"][:, cs], False, True, [d["BGbd"], d["nU0"]], [tb_])
                P.copy("act", T[:], tb_[:, 0:128], [tb_], [T])

            def fin(pb, inst, l=l, dbg_on=dbg_on):
                assert lru_done[0], "fin emitted before the LRU chain finished (scratch lf[3] still live)"
                d = pp[inst]
                yb = banks[4 + inst]
                y32, yc, ysq, rs = spf[0], spf[1], spf[2], lf[3]
                P.copy("act", y32[:], yb[:, 128:128 + TT], [yb], [y32])
                if dbg_on:
                    dbg_dump("y_rw%d" % pb, y32, y32[:], [128, TT])
                bm = bank("rw")
                P.mm(bm[:, 0:TT], onesbd_f, y32[:], True, True, [cst, y32], [bm])
                P.stt(yc[:], bm[:, 0:TT], -1.0 / 64, y32[:], ALU.mult, ALU.add, [bm, y32], [yc])
                P.act(ysq[:], yc[:], AF.Square, [yc], [ysq])
                bv = bank("rw")
                P.mm(bv[:, 0:TT], onesbd_f, ysq[:], True, True, [cst, ysq], [bv])
                P.act(rs[:], bv[:, 0:TT], AF.Ln, [bv, eps_t[GN_EPS]], [rs], bias=eps_t[GN_EPS][:, 0:1], scale=1.0 / 64)
                P.act(rs[:], rs[:], AF.Exp, [rs], [rs], scale=-0.5)
                P.tt("dve", yc[:], yc[:], rs[:], ALU.mult, [yc, rs], [yc])
                P.ts("dve", yc[:], yc[:], V(l, "lnw", pb), V(l, "lnb", pb), ALU.mult, ALU.add, [yc, vec], [yc])
                P.tt("pool", yc[:], yc[:], d["bonus"][:], ALU.add, [yc, d["bonus"]], [yc])
                P.tt("dve", ycat[:, pb, :], yc[:], sgate[:, pb, :], ALU.mult, [yc, sgv[pb]], [ycv[pb]])

            def rwkv_gen():
                def chain(pb, inst):
                    yield from prep(pb, inst)
                    if pb == 3:
                        prep3_done[0] = True
                    yield from solve(pb, inst)

                def tail(pbs):
                    for c in range(NCH):
                        for inst, pb in enumerate(pbs):
                            seq(pb, inst, c)
                            yield
                    for inst, pb in enumerate(pbs):
                        fin(pb, inst)
                        yield

                def merge(ga, gb):
                    da = db = False
                    while not (da and db):
                        if not da:
                            try:
                                next(ga)
                                yield
                            except StopIteration:
                                da = True
                        if not db:
                            try:
                                next(gb)
                                yield
                            except StopIteration:
                                db = True

                def half_front(pbs):
                    gA, gB = chain(pbs[0], 0), chain(pbs[1], 1)
                    for v in gA:
                        yield
                        if v == "tok_done":
                            break
                    yield from merge(gA, gB)

                yield from half_front((0, 1))
                t0_ = tail((0, 1))
                gA, gB = chain(2, 0), chain(3, 1)

                def front2a():
                    for v in gA:
                        yield
                        if v == "pre_pp":
                            break
                yield from merge(t0_, front2a())
                half0_done[0] = True
                for v in gA:
                    yield
                    if v == "tok_done":
                        break
                yield from merge(gA, gB)
                yield from tail((2, 3))

            def s5_gen(l=l, dbg_on=dbg_on):
                Bv = s5B[l][:, :].rearrange("p (r b q m) -> p r b q m", r=2, b=4, q=2)
                Cv = s5C[l][:, :].rearrange("p (r j m) -> p r j m", r=2, j=16)
                rotk, keep = s5rot[l]
                NS = TT // TS5
                yb5 = banks[7]

                def stage0(blk, s, jj, k):
                    hf, jh = jj // 2, jj % 2
                    hs = slice(64 * hf, 64 * hf + 64)
                    tsl = slice(s * TS5, (s + 1) * TS5)
                    bu = banks[k % 2]
                    c0 = 0
                    bre = bu[:, c0:c0 + TS5]
                    bim = bu[:, c0 + TS5:c0 + 2 * TS5]
                    P.mm(bre, Bv[hs, 0, blk, jh, :], ubf[hs, blk, tsl], True, True, [s5B[l], ubf], [bu])
                    P.mm(bim, Bv[hs, 1, blk, jh, :], ubf[hs, blk, tsl], True, True, [s5B[l], ubf], [bu])

                def stage1(blk, s, jj, k):
                    j = blk * 4 + jj
                    (t1, t2, bzr, bzi, zr, zi, t3, t4) = s5f[k]
                    bu = banks[k % 2]
                    c0 = 0
                    bre = bu[:, c0:c0 + TS5]
                    bim = bu[:, c0 + TS5:c0 + 2 * TS5]
                    cosT = s5tab[:, 0, j, :]
                    sinT = s5tab[:, 1, j, :]
                    P.tt("dve", t1[:], bre, cosT, ALU.mult, [bu, s5tab], [t1])
                    P.tt("dve", t2[:], bim, sinT, ALU.mult, [bu, s5tab], [t2])
                    P.tt("dve", t3[:], bim, cosT, ALU.mult, [bu, s5tab], [t3])
                    P.tt("dve", t4[:], bre, sinT, ALU.mult, [bu, s5tab], [t4])
                    P.tt(S5E, bzr[:], t1[:], t2[:], ALU.add, [t1, t2], [bzr])
                    P.tt(S5E, bzi[:], t3[:], t4[:], ALU.subtract, [t3, t4], [bzi])

                def stage2(blk, s, jj, k, kx):
                    j = blk * 4 + jj
                    hf, jh = jj // 2, jj % 2
                    hs = slice(64 * hf, 64 * hf + 64)
                    tsl = slice(s * TS5, (s + 1) * TS5)
                    (t1, t2, bzr, bzi, zr, zi, t3, t4) = s5f[k]
                    sx = s5x[kx]
                    zl = s5zl[k]
                    zv = s5zv[l][j]
                    cosT = s5tab[:, 0, j, :]
                    sinT = s5tab[:, 1, j, :]
                    rho_b = keep[:, 0, j:j + 1].to_broadcast([128, TS5])
                    P.scan(zr[:], rho_b, bzr[:], s5z[l][:, 0, j:j + 1], [keep, bzr, zv], [zr])
                    P.scan(zi[:], rho_b, bzi[:], s5z[l][:, 1, j:j + 1], [keep, bzi, zv], [zi])
                    P.tt("dve", t1[:], zr[:], cosT, ALU.mult, [zr, s5tab], [t1])
                    P.tt("dve", t2[:], zi[:], sinT, ALU.mult, [zi, s5tab], [t2])
                    P.tt(S5E, sx[:, 0, :], t1[:], t2[:], ALU.subtract, [t1, t2], [sx])
                    P.tt("dve", t3[:], zr[:], sinT, ALU.mult, [zr, s5tab], [t3])
                    P.tt(S5E, t4[:], zi[:], cosT, ALU.mult, [zi, s5tab], [t4])
                    P.tt(S5E, sx[:, 1, :], t3[:], t4[:], ALU.add, [t3, t4], [sx])
                    rc = rotk[:, 0, j:j + 1]
                    rs_ = rotk[:, 1, j:j + 1]
                    zlr = zr[:, TS5 - 1:TS5]
                    zli = zi[:, TS5 - 1:TS5]
                    P.ts("dve", zl[:, 0:1], zli, rs_, None, ALU.mult, None, [zi, rotk], [zl])
                    P.ts("dve", zl[:, 1:2], zlr, rs_, None, ALU.mult, None, [zr, rotk], [zl])
                    P.stt(s5z[l][:, 0, j:j + 1], zlr, rc, zl[:, 0:1], ALU.mult, ALU.subtract, [zr, rotk, zl], [zv])
                    P.stt(s5z[l][:, 1, j:j + 1], zli, rc, zl[:, 1:2], ALU.mult, ALU.add, [zi, rotk, zl], [zv])

                def stage3(blk, s, jj, kx):
                    j = blk * 4 + jj
                    hf, jh = jj // 2, jj % 2
                    hs = slice(64 * hf, 64 * hf + 64)
                    tsl = slice(s * TS5, (s + 1) * TS5)
                    sx = s5x[kx]
                    P.mm(yb5[hs, tsl], Cv[:, 0, j, :], sx[:, 0, :], jh == 0, False, [s5C[l], sx], [yb5])
                    P.mm(yb5[hs, tsl], Cv[:, 1, j, :], sx[:, 1, :], False, jh == 1, [s5C[l], sx], [yb5])

                for blk in range(4):
                    units = [(s, jj) for s in range(NS) for jj in range(4)]
                    ks = []
                    for (s, jj) in units:
                        ks.append(s5ctr[0] % NS5SET)
                        s5ctr[0] += 1
                    nU = len(units)
                    stage0(blk, units[0][0], units[0][1], ks[0])
                    stage0(blk, units[1][0], units[1][1], ks[1])
                    stage1(blk, units[0][0], units[0][1], ks[0])
                    yield
                    for u in range(nU):
                        if u + 1 < nU:
                            stage1(blk, units[u + 1][0], units[u + 1][1], ks[u + 1])
                            if u + 2 < nU:
                                stage0(blk, units[u + 2][0], units[u + 2][1], ks[u + 2])
                            yield
                        stage2(blk, units[u][0], units[u][1], ks[u], u % NS5X)
                        if u >= 2:
                            stage3(blk, units[u - 2][0], units[u - 2][1], (u - 2) % NS5X)
                        yield
                    stage3(blk, units[nU - 2][0], units[nU - 2][1], (nU - 2) % NS5X)
                    stage3(blk, units[nU - 1][0], units[nU - 1][1], (nU - 1) % NS5X)
                    ys, x2, q_ = spf
                    P.stt(ys[:], zu[:, blk, :], V(l, "s5d", blk), yb5[:, 0:TT], ALU.mult, ALU.add, [zu, vec, yb5], [ys])
                    if dbg_on:
                        dbg_dump("s5y%d" % blk, ys, ys[:], [128, TT])
                    P.act(x2[:], ys[:], AF.Square, [ys], [x2])
                    P.ts("dve", x2[:], x2[:], 0.044715, 1.0, ALU.mult, ALU.add, [x2], [x2])
                    P.tt("pool", q_[:], x2[:], ys[:], ALU.mult, [x2, ys], [q_])
                    P.act(x2[:], q_[:], AF.Sigmoid, [q_], [x2], scale=2.0 * math.sqrt(2.0 / math.pi))
                    P.tt("dve", mix[:, blk, :], ys[:], x2[:], ALU.mult, [ys, x2], [mixv[blk]])
                    yield
                zgb = ubf
                P.copy("pool", zgb[:], mix[:], mixv, [zgb])
                rg = next_ring()
                P.dma(rg[:, 0:2048].rearrange("p (k n) -> p k n", k=4), glu_w_b[l].rearrange("(k p) n -> p k n", p=128),
                      reads=[wsb], writes=[rg])
                rgv = rg[:, 0:2048].rearrange("p (k n) -> p k n", k=4)
                for ob in range(4):
                    b = banks[0]
                    for k in range(4):
                        P.mm(b[:, 0:TT], rgv[:, k, ob * 128:(ob + 1) * 128], zgb[:, k, :], k == 0, k == 3, [rg, zgb], [b])
                    sg_ = spf[ob % 2]
                    P.act(sg_[:], b[:, 0:TT], AF.Sigmoid, [b, vec], [sg_], bias=V(l, "glub", ob))
                    P.tt("pool", sg_[:], sg_[:], sgate[:, 4 + ob, :], ALU.mult, [sg_, sgv[4 + ob]], [sg_])
                    P.tt("dve", ycat[:, 4 + ob, :], mix[:, ob, :], sg_[:], ALU.mult, [mixv[ob], sg_], [ycv[4 + ob]])
                    if ob == 3:
                        s5_done[0] = True
                    yield

            lru_done = [("lru" in SKIP)]
            s5_done = [False]
            half0_done = [False]
            prep3_done = [False]

            def lru_gen(l=l, dbg_on=dbg_on):
                for blk in range(4):
                    A, B, C, Dd = lf
                    bl = banks[5]
                    P.ts("dve", A[:], zx[:, blk, 0:TT], V(l, "cw", 0 * 4 + blk), V(l, "cb", blk), ALU.mult, ALU.add,
                         [zx, vec], [A])
                    for j in range(1, 4):
                        P.stt(A[:], zx[:, blk, j:j + TT], V(l, "cw", j * 4 + blk), A[:], ALU.mult, ALU.add,
                              [zx, vec, A], [A])
                    P.copy("act", lb[:], A[:], [A], [lb])
                    yield
                    P.mm(bl[:, 0:TT], lruw[l][:, blk * 128:(blk + 1) * 128], lb[:], True, True, [lruw[l], lb], [bl])
                    P.mm(bl[:, TT:2 * TT], lruw[l][:, (4 + blk) * 128:(5 + blk) * 128], lb[:], True, True,
                         [lruw[l], lb], [bl])
                    P.act(B[:], bl[:, 0:TT], AF.Sigmoid, [bl, vec], [B], bias=V(l, "ba", blk))
                    P.act(C[:], bl[:, TT:2 * TT], AF.Sigmoid, [bl, vec], [C], bias=V(l, "bx", blk))
                    yield
                    P.act(Dd[:], B[:], AF.Exp, [B, lru_c], [Dd], scale=lru_c[:, l, blk:blk + 1])
                    P.act(B[:], B[:], AF.Exp, [B, lru_c], [B], scale=lru_c[:, l, 4 + blk:5 + blk])
                    P.act(B[:], B[:], AF.Ln, [B, one_t], [B], bias=one_t[:, 0:1], scale=-1.0)
                    P.act(B[:], B[:], AF.Exp, [B], [B], scale=0.5)
                    P.tt("pool", C[:], C[:], A[:], ALU.mult, [C, A], [C])
                    P.tt("pool", C[:], C[:], B[:], ALU.mult, [C, B], [C])
                    yield
                    P.scan(A[:], Dd[:], C[:], ch[l][:, blk:blk + 1], [Dd, C, ch[l]], [A])
                    P.copy("act", ch[l][:, blk:blk + 1], A[:, TT - 1:TT], [A], [ch[l]])
                    if dbg_on:
                        dbg_dump("lru%d" % blk, A, A[:], [128, TT])
                    P.tt("pool", ycat[:, 8 + blk, :], A[:], sgate[:, 8 + blk, :], ALU.mult, [A, sgv[8 + blk]], [ycv[8 + blk]])
                    if blk == 3:
                        lru_done[0] = True
                    yield

            KT_EARLY = (0, 1, 4, 5, 6, 7, 8, 9, 10, 11)
            KT_LATE = (2, 3)

            def oproj_early(l=l):
                while not (s5_done[0] and lru_done[0] and half0_done[0]):
                    yield
                for oc in range(4):
                    r = next_ring()
                    P.dma(r[:, 0:12 * 256].rearrange("p (k n) -> p k n", k=12),
                          w_out_b[l].rearrange("(k p) n -> p k n", p=128)[:, :, oc * 256:(oc + 1) * 256],
                          reads=[wsb], writes=[r])
                    rv = r[:, 0:12 * 256].rearrange("p (k n) -> p k n", k=12)
                    yield
                    for ob2 in range(2):
                        ob = oc * 2 + ob2
                        b = bank("proj")
                        for i_, k in enumerate(KT_EARLY):
                            P.mm(b[:, 0:TT], rv[:, k, ob2 * 128:(ob2 + 1) * 128], ycat[:, k, :], i_ == 0,
                                 i_ == len(KT_EARLY) - 1, [r, ycv[k]], [b])
                        P.tt("dve", hT[:, ob, :], hT[:, ob, :], b[:, 0:TT], ALU.add, [hTv[ob], b], [hTv[ob]])
                        yield

            def ple_early(l=l):
                while not (prep3_done[0] and s5_done[0]):
                    yield
                r = next_ring()
                P.dma(r[:, 0:2048].rearrange("p (k n) -> p k n", k=2), ple_w_b[l].rearrange("(k p) n -> p k n", p=128),
                      reads=[wsb], writes=[r])
                rv = r[:, 0:2048].rearrange("p (k n) -> p k n", k=2)
                yield
                for ob in range(8):
                    b = bank("proj")
                    for k in range(2):
                        P.mm(b[:, 0:TT], rv[:, k, ob * 128:(ob + 1) * 128], pbf[:, k, :], k == 0, k == 1, [r, pbf], [b])
                    P.copy("act", zs[:, ob, :], b[:, 0:TT], [b], [zsv[ob]])
                    yield
                rms_rstd(zsv[0:8], lambda k: zs[:, k, :], 8, NORM_EPS)
                yield

            early_ok = ("rwkv" not in SKIP) and ("s5" not in SKIP) and ("lru" not in SKIP)
            gens = []
            if early_ok:
                gens.append((oproj_early(), 1))
                gens.append((ple_early(), 1))
            if "rwkv" not in SKIP:
                gens.append((rwkv_gen(), GW[0]))
            if "s5" not in SKIP:
                gens.append((s5_gen(), GW[1]))
            if "lru" not in SKIP:
                gens.append((lru_gen(), GW[2]))
            if "serial" in SKIP:
                for g, w in gens:
                    drive([(g, 1)])
            else:
                drive(gens)
            if dbg_on:
                dbg_dump("ycat_rw", ycat, ycat[:, 0:4, :].rearrange("p a t -> p (a t)"), [128, 4 * TT], BF16)

            if early_ok:
                r = next_ring()
                P.dma(r[:, 0:2 * 1024].rearrange("p (k n) -> p k n", k=2),
                      w_out_b[l].rearrange("(k p) n -> p k n", p=128)[:, 2:4, :],
                      reads=[wsb], writes=[r])
                rv = r[:, 0:2 * 1024].rearrange("p (k n) -> p k n", k=2)
                for ob in range(8):
                    b = bank("proj")
                    for i_, k in enumerate(KT_LATE):
                        P.mm(b[:, 0:TT], rv[:, i_, ob * 128:(ob + 1) * 128], ycat[:, k, :], i_ == 0, i_ == 1,
                             [r, ycv[k]], [b])
                    P.tt("dve", hT[:, ob, :], hT[:, ob, :], b[:, 0:TT], ALU.add, [hTv[ob], b], [hTv[ob]])
            else:
                for oc in range(4):
                    r = next_ring()
                    P.dma(r[:, 0:12 * 256].rearrange("p (k n) -> p k n", k=12),
                          w_out_b[l].rearrange("(k p) n -> p k n", p=128)[:, :, oc * 256:(oc + 1) * 256],
                          reads=[wsb], writes=[r])
                    rv = r[:, 0:12 * 256].rearrange("p (k n) -> p k n", k=12)
                    for ob2 in range(2):
                        ob = oc * 2 + ob2
                        b = bank("proj")
                        for k in range(12):
                            P.mm(b[:, 0:TT], rv[:, k, ob2 * 128:(ob2 + 1) * 128], ycat[:, k, :], k == 0, k == 11,
                                 [r] + ycv, [b])
                        P.tt("dve", hT[:, ob, :], hT[:, ob, :], b[:, 0:TT], ALU.add, [hTv[ob], b], [hTv[ob]])
            for k in range(8):
                P.copy("act" if k % 2 == 0 else "dve", xn[:, k, :], hT[:, k, :], [hTv[k]], [xnv[k]])
            epre = zs
            if not early_ok:
                r = next_ring()
                P.dma(r[:, 0:2048].rearrange("p (k n) -> p k n", k=2), ple_w_b[l].rearrange("(k p) n -> p k n", p=128),
                      reads=[wsb], writes=[r])
                rv = r[:, 0:2048].rearrange("p (k n) -> p k n", k=2)
                for ob in range(8):
                    b = bank("proj")
                    for k in range(2):
                        P.mm(b[:, 0:TT], rv[:, k, ob * 128:(ob + 1) * 128], pbf[:, k, :], k == 0, k == 1, [r, pbf], [b])
                    P.copy("act", epre[:, ob, :], b[:, 0:TT], [b], [zsv[ob]])
                rms_rstd(zsv[0:8], lambda k: epre[:, k, :], 8, NORM_EPS)
            for gc in range(2):
                r = next_ring()
                P.dma(r[:, 0:4096].rearrange("p (k n) -> p k n", k=8),
                      ple_gw_b[l].rearrange("(k p) n -> p k n", p=128)[:, :, gc * 512:(gc + 1) * 512],
                      reads=[wsb], writes=[r])
                rv = r[:, 0:4096].rearrange("p (k n) -> p k n", k=8)
                for ob2 in range(4):
                    ob = gc * 4 + ob2
                    b = bank("proj")
                    for k in range(8):
                        P.mm(b[:, 0:TT], rv[:, k, ob2 * 128:(ob2 + 1) * 128], xn[:, k, :], k == 0, k == 7, [r, xnv[k]], [b])
                    sg_, e_ = fs[6 + 2 * (ob % 2)], fs[7 + 2 * (ob % 2)]
                    P.act(sg_[:], b[:, 0:TT], AF.Sigmoid, [b], [sg_])
                    P.stt(e_[:], epre[:, ob, :], V(l, "png", ob), rstd[:], ALU.mult, ALU.mult, [zsv[ob], vec, rstd], [e_])
                    P.tt("dve", e_[:], e_[:], sg_[:], ALU.mult, [e_, sg_], [e_])
                    P.tt("dve", hT[:, ob, :], hT[:, ob, :], e_[:], ALU.add, [hTv[ob], e_], [hTv[ob]])
            if dbg_on:
                dbg_dump("h1", hTv[7], hT[:, :, :].rearrange("p a t -> p (a t)"), [128, 8 * TT])
        rms_rstd(hTv, lambda k: hT[:, k, :], 8, NORM_EPS)
        fo = 2 * VEC_PER_LAYER
        for k in range(8):
            P.stt(zs[:, k, :], hT[:, k, :], vec[:, fo + k:fo + k + 1], rstd[:], ALU.mult, ALU.mult, [hTv[k], vec, rstd], [zsv[k]])
        out_toks.append(P.dma(oT[:, t0:t0 + TT].rearrange("(k p) t -> p k t", p=128), zs[:, 0:8, :], reads=zsv[0:8]))
    out_toks.extend(dbg_out.values())
    ninst = P.ninst
    P.finish(out_toks)
    return nc, ninst


def pack_shared(inp):
    f = lambda a: np.asarray(a, np.float32)
    vec = np.zeros((128, NVEC), np.float32)
    for l in range(2):
        def put(name, arr, n):
            c = vcol(l, name)
            vec[:, c:c + n] = _pp(arr, n)
        put("ng", f(inp["norm_g"])[l], 8)
        put("mu", f(inp["rwkv_mu"])[l], 13)
        put("w0", f(inp["rwkv_w0"])[l], 4)
        put("a0", f(inp["rwkv_a0"])[l], 4)
        put("kk", f(inp["rwkv_k_k"])[l], 4)
        put("ka", f(inp["rwkv_k_a"])[l], 4)
        put("rk", f(inp["rwkv_r_k"])[l].reshape(512), 4)
        put("lnw", f(inp["rwkv_ln_w"])[l], 4)
        put("lnb", f(inp["rwkv_ln_b"])[l], 4)
        put("s5d", f(inp["s5_d"])[l], 4)
        put("glub", f(inp["s5_glu_b"])[l], 4)
        cw = f(inp["lru_conv_w"])[l]
        c = vcol(l, "cw")
        for j in range(4):
            vec[:, c + 4 * j:c + 4 * j + 4] = _pp(cw[j], 4)
        put("cb", f(inp["lru_conv_b"])[l], 4)
        put("ba", f(inp["lru_ba"])[l], 4)
        put("bx", f(inp["lru_bx"])[l], 4)
        put("lam", f(inp["lru_lambda"])[l], 4)
        put("png", f(inp["ple_norm_g"])[l], 8)
    vec[:, 2 * VEC_PER_LAYER:2 * VEC_PER_LAYER + 8] = _pp(f(inp["final_norm_g"]), 8)

    w2a2 = np.zeros((2, 128, 512), np.float32)
    w2a2[:, 0:64] = f(inp["rwkv_w2"])
    w2a2[:, 64:128] = f(inp["rwkv_a2"])
    lruw = np.zeros((2, 128, 8, 128), np.float32)
    for l in range(2):
        for m, key in enumerate(("lru_wa", "lru_wx")):
            w = f(inp[key])[l]
            for q in range(4):
                for b2 in range(2):
                    lruw[l, b2 * 64:(b2 + 1) * 64, m * 4 + q, b2 * 64:(b2 + 1) * 64] = w[2 * q + b2]
    lruw = lruw.reshape(2, 128, 1024)
    def modes(a):
        a = f(a).reshape(2, 16, 2, 64)
        return np.ascontiguousarray(a.transpose(0, 2, 3, 1).reshape(2, 128, 16))
    s5s = np.zeros((2, 128, 3, 16), np.float32)
    s5s[:, :, 0] = modes(inp["s5_a_re"])
    s5s[:, :, 1] = modes(inp["s5_a_im"])
    ldt = np.broadcast_to(f(inp["s5_log_dt"])[:, :, None], (2, 32, 64))
    s5s[:, :, 2] = modes(ldt)
    s5s = s5s.reshape(2, 128, 48)
    def bmodes(a):
        a = f(a).reshape(2, 16, 2, 64, 16)
        return a.transpose(0, 2, 3, 1, 4).reshape(2, 128, 16, 16)
    s5b = np.stack([bmodes(inp["s5_b_re"]), bmodes(inp["s5_b_im"])], axis=2).reshape(2, 128, 512)
    s5c = np.zeros((2, 128, 2, 16, 64), np.float32)
    for ri, key in enumerate(("s5_c_re", "s5_c_im")):
        c = f(inp[key]).reshape(2, 16, 2, 16, 64)
        for gh in range(2):
            for jh in range(2):
                c0 = 32 * jh + 16 * gh
                s5c[:, gh * 64:(gh + 1) * 64, ri, jh::2, c0:c0 + 16] = c[:, jh::2, gh].transpose(0, 3, 1, 2)
    s5c = s5c.reshape(2, 128, 2048)
    return {
        "w_in": np.ascontiguousarray(f(inp["w_in"])), "w_out": np.ascontiguousarray(f(inp["w_out"])),
        "ple_w": np.ascontiguousarray(f(inp["ple_w"])), "ple_gw": np.ascontiguousarray(f(inp["ple_gate_w"])),
        "glu_w": np.ascontiguousarray(f(inp["s5_glu_w"])), "vec": vec, "cst": make_consts()[0], "msk": make_consts()[1],
        "w2a2": w2a2, "lruw": np.ascontiguousarray(lruw), "s5s": np.ascontiguousarray(s5s),
        "s5b": np.ascontiguousarray(s5b), "s5c": np.ascontiguousarray(s5c),
    }


_NC_CACHE = {}


def run_cores(inp, TC, batches, dbg=None):
    key = (TC, tuple(sorted(dbg)) if dbg else None)
    if key not in _NC_CACHE:
        _NC_CACHE[key] = build_nc(TC, dbg)
    nc, ninst = _NC_CACHE[key]
    shared = pack_shared(inp)
    x = np.asarray(inp["x"], np.float32)
    p = np.asarray(inp["p"], np.float32)
    in_maps = []
    for b in batches:
        m = dict(shared)
        m["xT"] = np.ascontiguousarray(x[b, :TC].T)
        m["pT"] = np.ascontiguousarray(p[:, b, :TC].transpose(0, 2, 1))
        in_maps.append(m)
    res = run_bass_kernel_spmd(nc, in_maps, core_ids=list(range(len(batches))))
    return res


def kernel(**inputs):
    x = np.asarray(inputs["x"])
    B, S, _ = x.shape
    batches = [c % B for c in range(8)]
    res = run_cores(inputs, S, batches)
    out = np.empty((B, S, D), np.float32)
    for b in range(B):
        out[b] = res.results[b]["oT"].T
    return out.astype(x.dtype)
```
